# Optimizing a Trainium2 kernel written in Bass

```python
import math
import jax, jax.numpy as jnp
from jax import lax
import numpy as np

D_MODEL = 1024
BATCH = 16
SEQ = 256
DEPTH = 2
DEC_BATCH = 8
DEC_SEQ = 1024
PAST_LEN = 512

GRID_W = 64
HEAD_DIM = 64
N_Q_HEADS = 8
N_KV_HEADS = 2
GQA_GROUP = N_Q_HEADS // N_KV_HEADS
ATTN_W = N_Q_HEADS * HEAD_DIM
KV_W = N_KV_HEADS * HEAD_DIM
ROPE_AXIS_PAIRS = HEAD_DIM // 4
ROPE_THETA = 10000.0
Q_BLOCK = 128
SSM_W = D_MODEL // 2
SSM_GROUP = 16
N_SSM_GROUPS = SSM_W // SSM_GROUP
STATE_P = 64
DT_MIN = 1e-3
DT_MAX = 1e-1
FOURIER_W = D_MODEL // 2
FOURIER_GROUPS = 4
FOURIER_GW = FOURIER_W // FOURIER_GROUPS
N_BRANCH = 3
D_IN = ATTN_W + 2 * KV_W + SSM_W + FOURIER_W + N_BRANCH * D_MODEL
SPLITS = [ATTN_W, ATTN_W + KV_W, ATTN_W + 2 * KV_W, ATTN_W + 2 * KV_W + SSM_W,
          ATTN_W + 2 * KV_W + SSM_W + FOURIER_W]
D_FF = ((8 * D_MODEL // 3 + 127) // 128) * 128
CONV_W = 3
EPS = 1e-6
ALPHA = (2 * DEPTH) ** 0.25
BETA = (8 * DEPTH) ** -0.25

kernel_name = "hybrid_diffusion_prefix_trunk_step"


def _layer_norm(x, g, b):
    xf = x.astype(jnp.float32)
    mu = jnp.mean(xf, axis=-1, keepdims=True)
    xc = xf - mu
    var = jnp.mean(xc * xc, axis=-1, keepdims=True)
    y = xc * lax.rsqrt(var + EPS) * g.astype(jnp.float32) + b.astype(jnp.float32)
    return y.astype(x.dtype)


def _rms_norm(x, g):
    xf = x.astype(jnp.float32)
    y = xf * lax.rsqrt(jnp.mean(xf * xf, axis=-1, keepdims=True) + EPS) * g.astype(jnp.float32)
    return y.astype(x.dtype)


def _axial_rope_tables(n_tokens):
    rows = n_tokens // GRID_W
    row = jnp.repeat(jnp.arange(rows, dtype=jnp.float32), GRID_W)
    col = jnp.tile(jnp.arange(GRID_W, dtype=jnp.float32), rows)
    freqs = ROPE_THETA ** (-jnp.arange(ROPE_AXIS_PAIRS, dtype=jnp.float32) / ROPE_AXIS_PAIRS)
    ang = jnp.concatenate([row[:, None] * freqs, col[:, None] * freqs], axis=-1)
    return jnp.cos(ang), jnp.sin(ang)


def _apply_rope(x, cos, sin):
    b, l, h, _ = x.shape
    xr = x.astype(jnp.float32).reshape(b, l, h, HEAD_DIM // 2, 2)
    x0, x1 = xr[..., 0], xr[..., 1]
    c = cos[None, :, None, :]
    s = sin[None, :, None, :]
    out = jnp.stack([x0 * c - x1 * s, x0 * s + x1 * c], axis=-1)
    return out.reshape(b, l, h, HEAD_DIM).astype(x.dtype)


def _block_attention(q, k, v):
    bsz, lq = q.shape[:2]
    n_blocks = lq // Q_BLOCK
    qb = q.reshape(bsz, n_blocks, Q_BLOCK, N_KV_HEADS, GQA_GROUP, HEAD_DIM).transpose(1, 0, 2, 3, 4, 5)
    scale = HEAD_DIM ** -0.5

    def one_block(q_blk):
        s = jnp.einsum("bqhgd,bkhd->bhgqk", q_blk, k).astype(jnp.float32) * scale
        p = jax.nn.softmax(s, axis=-1).astype(v.dtype)
        return jnp.einsum("bhgqk,bkhd->bqhgd", p, v)

    o = lax.map(one_block, qb)
    return o.transpose(1, 0, 2, 3, 4, 5).reshape(bsz, lq, ATTN_W)


def _zoh(a_re, a_im, log_dt, b_re, b_im):
    a_re = a_re.astype(jnp.float32)
    a_im = a_im.astype(jnp.float32)
    b_re = b_re.astype(jnp.float32)
    b_im = b_im.astype(jnp.float32)
    dt = jnp.exp(log_dt.astype(jnp.float32))[:, None]
    mag = jnp.exp(a_re * dt)
    ang = a_im * dt
    lb_re = mag * jnp.cos(ang)
    lb_im = mag * jnp.sin(ang)
    den = a_re * a_re + a_im * a_im
    n_re = lb_re - 1.0
    n_im = lb_im
    f_re = (n_re * a_re + n_im * a_im) / den
    f_im = (n_im * a_re - n_re * a_im) / den
    bb_re = f_re[..., None] * b_re - f_im[..., None] * b_im
    bb_im = f_re[..., None] * b_im + f_im[..., None] * b_re
    return lb_re, lb_im, bb_re, bb_im


def _complex_linear_combine(e1, e2):
    a1r, a1i, b1r, b1i = e1
    a2r, a2i, b2r, b2i = e2
    return (a2r * a1r - a2i * a1i, a2r * a1i + a2i * a1r,
            a2r * b1r - a2i * b1i + b2r, a2r * b1i + a2i * b1r + b2i)


def _s5_bidirectional(u, a_re, a_im, log_dt, b_re, b_im, c_re, c_im, d, h0):
    bsz, L = u.shape[:2]
    uf = u.astype(jnp.float32)
    ug = uf.reshape(bsz, L, N_SSM_GROUPS, SSM_GROUP)
    h0 = h0.astype(jnp.float32)
    y = d.astype(jnp.float32) * uf
    finals = []
    for di in range(2):
        lb_re, lb_im, bb_re, bb_im = _zoh(a_re[di], a_im[di], log_dt[di], b_re[di], b_im[di])
        bu_re = jnp.einsum("blgn,gpn->blgp", ug, bb_re)
        bu_im = jnp.einsum("blgn,gpn->blgp", ug, bb_im)
        if di == 1:
            bu_re = jnp.flip(bu_re, axis=1)
            bu_im = jnp.flip(bu_im, axis=1)
        hr, hi = h0[:, di, ..., 0], h0[:, di, ..., 1]
        bu_re = bu_re.at[:, 0].add(lb_re * hr - lb_im * hi)
        bu_im = bu_im.at[:, 0].add(lb_re * hi + lb_im * hr)
        a_r = jnp.broadcast_to(lb_re, bu_re.shape)
        a_i = jnp.broadcast_to(lb_im, bu_im.shape)
        _, _, s_re, s_im = lax.associative_scan(_complex_linear_combine, (a_r, a_i, bu_re, bu_im), axis=1)
        finals.append(jnp.stack([s_re[:, -1], s_im[:, -1]], axis=-1))
        if di == 1:
            s_re = jnp.flip(s_re, axis=1)
            s_im = jnp.flip(s_im, axis=1)
        yd = (jnp.einsum("blgp,gnp->blgn", s_re, c_re[di].astype(jnp.float32))
              - jnp.einsum("blgp,gnp->blgn", s_im, c_im[di].astype(jnp.float32)))
        y = y + yd.reshape(bsz, L, SSM_W)
    state = jnp.stack(finals, axis=1)
    return y.astype(u.dtype), state


def _fourier_mix(f):
    bsz, L = f.shape[:2]
    fg = f.astype(jnp.float32).reshape(bsz, L, FOURIER_GROUPS, FOURIER_GW)
    out = jnp.fft.fft2(fg, axes=(1, 3), norm="ortho").real
    return out.reshape(bsz, L, FOURIER_W).astype(f.dtype)


def _dwconv3(h, w, b):
    hp = jnp.pad(h, ((0, 0), (1, 1), (0, 0)))
    return hp[:, :-2] * w[0] + hp[:, 1:-1] * w[1] + hp[:, 2:] * w[2] + b


def _trunk_layer(x, mod, p, rope, ctx_k, ctx_v, h0):
    bsz, L = x.shape[:2]
    sh1, sc1, g1, sh2, sc2, g2 = jnp.split(mod[:, None, :], 6, axis=-1)
    u = x * (1.0 + sc1) + sh1
    z = u @ p["w_in"]
    q, k, v, s_in, f_in, gates = jnp.split(z, SPLITS, axis=-1)
    q = _rms_norm(q.reshape(bsz, L, N_Q_HEADS, HEAD_DIM), p["q_norm_g"])
    k = _rms_norm(k.reshape(bsz, L, N_KV_HEADS, HEAD_DIM), p["k_norm_g"])
    v = v.reshape(bsz, L, N_KV_HEADS, HEAD_DIM)
    if rope is None:
        keys, vals = k, v
    else:
        cos, sin = rope
        q = _apply_rope(q, cos, sin)
        k = _apply_rope(k, cos, sin)
        keys = jnp.concatenate([k, ctx_k.astype(k.dtype)], axis=1)
        vals = jnp.concatenate([v, ctx_v.astype(v.dtype)], axis=1)
    attn = _block_attention(q, keys, vals)
    ssm_y, ssm_state = _s5_bidirectional(s_in, p["ssm_a_re"], p["ssm_a_im"], p["ssm_log_dt"],
                                         p["ssm_b_re"], p["ssm_b_im"], p["ssm_c_re"], p["ssm_c_im"],
                                         p["ssm_d"], h0)
    ssm_y = jax.nn.gelu(ssm_y)
    ssm_y = ssm_y * jax.nn.sigmoid(ssm_y @ p["w_glu"] + p["b_glu"])
    four = _fourier_mix(f_in)
    g_a, g_s, g_f = jnp.split(jax.nn.sigmoid(gates), N_BRANCH, axis=-1)
    merged = (g_a * (attn @ p["w_br_attn"]) + g_s * (ssm_y @ p["w_br_ssm"])
              + g_f * (four @ p["w_br_four"]))
    mix_out = merged @ p["w_out"]
    x = _layer_norm(ALPHA * x + g1 * mix_out, p["ln1_g"], p["ln1_b"])
    u2 = x * (1.0 + sc2) + sh2
    h = _dwconv3(u2 @ p["w_up"], p["conv_w"], p["conv_b"])
    ha, hb = jnp.split(h, 2, axis=-1)
    ffn_out = (jax.nn.silu(ha) * hb) @ p["w_down"]
    x = _layer_norm(ALPHA * x + g2 * ffn_out, p["ln2_g"], p["ln2_b"])
    return x, k, v, ssm_state


def setup_inputs(seed: int = 0) -> dict:
    key = jax.random.key(seed)
    ks = jax.random.split(key, 40)
    f32 = jnp.float32
    nrm = lambda k_, shape, s: jax.random.normal(k_, shape, f32) * s
    G, P, N = N_SSM_GROUPS, STATE_P, SSM_GROUP
    n_idx = jnp.arange(P, dtype=f32)
    inp = {}
    inp["x_prompt"] = nrm(ks[0], (BATCH, SEQ, D_MODEL), 1.0)
    inp["x_sample"] = nrm(ks[1], (DEC_BATCH, DEC_SEQ, D_MODEL), 1.0)
    inp["c"] = nrm(ks[2], (DEC_BATCH, D_MODEL), 1.0)
    inp["cache_k"] = nrm(ks[3], (DEC_BATCH, DEPTH, PAST_LEN, N_KV_HEADS, HEAD_DIM), 1.0)
    inp["cache_v"] = nrm(ks[4], (DEC_BATCH, DEPTH, PAST_LEN, N_KV_HEADS, HEAD_DIM), 1.0)
    inp["state_ssm"] = nrm(ks[5], (DEC_BATCH, DEPTH, 2, G, P, 2), 0.3)
    inp["c_ctx"] = nrm(ks[6], (D_MODEL,), 1.0)
    inp["w_ada"] = nrm(ks[7], (DEPTH, D_MODEL, 6 * D_MODEL), 0.5 * D_MODEL ** -0.5)
    inp["b_ada"] = nrm(ks[8], (DEPTH, 6 * D_MODEL), 0.01)
    inp["w_in"] = nrm(ks[9], (DEPTH, D_MODEL, D_IN), D_MODEL ** -0.5)
    inp["q_norm_g"] = 1.0 + nrm(ks[10], (DEPTH, HEAD_DIM), 0.01)
    inp["k_norm_g"] = 1.0 + nrm(ks[11], (DEPTH, HEAD_DIM), 0.01)
    inp["ssm_a_re"] = -0.5 + nrm(ks[12], (DEPTH, 2, G, P), 0.01)
    inp["ssm_a_im"] = math.pi * n_idx + nrm(ks[13], (DEPTH, 2, G, P), 0.01)
    inp["ssm_log_dt"] = jax.random.uniform(ks[14], (DEPTH, 2, G), f32, math.log(DT_MIN), math.log(DT_MAX))
    inp["ssm_b_re"] = nrm(ks[15], (DEPTH, 2, G, P, N), (2 * N) ** -0.5)
    inp["ssm_b_im"] = nrm(ks[16], (DEPTH, 2, G, P, N), (2 * N) ** -0.5)
    inp["ssm_c_re"] = nrm(ks[17], (DEPTH, 2, G, N, P), (2 * P) ** -0.5)
    inp["ssm_c_im"] = nrm(ks[18], (DEPTH, 2, G, N, P), (2 * P) ** -0.5)
    inp["ssm_d"] = nrm(ks[19], (DEPTH, SSM_W), 1.0)
    inp["w_glu"] = nrm(ks[20], (DEPTH, SSM_W, SSM_W), SSM_W ** -0.5)
    inp["b_glu"] = nrm(ks[21], (DEPTH, SSM_W), 0.01)
    inp["w_br_attn"] = nrm(ks[22], (DEPTH, ATTN_W, D_MODEL), ATTN_W ** -0.5)
    inp["w_br_ssm"] = nrm(ks[23], (DEPTH, SSM_W, D_MODEL), SSM_W ** -0.5)
    inp["w_br_four"] = nrm(ks[24], (DEPTH, FOURIER_W, D_MODEL), FOURIER_W ** -0.5)
    inp["w_out"] = nrm(ks[25], (DEPTH, D_MODEL, D_MODEL), BETA * D_MODEL ** -0.5)
    inp["ln1_g"] = 1.0 + nrm(ks[26], (DEPTH, D_MODEL), 0.01)
    inp["ln1_b"] = nrm(ks[27], (DEPTH, D_MODEL), 0.01)
    inp["w_up"] = nrm(ks[28], (DEPTH, D_MODEL, 2 * D_FF), D_MODEL ** -0.5)
    inp["conv_w"] = nrm(ks[29], (DEPTH, CONV_W, 2 * D_FF), CONV_W ** -0.5)
    inp["conv_b"] = nrm(ks[30], (DEPTH, 2 * D_FF), 0.01)
    inp["w_down"] = nrm(ks[31], (DEPTH, D_FF, D_MODEL), BETA * D_FF ** -0.5)
    inp["ln2_g"] = 1.0 + nrm(ks[32], (DEPTH, D_MODEL), 0.01)
    inp["ln2_b"] = nrm(ks[33], (DEPTH, D_MODEL), 0.01)
    return inp


def reference(x_prompt, x_sample, c, cache_k, cache_v, state_ssm, c_ctx,
              w_ada, b_ada, w_in, q_norm_g, k_norm_g,
              ssm_a_re, ssm_a_im, ssm_log_dt, ssm_b_re, ssm_b_im, ssm_c_re, ssm_c_im, ssm_d,
              w_glu, b_glu, w_br_attn, w_br_ssm, w_br_four, w_out, ln1_g, ln1_b,
              w_up, conv_w, conv_b, w_down, ln2_g, ln2_b):
    stacked = dict(w_in=w_in, q_norm_g=q_norm_g, k_norm_g=k_norm_g,
                   ssm_a_re=ssm_a_re, ssm_a_im=ssm_a_im, ssm_log_dt=ssm_log_dt,
                   ssm_b_re=ssm_b_re, ssm_b_im=ssm_b_im, ssm_c_re=ssm_c_re, ssm_c_im=ssm_c_im,
                   ssm_d=ssm_d, w_glu=w_glu, b_glu=b_glu, w_br_attn=w_br_attn, w_br_ssm=w_br_ssm,
                   w_br_four=w_br_four, w_out=w_out, ln1_g=ln1_g, ln1_b=ln1_b,
                   w_up=w_up, conv_w=conv_w, conv_b=conv_b, w_down=w_down, ln2_g=ln2_g, ln2_b=ln2_b)

    h = x_prompt
    h0_ctx = jnp.zeros((x_prompt.shape[0], 2, N_SSM_GROUPS, STATE_P, 2), jnp.float32)
    ks_out, vs_out, ss_out = [], [], []
    for l in range(DEPTH):
        p = {name: arr[l] for name, arr in stacked.items()}
        mod_ctx = jax.nn.silu(c_ctx)[None, :] @ w_ada[l] + b_ada[l]
        h, k_l, v_l, s_l = _trunk_layer(h, mod_ctx, p, None, None, None, h0_ctx)
        ks_out.append(k_l)
        vs_out.append(v_l)
        ss_out.append(s_l)
    new_cache_k = jnp.stack(ks_out, axis=1)
    new_cache_v = jnp.stack(vs_out, axis=1)
    new_state_ssm = jnp.stack(ss_out, axis=1)

    rope = _axial_rope_tables(x_sample.shape[1])
    zx = x_sample
    for l in range(DEPTH):
        p = {name: arr[l] for name, arr in stacked.items()}
        mod = jax.nn.silu(c) @ w_ada[l] + b_ada[l]
        zx, _, _, _ = _trunk_layer(zx, mod, p, rope, cache_k[:, l], cache_v[:, l], state_ssm[:, l])

    return (h, zx, new_cache_k, new_cache_v, new_state_ssm)
```

```python
import contextlib, os
SK = os.environ.get("SK", "")
import numpy as np
import concourse.bass as bass
import concourse.mybir as mybir
from concourse.bass_utils import run_bass_kernel_spmd

F32 = mybir.dt.float32; BF16 = mybir.dt.bfloat16
AF = mybir.ActivationFunctionType; ALU = mybir.AluOpType; AX = mybir.AxisListType

D_MODEL = 1024; DEPTH = 2; D_IN = 4864; D_FF = 2816
ALPHA = (2 * DEPTH) ** 0.25
EPS = 1e-6
NT = 12; TOK = 1536; T = 16
SEQS = [(0, 1024), (1024, 256), (1280, 256)]
BLKS = [(0, 512), (512, 512), (1024, 512)]
TILE_VEC = [0] * 8 + [1] * 4
NS = 99
SOFF = [0, 65, 82]
SCH = [64, 16, 16]


class StopBuild(Exception):
    pass


class Res:
    __slots__ = ("name", "w", "r")
    def __init__(self, name):
        self.name = name; self.w = None; self.r = {}


class Eng:
    def __init__(self, name, h, sem):
        self.name = name; self.h = h; self.sem = sem; self.count = 0; self.waited = {}


class FW:
    NDMA = 32
    def __init__(self, nc, es):
        self.nc = nc; self.sems = {}; self.eng = {}
        for name, h in (("pe", nc.tensor), ("act", nc.scalar), ("dve", nc.vector), ("pool", nc.gpsimd), ("sp", nc.sync)):
            s = es.enter_context(nc.semaphore("s_" + name))
            self.sems[name] = s; self.eng[name] = Eng(name, h, s)
        self.dma_sems = []
        for i in range(self.NDMA):
            s = es.enter_context(nc.semaphore("s_dma%d" % i))
            self.sems["dma%d" % i] = s; self.dma_sems.append(["dma%d" % i, 0])
        self.dma_i = 0; self.ninstr = 0; self.dead = False

    def _wait(self, e, key, val):
        if e.waited.get(key, 0) >= val: return
        e.h.wait_ge(self.sems[key], val); e.waited[key] = val

    def _deps(self, e, reads, writes, chain=False):
        deps = {}
        def add(kv):
            if kv is None: return
            k, v = kv
            if deps.get(k, 0) < v: deps[k] = v
        for r in reads: add(r.w)
        for w in writes:
            if w.w is not None and not (chain and w.w[0] == e.name):
                add(w.w)
            for k, v in w.r.items():
                if k == e.name: continue
                add((k, v))
        return deps

    def op(self, engname, fn, reads=(), writes=(), chain=False):
        if self.dead: return None
        e = self.eng[engname]
        for k, v in self._deps(e, reads, writes, chain).items(): self._wait(e, k, v)
        ins = fn(e.h); e.count += 1; ins.then_inc(e.sem, 1)
        for r in reads: r.r[e.name] = e.count
        for w in writes: w.w = (e.name, e.count); w.r = {}
        self.ninstr += 1
        return ins

    def dma(self, qname, out, in_, reads=(), writes=()):
        if self.dead: return None
        e = self.eng[qname]
        deps = self._deps(e, reads, writes)
        slot = self.dma_sems[self.dma_i % self.NDMA]; self.dma_i += 1
        key = slot[0]
        if slot[1] > 0: deps[key] = max(deps.get(key, 0), slot[1])
        for k, v in deps.items(): self._wait(e, k, v)
        ins = e.h.dma_start(out=out, in_=in_)
        slot[1] += 16; ins.then_inc(self.sems[key], 16)
        for r in reads: r.r[key] = slot[1]
        for w in writes: w.w = (key, slot[1]); w.r = {}
        self.ninstr += 1
        return ins

    def barrier(self):
        if self.dead: return
        targets = {n: e.count for n, e in self.eng.items() if e.count > 0}
        for key, cnt in self.dma_sems:
            if cnt > 0: targets[key] = cnt
        for n, e in self.eng.items():
            for k, v in targets.items(): self._wait(e, k, v)

    def finish(self):
        self.dead = False
        self.barrier()


def bl(ap, n):
    return ap.unsqueeze(2).to_broadcast([ap.shape[0], ap.shape[1], n])


def bm(ap, n):
    return ap.unsqueeze(1).to_broadcast([ap.shape[0], n, ap.shape[1]])


def _rep(a, n=128):
    return np.ascontiguousarray(np.broadcast_to(a[None], (n,) + a.shape))


def prep_shared(inp):
    f = lambda a: np.ascontiguousarray(a, dtype=np.float32)
    L = DEPTH
    S = {}
    for k in ("w_ada", "w_in", "w_glu", "w_br_ssm", "w_br_four", "w_out", "w_up", "w_down"):
        S[k] = f(inp[k])
    S["w_br_attn"] = f(inp["w_br_attn"])
    S["b_adaP"] = f(inp["b_ada"].reshape(L, 48, 128).transpose(2, 0, 1))
    S["bgbc"] = f(np.stack([np.stack([_rep(inp["b_ada"][l, 2048:3072]), _rep(inp["b_ada"][l, 5120:6144])], 1) for l in range(L)]))
    S["lnbc"] = f(np.stack([np.stack([_rep(inp[k][l]) for k in ("ln1_g", "ln1_b", "ln2_g", "ln2_b")]) for l in range(L)]))
    cw = inp["conv_w"].reshape(L, 3, 44, 128).transpose(3, 0, 2, 1)
    cb = inp["conv_b"].reshape(L, 44, 128).transpose(2, 0, 1)[..., None]
    S["convp"] = f(np.concatenate([cw, cb], -1))
    S["bglu"] = f(inp["b_glu"].reshape(L, 4, 128).transpose(2, 0, 1))
    S["g640"] = f(np.stack([_rep(np.concatenate([np.tile(inp["q_norm_g"][l], 8), np.tile(inp["k_norm_g"][l], 4)])) for l in range(L)]))
    def lay(a):
        aP = a.reshape(L, 2, 16, 2, 64).transpose(0, 1, 3, 4, 2).reshape(L, 2, 128, 16)
        aF = a.reshape(L, 2, 4, 4, 2, 64).transpose(0, 1, 3, 2, 4, 5)
        aF = np.broadcast_to(aF[:, :, :, None], (L, 2, 4, 32, 4, 2, 64)).reshape(L, 2, 128, 512)
        return np.concatenate([aP, aF], -1)
    ldt = np.broadcast_to(inp["ssm_log_dt"][..., None], (L, 2, 32, 64))
    S["lamin"] = f(np.stack([lay(inp["ssm_a_re"]), lay(inp["ssm_a_im"]), lay(ldt)], 2))
    def bbdP(b):
        Bp = b.reshape(L, 2, 16, 2, 64, 16)
        out = np.zeros((L, 2, 2, 64, 16, 2, 16), np.float32)
        for g2 in range(2):
            out[:, :, g2, :, :, g2, :] = Bp[:, :, :, g2].transpose(0, 1, 3, 2, 4)
        return out.reshape(L, 2, 128, 16, 32)
    def bbdT(b):
        Bq = b.reshape(L, 2, 4, 4, 2, 64, 16)
        out = np.zeros((L, 2, 4, 2, 16, 4, 2, 64), np.float32)
        for g2 in range(2):
            out[:, :, :, g2, :, :, g2, :] = Bq[:, :, :, :, g2].transpose(0, 1, 3, 5, 2, 4)
        return out.reshape(L, 2, 128, 4, 128)
    def cbdP(c):
        Cp = c.reshape(L, 2, 16, 2, 16, 64)
        out = np.zeros((L, 2, 2, 64, 16, 2, 16), np.float32)
        for g2 in range(2):
            out[:, :, g2, :, :, g2, :] = Cp[:, :, :, g2].transpose(0, 1, 4, 2, 3)
        return out.reshape(L, 2, 128, 16, 32)
    S["BbdP"] = f(np.stack([bbdP(inp["ssm_b_re"]), bbdP(inp["ssm_b_im"])], 2))
    S["BbdT"] = f(np.stack([bbdT(inp["ssm_b_re"]), bbdT(inp["ssm_b_im"])], 2))
    S["CbdP"] = f(np.stack([cbdP(inp["ssm_c_re"]), cbdP(inp["ssm_c_im"])], 2))
    Dblk = np.zeros((L, 128, 4, 128), np.float32)
    for l in range(L):
        for q in range(4):
            Dblk[l, np.arange(128), q, np.arange(128)] = inp["ssm_d"][l, q * 128:(q + 1) * 128]
    S["Dblk"] = Dblk
    S["ident"] = np.eye(128, dtype=np.float32)
    mk = np.zeros((128, 4), np.float32); mk[np.arange(128), np.arange(128) // 32] = 1.0
    S["maskP"] = mk
    t = np.arange(1024)
    freqs = (10000.0 ** (-np.arange(16, dtype=np.float32) / 16)).astype(np.float32)
    ang = np.concatenate([(t // 64).astype(np.float32)[:, None] * freqs, (t % 64).astype(np.float32)[:, None] * freqs], -1)
    S["ropec"] = f(np.cos(ang).reshape(8, 128, 32).transpose(1, 0, 2))
    S["ropes"] = f(np.sin(ang).reshape(8, 128, 32).transpose(1, 0, 2))
    def dft(n):
        i = np.arange(n, dtype=np.int64)
        m = (i[:, None] * i[None, :]) % n
        a = 2.0 * np.pi * m.astype(np.float64) / n
        return np.cos(a) / np.sqrt(n), np.sin(a) / np.sqrt(n)
    c128, s128 = dft(128)
    S["cs128"] = f(np.concatenate([c128, s128], 1))
    for n in (1024, 256):
        c, s = dft(n)
        S["cl%d" % n] = f(c.reshape(n // 128, 128, n).transpose(1, 0, 2))
        S["sl%d" % n] = f((-s).reshape(n // 128, 128, n).transpose(1, 0, 2))
    return S


def prep_core(inp, cid):
    f = lambda a: np.ascontiguousarray(a, dtype=np.float32)
    m = {}
    m["xin"] = f(np.concatenate([inp["x_sample"][cid], inp["x_prompt"][2 * cid:2 * cid + 2].reshape(512, 1024)], 0))
    cv = np.stack([inp["c"][cid], inp["c_ctx"]], -1)
    m["cvec"] = f(cv.reshape(8, 128, 2).transpose(1, 0, 2))
    m["ckT"] = f(inp["cache_k"][cid].reshape(DEPTH, 512, 128).transpose(0, 2, 1))
    m["cv"] = f(inp["cache_v"][cid].reshape(DEPTH, 512, 128))
    st = inp["state_ssm"][cid].reshape(DEPTH, 2, 16, 2, 64, 2)
    m["h0"] = f(st.transpose(3, 4, 0, 1, 5, 2).reshape(128, DEPTH, 2, 2, 16))
    return m


IN_SHAPES = {
    "xin": (1536, 1024), "cvec": (128, 8, 2), "ckT": (2, 128, 512), "cv": (2, 512, 128), "h0": (128, 2, 2, 2, 16),
    "w_ada": (2, 1024, 6144), "w_in": (2, 1024, 4864), "w_glu": (2, 512, 512), "w_br_attn": (2, 512, 1024),
    "w_br_ssm": (2, 512, 1024), "w_br_four": (2, 512, 1024), "w_out": (2, 1024, 1024), "w_up": (2, 1024, 5632),
    "w_down": (2, 2816, 1024), "b_adaP": (128, 2, 48), "bgbc": (2, 128, 2, 1024), "lnbc": (2, 4, 128, 1024),
    "convp": (128, 2, 44, 4), "bglu": (128, 2, 4), "g640": (2, 128, 768), "lamin": (2, 2, 3, 128, 528),
    "BbdP": (2, 2, 2, 128, 16, 32), "BbdT": (2, 2, 2, 128, 4, 128), "CbdP": (2, 2, 2, 128, 16, 32),
    "Dblk": (2, 128, 4, 128), "ident": (128, 128), "maskP": (128, 4), "ropec": (128, 8, 32), "ropes": (128, 8, 32),
    "cs128": (128, 256), "cl1024": (128, 8, 1024), "sl1024": (128, 8, 1024), "cl256": (128, 2, 256), "sl256": (128, 2, 256),
}
OUT_SHAPES = {"y": (1536, 1024), "nk": (2, 2, 256, 128), "nv": (2, 2, 256, 128), "nst": (2, 2, 2, 32, 64, 2)}


def build(n_layers=DEPTH, stop=None, dbg_shapes=None):
    nc = bass.Bass("TRN2", target_bir_lowering=False)
    Din = {k: nc.dram_tensor(k, list(s), F32, kind="ExternalInput").ap() for k, s in IN_SHAPES.items()}
    Dout = {k: nc.dram_tensor(k, list(s), F32, kind="ExternalOutput").ap() for k, s in OUT_SHAPES.items()}
    Ddbg = {k: nc.dram_tensor(k, list(s), F32, kind="ExternalOutput").ap() for k, s in (dbg_shapes or {}).items()}
    es = contextlib.ExitStack()
    with es:
        fw = FW(nc, es)
        uid = [0]
        def sb(st, name, shape, dt=F32):
            uid[0] += 1
            return st.enter_context(nc.sbuf_tensor("%s_%d" % (name, uid[0]), list(shape), dt))
        OP = fw.op
        CH = [True]
        def MM(out, lhsT, rhs, start, stop_, r, w, tp=None, chain=None):
            if chain is None: chain = CH[0]
            if tp is None:
                return fw.op("pe", lambda e: e.matmul(out, lhsT=lhsT, rhs=rhs, start=start, stop=stop_), r, w, chain=chain)
            return fw.op("pe", lambda e: e.matmul(out, lhsT=lhsT, rhs=rhs, start=start, stop=stop_, tile_position=tp), r, w, chain=chain)
        def TT(eng, out, in0, in1, op, r, w):
            return fw.op(eng, lambda e: e.tensor_tensor(out=out, in0=in0, in1=in1, op=op), r, w)
        def TS(eng, out, in0, s1, op0, r, w, s2=None, op1=None):
            if op1 is None:
                return fw.op(eng, lambda e: e.tensor_scalar(out=out, in0=in0, scalar1=s1, scalar2=None, op0=op0), r, w)
            return fw.op(eng, lambda e: e.tensor_scalar(out=out, in0=in0, scalar1=s1, scalar2=s2, op0=op0, op1=op1), r, w)
        def ACT(out, in_, func, r, w, scale=1.0, bias=0.0):
            return fw.op("act", lambda e: e.activation(out=out, in_=in_, func=func, bias=bias, scale=scale), r, w)
        def CP(eng, out, in_, r, w):
            if eng == "act":
                return fw.op("act", lambda e: e.copy(out=out, in_=in_), r, w)
            return fw.op(eng, lambda e: e.tensor_copy(out=out, in_=in_), r, w)
        Rdbg = Res("dbg")
        def DBG(name, ap, r):
            if name in Ddbg:
                fw.dma("pool", Ddbg[name], ap, reads=r, writes=[Rdbg])

        X = sb(es, "X", [128, NT, 1024]); RX = [Res("X%d" % i) for i in range(NT)]
        ident_f = sb(es, "ident_f", [128, 128]); ident_b = sb(es, "ident_b", [128, 128], BF16); Rid = Res("id")
        maskP = sb(es, "maskP", [128, 4]); Rmask = Res("mask")
        modP = sb(es, "modP", [128, 6, 8, 2]); opsc = sb(es, "opsc", [128, 2, 8, 2]); Rmod = Res("mod")
        csl = sb(es, "csl", [128, 8, 2]); sTb = sb(es, "sTb", [128, 8, 2], BF16); sRep = sb(es, "sRep", [128, 2, 8, 128], BF16)
        Rcs = Res("cs")
        stat = sb(es, "stat", [128, NT, 2, 6]); mv = sb(es, "mv", [128, NT, 2]); rs = sb(es, "rs", [128, NT, 2]); Rstat = Res("stat")
        ps = es.enter_context(nc.psum_tensor("ps", [128, 3584], F32)); PB = [Res("pb%d" % i) for i in range(7)]
        pst = es.enter_context(nc.psum_tensor("pst", [128, 1024], BF16)); PT = Res("pt")
        wslot = [None, None]; Rws = [None, None]
        wctr = [0]
        def alloc_ws(st):
            for i_ in range(2):
                wslot[i_] = sb(st, "wslot%d" % i_, [128, 4096], BF16); Rws[i_] = Res("ws%d" % i_)
        def load_w(views):
            s_ = wctr[0] % 2; wctr[0] += 1
            for dstf, src in views:
                fw.dma("pool", dstf(wslot[s_]), src, writes=[Rws[s_]])
            return wslot[s_], Rws[s_]
        def bank(i, n=1):
            return ps[:, i * 512:(i + n) * 512]

        fw.dma("sp", ident_f[:], Din["ident"], writes=[Rid])
        fw.dma("pool", ident_b[:], Din["ident"], writes=[Rid])
        fw.dma("sp", maskP[:], Din["maskP"], writes=[Rmask])
        for i in range(NT):
            fw.dma("sp", X[:, i, :], Din["xin"][i * 128:(i + 1) * 128, :], writes=[RX[i]])
        fw.dma("sp", csl[:], Din["cvec"], writes=[Rcs])
        ACT(csl[:], csl[:], AF.Silu, [Rcs], [Rcs])
        CP("dve", sTb[:], csl[:], [Rcs], [Rcs])
        for v in range(2):
            CP("dve", sRep[:, v, :, :], bl(csl[:, :, v], 128), [Rcs], [Rcs])

        def build_uT(uT, RuT, sub):
            sh_sec = 0 if sub == 0 else 3
            for i in range(NT):
                v = TILE_VEC[i]
                for kk in range(2):
                    bk = kk
                    for k4 in range(4):
                        k = kk * 4 + k4
                        OP("pe", lambda e: e.transpose(bank(bk)[:, k4 * 128:(k4 + 1) * 128], X[:, i, k * 128:(k + 1) * 128], ident_f[:]),
                           [RX[i], Rid], [PB[bk]])
                    for k4 in range(4):
                        k = kk * 4 + k4
                        ACT(uT[:, k, i * 128:(i + 1) * 128], bank(bk)[:, k4 * 128:(k4 + 1) * 128], AF.Identity,
                            [PB[bk], Rmod], [RuT[i]], scale=opsc[:, sub, k, v:v + 1], bias=modP[:, sh_sec, k, v:v + 1])

        def layer_norm_residual(l, which, kc, lhsT_fn, lres_fn, wsrc3):
            with contextlib.ExitStack() as ph:
                ch_save = CH[0]; CH[0] = 'L' in os.environ.get('CHP', 'mabfFGML')
                alloc_ws(ph)
                lnb = sb(ph, "lnb", [128, 2, 1024]); Rln = Res("lnb")
                tmp = [sb(ph, "lntmp%d" % j, [128, 512]) for j in range(2)]; Rtmp = [Res("lntmp0"), Res("lntmp1")]
                Wd = sb(ph, "Wd", [128, kc, 512], BF16); RWd = Res("Wd")
                fw.dma("sp", lnb[:, 0, :], Din["lnbc"][l, 2 * which], writes=[Rln])
                fw.dma("sp", lnb[:, 1, :], Din["lnbc"][l, 2 * which + 1], writes=[Rln])
                gbc = sb(ph, "gbc", [128, 2, 1024]); Rgbc = Res("gbc")
                bg = sb(ph, "bg", [128, 1024]); Rbg = Res("bg")
                fw.dma("sp", bg[:], Din["bgbc"][l, :, which, :], writes=[Rbg])
                sec = 2 if which == 0 else 5
                wsrc = Din["w_ada"][l].rearrange("(k p) n -> p k n", p=128)
                for half in range(2):
                    Wg, RWg = load_w([(lambda t: t[:, 0:4096].rearrange("p (k n) -> p k n", k=8), wsrc[:, :, sec * 1024 + half * 512:sec * 1024 + (half + 1) * 512])])
                    Wgv = Wg[:, 0:4096].rearrange("p (k n) -> p k n", k=8)
                    for v in range(2):
                        for k in range(8):
                            MM(bank(v), sRep[:, v, k, :], Wgv[:, k, :], k == 0, k == 7, [RWg, Rcs], [PB[v]])
                        TT("dve", gbc[:, v, half * 512:(half + 1) * 512], bank(v), bg[:, half * 512:(half + 1) * 512], ALU.add, [PB[v], Rbg], [Rgbc])
                cnt = 0
                for nb in range(2):
                    cs = slice(nb * 512, (nb + 1) * 512)
                    for k0 in range(0, kc, 8):
                        k1 = min(kc, k0 + 8)
                        fw.dma("pool", Wd[:, k0:k1, :], wsrc3[:, k0:k1, cs], writes=[RWd])
                    for i in range(NT):
                        v = TILE_VEC[i]
                        bk = 2 + (cnt % 4); tj = cnt % 2; cnt += 1
                        for k in range(kc):
                            MM(bank(bk), lhsT_fn(i, k), Wd[:, k, :], k == 0, k == kc - 1, lres_fn(i, k) + [RWd], [PB[bk]])
                        TT("dve", tmp[tj][:], bank(bk), gbc[:, v, cs], ALU.mult, [PB[bk], Rgbc], [Rtmp[tj]])
                        OP("dve", lambda e: e.scalar_tensor_tensor(out=X[:, i, cs], in0=X[:, i, cs], scalar=float(ALPHA), in1=tmp[tj][:],
                                                                   op0=ALU.mult, op1=ALU.add), [RX[i], Rtmp[tj]], [RX[i]])
                for i in range(NT):
                    for hh in range(2):
                        OP("dve", lambda e: e.bn_stats(out=stat[:, i, hh, :], in_=X[:, i, hh * 512:(hh + 1) * 512]), [RX[i]], [Rstat])
                    OP("dve", lambda e: e.bn_aggr(out=mv[:, i, :], in_=stat[:, i, :, :].rearrange("p a b -> p (a b)")), [Rstat], [Rstat])
                TS("dve", rs[:, :, 0], mv[:, :, 1], EPS, ALU.add, [Rstat], [Rstat])
                ACT(rs[:, :, 0], rs[:, :, 0], AF.Sqrt, [Rstat], [Rstat])
                OP("dve", lambda e: e.reciprocal(out=rs[:, :, 0], in_=rs[:, :, 0]), [Rstat], [Rstat])
                OP("dve", lambda e: e.scalar_tensor_tensor(out=rs[:, :, 1], in0=mv[:, :, 0], scalar=-1.0, in1=rs[:, :, 0],
                                                           op0=ALU.mult, op1=ALU.mult), [Rstat], [Rstat])
                for i in range(NT):
                    ACT(X[:, i, :], X[:, i, :], AF.Identity, [RX[i], Rstat], [RX[i]], scale=rs[:, i, 0:1], bias=rs[:, i, 1:2])
                    TT("pool", X[:, i, :], X[:, i, :], lnb[:, 0, :], ALU.mult, [RX[i], Rln], [RX[i]])
                    TT("pool", X[:, i, :], X[:, i, :], lnb[:, 1, :], ALU.add, [RX[i], Rln], [RX[i]])
                fw.barrier()
                CH[0] = ch_save

        def layers():
          for l in range(n_layers):
              with contextlib.ExitStack() as ph:
                  wa = [sb(ph, "wa%d" % i, [128, 8, 512], BF16) for i in range(3)]; Rwa = [Res("wa%d" % i) for i in range(3)]
                  bP = sb(ph, "bP", [128, 48]); Rb = Res("bmod")
                  fw.dma("sp", bP[:], Din["b_adaP"][:, l, :], writes=[Rb])
                  wsrc = Din["w_ada"][l].rearrange("(k p) n -> p k n", p=128)
                  pm = bank(6)[:, 0:96]
                  CH[0] = 'm' in os.environ.get('CHP', 'mabfFGML')
                  for cb in range(12):
                      s = cb % 3
                      sec, half = cb // 2, cb % 2
                      if sec in (2, 5):
                          continue
                      fw.dma("pool", wa[s][:], wsrc[:, :, cb * 512:(cb + 1) * 512], writes=[Rwa[s]])
                      for m in range(4):
                          col = (sec * 8 + half * 4 + m) * 2
                          for k in range(8):
                              MM(pm[:, col:col + 2], wa[s][:, k, m * 128:(m + 1) * 128], sTb[:, k, :], k == 0, k == 7,
                                 [Rwa[s], Rcs], [PB[6]])
                  for sec in (0, 1, 3, 4):
                      TT("dve", modP[:, sec, :, :], pm[:, sec * 16:(sec + 1) * 16].rearrange("p (c v) -> p c v", v=2),
                         bl(bP[:, sec * 8:(sec + 1) * 8], 2), ALU.add, [PB[6], Rb], [Rmod])
                  TS("dve", opsc[:, 0, :, :], modP[:, 1, :, :], 1.0, ALU.add, [Rmod], [Rmod])
                  TS("dve", opsc[:, 1, :, :], modP[:, 4, :, :], 1.0, ALU.add, [Rmod], [Rmod])
                  fw.barrier()
              if stop == "mod": fw.dead = True
              DBG("d_modP", modP[:].rearrange("p a b c -> p (a b c)"), [Rmod])

              with contextlib.ExitStack() as L1:
                  QT = sb(L1, "QT", [128, 4, TOK], BF16); RQ = [Res("QT%d" % j) for j in range(4)]; attnT = QT; RattnT = RQ
                  sfT = sb(L1, "sfT", [128, 8, TOK], BF16); RsfT = [Res("sfT%d" % j) for j in range(8)]
                  with contextlib.ExitStack() as L2:
                      KT = sb(L2, "KT", [128, 2, 2048], BF16); RKT = Res("KT")
                      Vsb = sb(L2, "Vsb", [128, 16, 2, 2, 128], BF16); RV = Res("V")
                      with contextlib.ExitStack() as ph:
                          alloc_ws(ph)
                          uT = sb(ph, "uT", [128, 8, TOK], BF16); RuT = [Res("uT%d" % i) for i in range(NT)]
                          qn = [sb(ph, "qn%d" % j, [128, 512]) for j in range(2)]; Rqn = [Res("qn0"), Res("qn1")]
                          vf = [sb(ph, "vf%d" % j, [128, 128]) for j in range(2)]; Rvf = [Res("vf0"), Res("vf1")]
                          qkb = [sb(ph, "qkb%d" % j, [128, 512], BF16) for j in range(2)]; Rqkb = [Res("qkb0"), Res("qkb1")]
                          sq = sb(ph, "sq", [128, 512]); Rsq = Res("sq")
                          kb2 = [sb(ph, "kb2%d" % j, [128, 128], BF16) for j in range(2)]; Rkb2 = [Res("kb20"), Res("kb21")]
                          ssq = sb(ph, "ssq", [128, 8]); Rssq = Res("ssq")
                          g640 = sb(ph, "g640", [128, 768]); rc = sb(ph, "rc", [128, 8, 32]); rsn = sb(ph, "rsn", [128, 8, 32]); Rcst = Res("cst")
                          rt = [sb(ph, "rt%d" % j, [128, 8, 32]) for j in range(2)]; Rrt = [Res("rt0"), Res("rt1")]
                          win = Din["w_in"][l].rearrange("(k p) n -> p k n", p=128)
                          fw.dma("sp", g640[:], Din["g640"][l], writes=[Rcst])
                          fw.dma("sp", rc[:], Din["ropec"], writes=[Rcst])
                          fw.dma("sp", rsn[:], Din["ropes"], writes=[Rcst])
                          fw.dma("pool", KT[:, 0, 1024:1536], Din["ckT"][l], writes=[RKT])
                          fw.dma("pool", KT[0:64, 1, 1024:1536], Din["ckT"][l, 64:128, :], writes=[RKT])
                          fw.dma("pool", KT[64:128, 1, 1024:1536], Din["ckT"][l, 0:64, :], writes=[RKT])
                          OP("dve", lambda e: e.memset(Vsb[:].rearrange("p a b c d -> p (a b c d)"), 1.0), [], [RV])
                          cvv = Din["cv"][l].rearrange("(t p) (h d) -> p t h d", p=128, h=2)
                          for lay in range(2):
                              for kvh_ in range(2):
                                  fw.dma("pool", Vsb[:, 8:12, kvh_, lay, lay * 64:(lay + 1) * 64], cvv[:, :, kvh_, :], writes=[RV])
                          build_uT(uT, RuT, 0)
                          if stop == "ut": fw.dead = True
                          DBG("d_uT", uT[:].rearrange("p a b -> p (a b)"), RuT)

                          def norm_rope(i, pa, pr, nh, gap, outb, Routb, slot):
                              w_ = nh * 64
                              ACT(sq[:, 0:w_], pa, AF.Square, pr, [Rsq])
                              OP("dve", lambda e: e.reduce_sum(out=ssq[:, 0:nh], in_=sq[:, 0:w_].rearrange("p (h d) -> p h d", d=64), axis=AX.X), [Rsq], [Rssq])
                              TS("dve", ssq[:, 0:nh], ssq[:, 0:nh], 1.0 / 64, ALU.mult, [Rssq], [Rssq], s2=EPS, op1=ALU.add)
                              ACT(ssq[:, 0:nh], ssq[:, 0:nh], AF.Sqrt, [Rssq], [Rssq])
                              OP("dve", lambda e: e.reciprocal(out=ssq[:, 0:nh], in_=ssq[:, 0:nh]), [Rssq], [Rssq])
                              qv = qn[slot][:, 0:w_]
                              TT("dve", qv.rearrange("p (h d) -> p h d", d=64), pa.rearrange("p (h d) -> p h d", d=64), bl(ssq[:, 0:nh], 64),
                                 ALU.mult, pr + [Rssq], [Rqn[slot]])
                              TT("dve", qv, qv, gap, ALU.mult, [Rqn[slot], Rcst], [Rqn[slot]])
                              if i >= 8:
                                  CP("act", outb, qv, [Rqn[slot]], [Routb])
                              else:
                                  x4 = qv.rearrange("p (h d two) -> p h d two", d=32, two=2)
                                  o4 = outb.rearrange("p (h d two) -> p h d two", d=32, two=2)
                                  x0, x1 = x4[:, :, :, 0], x4[:, :, :, 1]
                                  cc, ss_ = bm(rc[:, i, :], nh), bm(rsn[:, i, :], nh)
                                  r0, r1 = rt[0][:, 0:nh, :], rt[1][:, 0:nh, :]
                                  TT("dve", r0, x0, cc, ALU.mult, [Rqn[slot], Rcst], [Rrt[0]])
                                  TT("pool", r1, x1, ss_, ALU.mult, [Rqn[slot], Rcst], [Rrt[1]])
                                  TT("dve", o4[:, :, :, 0], r0, r1, ALU.subtract, Rrt, [Routb])
                                  TT("dve", r0, x0, ss_, ALU.mult, [Rqn[slot], Rcst], [Rrt[0]])
                                  TT("pool", r1, x1, cc, ALU.mult, [Rqn[slot], Rcst], [Rrt[1]])
                                  TT("dve", o4[:, :, :, 1], r0, r1, ALU.add, Rrt, [Routb])

                          CH[0] = 'a' in os.environ.get('CHP', 'mabfFGML')
                          W, RW = load_w([(lambda t: t[:, 0:4096].rearrange("p (k n) -> p k n", k=8), win[:, :, 0:512])])
                          Wv = W[:, 0:4096].rearrange("p (k n) -> p k n", k=8)
                          for i in range(NT):
                              bk = 2 + (i % 2); sl = i % 2
                              for k in range(8):
                                  MM(bank(bk), uT[:, k, i * 128:(i + 1) * 128], Wv[:, k, :], k == 0, k == 7, [RuT[i], RW], [PB[bk]])
                              norm_rope(i, bank(bk), [PB[bk]], 8, g640[:, 0:512], qkb[sl][:, :], Rqkb[sl], sl)
                              for j in range(4):
                                  src = qkb[sl][:, j * 128:(j + 1) * 128]
                                  OP("pe", lambda e: e.transpose(pst[:, j * 128:(j + 1) * 128], src, ident_b[:]), [Rqkb[sl], Rid], [PT])
                              CP("act", QT[:, :, i * 128:(i + 1) * 128], pst[:, 0:512].rearrange("p (j c) -> p j c", j=4), [PT], RQ)
                          if stop == "passA": fw.dead = True
                          CH[0] = 'b' in os.environ.get('CHP', 'mabfFGML')
                          W, RW = load_w([(lambda t: t[:, 0:2048].rearrange("p (k n) -> p k n", k=8), win[:, :, 512:768])])
                          Wv = W[:, 0:2048].rearrange("p (k n) -> p k n", k=8)
                          for i in range(NT):
                              bk = 2 + (i % 2); sl = i % 2
                              for k in range(8):
                                  MM(bank(bk)[:, 0:256], uT[:, k, i * 128:(i + 1) * 128], Wv[:, k, :], k == 0, k == 7, [RuT[i], RW], [PB[bk]])
                              norm_rope(i, bank(bk)[:, 0:256], [PB[bk]], 4, g640[:, 512:768], qkb[sl][:, 0:256], Rqkb[sl], sl)
                              kt = i if i < 8 else 12 + (i - 8)
                              for lay in range(2):
                                  CP("act", Vsb[:, kt, :, lay, lay * 64:(lay + 1) * 64], bank(bk)[:, 128:256].rearrange("p (h d) -> p h d", h=2), [PB[bk]], [RV])
                              if i >= 8:
                                  p_, t_ = (i - 8) // 2, (i - 8) % 2
                                  CP("act", vf[sl][:], bank(bk)[:, 128:256], [PB[bk]], [Rvf[sl]])
                                  fw.dma("sp", Dout["nk"][p_, l, t_ * 128:(t_ + 1) * 128, :], qn[sl][:, 0:128], reads=[Rqn[sl]], writes=[Rdbg])
                                  fw.dma("sp", Dout["nv"][p_, l, t_ * 128:(t_ + 1) * 128, :], vf[sl][:], reads=[Rvf[sl]], writes=[Rdbg])
                              CP("act", kb2[sl][:, 0:64], qkb[sl][:, 64:128], [Rqkb[sl]], [Rkb2[sl]])
                              CP("act", kb2[sl][:, 64:128], qkb[sl][:, 0:64], [Rqkb[sl]], [Rkb2[sl]])
                              OP("pe", lambda e: e.transpose(pst[:, 512:640], qkb[sl][:, 0:128], ident_b[:]), [Rqkb[sl], Rid], [PT])
                              OP("pe", lambda e: e.transpose(pst[:, 640:768], kb2[sl][:, :], ident_b[:]), [Rkb2[sl], Rid], [PT])
                              kc = i * 128 if i < 8 else 1536 + (i - 8) * 128
                              CP("act", KT[:, 0, kc:kc + 128], pst[:, 512:640], [PT], [RKT])
                              CP("act", KT[:, 1, kc:kc + 128], pst[:, 640:768], [PT], [RKT])
                          if stop == "passB": fw.dead = True
                          CH[0] = 'f' in os.environ.get('CHP', 'mabfFGML')
                          for wb in range(2):
                              W, RW = load_w([(lambda t: t[:, 0:4096].rearrange("p (k n) -> p k n", k=8), win[:, :, 768 + wb * 512:1280 + wb * 512])])
                              Wv = W[:, 0:4096].rearrange("p (k n) -> p k n", k=8)
                              for m4 in range(4):
                                  m = wb * 4 + m4
                                  for tb, (c0, cn) in enumerate(BLKS):
                                      bk = (0, 1, 6)[(m * 3 + tb) % 3]
                                      for k in range(8):
                                          MM(bank(bk), Wv[:, k, m4 * 128:(m4 + 1) * 128], uT[:, k, c0:c0 + cn], k == 0, k == 7,
                                             RuT[c0 // 128:(c0 + cn) // 128] + [RW], [PB[bk]])
                                      CP("act", sfT[:, m, c0:c0 + cn], bank(bk), [PB[bk]], [RsfT[m]])
                          DBG("d_sfT", sfT[:].rearrange("p a b -> p (a b)"), RsfT)
                          fw.barrier()
                      if stop == "inproj": fw.dead = True
                      DBG("d_QT", QT[:].rearrange("p a b -> p (a b)"), RQ)
                      DBG("d_KT", KT[:].rearrange("p a b -> p (a b)"), [RKT])
                      CH[0] = 't' in os.environ.get('CHP', 'mabfFGML')
                      with contextlib.ExitStack() as ph:
                          Eb = [sb(ph, "Eb%d" % j, [128, 512], BF16) for j in range(3)]; REb = [Res("Eb%d" % j) for j in range(3)]
                          rsb = [sb(ph, "rsb%d" % j, [128, 512]) for j in range(2)]; Rrsb = [Res("rsb0"), Res("rsb1")]
                          cnt = 0; cnt2 = 0
                          for si, (t0, Ls) in enumerate(SEQS):
                              if si == 0:
                                  qblks = [(0, 512), (512, 512)]; nkt = 12; kt0 = 0; kc0 = 0
                              else:
                                  qblks = [(t0, 256)]; nkt = 2; kt0 = 12 + 2 * (si - 1); kc0 = 1536 + 256 * (si - 1)
                              for h in range(8):
                                  j = h // 2; hh = h % 2; kvh = h // 4; var = 0 if kvh == hh else 1
                                  pv, sm = (slice(0, 64), slice(64, 128)) if hh == 0 else (slice(64, 128), slice(0, 64))
                                  for (q0, qn) in qblks:
                                      pvb = 3 + (cnt2 % 2); cnt2 += 1
                                      for kt in range(nkt):
                                          sbk = cnt % 3; eb = cnt % 3; cnt += 1
                                          MM(bank(sbk)[:, 0:qn], KT[hh * 64:(hh + 1) * 64, var, kc0 + kt * 128:kc0 + (kt + 1) * 128],
                                             QT[hh * 64:(hh + 1) * 64, j, q0:q0 + qn], True, True, [RKT, RQ[j]], [PB[sbk]], chain=('s' in os.environ.get('CHT', '')))
                                          ACT(Eb[eb][:, 0:qn], bank(sbk)[:, 0:qn], AF.Exp, [PB[sbk]], [REb[eb]], scale=0.125)
                                          MM(bank(pvb)[:, 0:qn], Vsb[:, kt0 + kt, kvh, hh, :], Eb[eb][:, 0:qn], kt == 0, kt == nkt - 1,
                                             [RV, REb[eb]], [PB[pvb]], chain=('p' in os.environ.get('CHT', '')))
                                      rb = cnt2 % 2
                                      ACT(rsb[rb][sm, 0:qn], bank(pvb)[sm, 0:qn], AF.Ln, [PB[pvb]], [Rrsb[rb]])
                                      ACT(rsb[rb][sm, 0:qn], rsb[rb][sm, 0:qn], AF.Exp, [Rrsb[rb]], [Rrsb[rb]], scale=-1.0)
                                      TT("dve", attnT[pv, j, q0:q0 + qn], bank(pvb)[pv, 0:qn], rsb[rb][sm, 0:qn], ALU.mult,
                                         [PB[pvb], Rrsb[rb]], [RattnT[j]])
                          fw.barrier()
                  DBG("d_attnT", attnT[:].rearrange("p a b -> p (a b)"), RattnT)
                  CH[0] = 'F' in os.environ.get('CHP', 'mabfFGML')
                  with contextlib.ExitStack() as L3:
                    fourT = sb(L3, "fourT", [128, 4, TOK], BF16); RfourT = [Res("fourT%d" % j) for j in range(4)]
                    with contextlib.ExitStack() as ph:
                        cs128 = sb(ph, "cs128", [128, 256], BF16); cl = sb(ph, "cl", [128, 8, 1024], BF16); sl_ = sb(ph, "sl", [128, 8, 1024], BF16)
                        clp = sb(ph, "clp", [128, 2, 256], BF16); slp = sb(ph, "slp", [128, 2, 256], BF16); Rft = Res("ftab")
                        Pfm = sb(ph, "Pfm", [128, NT, 1024], BF16); RPfm = [Res("Pfm%d" % i) for i in range(NT)]
                        fw.dma("pool", cs128[:], Din["cs128"], writes=[Rft])
                        for k in range(8):
                            fw.dma("pool", cl[:, k, :], Din["cl1024"][:, k, :], writes=[Rft])
                            fw.dma("pool", sl_[:, k, :], Din["sl1024"][:, k, :], writes=[Rft])
                        fw.dma("pool", clp[:], Din["cl256"], writes=[Rft])
                        fw.dma("pool", slp[:], Din["sl256"], writes=[Rft])
                        for i in range(NT):
                            b0 = 2 + 2 * (i % 2)
                            for g in range(4):
                                MM(bank(b0 + g // 2)[:, (g % 2) * 256:(g % 2 + 1) * 256], sfT[:, 4 + g, i * 128:(i + 1) * 128], cs128[:], True, True,
                                   [RsfT[4 + g], Rft], [PB[b0 + g // 2]])
                            CP("act", Pfm[:, i, :], bank(b0, 2), [PB[b0], PB[b0 + 1]], [RPfm[i]])
                        cnt = 0
                        for si, (t0, Ls) in enumerate(SEQS):
                            ntl = Ls // 128; tb0 = t0 // 128
                            tc_, ts_ = (cl, sl_) if si == 0 else (clp, slp)
                            for g in range(4):
                                for lb in range(0, Ls, 512):
                                    n = min(512, Ls - lb)
                                    bk = (0, 1, 6)[cnt % 3]; cnt += 1
                                    for k in range(ntl):
                                        MM(bank(bk)[:, 0:n], Pfm[:, tb0 + k, g * 256:g * 256 + 128], tc_[:, k, lb:lb + n], k == 0, False,
                                           [RPfm[tb0 + k], Rft], [PB[bk]])
                                        MM(bank(bk)[:, 0:n], Pfm[:, tb0 + k, g * 256 + 128:g * 256 + 256], ts_[:, k, lb:lb + n], False, k == ntl - 1,
                                           [RPfm[tb0 + k], Rft], [PB[bk]])
                                    CP("act", fourT[:, g, t0 + lb:t0 + lb + n], bank(bk)[:, 0:n], [PB[bk]], [RfourT[g]])
                        fw.barrier()
                    DBG("d_fourT", fourT[:].rearrange("p a b -> p (a b)"), RfourT)
                    if stop == "four": fw.dead = True
                    CH[0] = 'S' in os.environ.get('CHP', 'mabfFGML')
                    ssmT = sb(L3, "ssmT", [128, 4, TOK], BF16); RssmT = [Res("ssmT%d" % j) for j in range(4)]
                    with contextlib.ExitStack() as S:
                        PowP = [sb(S, "PowP%d" % r, [128, 2, 16, 17]) for r in range(2)]
                        LH = [sb(S, "LH%d" % r, [128, 2, 16, 7]) for r in range(2)]
                        fP = [sb(S, "fP%d" % r, [128, 2, 16]) for r in range(2)]
                        Hb = sb(S, "Hb", [128, 2, 2, 16, NS], BF16)
                        h0 = sb(S, "h0", [128, 2, 2, 16]); stg = sb(S, "stg", [128, 16, 2])
                        RS = Res("ssm"); RHb = Res("Hb"); Rstg = Res("stg")
                        fw.dma("sp", h0[:], Din["h0"][:, l], writes=[RS])
                        def cmul(dr, di, ar, ai, br, bi, m1, m2):
                            TT("dve", m1, ar, br, ALU.mult, [RS], [RS]); TT("dve", m2, ai, bi, ALU.mult, [RS], [RS])
                            TT("dve", dr, m1, m2, ALU.subtract, [RS], [RS])
                            TT("dve", m1, ar, bi, ALU.mult, [RS], [RS]); TT("dve", m2, ai, br, ALU.mult, [RS], [RS])
                            TT("dve", di, m1, m2, ALU.add, [RS], [RS])
                        for d in range(2):
                            with contextlib.ExitStack() as SD:
                                lam = [sb(SD, "lam%d" % r, [128, 528]) for r in range(2)]
                                ff = [sb(SD, "ff%d" % r, [128, 528]) for r in range(2)]
                                A = [sb(SD, "A%d" % r, [128, 16, NS]) for r in range(2)]
                                with contextlib.ExitStack() as SP:
                                    are, aim, ldt, mag, cc, ss, t1, t2, t3 = [sb(SP, "sp%d" % j, [128, 528]) for j in range(9)]
                                    fw.dma("sp", are[:], Din["lamin"][l, d, 0], writes=[RS])
                                    fw.dma("sp", aim[:], Din["lamin"][l, d, 1], writes=[RS])
                                    fw.dma("sp", ldt[:], Din["lamin"][l, d, 2], writes=[RS])
                                    ACT(ldt[:], ldt[:], AF.Exp, [RS], [RS])
                                    TT("dve", t1[:], are[:], ldt[:], ALU.mult, [RS], [RS])
                                    ACT(mag[:], t1[:], AF.Exp, [RS], [RS])
                                    TT("dve", t2[:], aim[:], ldt[:], ALU.mult, [RS], [RS])
                                    ACT(ss[:], t2[:], AF.Sin, [RS], [RS], scale=1.0 / 16)
                                    ACT(t1[:], t2[:], AF.Sin, [RS], [RS], scale=1.0 / 32)
                                    TT("dve", t1[:], t1[:], t1[:], ALU.mult, [RS], [RS])
                                    TS("dve", cc[:], t1[:], -2.0, ALU.mult, [RS], [RS], s2=1.0, op1=ALU.add)
                                    for _ in range(4):
                                        TT("dve", t1[:], cc[:], cc[:], ALU.mult, [RS], [RS])
                                        TT("dve", t2[:], ss[:], ss[:], ALU.mult, [RS], [RS])
                                        TT("dve", t3[:], cc[:], ss[:], ALU.mult, [RS], [RS])
                                        TT("dve", cc[:], t1[:], t2[:], ALU.subtract, [RS], [RS])
                                        TS("dve", ss[:], t3[:], 2.0, ALU.mult, [RS], [RS])
                                    TT("dve", lam[0][:], mag[:], cc[:], ALU.mult, [RS], [RS])
                                    TT("dve", lam[1][:], mag[:], ss[:], ALU.mult, [RS], [RS])
                                    TT("dve", t1[:], are[:], are[:], ALU.mult, [RS], [RS])
                                    TT("dve", t2[:], aim[:], aim[:], ALU.mult, [RS], [RS])
                                    TT("dve", t1[:], t1[:], t2[:], ALU.add, [RS], [RS])
                                    OP("dve", lambda e: e.reciprocal(out=t1[:], in_=t1[:]), [RS], [RS])
                                    TS("dve", t2[:], lam[0][:], -1.0, ALU.add, [RS], [RS])
                                    TT("dve", t3[:], t2[:], are[:], ALU.mult, [RS], [RS])
                                    TT("dve", cc[:], lam[1][:], aim[:], ALU.mult, [RS], [RS])
                                    TT("dve", t3[:], t3[:], cc[:], ALU.add, [RS], [RS])
                                    TT("dve", ff[0][:], t3[:], t1[:], ALU.mult, [RS], [RS])
                                    TT("dve", t3[:], lam[1][:], are[:], ALU.mult, [RS], [RS])
                                    TT("dve", cc[:], t2[:], aim[:], ALU.mult, [RS], [RS])
                                    TT("dve", t3[:], t3[:], cc[:], ALU.subtract, [RS], [RS])
                                    TT("dve", ff[1][:], t3[:], t1[:], ALU.mult, [RS], [RS])
                                    pm1 = t1[:, 0:256].rearrange("p (g k) -> p g k", g=16); pm2 = t2[:, 0:256].rearrange("p (g k) -> p g k", g=16)
                                    OP("dve", lambda e: e.memset(PowP[0][:, d, :, 0:1], 1.0), [RS], [RS])
                                    OP("dve", lambda e: e.memset(PowP[1][:, d, :, 0:1], 0.0), [RS], [RS])
                                    for r in range(2):
                                        ACT(PowP[r][:, d, :, 1], lam[r][:, 0:16], AF.Identity, [RS], [RS])
                                        ACT(fP[r][:, d, :], ff[r][:, 0:16], AF.Identity, [RS], [RS])
                                    for w in (1, 2, 4, 8):
                                        br_ = PowP[0][:, d, :, w:w + 1].to_broadcast([128, 16, w]); bi_ = PowP[1][:, d, :, w:w + 1].to_broadcast([128, 16, w])
                                        cmul(PowP[0][:, d, :, w + 1:2 * w + 1], PowP[1][:, d, :, w + 1:2 * w + 1],
                                             PowP[0][:, d, :, 1:w + 1], PowP[1][:, d, :, 1:w + 1], br_, bi_, pm1[:, :, 0:w], pm2[:, :, 0:w])
                                    for r in range(2):
                                        ACT(LH[r][:, d, :, 0], PowP[r][:, d, :, 16], AF.Identity, [RS], [RS])
                                    for m in range(6):
                                        cmul(LH[0][:, d, :, m + 1], LH[1][:, d, :, m + 1], LH[0][:, d, :, m], LH[1][:, d, :, m],
                                             LH[0][:, d, :, m], LH[1][:, d, :, m], t1[:, 256:272], t2[:, 256:272])
                                for r in range(2):
                                    OP("dve", lambda e: e.memset(A[r][:].rearrange("p a b -> p (a b)"), 0.0), [RS], [RS])
                                with contextlib.ExitStack() as SW:
                                    W = [sb(SW, "W%d" % r, [128, 16, 128]) for r in range(2)]
                                    WT = [sb(SW, "WT%d" % r, [128, 16, 128], BF16) for r in range(2)]
                                    wm = [sb(SW, "wm%d" % r, [128, 8, 128]) for r in range(3)]
                                    BT = [sb(SW, "BT%d" % r, [128, 128]) for r in range(2)]
                                    Lw = [sb(SW, "Lw%d" % r, [128, 128]) for r in range(2)]
                                    RWT = Res("WT")
                                    for q in range(4):
                                        fs = slice(16 + q * 128, 16 + (q + 1) * 128)
                                        for r in range(2):
                                            fw.dma("sp", BT[r][:], Din["BbdT"][l, d, r, :, q, :], writes=[RS])
                                            ACT(Lw[r][:], lam[r][:, fs], AF.Identity, [RS], [RS])
                                        cmul(W[0][:, 0, :], W[1][:, 0, :], ff[0][:, fs], ff[1][:, fs], BT[0][:], BT[1][:], wm[0][:, 0, :], wm[1][:, 0, :])
                                        for w in (1, 2, 4, 8):
                                            cmul(W[0][:, w:2 * w, :], W[1][:, w:2 * w, :], bm(Lw[0][:], w), bm(Lw[1][:], w),
                                                 W[0][:, 0:w, :], W[1][:, 0:w, :], wm[0][:, 0:w, :], wm[1][:, 0:w, :])
                                            if w < 8:
                                                TT("dve", wm[0][:, 0, :], Lw[0][:], Lw[0][:], ALU.mult, [RS], [RS])
                                                TT("dve", wm[1][:, 0, :], Lw[1][:], Lw[1][:], ALU.mult, [RS], [RS])
                                                TT("dve", wm[2][:, 0, :], Lw[0][:], Lw[1][:], ALU.mult, [RS], [RS])
                                                TT("dve", Lw[0][:], wm[0][:, 0, :], wm[1][:, 0, :], ALU.subtract, [RS], [RS])
                                                TS("dve", Lw[1][:], wm[2][:, 0, :], 2.0, ALU.mult, [RS], [RS])
                                        for r in range(2):
                                            ACT(WT[r][:].rearrange("p a b -> p (a b)"), W[r][:].rearrange("p a b -> p (a b)"), AF.Identity, [RS], [RWT])
                                        Sps = bank(0, 2).rearrange("p (b r c) -> p b r c", b=4, r=2)
                                        for b in range(4):
                                            sv = sfT[32 * b:32 * b + 32, q, :].rearrange("p (c j) -> p c j", j=16)
                                            for r in range(2):
                                                for k in range(16):
                                                    j = 15 - k if d == 0 else k
                                                    MM(Sps[:, b, r, 0:96], WT[r][32 * b:32 * b + 32, k, :], sv[:, :, j], k == 0, k == 15,
                                                       [RWT, RsfT[q]], [PB[0], PB[1]], tp=(32 * b, 0))
                                        for r in range(2):
                                            for s_i in range(3):
                                                n = SCH[s_i]; c0 = (0, 64, 80)[s_i]; lo = SOFF[s_i] + (1 if d == 0 else 0)
                                                ACT(A[r][:, 4 * q:4 * q + 4, lo:lo + n], Sps[:, :, r, c0:c0 + n], AF.Identity, [PB[0], PB[1]], [RS])
                                for r in range(2):
                                    ACT(A[r][:, :, 0 if d == 0 else 64], h0[:, d, r, :], AF.Identity, [RS], [RS])
                                with contextlib.ExitStack() as SC:
                                    sm = [sb(SC, "sm%d" % r, [128, 16, NS]) for r in range(3)]
                                    for m in range(7):
                                        dd = 1 << m
                                        groups = []
                                        if dd < 65:
                                            groups.append((lambda t_, lo, hi: t_[:, :, lo:hi], 65, None))
                                        if dd < 17:
                                            groups.append((lambda t_, lo, hi: t_[:, :, 65:99].rearrange("p g (s c) -> p g s c", s=2)[:, :, :, lo:hi], 17, 2))
                                        for view, n, ns in groups:
                                            cntn = n - dd
                                            (dlo, dhi, slo, shi) = (dd, n, 0, cntn) if d == 0 else (0, cntn, dd, n)
                                            if ns is None:
                                                Lr = bl(LH[0][:, d, :, m], cntn); Li = bl(LH[1][:, d, :, m], cntn)
                                                tv = lambda t_: t_[:, :, 0:cntn]
                                            else:
                                                Lr = LH[0][:, d, :, m].unsqueeze(2).unsqueeze(3).to_broadcast([128, 16, 2, cntn])
                                                Li = LH[1][:, d, :, m].unsqueeze(2).unsqueeze(3).to_broadcast([128, 16, 2, cntn])
                                                tv = lambda t_: t_[:, :, 0:2 * cntn].rearrange("p g (s c) -> p g s c", s=2)
                                            sr, si = view(A[0], slo, shi), view(A[1], slo, shi)
                                            dr, di = view(A[0], dlo, dhi), view(A[1], dlo, dhi)
                                            m1, m2, m3 = tv(sm[0]), tv(sm[1]), tv(sm[2])
                                            TT("dve", m1, sr, Lr, ALU.mult, [RS], [RS]); TT("dve", m2, si, Li, ALU.mult, [RS], [RS])
                                            TT("dve", m1, m1, m2, ALU.subtract, [RS], [RS])
                                            TT("dve", m2, si, Lr, ALU.mult, [RS], [RS]); TT("dve", m3, sr, Li, ALU.mult, [RS], [RS])
                                            TT("dve", m2, m2, m3, ALU.add, [RS], [RS])
                                            TT("dve", dr, dr, m1, ALU.add, [RS], [RS]); TT("dve", di, di, m2, ALU.add, [RS], [RS])
                                for r in range(2):
                                    ACT(Hb[:, d, r, :, :], A[r][:], AF.Identity, [RS], [RHb])
                                for s_i in (1, 2):
                                    idx = SOFF[s_i] + (16 if d == 0 else 0)
                                    for r in range(2):
                                        ACT(stg[:, :, r], A[r][:, :, idx], AF.Identity, [RS], [Rstg])
                                    fw.dma("sp", Dout["nst"][s_i - 1, l, d].rearrange("(pair g2) p r -> (g2 p) pair r", g2=2), stg[:], reads=[Rstg], writes=[Rdbg])
                                DBG("d_A%d" % d, A[0][:].rearrange("p a b -> p (a b)"), [RS])
                                fw.barrier()
                        with contextlib.ExitStack() as SQ:
                            CL = [[sb(SQ, "CL%d%d" % (d, r), [128, 4, 17, 32], BF16) for r in range(2)] for d in range(2)]
                            Kblk = [sb(SQ, "Kblk%d" % d, [128, 16, 128], BF16) for d in range(2)]
                            Bb = [[sb(SQ, "Bb%d%d" % (d, r), [128, 4, 32], BF16) for r in range(2)] for d in range(2)]
                            Dbl = sb(SQ, "Dbl", [128, 4, 128], BF16)
                            cm = [sb(SQ, "cm%d" % j, [128, 2, 17, 32]) for j in range(2)]
                            Cl = [sb(SQ, "Cl%d" % r, [128, 4, 32]) for r in range(2)]
                            Bl = [sb(SQ, "Bl%d" % r, [128, 4, 32]) for r in range(2)]
                            bt = [sb(SQ, "bt%d" % r, [128, 4, 32]) for r in range(2)]
                            gt = [sb(SQ, "gt%d" % r, [128, 512]) for r in range(2)]; Rgt = [Res("gt0"), Res("gt1")]
                            RCL = Res("CL"); RK = Res("Kblk")
                            ydb = sb(SQ, "ydb", [128, 512]); Rydb = Res("ydb")
                            fw.dma("pool", Dbl[:], Din["Dblk"][l], writes=[RCL])
                            for q in range(4):
                                ps4 = slice(4 * q, 4 * q + 4)
                                for d in range(2):
                                    for r in range(2):
                                        fw.dma("sp", Cl[r][:], Din["CbdP"][l, d, r, :, ps4, :], writes=[RS])
                                        fw.dma("sp", Bl[r][:], Din["BbdP"][l, d, r, :, ps4, :], writes=[RS])
                                    for hp in range(2):
                                        pp = slice(2 * hp, 2 * hp + 2); pq = slice(4 * q + 2 * hp, 4 * q + 2 * hp + 2)
                                        Pr = PowP[0][:, d, pq, :].unsqueeze(3).to_broadcast([128, 2, 17, 32])
                                        Pi = PowP[1][:, d, pq, :].unsqueeze(3).to_broadcast([128, 2, 17, 32])
                                        Cr = Cl[0][:, pp, :].unsqueeze(2).to_broadcast([128, 2, 17, 32])
                                        Ci = Cl[1][:, pp, :].unsqueeze(2).to_broadcast([128, 2, 17, 32])
                                        TT("dve", cm[0][:], Pr, Cr, ALU.mult, [RS], [RS]); TT("dve", cm[1][:], Pi, Ci, ALU.mult, [RS], [RS])
                                        TT("dve", CL[d][0][:, pp, :, :], cm[0][:], cm[1][:], ALU.subtract, [RS], [RCL])
                                        TT("dve", cm[0][:], Pr, Ci, ALU.mult, [RS], [RS]); TT("dve", cm[1][:], Pi, Cr, ALU.mult, [RS], [RS])
                                        TT("dve", cm[0][:], cm[0][:], cm[1][:], ALU.add, [RS], [RS])
                                        TS("dve", CL[d][1][:, pp, :, :], cm[0][:], -1.0, ALU.mult, [RS], [RCL])
                                    fr = bl(fP[0][:, d, ps4], 32); fi = bl(fP[1][:, d, ps4], 32)
                                    TT("dve", bt[0][:], fr, Bl[0][:], ALU.mult, [RS], [RS]); TT("dve", bt[1][:], fi, Bl[1][:], ALU.mult, [RS], [RS])
                                    TT("dve", Bb[d][0][:], bt[0][:], bt[1][:], ALU.subtract, [RS], [RCL])
                                    TT("dve", bt[0][:], fr, Bl[1][:], ALU.mult, [RS], [RS]); TT("dve", bt[1][:], fi, Bl[0][:], ALU.mult, [RS], [RS])
                                    TT("dve", Bb[d][1][:], bt[0][:], bt[1][:], ALU.add, [RS], [RCL])
                                    kb = 6
                                    for b in range(4):
                                        MM(bank(kb)[32 * b:32 * b + 32, :], Bb[d][0][:, b, :], CL[d][0][:, b, 0:16, :].rearrange("p t n -> p (t n)"), True, False,
                                           [RCL], [PB[kb]], tp=(0, 32 * b))
                                        MM(bank(kb)[32 * b:32 * b + 32, :], Bb[d][1][:, b, :], CL[d][1][:, b, 0:16, :].rearrange("p t n -> p (t n)"), False, True,
                                           [RCL], [PB[kb]], tp=(0, 32 * b))
                                    for cb in range(4):
                                        TS("dve", Kblk[d][:, :, 32 * cb:32 * cb + 32], bank(kb).rearrange("p (t n) -> p t n", n=32), maskP[:, cb:cb + 1], ALU.mult,
                                           [PB[kb], Rmask], [RK])
                                for tb, (c0, cn) in enumerate(BLKS):
                                    yb = 3 + tb
                                    Yb = bank(yb).rearrange("p (c j) -> p c j", j=16)
                                    sblk = sfT[:, q, c0:c0 + cn].rearrange("p (c j) -> p c j", j=16)
                                    MM(bank(yb), Dbl[:, q, :], sfT[:, q, c0:c0 + cn], True, False, [RCL, RsfT[q]], [PB[yb]])
                                    for d in range(2):
                                        for t in range(16):
                                            if d == 0:
                                                o_, r_ = Yb[:, :, t:16], sblk[:, :, 0:16 - t]
                                            else:
                                                o_, r_ = Yb[:, :, 0:16 - t], sblk[:, :, t:16]
                                            MM(o_, Kblk[d][:, t, :], r_, False, False, [RK, RsfT[q]], [PB[yb]])
                                    for d in range(2):
                                        for b in range(4):
                                            for j in range(16):
                                                tt = j + 1 if d == 0 else 16 - j
                                                sh = 0 if d == 0 else 1
                                                for r in range(2):
                                                    if tb < 2:
                                                        rhs = Hb[:, d, r, 4 * q + b, 32 * tb + sh:32 * tb + sh + 32]
                                                        o_ = Yb[32 * b:32 * b + 32, :, j]
                                                    else:
                                                        rhs = Hb[:, d, r, 4 * q + b, 65:99].rearrange("p (s c) -> p s c", s=2)[:, :, sh:sh + 16]
                                                        o_ = bank(yb).rearrange("p (s c j) -> p s c j", s=2, j=16)[32 * b:32 * b + 32, :, :, j]
                                                    last = (d == 1 and b == 3 and j == 15 and r == 1)
                                                    MM(o_, CL[d][r][:, b, tt, :], rhs, False, last, [RCL, RHb], [PB[yb]], tp=(0, 32 * b))
                                    if ("d_y%d_%d" % (q, tb)) in Ddbg:
                                        ACT(ydb[:], bank(yb), AF.Identity, [PB[yb]], [Rydb])
                                        DBG("d_y%d_%d" % (q, tb), ydb[:], [Rydb])
                                    g_ = tb % 2
                                    ACT(gt[g_][:], bank(yb), AF.Square, [PB[yb]], [Rgt[g_]])
                                    TS("dve", gt[g_][:], gt[g_][:], 0.044715, ALU.mult, [Rgt[g_]], [Rgt[g_]], s2=1.0, op1=ALU.add)
                                    TT("dve", gt[g_][:], gt[g_][:], bank(yb), ALU.mult, [Rgt[g_], PB[yb]], [Rgt[g_]])
                                    ACT(gt[g_][:], gt[g_][:], AF.Sigmoid, [Rgt[g_]], [Rgt[g_]], scale=1.5957691216)
                                    TT("dve", sfT[:, q, c0:c0 + cn], gt[g_][:], bank(yb), ALU.mult, [Rgt[g_], PB[yb]], [RsfT[q]])
                            fw.barrier()
                    CH[0] = 'G' in os.environ.get('CHP', 'mabfFGML')
                    with contextlib.ExitStack() as ph:
                        alloc_ws(ph)
                        bgl = sb(ph, "bgl", [128, 4]); Rbgl = Res("bgl"); sg = [sb(ph, "sg%d" % j, [128, 512]) for j in range(2)]; Rsg = [Res("sg0"), Res("sg1")]
                        fw.dma("sp", bgl[:], Din["bglu"][:, l, :], writes=[Rbgl])
                        Wg, RWg = load_w([(lambda t: t[:, 0:2048].rearrange("p (k n) -> p k n", k=4), Din["w_glu"][l].rearrange("(k p) n -> p k n", p=128))])
                        Wgv = Wg[:, 0:2048].rearrange("p (k n) -> p k n", k=4)
                        cnt = 0
                        for m in range(4):
                            for tb, (c0, cn) in enumerate(BLKS):
                                bk = (0, 1, 2)[cnt % 3]; sj = cnt % 2; cnt += 1
                                for k in range(4):
                                    MM(bank(bk), Wgv[:, k, m * 128:(m + 1) * 128], sfT[:, k, c0:c0 + cn], k == 0, k == 3, [RWg, RsfT[k]], [PB[bk]])
                                ACT(sg[sj][:], bank(bk), AF.Sigmoid, [PB[bk], Rbgl], [Rsg[sj]], bias=bgl[:, m:m + 1])
                                TT("dve", ssmT[:, m, c0:c0 + cn], sg[sj][:], sfT[:, m, c0:c0 + cn], ALU.mult, [Rsg[sj], RsfT[m]], [RssmT[m]])
                        fw.barrier()
                    DBG("d_gy", sfT[:, 0:4, :].rearrange("p a b -> p (a b)"), RsfT[0:4])
                    DBG("d_ssmT", ssmT[:].rearrange("p a b -> p (a b)"), RssmT)
                    if stop == "ssm": fw.dead = True
                    CH[0] = 'M' in os.environ.get('CHP', 'mabfFGML')
                    with contextlib.ExitStack() as M:
                        mergedT = sb(M, "mergedT", [128, 8, TOK], BF16); Rmg = [Res("mg%d" % j) for j in range(8)]
                        with contextlib.ExitStack() as ph:
                            alloc_ws(ph)
                            uT = sb(ph, "uT", [128, 8, TOK], BF16); RuT = [Res("uT%d" % i) for i in range(NT)]
                            gs = [sb(ph, "gs%d" % j, [128, 512]) for j in range(3)]; Rgs = [Res("gs%d" % j) for j in range(3)]
                            ac = [sb(ph, "ac%d" % j, [128, 512]) for j in range(2)]; Rac = [Res("ac0"), Res("ac1")]
                            build_uT(uT, RuT, 0)
                            win = Din["w_in"][l].rearrange("(k p) n -> p k n", p=128)
                            wbr = [Din[nm][l].rearrange("(k p) n -> p k n", p=128) for nm in ("w_br_attn", "w_br_ssm", "w_br_four")]
                            srcs = [(attnT, RattnT), (ssmT, RssmT), (fourT, RfourT)]
                            for c in range(8):
                                Wg, RWg = load_w([((lambda t, g=g: t[:, g * 1024:(g + 1) * 1024].rearrange("p (k n) -> p k n", k=8)),
                                                   win[:, :, 1792 + g * 1024 + c * 128:1792 + g * 1024 + (c + 1) * 128]) for g in range(3)])
                                Wb, RWb = load_w([((lambda t, x=x: t[:, x * 512:(x + 1) * 512].rearrange("p (k n) -> p k n", k=4)),
                                                   wbr[x][:, :, c * 128:(c + 1) * 128]) for x in range(3)])
                                for tb, (c0, cn) in enumerate(BLKS):
                                    for g in range(3):
                                        Wgv = Wg[:, g * 1024:(g + 1) * 1024].rearrange("p (k n) -> p k n", k=8)
                                        for k in range(8):
                                            MM(bank(g), Wgv[:, k, :], uT[:, k, c0:c0 + cn], k == 0, k == 7, RuT[c0 // 128:(c0 + cn) // 128] + [RWg], [PB[g]])
                                        ACT(gs[g][:], bank(g), AF.Sigmoid, [PB[g]], [Rgs[g]])
                                    for x in range(3):
                                        Wbv = Wb[:, x * 512:(x + 1) * 512].rearrange("p (k n) -> p k n", k=4)
                                        st_, Rst = srcs[x]
                                        for k in range(4):
                                            MM(bank(3 + x), Wbv[:, k, :], st_[:, k, c0:c0 + cn], k == 0, k == 3, [Rst[k], RWb], [PB[3 + x]])
                                    TT("dve", ac[0][:], gs[0][:], bank(3), ALU.mult, [Rgs[0], PB[3]], [Rac[0]])
                                    TT("dve", ac[1][:], gs[1][:], bank(4), ALU.mult, [Rgs[1], PB[4]], [Rac[1]])
                                    TT("dve", ac[0][:], ac[0][:], ac[1][:], ALU.add, Rac, [Rac[0]])
                                    TT("dve", ac[1][:], gs[2][:], bank(5), ALU.mult, [Rgs[2], PB[5]], [Rac[1]])
                                    TT("dve", mergedT[:, c, c0:c0 + cn], ac[0][:], ac[1][:], ALU.add, Rac, [Rmg[c]])
                            fw.barrier()
                        DBG("d_mergedT", mergedT[:].rearrange("p a b -> p (a b)"), Rmg)
                        if stop == "merge": fw.dead = True
                        layer_norm_residual(l, 0, 8, lambda i, k: mergedT[:, k, i * 128:(i + 1) * 128], lambda i, k: [Rmg[k]],
                                            Din["w_out"][l].rearrange("(k p) n -> p k n", p=128))
              if "d_x1" in Ddbg:
                  for i in range(NT):
                      fw.dma("sp", Ddbg["d_x1"][i * 128:(i + 1) * 128, :], X[:, i, :], reads=[RX[i]], writes=[Rdbg])
              if stop == "ln1": fw.dead = True
              CH[0] = 'U' in os.environ.get('CHP', 'mabfFGML')
              with contextlib.ExitStack() as Fs:
                  actT = sb(Fs, "actT", [128, 22, TOK], BF16); Ract = [Res("act%d" % j) for j in range(22)]
                  with contextlib.ExitStack() as ph:
                      alloc_ws(ph)
                      uT = sb(ph, "uT", [128, 8, TOK], BF16); RuT = [Res("uT%d" % i) for i in range(NT)]
                      cvp = sb(ph, "cvp", [128, 44, 4]); Rcv = Res("cvp")
                      hc = [sb(ph, "hc%d" % j, [128, TOK]) for j in range(2)]; Rhc = [Res("hc0"), Res("hc1")]
                      fw.dma("sp", cvp[:], Din["convp"][:, l], writes=[Rcv])
                      build_uT(uT, RuT, 1)
                      wup = Din["w_up"][l].rearrange("(k p) n -> p k n", p=128)
                      for ch in range(44):
                          if ch % 4 == 0:
                              W, RW = load_w([(lambda t: t[:, 0:4096].rearrange("p (k n) -> p k n", k=8), wup[:, :, ch * 128:ch * 128 + 512])])
                              Wv = W[:, 0:4096].rearrange("p (k n) -> p k n", k=8)
                          cw = ch % 4; pb0 = 3 * (ch % 2); hj = ch % 2
                          prs = [PB[pb0], PB[pb0 + 1], PB[pb0 + 2]]
                          for tb, (c0, cn) in enumerate(BLKS):
                              for k in range(8):
                                  MM(bank(pb0 + tb), Wv[:, k, cw * 128:(cw + 1) * 128], uT[:, k, c0:c0 + cn], k == 0, k == 7,
                                     RuT[c0 // 128:(c0 + cn) // 128] + [RW], [prs[tb]])
                          hp = bank(pb0, 3); h_ = hc[hj]
                          ACT(h_[:], hp, AF.Identity, prs + [Rcv], [Rhc[hj]], scale=cvp[:, ch, 1:2], bias=cvp[:, ch, 3:4])
                          def stt(o_, i0, sc, i1):
                              OP("dve", lambda e: e.scalar_tensor_tensor(out=o_, in0=i0, scalar=sc, in1=i1, op0=ALU.mult, op1=ALU.add),
                                 prs + [Rcv, Rhc[hj]], [Rhc[hj]])
                          stt(h_[:, 1:1024], hp[:, 0:1023], cvp[:, ch, 0:1], h_[:, 1:1024])
                          stt(h_[:, 0:1023], hp[:, 1:1024], cvp[:, ch, 2:3], h_[:, 0:1023])
                          h3 = h_[:, 1024:1536].rearrange("p (s c) -> p s c", s=2); p3 = hp[:, 1024:1536].rearrange("p (s c) -> p s c", s=2)
                          stt(h3[:, :, 1:256], p3[:, :, 0:255], cvp[:, ch, 0:1], h3[:, :, 1:256])
                          stt(h3[:, :, 0:255], p3[:, :, 1:256], cvp[:, ch, 2:3], h3[:, :, 0:255])
                          if ch < 22:
                              ACT(actT[:, ch, :], h_[:], AF.Silu, [Rhc[hj]], [Ract[ch]])
                          else:
                              TT("dve", actT[:, ch - 22, :], actT[:, ch - 22, :], h_[:], ALU.mult, [Ract[ch - 22], Rhc[hj]], [Ract[ch - 22]])
                      fw.barrier()
                  DBG("d_actT", actT[:].rearrange("p a b -> p (a b)"), Ract)
                  if stop == "ffn": fw.dead = True
                  layer_norm_residual(l, 1, 22, lambda i, k: actT[:, k, i * 128:(i + 1) * 128], lambda i, k: [Ract[k]],
                                      Din["w_down"][l].rearrange("(k p) n -> p k n", p=128))
        layers()
        fw.dead = False
        for i in range(NT):
            fw.dma("sp", Dout["y"][i * 128:(i + 1) * 128, :], X[:, i, :], reads=[RX[i]], writes=[Rdbg])
        fw.finish()
    return nc


def kernel(**inputs):
    inp = {k: np.asarray(v) for k, v in inputs.items()}
    S = prep_shared(inp)
    nc = build()
    in_maps = []
    for cid in range(8):
        m = dict(S); m.update(prep_core(inp, cid))
        in_maps.append(m)
    res = run_bass_kernel_spmd(nc, in_maps, core_ids=list(range(8)))
    y_p = np.zeros((16, 256, 1024), np.float32); y_s = np.zeros((8, 1024, 1024), np.float32)
    nk = np.zeros((16, 2, 256, 2, 64), np.float32); nv = np.zeros((16, 2, 256, 2, 64), np.float32)
    nst = np.zeros((16, 2, 2, 32, 64, 2), np.float32)
    for cid in range(8):
        r = res.results[cid]
        y_s[cid] = r["y"][0:1024]
        y_p[2 * cid:2 * cid + 2] = r["y"][1024:1536].reshape(2, 256, 1024)
        nk[2 * cid:2 * cid + 2] = r["nk"].reshape(2, 2, 256, 2, 64)
        nv[2 * cid:2 * cid + 2] = r["nv"].reshape(2, 2, 256, 2, 64)
        nst[2 * cid:2 * cid + 2] = r["nst"]
    return (y_p, y_s, nk, nv, nst)
```

```python
import contextlib, os
SK = os.environ.get("SK", "")
import numpy as np
import concourse.bass as bass
import concourse.mybir as mybir
from concourse.bass_utils import run_bass_kernel_spmd

F32 = mybir.dt.float32; BF16 = mybir.dt.bfloat16
AF = mybir.ActivationFunctionType; ALU = mybir.AluOpType; AX = mybir.AxisListType

D_MODEL = 1024; DEPTH = 2; D_IN = 4864; D_FF = 2816
ALPHA = (2 * DEPTH) ** 0.25
EPS = 1e-6
NT = 12; TOK = 1536; T = 16
SEQS = [(0, 1024), (1024, 256), (1280, 256)]
BLKS = [(0, 512), (512, 512), (1024, 512)]
TILE_VEC = [0] * 8 + [1] * 4
NS = 99
SOFF = [0, 65, 82]
SCH = [64, 16, 16]


class StopBuild(Exception):
    pass


class Res:
    __slots__ = ("name", "w", "r")
    def __init__(self, name):
        self.name = name; self.w = None; self.r = {}


class Eng:
    def __init__(self, name, h, sem):
        self.name = name; self.h = h; self.sem = sem; self.count = 0; self.waited = {}


class FW:
    NDMA = 32
    def __init__(self, nc, es):
        self.nc = nc; self.sems = {}; self.eng = {}
        for name, h in (("pe", nc.tensor), ("act", nc.scalar), ("dve", nc.vector), ("pool", nc.gpsimd), ("sp", nc.sync)):
            s = es.enter_context(nc.semaphore("s_" + name))
            self.sems[name] = s; self.eng[name] = Eng(name, h, s)
        self.dma_sems = []
        for i in range(self.NDMA):
            s = es.enter_context(nc.semaphore("s_dma%d" % i))
            self.sems["dma%d" % i] = s; self.dma_sems.append(["dma%d" % i, 0])
        self.dma_i = 0; self.ninstr = 0; self.dead = False; self.prev_plain = False

    def _wait(self, e, key, val):
        if e.waited.get(key, 0) >= val: return
        e.h.wait_ge(self.sems[key], val); e.waited[key] = val

    def _deps(self, e, reads, writes, chain=False):
        deps = {}
        def add(kv):
            if kv is None: return
            k, v = kv
            if deps.get(k, 0) < v: deps[k] = v
        for r in reads: add(r.w)
        for w in writes:
            if w.w is not None and not (chain and w.w[0] == e.name):
                add(w.w)
            for k, v in w.r.items():
                if k == e.name: continue
                add((k, v))
        return deps

    def op(self, engname, fn, reads=(), writes=(), chain=False):
        if self.dead: return None
        e = self.eng[engname]
        if engname == "pe":
            plain = chain
            chain = chain and self.prev_plain
            if plain and not chain and e.count > 0:
                self._wait(e, "pe", e.count)
            self.prev_plain = plain
        for k, v in self._deps(e, reads, writes, chain).items(): self._wait(e, k, v)
        ins = fn(e.h); e.count += 1; ins.then_inc(e.sem, 1)
        for r in reads: r.r[e.name] = e.count
        for w in writes: w.w = (e.name, e.count); w.r = {}
        self.ninstr += 1
        return ins

    def dma(self, qname, out, in_, reads=(), writes=()):
        if self.dead: return None
        e = self.eng[qname]
        deps = self._deps(e, reads, writes)
        slot = self.dma_sems[self.dma_i % self.NDMA]; self.dma_i += 1
        key = slot[0]
        if slot[1] > 0: deps[key] = max(deps.get(key, 0), slot[1])
        for k, v in deps.items(): self._wait(e, k, v)
        ins = e.h.dma_start(out=out, in_=in_)
        slot[1] += 16; ins.then_inc(self.sems[key], 16)
        for r in reads: r.r[key] = slot[1]
        for w in writes: w.w = (key, slot[1]); w.r = {}
        self.ninstr += 1
        return ins

    def barrier(self):
        if self.dead: return
        targets = {n: e.count for n, e in self.eng.items() if e.count > 0}
        for key, cnt in self.dma_sems:
            if cnt > 0: targets[key] = cnt
        for n, e in self.eng.items():
            for k, v in targets.items(): self._wait(e, k, v)

    def finish(self):
        self.dead = False
        self.barrier()


def bl(ap, n):
    return ap.unsqueeze(2).to_broadcast([ap.shape[0], ap.shape[1], n])


def bm(ap, n):
    return ap.unsqueeze(1).to_broadcast([ap.shape[0], n, ap.shape[1]])


def _rep(a, n=128):
    return np.ascontiguousarray(np.broadcast_to(a[None], (n,) + a.shape))


def prep_shared(inp):
    f = lambda a: np.ascontiguousarray(a, dtype=np.float32)
    L = DEPTH
    S = {}
    for k in ("w_ada", "w_in", "w_glu", "w_br_ssm", "w_br_four", "w_out", "w_up", "w_down"):
        S[k] = f(inp[k])
    S["w_br_attn"] = f(inp["w_br_attn"])
    S["b_adaP"] = f(inp["b_ada"].reshape(L, 48, 128).transpose(2, 0, 1))
    S["bgbc"] = f(np.stack([np.stack([_rep(inp["b_ada"][l, 2048:3072]), _rep(inp["b_ada"][l, 5120:6144])], 1) for l in range(L)]))
    S["lnbc"] = f(np.stack([np.stack([_rep(inp[k][l]) for k in ("ln1_g", "ln1_b", "ln2_g", "ln2_b")]) for l in range(L)]))
    cw = inp["conv_w"].reshape(L, 3, 44, 128).transpose(3, 0, 2, 1)
    cb = inp["conv_b"].reshape(L, 44, 128).transpose(2, 0, 1)[..., None]
    S["convp"] = f(np.concatenate([cw, cb], -1))
    S["bglu"] = f(inp["b_glu"].reshape(L, 4, 128).transpose(2, 0, 1))
    S["g640"] = f(np.stack([_rep(np.concatenate([np.tile(inp["q_norm_g"][l], 8), np.tile(inp["k_norm_g"][l], 4)])) for l in range(L)]))
    def lay(a):
        aP = a.reshape(L, 2, 16, 2, 64).transpose(0, 1, 3, 4, 2).reshape(L, 2, 128, 16)
        aF = a.reshape(L, 2, 4, 4, 2, 64).transpose(0, 1, 3, 2, 4, 5)
        aF = np.broadcast_to(aF[:, :, :, None], (L, 2, 4, 32, 4, 2, 64)).reshape(L, 2, 128, 512)
        return np.concatenate([aP, aF], -1)
    ldt = np.broadcast_to(inp["ssm_log_dt"][..., None], (L, 2, 32, 64))
    S["lamin"] = f(np.stack([lay(inp["ssm_a_re"]), lay(inp["ssm_a_im"]), lay(ldt)], 2))
    def bbdP(b):
        Bp = b.reshape(L, 2, 16, 2, 64, 16)
        out = np.zeros((L, 2, 2, 64, 16, 2, 16), np.float32)
        for g2 in range(2):
            out[:, :, g2, :, :, g2, :] = Bp[:, :, :, g2].transpose(0, 1, 3, 2, 4)
        return out.reshape(L, 2, 128, 16, 32)
    def bbdT(b):
        Bq = b.reshape(L, 2, 4, 4, 2, 64, 16)
        out = np.zeros((L, 2, 4, 2, 16, 4, 2, 64), np.float32)
        for g2 in range(2):
            out[:, :, :, g2, :, :, g2, :] = Bq[:, :, :, :, g2].transpose(0, 1, 3, 5, 2, 4)
        return out.reshape(L, 2, 128, 4, 128)
    def cbdP(c):
        Cp = c.reshape(L, 2, 16, 2, 16, 64)
        out = np.zeros((L, 2, 2, 64, 16, 2, 16), np.float32)
        for g2 in range(2):
            out[:, :, g2, :, :, g2, :] = Cp[:, :, :, g2].transpose(0, 1, 4, 2, 3)
        return out.reshape(L, 2, 128, 16, 32)
    S["BbdP"] = f(np.stack([bbdP(inp["ssm_b_re"]), bbdP(inp["ssm_b_im"])], 2))
    S["BbdT"] = f(np.stack([bbdT(inp["ssm_b_re"]), bbdT(inp["ssm_b_im"])], 2))
    S["CbdP"] = f(np.stack([cbdP(inp["ssm_c_re"]), cbdP(inp["ssm_c_im"])], 2))
    Dblk = np.zeros((L, 128, 4, 128), np.float32)
    for l in range(L):
        for q in range(4):
            Dblk[l, np.arange(128), q, np.arange(128)] = inp["ssm_d"][l, q * 128:(q + 1) * 128]
    S["Dblk"] = Dblk
    S["ident"] = np.eye(128, dtype=np.float32)
    mk = np.zeros((128, 4), np.float32); mk[np.arange(128), np.arange(128) // 32] = 1.0
    S["maskP"] = mk
    t = np.arange(1024)
    freqs = (10000.0 ** (-np.arange(16, dtype=np.float32) / 16)).astype(np.float32)
    ang = np.concatenate([(t // 64).astype(np.float32)[:, None] * freqs, (t % 64).astype(np.float32)[:, None] * freqs], -1)
    S["ropec"] = f(np.cos(ang).reshape(8, 128, 32).transpose(1, 0, 2))
    S["ropes"] = f(np.sin(ang).reshape(8, 128, 32).transpose(1, 0, 2))
    def dft(n):
        i = np.arange(n, dtype=np.int64)
        m = (i[:, None] * i[None, :]) % n
        a = 2.0 * np.pi * m.astype(np.float64) / n
        return np.cos(a) / np.sqrt(n), np.sin(a) / np.sqrt(n)
    c128, s128 = dft(128)
    S["cs128"] = f(np.concatenate([c128, s128], 1))
    for n in (1024, 256):
        c, s = dft(n)
        S["cl%d" % n] = f(c.reshape(n // 128, 128, n).transpose(1, 0, 2))
        S["sl%d" % n] = f((-s).reshape(n // 128, 128, n).transpose(1, 0, 2))
    return S


def prep_core(inp, cid):
    f = lambda a: np.ascontiguousarray(a, dtype=np.float32)
    m = {}
    m["xin"] = f(np.concatenate([inp["x_sample"][cid], inp["x_prompt"][2 * cid:2 * cid + 2].reshape(512, 1024)], 0))
    cv = np.stack([inp["c"][cid], inp["c_ctx"]], -1)
    m["cvec"] = f(cv.reshape(8, 128, 2).transpose(1, 0, 2))
    m["ckT"] = f(inp["cache_k"][cid].reshape(DEPTH, 512, 128).transpose(0, 2, 1))
    m["cv"] = f(inp["cache_v"][cid].reshape(DEPTH, 512, 128))
    st = inp["state_ssm"][cid].reshape(DEPTH, 2, 16, 2, 64, 2)
    m["h0"] = f(st.transpose(3, 4, 0, 1, 5, 2).reshape(128, DEPTH, 2, 2, 16))
    return m


IN_SHAPES = {
    "xin": (1536, 1024), "cvec": (128, 8, 2), "ckT": (2, 128, 512), "cv": (2, 512, 128), "h0": (128, 2, 2, 2, 16),
    "w_ada": (2, 1024, 6144), "w_in": (2, 1024, 4864), "w_glu": (2, 512, 512), "w_br_attn": (2, 512, 1024),
    "w_br_ssm": (2, 512, 1024), "w_br_four": (2, 512, 1024), "w_out": (2, 1024, 1024), "w_up": (2, 1024, 5632),
    "w_down": (2, 2816, 1024), "b_adaP": (128, 2, 48), "bgbc": (2, 128, 2, 1024), "lnbc": (2, 4, 128, 1024),
    "convp": (128, 2, 44, 4), "bglu": (128, 2, 4), "g640": (2, 128, 768), "lamin": (2, 2, 3, 128, 528),
    "BbdP": (2, 2, 2, 128, 16, 32), "BbdT": (2, 2, 2, 128, 4, 128), "CbdP": (2, 2, 2, 128, 16, 32),
    "Dblk": (2, 128, 4, 128), "ident": (128, 128), "maskP": (128, 4), "ropec": (128, 8, 32), "ropes": (128, 8, 32),
    "cs128": (128, 256), "cl1024": (128, 8, 1024), "sl1024": (128, 8, 1024), "cl256": (128, 2, 256), "sl256": (128, 2, 256),
}
OUT_SHAPES = {"y": (1536, 1024), "nk": (2, 2, 256, 128), "nv": (2, 2, 256, 128), "nst": (2, 2, 2, 32, 64, 2)}


def build(n_layers=DEPTH, stop=None, dbg_shapes=None):
    nc = bass.Bass("TRN2", target_bir_lowering=False)
    Din = {k: nc.dram_tensor(k, list(s), F32, kind="ExternalInput").ap() for k, s in IN_SHAPES.items()}
    Dout = {k: nc.dram_tensor(k, list(s), F32, kind="ExternalOutput").ap() for k, s in OUT_SHAPES.items()}
    Ddbg = {k: nc.dram_tensor(k, list(s), F32, kind="ExternalOutput").ap() for k, s in (dbg_shapes or {}).items()}
    es = contextlib.ExitStack()
    with es:
        fw = FW(nc, es)
        uid = [0]
        def sb(st, name, shape, dt=F32):
            uid[0] += 1
            return st.enter_context(nc.sbuf_tensor("%s_%d" % (name, uid[0]), list(shape), dt))
        OP = fw.op
        CH = [True]
        def MM(out, lhsT, rhs, start, stop_, r, w, tp=None, chain=None):
            if chain is None: chain = CH[0] and (tp is None)
            if tp is None:
                return fw.op("pe", lambda e: e.matmul(out, lhsT=lhsT, rhs=rhs, start=start, stop=stop_), r, w, chain=chain)
            return fw.op("pe", lambda e: e.matmul(out, lhsT=lhsT, rhs=rhs, start=start, stop=stop_, tile_position=tp), r, w, chain=chain)
        def TT(eng, out, in0, in1, op, r, w):
            return fw.op(eng, lambda e: e.tensor_tensor(out=out, in0=in0, in1=in1, op=op), r, w)
        def TS(eng, out, in0, s1, op0, r, w, s2=None, op1=None):
            if op1 is None:
                return fw.op(eng, lambda e: e.tensor_scalar(out=out, in0=in0, scalar1=s1, scalar2=None, op0=op0), r, w)
            return fw.op(eng, lambda e: e.tensor_scalar(out=out, in0=in0, scalar1=s1, scalar2=s2, op0=op0, op1=op1), r, w)
        def ACT(out, in_, func, r, w, scale=1.0, bias=0.0):
            return fw.op("act", lambda e: e.activation(out=out, in_=in_, func=func, bias=bias, scale=scale), r, w)
        def CP(eng, out, in_, r, w):
            if eng == "act":
                return fw.op("act", lambda e: e.copy(out=out, in_=in_), r, w)
            return fw.op(eng, lambda e: e.tensor_copy(out=out, in_=in_), r, w)
        Rdbg = Res("dbg")
        def DBG(name, ap, r):
            if name in Ddbg:
                fw.dma("pool", Ddbg[name], ap, reads=r, writes=[Rdbg])

        X = sb(es, "X", [128, NT, 1024]); RX = [Res("X%d" % i) for i in range(NT)]
        ident_f = sb(es, "ident_f", [128, 128]); ident_b = sb(es, "ident_b", [128, 128], BF16); Rid = Res("id")
        maskP = sb(es, "maskP", [128, 4]); Rmask = Res("mask")
        modP = sb(es, "modP", [128, 6, 8, 2]); opsc = sb(es, "opsc", [128, 2, 8, 2]); Rmod = Res("mod")
        csl = sb(es, "csl", [128, 8, 2]); sTb = sb(es, "sTb", [128, 8, 2], BF16); sRep = sb(es, "sRep", [128, 2, 8, 128], BF16)
        Rcs = Res("cs")
        stat = sb(es, "stat", [128, NT, 2, 6]); mv = sb(es, "mv", [128, NT, 2]); rs = sb(es, "rs", [128, NT, 2]); Rstat = Res("stat")
        ps = es.enter_context(nc.psum_tensor("ps", [128, 3584], F32)); PB = [Res("pb%d" % i) for i in range(7)]
        pst = es.enter_context(nc.psum_tensor("pst", [128, 1024], BF16)); PT = Res("pt")
        wslot = [None, None]; Rws = [None, None]
        wctr = [0]
        def alloc_ws(st):
            for i_ in range(2):
                wslot[i_] = sb(st, "wslot%d" % i_, [128, 4096], BF16); Rws[i_] = Res("ws%d" % i_)
        def load_w(views):
            s_ = wctr[0] % 2; wctr[0] += 1
            for dstf, src in views:
                fw.dma("pool", dstf(wslot[s_]), src, writes=[Rws[s_]])
            return wslot[s_], Rws[s_]
        def bank(i, n=1):
            return ps[:, i * 512:(i + n) * 512]

        fw.dma("sp", ident_f[:], Din["ident"], writes=[Rid])
        fw.dma("pool", ident_b[:], Din["ident"], writes=[Rid])
        fw.dma("sp", maskP[:], Din["maskP"], writes=[Rmask])
        for i in range(NT):
            fw.dma("sp", X[:, i, :], Din["xin"][i * 128:(i + 1) * 128, :], writes=[RX[i]])
        fw.dma("sp", csl[:], Din["cvec"], writes=[Rcs])
        ACT(csl[:], csl[:], AF.Silu, [Rcs], [Rcs])
        CP("dve", sTb[:], csl[:], [Rcs], [Rcs])
        for v in range(2):
            CP("dve", sRep[:, v, :, :], bl(csl[:, :, v], 128), [Rcs], [Rcs])

        def build_uT(uT, RuT, sub):
            sh_sec = 0 if sub == 0 else 3
            for i in range(NT):
                v = TILE_VEC[i]
                for kk in range(2):
                    bk = kk
                    for k4 in range(4):
                        k = kk * 4 + k4
                        OP("pe", lambda e: e.transpose(bank(bk)[:, k4 * 128:(k4 + 1) * 128], X[:, i, k * 128:(k + 1) * 128], ident_f[:]),
                           [RX[i], Rid], [PB[bk]])
                    for k4 in range(4):
                        k = kk * 4 + k4
                        ACT(uT[:, k, i * 128:(i + 1) * 128], bank(bk)[:, k4 * 128:(k4 + 1) * 128], AF.Identity,
                            [PB[bk], Rmod], [RuT[i]], scale=opsc[:, sub, k, v:v + 1], bias=modP[:, sh_sec, k, v:v + 1])

        def layer_norm_residual(l, which, kc, lhsT_fn, lres_fn, wsrc3):
            with contextlib.ExitStack() as ph:
                ch_save = CH[0]; CH[0] = 'L' in os.environ.get('CHP', 'mabfFGMLUS')
                alloc_ws(ph)
                lnb = sb(ph, "lnb", [128, 2, 1024]); Rln = Res("lnb")
                tmp = [sb(ph, "lntmp%d" % j, [128, 512]) for j in range(2)]; Rtmp = [Res("lntmp0"), Res("lntmp1")]
                Wd = sb(ph, "Wd", [128, kc, 512], BF16); RWd = Res("Wd")
                fw.dma("sp", lnb[:, 0, :], Din["lnbc"][l, 2 * which], writes=[Rln])
                fw.dma("sp", lnb[:, 1, :], Din["lnbc"][l, 2 * which + 1], writes=[Rln])
                gbc = sb(ph, "gbc", [128, 2, 1024]); Rgbc = Res("gbc")
                bg = sb(ph, "bg", [128, 1024]); Rbg = Res("bg")
                fw.dma("sp", bg[:], Din["bgbc"][l, :, which, :], writes=[Rbg])
                sec = 2 if which == 0 else 5
                wsrc = Din["w_ada"][l].rearrange("(k p) n -> p k n", p=128)
                for half in range(2):
                    Wg, RWg = load_w([(lambda t: t[:, 0:4096].rearrange("p (k n) -> p k n", k=8), wsrc[:, :, sec * 1024 + half * 512:sec * 1024 + (half + 1) * 512])])
                    Wgv = Wg[:, 0:4096].rearrange("p (k n) -> p k n", k=8)
                    for v in range(2):
                        for k in range(8):
                            MM(bank(v), sRep[:, v, k, :], Wgv[:, k, :], k == 0, k == 7, [RWg, Rcs], [PB[v]])
                        TT("dve", gbc[:, v, half * 512:(half + 1) * 512], bank(v), bg[:, half * 512:(half + 1) * 512], ALU.add, [PB[v], Rbg], [Rgbc])
                cnt = 0
                for nb in range(2):
                    cs = slice(nb * 512, (nb + 1) * 512)
                    for k0 in range(0, kc, 8):
                        k1 = min(kc, k0 + 8)
                        fw.dma("pool", Wd[:, k0:k1, :], wsrc3[:, k0:k1, cs], writes=[RWd])
                    for i in range(NT):
                        v = TILE_VEC[i]
                        bk = 2 + (cnt % 4); tj = cnt % 2; cnt += 1
                        for k in range(kc):
                            MM(bank(bk), lhsT_fn(i, k), Wd[:, k, :], k == 0, k == kc - 1, lres_fn(i, k) + [RWd], [PB[bk]])
                        TT("dve", tmp[tj][:], bank(bk), gbc[:, v, cs], ALU.mult, [PB[bk], Rgbc], [Rtmp[tj]])
                        OP("dve", lambda e: e.scalar_tensor_tensor(out=X[:, i, cs], in0=X[:, i, cs], scalar=float(ALPHA), in1=tmp[tj][:],
                                                                   op0=ALU.mult, op1=ALU.add), [RX[i], Rtmp[tj]], [RX[i]])
                for i in range(NT):
                    for hh in range(2):
                        OP("dve", lambda e: e.bn_stats(out=stat[:, i, hh, :], in_=X[:, i, hh * 512:(hh + 1) * 512]), [RX[i]], [Rstat])
                    OP("dve", lambda e: e.bn_aggr(out=mv[:, i, :], in_=stat[:, i, :, :].rearrange("p a b -> p (a b)")), [Rstat], [Rstat])
                TS("dve", rs[:, :, 0], mv[:, :, 1], EPS, ALU.add, [Rstat], [Rstat])
                ACT(rs[:, :, 0], rs[:, :, 0], AF.Sqrt, [Rstat], [Rstat])
                OP("dve", lambda e: e.reciprocal(out=rs[:, :, 0], in_=rs[:, :, 0]), [Rstat], [Rstat])
                OP("dve", lambda e: e.scalar_tensor_tensor(out=rs[:, :, 1], in0=mv[:, :, 0], scalar=-1.0, in1=rs[:, :, 0],
                                                           op0=ALU.mult, op1=ALU.mult), [Rstat], [Rstat])
                for i in range(NT):
                    ACT(X[:, i, :], X[:, i, :], AF.Identity, [RX[i], Rstat], [RX[i]], scale=rs[:, i, 0:1], bias=rs[:, i, 1:2])
                    TT("pool", X[:, i, :], X[:, i, :], lnb[:, 0, :], ALU.mult, [RX[i], Rln], [RX[i]])
                    TT("pool", X[:, i, :], X[:, i, :], lnb[:, 1, :], ALU.add, [RX[i], Rln], [RX[i]])
                fw.barrier()
                CH[0] = ch_save

        def layers():
          for l in range(n_layers):
              with contextlib.ExitStack() as ph:
                  wa = [sb(ph, "wa%d" % i, [128, 8, 512], BF16) for i in range(3)]; Rwa = [Res("wa%d" % i) for i in range(3)]
                  bP = sb(ph, "bP", [128, 48]); Rb = Res("bmod")
                  fw.dma("sp", bP[:], Din["b_adaP"][:, l, :], writes=[Rb])
                  wsrc = Din["w_ada"][l].rearrange("(k p) n -> p k n", p=128)
                  pm = bank(6)[:, 0:96]
                  CH[0] = 'm' in os.environ.get('CHP', 'mabfFGMLUS')
                  for cb in range(12):
                      s = cb % 3
                      sec, half = cb // 2, cb % 2
                      if sec in (2, 5):
                          continue
                      fw.dma("pool", wa[s][:], wsrc[:, :, cb * 512:(cb + 1) * 512], writes=[Rwa[s]])
                      for m in range(4):
                          col = (sec * 8 + half * 4 + m) * 2
                          for k in range(8):
                              MM(pm[:, col:col + 2], wa[s][:, k, m * 128:(m + 1) * 128], sTb[:, k, :], k == 0, k == 7,
                                 [Rwa[s], Rcs], [PB[6]])
                  for sec in (0, 1, 3, 4):
                      TT("dve", modP[:, sec, :, :], pm[:, sec * 16:(sec + 1) * 16].rearrange("p (c v) -> p c v", v=2),
                         bl(bP[:, sec * 8:(sec + 1) * 8], 2), ALU.add, [PB[6], Rb], [Rmod])
                  TS("dve", opsc[:, 0, :, :], modP[:, 1, :, :], 1.0, ALU.add, [Rmod], [Rmod])
                  TS("dve", opsc[:, 1, :, :], modP[:, 4, :, :], 1.0, ALU.add, [Rmod], [Rmod])
                  fw.barrier()
              if stop == "mod": fw.dead = True
              DBG("d_modP", modP[:].rearrange("p a b c -> p (a b c)"), [Rmod])

              with contextlib.ExitStack() as L1:
                  QT = sb(L1, "QT", [128, 4, TOK], BF16); RQ = [Res("QT%d" % j) for j in range(4)]; attnT = QT; RattnT = RQ
                  sfT = sb(L1, "sfT", [128, 8, TOK], BF16); RsfT = [Res("sfT%d" % j) for j in range(8)]
                  with contextlib.ExitStack() as L2:
                      KT = sb(L2, "KT", [128, 2, 2048], BF16); RKT = Res("KT")
                      Vsb = sb(L2, "Vsb", [128, 16, 2, 2, 128], BF16); RV = Res("V")
                      with contextlib.ExitStack() as ph:
                          alloc_ws(ph)
                          uT = sb(ph, "uT", [128, 8, TOK], BF16); RuT = [Res("uT%d" % i) for i in range(NT)]
                          qn = [sb(ph, "qn%d" % j, [128, 512]) for j in range(2)]; Rqn = [Res("qn0"), Res("qn1")]
                          vf = [sb(ph, "vf%d" % j, [128, 128]) for j in range(2)]; Rvf = [Res("vf0"), Res("vf1")]
                          qkb = [sb(ph, "qkb%d" % j, [128, 512], BF16) for j in range(2)]; Rqkb = [Res("qkb0"), Res("qkb1")]
                          sq = sb(ph, "sq", [128, 512]); Rsq = Res("sq")
                          kb2 = [sb(ph, "kb2%d" % j, [128, 128], BF16) for j in range(2)]; Rkb2 = [Res("kb20"), Res("kb21")]
                          ssq = sb(ph, "ssq", [128, 8]); Rssq = Res("ssq")
                          g640 = sb(ph, "g640", [128, 768]); rc = sb(ph, "rc", [128, 8, 32]); rsn = sb(ph, "rsn", [128, 8, 32]); Rcst = Res("cst")
                          rt = [sb(ph, "rt%d" % j, [128, 8, 32]) for j in range(2)]; Rrt = [Res("rt0"), Res("rt1")]
                          win = Din["w_in"][l].rearrange("(k p) n -> p k n", p=128)
                          fw.dma("sp", g640[:], Din["g640"][l], writes=[Rcst])
                          fw.dma("sp", rc[:], Din["ropec"], writes=[Rcst])
                          fw.dma("sp", rsn[:], Din["ropes"], writes=[Rcst])
                          fw.dma("pool", KT[:, 0, 1024:1536], Din["ckT"][l], writes=[RKT])
                          fw.dma("pool", KT[0:64, 1, 1024:1536], Din["ckT"][l, 64:128, :], writes=[RKT])
                          fw.dma("pool", KT[64:128, 1, 1024:1536], Din["ckT"][l, 0:64, :], writes=[RKT])
                          OP("dve", lambda e: e.memset(Vsb[:].rearrange("p a b c d -> p (a b c d)"), 1.0), [], [RV])
                          cvv = Din["cv"][l].rearrange("(t p) (h d) -> p t h d", p=128, h=2)
                          for lay in range(2):
                              for kvh_ in range(2):
                                  fw.dma("pool", Vsb[:, 8:12, kvh_, lay, lay * 64:(lay + 1) * 64], cvv[:, :, kvh_, :], writes=[RV])
                          build_uT(uT, RuT, 0)
                          if stop == "ut": fw.dead = True
                          DBG("d_uT", uT[:].rearrange("p a b -> p (a b)"), RuT)

                          def norm_rope(i, pa, pr, nh, gap, outb, Routb, slot):
                              w_ = nh * 64
                              ACT(sq[:, 0:w_], pa, AF.Square, pr, [Rsq])
                              OP("dve", lambda e: e.reduce_sum(out=ssq[:, 0:nh], in_=sq[:, 0:w_].rearrange("p (h d) -> p h d", d=64), axis=AX.X), [Rsq], [Rssq])
                              TS("dve", ssq[:, 0:nh], ssq[:, 0:nh], 1.0 / 64, ALU.mult, [Rssq], [Rssq], s2=EPS, op1=ALU.add)
                              ACT(ssq[:, 0:nh], ssq[:, 0:nh], AF.Sqrt, [Rssq], [Rssq])
                              OP("dve", lambda e: e.reciprocal(out=ssq[:, 0:nh], in_=ssq[:, 0:nh]), [Rssq], [Rssq])
                              qv = qn[slot][:, 0:w_]
                              TT("dve", qv.rearrange("p (h d) -> p h d", d=64), pa.rearrange("p (h d) -> p h d", d=64), bl(ssq[:, 0:nh], 64),
                                 ALU.mult, pr + [Rssq], [Rqn[slot]])
                              TT("dve", qv, qv, gap, ALU.mult, [Rqn[slot], Rcst], [Rqn[slot]])
                              if i >= 8:
                                  CP("act", outb, qv, [Rqn[slot]], [Routb])
                              else:
                                  x4 = qv.rearrange("p (h d two) -> p h d two", d=32, two=2)
                                  o4 = outb.rearrange("p (h d two) -> p h d two", d=32, two=2)
                                  x0, x1 = x4[:, :, :, 0], x4[:, :, :, 1]
                                  cc, ss_ = bm(rc[:, i, :], nh), bm(rsn[:, i, :], nh)
                                  r0, r1 = rt[0][:, 0:nh, :], rt[1][:, 0:nh, :]
                                  TT("dve", r0, x0, cc, ALU.mult, [Rqn[slot], Rcst], [Rrt[0]])
                                  TT("pool", r1, x1, ss_, ALU.mult, [Rqn[slot], Rcst], [Rrt[1]])
                                  TT("dve", o4[:, :, :, 0], r0, r1, ALU.subtract, Rrt, [Routb])
                                  TT("dve", r0, x0, ss_, ALU.mult, [Rqn[slot], Rcst], [Rrt[0]])
                                  TT("pool", r1, x1, cc, ALU.mult, [Rqn[slot], Rcst], [Rrt[1]])
                                  TT("dve", o4[:, :, :, 1], r0, r1, ALU.add, Rrt, [Routb])

                          CH[0] = 'a' in os.environ.get('CHP', 'mabfFGMLUS')
                          W, RW = load_w([(lambda t: t[:, 0:4096].rearrange("p (k n) -> p k n", k=8), win[:, :, 0:512])])
                          Wv = W[:, 0:4096].rearrange("p (k n) -> p k n", k=8)
                          for i in range(NT):
                              bk = 2 + (i % 2); sl = i % 2
                              for k in range(8):
                                  MM(bank(bk), uT[:, k, i * 128:(i + 1) * 128], Wv[:, k, :], k == 0, k == 7, [RuT[i], RW], [PB[bk]])
                              norm_rope(i, bank(bk), [PB[bk]], 8, g640[:, 0:512], qkb[sl][:, :], Rqkb[sl], sl)
                              for j in range(4):
                                  src = qkb[sl][:, j * 128:(j + 1) * 128]
                                  OP("pe", lambda e: e.transpose(pst[:, j * 128:(j + 1) * 128], src, ident_b[:]), [Rqkb[sl], Rid], [PT])
                              CP("act", QT[:, :, i * 128:(i + 1) * 128], pst[:, 0:512].rearrange("p (j c) -> p j c", j=4), [PT], RQ)
                          if stop == "passA": fw.dead = True
                          CH[0] = 'b' in os.environ.get('CHP', 'mabfFGMLUS')
                          W, RW = load_w([(lambda t: t[:, 0:2048].rearrange("p (k n) -> p k n", k=8), win[:, :, 512:768])])
                          Wv = W[:, 0:2048].rearrange("p (k n) -> p k n", k=8)
                          for i in range(NT):
                              bk = 2 + (i % 2); sl = i % 2
                              for k in range(8):
                                  MM(bank(bk)[:, 0:256], uT[:, k, i * 128:(i + 1) * 128], Wv[:, k, :], k == 0, k == 7, [RuT[i], RW], [PB[bk]])
                              norm_rope(i, bank(bk)[:, 0:256], [PB[bk]], 4, g640[:, 512:768], qkb[sl][:, 0:256], Rqkb[sl], sl)
                              kt = i if i < 8 else 12 + (i - 8)
                              for lay in range(2):
                                  CP("act", Vsb[:, kt, :, lay, lay * 64:(lay + 1) * 64], bank(bk)[:, 128:256].rearrange("p (h d) -> p h d", h=2), [PB[bk]], [RV])
                              if i >= 8:
                                  p_, t_ = (i - 8) // 2, (i - 8) % 2
                                  CP("act", vf[sl][:], bank(bk)[:, 128:256], [PB[bk]], [Rvf[sl]])
                                  fw.dma("sp", Dout["nk"][p_, l, t_ * 128:(t_ + 1) * 128, :], qn[sl][:, 0:128], reads=[Rqn[sl]], writes=[Rdbg])
                                  fw.dma("sp", Dout["nv"][p_, l, t_ * 128:(t_ + 1) * 128, :], vf[sl][:], reads=[Rvf[sl]], writes=[Rdbg])
                              CP("act", kb2[sl][:, 0:64], qkb[sl][:, 64:128], [Rqkb[sl]], [Rkb2[sl]])
                              CP("act", kb2[sl][:, 64:128], qkb[sl][:, 0:64], [Rqkb[sl]], [Rkb2[sl]])
                              OP("pe", lambda e: e.transpose(pst[:, 512:640], qkb[sl][:, 0:128], ident_b[:]), [Rqkb[sl], Rid], [PT])
                              OP("pe", lambda e: e.transpose(pst[:, 640:768], kb2[sl][:, :], ident_b[:]), [Rkb2[sl], Rid], [PT])
                              kc = i * 128 if i < 8 else 1536 + (i - 8) * 128
                              CP("act", KT[:, 0, kc:kc + 128], pst[:, 512:640], [PT], [RKT])
                              CP("act", KT[:, 1, kc:kc + 128], pst[:, 640:768], [PT], [RKT])
                          if stop == "passB": fw.dead = True
                          CH[0] = 'f' in os.environ.get('CHP', 'mabfFGMLUS')
                          for wb in range(2):
                              W, RW = load_w([(lambda t: t[:, 0:4096].rearrange("p (k n) -> p k n", k=8), win[:, :, 768 + wb * 512:1280 + wb * 512])])
                              Wv = W[:, 0:4096].rearrange("p (k n) -> p k n", k=8)
                              for m4 in range(4):
                                  m = wb * 4 + m4
                                  for tb, (c0, cn) in enumerate(BLKS):
                                      bk = (0, 1, 6)[(m * 3 + tb) % 3]
                                      for k in range(8):
                                          MM(bank(bk), Wv[:, k, m4 * 128:(m4 + 1) * 128], uT[:, k, c0:c0 + cn], k == 0, k == 7,
                                             RuT[c0 // 128:(c0 + cn) // 128] + [RW], [PB[bk]])
                                      CP("act", sfT[:, m, c0:c0 + cn], bank(bk), [PB[bk]], [RsfT[m]])
                          DBG("d_sfT", sfT[:].rearrange("p a b -> p (a b)"), RsfT)
                          fw.barrier()
                      if stop == "inproj": fw.dead = True
                      DBG("d_QT", QT[:].rearrange("p a b -> p (a b)"), RQ)
                      DBG("d_KT", KT[:].rearrange("p a b -> p (a b)"), [RKT])
                      CH[0] = 't' in os.environ.get('CHP', 'mabfFGMLUS')
                      with contextlib.ExitStack() as ph:
                          Eb = [sb(ph, "Eb%d" % j, [128, 512], BF16) for j in range(3)]; REb = [Res("Eb%d" % j) for j in range(3)]
                          rsb = [sb(ph, "rsb%d" % j, [128, 512]) for j in range(2)]; Rrsb = [Res("rsb0"), Res("rsb1")]
                          cnt = 0; cnt2 = 0
                          for si, (t0, Ls) in enumerate(SEQS):
                              if si == 0:
                                  qblks = [(0, 512), (512, 512)]; nkt = 12; kt0 = 0; kc0 = 0
                              else:
                                  qblks = [(t0, 256)]; nkt = 2; kt0 = 12 + 2 * (si - 1); kc0 = 1536 + 256 * (si - 1)
                              for h in range(8):
                                  j = h // 2; hh = h % 2; kvh = h // 4; var = 0 if kvh == hh else 1
                                  pv, sm = (slice(0, 64), slice(64, 128)) if hh == 0 else (slice(64, 128), slice(0, 64))
                                  for (q0, qn) in qblks:
                                      pvb = 3 + (cnt2 % 2); cnt2 += 1
                                      for kt in range(nkt):
                                          sbk = cnt % 3; eb = cnt % 3; cnt += 1
                                          MM(bank(sbk)[:, 0:qn], KT[hh * 64:(hh + 1) * 64, var, kc0 + kt * 128:kc0 + (kt + 1) * 128],
                                             QT[hh * 64:(hh + 1) * 64, j, q0:q0 + qn], True, True, [RKT, RQ[j]], [PB[sbk]], chain=('s' in os.environ.get('CHT', '')))
                                          ACT(Eb[eb][:, 0:qn], bank(sbk)[:, 0:qn], AF.Exp, [PB[sbk]], [REb[eb]], scale=0.125)
                                          MM(bank(pvb)[:, 0:qn], Vsb[:, kt0 + kt, kvh, hh, :], Eb[eb][:, 0:qn], kt == 0, kt == nkt - 1,
                                             [RV, REb[eb]], [PB[pvb]], chain=('p' in os.environ.get('CHT', '')))
                                      rb = cnt2 % 2
                                      ACT(rsb[rb][sm, 0:qn], bank(pvb)[sm, 0:qn], AF.Ln, [PB[pvb]], [Rrsb[rb]])
                                      ACT(rsb[rb][sm, 0:qn], rsb[rb][sm, 0:qn], AF.Exp, [Rrsb[rb]], [Rrsb[rb]], scale=-1.0)
                                      TT("dve", attnT[pv, j, q0:q0 + qn], bank(pvb)[pv, 0:qn], rsb[rb][sm, 0:qn], ALU.mult,
                                         [PB[pvb], Rrsb[rb]], [RattnT[j]])
                          fw.barrier()
                  DBG("d_attnT", attnT[:].rearrange("p a b -> p (a b)"), RattnT)
                  CH[0] = 'F' in os.environ.get('CHP', 'mabfFGMLUS')
                  with contextlib.ExitStack() as L3:
                    fourT = sb(L3, "fourT", [128, 4, TOK], BF16); RfourT = [Res("fourT%d" % j) for j in range(4)]
                    with contextlib.ExitStack() as ph:
                        cs128 = sb(ph, "cs128", [128, 256], BF16); cl = sb(ph, "cl", [128, 8, 1024], BF16); sl_ = sb(ph, "sl", [128, 8, 1024], BF16)
                        clp = sb(ph, "clp", [128, 2, 256], BF16); slp = sb(ph, "slp", [128, 2, 256], BF16); Rft = Res("ftab")
                        Pfm = sb(ph, "Pfm", [128, NT, 1024], BF16); RPfm = [Res("Pfm%d" % i) for i in range(NT)]
                        fw.dma("pool", cs128[:], Din["cs128"], writes=[Rft])
                        for k in range(8):
                            fw.dma("pool", cl[:, k, :], Din["cl1024"][:, k, :], writes=[Rft])
                            fw.dma("pool", sl_[:, k, :], Din["sl1024"][:, k, :], writes=[Rft])
                        fw.dma("pool", clp[:], Din["cl256"], writes=[Rft])
                        fw.dma("pool", slp[:], Din["sl256"], writes=[Rft])
                        for i in range(NT):
                            b0 = 2 + 2 * (i % 2)
                            for g in range(4):
                                MM(bank(b0 + g // 2)[:, (g % 2) * 256:(g % 2 + 1) * 256], sfT[:, 4 + g, i * 128:(i + 1) * 128], cs128[:], True, True,
                                   [RsfT[4 + g], Rft], [PB[b0 + g // 2]])
                            CP("act", Pfm[:, i, :], bank(b0, 2), [PB[b0], PB[b0 + 1]], [RPfm[i]])
                        cnt = 0
                        for si, (t0, Ls) in enumerate(SEQS):
                            ntl = Ls // 128; tb0 = t0 // 128
                            tc_, ts_ = (cl, sl_) if si == 0 else (clp, slp)
                            for g in range(4):
                                for lb in range(0, Ls, 512):
                                    n = min(512, Ls - lb)
                                    bk = (0, 1, 6)[cnt % 3]; cnt += 1
                                    for k in range(ntl):
                                        MM(bank(bk)[:, 0:n], Pfm[:, tb0 + k, g * 256:g * 256 + 128], tc_[:, k, lb:lb + n], k == 0, False,
                                           [RPfm[tb0 + k], Rft], [PB[bk]])
                                        MM(bank(bk)[:, 0:n], Pfm[:, tb0 + k, g * 256 + 128:g * 256 + 256], ts_[:, k, lb:lb + n], False, k == ntl - 1,
                                           [RPfm[tb0 + k], Rft], [PB[bk]])
                                    CP("act", fourT[:, g, t0 + lb:t0 + lb + n], bank(bk)[:, 0:n], [PB[bk]], [RfourT[g]])
                        fw.barrier()
                    DBG("d_fourT", fourT[:].rearrange("p a b -> p (a b)"), RfourT)
                    if stop == "four": fw.dead = True
                    CH[0] = 'S' in os.environ.get('CHP', 'mabfFGMLUS')
                    ssmT = sb(L3, "ssmT", [128, 4, TOK], BF16); RssmT = [Res("ssmT%d" % j) for j in range(4)]
                    with contextlib.ExitStack() as S:
                        PowP = [sb(S, "PowP%d" % r, [128, 2, 16, 17]) for r in range(2)]
                        LH = [sb(S, "LH%d" % r, [128, 2, 16, 7]) for r in range(2)]
                        fP = [sb(S, "fP%d" % r, [128, 2, 16]) for r in range(2)]
                        Hb = sb(S, "Hb", [128, 2, 2, 16, NS], BF16)
                        h0 = sb(S, "h0", [128, 2, 2, 16]); stg = sb(S, "stg", [128, 16, 2])
                        RS = Res("ssm"); RHb = Res("Hb"); Rstg = Res("stg")
                        fw.dma("sp", h0[:], Din["h0"][:, l], writes=[RS])
                        def cmul(dr, di, ar, ai, br, bi, m1, m2):
                            TT("dve", m1, ar, br, ALU.mult, [RS], [RS]); TT("dve", m2, ai, bi, ALU.mult, [RS], [RS])
                            TT("dve", dr, m1, m2, ALU.subtract, [RS], [RS])
                            TT("dve", m1, ar, bi, ALU.mult, [RS], [RS]); TT("dve", m2, ai, br, ALU.mult, [RS], [RS])
                            TT("dve", di, m1, m2, ALU.add, [RS], [RS])
                        for d in range(2):
                            with contextlib.ExitStack() as SD:
                                lam = [sb(SD, "lam%d" % r, [128, 528]) for r in range(2)]
                                ff = [sb(SD, "ff%d" % r, [128, 528]) for r in range(2)]
                                A = [sb(SD, "A%d" % r, [128, 16, NS]) for r in range(2)]
                                with contextlib.ExitStack() as SP:
                                    are, aim, ldt, mag, cc, ss, t1, t2, t3 = [sb(SP, "sp%d" % j, [128, 528]) for j in range(9)]
                                    fw.dma("sp", are[:], Din["lamin"][l, d, 0], writes=[RS])
                                    fw.dma("sp", aim[:], Din["lamin"][l, d, 1], writes=[RS])
                                    fw.dma("sp", ldt[:], Din["lamin"][l, d, 2], writes=[RS])
                                    ACT(ldt[:], ldt[:], AF.Exp, [RS], [RS])
                                    TT("dve", t1[:], are[:], ldt[:], ALU.mult, [RS], [RS])
                                    ACT(mag[:], t1[:], AF.Exp, [RS], [RS])
                                    TT("dve", t2[:], aim[:], ldt[:], ALU.mult, [RS], [RS])
                                    ACT(ss[:], t2[:], AF.Sin, [RS], [RS], scale=1.0 / 16)
                                    ACT(t1[:], t2[:], AF.Sin, [RS], [RS], scale=1.0 / 32)
                                    TT("dve", t1[:], t1[:], t1[:], ALU.mult, [RS], [RS])
                                    TS("dve", cc[:], t1[:], -2.0, ALU.mult, [RS], [RS], s2=1.0, op1=ALU.add)
                                    for _ in range(4):
                                        TT("dve", t1[:], cc[:], cc[:], ALU.mult, [RS], [RS])
                                        TT("dve", t2[:], ss[:], ss[:], ALU.mult, [RS], [RS])
                                        TT("dve", t3[:], cc[:], ss[:], ALU.mult, [RS], [RS])
                                        TT("dve", cc[:], t1[:], t2[:], ALU.subtract, [RS], [RS])
                                        TS("dve", ss[:], t3[:], 2.0, ALU.mult, [RS], [RS])
                                    TT("dve", lam[0][:], mag[:], cc[:], ALU.mult, [RS], [RS])
                                    TT("dve", lam[1][:], mag[:], ss[:], ALU.mult, [RS], [RS])
                                    TT("dve", t1[:], are[:], are[:], ALU.mult, [RS], [RS])
                                    TT("dve", t2[:], aim[:], aim[:], ALU.mult, [RS], [RS])
                                    TT("dve", t1[:], t1[:], t2[:], ALU.add, [RS], [RS])
                                    OP("dve", lambda e: e.reciprocal(out=t1[:], in_=t1[:]), [RS], [RS])
                                    TS("dve", t2[:], lam[0][:], -1.0, ALU.add, [RS], [RS])
                                    TT("dve", t3[:], t2[:], are[:], ALU.mult, [RS], [RS])
                                    TT("dve", cc[:], lam[1][:], aim[:], ALU.mult, [RS], [RS])
                                    TT("dve", t3[:], t3[:], cc[:], ALU.add, [RS], [RS])
                                    TT("dve", ff[0][:], t3[:], t1[:], ALU.mult, [RS], [RS])
                                    TT("dve", t3[:], lam[1][:], are[:], ALU.mult, [RS], [RS])
                                    TT("dve", cc[:], t2[:], aim[:], ALU.mult, [RS], [RS])
                                    TT("dve", t3[:], t3[:], cc[:], ALU.subtract, [RS], [RS])
                                    TT("dve", ff[1][:], t3[:], t1[:], ALU.mult, [RS], [RS])
                                    pm1 = t1[:, 0:256].rearrange("p (g k) -> p g k", g=16); pm2 = t2[:, 0:256].rearrange("p (g k) -> p g k", g=16)
                                    OP("dve", lambda e: e.memset(PowP[0][:, d, :, 0:1], 1.0), [RS], [RS])
                                    OP("dve", lambda e: e.memset(PowP[1][:, d, :, 0:1], 0.0), [RS], [RS])
                                    for r in range(2):
                                        ACT(PowP[r][:, d, :, 1], lam[r][:, 0:16], AF.Identity, [RS], [RS])
                                        ACT(fP[r][:, d, :], ff[r][:, 0:16], AF.Identity, [RS], [RS])
                                    for w in (1, 2, 4, 8):
                                        br_ = PowP[0][:, d, :, w:w + 1].to_broadcast([128, 16, w]); bi_ = PowP[1][:, d, :, w:w + 1].to_broadcast([128, 16, w])
                                        cmul(PowP[0][:, d, :, w + 1:2 * w + 1], PowP[1][:, d, :, w + 1:2 * w + 1],
                                             PowP[0][:, d, :, 1:w + 1], PowP[1][:, d, :, 1:w + 1], br_, bi_, pm1[:, :, 0:w], pm2[:, :, 0:w])
                                    for r in range(2):
                                        ACT(LH[r][:, d, :, 0], PowP[r][:, d, :, 16], AF.Identity, [RS], [RS])
                                    for m in range(6):
                                        cmul(LH[0][:, d, :, m + 1], LH[1][:, d, :, m + 1], LH[0][:, d, :, m], LH[1][:, d, :, m],
                                             LH[0][:, d, :, m], LH[1][:, d, :, m], t1[:, 256:272], t2[:, 256:272])
                                for r in range(2):
                                    OP("dve", lambda e: e.memset(A[r][:].rearrange("p a b -> p (a b)"), 0.0), [RS], [RS])
                                with contextlib.ExitStack() as SW:
                                    W = [sb(SW, "W%d" % r, [128, 16, 128]) for r in range(2)]
                                    WT = [sb(SW, "WT%d" % r, [128, 16, 128], BF16) for r in range(2)]
                                    wm = [sb(SW, "wm%d" % r, [128, 8, 128]) for r in range(3)]
                                    BT = [sb(SW, "BT%d" % r, [128, 128]) for r in range(2)]
                                    Lw = [sb(SW, "Lw%d" % r, [128, 128]) for r in range(2)]
                                    RWT = Res("WT")
                                    for q in range(4):
                                        fs = slice(16 + q * 128, 16 + (q + 1) * 128)
                                        for r in range(2):
                                            fw.dma("sp", BT[r][:], Din["BbdT"][l, d, r, :, q, :], writes=[RS])
                                            ACT(Lw[r][:], lam[r][:, fs], AF.Identity, [RS], [RS])
                                        cmul(W[0][:, 0, :], W[1][:, 0, :], ff[0][:, fs], ff[1][:, fs], BT[0][:], BT[1][:], wm[0][:, 0, :], wm[1][:, 0, :])
                                        for w in (1, 2, 4, 8):
                                            cmul(W[0][:, w:2 * w, :], W[1][:, w:2 * w, :], bm(Lw[0][:], w), bm(Lw[1][:], w),
                                                 W[0][:, 0:w, :], W[1][:, 0:w, :], wm[0][:, 0:w, :], wm[1][:, 0:w, :])
                                            if w < 8:
                                                TT("dve", wm[0][:, 0, :], Lw[0][:], Lw[0][:], ALU.mult, [RS], [RS])
                                                TT("dve", wm[1][:, 0, :], Lw[1][:], Lw[1][:], ALU.mult, [RS], [RS])
                                                TT("dve", wm[2][:, 0, :], Lw[0][:], Lw[1][:], ALU.mult, [RS], [RS])
                                                TT("dve", Lw[0][:], wm[0][:, 0, :], wm[1][:, 0, :], ALU.subtract, [RS], [RS])
                                                TS("dve", Lw[1][:], wm[2][:, 0, :], 2.0, ALU.mult, [RS], [RS])
                                        for r in range(2):
                                            ACT(WT[r][:].rearrange("p a b -> p (a b)"), W[r][:].rearrange("p a b -> p (a b)"), AF.Identity, [RS], [RWT])
                                        Sps = bank(0, 2).rearrange("p (b r c) -> p b r c", b=4, r=2)
                                        for b in range(4):
                                            sv = sfT[32 * b:32 * b + 32, q, :].rearrange("p (c j) -> p c j", j=16)
                                            for r in range(2):
                                                for k in range(16):
                                                    j = 15 - k if d == 0 else k
                                                    MM(Sps[:, b, r, 0:96], WT[r][32 * b:32 * b + 32, k, :], sv[:, :, j], k == 0, k == 15,
                                                       [RWT, RsfT[q]], [PB[0], PB[1]], tp=(32 * b, 0))
                                        for r in range(2):
                                            for s_i in range(3):
                                                n = SCH[s_i]; c0 = (0, 64, 80)[s_i]; lo = SOFF[s_i] + (1 if d == 0 else 0)
                                                ACT(A[r][:, 4 * q:4 * q + 4, lo:lo + n], Sps[:, :, r, c0:c0 + n], AF.Identity, [PB[0], PB[1]], [RS])
                                for r in range(2):
                                    ACT(A[r][:, :, 0 if d == 0 else 64], h0[:, d, r, :], AF.Identity, [RS], [RS])
                                with contextlib.ExitStack() as SC:
                                    sm = [sb(SC, "sm%d" % r, [128, 16, NS]) for r in range(3)]
                                    for m in range(7):
                                        dd = 1 << m
                                        groups = []
                                        if dd < 65:
                                            groups.append((lambda t_, lo, hi: t_[:, :, lo:hi], 65, None))
                                        if dd < 17:
                                            groups.append((lambda t_, lo, hi: t_[:, :, 65:99].rearrange("p g (s c) -> p g s c", s=2)[:, :, :, lo:hi], 17, 2))
                                        for view, n, ns in groups:
                                            cntn = n - dd
                                            (dlo, dhi, slo, shi) = (dd, n, 0, cntn) if d == 0 else (0, cntn, dd, n)
                                            if ns is None:
                                                Lr = bl(LH[0][:, d, :, m], cntn); Li = bl(LH[1][:, d, :, m], cntn)
                                                tv = lambda t_: t_[:, :, 0:cntn]
                                            else:
                                                Lr = LH[0][:, d, :, m].unsqueeze(2).unsqueeze(3).to_broadcast([128, 16, 2, cntn])
                                                Li = LH[1][:, d, :, m].unsqueeze(2).unsqueeze(3).to_broadcast([128, 16, 2, cntn])
                                                tv = lambda t_: t_[:, :, 0:2 * cntn].rearrange("p g (s c) -> p g s c", s=2)
                                            sr, si = view(A[0], slo, shi), view(A[1], slo, shi)
                                            dr, di = view(A[0], dlo, dhi), view(A[1], dlo, dhi)
                                            m1, m2, m3 = tv(sm[0]), tv(sm[1]), tv(sm[2])
                                            TT("dve", m1, sr, Lr, ALU.mult, [RS], [RS]); TT("dve", m2, si, Li, ALU.mult, [RS], [RS])
                                            TT("dve", m1, m1, m2, ALU.subtract, [RS], [RS])
                                            TT("dve", m2, si, Lr, ALU.mult, [RS], [RS]); TT("dve", m3, sr, Li, ALU.mult, [RS], [RS])
                                            TT("dve", m2, m2, m3, ALU.add, [RS], [RS])
                                            TT("dve", dr, dr, m1, ALU.add, [RS], [RS]); TT("dve", di, di, m2, ALU.add, [RS], [RS])
                                for r in range(2):
                                    ACT(Hb[:, d, r, :, :], A[r][:], AF.Identity, [RS], [RHb])
                                for s_i in (1, 2):
                                    idx = SOFF[s_i] + (16 if d == 0 else 0)
                                    for r in range(2):
                                        ACT(stg[:, :, r], A[r][:, :, idx], AF.Identity, [RS], [Rstg])
                                    fw.dma("sp", Dout["nst"][s_i - 1, l, d].rearrange("(pair g2) p r -> (g2 p) pair r", g2=2), stg[:], reads=[Rstg], writes=[Rdbg])
                                DBG("d_A%d" % d, A[0][:].rearrange("p a b -> p (a b)"), [RS])
                                fw.barrier()
                        with contextlib.ExitStack() as SQ:
                            CL = [[sb(SQ, "CL%d%d" % (d, r), [128, 4, 17, 32], BF16) for r in range(2)] for d in range(2)]
                            Kblk = [sb(SQ, "Kblk%d" % d, [128, 16, 128], BF16) for d in range(2)]
                            Bb = [[sb(SQ, "Bb%d%d" % (d, r), [128, 4, 32], BF16) for r in range(2)] for d in range(2)]
                            Dbl = sb(SQ, "Dbl", [128, 4, 128], BF16)
                            cm = [sb(SQ, "cm%d" % j, [128, 2, 17, 32]) for j in range(2)]
                            Cl = [sb(SQ, "Cl%d" % r, [128, 4, 32]) for r in range(2)]
                            Bl = [sb(SQ, "Bl%d" % r, [128, 4, 32]) for r in range(2)]
                            bt = [sb(SQ, "bt%d" % r, [128, 4, 32]) for r in range(2)]
                            gt = [sb(SQ, "gt%d" % r, [128, 512]) for r in range(2)]; Rgt = [Res("gt0"), Res("gt1")]
                            RCL = Res("CL"); RK = Res("Kblk")
                            ydb = sb(SQ, "ydb", [128, 512]); Rydb = Res("ydb")
                            fw.dma("pool", Dbl[:], Din["Dblk"][l], writes=[RCL])
                            for q in range(4):
                                ps4 = slice(4 * q, 4 * q + 4)
                                for d in range(2):
                                    for r in range(2):
                                        fw.dma("sp", Cl[r][:], Din["CbdP"][l, d, r, :, ps4, :], writes=[RS])
                                        fw.dma("sp", Bl[r][:], Din["BbdP"][l, d, r, :, ps4, :], writes=[RS])
                                    for hp in range(2):
                                        pp = slice(2 * hp, 2 * hp + 2); pq = slice(4 * q + 2 * hp, 4 * q + 2 * hp + 2)
                                        Pr = PowP[0][:, d, pq, :].unsqueeze(3).to_broadcast([128, 2, 17, 32])
                                        Pi = PowP[1][:, d, pq, :].unsqueeze(3).to_broadcast([128, 2, 17, 32])
                                        Cr = Cl[0][:, pp, :].unsqueeze(2).to_broadcast([128, 2, 17, 32])
                                        Ci = Cl[1][:, pp, :].unsqueeze(2).to_broadcast([128, 2, 17, 32])
                                        TT("dve", cm[0][:], Pr, Cr, ALU.mult, [RS], [RS]); TT("dve", cm[1][:], Pi, Ci, ALU.mult, [RS], [RS])
                                        TT("dve", CL[d][0][:, pp, :, :], cm[0][:], cm[1][:], ALU.subtract, [RS], [RCL])
                                        TT("dve", cm[0][:], Pr, Ci, ALU.mult, [RS], [RS]); TT("dve", cm[1][:], Pi, Cr, ALU.mult, [RS], [RS])
                                        TT("dve", cm[0][:], cm[0][:], cm[1][:], ALU.add, [RS], [RS])
                                        TS("dve", CL[d][1][:, pp, :, :], cm[0][:], -1.0, ALU.mult, [RS], [RCL])
                                    fr = bl(fP[0][:, d, ps4], 32); fi = bl(fP[1][:, d, ps4], 32)
                                    TT("dve", bt[0][:], fr, Bl[0][:], ALU.mult, [RS], [RS]); TT("dve", bt[1][:], fi, Bl[1][:], ALU.mult, [RS], [RS])
                                    TT("dve", Bb[d][0][:], bt[0][:], bt[1][:], ALU.subtract, [RS], [RCL])
                                    TT("dve", bt[0][:], fr, Bl[1][:], ALU.mult, [RS], [RS]); TT("dve", bt[1][:], fi, Bl[0][:], ALU.mult, [RS], [RS])
                                    TT("dve", Bb[d][1][:], bt[0][:], bt[1][:], ALU.add, [RS], [RCL])
                                    kb = 6
                                    for b in range(4):
                                        MM(bank(kb)[32 * b:32 * b + 32, :], Bb[d][0][:, b, :], CL[d][0][:, b, 0:16, :].rearrange("p t n -> p (t n)"), True, False,
                                           [RCL], [PB[kb]], tp=(0, 32 * b))
                                        MM(bank(kb)[32 * b:32 * b + 32, :], Bb[d][1][:, b, :], CL[d][1][:, b, 0:16, :].rearrange("p t n -> p (t n)"), False, True,
                                           [RCL], [PB[kb]], tp=(0, 32 * b))
                                    for cb in range(4):
                                        TS("dve", Kblk[d][:, :, 32 * cb:32 * cb + 32], bank(kb).rearrange("p (t n) -> p t n", n=32), maskP[:, cb:cb + 1], ALU.mult,
                                           [PB[kb], Rmask], [RK])
                                for tb, (c0, cn) in enumerate(BLKS):
                                    yb = 3 + tb
                                    Yb = bank(yb).rearrange("p (c j) -> p c j", j=16)
                                    sblk = sfT[:, q, c0:c0 + cn].rearrange("p (c j) -> p c j", j=16)
                                    MM(bank(yb), Dbl[:, q, :], sfT[:, q, c0:c0 + cn], True, False, [RCL, RsfT[q]], [PB[yb]])
                                    for d in range(2):
                                        for t in range(16):
                                            if d == 0:
                                                o_, r_ = Yb[:, :, t:16], sblk[:, :, 0:16 - t]
                                            else:
                                                o_, r_ = Yb[:, :, 0:16 - t], sblk[:, :, t:16]
                                            MM(o_, Kblk[d][:, t, :], r_, False, False, [RK, RsfT[q]], [PB[yb]])
                                    for d in range(2):
                                        for b in range(4):
                                            for j in range(16):
                                                tt = j + 1 if d == 0 else 16 - j
                                                sh = 0 if d == 0 else 1
                                                for r in range(2):
                                                    if tb < 2:
                                                        rhs = Hb[:, d, r, 4 * q + b, 32 * tb + sh:32 * tb + sh + 32]
                                                        o_ = Yb[32 * b:32 * b + 32, :, j]
                                                    else:
                                                        rhs = Hb[:, d, r, 4 * q + b, 65:99].rearrange("p (s c) -> p s c", s=2)[:, :, sh:sh + 16]
                                                        o_ = bank(yb).rearrange("p (s c j) -> p s c j", s=2, j=16)[32 * b:32 * b + 32, :, :, j]
                                                    last = (d == 1 and b == 3 and j == 15 and r == 1)
                                                    MM(o_, CL[d][r][:, b, tt, :], rhs, False, last, [RCL, RHb], [PB[yb]], tp=(0, 32 * b))
                                    if ("d_y%d_%d" % (q, tb)) in Ddbg:
                                        ACT(ydb[:], bank(yb), AF.Identity, [PB[yb]], [Rydb])
                                        DBG("d_y%d_%d" % (q, tb), ydb[:], [Rydb])
                                    g_ = tb % 2
                                    ACT(gt[g_][:], bank(yb), AF.Square, [PB[yb]], [Rgt[g_]])
                                    TS("dve", gt[g_][:], gt[g_][:], 0.044715, ALU.mult, [Rgt[g_]], [Rgt[g_]], s2=1.0, op1=ALU.add)
                                    TT("dve", gt[g_][:], gt[g_][:], bank(yb), ALU.mult, [Rgt[g_], PB[yb]], [Rgt[g_]])
                                    ACT(gt[g_][:], gt[g_][:], AF.Sigmoid, [Rgt[g_]], [Rgt[g_]], scale=1.5957691216)
                                    TT("dve", sfT[:, q, c0:c0 + cn], gt[g_][:], bank(yb), ALU.mult, [Rgt[g_], PB[yb]], [RsfT[q]])
                            fw.barrier()
                    CH[0] = 'G' in os.environ.get('CHP', 'mabfFGMLUS')
                    with contextlib.ExitStack() as ph:
                        alloc_ws(ph)
                        bgl = sb(ph, "bgl", [128, 4]); Rbgl = Res("bgl"); sg = [sb(ph, "sg%d" % j, [128, 512]) for j in range(2)]; Rsg = [Res("sg0"), Res("sg1")]
                        fw.dma("sp", bgl[:], Din["bglu"][:, l, :], writes=[Rbgl])
                        Wg, RWg = load_w([(lambda t: t[:, 0:2048].rearrange("p (k n) -> p k n", k=4), Din["w_glu"][l].rearrange("(k p) n -> p k n", p=128))])
                        Wgv = Wg[:, 0:2048].rearrange("p (k n) -> p k n", k=4)
                        cnt = 0
                        for m in range(4):
                            for tb, (c0, cn) in enumerate(BLKS):
                                bk = (0, 1, 2)[cnt % 3]; sj = cnt % 2; cnt += 1
                                for k in range(4):
                                    MM(bank(bk), Wgv[:, k, m * 128:(m + 1) * 128], sfT[:, k, c0:c0 + cn], k == 0, k == 3, [RWg, RsfT[k]], [PB[bk]])
                                ACT(sg[sj][:], bank(bk), AF.Sigmoid, [PB[bk], Rbgl], [Rsg[sj]], bias=bgl[:, m:m + 1])
                                TT("dve", ssmT[:, m, c0:c0 + cn], sg[sj][:], sfT[:, m, c0:c0 + cn], ALU.mult, [Rsg[sj], RsfT[m]], [RssmT[m]])
                        fw.barrier()
                    DBG("d_gy", sfT[:, 0:4, :].rearrange("p a b -> p (a b)"), RsfT[0:4])
                    DBG("d_ssmT", ssmT[:].rearrange("p a b -> p (a b)"), RssmT)
                    if stop == "ssm": fw.dead = True
                    CH[0] = 'M' in os.environ.get('CHP', 'mabfFGMLUS')
                    with contextlib.ExitStack() as M:
                        mergedT = sb(M, "mergedT", [128, 8, TOK], BF16); Rmg = [Res("mg%d" % j) for j in range(8)]
                        with contextlib.ExitStack() as ph:
                            alloc_ws(ph)
                            uT = sb(ph, "uT", [128, 8, TOK], BF16); RuT = [Res("uT%d" % i) for i in range(NT)]
                            gs = [sb(ph, "gs%d" % j, [128, 512]) for j in range(3)]; Rgs = [Res("gs%d" % j) for j in range(3)]
                            ac = [sb(ph, "ac%d" % j, [128, 512]) for j in range(2)]; Rac = [Res("ac0"), Res("ac1")]
                            build_uT(uT, RuT, 0)
                            win = Din["w_in"][l].rearrange("(k p) n -> p k n", p=128)
                            wbr = [Din[nm][l].rearrange("(k p) n -> p k n", p=128) for nm in ("w_br_attn", "w_br_ssm", "w_br_four")]
                            srcs = [(attnT, RattnT), (ssmT, RssmT), (fourT, RfourT)]
                            for c in range(8):
                                Wg, RWg = load_w([((lambda t, g=g: t[:, g * 1024:(g + 1) * 1024].rearrange("p (k n) -> p k n", k=8)),
                                                   win[:, :, 1792 + g * 1024 + c * 128:1792 + g * 1024 + (c + 1) * 128]) for g in range(3)])
                                Wb, RWb = load_w([((lambda t, x=x: t[:, x * 512:(x + 1) * 512].rearrange("p (k n) -> p k n", k=4)),
                                                   wbr[x][:, :, c * 128:(c + 1) * 128]) for x in range(3)])
                                for tb, (c0, cn) in enumerate(BLKS):
                                    for g in range(3):
                                        Wgv = Wg[:, g * 1024:(g + 1) * 1024].rearrange("p (k n) -> p k n", k=8)
                                        for k in range(8):
                                            MM(bank(g), Wgv[:, k, :], uT[:, k, c0:c0 + cn], k == 0, k == 7, RuT[c0 // 128:(c0 + cn) // 128] + [RWg], [PB[g]])
                                        ACT(gs[g][:], bank(g), AF.Sigmoid, [PB[g]], [Rgs[g]])
                                    for x in range(3):
                                        Wbv = Wb[:, x * 512:(x + 1) * 512].rearrange("p (k n) -> p k n", k=4)
                                        st_, Rst = srcs[x]
                                        for k in range(4):
                                            MM(bank(3 + x), Wbv[:, k, :], st_[:, k, c0:c0 + cn], k == 0, k == 3, [Rst[k], RWb], [PB[3 + x]])
                                    TT("dve", ac[0][:], gs[0][:], bank(3), ALU.mult, [Rgs[0], PB[3]], [Rac[0]])
                                    TT("dve", ac[1][:], gs[1][:], bank(4), ALU.mult, [Rgs[1], PB[4]], [Rac[1]])
                                    TT("dve", ac[0][:], ac[0][:], ac[1][:], ALU.add, Rac, [Rac[0]])
                                    TT("dve", ac[1][:], gs[2][:], bank(5), ALU.mult, [Rgs[2], PB[5]], [Rac[1]])
                                    TT("dve", mergedT[:, c, c0:c0 + cn], ac[0][:], ac[1][:], ALU.add, Rac, [Rmg[c]])
                            fw.barrier()
                        DBG("d_mergedT", mergedT[:].rearrange("p a b -> p (a b)"), Rmg)
                        if stop == "merge": fw.dead = True
                        layer_norm_residual(l, 0, 8, lambda i, k: mergedT[:, k, i * 128:(i + 1) * 128], lambda i, k: [Rmg[k]],
                                            Din["w_out"][l].rearrange("(k p) n -> p k n", p=128))
              if "d_x1" in Ddbg:
                  for i in range(NT):
                      fw.dma("sp", Ddbg["d_x1"][i * 128:(i + 1) * 128, :], X[:, i, :], reads=[RX[i]], writes=[Rdbg])
              if stop == "ln1": fw.dead = True
              CH[0] = 'U' in os.environ.get('CHP', 'mabfFGMLUS')
              with contextlib.ExitStack() as Fs:
                  actT = sb(Fs, "actT", [128, 22, TOK], BF16); Ract = [Res("act%d" % j) for j in range(22)]
                  with contextlib.ExitStack() as ph:
                      alloc_ws(ph)
                      uT = sb(ph, "uT", [128, 8, TOK], BF16); RuT = [Res("uT%d" % i) for i in range(NT)]
                      cvp = sb(ph, "cvp", [128, 44, 4]); Rcv = Res("cvp")
                      hc = [sb(ph, "hc%d" % j, [128, TOK]) for j in range(2)]; Rhc = [Res("hc0"), Res("hc1")]
                      fw.dma("sp", cvp[:], Din["convp"][:, l], writes=[Rcv])
                      build_uT(uT, RuT, 1)
                      wup = Din["w_up"][l].rearrange("(k p) n -> p k n", p=128)
                      for ch in range(44):
                          if ch % 4 == 0:
                              W, RW = load_w([(lambda t: t[:, 0:4096].rearrange("p (k n) -> p k n", k=8), wup[:, :, ch * 128:ch * 128 + 512])])
                              Wv = W[:, 0:4096].rearrange("p (k n) -> p k n", k=8)
                          cw = ch % 4; pb0 = 3 * (ch % 2); hj = ch % 2
                          prs = [PB[pb0], PB[pb0 + 1], PB[pb0 + 2]]
                          for tb, (c0, cn) in enumerate(BLKS):
                              for k in range(8):
                                  MM(bank(pb0 + tb), Wv[:, k, cw * 128:(cw + 1) * 128], uT[:, k, c0:c0 + cn], k == 0, k == 7,
                                     RuT[c0 // 128:(c0 + cn) // 128] + [RW], [prs[tb]])
                          hp = bank(pb0, 3); h_ = hc[hj]
                          ACT(h_[:], hp, AF.Identity, prs + [Rcv], [Rhc[hj]], scale=cvp[:, ch, 1:2], bias=cvp[:, ch, 3:4])
                          def stt(o_, i0, sc, i1):
                              OP("dve", lambda e: e.scalar_tensor_tensor(out=o_, in0=i0, scalar=sc, in1=i1, op0=ALU.mult, op1=ALU.add),
                                 prs + [Rcv, Rhc[hj]], [Rhc[hj]])
                          stt(h_[:, 1:1024], hp[:, 0:1023], cvp[:, ch, 0:1], h_[:, 1:1024])
                          stt(h_[:, 0:1023], hp[:, 1:1024], cvp[:, ch, 2:3], h_[:, 0:1023])
                          h3 = h_[:, 1024:1536].rearrange("p (s c) -> p s c", s=2); p3 = hp[:, 1024:1536].rearrange("p (s c) -> p s c", s=2)
                          stt(h3[:, :, 1:256], p3[:, :, 0:255], cvp[:, ch, 0:1], h3[:, :, 1:256])
                          stt(h3[:, :, 0:255], p3[:, :, 1:256], cvp[:, ch, 2:3], h3[:, :, 0:255])
                          if ch < 22:
                              ACT(actT[:, ch, :], h_[:], AF.Silu, [Rhc[hj]], [Ract[ch]])
                          else:
                              TT("dve", actT[:, ch - 22, :], actT[:, ch - 22, :], h_[:], ALU.mult, [Ract[ch - 22], Rhc[hj]], [Ract[ch - 22]])
                      fw.barrier()
                  DBG("d_actT", actT[:].rearrange("p a b -> p (a b)"), Ract)
                  if stop == "ffn": fw.dead = True
                  layer_norm_residual(l, 1, 22, lambda i, k: actT[:, k, i * 128:(i + 1) * 128], lambda i, k: [Ract[k]],
                                      Din["w_down"][l].rearrange("(k p) n -> p k n", p=128))
        layers()
        fw.dead = False
        for i in range(NT):
            fw.dma("sp", Dout["y"][i * 128:(i + 1) * 128, :], X[:, i, :], reads=[RX[i]], writes=[Rdbg])
        fw.finish()
    return nc


def kernel(**inputs):
    inp = {k: np.asarray(v) for k, v in inputs.items()}
    S = prep_shared(inp)
    nc = build()
    in_maps = []
    for cid in range(8):
        m = dict(S); m.update(prep_core(inp, cid))
        in_maps.append(m)
    res = run_bass_kernel_spmd(nc, in_maps, core_ids=list(range(8)))
    y_p = np.zeros((16, 256, 1024), np.float32); y_s = np.zeros((8, 1024, 1024), np.float32)
    nk = np.zeros((16, 2, 256, 2, 64), np.float32); nv = np.zeros((16, 2, 256, 2, 64), np.float32)
    nst = np.zeros((16, 2, 2, 32, 64, 2), np.float32)
    for cid in range(8):
        r = res.results[cid]
        y_s[cid] = r["y"][0:1024]
        y_p[2 * cid:2 * cid + 2] = r["y"][1024:1536].reshape(2, 256, 1024)
        nk[2 * cid:2 * cid + 2] = r["nk"].reshape(2, 2, 256, 2, 64)
        nv[2 * cid:2 * cid + 2] = r["nv"].reshape(2, 2, 256, 2, 64)
        nst[2 * cid:2 * cid + 2] = r["nst"]
    return (y_p, y_s, nk, nv, nst)
```

```python
import contextlib, os
SK = os.environ.get("SK", "")
import numpy as np
import concourse.bass as bass
import concourse.mybir as mybir
from concourse.bass_utils import run_bass_kernel_spmd

F32 = mybir.dt.float32; BF16 = mybir.dt.bfloat16
AF = mybir.ActivationFunctionType; ALU = mybir.AluOpType; AX = mybir.AxisListType

D_MODEL = 1024; DEPTH = 2; D_IN = 4864; D_FF = 2816
ALPHA = (2 * DEPTH) ** 0.25
EPS = 1e-6
NT = 12; TOK = 1536; T = 16
SEQS = [(0, 1024), (1024, 256), (1280, 256)]
BLKS = [(0, 512), (512, 512), (1024, 512)]
TILE_VEC = [0] * 8 + [1] * 4
NS = 99
SOFF = [0, 65, 82]
SCH = [64, 16, 16]


class StopBuild(Exception):
    pass


class Res:
    __slots__ = ("name", "w", "r")
    def __init__(self, name):
        self.name = name; self.w = None; self.r = {}


class Eng:
    def __init__(self, name, h, sem):
        self.name = name; self.h = h; self.sem = sem; self.count = 0; self.waited = {}


class FW:
    NDMA = 32
    def __init__(self, nc, es):
        self.nc = nc; self.sems = {}; self.eng = {}
        for name, h in (("pe", nc.tensor), ("act", nc.scalar), ("dve", nc.vector), ("pool", nc.gpsimd), ("sp", nc.sync)):
            s = es.enter_context(nc.semaphore("s_" + name))
            self.sems[name] = s; self.eng[name] = Eng(name, h, s)
        self.dma_sems = []
        for i in range(self.NDMA):
            s = es.enter_context(nc.semaphore("s_dma%d" % i))
            self.sems["dma%d" % i] = s; self.dma_sems.append(["dma%d" % i, 0])
        self.dma_i = 0; self.ninstr = 0; self.dead = False; self.prev_plain = False

    def _wait(self, e, key, val):
        if e.waited.get(key, 0) >= val: return
        e.h.wait_ge(self.sems[key], val); e.waited[key] = val

    def _deps(self, e, reads, writes, chain=False):
        deps = {}
        def add(kv):
            if kv is None: return
            k, v = kv
            if deps.get(k, 0) < v: deps[k] = v
        for r in reads: add(r.w)
        for w in writes:
            if w.w is not None and not (chain and w.w[0] == e.name):
                add(w.w)
            for k, v in w.r.items():
                if k == e.name: continue
                add((k, v))
        return deps

    def op(self, engname, fn, reads=(), writes=(), chain=False):
        if self.dead: return None
        e = self.eng[engname]
        if engname == "pe" and chain in ("gfirst", "gnext"):
            if chain == "gfirst" and e.count > 0:
                self._wait(e, "pe", e.count)
            chain = True; self.prev_plain = False
        elif engname == "pe":
            plain = chain
            chain = chain and self.prev_plain
            if plain and not chain and e.count > 0:
                self._wait(e, "pe", e.count)
            self.prev_plain = plain
        for k, v in self._deps(e, reads, writes, chain).items(): self._wait(e, k, v)
        ins = fn(e.h); e.count += 1; ins.then_inc(e.sem, 1)
        for r in reads: r.r[e.name] = e.count
        for w in writes: w.w = (e.name, e.count); w.r = {}
        self.ninstr += 1
        return ins

    def dma(self, qname, out, in_, reads=(), writes=()):
        if self.dead: return None
        e = self.eng[qname]
        deps = self._deps(e, reads, writes)
        slot = self.dma_sems[self.dma_i % self.NDMA]; self.dma_i += 1
        key = slot[0]
        if slot[1] > 0: deps[key] = max(deps.get(key, 0), slot[1])
        for k, v in deps.items(): self._wait(e, k, v)
        ins = e.h.dma_start(out=out, in_=in_)
        slot[1] += 16; ins.then_inc(self.sems[key], 16)
        for r in reads: r.r[key] = slot[1]
        for w in writes: w.w = (key, slot[1]); w.r = {}
        self.ninstr += 1
        return ins

    def barrier(self):
        if self.dead: return
        targets = {n: e.count for n, e in self.eng.items() if e.count > 0}
        for key, cnt in self.dma_sems:
            if cnt > 0: targets[key] = cnt
        for n, e in self.eng.items():
            for k, v in targets.items(): self._wait(e, k, v)

    def finish(self):
        self.dead = False
        self.barrier()


def bl(ap, n):
    return ap.unsqueeze(2).to_broadcast([ap.shape[0], ap.shape[1], n])


def bm(ap, n):
    return ap.unsqueeze(1).to_broadcast([ap.shape[0], n, ap.shape[1]])


def _rep(a, n=128):
    return np.ascontiguousarray(np.broadcast_to(a[None], (n,) + a.shape))


def prep_shared(inp):
    f = lambda a: np.ascontiguousarray(a, dtype=np.float32)
    L = DEPTH
    S = {}
    for k in ("w_ada", "w_in", "w_glu", "w_br_ssm", "w_br_four", "w_out", "w_up", "w_down"):
        S[k] = f(inp[k])
    S["w_br_attn"] = f(inp["w_br_attn"])
    S["b_adaP"] = f(inp["b_ada"].reshape(L, 48, 128).transpose(2, 0, 1))
    S["bgbc"] = f(np.stack([np.stack([_rep(inp["b_ada"][l, 2048:3072]), _rep(inp["b_ada"][l, 5120:6144])], 1) for l in range(L)]))
    S["lnbc"] = f(np.stack([np.stack([_rep(inp[k][l]) for k in ("ln1_g", "ln1_b", "ln2_g", "ln2_b")]) for l in range(L)]))
    cw = inp["conv_w"].reshape(L, 3, 44, 128).transpose(3, 0, 2, 1)
    cb = inp["conv_b"].reshape(L, 44, 128).transpose(2, 0, 1)[..., None]
    S["convp"] = f(np.concatenate([cw, cb], -1))
    S["bglu"] = f(inp["b_glu"].reshape(L, 4, 128).transpose(2, 0, 1))
    S["g640"] = f(np.stack([_rep(np.concatenate([np.tile(inp["q_norm_g"][l], 8), np.tile(inp["k_norm_g"][l], 4)])) for l in range(L)]))
    def lay(a):
        aP = a.reshape(L, 2, 16, 2, 64).transpose(0, 1, 3, 4, 2).reshape(L, 2, 128, 16)
        aF = a.reshape(L, 2, 4, 4, 2, 64).transpose(0, 1, 3, 2, 4, 5)
        aF = np.broadcast_to(aF[:, :, :, None], (L, 2, 4, 32, 4, 2, 64)).reshape(L, 2, 128, 512)
        return np.concatenate([aP, aF], -1)
    ldt = np.broadcast_to(inp["ssm_log_dt"][..., None], (L, 2, 32, 64))
    S["lamin"] = f(np.stack([lay(inp["ssm_a_re"]), lay(inp["ssm_a_im"]), lay(ldt)], 2))
    def bbdP(b):
        Bp = b.reshape(L, 2, 16, 2, 64, 16)
        out = np.zeros((L, 2, 2, 64, 16, 2, 16), np.float32)
        for g2 in range(2):
            out[:, :, g2, :, :, g2, :] = Bp[:, :, :, g2].transpose(0, 1, 3, 2, 4)
        return out.reshape(L, 2, 128, 16, 32)
    def bbdT(b):
        Bq = b.reshape(L, 2, 4, 4, 2, 64, 16)
        out = np.zeros((L, 2, 4, 2, 16, 4, 2, 64), np.float32)
        for g2 in range(2):
            out[:, :, :, g2, :, :, g2, :] = Bq[:, :, :, :, g2].transpose(0, 1, 3, 5, 2, 4)
        return out.reshape(L, 2, 128, 4, 128)
    def cbdP(c):
        Cp = c.reshape(L, 2, 16, 2, 16, 64)
        out = np.zeros((L, 2, 2, 64, 16, 2, 16), np.float32)
        for g2 in range(2):
            out[:, :, g2, :, :, g2, :] = Cp[:, :, :, g2].transpose(0, 1, 4, 2, 3)
        return out.reshape(L, 2, 128, 16, 32)
    S["BbdP"] = f(np.stack([bbdP(inp["ssm_b_re"]), bbdP(inp["ssm_b_im"])], 2))
    S["BbdT"] = f(np.stack([bbdT(inp["ssm_b_re"]), bbdT(inp["ssm_b_im"])], 2))
    S["CbdP"] = f(np.stack([cbdP(inp["ssm_c_re"]), cbdP(inp["ssm_c_im"])], 2))
    Dblk = np.zeros((L, 128, 4, 128), np.float32)
    for l in range(L):
        for q in range(4):
            Dblk[l, np.arange(128), q, np.arange(128)] = inp["ssm_d"][l, q * 128:(q + 1) * 128]
    S["Dblk"] = Dblk
    S["ident"] = np.eye(128, dtype=np.float32)
    mk = np.zeros((128, 4), np.float32); mk[np.arange(128), np.arange(128) // 32] = 1.0
    S["maskP"] = mk
    t = np.arange(1024)
    freqs = (10000.0 ** (-np.arange(16, dtype=np.float32) / 16)).astype(np.float32)
    ang = np.concatenate([(t // 64).astype(np.float32)[:, None] * freqs, (t % 64).astype(np.float32)[:, None] * freqs], -1)
    S["ropec"] = f(np.cos(ang).reshape(8, 128, 32).transpose(1, 0, 2))
    S["ropes"] = f(np.sin(ang).reshape(8, 128, 32).transpose(1, 0, 2))
    def dft(n):
        i = np.arange(n, dtype=np.int64)
        m = (i[:, None] * i[None, :]) % n
        a = 2.0 * np.pi * m.astype(np.float64) / n
        return np.cos(a) / np.sqrt(n), np.sin(a) / np.sqrt(n)
    c128, s128 = dft(128)
    S["cs128"] = f(np.concatenate([c128, s128], 1))
    for n in (1024, 256):
        c, s = dft(n)
        S["cl%d" % n] = f(c.reshape(n // 128, 128, n).transpose(1, 0, 2))
        S["sl%d" % n] = f((-s).reshape(n // 128, 128, n).transpose(1, 0, 2))
    return S


def prep_core(inp, cid):
    f = lambda a: np.ascontiguousarray(a, dtype=np.float32)
    m = {}
    m["xin"] = f(np.concatenate([inp["x_sample"][cid], inp["x_prompt"][2 * cid:2 * cid + 2].reshape(512, 1024)], 0))
    cv = np.stack([inp["c"][cid], inp["c_ctx"]], -1)
    m["cvec"] = f(cv.reshape(8, 128, 2).transpose(1, 0, 2))
    m["ckT"] = f(inp["cache_k"][cid].reshape(DEPTH, 512, 128).transpose(0, 2, 1))
    m["cv"] = f(inp["cache_v"][cid].reshape(DEPTH, 512, 128))
    st = inp["state_ssm"][cid].reshape(DEPTH, 2, 16, 2, 64, 2)
    m["h0"] = f(st.transpose(3, 4, 0, 1, 5, 2).reshape(128, DEPTH, 2, 2, 16))
    return m


IN_SHAPES = {
    "xin": (1536, 1024), "cvec": (128, 8, 2), "ckT": (2, 128, 512), "cv": (2, 512, 128), "h0": (128, 2, 2, 2, 16),
    "w_ada": (2, 1024, 6144), "w_in": (2, 1024, 4864), "w_glu": (2, 512, 512), "w_br_attn": (2, 512, 1024),
    "w_br_ssm": (2, 512, 1024), "w_br_four": (2, 512, 1024), "w_out": (2, 1024, 1024), "w_up": (2, 1024, 5632),
    "w_down": (2, 2816, 1024), "b_adaP": (128, 2, 48), "bgbc": (2, 128, 2, 1024), "lnbc": (2, 4, 128, 1024),
    "convp": (128, 2, 44, 4), "bglu": (128, 2, 4), "g640": (2, 128, 768), "lamin": (2, 2, 3, 128, 528),
    "BbdP": (2, 2, 2, 128, 16, 32), "BbdT": (2, 2, 2, 128, 4, 128), "CbdP": (2, 2, 2, 128, 16, 32),
    "Dblk": (2, 128, 4, 128), "ident": (128, 128), "maskP": (128, 4), "ropec": (128, 8, 32), "ropes": (128, 8, 32),
    "cs128": (128, 256), "cl1024": (128, 8, 1024), "sl1024": (128, 8, 1024), "cl256": (128, 2, 256), "sl256": (128, 2, 256),
}
OUT_SHAPES = {"y": (1536, 1024), "nk": (2, 2, 256, 128), "nv": (2, 2, 256, 128), "nst": (2, 2, 2, 32, 64, 2)}


def build(n_layers=DEPTH, stop=None, dbg_shapes=None):
    nc = bass.Bass("TRN2", target_bir_lowering=False)
    Din = {k: nc.dram_tensor(k, list(s), F32, kind="ExternalInput").ap() for k, s in IN_SHAPES.items()}
    Dout = {k: nc.dram_tensor(k, list(s), F32, kind="ExternalOutput").ap() for k, s in OUT_SHAPES.items()}
    Ddbg = {k: nc.dram_tensor(k, list(s), F32, kind="ExternalOutput").ap() for k, s in (dbg_shapes or {}).items()}
    es = contextlib.ExitStack()
    with es:
        fw = FW(nc, es)
        uid = [0]
        def sb(st, name, shape, dt=F32):
            uid[0] += 1
            return st.enter_context(nc.sbuf_tensor("%s_%d" % (name, uid[0]), list(shape), dt))
        OP = fw.op
        CH = [True]
        def MM(out, lhsT, rhs, start, stop_, r, w, tp=None, chain=None):
            if chain is None: chain = CH[0] and (tp is None)
            if tp is None:
                return fw.op("pe", lambda e: e.matmul(out, lhsT=lhsT, rhs=rhs, start=start, stop=stop_), r, w, chain=chain)
            return fw.op("pe", lambda e: e.matmul(out, lhsT=lhsT, rhs=rhs, start=start, stop=stop_, tile_position=tp), r, w, chain=chain)
        def TT(eng, out, in0, in1, op, r, w):
            return fw.op(eng, lambda e: e.tensor_tensor(out=out, in0=in0, in1=in1, op=op), r, w)
        def TS(eng, out, in0, s1, op0, r, w, s2=None, op1=None):
            if op1 is None:
                return fw.op(eng, lambda e: e.tensor_scalar(out=out, in0=in0, scalar1=s1, scalar2=None, op0=op0), r, w)
            return fw.op(eng, lambda e: e.tensor_scalar(out=out, in0=in0, scalar1=s1, scalar2=s2, op0=op0, op1=op1), r, w)
        def ACT(out, in_, func, r, w, scale=1.0, bias=0.0):
            return fw.op("act", lambda e: e.activation(out=out, in_=in_, func=func, bias=bias, scale=scale), r, w)
        def CP(eng, out, in_, r, w):
            if eng == "act":
                return fw.op("act", lambda e: e.copy(out=out, in_=in_), r, w)
            return fw.op(eng, lambda e: e.tensor_copy(out=out, in_=in_), r, w)
        Rdbg = Res("dbg")
        def DBG(name, ap, r):
            if name in Ddbg:
                fw.dma("pool", Ddbg[name], ap, reads=r, writes=[Rdbg])

        X = sb(es, "X", [128, NT, 1024]); RX = [Res("X%d" % i) for i in range(NT)]
        ident_f = sb(es, "ident_f", [128, 128]); ident_b = sb(es, "ident_b", [128, 128], BF16); Rid = Res("id")
        maskP = sb(es, "maskP", [128, 4]); Rmask = Res("mask")
        modP = sb(es, "modP", [128, 6, 8, 2]); opsc = sb(es, "opsc", [128, 2, 8, 2]); Rmod = Res("mod")
        csl = sb(es, "csl", [128, 8, 2]); sTb = sb(es, "sTb", [128, 8, 2], BF16); sRep = sb(es, "sRep", [128, 2, 8, 128], BF16)
        Rcs = Res("cs")
        stat = sb(es, "stat", [128, NT, 2, 6]); mv = sb(es, "mv", [128, NT, 2]); rs = sb(es, "rs", [128, NT, 2]); Rstat = Res("stat")
        ps = es.enter_context(nc.psum_tensor("ps", [128, 3584], F32)); PB = [Res("pb%d" % i) for i in range(7)]
        pst = es.enter_context(nc.psum_tensor("pst", [128, 1024], BF16)); PT = Res("pt")
        wslot = [None, None]; Rws = [None, None]
        wctr = [0]
        def alloc_ws(st):
            for i_ in range(2):
                wslot[i_] = sb(st, "wslot%d" % i_, [128, 4096], BF16); Rws[i_] = Res("ws%d" % i_)
        def load_w(views):
            s_ = wctr[0] % 2; wctr[0] += 1
            for dstf, src in views:
                fw.dma("pool", dstf(wslot[s_]), src, writes=[Rws[s_]])
            return wslot[s_], Rws[s_]
        def bank(i, n=1):
            return ps[:, i * 512:(i + n) * 512]

        fw.dma("sp", ident_f[:], Din["ident"], writes=[Rid])
        fw.dma("pool", ident_b[:], Din["ident"], writes=[Rid])
        fw.dma("sp", maskP[:], Din["maskP"], writes=[Rmask])
        for i in range(NT):
            fw.dma("sp", X[:, i, :], Din["xin"][i * 128:(i + 1) * 128, :], writes=[RX[i]])
        fw.dma("sp", csl[:], Din["cvec"], writes=[Rcs])
        ACT(csl[:], csl[:], AF.Silu, [Rcs], [Rcs])
        CP("dve", sTb[:], csl[:], [Rcs], [Rcs])
        for v in range(2):
            CP("dve", sRep[:, v, :, :], bl(csl[:, :, v], 128), [Rcs], [Rcs])

        def build_uT(uT, RuT, sub):
            sh_sec = 0 if sub == 0 else 3
            for i in range(NT):
                v = TILE_VEC[i]
                for kk in range(2):
                    bk = kk
                    for k4 in range(4):
                        k = kk * 4 + k4
                        OP("pe", lambda e: e.transpose(bank(bk)[:, k4 * 128:(k4 + 1) * 128], X[:, i, k * 128:(k + 1) * 128], ident_f[:]),
                           [RX[i], Rid], [PB[bk]])
                    for k4 in range(4):
                        k = kk * 4 + k4
                        ACT(uT[:, k, i * 128:(i + 1) * 128], bank(bk)[:, k4 * 128:(k4 + 1) * 128], AF.Identity,
                            [PB[bk], Rmod], [RuT[i]], scale=opsc[:, sub, k, v:v + 1], bias=modP[:, sh_sec, k, v:v + 1])

        def layer_norm_residual(l, which, kc, lhsT_fn, lres_fn, wsrc3):
            with contextlib.ExitStack() as ph:
                ch_save = CH[0]; CH[0] = 'L' in os.environ.get('CHP', 'mabfFGMLUS')
                alloc_ws(ph)
                lnb = sb(ph, "lnb", [128, 2, 1024]); Rln = Res("lnb")
                tmp = [sb(ph, "lntmp%d" % j, [128, 512]) for j in range(2)]; Rtmp = [Res("lntmp0"), Res("lntmp1")]
                Wd = sb(ph, "Wd", [128, kc, 512], BF16); RWd = Res("Wd")
                fw.dma("sp", lnb[:, 0, :], Din["lnbc"][l, 2 * which], writes=[Rln])
                fw.dma("sp", lnb[:, 1, :], Din["lnbc"][l, 2 * which + 1], writes=[Rln])
                gbc = sb(ph, "gbc", [128, 2, 1024]); Rgbc = Res("gbc")
                bg = sb(ph, "bg", [128, 1024]); Rbg = Res("bg")
                fw.dma("sp", bg[:], Din["bgbc"][l, :, which, :], writes=[Rbg])
                sec = 2 if which == 0 else 5
                wsrc = Din["w_ada"][l].rearrange("(k p) n -> p k n", p=128)
                for half in range(2):
                    Wg, RWg = load_w([(lambda t: t[:, 0:4096].rearrange("p (k n) -> p k n", k=8), wsrc[:, :, sec * 1024 + half * 512:sec * 1024 + (half + 1) * 512])])
                    Wgv = Wg[:, 0:4096].rearrange("p (k n) -> p k n", k=8)
                    for v in range(2):
                        for k in range(8):
                            MM(bank(v), sRep[:, v, k, :], Wgv[:, k, :], k == 0, k == 7, [RWg, Rcs], [PB[v]])
                        TT("dve", gbc[:, v, half * 512:(half + 1) * 512], bank(v), bg[:, half * 512:(half + 1) * 512], ALU.add, [PB[v], Rbg], [Rgbc])
                cnt = 0
                for nb in range(2):
                    cs = slice(nb * 512, (nb + 1) * 512)
                    for k0 in range(0, kc, 8):
                        k1 = min(kc, k0 + 8)
                        fw.dma("pool", Wd[:, k0:k1, :], wsrc3[:, k0:k1, cs], writes=[RWd])
                    for i in range(NT):
                        v = TILE_VEC[i]
                        bk = 2 + (cnt % 4); tj = cnt % 2; cnt += 1
                        for k in range(kc):
                            MM(bank(bk), lhsT_fn(i, k), Wd[:, k, :], k == 0, k == kc - 1, lres_fn(i, k) + [RWd], [PB[bk]])
                        TT("dve", tmp[tj][:], bank(bk), gbc[:, v, cs], ALU.mult, [PB[bk], Rgbc], [Rtmp[tj]])
                        OP("dve", lambda e: e.scalar_tensor_tensor(out=X[:, i, cs], in0=X[:, i, cs], scalar=float(ALPHA), in1=tmp[tj][:],
                                                                   op0=ALU.mult, op1=ALU.add), [RX[i], Rtmp[tj]], [RX[i]])
                for i in range(NT):
                    for hh in range(2):
                        OP("dve", lambda e: e.bn_stats(out=stat[:, i, hh, :], in_=X[:, i, hh * 512:(hh + 1) * 512]), [RX[i]], [Rstat])
                    OP("dve", lambda e: e.bn_aggr(out=mv[:, i, :], in_=stat[:, i, :, :].rearrange("p a b -> p (a b)")), [Rstat], [Rstat])
                TS("dve", rs[:, :, 0], mv[:, :, 1], EPS, ALU.add, [Rstat], [Rstat])
                ACT(rs[:, :, 0], rs[:, :, 0], AF.Sqrt, [Rstat], [Rstat])
                OP("dve", lambda e: e.reciprocal(out=rs[:, :, 0], in_=rs[:, :, 0]), [Rstat], [Rstat])
                OP("dve", lambda e: e.scalar_tensor_tensor(out=rs[:, :, 1], in0=mv[:, :, 0], scalar=-1.0, in1=rs[:, :, 0],
                                                           op0=ALU.mult, op1=ALU.mult), [Rstat], [Rstat])
                for i in range(NT):
                    ACT(X[:, i, :], X[:, i, :], AF.Identity, [RX[i], Rstat], [RX[i]], scale=rs[:, i, 0:1], bias=rs[:, i, 1:2])
                    TT("pool", X[:, i, :], X[:, i, :], lnb[:, 0, :], ALU.mult, [RX[i], Rln], [RX[i]])
                    TT("pool", X[:, i, :], X[:, i, :], lnb[:, 1, :], ALU.add, [RX[i], Rln], [RX[i]])
                fw.barrier()
                CH[0] = ch_save

        def layers():
          for l in range(n_layers):
              with contextlib.ExitStack() as ph:
                  wa = [sb(ph, "wa%d" % i, [128, 8, 512], BF16) for i in range(3)]; Rwa = [Res("wa%d" % i) for i in range(3)]
                  bP = sb(ph, "bP", [128, 48]); Rb = Res("bmod")
                  fw.dma("sp", bP[:], Din["b_adaP"][:, l, :], writes=[Rb])
                  wsrc = Din["w_ada"][l].rearrange("(k p) n -> p k n", p=128)
                  pm = bank(6)[:, 0:96]
                  CH[0] = 'm' in os.environ.get('CHP', 'mabfFGMLUS')
                  for cb in range(12):
                      s = cb % 3
                      sec, half = cb // 2, cb % 2
                      if sec in (2, 5):
                          continue
                      fw.dma("pool", wa[s][:], wsrc[:, :, cb * 512:(cb + 1) * 512], writes=[Rwa[s]])
                      for m in range(4):
                          col = (sec * 8 + half * 4 + m) * 2
                          for k in range(8):
                              MM(pm[:, col:col + 2], wa[s][:, k, m * 128:(m + 1) * 128], sTb[:, k, :], k == 0, k == 7,
                                 [Rwa[s], Rcs], [PB[6]])
                  for sec in (0, 1, 3, 4):
                      TT("dve", modP[:, sec, :, :], pm[:, sec * 16:(sec + 1) * 16].rearrange("p (c v) -> p c v", v=2),
                         bl(bP[:, sec * 8:(sec + 1) * 8], 2), ALU.add, [PB[6], Rb], [Rmod])
                  TS("dve", opsc[:, 0, :, :], modP[:, 1, :, :], 1.0, ALU.add, [Rmod], [Rmod])
                  TS("dve", opsc[:, 1, :, :], modP[:, 4, :, :], 1.0, ALU.add, [Rmod], [Rmod])
                  fw.barrier()
              if stop == "mod": fw.dead = True
              DBG("d_modP", modP[:].rearrange("p a b c -> p (a b c)"), [Rmod])

              with contextlib.ExitStack() as L1:
                  QT = sb(L1, "QT", [128, 4, TOK], BF16); RQ = [Res("QT%d" % j) for j in range(4)]; attnT = QT; RattnT = RQ
                  sfT = sb(L1, "sfT", [128, 8, TOK], BF16); RsfT = [Res("sfT%d" % j) for j in range(8)]
                  with contextlib.ExitStack() as L2:
                      KT = sb(L2, "KT", [128, 2, 2048], BF16); RKT = Res("KT")
                      Vsb = sb(L2, "Vsb", [128, 16, 2, 2, 128], BF16); RV = Res("V")
                      with contextlib.ExitStack() as ph:
                          alloc_ws(ph)
                          uT = sb(ph, "uT", [128, 8, TOK], BF16); RuT = [Res("uT%d" % i) for i in range(NT)]
                          qn = [sb(ph, "qn%d" % j, [128, 512]) for j in range(2)]; Rqn = [Res("qn0"), Res("qn1")]
                          vf = [sb(ph, "vf%d" % j, [128, 128]) for j in range(2)]; Rvf = [Res("vf0"), Res("vf1")]
                          qkb = [sb(ph, "qkb%d" % j, [128, 512], BF16) for j in range(2)]; Rqkb = [Res("qkb0"), Res("qkb1")]
                          sq = sb(ph, "sq", [128, 512]); Rsq = Res("sq")
                          kb2 = [sb(ph, "kb2%d" % j, [128, 128], BF16) for j in range(2)]; Rkb2 = [Res("kb20"), Res("kb21")]
                          ssq = sb(ph, "ssq", [128, 8]); Rssq = Res("ssq")
                          g640 = sb(ph, "g640", [128, 768]); rc = sb(ph, "rc", [128, 8, 32]); rsn = sb(ph, "rsn", [128, 8, 32]); Rcst = Res("cst")
                          rt = [sb(ph, "rt%d" % j, [128, 8, 32]) for j in range(2)]; Rrt = [Res("rt0"), Res("rt1")]
                          win = Din["w_in"][l].rearrange("(k p) n -> p k n", p=128)
                          fw.dma("sp", g640[:], Din["g640"][l], writes=[Rcst])
                          fw.dma("sp", rc[:], Din["ropec"], writes=[Rcst])
                          fw.dma("sp", rsn[:], Din["ropes"], writes=[Rcst])
                          fw.dma("pool", KT[:, 0, 1024:1536], Din["ckT"][l], writes=[RKT])
                          fw.dma("pool", KT[0:64, 1, 1024:1536], Din["ckT"][l, 64:128, :], writes=[RKT])
                          fw.dma("pool", KT[64:128, 1, 1024:1536], Din["ckT"][l, 0:64, :], writes=[RKT])
                          OP("dve", lambda e: e.memset(Vsb[:].rearrange("p a b c d -> p (a b c d)"), 1.0), [], [RV])
                          cvv = Din["cv"][l].rearrange("(t p) (h d) -> p t h d", p=128, h=2)
                          for lay in range(2):
                              for kvh_ in range(2):
                                  fw.dma("pool", Vsb[:, 8:12, kvh_, lay, lay * 64:(lay + 1) * 64], cvv[:, :, kvh_, :], writes=[RV])
                          build_uT(uT, RuT, 0)
                          if stop == "ut": fw.dead = True
                          DBG("d_uT", uT[:].rearrange("p a b -> p (a b)"), RuT)

                          def norm_rope(i, pa, pr, nh, gap, outb, Routb, slot):
                              w_ = nh * 64
                              ACT(sq[:, 0:w_], pa, AF.Square, pr, [Rsq])
                              OP("dve", lambda e: e.reduce_sum(out=ssq[:, 0:nh], in_=sq[:, 0:w_].rearrange("p (h d) -> p h d", d=64), axis=AX.X), [Rsq], [Rssq])
                              TS("dve", ssq[:, 0:nh], ssq[:, 0:nh], 1.0 / 64, ALU.mult, [Rssq], [Rssq], s2=EPS, op1=ALU.add)
                              ACT(ssq[:, 0:nh], ssq[:, 0:nh], AF.Sqrt, [Rssq], [Rssq])
                              OP("dve", lambda e: e.reciprocal(out=ssq[:, 0:nh], in_=ssq[:, 0:nh]), [Rssq], [Rssq])
                              qv = qn[slot][:, 0:w_]
                              TT("dve", qv.rearrange("p (h d) -> p h d", d=64), pa.rearrange("p (h d) -> p h d", d=64), bl(ssq[:, 0:nh], 64),
                                 ALU.mult, pr + [Rssq], [Rqn[slot]])
                              TT("dve", qv, qv, gap, ALU.mult, [Rqn[slot], Rcst], [Rqn[slot]])
                              if i >= 8:
                                  CP("act", outb, qv, [Rqn[slot]], [Routb])
                              else:
                                  x4 = qv.rearrange("p (h d two) -> p h d two", d=32, two=2)
                                  o4 = outb.rearrange("p (h d two) -> p h d two", d=32, two=2)
                                  x0, x1 = x4[:, :, :, 0], x4[:, :, :, 1]
                                  cc, ss_ = bm(rc[:, i, :], nh), bm(rsn[:, i, :], nh)
                                  r0, r1 = rt[0][:, 0:nh, :], rt[1][:, 0:nh, :]
                                  TT("dve", r0, x0, cc, ALU.mult, [Rqn[slot], Rcst], [Rrt[0]])
                                  TT("pool", r1, x1, ss_, ALU.mult, [Rqn[slot], Rcst], [Rrt[1]])
                                  TT("dve", o4[:, :, :, 0], r0, r1, ALU.subtract, Rrt, [Routb])
                                  TT("dve", r0, x0, ss_, ALU.mult, [Rqn[slot], Rcst], [Rrt[0]])
                                  TT("pool", r1, x1, cc, ALU.mult, [Rqn[slot], Rcst], [Rrt[1]])
                                  TT("dve", o4[:, :, :, 1], r0, r1, ALU.add, Rrt, [Routb])

                          CH[0] = 'a' in os.environ.get('CHP', 'mabfFGMLUS')
                          W, RW = load_w([(lambda t: t[:, 0:4096].rearrange("p (k n) -> p k n", k=8), win[:, :, 0:512])])
                          Wv = W[:, 0:4096].rearrange("p (k n) -> p k n", k=8)
                          for i in range(NT):
                              bk = 2 + (i % 2); sl = i % 2
                              for k in range(8):
                                  MM(bank(bk), uT[:, k, i * 128:(i + 1) * 128], Wv[:, k, :], k == 0, k == 7, [RuT[i], RW], [PB[bk]])
                              norm_rope(i, bank(bk), [PB[bk]], 8, g640[:, 0:512], qkb[sl][:, :], Rqkb[sl], sl)
                              for j in range(4):
                                  src = qkb[sl][:, j * 128:(j + 1) * 128]
                                  OP("pe", lambda e: e.transpose(pst[:, j * 128:(j + 1) * 128], src, ident_b[:]), [Rqkb[sl], Rid], [PT])
                              CP("act", QT[:, :, i * 128:(i + 1) * 128], pst[:, 0:512].rearrange("p (j c) -> p j c", j=4), [PT], RQ)
                          if stop == "passA": fw.dead = True
                          CH[0] = 'b' in os.environ.get('CHP', 'mabfFGMLUS')
                          W, RW = load_w([(lambda t: t[:, 0:2048].rearrange("p (k n) -> p k n", k=8), win[:, :, 512:768])])
                          Wv = W[:, 0:2048].rearrange("p (k n) -> p k n", k=8)
                          for i in range(NT):
                              bk = 2 + (i % 2); sl = i % 2
                              for k in range(8):
                                  MM(bank(bk)[:, 0:256], uT[:, k, i * 128:(i + 1) * 128], Wv[:, k, :], k == 0, k == 7, [RuT[i], RW], [PB[bk]])
                              norm_rope(i, bank(bk)[:, 0:256], [PB[bk]], 4, g640[:, 512:768], qkb[sl][:, 0:256], Rqkb[sl], sl)
                              kt = i if i < 8 else 12 + (i - 8)
                              for lay in range(2):
                                  CP("act", Vsb[:, kt, :, lay, lay * 64:(lay + 1) * 64], bank(bk)[:, 128:256].rearrange("p (h d) -> p h d", h=2), [PB[bk]], [RV])
                              if i >= 8:
                                  p_, t_ = (i - 8) // 2, (i - 8) % 2
                                  CP("act", vf[sl][:], bank(bk)[:, 128:256], [PB[bk]], [Rvf[sl]])
                                  fw.dma("sp", Dout["nk"][p_, l, t_ * 128:(t_ + 1) * 128, :], qn[sl][:, 0:128], reads=[Rqn[sl]], writes=[Rdbg])
                                  fw.dma("sp", Dout["nv"][p_, l, t_ * 128:(t_ + 1) * 128, :], vf[sl][:], reads=[Rvf[sl]], writes=[Rdbg])
                              CP("act", kb2[sl][:, 0:64], qkb[sl][:, 64:128], [Rqkb[sl]], [Rkb2[sl]])
                              CP("act", kb2[sl][:, 64:128], qkb[sl][:, 0:64], [Rqkb[sl]], [Rkb2[sl]])
                              OP("pe", lambda e: e.transpose(pst[:, 512:640], qkb[sl][:, 0:128], ident_b[:]), [Rqkb[sl], Rid], [PT])
                              OP("pe", lambda e: e.transpose(pst[:, 640:768], kb2[sl][:, :], ident_b[:]), [Rkb2[sl], Rid], [PT])
                              kc = i * 128 if i < 8 else 1536 + (i - 8) * 128
                              CP("act", KT[:, 0, kc:kc + 128], pst[:, 512:640], [PT], [RKT])
                              CP("act", KT[:, 1, kc:kc + 128], pst[:, 640:768], [PT], [RKT])
                          if stop == "passB": fw.dead = True
                          CH[0] = 'f' in os.environ.get('CHP', 'mabfFGMLUS')
                          for wb in range(2):
                              W, RW = load_w([(lambda t: t[:, 0:4096].rearrange("p (k n) -> p k n", k=8), win[:, :, 768 + wb * 512:1280 + wb * 512])])
                              Wv = W[:, 0:4096].rearrange("p (k n) -> p k n", k=8)
                              for m4 in range(4):
                                  m = wb * 4 + m4
                                  for tb, (c0, cn) in enumerate(BLKS):
                                      bk = (0, 1, 6)[(m * 3 + tb) % 3]
                                      for k in range(8):
                                          MM(bank(bk), Wv[:, k, m4 * 128:(m4 + 1) * 128], uT[:, k, c0:c0 + cn], k == 0, k == 7,
                                             RuT[c0 // 128:(c0 + cn) // 128] + [RW], [PB[bk]])
                                      CP("act", sfT[:, m, c0:c0 + cn], bank(bk), [PB[bk]], [RsfT[m]])
                          DBG("d_sfT", sfT[:].rearrange("p a b -> p (a b)"), RsfT)
                          fw.barrier()
                      if stop == "inproj": fw.dead = True
                      DBG("d_QT", QT[:].rearrange("p a b -> p (a b)"), RQ)
                      DBG("d_KT", KT[:].rearrange("p a b -> p (a b)"), [RKT])
                      CH[0] = 't' in os.environ.get('CHP', 'mabfFGMLUS')
                      with contextlib.ExitStack() as ph:
                          Eb = [sb(ph, "Eb%d" % j, [128, 512], BF16) for j in range(3)]; REb = [Res("Eb%d" % j) for j in range(3)]
                          rsb = [sb(ph, "rsb%d" % j, [128, 512]) for j in range(2)]; Rrsb = [Res("rsb0"), Res("rsb1")]
                          cnt = 0; cnt2 = 0
                          for si, (t0, Ls) in enumerate(SEQS):
                              if si == 0:
                                  qblks = [(0, 512), (512, 512)]; nkt = 12; kt0 = 0; kc0 = 0
                              else:
                                  qblks = [(t0, 256)]; nkt = 2; kt0 = 12 + 2 * (si - 1); kc0 = 1536 + 256 * (si - 1)
                              for h in range(8):
                                  j = h // 2; hh = h % 2; kvh = h // 4; var = 0 if kvh == hh else 1
                                  pv, sm = (slice(0, 64), slice(64, 128)) if hh == 0 else (slice(64, 128), slice(0, 64))
                                  for (q0, qn) in qblks:
                                      pvb = 3 + (cnt2 % 2); cnt2 += 1
                                      def S_mm(kt, sbk):
                                          MM(bank(sbk)[:, 0:qn], KT[hh * 64:(hh + 1) * 64, var, kc0 + kt * 128:kc0 + (kt + 1) * 128],
                                             QT[hh * 64:(hh + 1) * 64, j, q0:q0 + qn], True, True, [RKT, RQ[j]], [PB[sbk]], chain=False)
                                      slots = [(cnt + i_) % 3 for i_ in range(nkt)]; cnt += nkt
                                      S_mm(0, slots[0])
                                      for kt in range(nkt):
                                          sbk = slots[kt]; eb = sbk
                                          if kt + 1 < nkt:
                                              S_mm(kt + 1, slots[kt + 1])
                                          ACT(Eb[eb][:, 0:qn], bank(sbk)[:, 0:qn], AF.Exp, [PB[sbk]], [REb[eb]], scale=0.125)
                                          MM(bank(pvb)[:, 0:qn], Vsb[:, kt0 + kt, kvh, hh, :], Eb[eb][:, 0:qn], kt == 0, kt == nkt - 1,
                                             [RV, REb[eb]], [PB[pvb]], chain=False)
                                      rb = cnt2 % 2
                                      ACT(rsb[rb][sm, 0:qn], bank(pvb)[sm, 0:qn], AF.Ln, [PB[pvb]], [Rrsb[rb]])
                                      ACT(rsb[rb][sm, 0:qn], rsb[rb][sm, 0:qn], AF.Exp, [Rrsb[rb]], [Rrsb[rb]], scale=-1.0)
                                      TT("dve", attnT[pv, j, q0:q0 + qn], bank(pvb)[pv, 0:qn], rsb[rb][sm, 0:qn], ALU.mult,
                                         [PB[pvb], Rrsb[rb]], [RattnT[j]])
                          fw.barrier()
                  DBG("d_attnT", attnT[:].rearrange("p a b -> p (a b)"), RattnT)
                  CH[0] = 'F' in os.environ.get('CHP', 'mabfFGMLUS')
                  with contextlib.ExitStack() as L3:
                    fourT = sb(L3, "fourT", [128, 4, TOK], BF16); RfourT = [Res("fourT%d" % j) for j in range(4)]
                    with contextlib.ExitStack() as ph:
                        cs128 = sb(ph, "cs128", [128, 256], BF16); cl = sb(ph, "cl", [128, 8, 1024], BF16); sl_ = sb(ph, "sl", [128, 8, 1024], BF16)
                        clp = sb(ph, "clp", [128, 2, 256], BF16); slp = sb(ph, "slp", [128, 2, 256], BF16); Rft = Res("ftab")
                        Pfm = sb(ph, "Pfm", [128, NT, 1024], BF16); RPfm = [Res("Pfm%d" % i) for i in range(NT)]
                        fw.dma("pool", cs128[:], Din["cs128"], writes=[Rft])
                        for k in range(8):
                            fw.dma("pool", cl[:, k, :], Din["cl1024"][:, k, :], writes=[Rft])
                            fw.dma("pool", sl_[:, k, :], Din["sl1024"][:, k, :], writes=[Rft])
                        fw.dma("pool", clp[:], Din["cl256"], writes=[Rft])
                        fw.dma("pool", slp[:], Din["sl256"], writes=[Rft])
                        for i in range(NT):
                            b0 = 2 + 2 * (i % 2)
                            for g in range(4):
                                MM(bank(b0 + g // 2)[:, (g % 2) * 256:(g % 2 + 1) * 256], sfT[:, 4 + g, i * 128:(i + 1) * 128], cs128[:], True, True,
                                   [RsfT[4 + g], Rft], [PB[b0 + g // 2]])
                            CP("act", Pfm[:, i, :], bank(b0, 2), [PB[b0], PB[b0 + 1]], [RPfm[i]])
                        cnt = 0
                        for si, (t0, Ls) in enumerate(SEQS):
                            ntl = Ls // 128; tb0 = t0 // 128
                            tc_, ts_ = (cl, sl_) if si == 0 else (clp, slp)
                            for g in range(4):
                                for lb in range(0, Ls, 512):
                                    n = min(512, Ls - lb)
                                    bk = (0, 1, 6)[cnt % 3]; cnt += 1
                                    for k in range(ntl):
                                        MM(bank(bk)[:, 0:n], Pfm[:, tb0 + k, g * 256:g * 256 + 128], tc_[:, k, lb:lb + n], k == 0, False,
                                           [RPfm[tb0 + k], Rft], [PB[bk]])
                                        MM(bank(bk)[:, 0:n], Pfm[:, tb0 + k, g * 256 + 128:g * 256 + 256], ts_[:, k, lb:lb + n], False, k == ntl - 1,
                                           [RPfm[tb0 + k], Rft], [PB[bk]])
                                    CP("act", fourT[:, g, t0 + lb:t0 + lb + n], bank(bk)[:, 0:n], [PB[bk]], [RfourT[g]])
                        fw.barrier()
                    DBG("d_fourT", fourT[:].rearrange("p a b -> p (a b)"), RfourT)
                    if stop == "four": fw.dead = True
                    CH[0] = 'S' in os.environ.get('CHP', 'mabfFGMLUS')
                    ssmT = sb(L3, "ssmT", [128, 4, TOK], BF16); RssmT = [Res("ssmT%d" % j) for j in range(4)]
                    with contextlib.ExitStack() as S:
                        PowP = [sb(S, "PowP%d" % r, [128, 2, 16, 17]) for r in range(2)]
                        LH = [sb(S, "LH%d" % r, [128, 2, 16, 7]) for r in range(2)]
                        fP = [sb(S, "fP%d" % r, [128, 2, 16]) for r in range(2)]
                        Hb = sb(S, "Hb", [128, 2, 2, 16, NS], BF16)
                        h0 = sb(S, "h0", [128, 2, 2, 16]); stg = sb(S, "stg", [128, 16, 2])
                        RS = Res("ssm"); RHb = Res("Hb"); Rstg = Res("stg")
                        fw.dma("sp", h0[:], Din["h0"][:, l], writes=[RS])
                        def cmul(dr, di, ar, ai, br, bi, m1, m2):
                            TT("dve", m1, ar, br, ALU.mult, [RS], [RS]); TT("dve", m2, ai, bi, ALU.mult, [RS], [RS])
                            TT("dve", dr, m1, m2, ALU.subtract, [RS], [RS])
                            TT("dve", m1, ar, bi, ALU.mult, [RS], [RS]); TT("dve", m2, ai, br, ALU.mult, [RS], [RS])
                            TT("dve", di, m1, m2, ALU.add, [RS], [RS])
                        for d in range(2):
                            with contextlib.ExitStack() as SD:
                                lam = [sb(SD, "lam%d" % r, [128, 528]) for r in range(2)]
                                ff = [sb(SD, "ff%d" % r, [128, 528]) for r in range(2)]
                                A = [sb(SD, "A%d" % r, [128, 16, NS]) for r in range(2)]
                                with contextlib.ExitStack() as SP:
                                    are, aim, ldt, mag, cc, ss, t1, t2, t3 = [sb(SP, "sp%d" % j, [128, 528]) for j in range(9)]
                                    fw.dma("sp", are[:], Din["lamin"][l, d, 0], writes=[RS])
                                    fw.dma("sp", aim[:], Din["lamin"][l, d, 1], writes=[RS])
                                    fw.dma("sp", ldt[:], Din["lamin"][l, d, 2], writes=[RS])
                                    ACT(ldt[:], ldt[:], AF.Exp, [RS], [RS])
                                    TT("dve", t1[:], are[:], ldt[:], ALU.mult, [RS], [RS])
                                    ACT(mag[:], t1[:], AF.Exp, [RS], [RS])
                                    TT("dve", t2[:], aim[:], ldt[:], ALU.mult, [RS], [RS])
                                    ACT(ss[:], t2[:], AF.Sin, [RS], [RS], scale=1.0 / 16)
                                    ACT(t1[:], t2[:], AF.Sin, [RS], [RS], scale=1.0 / 32)
                                    TT("dve", t1[:], t1[:], t1[:], ALU.mult, [RS], [RS])
                                    TS("dve", cc[:], t1[:], -2.0, ALU.mult, [RS], [RS], s2=1.0, op1=ALU.add)
                                    for _ in range(4):
                                        TT("dve", t1[:], cc[:], cc[:], ALU.mult, [RS], [RS])
                                        TT("dve", t2[:], ss[:], ss[:], ALU.mult, [RS], [RS])
                                        TT("dve", t3[:], cc[:], ss[:], ALU.mult, [RS], [RS])
                                        TT("dve", cc[:], t1[:], t2[:], ALU.subtract, [RS], [RS])
                                        TS("dve", ss[:], t3[:], 2.0, ALU.mult, [RS], [RS])
                                    TT("dve", lam[0][:], mag[:], cc[:], ALU.mult, [RS], [RS])
                                    TT("dve", lam[1][:], mag[:], ss[:], ALU.mult, [RS], [RS])
                                    TT("dve", t1[:], are[:], are[:], ALU.mult, [RS], [RS])
                                    TT("dve", t2[:], aim[:], aim[:], ALU.mult, [RS], [RS])
                                    TT("dve", t1[:], t1[:], t2[:], ALU.add, [RS], [RS])
                                    OP("dve", lambda e: e.reciprocal(out=t1[:], in_=t1[:]), [RS], [RS])
                                    TS("dve", t2[:], lam[0][:], -1.0, ALU.add, [RS], [RS])
                                    TT("dve", t3[:], t2[:], are[:], ALU.mult, [RS], [RS])
                                    TT("dve", cc[:], lam[1][:], aim[:], ALU.mult, [RS], [RS])
                                    TT("dve", t3[:], t3[:], cc[:], ALU.add, [RS], [RS])
                                    TT("dve", ff[0][:], t3[:], t1[:], ALU.mult, [RS], [RS])
                                    TT("dve", t3[:], lam[1][:], are[:], ALU.mult, [RS], [RS])
                                    TT("dve", cc[:], t2[:], aim[:], ALU.mult, [RS], [RS])
                                    TT("dve", t3[:], t3[:], cc[:], ALU.subtract, [RS], [RS])
                                    TT("dve", ff[1][:], t3[:], t1[:], ALU.mult, [RS], [RS])
                                    pm1 = t1[:, 0:256].rearrange("p (g k) -> p g k", g=16); pm2 = t2[:, 0:256].rearrange("p (g k) -> p g k", g=16)
                                    OP("dve", lambda e: e.memset(PowP[0][:, d, :, 0:1], 1.0), [RS], [RS])
                                    OP("dve", lambda e: e.memset(PowP[1][:, d, :, 0:1], 0.0), [RS], [RS])
                                    for r in range(2):
                                        ACT(PowP[r][:, d, :, 1], lam[r][:, 0:16], AF.Identity, [RS], [RS])
                                        ACT(fP[r][:, d, :], ff[r][:, 0:16], AF.Identity, [RS], [RS])
                                    for w in (1, 2, 4, 8):
                                        br_ = PowP[0][:, d, :, w:w + 1].to_broadcast([128, 16, w]); bi_ = PowP[1][:, d, :, w:w + 1].to_broadcast([128, 16, w])
                                        cmul(PowP[0][:, d, :, w + 1:2 * w + 1], PowP[1][:, d, :, w + 1:2 * w + 1],
                                             PowP[0][:, d, :, 1:w + 1], PowP[1][:, d, :, 1:w + 1], br_, bi_, pm1[:, :, 0:w], pm2[:, :, 0:w])
                                    for r in range(2):
                                        ACT(LH[r][:, d, :, 0], PowP[r][:, d, :, 16], AF.Identity, [RS], [RS])
                                    for m in range(6):
                                        cmul(LH[0][:, d, :, m + 1], LH[1][:, d, :, m + 1], LH[0][:, d, :, m], LH[1][:, d, :, m],
                                             LH[0][:, d, :, m], LH[1][:, d, :, m], t1[:, 256:272], t2[:, 256:272])
                                for r in range(2):
                                    OP("dve", lambda e: e.memset(A[r][:].rearrange("p a b -> p (a b)"), 0.0), [RS], [RS])
                                with contextlib.ExitStack() as SW:
                                    W = [sb(SW, "W%d" % r, [128, 16, 128]) for r in range(2)]
                                    WT = [sb(SW, "WT%d" % r, [128, 16, 128], BF16) for r in range(2)]
                                    wm = [sb(SW, "wm%d" % r, [128, 8, 128]) for r in range(3)]
                                    BT = [sb(SW, "BT%d" % r, [128, 128]) for r in range(2)]
                                    Lw = [sb(SW, "Lw%d" % r, [128, 128]) for r in range(2)]
                                    RWT = Res("WT")
                                    for q in range(4):
                                        fs = slice(16 + q * 128, 16 + (q + 1) * 128)
                                        for r in range(2):
                                            fw.dma("sp", BT[r][:], Din["BbdT"][l, d, r, :, q, :], writes=[RS])
                                            ACT(Lw[r][:], lam[r][:, fs], AF.Identity, [RS], [RS])
                                        cmul(W[0][:, 0, :], W[1][:, 0, :], ff[0][:, fs], ff[1][:, fs], BT[0][:], BT[1][:], wm[0][:, 0, :], wm[1][:, 0, :])
                                        for w in (1, 2, 4, 8):
                                            cmul(W[0][:, w:2 * w, :], W[1][:, w:2 * w, :], bm(Lw[0][:], w), bm(Lw[1][:], w),
                                                 W[0][:, 0:w, :], W[1][:, 0:w, :], wm[0][:, 0:w, :], wm[1][:, 0:w, :])
                                            if w < 8:
                                                TT("dve", wm[0][:, 0, :], Lw[0][:], Lw[0][:], ALU.mult, [RS], [RS])
                                                TT("dve", wm[1][:, 0, :], Lw[1][:], Lw[1][:], ALU.mult, [RS], [RS])
                                                TT("dve", wm[2][:, 0, :], Lw[0][:], Lw[1][:], ALU.mult, [RS], [RS])
                                                TT("dve", Lw[0][:], wm[0][:, 0, :], wm[1][:, 0, :], ALU.subtract, [RS], [RS])
                                                TS("dve", Lw[1][:], wm[2][:, 0, :], 2.0, ALU.mult, [RS], [RS])
                                        for r in range(2):
                                            ACT(WT[r][:].rearrange("p a b -> p (a b)"), W[r][:].rearrange("p a b -> p (a b)"), AF.Identity, [RS], [RWT])
                                        Sps = bank(0, 2).rearrange("p (b r c) -> p b r c", b=4, r=2)
                                        for b in range(4):
                                            sv = sfT[32 * b:32 * b + 32, q, :].rearrange("p (c j) -> p c j", j=16)
                                            for r in range(2):
                                                for k in range(16):
                                                    j = 15 - k if d == 0 else k
                                                    MM(Sps[:, b, r, 0:96], WT[r][32 * b:32 * b + 32, k, :], sv[:, :, j], k == 0, k == 15,
                                                       [RWT, RsfT[q]], [PB[0], PB[1]], tp=(32 * b, 0))
                                        for r in range(2):
                                            for s_i in range(3):
                                                n = SCH[s_i]; c0 = (0, 64, 80)[s_i]; lo = SOFF[s_i] + (1 if d == 0 else 0)
                                                ACT(A[r][:, 4 * q:4 * q + 4, lo:lo + n], Sps[:, :, r, c0:c0 + n], AF.Identity, [PB[0], PB[1]], [RS])
                                for r in range(2):
                                    ACT(A[r][:, :, 0 if d == 0 else 64], h0[:, d, r, :], AF.Identity, [RS], [RS])
                                with contextlib.ExitStack() as SC:
                                    sm = [sb(SC, "sm%d" % r, [128, 16, NS]) for r in range(3)]
                                    for m in range(7):
                                        dd = 1 << m
                                        groups = []
                                        if dd < 65:
                                            groups.append((lambda t_, lo, hi: t_[:, :, lo:hi], 65, None))
                                        if dd < 17:
                                            groups.append((lambda t_, lo, hi: t_[:, :, 65:99].rearrange("p g (s c) -> p g s c", s=2)[:, :, :, lo:hi], 17, 2))
                                        for view, n, ns in groups:
                                            cntn = n - dd
                                            (dlo, dhi, slo, shi) = (dd, n, 0, cntn) if d == 0 else (0, cntn, dd, n)
                                            if ns is None:
                                                Lr = bl(LH[0][:, d, :, m], cntn); Li = bl(LH[1][:, d, :, m], cntn)
                                                tv = lambda t_: t_[:, :, 0:cntn]
                                            else:
                                                Lr = LH[0][:, d, :, m].unsqueeze(2).unsqueeze(3).to_broadcast([128, 16, 2, cntn])
                                                Li = LH[1][:, d, :, m].unsqueeze(2).unsqueeze(3).to_broadcast([128, 16, 2, cntn])
                                                tv = lambda t_: t_[:, :, 0:2 * cntn].rearrange("p g (s c) -> p g s c", s=2)
                                            sr, si = view(A[0], slo, shi), view(A[1], slo, shi)
                                            dr, di = view(A[0], dlo, dhi), view(A[1], dlo, dhi)
                                            m1, m2, m3 = tv(sm[0]), tv(sm[1]), tv(sm[2])
                                            TT("dve", m1, sr, Lr, ALU.mult, [RS], [RS]); TT("dve", m2, si, Li, ALU.mult, [RS], [RS])
                                            TT("dve", m1, m1, m2, ALU.subtract, [RS], [RS])
                                            TT("dve", m2, si, Lr, ALU.mult, [RS], [RS]); TT("dve", m3, sr, Li, ALU.mult, [RS], [RS])
                                            TT("dve", m2, m2, m3, ALU.add, [RS], [RS])
                                            TT("dve", dr, dr, m1, ALU.add, [RS], [RS]); TT("dve", di, di, m2, ALU.add, [RS], [RS])
                                for r in range(2):
                                    ACT(Hb[:, d, r, :, :], A[r][:], AF.Identity, [RS], [RHb])
                                for s_i in (1, 2):
                                    idx = SOFF[s_i] + (16 if d == 0 else 0)
                                    for r in range(2):
                                        ACT(stg[:, :, r], A[r][:, :, idx], AF.Identity, [RS], [Rstg])
                                    fw.dma("sp", Dout["nst"][s_i - 1, l, d].rearrange("(pair g2) p r -> (g2 p) pair r", g2=2), stg[:], reads=[Rstg], writes=[Rdbg])
                                DBG("d_A%d" % d, A[0][:].rearrange("p a b -> p (a b)"), [RS])
                                fw.barrier()
                        with contextlib.ExitStack() as SQ:
                            CL = [[sb(SQ, "CL%d%d" % (d, r), [128, 4, 17, 32], BF16) for r in range(2)] for d in range(2)]
                            Kblk = [sb(SQ, "Kblk%d" % d, [128, 16, 128], BF16) for d in range(2)]
                            Bb = [[sb(SQ, "Bb%d%d" % (d, r), [128, 4, 32], BF16) for r in range(2)] for d in range(2)]
                            Dbl = sb(SQ, "Dbl", [128, 4, 128], BF16)
                            cm = [sb(SQ, "cm%d" % j, [128, 2, 17, 32]) for j in range(2)]
                            Cl = [sb(SQ, "Cl%d" % r, [128, 4, 32]) for r in range(2)]
                            Bl = [sb(SQ, "Bl%d" % r, [128, 4, 32]) for r in range(2)]
                            bt = [sb(SQ, "bt%d" % r, [128, 4, 32]) for r in range(2)]
                            gt = [sb(SQ, "gt%d" % r, [128, 512]) for r in range(2)]; Rgt = [Res("gt0"), Res("gt1")]
                            RCL = Res("CL"); RK = Res("Kblk")
                            ydb = sb(SQ, "ydb", [128, 512]); Rydb = Res("ydb")
                            fw.dma("pool", Dbl[:], Din["Dblk"][l], writes=[RCL])
                            for q in range(4):
                                ps4 = slice(4 * q, 4 * q + 4)
                                for d in range(2):
                                    for r in range(2):
                                        fw.dma("sp", Cl[r][:], Din["CbdP"][l, d, r, :, ps4, :], writes=[RS])
                                        fw.dma("sp", Bl[r][:], Din["BbdP"][l, d, r, :, ps4, :], writes=[RS])
                                    for hp in range(2):
                                        pp = slice(2 * hp, 2 * hp + 2); pq = slice(4 * q + 2 * hp, 4 * q + 2 * hp + 2)
                                        Pr = PowP[0][:, d, pq, :].unsqueeze(3).to_broadcast([128, 2, 17, 32])
                                        Pi = PowP[1][:, d, pq, :].unsqueeze(3).to_broadcast([128, 2, 17, 32])
                                        Cr = Cl[0][:, pp, :].unsqueeze(2).to_broadcast([128, 2, 17, 32])
                                        Ci = Cl[1][:, pp, :].unsqueeze(2).to_broadcast([128, 2, 17, 32])
                                        TT("dve", cm[0][:], Pr, Cr, ALU.mult, [RS], [RS]); TT("dve", cm[1][:], Pi, Ci, ALU.mult, [RS], [RS])
                                        TT("dve", CL[d][0][:, pp, :, :], cm[0][:], cm[1][:], ALU.subtract, [RS], [RCL])
                                        TT("dve", cm[0][:], Pr, Ci, ALU.mult, [RS], [RS]); TT("dve", cm[1][:], Pi, Cr, ALU.mult, [RS], [RS])
                                        TT("dve", cm[0][:], cm[0][:], cm[1][:], ALU.add, [RS], [RS])
                                        TS("dve", CL[d][1][:, pp, :, :], cm[0][:], -1.0, ALU.mult, [RS], [RCL])
                                    fr = bl(fP[0][:, d, ps4], 32); fi = bl(fP[1][:, d, ps4], 32)
                                    TT("dve", bt[0][:], fr, Bl[0][:], ALU.mult, [RS], [RS]); TT("dve", bt[1][:], fi, Bl[1][:], ALU.mult, [RS], [RS])
                                    TT("dve", Bb[d][0][:], bt[0][:], bt[1][:], ALU.subtract, [RS], [RCL])
                                    TT("dve", bt[0][:], fr, Bl[1][:], ALU.mult, [RS], [RS]); TT("dve", bt[1][:], fi, Bl[0][:], ALU.mult, [RS], [RS])
                                    TT("dve", Bb[d][1][:], bt[0][:], bt[1][:], ALU.add, [RS], [RCL])
                                    kb = 6
                                    for b in range(4):
                                        MM(bank(kb)[32 * b:32 * b + 32, :], Bb[d][0][:, b, :], CL[d][0][:, b, 0:16, :].rearrange("p t n -> p (t n)"), True, False,
                                           [RCL], [PB[kb]], tp=(0, 32 * b))
                                        MM(bank(kb)[32 * b:32 * b + 32, :], Bb[d][1][:, b, :], CL[d][1][:, b, 0:16, :].rearrange("p t n -> p (t n)"), False, True,
                                           [RCL], [PB[kb]], tp=(0, 32 * b))
                                    for cb in range(4):
                                        TS("dve", Kblk[d][:, :, 32 * cb:32 * cb + 32], bank(kb).rearrange("p (t n) -> p t n", n=32), maskP[:, cb:cb + 1], ALU.mult,
                                           [PB[kb], Rmask], [RK])
                                for tb, (c0, cn) in enumerate(BLKS):
                                    yb = 3 + tb
                                    Yb = bank(yb).rearrange("p (c j) -> p c j", j=16)
                                    sblk = sfT[:, q, c0:c0 + cn].rearrange("p (c j) -> p c j", j=16)
                                    MM(bank(yb), Dbl[:, q, :], sfT[:, q, c0:c0 + cn], True, False, [RCL, RsfT[q]], [PB[yb]])
                                    for d in range(2):
                                        for t in range(16):
                                            if d == 0:
                                                o_, r_ = Yb[:, :, t:16], sblk[:, :, 0:16 - t]
                                            else:
                                                o_, r_ = Yb[:, :, 0:16 - t], sblk[:, :, t:16]
                                            MM(o_, Kblk[d][:, t, :], r_, False, False, [RK, RsfT[q]], [PB[yb]])
                                    GC = os.environ.get('GC', '1') == '1'
                                    for d in range(2):
                                        for j in range(16):
                                            tt = j + 1 if d == 0 else 16 - j
                                            sh = 0 if d == 0 else 1
                                            for r in range(2):
                                                for b in range(4):
                                                    if tb < 2:
                                                        rhs = Hb[:, d, r, 4 * q + b, 32 * tb + sh:32 * tb + sh + 32]
                                                        o_ = Yb[32 * b:32 * b + 32, :, j]
                                                    else:
                                                        rhs = Hb[:, d, r, 4 * q + b, 65:99].rearrange("p (s c) -> p s c", s=2)[:, :, sh:sh + 16]
                                                        o_ = bank(yb).rearrange("p (s c j) -> p s c j", s=2, j=16)[32 * b:32 * b + 32, :, :, j]
                                                    last = (d == 1 and b == 3 and j == 15 and r == 1)
                                                    MM(o_, CL[d][r][:, b, tt, :], rhs, False, last, [RCL, RHb], [PB[yb]], tp=(0, 32 * b),
                                                       chain=(("gfirst" if b == 0 else "gnext") if GC else False))
                                    if ("d_y%d_%d" % (q, tb)) in Ddbg:
                                        ACT(ydb[:], bank(yb), AF.Identity, [PB[yb]], [Rydb])
                                        DBG("d_y%d_%d" % (q, tb), ydb[:], [Rydb])
                                    g_ = tb % 2
                                    ACT(gt[g_][:], bank(yb), AF.Square, [PB[yb]], [Rgt[g_]])
                                    TS("dve", gt[g_][:], gt[g_][:], 0.044715, ALU.mult, [Rgt[g_]], [Rgt[g_]], s2=1.0, op1=ALU.add)
                                    TT("dve", gt[g_][:], gt[g_][:], bank(yb), ALU.mult, [Rgt[g_], PB[yb]], [Rgt[g_]])
                                    ACT(gt[g_][:], gt[g_][:], AF.Sigmoid, [Rgt[g_]], [Rgt[g_]], scale=1.5957691216)
                                    TT("dve", sfT[:, q, c0:c0 + cn], gt[g_][:], bank(yb), ALU.mult, [Rgt[g_], PB[yb]], [RsfT[q]])
                            fw.barrier()
                    CH[0] = 'G' in os.environ.get('CHP', 'mabfFGMLUS')
                    with contextlib.ExitStack() as ph:
                        alloc_ws(ph)
                        bgl = sb(ph, "bgl", [128, 4]); Rbgl = Res("bgl"); sg = [sb(ph, "sg%d" % j, [128, 512]) for j in range(2)]; Rsg = [Res("sg0"), Res("sg1")]
                        fw.dma("sp", bgl[:], Din["bglu"][:, l, :], writes=[Rbgl])
                        Wg, RWg = load_w([(lambda t: t[:, 0:2048].rearrange("p (k n) -> p k n", k=4), Din["w_glu"][l].rearrange("(k p) n -> p k n", p=128))])
                        Wgv = Wg[:, 0:2048].rearrange("p (k n) -> p k n", k=4)
                        cnt = 0
                        for m in range(4):
                            for tb, (c0, cn) in enumerate(BLKS):
                                bk = (0, 1, 2)[cnt % 3]; sj = cnt % 2; cnt += 1
                                for k in range(4):
                                    MM(bank(bk), Wgv[:, k, m * 128:(m + 1) * 128], sfT[:, k, c0:c0 + cn], k == 0, k == 3, [RWg, RsfT[k]], [PB[bk]])
                                ACT(sg[sj][:], bank(bk), AF.Sigmoid, [PB[bk], Rbgl], [Rsg[sj]], bias=bgl[:, m:m + 1])
                                TT("dve", ssmT[:, m, c0:c0 + cn], sg[sj][:], sfT[:, m, c0:c0 + cn], ALU.mult, [Rsg[sj], RsfT[m]], [RssmT[m]])
                        fw.barrier()
                    DBG("d_gy", sfT[:, 0:4, :].rearrange("p a b -> p (a b)"), RsfT[0:4])
                    DBG("d_ssmT", ssmT[:].rearrange("p a b -> p (a b)"), RssmT)
                    if stop == "ssm": fw.dead = True
                    CH[0] = 'M' in os.environ.get('CHP', 'mabfFGMLUS')
                    with contextlib.ExitStack() as M:
                        mergedT = sb(M, "mergedT", [128, 8, TOK], BF16); Rmg = [Res("mg%d" % j) for j in range(8)]
                        with contextlib.ExitStack() as ph:
                            alloc_ws(ph)
                            uT = sb(ph, "uT", [128, 8, TOK], BF16); RuT = [Res("uT%d" % i) for i in range(NT)]
                            gs = [sb(ph, "gs%d" % j, [128, 512]) for j in range(3)]; Rgs = [Res("gs%d" % j) for j in range(3)]
                            ac = [sb(ph, "ac%d" % j, [128, 512]) for j in range(2)]; Rac = [Res("ac0"), Res("ac1")]
                            build_uT(uT, RuT, 0)
                            win = Din["w_in"][l].rearrange("(k p) n -> p k n", p=128)
                            wbr = [Din[nm][l].rearrange("(k p) n -> p k n", p=128) for nm in ("w_br_attn", "w_br_ssm", "w_br_four")]
                            srcs = [(attnT, RattnT), (ssmT, RssmT), (fourT, RfourT)]
                            for c in range(8):
                                Wg, RWg = load_w([((lambda t, g=g: t[:, g * 1024:(g + 1) * 1024].rearrange("p (k n) -> p k n", k=8)),
                                                   win[:, :, 1792 + g * 1024 + c * 128:1792 + g * 1024 + (c + 1) * 128]) for g in range(3)])
                                Wb, RWb = load_w([((lambda t, x=x: t[:, x * 512:(x + 1) * 512].rearrange("p (k n) -> p k n", k=4)),
                                                   wbr[x][:, :, c * 128:(c + 1) * 128]) for x in range(3)])
                                for tb, (c0, cn) in enumerate(BLKS):
                                    for g in range(3):
                                        Wgv = Wg[:, g * 1024:(g + 1) * 1024].rearrange("p (k n) -> p k n", k=8)
                                        for k in range(8):
                                            MM(bank(g), Wgv[:, k, :], uT[:, k, c0:c0 + cn], k == 0, k == 7, RuT[c0 // 128:(c0 + cn) // 128] + [RWg], [PB[g]])
                                        ACT(gs[g][:], bank(g), AF.Sigmoid, [PB[g]], [Rgs[g]])
                                    for x in range(3):
                                        Wbv = Wb[:, x * 512:(x + 1) * 512].rearrange("p (k n) -> p k n", k=4)
                                        st_, Rst = srcs[x]
                                        for k in range(4):
                                            MM(bank(3 + x), Wbv[:, k, :], st_[:, k, c0:c0 + cn], k == 0, k == 3, [Rst[k], RWb], [PB[3 + x]])
                                    TT("dve", ac[0][:], gs[0][:], bank(3), ALU.mult, [Rgs[0], PB[3]], [Rac[0]])
                                    TT("dve", ac[1][:], gs[1][:], bank(4), ALU.mult, [Rgs[1], PB[4]], [Rac[1]])
                                    TT("dve", ac[0][:], ac[0][:], ac[1][:], ALU.add, Rac, [Rac[0]])
                                    TT("dve", ac[1][:], gs[2][:], bank(5), ALU.mult, [Rgs[2], PB[5]], [Rac[1]])
                                    TT("dve", mergedT[:, c, c0:c0 + cn], ac[0][:], ac[1][:], ALU.add, Rac, [Rmg[c]])
                            fw.barrier()
                        DBG("d_mergedT", mergedT[:].rearrange("p a b -> p (a b)"), Rmg)
                        if stop == "merge": fw.dead = True
                        layer_norm_residual(l, 0, 8, lambda i, k: mergedT[:, k, i * 128:(i + 1) * 128], lambda i, k: [Rmg[k]],
                                            Din["w_out"][l].rearrange("(k p) n -> p k n", p=128))
              if "d_x1" in Ddbg:
                  for i in range(NT):
                      fw.dma("sp", Ddbg["d_x1"][i * 128:(i + 1) * 128, :], X[:, i, :], reads=[RX[i]], writes=[Rdbg])
              if stop == "ln1": fw.dead = True
              CH[0] = 'U' in os.environ.get('CHP', 'mabfFGMLUS')
              with contextlib.ExitStack() as Fs:
                  actT = sb(Fs, "actT", [128, 22, TOK], BF16); Ract = [Res("act%d" % j) for j in range(22)]
                  with contextlib.ExitStack() as ph:
                      alloc_ws(ph)
                      uT = sb(ph, "uT", [128, 8, TOK], BF16); RuT = [Res("uT%d" % i) for i in range(NT)]
                      cvp = sb(ph, "cvp", [128, 44, 4]); Rcv = Res("cvp")
                      hc = [sb(ph, "hc%d" % j, [128, TOK]) for j in range(2)]; Rhc = [Res("hc0"), Res("hc1")]
                      fw.dma("sp", cvp[:], Din["convp"][:, l], writes=[Rcv])
                      build_uT(uT, RuT, 1)
                      wup = Din["w_up"][l].rearrange("(k p) n -> p k n", p=128)
                      for ch in range(44):
                          if ch % 4 == 0:
                              W, RW = load_w([(lambda t: t[:, 0:4096].rearrange("p (k n) -> p k n", k=8), wup[:, :, ch * 128:ch * 128 + 512])])
                              Wv = W[:, 0:4096].rearrange("p (k n) -> p k n", k=8)
                          cw = ch % 4; pb0 = 3 * (ch % 2); hj = ch % 2
                          prs = [PB[pb0], PB[pb0 + 1], PB[pb0 + 2]]
                          for tb, (c0, cn) in enumerate(BLKS):
                              for k in range(8):
                                  MM(bank(pb0 + tb), Wv[:, k, cw * 128:(cw + 1) * 128], uT[:, k, c0:c0 + cn], k == 0, k == 7,
                                     RuT[c0 // 128:(c0 + cn) // 128] + [RW], [prs[tb]])
                          hp = bank(pb0, 3); h_ = hc[hj]
                          ACT(h_[:], hp, AF.Identity, prs + [Rcv], [Rhc[hj]], scale=cvp[:, ch, 1:2], bias=cvp[:, ch, 3:4])
                          def stt(o_, i0, sc, i1):
                              OP("dve", lambda e: e.scalar_tensor_tensor(out=o_, in0=i0, scalar=sc, in1=i1, op0=ALU.mult, op1=ALU.add),
                                 prs + [Rcv, Rhc[hj]], [Rhc[hj]])
                          stt(h_[:, 1:1024], hp[:, 0:1023], cvp[:, ch, 0:1], h_[:, 1:1024])
                          stt(h_[:, 0:1023], hp[:, 1:1024], cvp[:, ch, 2:3], h_[:, 0:1023])
                          h3 = h_[:, 1024:1536].rearrange("p (s c) -> p s c", s=2); p3 = hp[:, 1024:1536].rearrange("p (s c) -> p s c", s=2)
                          stt(h3[:, :, 1:256], p3[:, :, 0:255], cvp[:, ch, 0:1], h3[:, :, 1:256])
                          stt(h3[:, :, 0:255], p3[:, :, 1:256], cvp[:, ch, 2:3], h3[:, :, 0:255])
                          if ch < 22:
                              ACT(actT[:, ch, :], h_[:], AF.Silu, [Rhc[hj]], [Ract[ch]])
                          else:
                              TT("dve", actT[:, ch - 22, :], actT[:, ch - 22, :], h_[:], ALU.mult, [Ract[ch - 22], Rhc[hj]], [Ract[ch - 22]])
                      fw.barrier()
                  DBG("d_actT", actT[:].rearrange("p a b -> p (a b)"), Ract)
                  if stop == "ffn": fw.dead = True
                  layer_norm_residual(l, 1, 22, lambda i, k: actT[:, k, i * 128:(i + 1) * 128], lambda i, k: [Ract[k]],
                                      Din["w_down"][l].rearrange("(k p) n -> p k n", p=128))
        layers()
        fw.dead = False
        for i in range(NT):
            fw.dma("sp", Dout["y"][i * 128:(i + 1) * 128, :], X[:, i, :], reads=[RX[i]], writes=[Rdbg])
        fw.finish()
    return nc


def kernel(**inputs):
    inp = {k: np.asarray(v) for k, v in inputs.items()}
    S = prep_shared(inp)
    nc = build()
    in_maps = []
    for cid in range(8):
        m = dict(S); m.update(prep_core(inp, cid))
        in_maps.append(m)
    res = run_bass_kernel_spmd(nc, in_maps, core_ids=list(range(8)))
    y_p = np.zeros((16, 256, 1024), np.float32); y_s = np.zeros((8, 1024, 1024), np.float32)
    nk = np.zeros((16, 2, 256, 2, 64), np.float32); nv = np.zeros((16, 2, 256, 2, 64), np.float32)
    nst = np.zeros((16, 2, 2, 32, 64, 2), np.float32)
    for cid in range(8):
        r = res.results[cid]
        y_s[cid] = r["y"][0:1024]
        y_p[2 * cid:2 * cid + 2] = r["y"][1024:1536].reshape(2, 256, 1024)
        nk[2 * cid:2 * cid + 2] = r["nk"].reshape(2, 2, 256, 2, 64)
        nv[2 * cid:2 * cid + 2] = r["nv"].reshape(2, 2, 256, 2, 64)
        nst[2 * cid:2 * cid + 2] = r["nst"]
    return (y_p, y_s, nk, nv, nst)
```

```python
import contextlib, os
SK = os.environ.get("SK", "")
import numpy as np
import concourse.bass as bass
import concourse.mybir as mybir
from concourse.bass_utils import run_bass_kernel_spmd

F32 = mybir.dt.float32; BF16 = mybir.dt.bfloat16
AF = mybir.ActivationFunctionType; ALU = mybir.AluOpType; AX = mybir.AxisListType

D_MODEL = 1024; DEPTH = 2; D_IN = 4864; D_FF = 2816
ALPHA = (2 * DEPTH) ** 0.25
EPS = 1e-6
NT = 12; TOK = 1536; T = 16
SEQS = [(0, 1024), (1024, 256), (1280, 256)]
BLKS = [(0, 512), (512, 512), (1024, 512)]
TILE_VEC = [0] * 8 + [1] * 4
NS = 99
SOFF = [0, 65, 82]
SCH = [64, 16, 16]


class StopBuild(Exception):
    pass


class Res:
    __slots__ = ("name", "w", "r")
    def __init__(self, name):
        self.name = name; self.w = None; self.r = {}


class Eng:
    def __init__(self, name, h, sem):
        self.name = name; self.h = h; self.sem = sem; self.count = 0; self.waited = {}


class FW:
    NDMA = 32
    def __init__(self, nc, es):
        self.nc = nc; self.sems = {}; self.eng = {}
        for name, h in (("pe", nc.tensor), ("act", nc.scalar), ("dve", nc.vector), ("pool", nc.gpsimd), ("sp", nc.sync)):
            s = es.enter_context(nc.semaphore("s_" + name))
            self.sems[name] = s; self.eng[name] = Eng(name, h, s)
        self.dma_sems = []
        for i in range(self.NDMA):
            s = es.enter_context(nc.semaphore("s_dma%d" % i))
            self.sems["dma%d" % i] = s; self.dma_sems.append(["dma%d" % i, 0])
        self.dma_i = 0; self.ninstr = 0; self.dead = False; self.prev_plain = False

    def _wait(self, e, key, val):
        if e.waited.get(key, 0) >= val: return
        e.h.wait_ge(self.sems[key], val); e.waited[key] = val

    def _deps(self, e, reads, writes, chain=False):
        deps = {}
        def add(kv):
            if kv is None: return
            k, v = kv
            if deps.get(k, 0) < v: deps[k] = v
        for r in reads: add(r.w)
        for w in writes:
            if w.w is not None and not (chain and w.w[0] == e.name):
                add(w.w)
            for k, v in w.r.items():
                if k == e.name: continue
                add((k, v))
        return deps

    def op(self, engname, fn, reads=(), writes=(), chain=False):
        if self.dead: return None
        e = self.eng[engname]
        if engname == "pe" and chain in ("gfirst", "gnext"):
            if chain == "gfirst" and e.count > 0:
                self._wait(e, "pe", e.count)
            chain = True; self.prev_plain = False
        elif engname == "pe":
            plain = chain
            chain = chain and self.prev_plain
            if plain and not chain and e.count > 0:
                self._wait(e, "pe", e.count)
            self.prev_plain = plain
        for k, v in self._deps(e, reads, writes, chain).items(): self._wait(e, k, v)
        ins = fn(e.h); e.count += 1; ins.then_inc(e.sem, 1)
        for r in reads: r.r[e.name] = e.count
        for w in writes: w.w = (e.name, e.count); w.r = {}
        self.ninstr += 1
        return ins

    def dma(self, qname, out, in_, reads=(), writes=()):
        if self.dead: return None
        e = self.eng[qname]
        deps = self._deps(e, reads, writes)
        slot = self.dma_sems[self.dma_i % self.NDMA]; self.dma_i += 1
        key = slot[0]
        if slot[1] > 0: deps[key] = max(deps.get(key, 0), slot[1])
        for k, v in deps.items(): self._wait(e, k, v)
        ins = e.h.dma_start(out=out, in_=in_)
        slot[1] += 16; ins.then_inc(self.sems[key], 16)
        for r in reads: r.r[key] = slot[1]
        for w in writes: w.w = (key, slot[1]); w.r = {}
        self.ninstr += 1
        return ins

    def barrier(self):
        if self.dead: return
        targets = {n: e.count for n, e in self.eng.items() if e.count > 0}
        for key, cnt in self.dma_sems:
            if cnt > 0: targets[key] = cnt
        for n, e in self.eng.items():
            for k, v in targets.items(): self._wait(e, k, v)

    def finish(self):
        self.dead = False
        self.barrier()


def bl(ap, n):
    return ap.unsqueeze(2).to_broadcast([ap.shape[0], ap.shape[1], n])


def bm(ap, n):
    return ap.unsqueeze(1).to_broadcast([ap.shape[0], n, ap.shape[1]])


def _rep(a, n=128):
    return np.ascontiguousarray(np.broadcast_to(a[None], (n,) + a.shape))


def prep_shared(inp):
    f = lambda a: np.ascontiguousarray(a, dtype=np.float32)
    L = DEPTH
    S = {}
    for k in ("w_ada", "w_in", "w_glu", "w_br_ssm", "w_br_four", "w_out", "w_up", "w_down"):
        S[k] = f(inp[k])
    S["w_br_attn"] = f(inp["w_br_attn"])
    S["b_adaP"] = f(inp["b_ada"].reshape(L, 48, 128).transpose(2, 0, 1))
    S["bgbc"] = f(np.stack([np.stack([_rep(inp["b_ada"][l, 2048:3072]), _rep(inp["b_ada"][l, 5120:6144])], 1) for l in range(L)]))
    S["lnbc"] = f(np.stack([np.stack([_rep(inp[k][l]) for k in ("ln1_g", "ln1_b", "ln2_g", "ln2_b")]) for l in range(L)]))
    cw = inp["conv_w"].reshape(L, 3, 44, 128).transpose(3, 0, 2, 1)
    cb = inp["conv_b"].reshape(L, 44, 128).transpose(2, 0, 1)[..., None]
    S["convp"] = f(np.concatenate([cw, cb], -1))
    S["bglu"] = f(inp["b_glu"].reshape(L, 4, 128).transpose(2, 0, 1))
    S["g640"] = f(np.stack([_rep(np.concatenate([np.tile(inp["q_norm_g"][l], 8), np.tile(inp["k_norm_g"][l], 4)])) for l in range(L)]))
    def lay(a):
        aP = a.reshape(L, 2, 16, 2, 64).transpose(0, 1, 3, 4, 2).reshape(L, 2, 128, 16)
        aF = a.reshape(L, 2, 4, 4, 2, 64).transpose(0, 1, 3, 2, 4, 5)
        aF = np.broadcast_to(aF[:, :, :, None], (L, 2, 4, 32, 4, 2, 64)).reshape(L, 2, 128, 512)
        return np.concatenate([aP, aF], -1)
    ldt = np.broadcast_to(inp["ssm_log_dt"][..., None], (L, 2, 32, 64))
    S["lamin"] = f(np.stack([lay(inp["ssm_a_re"]), lay(inp["ssm_a_im"]), lay(ldt)], 2))
    def bbdP(b):
        Bp = b.reshape(L, 2, 16, 2, 64, 16)
        out = np.zeros((L, 2, 2, 64, 16, 2, 16), np.float32)
        for g2 in range(2):
            out[:, :, g2, :, :, g2, :] = Bp[:, :, :, g2].transpose(0, 1, 3, 2, 4)
        return out.reshape(L, 2, 128, 16, 32)
    def bbdT(b):
        Bq = b.reshape(L, 2, 4, 4, 2, 64, 16)
        out = np.zeros((L, 2, 4, 2, 16, 4, 2, 64), np.float32)
        for g2 in range(2):
            out[:, :, :, g2, :, :, g2, :] = Bq[:, :, :, :, g2].transpose(0, 1, 3, 5, 2, 4)
        return out.reshape(L, 2, 128, 4, 128)
    def cbdP(c):
        Cp = c.reshape(L, 2, 16, 2, 16, 64)
        out = np.zeros((L, 2, 2, 64, 16, 2, 16), np.float32)
        for g2 in range(2):
            out[:, :, g2, :, :, g2, :] = Cp[:, :, :, g2].transpose(0, 1, 4, 2, 3)
        return out.reshape(L, 2, 128, 16, 32)
    S["BbdP"] = f(np.stack([bbdP(inp["ssm_b_re"]), bbdP(inp["ssm_b_im"])], 2))
    S["BbdT"] = f(np.stack([bbdT(inp["ssm_b_re"]), bbdT(inp["ssm_b_im"])], 2))
    S["CbdP"] = f(np.stack([cbdP(inp["ssm_c_re"]), cbdP(inp["ssm_c_im"])], 2))
    Dblk = np.zeros((L, 128, 4, 128), np.float32)
    for l in range(L):
        for q in range(4):
            Dblk[l, np.arange(128), q, np.arange(128)] = inp["ssm_d"][l, q * 128:(q + 1) * 128]
    S["Dblk"] = Dblk
    S["ident"] = np.eye(128, dtype=np.float32)
    mk = np.zeros((128, 4), np.float32); mk[np.arange(128), np.arange(128) // 32] = 1.0
    S["maskP"] = mk
    t = np.arange(1024)
    freqs = (10000.0 ** (-np.arange(16, dtype=np.float32) / 16)).astype(np.float32)
    ang = np.concatenate([(t // 64).astype(np.float32)[:, None] * freqs, (t % 64).astype(np.float32)[:, None] * freqs], -1)
    S["ropec"] = f(np.cos(ang).reshape(8, 128, 32).transpose(1, 0, 2))
    S["ropes"] = f(np.sin(ang).reshape(8, 128, 32).transpose(1, 0, 2))
    def dft(n):
        i = np.arange(n, dtype=np.int64)
        m = (i[:, None] * i[None, :]) % n
        a = 2.0 * np.pi * m.astype(np.float64) / n
        return np.cos(a) / np.sqrt(n), np.sin(a) / np.sqrt(n)
    c128, s128 = dft(128)
    S["cs128"] = f(np.concatenate([c128, s128], 1))
    for n in (1024, 256):
        c, s = dft(n)
        S["cl%d" % n] = f(c.reshape(n // 128, 128, n).transpose(1, 0, 2))
        S["sl%d" % n] = f((-s).reshape(n // 128, 128, n).transpose(1, 0, 2))
    return S


def prep_core(inp, cid):
    f = lambda a: np.ascontiguousarray(a, dtype=np.float32)
    m = {}
    m["xin"] = f(np.concatenate([inp["x_sample"][cid], inp["x_prompt"][2 * cid:2 * cid + 2].reshape(512, 1024)], 0))
    cv = np.stack([inp["c"][cid], inp["c_ctx"]], -1)
    m["cvec"] = f(cv.reshape(8, 128, 2).transpose(1, 0, 2))
    m["ckT"] = f(inp["cache_k"][cid].reshape(DEPTH, 512, 128).transpose(0, 2, 1))
    m["cv"] = f(inp["cache_v"][cid].reshape(DEPTH, 512, 128))
    st = inp["state_ssm"][cid].reshape(DEPTH, 2, 16, 2, 64, 2)
    m["h0"] = f(st.transpose(3, 4, 0, 1, 5, 2).reshape(128, DEPTH, 2, 2, 16))
    return m


IN_SHAPES = {
    "xin": (1536, 1024), "cvec": (128, 8, 2), "ckT": (2, 128, 512), "cv": (2, 512, 128), "h0": (128, 2, 2, 2, 16),
    "w_ada": (2, 1024, 6144), "w_in": (2, 1024, 4864), "w_glu": (2, 512, 512), "w_br_attn": (2, 512, 1024),
    "w_br_ssm": (2, 512, 1024), "w_br_four": (2, 512, 1024), "w_out": (2, 1024, 1024), "w_up": (2, 1024, 5632),
    "w_down": (2, 2816, 1024), "b_adaP": (128, 2, 48), "bgbc": (2, 128, 2, 1024), "lnbc": (2, 4, 128, 1024),
    "convp": (128, 2, 44, 4), "bglu": (128, 2, 4), "g640": (2, 128, 768), "lamin": (2, 2, 3, 128, 528),
    "BbdP": (2, 2, 2, 128, 16, 32), "BbdT": (2, 2, 2, 128, 4, 128), "CbdP": (2, 2, 2, 128, 16, 32),
    "Dblk": (2, 128, 4, 128), "ident": (128, 128), "maskP": (128, 4), "ropec": (128, 8, 32), "ropes": (128, 8, 32),
    "cs128": (128, 256), "cl1024": (128, 8, 1024), "sl1024": (128, 8, 1024), "cl256": (128, 2, 256), "sl256": (128, 2, 256),
}
OUT_SHAPES = {"y": (1536, 1024), "nk": (2, 2, 256, 128), "nv": (2, 2, 256, 128), "nst": (2, 2, 2, 32, 64, 2)}


def build(n_layers=DEPTH, stop=None, dbg_shapes=None):
    nc = bass.Bass("TRN2", target_bir_lowering=False)
    Din = {k: nc.dram_tensor(k, list(s), F32, kind="ExternalInput").ap() for k, s in IN_SHAPES.items()}
    Dout = {k: nc.dram_tensor(k, list(s), F32, kind="ExternalOutput").ap() for k, s in OUT_SHAPES.items()}
    Ddbg = {k: nc.dram_tensor(k, list(s), F32, kind="ExternalOutput").ap() for k, s in (dbg_shapes or {}).items()}
    es = contextlib.ExitStack()
    with es:
        fw = FW(nc, es)
        uid = [0]
        def sb(st, name, shape, dt=F32):
            uid[0] += 1
            return st.enter_context(nc.sbuf_tensor("%s_%d" % (name, uid[0]), list(shape), dt))
        OP = fw.op
        CH = [True]
        def MM(out, lhsT, rhs, start, stop_, r, w, tp=None, chain=None):
            if chain is None: chain = CH[0] and (tp is None)
            if tp is None:
                return fw.op("pe", lambda e: e.matmul(out, lhsT=lhsT, rhs=rhs, start=start, stop=stop_), r, w, chain=chain)
            return fw.op("pe", lambda e: e.matmul(out, lhsT=lhsT, rhs=rhs, start=start, stop=stop_, tile_position=tp), r, w, chain=chain)
        def TT(eng, out, in0, in1, op, r, w):
            return fw.op(eng, lambda e: e.tensor_tensor(out=out, in0=in0, in1=in1, op=op), r, w)
        def TS(eng, out, in0, s1, op0, r, w, s2=None, op1=None):
            if op1 is None:
                return fw.op(eng, lambda e: e.tensor_scalar(out=out, in0=in0, scalar1=s1, scalar2=None, op0=op0), r, w)
            return fw.op(eng, lambda e: e.tensor_scalar(out=out, in0=in0, scalar1=s1, scalar2=s2, op0=op0, op1=op1), r, w)
        def ACT(out, in_, func, r, w, scale=1.0, bias=0.0):
            return fw.op("act", lambda e: e.activation(out=out, in_=in_, func=func, bias=bias, scale=scale), r, w)
        def CP(eng, out, in_, r, w):
            if eng == "act":
                return fw.op("act", lambda e: e.copy(out=out, in_=in_), r, w)
            return fw.op(eng, lambda e: e.tensor_copy(out=out, in_=in_), r, w)
        Rdbg = Res("dbg")
        def DBG(name, ap, r):
            if name in Ddbg:
                fw.dma("pool", Ddbg[name], ap, reads=r, writes=[Rdbg])

        X = sb(es, "X", [128, NT, 1024]); RX = [Res("X%d" % i) for i in range(NT)]
        ident_f = sb(es, "ident_f", [128, 128]); ident_b = sb(es, "ident_b", [128, 128], BF16); Rid = Res("id")
        maskP = sb(es, "maskP", [128, 4]); Rmask = Res("mask")
        modP = sb(es, "modP", [128, 6, 8, 2]); opsc = sb(es, "opsc", [128, 2, 8, 2]); Rmod = Res("mod")
        csl = sb(es, "csl", [128, 8, 2]); sTb = sb(es, "sTb", [128, 8, 2], BF16); sRep = sb(es, "sRep", [128, 2, 8, 128], BF16)
        Rcs = Res("cs")
        stat = sb(es, "stat", [128, NT, 2, 6]); mv = sb(es, "mv", [128, NT, 2]); rs = sb(es, "rs", [128, NT, 2]); Rstat = Res("stat")
        ps = es.enter_context(nc.psum_tensor("ps", [128, 3584], F32)); PB = [Res("pb%d" % i) for i in range(7)]
        pst = es.enter_context(nc.psum_tensor("pst", [128, 1024], BF16)); PT = Res("pt")
        wslot = [None, None]; Rws = [None, None]
        wctr = [0]
        def alloc_ws(st):
            for i_ in range(2):
                wslot[i_] = sb(st, "wslot%d" % i_, [128, 4096], BF16); Rws[i_] = Res("ws%d" % i_)
        def load_w(views):
            s_ = wctr[0] % 2; wctr[0] += 1
            for dstf, src in views:
                fw.dma("pool", dstf(wslot[s_]), src, writes=[Rws[s_]])
            return wslot[s_], Rws[s_]
        def bank(i, n=1):
            return ps[:, i * 512:(i + n) * 512]

        fw.dma("sp", ident_f[:], Din["ident"], writes=[Rid])
        fw.dma("pool", ident_b[:], Din["ident"], writes=[Rid])
        fw.dma("sp", maskP[:], Din["maskP"], writes=[Rmask])
        for i in range(NT):
            fw.dma("sp", X[:, i, :], Din["xin"][i * 128:(i + 1) * 128, :], writes=[RX[i]])
        fw.dma("sp", csl[:], Din["cvec"], writes=[Rcs])
        ACT(csl[:], csl[:], AF.Silu, [Rcs], [Rcs])
        CP("dve", sTb[:], csl[:], [Rcs], [Rcs])
        for v in range(2):
            CP("dve", sRep[:, v, :, :], bl(csl[:, :, v], 128), [Rcs], [Rcs])

        def build_uT(uT, RuT, sub):
            sh_sec = 0 if sub == 0 else 3
            for i in range(NT):
                v = TILE_VEC[i]
                for kk in range(2):
                    bk = kk
                    for k4 in range(4):
                        k = kk * 4 + k4
                        OP("pe", lambda e: e.transpose(bank(bk)[:, k4 * 128:(k4 + 1) * 128], X[:, i, k * 128:(k + 1) * 128], ident_f[:]),
                           [RX[i], Rid], [PB[bk]])
                    for k4 in range(4):
                        k = kk * 4 + k4
                        ACT(uT[:, k, i * 128:(i + 1) * 128], bank(bk)[:, k4 * 128:(k4 + 1) * 128], AF.Identity,
                            [PB[bk], Rmod], [RuT[i]], scale=opsc[:, sub, k, v:v + 1], bias=modP[:, sh_sec, k, v:v + 1])

        def layer_norm_residual(l, which, kc, lhsT_fn, lres_fn, wsrc3):
            with contextlib.ExitStack() as ph:
                ch_save = CH[0]; CH[0] = 'L' in os.environ.get('CHP', 'mabfFGMLUS')
                alloc_ws(ph)
                lnb = sb(ph, "lnb", [128, 2, 1024]); Rln = Res("lnb")
                tmp = [sb(ph, "lntmp%d" % j, [128, 512]) for j in range(2)]; Rtmp = [Res("lntmp0"), Res("lntmp1")]
                Wd = sb(ph, "Wd", [128, kc, 512], BF16); RWd = Res("Wd")
                fw.dma("sp", lnb[:, 0, :], Din["lnbc"][l, 2 * which], writes=[Rln])
                fw.dma("sp", lnb[:, 1, :], Din["lnbc"][l, 2 * which + 1], writes=[Rln])
                gbc = sb(ph, "gbc", [128, 2, 1024]); Rgbc = Res("gbc")
                bg = sb(ph, "bg", [128, 1024]); Rbg = Res("bg")
                fw.dma("sp", bg[:], Din["bgbc"][l, :, which, :], writes=[Rbg])
                sec = 2 if which == 0 else 5
                wsrc = Din["w_ada"][l].rearrange("(k p) n -> p k n", p=128)
                for half in range(2):
                    Wg, RWg = load_w([(lambda t: t[:, 0:4096].rearrange("p (k n) -> p k n", k=8), wsrc[:, :, sec * 1024 + half * 512:sec * 1024 + (half + 1) * 512])])
                    Wgv = Wg[:, 0:4096].rearrange("p (k n) -> p k n", k=8)
                    for v in range(2):
                        for k in range(8):
                            MM(bank(v), sRep[:, v, k, :], Wgv[:, k, :], k == 0, k == 7, [RWg, Rcs], [PB[v]])
                        TT("dve", gbc[:, v, half * 512:(half + 1) * 512], bank(v), bg[:, half * 512:(half + 1) * 512], ALU.add, [PB[v], Rbg], [Rgbc])
                cnt = 0
                for nb in range(2):
                    cs = slice(nb * 512, (nb + 1) * 512)
                    for k0 in range(0, kc, 8):
                        k1 = min(kc, k0 + 8)
                        fw.dma("pool", Wd[:, k0:k1, :], wsrc3[:, k0:k1, cs], writes=[RWd])
                    for i in range(NT):
                        v = TILE_VEC[i]
                        bk = 2 + (cnt % 4); tj = cnt % 2; cnt += 1
                        for k in range(kc):
                            MM(bank(bk), lhsT_fn(i, k), Wd[:, k, :], k == 0, k == kc - 1, lres_fn(i, k) + [RWd], [PB[bk]])
                        TT("dve", tmp[tj][:], bank(bk), gbc[:, v, cs], ALU.mult, [PB[bk], Rgbc], [Rtmp[tj]])
                        OP("dve", lambda e: e.scalar_tensor_tensor(out=X[:, i, cs], in0=X[:, i, cs], scalar=float(ALPHA), in1=tmp[tj][:],
                                                                   op0=ALU.mult, op1=ALU.add), [RX[i], Rtmp[tj]], [RX[i]])
                for i in range(NT):
                    for hh in range(2):
                        OP("dve", lambda e: e.bn_stats(out=stat[:, i, hh, :], in_=X[:, i, hh * 512:(hh + 1) * 512]), [RX[i]], [Rstat])
                    OP("dve", lambda e: e.bn_aggr(out=mv[:, i, :], in_=stat[:, i, :, :].rearrange("p a b -> p (a b)")), [Rstat], [Rstat])
                TS("dve", rs[:, :, 0], mv[:, :, 1], EPS, ALU.add, [Rstat], [Rstat])
                ACT(rs[:, :, 0], rs[:, :, 0], AF.Sqrt, [Rstat], [Rstat])
                OP("dve", lambda e: e.reciprocal(out=rs[:, :, 0], in_=rs[:, :, 0]), [Rstat], [Rstat])
                OP("dve", lambda e: e.scalar_tensor_tensor(out=rs[:, :, 1], in0=mv[:, :, 0], scalar=-1.0, in1=rs[:, :, 0],
                                                           op0=ALU.mult, op1=ALU.mult), [Rstat], [Rstat])
                for i in range(NT):
                    ACT(X[:, i, :], X[:, i, :], AF.Identity, [RX[i], Rstat], [RX[i]], scale=rs[:, i, 0:1], bias=rs[:, i, 1:2])
                    TT("pool", X[:, i, :], X[:, i, :], lnb[:, 0, :], ALU.mult, [RX[i], Rln], [RX[i]])
                    TT("pool", X[:, i, :], X[:, i, :], lnb[:, 1, :], ALU.add, [RX[i], Rln], [RX[i]])
                fw.barrier()
                CH[0] = ch_save

        def layers():
          for l in range(n_layers):
              with contextlib.ExitStack() as ph:
                  wa = [sb(ph, "wa%d" % i, [128, 8, 512], BF16) for i in range(3)]; Rwa = [Res("wa%d" % i) for i in range(3)]
                  bP = sb(ph, "bP", [128, 48]); Rb = Res("bmod")
                  fw.dma("sp", bP[:], Din["b_adaP"][:, l, :], writes=[Rb])
                  wsrc = Din["w_ada"][l].rearrange("(k p) n -> p k n", p=128)
                  pm = bank(6)[:, 0:96]
                  CH[0] = 'm' in os.environ.get('CHP', 'mabfFGMLUS')
                  for cb in range(12):
                      s = cb % 3
                      sec, half = cb // 2, cb % 2
                      if sec in (2, 5):
                          continue
                      fw.dma("pool", wa[s][:], wsrc[:, :, cb * 512:(cb + 1) * 512], writes=[Rwa[s]])
                      for m in range(4):
                          col = (sec * 8 + half * 4 + m) * 2
                          for k in range(8):
                              MM(pm[:, col:col + 2], wa[s][:, k, m * 128:(m + 1) * 128], sTb[:, k, :], k == 0, k == 7,
                                 [Rwa[s], Rcs], [PB[6]])
                  for sec in (0, 1, 3, 4):
                      TT("dve", modP[:, sec, :, :], pm[:, sec * 16:(sec + 1) * 16].rearrange("p (c v) -> p c v", v=2),
                         bl(bP[:, sec * 8:(sec + 1) * 8], 2), ALU.add, [PB[6], Rb], [Rmod])
                  TS("dve", opsc[:, 0, :, :], modP[:, 1, :, :], 1.0, ALU.add, [Rmod], [Rmod])
                  TS("dve", opsc[:, 1, :, :], modP[:, 4, :, :], 1.0, ALU.add, [Rmod], [Rmod])
                  fw.barrier()
              if stop == "mod": fw.dead = True
              DBG("d_modP", modP[:].rearrange("p a b c -> p (a b c)"), [Rmod])

              with contextlib.ExitStack() as L1:
                  QT = sb(L1, "QT", [128, 4, TOK], BF16); RQ = [Res("QT%d" % j) for j in range(4)]; attnT = QT; RattnT = RQ
                  sfT = sb(L1, "sfT", [128, 8, TOK], BF16); RsfT = [Res("sfT%d" % j) for j in range(8)]
                  with contextlib.ExitStack() as L2:
                      KT = sb(L2, "KT", [128, 2, 2048], BF16); RKT = Res("KT")
                      Vsb = sb(L2, "Vsb", [128, 16, 2, 2, 128], BF16); RV = Res("V")
                      with contextlib.ExitStack() as ph:
                          alloc_ws(ph)
                          uT = sb(ph, "uT", [128, 8, TOK], BF16); RuT = [Res("uT%d" % i) for i in range(NT)]
                          qn = [sb(ph, "qn%d" % j, [128, 512]) for j in range(2)]; Rqn = [Res("qn0"), Res("qn1")]
                          vf = [sb(ph, "vf%d" % j, [128, 128]) for j in range(2)]; Rvf = [Res("vf0"), Res("vf1")]
                          qkb = [sb(ph, "qkb%d" % j, [128, 512], BF16) for j in range(2)]; Rqkb = [Res("qkb0"), Res("qkb1")]
                          sq = sb(ph, "sq", [128, 512]); Rsq = Res("sq")
                          kb2 = [sb(ph, "kb2%d" % j, [128, 128], BF16) for j in range(2)]; Rkb2 = [Res("kb20"), Res("kb21")]
                          ssq = sb(ph, "ssq", [128, 8]); Rssq = Res("ssq")
                          g640 = sb(ph, "g640", [128, 768]); rc = sb(ph, "rc", [128, 8, 32]); rsn = sb(ph, "rsn", [128, 8, 32]); Rcst = Res("cst")
                          rt = [sb(ph, "rt%d" % j, [128, 8, 32]) for j in range(2)]; Rrt = [Res("rt0"), Res("rt1")]
                          win = Din["w_in"][l].rearrange("(k p) n -> p k n", p=128)
                          fw.dma("sp", g640[:], Din["g640"][l], writes=[Rcst])
                          fw.dma("sp", rc[:], Din["ropec"], writes=[Rcst])
                          fw.dma("sp", rsn[:], Din["ropes"], writes=[Rcst])
                          fw.dma("pool", KT[:, 0, 1024:1536], Din["ckT"][l], writes=[RKT])
                          fw.dma("pool", KT[0:64, 1, 1024:1536], Din["ckT"][l, 64:128, :], writes=[RKT])
                          fw.dma("pool", KT[64:128, 1, 1024:1536], Din["ckT"][l, 0:64, :], writes=[RKT])
                          OP("dve", lambda e: e.memset(Vsb[:].rearrange("p a b c d -> p (a b c d)"), 1.0), [], [RV])
                          cvv = Din["cv"][l].rearrange("(t p) (h d) -> p t h d", p=128, h=2)
                          for lay in range(2):
                              for kvh_ in range(2):
                                  fw.dma("pool", Vsb[:, 8:12, kvh_, lay, lay * 64:(lay + 1) * 64], cvv[:, :, kvh_, :], writes=[RV])
                          build_uT(uT, RuT, 0)
                          if stop == "ut": fw.dead = True
                          DBG("d_uT", uT[:].rearrange("p a b -> p (a b)"), RuT)

                          def norm_rope(i, pa, pr, nh, gap, outb, Routb, slot):
                              w_ = nh * 64
                              ACT(sq[:, 0:w_], pa, AF.Square, pr, [Rsq])
                              OP("dve", lambda e: e.reduce_sum(out=ssq[:, 0:nh], in_=sq[:, 0:w_].rearrange("p (h d) -> p h d", d=64), axis=AX.X), [Rsq], [Rssq])
                              TS("dve", ssq[:, 0:nh], ssq[:, 0:nh], 1.0 / 64, ALU.mult, [Rssq], [Rssq], s2=EPS, op1=ALU.add)
                              ACT(ssq[:, 0:nh], ssq[:, 0:nh], AF.Sqrt, [Rssq], [Rssq])
                              OP("dve", lambda e: e.reciprocal(out=ssq[:, 0:nh], in_=ssq[:, 0:nh]), [Rssq], [Rssq])
                              qv = qn[slot][:, 0:w_]
                              TT("dve", qv.rearrange("p (h d) -> p h d", d=64), pa.rearrange("p (h d) -> p h d", d=64), bl(ssq[:, 0:nh], 64),
                                 ALU.mult, pr + [Rssq], [Rqn[slot]])
                              TT("dve", qv, qv, gap, ALU.mult, [Rqn[slot], Rcst], [Rqn[slot]])
                              if i >= 8:
                                  CP("act", outb, qv, [Rqn[slot]], [Routb])
                              else:
                                  x4 = qv.rearrange("p (h d two) -> p h d two", d=32, two=2)
                                  o4 = outb.rearrange("p (h d two) -> p h d two", d=32, two=2)
                                  x0, x1 = x4[:, :, :, 0], x4[:, :, :, 1]
                                  cc, ss_ = bm(rc[:, i, :], nh), bm(rsn[:, i, :], nh)
                                  r0, r1 = rt[0][:, 0:nh, :], rt[1][:, 0:nh, :]
                                  TT("dve", r0, x0, cc, ALU.mult, [Rqn[slot], Rcst], [Rrt[0]])
                                  TT("pool", r1, x1, ss_, ALU.mult, [Rqn[slot], Rcst], [Rrt[1]])
                                  TT("dve", o4[:, :, :, 0], r0, r1, ALU.subtract, Rrt, [Routb])
                                  TT("dve", r0, x0, ss_, ALU.mult, [Rqn[slot], Rcst], [Rrt[0]])
                                  TT("pool", r1, x1, cc, ALU.mult, [Rqn[slot], Rcst], [Rrt[1]])
                                  TT("dve", o4[:, :, :, 1], r0, r1, ALU.add, Rrt, [Routb])

                          CH[0] = 'a' in os.environ.get('CHP', 'mabfFGMLUS')
                          W, RW = load_w([(lambda t: t[:, 0:4096].rearrange("p (k n) -> p k n", k=8), win[:, :, 0:512])])
                          Wv = W[:, 0:4096].rearrange("p (k n) -> p k n", k=8)
                          for i in range(NT):
                              bk = 2 + (i % 2); sl = i % 2
                              for k in range(8):
                                  MM(bank(bk), uT[:, k, i * 128:(i + 1) * 128], Wv[:, k, :], k == 0, k == 7, [RuT[i], RW], [PB[bk]])
                              norm_rope(i, bank(bk), [PB[bk]], 8, g640[:, 0:512], qkb[sl][:, :], Rqkb[sl], sl)
                              for j in range(4):
                                  src = qkb[sl][:, j * 128:(j + 1) * 128]
                                  OP("pe", lambda e: e.transpose(pst[:, j * 128:(j + 1) * 128], src, ident_b[:]), [Rqkb[sl], Rid], [PT])
                              CP("act", QT[:, :, i * 128:(i + 1) * 128], pst[:, 0:512].rearrange("p (j c) -> p j c", j=4), [PT], RQ)
                          if stop == "passA": fw.dead = True
                          CH[0] = 'b' in os.environ.get('CHP', 'mabfFGMLUS')
                          W, RW = load_w([(lambda t: t[:, 0:2048].rearrange("p (k n) -> p k n", k=8), win[:, :, 512:768])])
                          Wv = W[:, 0:2048].rearrange("p (k n) -> p k n", k=8)
                          for i in range(NT):
                              bk = 2 + (i % 2); sl = i % 2
                              for k in range(8):
                                  MM(bank(bk)[:, 0:256], uT[:, k, i * 128:(i + 1) * 128], Wv[:, k, :], k == 0, k == 7, [RuT[i], RW], [PB[bk]])
                              norm_rope(i, bank(bk)[:, 0:256], [PB[bk]], 4, g640[:, 512:768], qkb[sl][:, 0:256], Rqkb[sl], sl)
                              kt = i if i < 8 else 12 + (i - 8)
                              for lay in range(2):
                                  CP("act", Vsb[:, kt, :, lay, lay * 64:(lay + 1) * 64], bank(bk)[:, 128:256].rearrange("p (h d) -> p h d", h=2), [PB[bk]], [RV])
                              if i >= 8:
                                  p_, t_ = (i - 8) // 2, (i - 8) % 2
                                  CP("act", vf[sl][:], bank(bk)[:, 128:256], [PB[bk]], [Rvf[sl]])
                                  fw.dma("sp", Dout["nk"][p_, l, t_ * 128:(t_ + 1) * 128, :], qn[sl][:, 0:128], reads=[Rqn[sl]], writes=[Rdbg])
                                  fw.dma("sp", Dout["nv"][p_, l, t_ * 128:(t_ + 1) * 128, :], vf[sl][:], reads=[Rvf[sl]], writes=[Rdbg])
                              CP("act", kb2[sl][:, 0:64], qkb[sl][:, 64:128], [Rqkb[sl]], [Rkb2[sl]])
                              CP("act", kb2[sl][:, 64:128], qkb[sl][:, 0:64], [Rqkb[sl]], [Rkb2[sl]])
                              OP("pe", lambda e: e.transpose(pst[:, 512:640], qkb[sl][:, 0:128], ident_b[:]), [Rqkb[sl], Rid], [PT])
                              OP("pe", lambda e: e.transpose(pst[:, 640:768], kb2[sl][:, :], ident_b[:]), [Rkb2[sl], Rid], [PT])
                              kc = i * 128 if i < 8 else 1536 + (i - 8) * 128
                              CP("act", KT[:, 0, kc:kc + 128], pst[:, 512:640], [PT], [RKT])
                              CP("act", KT[:, 1, kc:kc + 128], pst[:, 640:768], [PT], [RKT])
                          if stop == "passB": fw.dead = True
                          CH[0] = 'f' in os.environ.get('CHP', 'mabfFGMLUS')
                          for wb in range(2):
                              W, RW = load_w([(lambda t: t[:, 0:4096].rearrange("p (k n) -> p k n", k=8), win[:, :, 768 + wb * 512:1280 + wb * 512])])
                              Wv = W[:, 0:4096].rearrange("p (k n) -> p k n", k=8)
                              for m4 in range(4):
                                  m = wb * 4 + m4
                                  for tb, (c0, cn) in enumerate(BLKS):
                                      bk = (0, 1, 6)[(m * 3 + tb) % 3]
                                      for k in range(8):
                                          MM(bank(bk), Wv[:, k, m4 * 128:(m4 + 1) * 128], uT[:, k, c0:c0 + cn], k == 0, k == 7,
                                             RuT[c0 // 128:(c0 + cn) // 128] + [RW], [PB[bk]])
                                      CP("act", sfT[:, m, c0:c0 + cn], bank(bk), [PB[bk]], [RsfT[m]])
                          DBG("d_sfT", sfT[:].rearrange("p a b -> p (a b)"), RsfT)
                          fw.barrier()
                      if stop == "inproj": fw.dead = True
                      DBG("d_QT", QT[:].rearrange("p a b -> p (a b)"), RQ)
                      DBG("d_KT", KT[:].rearrange("p a b -> p (a b)"), [RKT])
                      CH[0] = 't' in os.environ.get('CHP', 'mabfFGMLUS')
                      with contextlib.ExitStack() as ph:
                          Eb = [sb(ph, "Eb%d" % j, [128, 512], BF16) for j in range(3)]; REb = [Res("Eb%d" % j) for j in range(3)]
                          rsb = [sb(ph, "rsb%d" % j, [128, 512]) for j in range(2)]; Rrsb = [Res("rsb0"), Res("rsb1")]
                          cnt = 0; cnt2 = 0
                          for si, (t0, Ls) in enumerate(SEQS):
                              if si == 0:
                                  qblks = [(0, 512), (512, 512)]; nkt = 12; kt0 = 0; kc0 = 0
                              else:
                                  qblks = [(t0, 256)]; nkt = 2; kt0 = 12 + 2 * (si - 1); kc0 = 1536 + 256 * (si - 1)
                              for h in range(8):
                                  j = h // 2; hh = h % 2; kvh = h // 4; var = 0 if kvh == hh else 1
                                  pv, sm = (slice(0, 64), slice(64, 128)) if hh == 0 else (slice(64, 128), slice(0, 64))
                                  for (q0, qn) in qblks:
                                      pvb = 3 + (cnt2 % 2); cnt2 += 1
                                      def S_mm(kt, sbk):
                                          MM(bank(sbk)[:, 0:qn], KT[hh * 64:(hh + 1) * 64, var, kc0 + kt * 128:kc0 + (kt + 1) * 128],
                                             QT[hh * 64:(hh + 1) * 64, j, q0:q0 + qn], True, True, [RKT, RQ[j]], [PB[sbk]], chain=False)
                                      slots = [(cnt + i_) % 3 for i_ in range(nkt)]; cnt += nkt
                                      S_mm(0, slots[0])
                                      for kt in range(nkt):
                                          sbk = slots[kt]; eb = sbk
                                          if kt + 1 < nkt:
                                              S_mm(kt + 1, slots[kt + 1])
                                          ACT(Eb[eb][:, 0:qn], bank(sbk)[:, 0:qn], AF.Exp, [PB[sbk]], [REb[eb]], scale=0.125)
                                          MM(bank(pvb)[:, 0:qn], Vsb[:, kt0 + kt, kvh, hh, :], Eb[eb][:, 0:qn], kt == 0, kt == nkt - 1,
                                             [RV, REb[eb]], [PB[pvb]], chain=False)
                                      rb = cnt2 % 2
                                      ACT(rsb[rb][sm, 0:qn], bank(pvb)[sm, 0:qn], AF.Ln, [PB[pvb]], [Rrsb[rb]])
                                      ACT(rsb[rb][sm, 0:qn], rsb[rb][sm, 0:qn], AF.Exp, [Rrsb[rb]], [Rrsb[rb]], scale=-1.0)
                                      TT("dve", attnT[pv, j, q0:q0 + qn], bank(pvb)[pv, 0:qn], rsb[rb][sm, 0:qn], ALU.mult,
                                         [PB[pvb], Rrsb[rb]], [RattnT[j]])
                          fw.barrier()
                  DBG("d_attnT", attnT[:].rearrange("p a b -> p (a b)"), RattnT)
                  CH[0] = 'F' in os.environ.get('CHP', 'mabfFGMLUS')
                  with contextlib.ExitStack() as L3:
                    fourT = sb(L3, "fourT", [128, 4, TOK], BF16); RfourT = [Res("fourT%d" % j) for j in range(4)]
                    with contextlib.ExitStack() as ph:
                        cs128 = sb(ph, "cs128", [128, 256], BF16); cl = sb(ph, "cl", [128, 8, 1024], BF16); sl_ = sb(ph, "sl", [128, 8, 1024], BF16)
                        clp = sb(ph, "clp", [128, 2, 256], BF16); slp = sb(ph, "slp", [128, 2, 256], BF16); Rft = Res("ftab")
                        Pfm = sb(ph, "Pfm", [128, NT, 1024], BF16); RPfm = [Res("Pfm%d" % i) for i in range(NT)]
                        fw.dma("pool", cs128[:], Din["cs128"], writes=[Rft])
                        for k in range(8):
                            fw.dma("pool", cl[:, k, :], Din["cl1024"][:, k, :], writes=[Rft])
                            fw.dma("pool", sl_[:, k, :], Din["sl1024"][:, k, :], writes=[Rft])
                        fw.dma("pool", clp[:], Din["cl256"], writes=[Rft])
                        fw.dma("pool", slp[:], Din["sl256"], writes=[Rft])
                        for i in range(NT):
                            b0 = 2 + 2 * (i % 2)
                            for g in range(4):
                                MM(bank(b0 + g // 2)[:, (g % 2) * 256:(g % 2 + 1) * 256], sfT[:, 4 + g, i * 128:(i + 1) * 128], cs128[:], True, True,
                                   [RsfT[4 + g], Rft], [PB[b0 + g // 2]])
                            CP("act", Pfm[:, i, :], bank(b0, 2), [PB[b0], PB[b0 + 1]], [RPfm[i]])
                        cnt = 0
                        for si, (t0, Ls) in enumerate(SEQS):
                            ntl = Ls // 128; tb0 = t0 // 128
                            tc_, ts_ = (cl, sl_) if si == 0 else (clp, slp)
                            for g in range(4):
                                for lb in range(0, Ls, 512):
                                    n = min(512, Ls - lb)
                                    bk = (0, 1, 6)[cnt % 3]; cnt += 1
                                    for k in range(ntl):
                                        MM(bank(bk)[:, 0:n], Pfm[:, tb0 + k, g * 256:g * 256 + 128], tc_[:, k, lb:lb + n], k == 0, False,
                                           [RPfm[tb0 + k], Rft], [PB[bk]])
                                        MM(bank(bk)[:, 0:n], Pfm[:, tb0 + k, g * 256 + 128:g * 256 + 256], ts_[:, k, lb:lb + n], False, k == ntl - 1,
                                           [RPfm[tb0 + k], Rft], [PB[bk]])
                                    CP("act", fourT[:, g, t0 + lb:t0 + lb + n], bank(bk)[:, 0:n], [PB[bk]], [RfourT[g]])
                        fw.barrier()
                    DBG("d_fourT", fourT[:].rearrange("p a b -> p (a b)"), RfourT)
                    if stop == "four": fw.dead = True
                    CH[0] = 'S' in os.environ.get('CHP', 'mabfFGMLUS')
                    ssmT = sb(L3, "ssmT", [128, 4, TOK], BF16); RssmT = [Res("ssmT%d" % j) for j in range(4)]
                    with contextlib.ExitStack() as S:
                        PowP = [sb(S, "PowP%d" % r, [128, 2, 16, 17]) for r in range(2)]
                        LH = [sb(S, "LH%d" % r, [128, 2, 16, 7]) for r in range(2)]
                        fP = [sb(S, "fP%d" % r, [128, 2, 16]) for r in range(2)]
                        Hb = sb(S, "Hb", [128, 2, 2, 16, NS], BF16)
                        h0 = sb(S, "h0", [128, 2, 2, 16]); stg = sb(S, "stg", [128, 16, 2])
                        RS = Res("ssm"); RHb = Res("Hb"); Rstg = Res("stg")
                        fw.dma("sp", h0[:], Din["h0"][:, l], writes=[RS])
                        def cmul(dr, di, ar, ai, br, bi, m1, m2):
                            TT("dve", m1, ar, br, ALU.mult, [RS], [RS]); TT("dve", m2, ai, bi, ALU.mult, [RS], [RS])
                            TT("dve", dr, m1, m2, ALU.subtract, [RS], [RS])
                            TT("dve", m1, ar, bi, ALU.mult, [RS], [RS]); TT("dve", m2, ai, br, ALU.mult, [RS], [RS])
                            TT("dve", di, m1, m2, ALU.add, [RS], [RS])
                        for d in range(2):
                            with contextlib.ExitStack() as SD:
                                lam = [sb(SD, "lam%d" % r, [128, 528]) for r in range(2)]
                                ff = [sb(SD, "ff%d" % r, [128, 528]) for r in range(2)]
                                A = [sb(SD, "A%d" % r, [128, 16, NS]) for r in range(2)]
                                with contextlib.ExitStack() as SP:
                                    are, aim, ldt, mag, cc, ss, t1, t2, t3 = [sb(SP, "sp%d" % j, [128, 528]) for j in range(9)]
                                    fw.dma("sp", are[:], Din["lamin"][l, d, 0], writes=[RS])
                                    fw.dma("sp", aim[:], Din["lamin"][l, d, 1], writes=[RS])
                                    fw.dma("sp", ldt[:], Din["lamin"][l, d, 2], writes=[RS])
                                    ACT(ldt[:], ldt[:], AF.Exp, [RS], [RS])
                                    TT("dve", t1[:], are[:], ldt[:], ALU.mult, [RS], [RS])
                                    ACT(mag[:], t1[:], AF.Exp, [RS], [RS])
                                    TT("dve", t2[:], aim[:], ldt[:], ALU.mult, [RS], [RS])
                                    ACT(ss[:], t2[:], AF.Sin, [RS], [RS], scale=1.0 / 16)
                                    ACT(t1[:], t2[:], AF.Sin, [RS], [RS], scale=1.0 / 32)
                                    TT("dve", t1[:], t1[:], t1[:], ALU.mult, [RS], [RS])
                                    TS("dve", cc[:], t1[:], -2.0, ALU.mult, [RS], [RS], s2=1.0, op1=ALU.add)
                                    for _ in range(4):
                                        TT("dve", t1[:], cc[:], cc[:], ALU.mult, [RS], [RS])
                                        TT("dve", t2[:], ss[:], ss[:], ALU.mult, [RS], [RS])
                                        TT("dve", t3[:], cc[:], ss[:], ALU.mult, [RS], [RS])
                                        TT("dve", cc[:], t1[:], t2[:], ALU.subtract, [RS], [RS])
                                        TS("dve", ss[:], t3[:], 2.0, ALU.mult, [RS], [RS])
                                    TT("dve", lam[0][:], mag[:], cc[:], ALU.mult, [RS], [RS])
                                    TT("dve", lam[1][:], mag[:], ss[:], ALU.mult, [RS], [RS])
                                    TT("dve", t1[:], are[:], are[:], ALU.mult, [RS], [RS])
                                    TT("dve", t2[:], aim[:], aim[:], ALU.mult, [RS], [RS])
                                    TT("dve", t1[:], t1[:], t2[:], ALU.add, [RS], [RS])
                                    OP("dve", lambda e: e.reciprocal(out=t1[:], in_=t1[:]), [RS], [RS])
                                    TS("dve", t2[:], lam[0][:], -1.0, ALU.add, [RS], [RS])
                                    TT("dve", t3[:], t2[:], are[:], ALU.mult, [RS], [RS])
                                    TT("dve", cc[:], lam[1][:], aim[:], ALU.mult, [RS], [RS])
                                    TT("dve", t3[:], t3[:], cc[:], ALU.add, [RS], [RS])
                                    TT("dve", ff[0][:], t3[:], t1[:], ALU.mult, [RS], [RS])
                                    TT("dve", t3[:], lam[1][:], are[:], ALU.mult, [RS], [RS])
                                    TT("dve", cc[:], t2[:], aim[:], ALU.mult, [RS], [RS])
                                    TT("dve", t3[:], t3[:], cc[:], ALU.subtract, [RS], [RS])
                                    TT("dve", ff[1][:], t3[:], t1[:], ALU.mult, [RS], [RS])
                                    pm1 = t1[:, 0:256].rearrange("p (g k) -> p g k", g=16); pm2 = t2[:, 0:256].rearrange("p (g k) -> p g k", g=16)
                                    OP("dve", lambda e: e.memset(PowP[0][:, d, :, 0:1], 1.0), [RS], [RS])
                                    OP("dve", lambda e: e.memset(PowP[1][:, d, :, 0:1], 0.0), [RS], [RS])
                                    for r in range(2):
                                        ACT(PowP[r][:, d, :, 1], lam[r][:, 0:16], AF.Identity, [RS], [RS])
                                        ACT(fP[r][:, d, :], ff[r][:, 0:16], AF.Identity, [RS], [RS])
                                    for w in (1, 2, 4, 8):
                                        br_ = PowP[0][:, d, :, w:w + 1].to_broadcast([128, 16, w]); bi_ = PowP[1][:, d, :, w:w + 1].to_broadcast([128, 16, w])
                                        cmul(PowP[0][:, d, :, w + 1:2 * w + 1], PowP[1][:, d, :, w + 1:2 * w + 1],
                                             PowP[0][:, d, :, 1:w + 1], PowP[1][:, d, :, 1:w + 1], br_, bi_, pm1[:, :, 0:w], pm2[:, :, 0:w])
                                    for r in range(2):
                                        ACT(LH[r][:, d, :, 0], PowP[r][:, d, :, 16], AF.Identity, [RS], [RS])
                                    for m in range(6):
                                        cmul(LH[0][:, d, :, m + 1], LH[1][:, d, :, m + 1], LH[0][:, d, :, m], LH[1][:, d, :, m],
                                             LH[0][:, d, :, m], LH[1][:, d, :, m], t1[:, 256:272], t2[:, 256:272])
                                for r in range(2):
                                    OP("dve", lambda e: e.memset(A[r][:].rearrange("p a b -> p (a b)"), 0.0), [RS], [RS])
                                with contextlib.ExitStack() as SW:
                                    W = [sb(SW, "W%d" % r, [128, 16, 128]) for r in range(2)]
                                    WT2 = [[sb(SW, "WT%d_%d" % (z, r), [128, 16, 128], BF16) for r in range(2)] for z in range(2)]
                                    wm = [sb(SW, "wm%d" % r, [128, 8, 128]) for r in range(3)]
                                    BT = [sb(SW, "BT%d" % r, [128, 128]) for r in range(2)]
                                    Lw = [sb(SW, "Lw%d" % r, [128, 128]) for r in range(2)]
                                    RWT2 = [Res("WTa"), Res("WTb")]
                                    def build_W(q):
                                        fs = slice(16 + q * 128, 16 + (q + 1) * 128)
                                        for r in range(2):
                                            fw.dma("sp", BT[r][:], Din["BbdT"][l, d, r, :, q, :], writes=[RS])
                                            ACT(Lw[r][:], lam[r][:, fs], AF.Identity, [RS], [RS])
                                        cmul(W[0][:, 0, :], W[1][:, 0, :], ff[0][:, fs], ff[1][:, fs], BT[0][:], BT[1][:], wm[0][:, 0, :], wm[1][:, 0, :])
                                        for w in (1, 2, 4, 8):
                                            cmul(W[0][:, w:2 * w, :], W[1][:, w:2 * w, :], bm(Lw[0][:], w), bm(Lw[1][:], w),
                                                 W[0][:, 0:w, :], W[1][:, 0:w, :], wm[0][:, 0:w, :], wm[1][:, 0:w, :])
                                            if w < 8:
                                                TT("dve", wm[0][:, 0, :], Lw[0][:], Lw[0][:], ALU.mult, [RS], [RS])
                                                TT("dve", wm[1][:, 0, :], Lw[1][:], Lw[1][:], ALU.mult, [RS], [RS])
                                                TT("dve", wm[2][:, 0, :], Lw[0][:], Lw[1][:], ALU.mult, [RS], [RS])
                                                TT("dve", Lw[0][:], wm[0][:, 0, :], wm[1][:, 0, :], ALU.subtract, [RS], [RS])
                                                TS("dve", Lw[1][:], wm[2][:, 0, :], 2.0, ALU.mult, [RS], [RS])
                                        for r in range(2):
                                            ACT(WT2[q % 2][r][:].rearrange("p a b -> p (a b)"), W[r][:].rearrange("p a b -> p (a b)"), AF.Identity, [RS], [RWT2[q % 2]])
                                    def run_S(q):
                                        WTq = WT2[q % 2]; RWTq = RWT2[q % 2]
                                        Sps = bank(0, 2).rearrange("p (b r c) -> p b r c", b=4, r=2)
                                        for b in range(4):
                                            sv = sfT[32 * b:32 * b + 32, q, :].rearrange("p (c j) -> p c j", j=16)
                                            for r in range(2):
                                                for k in range(16):
                                                    j = 15 - k if d == 0 else k
                                                    MM(Sps[:, b, r, 0:96], WTq[r][32 * b:32 * b + 32, k, :], sv[:, :, j], k == 0, k == 15,
                                                       [RWTq, RsfT[q]], [PB[0], PB[1]], tp=(32 * b, 0))
                                        for r in range(2):
                                            for s_i in range(3):
                                                n = SCH[s_i]; c0 = (0, 64, 80)[s_i]; lo = SOFF[s_i] + (1 if d == 0 else 0)
                                                ACT(A[r][:, 4 * q:4 * q + 4, lo:lo + n], Sps[:, :, r, c0:c0 + n], AF.Identity, [PB[0], PB[1]], [RS])
                                    build_W(0)
                                    for q in range(4):
                                        if q + 1 < 4:
                                            build_W(q + 1)
                                        run_S(q)
                                for r in range(2):
                                    ACT(A[r][:, :, 0 if d == 0 else 64], h0[:, d, r, :], AF.Identity, [RS], [RS])
                                with contextlib.ExitStack() as SC:
                                    sm = [sb(SC, "sm%d" % r, [128, 16, NS]) for r in range(3)]
                                    for m in range(7):
                                        dd = 1 << m
                                        groups = []
                                        if dd < 65:
                                            groups.append((lambda t_, lo, hi: t_[:, :, lo:hi], 65, None))
                                        if dd < 17:
                                            groups.append((lambda t_, lo, hi: t_[:, :, 65:99].rearrange("p g (s c) -> p g s c", s=2)[:, :, :, lo:hi], 17, 2))
                                        for view, n, ns in groups:
                                            cntn = n - dd
                                            (dlo, dhi, slo, shi) = (dd, n, 0, cntn) if d == 0 else (0, cntn, dd, n)
                                            if ns is None:
                                                Lr = bl(LH[0][:, d, :, m], cntn); Li = bl(LH[1][:, d, :, m], cntn)
                                                tv = lambda t_: t_[:, :, 0:cntn]
                                            else:
                                                Lr = LH[0][:, d, :, m].unsqueeze(2).unsqueeze(3).to_broadcast([128, 16, 2, cntn])
                                                Li = LH[1][:, d, :, m].unsqueeze(2).unsqueeze(3).to_broadcast([128, 16, 2, cntn])
                                                tv = lambda t_: t_[:, :, 0:2 * cntn].rearrange("p g (s c) -> p g s c", s=2)
                                            sr, si = view(A[0], slo, shi), view(A[1], slo, shi)
                                            dr, di = view(A[0], dlo, dhi), view(A[1], dlo, dhi)
                                            m1, m2, m3 = tv(sm[0]), tv(sm[1]), tv(sm[2])
                                            TT("dve", m1, sr, Lr, ALU.mult, [RS], [RS]); TT("dve", m2, si, Li, ALU.mult, [RS], [RS])
                                            TT("dve", m1, m1, m2, ALU.subtract, [RS], [RS])
                                            TT("dve", m2, si, Lr, ALU.mult, [RS], [RS]); TT("dve", m3, sr, Li, ALU.mult, [RS], [RS])
                                            TT("dve", m2, m2, m3, ALU.add, [RS], [RS])
                                            TT("dve", dr, dr, m1, ALU.add, [RS], [RS]); TT("dve", di, di, m2, ALU.add, [RS], [RS])
                                for r in range(2):
                                    ACT(Hb[:, d, r, :, :], A[r][:], AF.Identity, [RS], [RHb])
                                for s_i in (1, 2):
                                    idx = SOFF[s_i] + (16 if d == 0 else 0)
                                    for r in range(2):
                                        ACT(stg[:, :, r], A[r][:, :, idx], AF.Identity, [RS], [Rstg])
                                    fw.dma("sp", Dout["nst"][s_i - 1, l, d].rearrange("(pair g2) p r -> (g2 p) pair r", g2=2), stg[:], reads=[Rstg], writes=[Rdbg])
                                DBG("d_A%d" % d, A[0][:].rearrange("p a b -> p (a b)"), [RS])
                                fw.barrier()
                        with contextlib.ExitStack() as SQ:
                            CL = [[sb(SQ, "CL%d%d" % (d, r), [128, 4, 17, 32], BF16) for r in range(2)] for d in range(2)]
                            Kblk = [sb(SQ, "Kblk%d" % d, [128, 16, 128], BF16) for d in range(2)]
                            Bb = [[sb(SQ, "Bb%d%d" % (d, r), [128, 4, 32], BF16) for r in range(2)] for d in range(2)]
                            Dbl = sb(SQ, "Dbl", [128, 4, 128], BF16)
                            cm = [sb(SQ, "cm%d" % j, [128, 2, 17, 32]) for j in range(2)]
                            Cl = [sb(SQ, "Cl%d" % r, [128, 4, 32]) for r in range(2)]
                            Bl = [sb(SQ, "Bl%d" % r, [128, 4, 32]) for r in range(2)]
                            bt = [sb(SQ, "bt%d" % r, [128, 4, 32]) for r in range(2)]
                            gt = [sb(SQ, "gt%d" % r, [128, 512]) for r in range(2)]; Rgt = [Res("gt0"), Res("gt1")]
                            RCL = Res("CL"); RK = Res("Kblk")
                            ydb = sb(SQ, "ydb", [128, 512]); Rydb = Res("ydb")
                            fw.dma("pool", Dbl[:], Din["Dblk"][l], writes=[RCL])
                            for q in range(4):
                                ps4 = slice(4 * q, 4 * q + 4)
                                for d in range(2):
                                    for r in range(2):
                                        fw.dma("sp", Cl[r][:], Din["CbdP"][l, d, r, :, ps4, :], writes=[RS])
                                        fw.dma("sp", Bl[r][:], Din["BbdP"][l, d, r, :, ps4, :], writes=[RS])
                                    for hp in range(2):
                                        pp = slice(2 * hp, 2 * hp + 2); pq = slice(4 * q + 2 * hp, 4 * q + 2 * hp + 2)
                                        Pr = PowP[0][:, d, pq, :].unsqueeze(3).to_broadcast([128, 2, 17, 32])
                                        Pi = PowP[1][:, d, pq, :].unsqueeze(3).to_broadcast([128, 2, 17, 32])
                                        Cr = Cl[0][:, pp, :].unsqueeze(2).to_broadcast([128, 2, 17, 32])
                                        Ci = Cl[1][:, pp, :].unsqueeze(2).to_broadcast([128, 2, 17, 32])
                                        TT("dve", cm[0][:], Pr, Cr, ALU.mult, [RS], [RS]); TT("dve", cm[1][:], Pi, Ci, ALU.mult, [RS], [RS])
                                        TT("dve", CL[d][0][:, pp, :, :], cm[0][:], cm[1][:], ALU.subtract, [RS], [RCL])
                                        TT("dve", cm[0][:], Pr, Ci, ALU.mult, [RS], [RS]); TT("dve", cm[1][:], Pi, Cr, ALU.mult, [RS], [RS])
                                        TT("dve", cm[0][:], cm[0][:], cm[1][:], ALU.add, [RS], [RS])
                                        TS("dve", CL[d][1][:, pp, :, :], cm[0][:], -1.0, ALU.mult, [RS], [RCL])
                                    fr = bl(fP[0][:, d, ps4], 32); fi = bl(fP[1][:, d, ps4], 32)
                                    TT("dve", bt[0][:], fr, Bl[0][:], ALU.mult, [RS], [RS]); TT("dve", bt[1][:], fi, Bl[1][:], ALU.mult, [RS], [RS])
                                    TT("dve", Bb[d][0][:], bt[0][:], bt[1][:], ALU.subtract, [RS], [RCL])
                                    TT("dve", bt[0][:], fr, Bl[1][:], ALU.mult, [RS], [RS]); TT("dve", bt[1][:], fi, Bl[0][:], ALU.mult, [RS], [RS])
                                    TT("dve", Bb[d][1][:], bt[0][:], bt[1][:], ALU.add, [RS], [RCL])
                                    kb = 6
                                    for b in range(4):
                                        MM(bank(kb)[32 * b:32 * b + 32, :], Bb[d][0][:, b, :], CL[d][0][:, b, 0:16, :].rearrange("p t n -> p (t n)"), True, False,
                                           [RCL], [PB[kb]], tp=(0, 32 * b))
                                        MM(bank(kb)[32 * b:32 * b + 32, :], Bb[d][1][:, b, :], CL[d][1][:, b, 0:16, :].rearrange("p t n -> p (t n)"), False, True,
                                           [RCL], [PB[kb]], tp=(0, 32 * b))
                                    for cb in range(4):
                                        TS("dve", Kblk[d][:, :, 32 * cb:32 * cb + 32], bank(kb).rearrange("p (t n) -> p t n", n=32), maskP[:, cb:cb + 1], ALU.mult,
                                           [PB[kb], Rmask], [RK])
                                for tb, (c0, cn) in enumerate(BLKS):
                                    yb = 3 + tb
                                    Yb = bank(yb).rearrange("p (c j) -> p c j", j=16)
                                    sblk = sfT[:, q, c0:c0 + cn].rearrange("p (c j) -> p c j", j=16)
                                    MM(bank(yb), Dbl[:, q, :], sfT[:, q, c0:c0 + cn], True, False, [RCL, RsfT[q]], [PB[yb]])
                                    for d in range(2):
                                        for t in range(16):
                                            if d == 0:
                                                o_, r_ = Yb[:, :, t:16], sblk[:, :, 0:16 - t]
                                            else:
                                                o_, r_ = Yb[:, :, 0:16 - t], sblk[:, :, t:16]
                                            MM(o_, Kblk[d][:, t, :], r_, False, False, [RK, RsfT[q]], [PB[yb]])
                                    GC = os.environ.get('GC', '1') == '1'
                                    for d in range(2):
                                        for j in range(16):
                                            tt = j + 1 if d == 0 else 16 - j
                                            sh = 0 if d == 0 else 1
                                            for r in range(2):
                                                for b in range(4):
                                                    if tb < 2:
                                                        rhs = Hb[:, d, r, 4 * q + b, 32 * tb + sh:32 * tb + sh + 32]
                                                        o_ = Yb[32 * b:32 * b + 32, :, j]
                                                    else:
                                                        rhs = Hb[:, d, r, 4 * q + b, 65:99].rearrange("p (s c) -> p s c", s=2)[:, :, sh:sh + 16]
                                                        o_ = bank(yb).rearrange("p (s c j) -> p s c j", s=2, j=16)[32 * b:32 * b + 32, :, :, j]
                                                    last = (d == 1 and b == 3 and j == 15 and r == 1)
                                                    MM(o_, CL[d][r][:, b, tt, :], rhs, False, last, [RCL, RHb], [PB[yb]], tp=(0, 32 * b),
                                                       chain=(("gfirst" if b == 0 else "gnext") if GC else False))
                                    if ("d_y%d_%d" % (q, tb)) in Ddbg:
                                        ACT(ydb[:], bank(yb), AF.Identity, [PB[yb]], [Rydb])
                                        DBG("d_y%d_%d" % (q, tb), ydb[:], [Rydb])
                                    g_ = tb % 2
                                    ACT(gt[g_][:], bank(yb), AF.Square, [PB[yb]], [Rgt[g_]])
                                    TS("dve", gt[g_][:], gt[g_][:], 0.044715, ALU.mult, [Rgt[g_]], [Rgt[g_]], s2=1.0, op1=ALU.add)
                                    TT("dve", gt[g_][:], gt[g_][:], bank(yb), ALU.mult, [Rgt[g_], PB[yb]], [Rgt[g_]])
                                    ACT(gt[g_][:], gt[g_][:], AF.Sigmoid, [Rgt[g_]], [Rgt[g_]], scale=1.5957691216)
                                    TT("dve", sfT[:, q, c0:c0 + cn], gt[g_][:], bank(yb), ALU.mult, [Rgt[g_], PB[yb]], [RsfT[q]])
                            fw.barrier()
                    CH[0] = 'G' in os.environ.get('CHP', 'mabfFGMLUS')
                    with contextlib.ExitStack() as ph:
                        alloc_ws(ph)
                        bgl = sb(ph, "bgl", [128, 4]); Rbgl = Res("bgl"); sg = [sb(ph, "sg%d" % j, [128, 512]) for j in range(2)]; Rsg = [Res("sg0"), Res("sg1")]
                        fw.dma("sp", bgl[:], Din["bglu"][:, l, :], writes=[Rbgl])
                        Wg, RWg = load_w([(lambda t: t[:, 0:2048].rearrange("p (k n) -> p k n", k=4), Din["w_glu"][l].rearrange("(k p) n -> p k n", p=128))])
                        Wgv = Wg[:, 0:2048].rearrange("p (k n) -> p k n", k=4)
                        cnt = 0
                        for m in range(4):
                            for tb, (c0, cn) in enumerate(BLKS):
                                bk = (0, 1, 2)[cnt % 3]; sj = cnt % 2; cnt += 1
                                for k in range(4):
                                    MM(bank(bk), Wgv[:, k, m * 128:(m + 1) * 128], sfT[:, k, c0:c0 + cn], k == 0, k == 3, [RWg, RsfT[k]], [PB[bk]])
                                ACT(sg[sj][:], bank(bk), AF.Sigmoid, [PB[bk], Rbgl], [Rsg[sj]], bias=bgl[:, m:m + 1])
                                TT("dve", ssmT[:, m, c0:c0 + cn], sg[sj][:], sfT[:, m, c0:c0 + cn], ALU.mult, [Rsg[sj], RsfT[m]], [RssmT[m]])
                        fw.barrier()
                    DBG("d_gy", sfT[:, 0:4, :].rearrange("p a b -> p (a b)"), RsfT[0:4])
                    DBG("d_ssmT", ssmT[:].rearrange("p a b -> p (a b)"), RssmT)
                    if stop == "ssm": fw.dead = True
                    CH[0] = 'M' in os.environ.get('CHP', 'mabfFGMLUS')
                    with contextlib.ExitStack() as M:
                        mergedT = sb(M, "mergedT", [128, 8, TOK], BF16); Rmg = [Res("mg%d" % j) for j in range(8)]
                        with contextlib.ExitStack() as ph:
                            alloc_ws(ph)
                            uT = sb(ph, "uT", [128, 8, TOK], BF16); RuT = [Res("uT%d" % i) for i in range(NT)]
                            gs = [sb(ph, "gs%d" % j, [128, 512]) for j in range(3)]; Rgs = [Res("gs%d" % j) for j in range(3)]
                            ac = [sb(ph, "ac%d" % j, [128, 512]) for j in range(2)]; Rac = [Res("ac0"), Res("ac1")]
                            build_uT(uT, RuT, 0)
                            win = Din["w_in"][l].rearrange("(k p) n -> p k n", p=128)
                            wbr = [Din[nm][l].rearrange("(k p) n -> p k n", p=128) for nm in ("w_br_attn", "w_br_ssm", "w_br_four")]
                            srcs = [(attnT, RattnT), (ssmT, RssmT), (fourT, RfourT)]
                            for c in range(8):
                                Wg, RWg = load_w([((lambda t, g=g: t[:, g * 1024:(g + 1) * 1024].rearrange("p (k n) -> p k n", k=8)),
                                                   win[:, :, 1792 + g * 1024 + c * 128:1792 + g * 1024 + (c + 1) * 128]) for g in range(3)])
                                Wb, RWb = load_w([((lambda t, x=x: t[:, x * 512:(x + 1) * 512].rearrange("p (k n) -> p k n", k=4)),
                                                   wbr[x][:, :, c * 128:(c + 1) * 128]) for x in range(3)])
                                for tb, (c0, cn) in enumerate(BLKS):
                                    for g in range(3):
                                        Wgv = Wg[:, g * 1024:(g + 1) * 1024].rearrange("p (k n) -> p k n", k=8)
                                        for k in range(8):
                                            MM(bank(g), Wgv[:, k, :], uT[:, k, c0:c0 + cn], k == 0, k == 7, RuT[c0 // 128:(c0 + cn) // 128] + [RWg], [PB[g]])
                                        ACT(gs[g][:], bank(g), AF.Sigmoid, [PB[g]], [Rgs[g]])
                                    for x in range(3):
                                        Wbv = Wb[:, x * 512:(x + 1) * 512].rearrange("p (k n) -> p k n", k=4)
                                        st_, Rst = srcs[x]
                                        for k in range(4):
                                            MM(bank(3 + x), Wbv[:, k, :], st_[:, k, c0:c0 + cn], k == 0, k == 3, [Rst[k], RWb], [PB[3 + x]])
                                    TT("dve", ac[0][:], gs[0][:], bank(3), ALU.mult, [Rgs[0], PB[3]], [Rac[0]])
                                    TT("dve", ac[1][:], gs[1][:], bank(4), ALU.mult, [Rgs[1], PB[4]], [Rac[1]])
                                    TT("dve", ac[0][:], ac[0][:], ac[1][:], ALU.add, Rac, [Rac[0]])
                                    TT("dve", ac[1][:], gs[2][:], bank(5), ALU.mult, [Rgs[2], PB[5]], [Rac[1]])
                                    TT("dve", mergedT[:, c, c0:c0 + cn], ac[0][:], ac[1][:], ALU.add, Rac, [Rmg[c]])
                            fw.barrier()
                        DBG("d_mergedT", mergedT[:].rearrange("p a b -> p (a b)"), Rmg)
                        if stop == "merge": fw.dead = True
                        layer_norm_residual(l, 0, 8, lambda i, k: mergedT[:, k, i * 128:(i + 1) * 128], lambda i, k: [Rmg[k]],
                                            Din["w_out"][l].rearrange("(k p) n -> p k n", p=128))
              if "d_x1" in Ddbg:
                  for i in range(NT):
                      fw.dma("sp", Ddbg["d_x1"][i * 128:(i + 1) * 128, :], X[:, i, :], reads=[RX[i]], writes=[Rdbg])
              if stop == "ln1": fw.dead = True
              CH[0] = 'U' in os.environ.get('CHP', 'mabfFGMLUS')
              with contextlib.ExitStack() as Fs:
                  actT = sb(Fs, "actT", [128, 22, TOK], BF16); Ract = [Res("act%d" % j) for j in range(22)]
                  with contextlib.ExitStack() as ph:
                      alloc_ws(ph)
                      uT = sb(ph, "uT", [128, 8, TOK], BF16); RuT = [Res("uT%d" % i) for i in range(NT)]
                      cvp = sb(ph, "cvp", [128, 44, 4]); Rcv = Res("cvp")
                      hc = [sb(ph, "hc%d" % j, [128, TOK]) for j in range(2)]; Rhc = [Res("hc0"), Res("hc1")]
                      fw.dma("sp", cvp[:], Din["convp"][:, l], writes=[Rcv])
                      build_uT(uT, RuT, 1)
                      wup = Din["w_up"][l].rearrange("(k p) n -> p k n", p=128)
                      for ch in range(44):
                          if ch % 4 == 0:
                              W, RW = load_w([(lambda t: t[:, 0:4096].rearrange("p (k n) -> p k n", k=8), wup[:, :, ch * 128:ch * 128 + 512])])
                              Wv = W[:, 0:4096].rearrange("p (k n) -> p k n", k=8)
                          cw = ch % 4; pb0 = 3 * (ch % 2); hj = ch % 2
                          prs = [PB[pb0], PB[pb0 + 1], PB[pb0 + 2]]
                          for tb, (c0, cn) in enumerate(BLKS):
                              for k in range(8):
                                  MM(bank(pb0 + tb), Wv[:, k, cw * 128:(cw + 1) * 128], uT[:, k, c0:c0 + cn], k == 0, k == 7,
                                     RuT[c0 // 128:(c0 + cn) // 128] + [RW], [prs[tb]])
                          hp = bank(pb0, 3); h_ = hc[hj]
                          ACT(h_[:], hp, AF.Identity, prs + [Rcv], [Rhc[hj]], scale=cvp[:, ch, 1:2], bias=cvp[:, ch, 3:4])
                          def stt(o_, i0, sc, i1):
                              OP("dve", lambda e: e.scalar_tensor_tensor(out=o_, in0=i0, scalar=sc, in1=i1, op0=ALU.mult, op1=ALU.add),
                                 prs + [Rcv, Rhc[hj]], [Rhc[hj]])
                          stt(h_[:, 1:1024], hp[:, 0:1023], cvp[:, ch, 0:1], h_[:, 1:1024])
                          stt(h_[:, 0:1023], hp[:, 1:1024], cvp[:, ch, 2:3], h_[:, 0:1023])
                          h3 = h_[:, 1024:1536].rearrange("p (s c) -> p s c", s=2); p3 = hp[:, 1024:1536].rearrange("p (s c) -> p s c", s=2)
                          stt(h3[:, :, 1:256], p3[:, :, 0:255], cvp[:, ch, 0:1], h3[:, :, 1:256])
                          stt(h3[:, :, 0:255], p3[:, :, 1:256], cvp[:, ch, 2:3], h3[:, :, 0:255])
                          if ch < 22:
                              ACT(actT[:, ch, :], h_[:], AF.Silu, [Rhc[hj]], [Ract[ch]])
                          else:
                              TT("dve", actT[:, ch - 22, :], actT[:, ch - 22, :], h_[:], ALU.mult, [Ract[ch - 22], Rhc[hj]], [Ract[ch - 22]])
                      fw.barrier()
                  DBG("d_actT", actT[:].rearrange("p a b -> p (a b)"), Ract)
                  if stop == "ffn": fw.dead = True
                  layer_norm_residual(l, 1, 22, lambda i, k: actT[:, k, i * 128:(i + 1) * 128], lambda i, k: [Ract[k]],
                                      Din["w_down"][l].rearrange("(k p) n -> p k n", p=128))
        layers()
        fw.dead = False
        for i in range(NT):
            fw.dma("sp", Dout["y"][i * 128:(i + 1) * 128, :], X[:, i, :], reads=[RX[i]], writes=[Rdbg])
        fw.finish()
    return nc


def kernel(**inputs):
    inp = {k: np.asarray(v) for k, v in inputs.items()}
    S = prep_shared(inp)
    nc = build()
    in_maps = []
    for cid in range(8):
        m = dict(S); m.update(prep_core(inp, cid))
        in_maps.append(m)
    res = run_bass_kernel_spmd(nc, in_maps, core_ids=list(range(8)))
    y_p = np.zeros((16, 256, 1024), np.float32); y_s = np.zeros((8, 1024, 1024), np.float32)
    nk = np.zeros((16, 2, 256, 2, 64), np.float32); nv = np.zeros((16, 2, 256, 2, 64), np.float32)
    nst = np.zeros((16, 2, 2, 32, 64, 2), np.float32)
    for cid in range(8):
        r = res.results[cid]
        y_s[cid] = r["y"][0:1024]
        y_p[2 * cid:2 * cid + 2] = r["y"][1024:1536].reshape(2, 256, 1024)
        nk[2 * cid:2 * cid + 2] = r["nk"].reshape(2, 2, 256, 2, 64)
        nv[2 * cid:2 * cid + 2] = r["nv"].reshape(2, 2, 256, 2, 64)
        nst[2 * cid:2 * cid + 2] = r["nst"]
    return (y_p, y_s, nk, nv, nst)
```

```python
import contextlib, os
SK = os.environ.get("SK", "")
import numpy as np
import concourse.bass as bass
import concourse.mybir as mybir
from concourse.bass_utils import run_bass_kernel_spmd

F32 = mybir.dt.float32; BF16 = mybir.dt.bfloat16
AF = mybir.ActivationFunctionType; ALU = mybir.AluOpType; AX = mybir.AxisListType

D_MODEL = 1024; DEPTH = 2; D_IN = 4864; D_FF = 2816
ALPHA = (2 * DEPTH) ** 0.25
EPS = 1e-6
NT = 12; TOK = 1536; T = 16
SEQS = [(0, 1024), (1024, 256), (1280, 256)]
BLKS = [(0, 512), (512, 512), (1024, 512)]
TILE_VEC = [0] * 8 + [1] * 4
NS = 99
SOFF = [0, 65, 82]
SCH = [64, 16, 16]


class StopBuild(Exception):
    pass


class Res:
    __slots__ = ("name", "w", "r")
    def __init__(self, name):
        self.name = name; self.w = None; self.r = {}


class Eng:
    def __init__(self, name, h, sem):
        self.name = name; self.h = h; self.sem = sem; self.count = 0; self.waited = {}


class FW:
    NDMA = 32
    def __init__(self, nc, es):
        self.nc = nc; self.sems = {}; self.eng = {}
        for name, h in (("pe", nc.tensor), ("act", nc.scalar), ("dve", nc.vector), ("pool", nc.gpsimd), ("sp", nc.sync)):
            s = es.enter_context(nc.semaphore("s_" + name))
            self.sems[name] = s; self.eng[name] = Eng(name, h, s)
        self.dma_sems = []
        for i in range(self.NDMA):
            s = es.enter_context(nc.semaphore("s_dma%d" % i))
            self.sems["dma%d" % i] = s; self.dma_sems.append(["dma%d" % i, 0])
        self.dma_i = 0; self.ninstr = 0; self.dead = False; self.prev_plain = False

    def _wait(self, e, key, val):
        if e.waited.get(key, 0) >= val: return
        e.h.wait_ge(self.sems[key], val); e.waited[key] = val

    def _deps(self, e, reads, writes, chain=False):
        deps = {}
        def add(kv):
            if kv is None: return
            k, v = kv
            if deps.get(k, 0) < v: deps[k] = v
        for r in reads: add(r.w)
        for w in writes:
            if w.w is not None and not (chain and w.w[0] == e.name):
                add(w.w)
            for k, v in w.r.items():
                if k == e.name: continue
                add((k, v))
        return deps

    def op(self, engname, fn, reads=(), writes=(), chain=False):
        if self.dead: return None
        e = self.eng[engname]
        if engname == "pe" and chain in ("gfirst", "gnext"):
            if chain == "gfirst" and e.count > 0:
                self._wait(e, "pe", e.count)
            chain = True; self.prev_plain = False
        elif engname == "pe":
            plain = chain
            chain = chain and self.prev_plain
            if plain and not chain and e.count > 0:
                self._wait(e, "pe", e.count)
            self.prev_plain = plain
        for k, v in self._deps(e, reads, writes, chain).items(): self._wait(e, k, v)
        ins = fn(e.h); e.count += 1; ins.then_inc(e.sem, 1)
        for r in reads: r.r[e.name] = e.count
        for w in writes: w.w = (e.name, e.count); w.r = {}
        self.ninstr += 1
        return ins

    def dma(self, qname, out, in_, reads=(), writes=()):
        if self.dead: return None
        e = self.eng[qname]
        deps = self._deps(e, reads, writes)
        slot = self.dma_sems[self.dma_i % self.NDMA]; self.dma_i += 1
        key = slot[0]
        if slot[1] > 0: deps[key] = max(deps.get(key, 0), slot[1])
        for k, v in deps.items(): self._wait(e, k, v)
        ins = e.h.dma_start(out=out, in_=in_)
        slot[1] += 16; ins.then_inc(self.sems[key], 16)
        for r in reads: r.r[key] = slot[1]
        for w in writes: w.w = (key, slot[1]); w.r = {}
        self.ninstr += 1
        return ins

    def barrier(self):
        if self.dead: return
        targets = {n: e.count for n, e in self.eng.items() if e.count > 0}
        for key, cnt in self.dma_sems:
            if cnt > 0: targets[key] = cnt
        for n, e in self.eng.items():
            for k, v in targets.items(): self._wait(e, k, v)

    def finish(self):
        self.dead = False
        self.barrier()


def bl(ap, n):
    return ap.unsqueeze(2).to_broadcast([ap.shape[0], ap.shape[1], n])


def bm(ap, n):
    return ap.unsqueeze(1).to_broadcast([ap.shape[0], n, ap.shape[1]])


def _rep(a, n=128):
    return np.ascontiguousarray(np.broadcast_to(a[None], (n,) + a.shape))


def prep_shared(inp):
    f = lambda a: np.ascontiguousarray(a, dtype=np.float32)
    L = DEPTH
    S = {}
    for k in ("w_ada", "w_in", "w_glu", "w_br_ssm", "w_br_four", "w_out", "w_up", "w_down"):
        S[k] = f(inp[k])
    S["w_br_attn"] = f(inp["w_br_attn"])
    S["b_adaP"] = f(inp["b_ada"].reshape(L, 48, 128).transpose(2, 0, 1))
    S["bgbc"] = f(np.stack([np.stack([_rep(inp["b_ada"][l, 2048:3072]), _rep(inp["b_ada"][l, 5120:6144])], 1) for l in range(L)]))
    S["lnbc"] = f(np.stack([np.stack([_rep(inp[k][l]) for k in ("ln1_g", "ln1_b", "ln2_g", "ln2_b")]) for l in range(L)]))
    cw = inp["conv_w"].reshape(L, 3, 44, 128).transpose(3, 0, 2, 1)
    cb = inp["conv_b"].reshape(L, 44, 128).transpose(2, 0, 1)[..., None]
    S["convp"] = f(np.concatenate([cw, cb], -1))
    S["bglu"] = f(inp["b_glu"].reshape(L, 4, 128).transpose(2, 0, 1))
    S["g640"] = f(np.stack([_rep(np.concatenate([np.tile(inp["q_norm_g"][l], 8), np.tile(inp["k_norm_g"][l], 4)])) for l in range(L)]))
    def lay(a):
        aP = a.reshape(L, 2, 16, 2, 64).transpose(0, 1, 3, 4, 2).reshape(L, 2, 128, 16)
        aF = a.reshape(L, 2, 4, 4, 2, 64).transpose(0, 1, 3, 2, 4, 5)
        aF = np.broadcast_to(aF[:, :, :, None], (L, 2, 4, 32, 4, 2, 64)).reshape(L, 2, 128, 512)
        return np.concatenate([aP, aF], -1)
    ldt = np.broadcast_to(inp["ssm_log_dt"][..., None], (L, 2, 32, 64))
    S["lamin"] = f(np.stack([lay(inp["ssm_a_re"]), lay(inp["ssm_a_im"]), lay(ldt)], 2))
    def bbdP(b):
        Bp = b.reshape(L, 2, 16, 2, 64, 16)
        out = np.zeros((L, 2, 2, 64, 16, 2, 16), np.float32)
        for g2 in range(2):
            out[:, :, g2, :, :, g2, :] = Bp[:, :, :, g2].transpose(0, 1, 3, 2, 4)
        return out.reshape(L, 2, 128, 16, 32)
    def bbdT(b):
        Bq = b.reshape(L, 2, 4, 4, 2, 64, 16)
        out = np.zeros((L, 2, 4, 2, 16, 4, 2, 64), np.float32)
        for g2 in range(2):
            out[:, :, :, g2, :, :, g2, :] = Bq[:, :, :, :, g2].transpose(0, 1, 3, 5, 2, 4)
        return out.reshape(L, 2, 128, 4, 128)
    def cbdP(c):
        Cp = c.reshape(L, 2, 16, 2, 16, 64)
        out = np.zeros((L, 2, 2, 64, 16, 2, 16), np.float32)
        for g2 in range(2):
            out[:, :, g2, :, :, g2, :] = Cp[:, :, :, g2].transpose(0, 1, 4, 2, 3)
        return out.reshape(L, 2, 128, 16, 32)
    S["BbdP"] = f(np.stack([bbdP(inp["ssm_b_re"]), bbdP(inp["ssm_b_im"])], 2))
    S["BbdT"] = f(np.stack([bbdT(inp["ssm_b_re"]), bbdT(inp["ssm_b_im"])], 2))
    S["CbdP"] = f(np.stack([cbdP(inp["ssm_c_re"]), cbdP(inp["ssm_c_im"])], 2))
    Dblk = np.zeros((L, 128, 4, 128), np.float32)
    for l in range(L):
        for q in range(4):
            Dblk[l, np.arange(128), q, np.arange(128)] = inp["ssm_d"][l, q * 128:(q + 1) * 128]
    S["Dblk"] = Dblk
    S["ident"] = np.eye(128, dtype=np.float32)
    mk = np.zeros((128, 4), np.float32); mk[np.arange(128), np.arange(128) // 32] = 1.0
    S["maskP"] = mk
    t = np.arange(1024)
    freqs = (10000.0 ** (-np.arange(16, dtype=np.float32) / 16)).astype(np.float32)
    ang = np.concatenate([(t // 64).astype(np.float32)[:, None] * freqs, (t % 64).astype(np.float32)[:, None] * freqs], -1)
    S["ropec"] = f(np.cos(ang).reshape(8, 128, 32).transpose(1, 0, 2))
    S["ropes"] = f(np.sin(ang).reshape(8, 128, 32).transpose(1, 0, 2))
    def dft(n):
        i = np.arange(n, dtype=np.int64)
        m = (i[:, None] * i[None, :]) % n
        a = 2.0 * np.pi * m.astype(np.float64) / n
        return np.cos(a) / np.sqrt(n), np.sin(a) / np.sqrt(n)
    c128, s128 = dft(128)
    S["cs128"] = f(np.concatenate([c128, s128], 1))
    for n in (1024, 256):
        c, s = dft(n)
        S["cl%d" % n] = f(c.reshape(n // 128, 128, n).transpose(1, 0, 2))
        S["sl%d" % n] = f((-s).reshape(n // 128, 128, n).transpose(1, 0, 2))
    return S


def prep_core(inp, cid):
    f = lambda a: np.ascontiguousarray(a, dtype=np.float32)
    m = {}
    m["xin"] = f(np.concatenate([inp["x_sample"][cid], inp["x_prompt"][2 * cid:2 * cid + 2].reshape(512, 1024)], 0))
    cv = np.stack([inp["c"][cid], inp["c_ctx"]], -1)
    m["cvec"] = f(cv.reshape(8, 128, 2).transpose(1, 0, 2))
    m["ckT"] = f(inp["cache_k"][cid].reshape(DEPTH, 512, 128).transpose(0, 2, 1))
    m["cv"] = f(inp["cache_v"][cid].reshape(DEPTH, 512, 128))
    st = inp["state_ssm"][cid].reshape(DEPTH, 2, 16, 2, 64, 2)
    m["h0"] = f(st.transpose(3, 4, 0, 1, 5, 2).reshape(128, DEPTH, 2, 2, 16))
    return m


IN_SHAPES = {
    "xin": (1536, 1024), "cvec": (128, 8, 2), "ckT": (2, 128, 512), "cv": (2, 512, 128), "h0": (128, 2, 2, 2, 16),
    "w_ada": (2, 1024, 6144), "w_in": (2, 1024, 4864), "w_glu": (2, 512, 512), "w_br_attn": (2, 512, 1024),
    "w_br_ssm": (2, 512, 1024), "w_br_four": (2, 512, 1024), "w_out": (2, 1024, 1024), "w_up": (2, 1024, 5632),
    "w_down": (2, 2816, 1024), "b_adaP": (128, 2, 48), "bgbc": (2, 128, 2, 1024), "lnbc": (2, 4, 128, 1024),
    "convp": (128, 2, 44, 4), "bglu": (128, 2, 4), "g640": (2, 128, 768), "lamin": (2, 2, 3, 128, 528),
    "BbdP": (2, 2, 2, 128, 16, 32), "BbdT": (2, 2, 2, 128, 4, 128), "CbdP": (2, 2, 2, 128, 16, 32),
    "Dblk": (2, 128, 4, 128), "ident": (128, 128), "maskP": (128, 4), "ropec": (128, 8, 32), "ropes": (128, 8, 32),
    "cs128": (128, 256), "cl1024": (128, 8, 1024), "sl1024": (128, 8, 1024), "cl256": (128, 2, 256), "sl256": (128, 2, 256),
}
OUT_SHAPES = {"y": (1536, 1024), "nk": (2, 2, 256, 128), "nv": (2, 2, 256, 128), "nst": (2, 2, 2, 32, 64, 2)}


def build(n_layers=DEPTH, stop=None, dbg_shapes=None):
    nc = bass.Bass("TRN2", target_bir_lowering=False)
    Din = {k: nc.dram_tensor(k, list(s), F32, kind="ExternalInput").ap() for k, s in IN_SHAPES.items()}
    Dout = {k: nc.dram_tensor(k, list(s), F32, kind="ExternalOutput").ap() for k, s in OUT_SHAPES.items()}
    Ddbg = {k: nc.dram_tensor(k, list(s), F32, kind="ExternalOutput").ap() for k, s in (dbg_shapes or {}).items()}
    es = contextlib.ExitStack()
    with es:
        fw = FW(nc, es)
        uid = [0]
        def sb(st, name, shape, dt=F32):
            uid[0] += 1
            return st.enter_context(nc.sbuf_tensor("%s_%d" % (name, uid[0]), list(shape), dt))
        OP = fw.op
        CH = [True]
        def MM(out, lhsT, rhs, start, stop_, r, w, tp=None, chain=None):
            if chain is None: chain = CH[0] and (tp is None)
            if tp is None:
                return fw.op("pe", lambda e: e.matmul(out, lhsT=lhsT, rhs=rhs, start=start, stop=stop_), r, w, chain=chain)
            return fw.op("pe", lambda e: e.matmul(out, lhsT=lhsT, rhs=rhs, start=start, stop=stop_, tile_position=tp), r, w, chain=chain)
        def TT(eng, out, in0, in1, op, r, w):
            return fw.op(eng, lambda e: e.tensor_tensor(out=out, in0=in0, in1=in1, op=op), r, w)
        def TS(eng, out, in0, s1, op0, r, w, s2=None, op1=None):
            if op1 is None:
                return fw.op(eng, lambda e: e.tensor_scalar(out=out, in0=in0, scalar1=s1, scalar2=None, op0=op0), r, w)
            return fw.op(eng, lambda e: e.tensor_scalar(out=out, in0=in0, scalar1=s1, scalar2=s2, op0=op0, op1=op1), r, w)
        def ACT(out, in_, func, r, w, scale=1.0, bias=0.0):
            return fw.op("act", lambda e: e.activation(out=out, in_=in_, func=func, bias=bias, scale=scale), r, w)
        def CP(eng, out, in_, r, w):
            if eng == "act":
                return fw.op("act", lambda e: e.copy(out=out, in_=in_), r, w)
            return fw.op(eng, lambda e: e.tensor_copy(out=out, in_=in_), r, w)
        Rdbg = Res("dbg")
        def DBG(name, ap, r):
            if name in Ddbg:
                fw.dma("pool", Ddbg[name], ap, reads=r, writes=[Rdbg])

        X = sb(es, "X", [128, NT, 1024]); RX = [Res("X%d" % i) for i in range(NT)]
        ident_f = sb(es, "ident_f", [128, 128]); ident_b = sb(es, "ident_b", [128, 128], BF16); Rid = Res("id")
        maskP = sb(es, "maskP", [128, 4]); Rmask = Res("mask")
        modP = sb(es, "modP", [128, 6, 8, 2]); opsc = sb(es, "opsc", [128, 2, 8, 2]); Rmod = Res("mod")
        csl = sb(es, "csl", [128, 8, 2]); sTb = sb(es, "sTb", [128, 8, 2], BF16); sRep = sb(es, "sRep", [128, 2, 8, 128], BF16)
        Rcs = Res("cs")
        stat = sb(es, "stat", [128, NT, 2, 6]); mv = sb(es, "mv", [128, NT, 2]); rs = sb(es, "rs", [128, NT, 2]); Rstat = Res("stat")
        ps = es.enter_context(nc.psum_tensor("ps", [128, 3584], F32)); PB = [Res("pb%d" % i) for i in range(7)]
        pst = es.enter_context(nc.psum_tensor("pst", [128, 1024], BF16)); PT = Res("pt")
        wslot = [None, None]; Rws = [None, None]
        wctr = [0]
        def alloc_ws(st):
            for i_ in range(2):
                wslot[i_] = sb(st, "wslot%d" % i_, [128, 4096], BF16); Rws[i_] = Res("ws%d" % i_)
        def load_w(views):
            s_ = wctr[0] % 2; wctr[0] += 1
            for dstf, src in views:
                fw.dma("pool", dstf(wslot[s_]), src, writes=[Rws[s_]])
            return wslot[s_], Rws[s_]
        def bank(i, n=1):
            return ps[:, i * 512:(i + n) * 512]

        fw.dma("sp", ident_f[:], Din["ident"], writes=[Rid])
        fw.dma("pool", ident_b[:], Din["ident"], writes=[Rid])
        fw.dma("sp", maskP[:], Din["maskP"], writes=[Rmask])
        for i in range(NT):
            fw.dma("sp", X[:, i, :], Din["xin"][i * 128:(i + 1) * 128, :], writes=[RX[i]])
        fw.dma("sp", csl[:], Din["cvec"], writes=[Rcs])
        ACT(csl[:], csl[:], AF.Silu, [Rcs], [Rcs])
        CP("dve", sTb[:], csl[:], [Rcs], [Rcs])
        for v in range(2):
            CP("dve", sRep[:, v, :, :], bl(csl[:, :, v], 128), [Rcs], [Rcs])

        def build_uT(uT, RuT, sub):
            sh_sec = 0 if sub == 0 else 3
            for i in range(NT):
                v = TILE_VEC[i]
                for kk in range(2):
                    bk = kk
                    for k4 in range(4):
                        k = kk * 4 + k4
                        OP("pe", lambda e: e.transpose(bank(bk)[:, k4 * 128:(k4 + 1) * 128], X[:, i, k * 128:(k + 1) * 128], ident_f[:]),
                           [RX[i], Rid], [PB[bk]])
                    for k4 in range(4):
                        k = kk * 4 + k4
                        ACT(uT[:, k, i * 128:(i + 1) * 128], bank(bk)[:, k4 * 128:(k4 + 1) * 128], AF.Identity,
                            [PB[bk], Rmod], [RuT[i]], scale=opsc[:, sub, k, v:v + 1], bias=modP[:, sh_sec, k, v:v + 1])

        def layer_norm_residual(l, which, kc, lhsT_fn, lres_fn, wsrc3):
            with contextlib.ExitStack() as ph:
                ch_save = CH[0]; CH[0] = 'L' in os.environ.get('CHP', 'mabfFGMLUS')
                alloc_ws(ph)
                lnb = sb(ph, "lnb", [128, 2, 1024]); Rln = Res("lnb")
                tmp = [sb(ph, "lntmp%d" % j, [128, 512]) for j in range(2)]; Rtmp = [Res("lntmp0"), Res("lntmp1")]
                Wd = sb(ph, "Wd", [128, kc, 512], BF16); RWd = Res("Wd")
                fw.dma("sp", lnb[:, 0, :], Din["lnbc"][l, 2 * which], writes=[Rln])
                fw.dma("sp", lnb[:, 1, :], Din["lnbc"][l, 2 * which + 1], writes=[Rln])
                gbc = sb(ph, "gbc", [128, 2, 1024]); Rgbc = Res("gbc")
                bg = sb(ph, "bg", [128, 1024]); Rbg = Res("bg")
                fw.dma("sp", bg[:], Din["bgbc"][l, :, which, :], writes=[Rbg])
                sec = 2 if which == 0 else 5
                wsrc = Din["w_ada"][l].rearrange("(k p) n -> p k n", p=128)
                for half in range(2):
                    Wg, RWg = load_w([(lambda t: t[:, 0:4096].rearrange("p (k n) -> p k n", k=8), wsrc[:, :, sec * 1024 + half * 512:sec * 1024 + (half + 1) * 512])])
                    Wgv = Wg[:, 0:4096].rearrange("p (k n) -> p k n", k=8)
                    for v in range(2):
                        for k in range(8):
                            MM(bank(v), sRep[:, v, k, :], Wgv[:, k, :], k == 0, k == 7, [RWg, Rcs], [PB[v]])
                        TT("dve", gbc[:, v, half * 512:(half + 1) * 512], bank(v), bg[:, half * 512:(half + 1) * 512], ALU.add, [PB[v], Rbg], [Rgbc])
                cnt = 0
                for nb in range(2):
                    cs = slice(nb * 512, (nb + 1) * 512)
                    for k0 in range(0, kc, 8):
                        k1 = min(kc, k0 + 8)
                        fw.dma("pool", Wd[:, k0:k1, :], wsrc3[:, k0:k1, cs], writes=[RWd])
                    for i in range(NT):
                        v = TILE_VEC[i]
                        bk = 2 + (cnt % 4); tj = cnt % 2; cnt += 1
                        for k in range(kc):
                            MM(bank(bk), lhsT_fn(i, k), Wd[:, k, :], k == 0, k == kc - 1, lres_fn(i, k) + [RWd], [PB[bk]])
                        TT("dve", tmp[tj][:], bank(bk), gbc[:, v, cs], ALU.mult, [PB[bk], Rgbc], [Rtmp[tj]])
                        OP("dve", lambda e: e.scalar_tensor_tensor(out=X[:, i, cs], in0=X[:, i, cs], scalar=float(ALPHA), in1=tmp[tj][:],
                                                                   op0=ALU.mult, op1=ALU.add), [RX[i], Rtmp[tj]], [RX[i]])
                for i in range(NT):
                    for hh in range(2):
                        OP("dve", lambda e: e.bn_stats(out=stat[:, i, hh, :], in_=X[:, i, hh * 512:(hh + 1) * 512]), [RX[i]], [Rstat])
                    OP("dve", lambda e: e.bn_aggr(out=mv[:, i, :], in_=stat[:, i, :, :].rearrange("p a b -> p (a b)")), [Rstat], [Rstat])
                TS("dve", rs[:, :, 0], mv[:, :, 1], EPS, ALU.add, [Rstat], [Rstat])
                ACT(rs[:, :, 0], rs[:, :, 0], AF.Sqrt, [Rstat], [Rstat])
                OP("dve", lambda e: e.reciprocal(out=rs[:, :, 0], in_=rs[:, :, 0]), [Rstat], [Rstat])
                OP("dve", lambda e: e.scalar_tensor_tensor(out=rs[:, :, 1], in0=mv[:, :, 0], scalar=-1.0, in1=rs[:, :, 0],
                                                           op0=ALU.mult, op1=ALU.mult), [Rstat], [Rstat])
                for i in range(NT):
                    ACT(X[:, i, :], X[:, i, :], AF.Identity, [RX[i], Rstat], [RX[i]], scale=rs[:, i, 0:1], bias=rs[:, i, 1:2])
                    TT("pool", X[:, i, :], X[:, i, :], lnb[:, 0, :], ALU.mult, [RX[i], Rln], [RX[i]])
                    TT("pool", X[:, i, :], X[:, i, :], lnb[:, 1, :], ALU.add, [RX[i], Rln], [RX[i]])
                fw.barrier()
                CH[0] = ch_save

        def layers():
          for l in range(n_layers):
              with contextlib.ExitStack() as ph:
                  wa = [sb(ph, "wa%d" % i, [128, 8, 512], BF16) for i in range(3)]; Rwa = [Res("wa%d" % i) for i in range(3)]
                  bP = sb(ph, "bP", [128, 48]); Rb = Res("bmod")
                  fw.dma("sp", bP[:], Din["b_adaP"][:, l, :], writes=[Rb])
                  wsrc = Din["w_ada"][l].rearrange("(k p) n -> p k n", p=128)
                  pm = bank(6)[:, 0:96]
                  CH[0] = 'm' in os.environ.get('CHP', 'mabfFGMLUS')
                  for cb in range(12):
                      s = cb % 3
                      sec, half = cb // 2, cb % 2
                      if sec in (2, 5):
                          continue
                      fw.dma("pool", wa[s][:], wsrc[:, :, cb * 512:(cb + 1) * 512], writes=[Rwa[s]])
                      for m in range(4):
                          col = (sec * 8 + half * 4 + m) * 2
                          for k in range(8):
                              MM(pm[:, col:col + 2], wa[s][:, k, m * 128:(m + 1) * 128], sTb[:, k, :], k == 0, k == 7,
                                 [Rwa[s], Rcs], [PB[6]])
                  for sec in (0, 1, 3, 4):
                      TT("dve", modP[:, sec, :, :], pm[:, sec * 16:(sec + 1) * 16].rearrange("p (c v) -> p c v", v=2),
                         bl(bP[:, sec * 8:(sec + 1) * 8], 2), ALU.add, [PB[6], Rb], [Rmod])
                  TS("dve", opsc[:, 0, :, :], modP[:, 1, :, :], 1.0, ALU.add, [Rmod], [Rmod])
                  TS("dve", opsc[:, 1, :, :], modP[:, 4, :, :], 1.0, ALU.add, [Rmod], [Rmod])
                  fw.barrier()
              if stop == "mod": fw.dead = True
              DBG("d_modP", modP[:].rearrange("p a b c -> p (a b c)"), [Rmod])

              with contextlib.ExitStack() as L1:
                  QT = sb(L1, "QT", [128, 4, TOK], BF16); RQ = [Res("QT%d" % j) for j in range(4)]; attnT = QT; RattnT = RQ
                  sfT = sb(L1, "sfT", [128, 8, TOK], BF16); RsfT = [Res("sfT%d" % j) for j in range(8)]
                  with contextlib.ExitStack() as L2:
                      KT = sb(L2, "KT", [128, 2, 2048], BF16); RKT = Res("KT")
                      Vsb = sb(L2, "Vsb", [128, 16, 2, 2, 128], BF16); RV = Res("V")
                      with contextlib.ExitStack() as ph:
                          alloc_ws(ph)
                          uT = sb(ph, "uT", [128, 8, TOK], BF16); RuT = [Res("uT%d" % i) for i in range(NT)]
                          qn = [sb(ph, "qn%d" % j, [128, 512]) for j in range(2)]; Rqn = [Res("qn0"), Res("qn1")]
                          vf = [sb(ph, "vf%d" % j, [128, 128]) for j in range(2)]; Rvf = [Res("vf0"), Res("vf1")]
                          qkb = [sb(ph, "qkb%d" % j, [128, 512], BF16) for j in range(2)]; Rqkb = [Res("qkb0"), Res("qkb1")]
                          sq = sb(ph, "sq", [128, 512]); Rsq = Res("sq")
                          kb2 = [sb(ph, "kb2%d" % j, [128, 128], BF16) for j in range(2)]; Rkb2 = [Res("kb20"), Res("kb21")]
                          ssq = sb(ph, "ssq", [128, 8]); Rssq = Res("ssq")
                          g640 = sb(ph, "g640", [128, 768]); rc = sb(ph, "rc", [128, 8, 32]); rsn = sb(ph, "rsn", [128, 8, 32]); Rcst = Res("cst")
                          rt = [sb(ph, "rt%d" % j, [128, 8, 32]) for j in range(2)]; Rrt = [Res("rt0"), Res("rt1")]
                          win = Din["w_in"][l].rearrange("(k p) n -> p k n", p=128)
                          fw.dma("sp", g640[:], Din["g640"][l], writes=[Rcst])
                          fw.dma("sp", rc[:], Din["ropec"], writes=[Rcst])
                          fw.dma("sp", rsn[:], Din["ropes"], writes=[Rcst])
                          fw.dma("pool", KT[:, 0, 1024:1536], Din["ckT"][l], writes=[RKT])
                          fw.dma("pool", KT[0:64, 1, 1024:1536], Din["ckT"][l, 64:128, :], writes=[RKT])
                          fw.dma("pool", KT[64:128, 1, 1024:1536], Din["ckT"][l, 0:64, :], writes=[RKT])
                          OP("dve", lambda e: e.memset(Vsb[:].rearrange("p a b c d -> p (a b c d)"), 1.0), [], [RV])
                          cvv = Din["cv"][l].rearrange("(t p) (h d) -> p t h d", p=128, h=2)
                          for lay in range(2):
                              for kvh_ in range(2):
                                  fw.dma("pool", Vsb[:, 8:12, kvh_, lay, lay * 64:(lay + 1) * 64], cvv[:, :, kvh_, :], writes=[RV])
                          build_uT(uT, RuT, 0)
                          if stop == "ut": fw.dead = True
                          DBG("d_uT", uT[:].rearrange("p a b -> p (a b)"), RuT)

                          def norm_rope(i, pa, pr, nh, gap, outb, Routb, slot):
                              w_ = nh * 64
                              ACT(sq[:, 0:w_], pa, AF.Square, pr, [Rsq])
                              OP("dve", lambda e: e.reduce_sum(out=ssq[:, 0:nh], in_=sq[:, 0:w_].rearrange("p (h d) -> p h d", d=64), axis=AX.X), [Rsq], [Rssq])
                              TS("dve", ssq[:, 0:nh], ssq[:, 0:nh], 1.0 / 64, ALU.mult, [Rssq], [Rssq], s2=EPS, op1=ALU.add)
                              ACT(ssq[:, 0:nh], ssq[:, 0:nh], AF.Sqrt, [Rssq], [Rssq])
                              OP("dve", lambda e: e.reciprocal(out=ssq[:, 0:nh], in_=ssq[:, 0:nh]), [Rssq], [Rssq])
                              qv = qn[slot][:, 0:w_]
                              TT("dve", qv.rearrange("p (h d) -> p h d", d=64), pa.rearrange("p (h d) -> p h d", d=64), bl(ssq[:, 0:nh], 64),
                                 ALU.mult, pr + [Rssq], [Rqn[slot]])
                              TT("dve", qv, qv, gap, ALU.mult, [Rqn[slot], Rcst], [Rqn[slot]])
                              if i >= 8:
                                  CP("act", outb, qv, [Rqn[slot]], [Routb])
                              else:
                                  x4 = qv.rearrange("p (h d two) -> p h d two", d=32, two=2)
                                  o4 = outb.rearrange("p (h d two) -> p h d two", d=32, two=2)
                                  x0, x1 = x4[:, :, :, 0], x4[:, :, :, 1]
                                  cc, ss_ = bm(rc[:, i, :], nh), bm(rsn[:, i, :], nh)
                                  r0, r1 = rt[0][:, 0:nh, :], rt[1][:, 0:nh, :]
                                  TT("dve", r0, x0, cc, ALU.mult, [Rqn[slot], Rcst], [Rrt[0]])
                                  TT("pool", r1, x1, ss_, ALU.mult, [Rqn[slot], Rcst], [Rrt[1]])
                                  TT("dve", o4[:, :, :, 0], r0, r1, ALU.subtract, Rrt, [Routb])
                                  TT("dve", r0, x0, ss_, ALU.mult, [Rqn[slot], Rcst], [Rrt[0]])
                                  TT("pool", r1, x1, cc, ALU.mult, [Rqn[slot], Rcst], [Rrt[1]])
                                  TT("dve", o4[:, :, :, 1], r0, r1, ALU.add, Rrt, [Routb])

                          CH[0] = 'a' in os.environ.get('CHP', 'mabfFGMLUS')
                          W, RW = load_w([(lambda t: t[:, 0:4096].rearrange("p (k n) -> p k n", k=8), win[:, :, 0:512])])
                          Wv = W[:, 0:4096].rearrange("p (k n) -> p k n", k=8)
                          for i in range(NT):
                              bk = 2 + (i % 2); sl = i % 2
                              for k in range(8):
                                  MM(bank(bk), uT[:, k, i * 128:(i + 1) * 128], Wv[:, k, :], k == 0, k == 7, [RuT[i], RW], [PB[bk]])
                              norm_rope(i, bank(bk), [PB[bk]], 8, g640[:, 0:512], qkb[sl][:, :], Rqkb[sl], sl)
                              for j in range(4):
                                  src = qkb[sl][:, j * 128:(j + 1) * 128]
                                  OP("pe", lambda e: e.transpose(pst[:, j * 128:(j + 1) * 128], src, ident_b[:]), [Rqkb[sl], Rid], [PT])
                              CP("act", QT[:, :, i * 128:(i + 1) * 128], pst[:, 0:512].rearrange("p (j c) -> p j c", j=4), [PT], RQ)
                          if stop == "passA": fw.dead = True
                          CH[0] = 'b' in os.environ.get('CHP', 'mabfFGMLUS')
                          W, RW = load_w([(lambda t: t[:, 0:2048].rearrange("p (k n) -> p k n", k=8), win[:, :, 512:768])])
                          Wv = W[:, 0:2048].rearrange("p (k n) -> p k n", k=8)
                          for i in range(NT):
                              bk = 2 + (i % 2); sl = i % 2
                              for k in range(8):
                                  MM(bank(bk)[:, 0:256], uT[:, k, i * 128:(i + 1) * 128], Wv[:, k, :], k == 0, k == 7, [RuT[i], RW], [PB[bk]])
                              norm_rope(i, bank(bk)[:, 0:256], [PB[bk]], 4, g640[:, 512:768], qkb[sl][:, 0:256], Rqkb[sl], sl)
                              kt = i if i < 8 else 12 + (i - 8)
                              for lay in range(2):
                                  CP("act", Vsb[:, kt, :, lay, lay * 64:(lay + 1) * 64], bank(bk)[:, 128:256].rearrange("p (h d) -> p h d", h=2), [PB[bk]], [RV])
                              if i >= 8:
                                  p_, t_ = (i - 8) // 2, (i - 8) % 2
                                  CP("act", vf[sl][:], bank(bk)[:, 128:256], [PB[bk]], [Rvf[sl]])
                                  fw.dma("sp", Dout["nk"][p_, l, t_ * 128:(t_ + 1) * 128, :], qn[sl][:, 0:128], reads=[Rqn[sl]], writes=[Rdbg])
                                  fw.dma("sp", Dout["nv"][p_, l, t_ * 128:(t_ + 1) * 128, :], vf[sl][:], reads=[Rvf[sl]], writes=[Rdbg])
                              CP("act", kb2[sl][:, 0:64], qkb[sl][:, 64:128], [Rqkb[sl]], [Rkb2[sl]])
                              CP("act", kb2[sl][:, 64:128], qkb[sl][:, 0:64], [Rqkb[sl]], [Rkb2[sl]])
                              OP("pe", lambda e: e.transpose(pst[:, 512:640], qkb[sl][:, 0:128], ident_b[:]), [Rqkb[sl], Rid], [PT])
                              OP("pe", lambda e: e.transpose(pst[:, 640:768], kb2[sl][:, :], ident_b[:]), [Rkb2[sl], Rid], [PT])
                              kc = i * 128 if i < 8 else 1536 + (i - 8) * 128
                              CP("act", KT[:, 0, kc:kc + 128], pst[:, 512:640], [PT], [RKT])
                              CP("act", KT[:, 1, kc:kc + 128], pst[:, 640:768], [PT], [RKT])
                          if stop == "passB": fw.dead = True
                          CH[0] = 'f' in os.environ.get('CHP', 'mabfFGMLUS')
                          for wb in range(2):
                              W, RW = load_w([(lambda t: t[:, 0:4096].rearrange("p (k n) -> p k n", k=8), win[:, :, 768 + wb * 512:1280 + wb * 512])])
                              Wv = W[:, 0:4096].rearrange("p (k n) -> p k n", k=8)
                              for m4 in range(4):
                                  m = wb * 4 + m4
                                  for tb, (c0, cn) in enumerate(BLKS):
                                      bk = (0, 1, 6)[(m * 3 + tb) % 3]
                                      for k in range(8):
                                          MM(bank(bk), Wv[:, k, m4 * 128:(m4 + 1) * 128], uT[:, k, c0:c0 + cn], k == 0, k == 7,
                                             RuT[c0 // 128:(c0 + cn) // 128] + [RW], [PB[bk]])
                                      CP("act", sfT[:, m, c0:c0 + cn], bank(bk), [PB[bk]], [RsfT[m]])
                          DBG("d_sfT", sfT[:].rearrange("p a b -> p (a b)"), RsfT)
                          fw.barrier()
                      if stop == "inproj": fw.dead = True
                      DBG("d_QT", QT[:].rearrange("p a b -> p (a b)"), RQ)
                      DBG("d_KT", KT[:].rearrange("p a b -> p (a b)"), [RKT])
                      CH[0] = 't' in os.environ.get('CHP', 'mabfFGMLUS')
                      with contextlib.ExitStack() as ph:
                          Eb = [sb(ph, "Eb%d" % j, [128, 512], BF16) for j in range(3)]; REb = [Res("Eb%d" % j) for j in range(3)]
                          rsb = [sb(ph, "rsb%d" % j, [128, 512]) for j in range(2)]; Rrsb = [Res("rsb0"), Res("rsb1")]
                          cnt = 0; cnt2 = 0
                          for si, (t0, Ls) in enumerate(SEQS):
                              if si == 0:
                                  qblks = [(0, 512), (512, 512)]; nkt = 12; kt0 = 0; kc0 = 0
                              else:
                                  qblks = [(t0, 256)]; nkt = 2; kt0 = 12 + 2 * (si - 1); kc0 = 1536 + 256 * (si - 1)
                              for h in range(8):
                                  j = h // 2; hh = h % 2; kvh = h // 4; var = 0 if kvh == hh else 1
                                  pv, sm = (slice(0, 64), slice(64, 128)) if hh == 0 else (slice(64, 128), slice(0, 64))
                                  for (q0, qn) in qblks:
                                      pvb = 3 + (cnt2 % 2); cnt2 += 1
                                      def S_mm(kt, sbk):
                                          MM(bank(sbk)[:, 0:qn], KT[hh * 64:(hh + 1) * 64, var, kc0 + kt * 128:kc0 + (kt + 1) * 128],
                                             QT[hh * 64:(hh + 1) * 64, j, q0:q0 + qn], True, True, [RKT, RQ[j]], [PB[sbk]], chain=False)
                                      slots = [(cnt + i_) % 3 for i_ in range(nkt)]; cnt += nkt
                                      S_mm(0, slots[0])
                                      for kt in range(nkt):
                                          sbk = slots[kt]; eb = sbk
                                          if kt + 1 < nkt:
                                              S_mm(kt + 1, slots[kt + 1])
                                          ACT(Eb[eb][:, 0:qn], bank(sbk)[:, 0:qn], AF.Exp, [PB[sbk]], [REb[eb]], scale=0.125)
                                          MM(bank(pvb)[:, 0:qn], Vsb[:, kt0 + kt, kvh, hh, :], Eb[eb][:, 0:qn], kt == 0, kt == nkt - 1,
                                             [RV, REb[eb]], [PB[pvb]], chain=False)
                                      rb = cnt2 % 2
                                      ACT(rsb[rb][sm, 0:qn], bank(pvb)[sm, 0:qn], AF.Ln, [PB[pvb]], [Rrsb[rb]])
                                      ACT(rsb[rb][sm, 0:qn], rsb[rb][sm, 0:qn], AF.Exp, [Rrsb[rb]], [Rrsb[rb]], scale=-1.0)
                                      TT("dve", attnT[pv, j, q0:q0 + qn], bank(pvb)[pv, 0:qn], rsb[rb][sm, 0:qn], ALU.mult,
                                         [PB[pvb], Rrsb[rb]], [RattnT[j]])
                          fw.barrier()
                  DBG("d_attnT", attnT[:].rearrange("p a b -> p (a b)"), RattnT)
                  CH[0] = 'F' in os.environ.get('CHP', 'mabfFGMLUS')
                  with contextlib.ExitStack() as L3:
                    fourT = sb(L3, "fourT", [128, 4, TOK], BF16); RfourT = [Res("fourT%d" % j) for j in range(4)]
                    with contextlib.ExitStack() as ph:
                        cs128 = sb(ph, "cs128", [128, 256], BF16); cl = sb(ph, "cl", [128, 8, 1024], BF16); sl_ = sb(ph, "sl", [128, 8, 1024], BF16)
                        clp = sb(ph, "clp", [128, 2, 256], BF16); slp = sb(ph, "slp", [128, 2, 256], BF16); Rft = Res("ftab")
                        Pfm = sb(ph, "Pfm", [128, NT, 1024], BF16); RPfm = [Res("Pfm%d" % i) for i in range(NT)]
                        fw.dma("pool", cs128[:], Din["cs128"], writes=[Rft])
                        for k in range(8):
                            fw.dma("pool", cl[:, k, :], Din["cl1024"][:, k, :], writes=[Rft])
                            fw.dma("pool", sl_[:, k, :], Din["sl1024"][:, k, :], writes=[Rft])
                        fw.dma("pool", clp[:], Din["cl256"], writes=[Rft])
                        fw.dma("pool", slp[:], Din["sl256"], writes=[Rft])
                        for i in range(NT):
                            b0 = 2 + 2 * (i % 2)
                            for g in range(4):
                                MM(bank(b0 + g // 2)[:, (g % 2) * 256:(g % 2 + 1) * 256], sfT[:, 4 + g, i * 128:(i + 1) * 128], cs128[:], True, True,
                                   [RsfT[4 + g], Rft], [PB[b0 + g // 2]])
                            CP("act", Pfm[:, i, :], bank(b0, 2), [PB[b0], PB[b0 + 1]], [RPfm[i]])
                        cnt = 0
                        for si, (t0, Ls) in enumerate(SEQS):
                            ntl = Ls // 128; tb0 = t0 // 128
                            tc_, ts_ = (cl, sl_) if si == 0 else (clp, slp)
                            for g in range(4):
                                for lb in range(0, Ls, 512):
                                    n = min(512, Ls - lb)
                                    bk = (0, 1, 6)[cnt % 3]; cnt += 1
                                    for k in range(ntl):
                                        MM(bank(bk)[:, 0:n], Pfm[:, tb0 + k, g * 256:g * 256 + 128], tc_[:, k, lb:lb + n], k == 0, False,
                                           [RPfm[tb0 + k], Rft], [PB[bk]])
                                        MM(bank(bk)[:, 0:n], Pfm[:, tb0 + k, g * 256 + 128:g * 256 + 256], ts_[:, k, lb:lb + n], False, k == ntl - 1,
                                           [RPfm[tb0 + k], Rft], [PB[bk]])
                                    CP("act", fourT[:, g, t0 + lb:t0 + lb + n], bank(bk)[:, 0:n], [PB[bk]], [RfourT[g]])
                        fw.barrier()
                    DBG("d_fourT", fourT[:].rearrange("p a b -> p (a b)"), RfourT)
                    if stop == "four": fw.dead = True
                    CH[0] = 'S' in os.environ.get('CHP', 'mabfFGMLUS')
                    ssmT = sb(L3, "ssmT", [128, 4, TOK], BF16); RssmT = [Res("ssmT%d" % j) for j in range(4)]
                    with contextlib.ExitStack() as S:
                        PowP = [sb(S, "PowP%d" % r, [128, 2, 16, 17]) for r in range(2)]
                        LH = [sb(S, "LH%d" % r, [128, 2, 16, 7]) for r in range(2)]
                        fP = [sb(S, "fP%d" % r, [128, 2, 16]) for r in range(2)]
                        Hb = sb(S, "Hb", [128, 2, 2, 16, NS], BF16)
                        h0 = sb(S, "h0", [128, 2, 2, 16]); stg = sb(S, "stg", [128, 16, 2])
                        RS = Res("ssm"); RHb = Res("Hb"); Rstg = Res("stg")
                        fw.dma("sp", h0[:], Din["h0"][:, l], writes=[RS])
                        def cmul(dr, di, ar, ai, br, bi, m1, m2):
                            TT("dve", m1, ar, br, ALU.mult, [RS], [RS]); TT("dve", m2, ai, bi, ALU.mult, [RS], [RS])
                            TT("dve", dr, m1, m2, ALU.subtract, [RS], [RS])
                            TT("dve", m1, ar, bi, ALU.mult, [RS], [RS]); TT("dve", m2, ai, br, ALU.mult, [RS], [RS])
                            TT("dve", di, m1, m2, ALU.add, [RS], [RS])
                        for d in range(2):
                            with contextlib.ExitStack() as SD:
                                lam = [sb(SD, "lam%d" % r, [128, 528]) for r in range(2)]
                                ff = [sb(SD, "ff%d" % r, [128, 528]) for r in range(2)]
                                A = [sb(SD, "A%d" % r, [128, 16, NS]) for r in range(2)]
                                with contextlib.ExitStack() as SP:
                                    are, aim, ldt, mag, cc, ss, t1, t2, t3 = [sb(SP, "sp%d" % j, [128, 528]) for j in range(9)]
                                    fw.dma("sp", are[:], Din["lamin"][l, d, 0], writes=[RS])
                                    fw.dma("sp", aim[:], Din["lamin"][l, d, 1], writes=[RS])
                                    fw.dma("sp", ldt[:], Din["lamin"][l, d, 2], writes=[RS])
                                    ACT(ldt[:], ldt[:], AF.Exp, [RS], [RS])
                                    TT("dve", t1[:], are[:], ldt[:], ALU.mult, [RS], [RS])
                                    ACT(mag[:], t1[:], AF.Exp, [RS], [RS])
                                    TT("dve", t2[:], aim[:], ldt[:], ALU.mult, [RS], [RS])
                                    ACT(ss[:], t2[:], AF.Sin, [RS], [RS], scale=1.0 / 16)
                                    ACT(t1[:], t2[:], AF.Sin, [RS], [RS], scale=1.0 / 32)
                                    TT("dve", t1[:], t1[:], t1[:], ALU.mult, [RS], [RS])
                                    TS("dve", cc[:], t1[:], -2.0, ALU.mult, [RS], [RS], s2=1.0, op1=ALU.add)
                                    for _ in range(4):
                                        TT("dve", t1[:], cc[:], cc[:], ALU.mult, [RS], [RS])
                                        TT("dve", t2[:], ss[:], ss[:], ALU.mult, [RS], [RS])
                                        TT("dve", t3[:], cc[:], ss[:], ALU.mult, [RS], [RS])
                                        TT("dve", cc[:], t1[:], t2[:], ALU.subtract, [RS], [RS])
                                        TS("dve", ss[:], t3[:], 2.0, ALU.mult, [RS], [RS])
                                    TT("dve", lam[0][:], mag[:], cc[:], ALU.mult, [RS], [RS])
                                    TT("dve", lam[1][:], mag[:], ss[:], ALU.mult, [RS], [RS])
                                    TT("dve", t1[:], are[:], are[:], ALU.mult, [RS], [RS])
                                    TT("dve", t2[:], aim[:], aim[:], ALU.mult, [RS], [RS])
                                    TT("dve", t1[:], t1[:], t2[:], ALU.add, [RS], [RS])
                                    OP("dve", lambda e: e.reciprocal(out=t1[:], in_=t1[:]), [RS], [RS])
                                    TS("dve", t2[:], lam[0][:], -1.0, ALU.add, [RS], [RS])
                                    TT("dve", t3[:], t2[:], are[:], ALU.mult, [RS], [RS])
                                    TT("dve", cc[:], lam[1][:], aim[:], ALU.mult, [RS], [RS])
                                    TT("dve", t3[:], t3[:], cc[:], ALU.add, [RS], [RS])
                                    TT("dve", ff[0][:], t3[:], t1[:], ALU.mult, [RS], [RS])
                                    TT("dve", t3[:], lam[1][:], are[:], ALU.mult, [RS], [RS])
                                    TT("dve", cc[:], t2[:], aim[:], ALU.mult, [RS], [RS])
                                    TT("dve", t3[:], t3[:], cc[:], ALU.subtract, [RS], [RS])
                                    TT("dve", ff[1][:], t3[:], t1[:], ALU.mult, [RS], [RS])
                                    pm1 = t1[:, 0:256].rearrange("p (g k) -> p g k", g=16); pm2 = t2[:, 0:256].rearrange("p (g k) -> p g k", g=16)
                                    OP("dve", lambda e: e.memset(PowP[0][:, d, :, 0:1], 1.0), [RS], [RS])
                                    OP("dve", lambda e: e.memset(PowP[1][:, d, :, 0:1], 0.0), [RS], [RS])
                                    for r in range(2):
                                        ACT(PowP[r][:, d, :, 1], lam[r][:, 0:16], AF.Identity, [RS], [RS])
                                        ACT(fP[r][:, d, :], ff[r][:, 0:16], AF.Identity, [RS], [RS])
                                    for w in (1, 2, 4, 8):
                                        br_ = PowP[0][:, d, :, w:w + 1].to_broadcast([128, 16, w]); bi_ = PowP[1][:, d, :, w:w + 1].to_broadcast([128, 16, w])
                                        cmul(PowP[0][:, d, :, w + 1:2 * w + 1], PowP[1][:, d, :, w + 1:2 * w + 1],
                                             PowP[0][:, d, :, 1:w + 1], PowP[1][:, d, :, 1:w + 1], br_, bi_, pm1[:, :, 0:w], pm2[:, :, 0:w])
                                    for r in range(2):
                                        ACT(LH[r][:, d, :, 0], PowP[r][:, d, :, 16], AF.Identity, [RS], [RS])
                                    for m in range(6):
                                        cmul(LH[0][:, d, :, m + 1], LH[1][:, d, :, m + 1], LH[0][:, d, :, m], LH[1][:, d, :, m],
                                             LH[0][:, d, :, m], LH[1][:, d, :, m], t1[:, 256:272], t2[:, 256:272])
                                for r in range(2):
                                    OP("dve", lambda e: e.memset(A[r][:].rearrange("p a b -> p (a b)"), 0.0), [RS], [RS])
                                with contextlib.ExitStack() as SW:
                                    W = [sb(SW, "W%d" % r, [128, 16, 128]) for r in range(2)]
                                    WT2 = [[sb(SW, "WT%d_%d" % (z, r), [128, 16, 128], BF16) for r in range(2)] for z in range(2)]
                                    wm = [sb(SW, "wm%d" % r, [128, 8, 128]) for r in range(3)]
                                    BT = [sb(SW, "BT%d" % r, [128, 128]) for r in range(2)]
                                    Lw = [sb(SW, "Lw%d" % r, [128, 128]) for r in range(2)]
                                    RWT2 = [Res("WTa"), Res("WTb")]
                                    def build_W(q):
                                        fs = slice(16 + q * 128, 16 + (q + 1) * 128)
                                        for r in range(2):
                                            fw.dma("sp", BT[r][:], Din["BbdT"][l, d, r, :, q, :], writes=[RS])
                                            ACT(Lw[r][:], lam[r][:, fs], AF.Identity, [RS], [RS])
                                        cmul(W[0][:, 0, :], W[1][:, 0, :], ff[0][:, fs], ff[1][:, fs], BT[0][:], BT[1][:], wm[0][:, 0, :], wm[1][:, 0, :])
                                        for w in (1, 2, 4, 8):
                                            cmul(W[0][:, w:2 * w, :], W[1][:, w:2 * w, :], bm(Lw[0][:], w), bm(Lw[1][:], w),
                                                 W[0][:, 0:w, :], W[1][:, 0:w, :], wm[0][:, 0:w, :], wm[1][:, 0:w, :])
                                            if w < 8:
                                                TT("dve", wm[0][:, 0, :], Lw[0][:], Lw[0][:], ALU.mult, [RS], [RS])
                                                TT("dve", wm[1][:, 0, :], Lw[1][:], Lw[1][:], ALU.mult, [RS], [RS])
                                                TT("dve", wm[2][:, 0, :], Lw[0][:], Lw[1][:], ALU.mult, [RS], [RS])
                                                TT("dve", Lw[0][:], wm[0][:, 0, :], wm[1][:, 0, :], ALU.subtract, [RS], [RS])
                                                TS("dve", Lw[1][:], wm[2][:, 0, :], 2.0, ALU.mult, [RS], [RS])
                                        for r in range(2):
                                            ACT(WT2[q % 2][r][:].rearrange("p a b -> p (a b)"), W[r][:].rearrange("p a b -> p (a b)"), AF.Identity, [RS], [RWT2[q % 2]])
                                    def run_S(q):
                                        WTq = WT2[q % 2]; RWTq = RWT2[q % 2]
                                        Sps = bank(0, 2).rearrange("p (b r c) -> p b r c", b=4, r=2)
                                        for b in range(4):
                                            sv = sfT[32 * b:32 * b + 32, q, :].rearrange("p (c j) -> p c j", j=16)
                                            for r in range(2):
                                                for k in range(16):
                                                    j = 15 - k if d == 0 else k
                                                    MM(Sps[:, b, r, 0:96], WTq[r][32 * b:32 * b + 32, k, :], sv[:, :, j], k == 0, k == 15,
                                                       [RWTq, RsfT[q]], [PB[0], PB[1]], tp=(32 * b, 0))
                                        for r in range(2):
                                            for s_i in range(3):
                                                n = SCH[s_i]; c0 = (0, 64, 80)[s_i]; lo = SOFF[s_i] + (1 if d == 0 else 0)
                                                ACT(A[r][:, 4 * q:4 * q + 4, lo:lo + n], Sps[:, :, r, c0:c0 + n], AF.Identity, [PB[0], PB[1]], [RS])
                                    build_W(0)
                                    for q in range(4):
                                        if q + 1 < 4:
                                            build_W(q + 1)
                                        run_S(q)
                                for r in range(2):
                                    ACT(A[r][:, :, 0 if d == 0 else 64], h0[:, d, r, :], AF.Identity, [RS], [RS])
                                with contextlib.ExitStack() as SC:
                                    sm = [sb(SC, "sm%d" % r, [128, 16, NS]) for r in range(3)]
                                    for m in range(7):
                                        dd = 1 << m
                                        groups = []
                                        if dd < 65:
                                            groups.append((lambda t_, lo, hi: t_[:, :, lo:hi], 65, None))
                                        if dd < 17:
                                            groups.append((lambda t_, lo, hi: t_[:, :, 65:99].rearrange("p g (s c) -> p g s c", s=2)[:, :, :, lo:hi], 17, 2))
                                        for view, n, ns in groups:
                                            cntn = n - dd
                                            (dlo, dhi, slo, shi) = (dd, n, 0, cntn) if d == 0 else (0, cntn, dd, n)
                                            if ns is None:
                                                Lr = bl(LH[0][:, d, :, m], cntn); Li = bl(LH[1][:, d, :, m], cntn)
                                                tv = lambda t_: t_[:, :, 0:cntn]
                                            else:
                                                Lr = LH[0][:, d, :, m].unsqueeze(2).unsqueeze(3).to_broadcast([128, 16, 2, cntn])
                                                Li = LH[1][:, d, :, m].unsqueeze(2).unsqueeze(3).to_broadcast([128, 16, 2, cntn])
                                                tv = lambda t_: t_[:, :, 0:2 * cntn].rearrange("p g (s c) -> p g s c", s=2)
                                            sr, si = view(A[0], slo, shi), view(A[1], slo, shi)
                                            dr, di = view(A[0], dlo, dhi), view(A[1], dlo, dhi)
                                            m1, m2, m3 = tv(sm[0]), tv(sm[1]), tv(sm[2])
                                            TT("dve", m1, sr, Lr, ALU.mult, [RS], [RS]); TT("dve", m2, si, Li, ALU.mult, [RS], [RS])
                                            TT("dve", m1, m1, m2, ALU.subtract, [RS], [RS])
                                            TT("dve", m2, si, Lr, ALU.mult, [RS], [RS]); TT("dve", m3, sr, Li, ALU.mult, [RS], [RS])
                                            TT("dve", m2, m2, m3, ALU.add, [RS], [RS])
                                            TT("dve", dr, dr, m1, ALU.add, [RS], [RS]); TT("dve", di, di, m2, ALU.add, [RS], [RS])
                                for r in range(2):
                                    ACT(Hb[:, d, r, :, :], A[r][:], AF.Identity, [RS], [RHb])
                                for s_i in (1, 2):
                                    idx = SOFF[s_i] + (16 if d == 0 else 0)
                                    for r in range(2):
                                        ACT(stg[:, :, r], A[r][:, :, idx], AF.Identity, [RS], [Rstg])
                                    fw.dma("sp", Dout["nst"][s_i - 1, l, d].rearrange("(pair g2) p r -> (g2 p) pair r", g2=2), stg[:], reads=[Rstg], writes=[Rdbg])
                                DBG("d_A%d" % d, A[0][:].rearrange("p a b -> p (a b)"), [RS])
                                fw.barrier()
                        with contextlib.ExitStack() as SQ:
                            CL = [[sb(SQ, "CL%d%d" % (d, r), [128, 4, 17, 32], BF16) for r in range(2)] for d in range(2)]
                            Kblk = [sb(SQ, "Kblk%d" % d, [128, 16, 128], BF16) for d in range(2)]
                            Bb = [[sb(SQ, "Bb%d%d" % (d, r), [128, 4, 32], BF16) for r in range(2)] for d in range(2)]
                            Dbl = sb(SQ, "Dbl", [128, 4, 128], BF16)
                            cm = [sb(SQ, "cm%d" % j, [128, 2, 17, 32]) for j in range(2)]
                            Cl = [sb(SQ, "Cl%d" % r, [128, 4, 32]) for r in range(2)]
                            Bl = [sb(SQ, "Bl%d" % r, [128, 4, 32]) for r in range(2)]
                            bt = [sb(SQ, "bt%d" % r, [128, 4, 32]) for r in range(2)]
                            gt = [sb(SQ, "gt%d" % r, [128, 512]) for r in range(2)]; Rgt = [Res("gt0"), Res("gt1")]
                            RCL = Res("CL"); RK = Res("Kblk")
                            ydb = sb(SQ, "ydb", [128, 512]); Rydb = Res("ydb")
                            fw.dma("pool", Dbl[:], Din["Dblk"][l], writes=[RCL])
                            for q in range(4):
                                ps4 = slice(4 * q, 4 * q + 4)
                                for d in range(2):
                                    for r in range(2):
                                        fw.dma("sp", Cl[r][:], Din["CbdP"][l, d, r, :, ps4, :], writes=[RS])
                                        fw.dma("sp", Bl[r][:], Din["BbdP"][l, d, r, :, ps4, :], writes=[RS])
                                    for hp in range(2):
                                        pp = slice(2 * hp, 2 * hp + 2); pq = slice(4 * q + 2 * hp, 4 * q + 2 * hp + 2)
                                        Pr = PowP[0][:, d, pq, :].unsqueeze(3).to_broadcast([128, 2, 17, 32])
                                        Pi = PowP[1][:, d, pq, :].unsqueeze(3).to_broadcast([128, 2, 17, 32])
                                        Cr = Cl[0][:, pp, :].unsqueeze(2).to_broadcast([128, 2, 17, 32])
                                        Ci = Cl[1][:, pp, :].unsqueeze(2).to_broadcast([128, 2, 17, 32])
                                        TT("dve", cm[0][:], Pr, Cr, ALU.mult, [RS], [RS]); TT("dve", cm[1][:], Pi, Ci, ALU.mult, [RS], [RS])
                                        TT("dve", CL[d][0][:, pp, :, :], cm[0][:], cm[1][:], ALU.subtract, [RS], [RCL])
                                        TT("dve", cm[0][:], Pr, Ci, ALU.mult, [RS], [RS]); TT("dve", cm[1][:], Pi, Cr, ALU.mult, [RS], [RS])
                                        TT("dve", cm[0][:], cm[0][:], cm[1][:], ALU.add, [RS], [RS])
                                        TS("dve", CL[d][1][:, pp, :, :], cm[0][:], -1.0, ALU.mult, [RS], [RCL])
                                    fr = bl(fP[0][:, d, ps4], 32); fi = bl(fP[1][:, d, ps4], 32)
                                    TT("dve", bt[0][:], fr, Bl[0][:], ALU.mult, [RS], [RS]); TT("dve", bt[1][:], fi, Bl[1][:], ALU.mult, [RS], [RS])
                                    TT("dve", Bb[d][0][:], bt[0][:], bt[1][:], ALU.subtract, [RS], [RCL])
                                    TT("dve", bt[0][:], fr, Bl[1][:], ALU.mult, [RS], [RS]); TT("dve", bt[1][:], fi, Bl[0][:], ALU.mult, [RS], [RS])
                                    TT("dve", Bb[d][1][:], bt[0][:], bt[1][:], ALU.add, [RS], [RCL])
                                    kb = 6
                                    for b in range(4):
                                        MM(bank(kb)[32 * b:32 * b + 32, :], Bb[d][0][:, b, :], CL[d][0][:, b, 0:16, :].rearrange("p t n -> p (t n)"), True, False,
                                           [RCL], [PB[kb]], tp=(0, 32 * b))
                                        MM(bank(kb)[32 * b:32 * b + 32, :], Bb[d][1][:, b, :], CL[d][1][:, b, 0:16, :].rearrange("p t n -> p (t n)"), False, True,
                                           [RCL], [PB[kb]], tp=(0, 32 * b))
                                    for cb in range(4):
                                        TS("dve", Kblk[d][:, :, 32 * cb:32 * cb + 32], bank(kb).rearrange("p (t n) -> p t n", n=32), maskP[:, cb:cb + 1], ALU.mult,
                                           [PB[kb], Rmask], [RK])
                                for tb, (c0, cn) in enumerate(BLKS):
                                    yb = 3 + tb
                                    Yb = bank(yb).rearrange("p (c j) -> p c j", j=16)
                                    sblk = sfT[:, q, c0:c0 + cn].rearrange("p (c j) -> p c j", j=16)
                                    MM(bank(yb), Dbl[:, q, :], sfT[:, q, c0:c0 + cn], True, False, [RCL, RsfT[q]], [PB[yb]])
                                    for d in range(2):
                                        for t in range(16):
                                            if d == 0:
                                                o_, r_ = Yb[:, :, t:16], sblk[:, :, 0:16 - t]
                                            else:
                                                o_, r_ = Yb[:, :, 0:16 - t], sblk[:, :, t:16]
                                            MM(o_, Kblk[d][:, t, :], r_, False, False, [RK, RsfT[q]], [PB[yb]])
                                    GC = os.environ.get('GC', '1') == '1'
                                    for d in range(2):
                                        for j in range(16):
                                            tt = j + 1 if d == 0 else 16 - j
                                            sh = 0 if d == 0 else 1
                                            for r in range(2):
                                                for b in range(4):
                                                    gfirst = (b == 0 and r == 0)
                                                    if tb < 2:
                                                        rhs = Hb[:, d, r, 4 * q + b, 32 * tb + sh:32 * tb + sh + 32]
                                                        o_ = Yb[32 * b:32 * b + 32, :, j]
                                                    else:
                                                        rhs = Hb[:, d, r, 4 * q + b, 65:99].rearrange("p (s c) -> p s c", s=2)[:, :, sh:sh + 16]
                                                        o_ = bank(yb).rearrange("p (s c j) -> p s c j", s=2, j=16)[32 * b:32 * b + 32, :, :, j]
                                                    last = (d == 1 and b == 3 and j == 15 and r == 1)
                                                    MM(o_, CL[d][r][:, b, tt, :], rhs, False, last, [RCL, RHb], [PB[yb]], tp=(0, 32 * b),
                                                       chain=(("gfirst" if gfirst else "gnext") if GC else False))
                                    if ("d_y%d_%d" % (q, tb)) in Ddbg:
                                        ACT(ydb[:], bank(yb), AF.Identity, [PB[yb]], [Rydb])
                                        DBG("d_y%d_%d" % (q, tb), ydb[:], [Rydb])
                                    g_ = tb % 2
                                    ACT(gt[g_][:], bank(yb), AF.Square, [PB[yb]], [Rgt[g_]])
                                    TS("dve", gt[g_][:], gt[g_][:], 0.044715, ALU.mult, [Rgt[g_]], [Rgt[g_]], s2=1.0, op1=ALU.add)
                                    TT("dve", gt[g_][:], gt[g_][:], bank(yb), ALU.mult, [Rgt[g_], PB[yb]], [Rgt[g_]])
                                    ACT(gt[g_][:], gt[g_][:], AF.Sigmoid, [Rgt[g_]], [Rgt[g_]], scale=1.5957691216)
                                    TT("dve", sfT[:, q, c0:c0 + cn], gt[g_][:], bank(yb), ALU.mult, [Rgt[g_], PB[yb]], [RsfT[q]])
                            fw.barrier()
                    CH[0] = 'G' in os.environ.get('CHP', 'mabfFGMLUS')
                    with contextlib.ExitStack() as ph:
                        alloc_ws(ph)
                        bgl = sb(ph, "bgl", [128, 4]); Rbgl = Res("bgl"); sg = [sb(ph, "sg%d" % j, [128, 512]) for j in range(2)]; Rsg = [Res("sg0"), Res("sg1")]
                        fw.dma("sp", bgl[:], Din["bglu"][:, l, :], writes=[Rbgl])
                        Wg, RWg = load_w([(lambda t: t[:, 0:2048].rearrange("p (k n) -> p k n", k=4), Din["w_glu"][l].rearrange("(k p) n -> p k n", p=128))])
                        Wgv = Wg[:, 0:2048].rearrange("p (k n) -> p k n", k=4)
                        cnt = 0
                        for m in range(4):
                            for tb, (c0, cn) in enumerate(BLKS):
                                bk = (0, 1, 2)[cnt % 3]; sj = cnt % 2; cnt += 1
                                for k in range(4):
                                    MM(bank(bk), Wgv[:, k, m * 128:(m + 1) * 128], sfT[:, k, c0:c0 + cn], k == 0, k == 3, [RWg, RsfT[k]], [PB[bk]])
                                ACT(sg[sj][:], bank(bk), AF.Sigmoid, [PB[bk], Rbgl], [Rsg[sj]], bias=bgl[:, m:m + 1])
                                TT("dve", ssmT[:, m, c0:c0 + cn], sg[sj][:], sfT[:, m, c0:c0 + cn], ALU.mult, [Rsg[sj], RsfT[m]], [RssmT[m]])
                        fw.barrier()
                    DBG("d_gy", sfT[:, 0:4, :].rearrange("p a b -> p (a b)"), RsfT[0:4])
                    DBG("d_ssmT", ssmT[:].rearrange("p a b -> p (a b)"), RssmT)
                    if stop == "ssm": fw.dead = True
                    CH[0] = 'M' in os.environ.get('CHP', 'mabfFGMLUS')
                    with contextlib.ExitStack() as M:
                        mergedT = sb(M, "mergedT", [128, 8, TOK], BF16); Rmg = [Res("mg%d" % j) for j in range(8)]
                        with contextlib.ExitStack() as ph:
                            alloc_ws(ph)
                            uT = sb(ph, "uT", [128, 8, TOK], BF16); RuT = [Res("uT%d" % i) for i in range(NT)]
                            gs = [sb(ph, "gs%d" % j, [128, 512]) for j in range(3)]; Rgs = [Res("gs%d" % j) for j in range(3)]
                            ac = [sb(ph, "ac%d" % j, [128, 512]) for j in range(2)]; Rac = [Res("ac0"), Res("ac1")]
                            build_uT(uT, RuT, 0)
                            win = Din["w_in"][l].rearrange("(k p) n -> p k n", p=128)
                            wbr = [Din[nm][l].rearrange("(k p) n -> p k n", p=128) for nm in ("w_br_attn", "w_br_ssm", "w_br_four")]
                            srcs = [(attnT, RattnT), (ssmT, RssmT), (fourT, RfourT)]
                            for c in range(8):
                                Wg, RWg = load_w([((lambda t, g=g: t[:, g * 1024:(g + 1) * 1024].rearrange("p (k n) -> p k n", k=8)),
                                                   win[:, :, 1792 + g * 1024 + c * 128:1792 + g * 1024 + (c + 1) * 128]) for g in range(3)])
                                Wb, RWb = load_w([((lambda t, x=x: t[:, x * 512:(x + 1) * 512].rearrange("p (k n) -> p k n", k=4)),
                                                   wbr[x][:, :, c * 128:(c + 1) * 128]) for x in range(3)])
                                for tb, (c0, cn) in enumerate(BLKS):
                                    for g in range(3):
                                        Wgv = Wg[:, g * 1024:(g + 1) * 1024].rearrange("p (k n) -> p k n", k=8)
                                        for k in range(8):
                                            MM(bank(g), Wgv[:, k, :], uT[:, k, c0:c0 + cn], k == 0, k == 7, RuT[c0 // 128:(c0 + cn) // 128] + [RWg], [PB[g]])
                                        ACT(gs[g][:], bank(g), AF.Sigmoid, [PB[g]], [Rgs[g]])
                                    for x in range(3):
                                        Wbv = Wb[:, x * 512:(x + 1) * 512].rearrange("p (k n) -> p k n", k=4)
                                        st_, Rst = srcs[x]
                                        for k in range(4):
                                            MM(bank(3 + x), Wbv[:, k, :], st_[:, k, c0:c0 + cn], k == 0, k == 3, [Rst[k], RWb], [PB[3 + x]])
                                    TT("dve", ac[0][:], gs[0][:], bank(3), ALU.mult, [Rgs[0], PB[3]], [Rac[0]])
                                    TT("dve", ac[1][:], gs[1][:], bank(4), ALU.mult, [Rgs[1], PB[4]], [Rac[1]])
                                    TT("dve", ac[0][:], ac[0][:], ac[1][:], ALU.add, Rac, [Rac[0]])
                                    TT("dve", ac[1][:], gs[2][:], bank(5), ALU.mult, [Rgs[2], PB[5]], [Rac[1]])
                                    TT("dve", mergedT[:, c, c0:c0 + cn], ac[0][:], ac[1][:], ALU.add, Rac, [Rmg[c]])
                            fw.barrier()
                        DBG("d_mergedT", mergedT[:].rearrange("p a b -> p (a b)"), Rmg)
                        if stop == "merge": fw.dead = True
                        layer_norm_residual(l, 0, 8, lambda i, k: mergedT[:, k, i * 128:(i + 1) * 128], lambda i, k: [Rmg[k]],
                                            Din["w_out"][l].rearrange("(k p) n -> p k n", p=128))
              if "d_x1" in Ddbg:
                  for i in range(NT):
                      fw.dma("sp", Ddbg["d_x1"][i * 128:(i + 1) * 128, :], X[:, i, :], reads=[RX[i]], writes=[Rdbg])
              if stop == "ln1": fw.dead = True
              CH[0] = 'U' in os.environ.get('CHP', 'mabfFGMLUS')
              with contextlib.ExitStack() as Fs:
                  actT = sb(Fs, "actT", [128, 22, TOK], BF16); Ract = [Res("act%d" % j) for j in range(22)]
                  with contextlib.ExitStack() as ph:
                      alloc_ws(ph)
                      uT = sb(ph, "uT", [128, 8, TOK], BF16); RuT = [Res("uT%d" % i) for i in range(NT)]
                      cvp = sb(ph, "cvp", [128, 44, 4]); Rcv = Res("cvp")
                      hc = [sb(ph, "hc%d" % j, [128, TOK]) for j in range(2)]; Rhc = [Res("hc0"), Res("hc1")]
                      fw.dma("sp", cvp[:], Din["convp"][:, l], writes=[Rcv])
                      build_uT(uT, RuT, 1)
                      wup = Din["w_up"][l].rearrange("(k p) n -> p k n", p=128)
                      for ch in range(44):
                          if ch % 4 == 0:
                              W, RW = load_w([(lambda t: t[:, 0:4096].rearrange("p (k n) -> p k n", k=8), wup[:, :, ch * 128:ch * 128 + 512])])
                              Wv = W[:, 0:4096].rearrange("p (k n) -> p k n", k=8)
                          cw = ch % 4; pb0 = 3 * (ch % 2); hj = ch % 2
                          prs = [PB[pb0], PB[pb0 + 1], PB[pb0 + 2]]
                          for tb, (c0, cn) in enumerate(BLKS):
                              for k in range(8):
                                  MM(bank(pb0 + tb), Wv[:, k, cw * 128:(cw + 1) * 128], uT[:, k, c0:c0 + cn], k == 0, k == 7,
                                     RuT[c0 // 128:(c0 + cn) // 128] + [RW], [prs[tb]])
                          hp = bank(pb0, 3); h_ = hc[hj]
                          ACT(h_[:], hp, AF.Identity, prs + [Rcv], [Rhc[hj]], scale=cvp[:, ch, 1:2], bias=cvp[:, ch, 3:4])
                          def stt(o_, i0, sc, i1):
                              OP("dve", lambda e: e.scalar_tensor_tensor(out=o_, in0=i0, scalar=sc, in1=i1, op0=ALU.mult, op1=ALU.add),
                                 prs + [Rcv, Rhc[hj]], [Rhc[hj]])
                          stt(h_[:, 1:1024], hp[:, 0:1023], cvp[:, ch, 0:1], h_[:, 1:1024])
                          stt(h_[:, 0:1023], hp[:, 1:1024], cvp[:, ch, 2:3], h_[:, 0:1023])
                          h3 = h_[:, 1024:1536].rearrange("p (s c) -> p s c", s=2); p3 = hp[:, 1024:1536].rearrange("p (s c) -> p s c", s=2)
                          stt(h3[:, :, 1:256], p3[:, :, 0:255], cvp[:, ch, 0:1], h3[:, :, 1:256])
                          stt(h3[:, :, 0:255], p3[:, :, 1:256], cvp[:, ch, 2:3], h3[:, :, 0:255])
                          if ch < 22:
                              ACT(actT[:, ch, :], h_[:], AF.Silu, [Rhc[hj]], [Ract[ch]])
                          else:
                              TT("dve", actT[:, ch - 22, :], actT[:, ch - 22, :], h_[:], ALU.mult, [Ract[ch - 22], Rhc[hj]], [Ract[ch - 22]])
                      fw.barrier()
                  DBG("d_actT", actT[:].rearrange("p a b -> p (a b)"), Ract)
                  if stop == "ffn": fw.dead = True
                  layer_norm_residual(l, 1, 22, lambda i, k: actT[:, k, i * 128:(i + 1) * 128], lambda i, k: [Ract[k]],
                                      Din["w_down"][l].rearrange("(k p) n -> p k n", p=128))
        layers()
        fw.dead = False
        for i in range(NT):
            fw.dma("sp", Dout["y"][i * 128:(i + 1) * 128, :], X[:, i, :], reads=[RX[i]], writes=[Rdbg])
        fw.finish()
    return nc


def kernel(**inputs):
    inp = {k: np.asarray(v) for k, v in inputs.items()}
    S = prep_shared(inp)
    nc = build()
    in_maps = []
    for cid in range(8):
        m = dict(S); m.update(prep_core(inp, cid))
        in_maps.append(m)
    res = run_bass_kernel_spmd(nc, in_maps, core_ids=list(range(8)))
    y_p = np.zeros((16, 256, 1024), np.float32); y_s = np.zeros((8, 1024, 1024), np.float32)
    nk = np.zeros((16, 2, 256, 2, 64), np.float32); nv = np.zeros((16, 2, 256, 2, 64), np.float32)
    nst = np.zeros((16, 2, 2, 32, 64, 2), np.float32)
    for cid in range(8):
        r = res.results[cid]
        y_s[cid] = r["y"][0:1024]
        y_p[2 * cid:2 * cid + 2] = r["y"][1024:1536].reshape(2, 256, 1024)
        nk[2 * cid:2 * cid + 2] = r["nk"].reshape(2, 2, 256, 2, 64)
        nv[2 * cid:2 * cid + 2] = r["nv"].reshape(2, 2, 256, 2, 64)
        nst[2 * cid:2 * cid + 2] = r["nst"]
    return (y_p, y_s, nk, nv, nst)
```

```python
import contextlib, os
SK = os.environ.get("SK", "")
import numpy as np
import concourse.bass as bass
import concourse.mybir as mybir
from concourse.bass_utils import run_bass_kernel_spmd

F32 = mybir.dt.float32; BF16 = mybir.dt.bfloat16
AF = mybir.ActivationFunctionType; ALU = mybir.AluOpType; AX = mybir.AxisListType

D_MODEL = 1024; DEPTH = 2; D_IN = 4864; D_FF = 2816
ALPHA = (2 * DEPTH) ** 0.25
EPS = 1e-6
NT = 12; TOK = 1536; T = 16
SEQS = [(0, 1024), (1024, 256), (1280, 256)]
BLKS = [(0, 512), (512, 512), (1024, 512)]
TILE_VEC = [0] * 8 + [1] * 4
NS = 99
SOFF = [0, 65, 82]
SCH = [64, 16, 16]


class StopBuild(Exception):
    pass


class Res:
    __slots__ = ("name", "w", "r")
    def __init__(self, name):
        self.name = name; self.w = None; self.r = {}


class Eng:
    def __init__(self, name, h, sem):
        self.name = name; self.h = h; self.sem = sem; self.count = 0; self.waited = {}


class FW:
    NDMA = 32
    def __init__(self, nc, es):
        self.nc = nc; self.sems = {}; self.eng = {}
        for name, h in (("pe", nc.tensor), ("act", nc.scalar), ("dve", nc.vector), ("pool", nc.gpsimd), ("sp", nc.sync)):
            s = es.enter_context(nc.semaphore("s_" + name))
            self.sems[name] = s; self.eng[name] = Eng(name, h, s)
        self.dma_sems = []
        for i in range(self.NDMA):
            s = es.enter_context(nc.semaphore("s_dma%d" % i))
            self.sems["dma%d" % i] = s; self.dma_sems.append(["dma%d" % i, 0])
        self.dma_i = 0; self.ninstr = 0; self.dead = False; self.prev_plain = False

    def _wait(self, e, key, val):
        if e.waited.get(key, 0) >= val: return
        e.h.wait_ge(self.sems[key], val); e.waited[key] = val

    def _deps(self, e, reads, writes, chain=False):
        deps = {}
        def add(kv):
            if kv is None: return
            k, v = kv
            if deps.get(k, 0) < v: deps[k] = v
        for r in reads: add(r.w)
        for w in writes:
            if w.w is not None and not (chain and w.w[0] == e.name):
                add(w.w)
            for k, v in w.r.items():
                if k == e.name: continue
                add((k, v))
        return deps

    def op(self, engname, fn, reads=(), writes=(), chain=False):
        if self.dead: return None
        e = self.eng[engname]
        if engname == "pe" and chain in ("gfirst", "gnext"):
            if chain == "gfirst" and e.count > 0:
                self._wait(e, "pe", e.count)
            chain = True; self.prev_plain = False
        elif engname == "pe":
            plain = chain
            chain = chain and self.prev_plain
            if plain and not chain and e.count > 0:
                self._wait(e, "pe", e.count)
            self.prev_plain = plain
        for k, v in self._deps(e, reads, writes, chain).items(): self._wait(e, k, v)
        ins = fn(e.h); e.count += 1; ins.then_inc(e.sem, 1)
        for r in reads: r.r[e.name] = e.count
        for w in writes: w.w = (e.name, e.count); w.r = {}
        self.ninstr += 1
        return ins

    def dma(self, qname, out, in_, reads=(), writes=()):
        if self.dead: return None
        e = self.eng[qname]
        deps = self._deps(e, reads, writes)
        slot = self.dma_sems[self.dma_i % self.NDMA]; self.dma_i += 1
        key = slot[0]
        if slot[1] > 0: deps[key] = max(deps.get(key, 0), slot[1])
        for k, v in deps.items(): self._wait(e, k, v)
        ins = e.h.dma_start(out=out, in_=in_)
        slot[1] += 16; ins.then_inc(self.sems[key], 16)
        for r in reads: r.r[key] = slot[1]
        for w in writes: w.w = (key, slot[1]); w.r = {}
        self.ninstr += 1
        return ins

    def barrier(self):
        if self.dead: return
        targets = {n: e.count for n, e in self.eng.items() if e.count > 0}
        for key, cnt in self.dma_sems:
            if cnt > 0: targets[key] = cnt
        for n, e in self.eng.items():
            for k, v in targets.items(): self._wait(e, k, v)

    def finish(self):
        self.dead = False
        self.barrier()


def bl(ap, n):
    return ap.unsqueeze(2).to_broadcast([ap.shape[0], ap.shape[1], n])


def bm(ap, n):
    return ap.unsqueeze(1).to_broadcast([ap.shape[0], n, ap.shape[1]])


def _rep(a, n=128):
    return np.ascontiguousarray(np.broadcast_to(a[None], (n,) + a.shape))


def prep_shared(inp):
    f = lambda a: np.ascontiguousarray(a, dtype=np.float32)
    L = DEPTH
    S = {}
    for k in ("w_ada", "w_in", "w_glu", "w_br_ssm", "w_br_four", "w_out", "w_up", "w_down"):
        S[k] = f(inp[k])
    S["w_br_attn"] = f(inp["w_br_attn"])
    S["b_adaP"] = f(inp["b_ada"].reshape(L, 48, 128).transpose(2, 0, 1))
    S["bgbc"] = f(np.stack([np.stack([_rep(inp["b_ada"][l, 2048:3072]), _rep(inp["b_ada"][l, 5120:6144])], 1) for l in range(L)]))
    S["lnbc"] = f(np.stack([np.stack([_rep(inp[k][l]) for k in ("ln1_g", "ln1_b", "ln2_g", "ln2_b")]) for l in range(L)]))
    cw = inp["conv_w"].reshape(L, 3, 44, 128).transpose(3, 0, 2, 1)
    cb = inp["conv_b"].reshape(L, 44, 128).transpose(2, 0, 1)[..., None]
    S["convp"] = f(np.concatenate([cw, cb], -1))
    S["bglu"] = f(inp["b_glu"].reshape(L, 4, 128).transpose(2, 0, 1))
    S["g640"] = f(np.stack([_rep(np.concatenate([np.tile(inp["q_norm_g"][l], 8), np.tile(inp["k_norm_g"][l], 4)])) for l in range(L)]))
    def lay(a):
        aP = a.reshape(L, 2, 16, 2, 64).transpose(0, 1, 3, 4, 2).reshape(L, 2, 128, 16)
        aF = a.reshape(L, 2, 4, 4, 2, 64).transpose(0, 1, 3, 2, 4, 5)
        aF = np.broadcast_to(aF[:, :, :, None], (L, 2, 4, 32, 4, 2, 64)).reshape(L, 2, 128, 512)
        return np.concatenate([aP, aF], -1)
    ldt = np.broadcast_to(inp["ssm_log_dt"][..., None], (L, 2, 32, 64))
    S["lamin"] = f(np.stack([lay(inp["ssm_a_re"]), lay(inp["ssm_a_im"]), lay(ldt)], 2))
    def bbdP(b):
        Bp = b.reshape(L, 2, 16, 2, 64, 16)
        out = np.zeros((L, 2, 2, 64, 16, 2, 16), np.float32)
        for g2 in range(2):
            out[:, :, g2, :, :, g2, :] = Bp[:, :, :, g2].transpose(0, 1, 3, 2, 4)
        return out.reshape(L, 2, 128, 16, 32)
    def bbdT(b):
        Bq = b.reshape(L, 2, 4, 4, 2, 64, 16)
        out = np.zeros((L, 2, 4, 2, 16, 4, 2, 64), np.float32)
        for g2 in range(2):
            out[:, :, :, g2, :, :, g2, :] = Bq[:, :, :, :, g2].transpose(0, 1, 3, 5, 2, 4)
        return out.reshape(L, 2, 128, 4, 128)
    def cbdP(c):
        Cp = c.reshape(L, 2, 16, 2, 16, 64)
        out = np.zeros((L, 2, 2, 64, 16, 2, 16), np.float32)
        for g2 in range(2):
            out[:, :, g2, :, :, g2, :] = Cp[:, :, :, g2].transpose(0, 1, 4, 2, 3)
        return out.reshape(L, 2, 128, 16, 32)
    S["BbdP"] = f(np.stack([bbdP(inp["ssm_b_re"]), bbdP(inp["ssm_b_im"])], 2))
    S["BbdT"] = f(np.stack([bbdT(inp["ssm_b_re"]), bbdT(inp["ssm_b_im"])], 2))
    S["CbdP"] = f(np.stack([cbdP(inp["ssm_c_re"]), cbdP(inp["ssm_c_im"])], 2))
    Dblk = np.zeros((L, 128, 4, 128), np.float32)
    for l in range(L):
        for q in range(4):
            Dblk[l, np.arange(128), q, np.arange(128)] = inp["ssm_d"][l, q * 128:(q + 1) * 128]
    S["Dblk"] = Dblk
    S["ident"] = np.eye(128, dtype=np.float32)
    mk = np.zeros((128, 4), np.float32); mk[np.arange(128), np.arange(128) // 32] = 1.0
    S["maskP"] = mk
    t = np.arange(1024)
    freqs = (10000.0 ** (-np.arange(16, dtype=np.float32) / 16)).astype(np.float32)
    ang = np.concatenate([(t // 64).astype(np.float32)[:, None] * freqs, (t % 64).astype(np.float32)[:, None] * freqs], -1)
    S["ropec"] = f(np.cos(ang).reshape(8, 128, 32).transpose(1, 0, 2))
    S["ropes"] = f(np.sin(ang).reshape(8, 128, 32).transpose(1, 0, 2))
    def dft(n):
        i = np.arange(n, dtype=np.int64)
        m = (i[:, None] * i[None, :]) % n
        a = 2.0 * np.pi * m.astype(np.float64) / n
        return np.cos(a) / np.sqrt(n), np.sin(a) / np.sqrt(n)
    c128, s128 = dft(128)
    S["cs128"] = f(np.concatenate([c128, s128], 1))
    for n in (1024, 256):
        c, s = dft(n)
        S["cl%d" % n] = f(c.reshape(n // 128, 128, n).transpose(1, 0, 2))
        S["sl%d" % n] = f((-s).reshape(n // 128, 128, n).transpose(1, 0, 2))
    return S


def prep_core(inp, cid):
    f = lambda a: np.ascontiguousarray(a, dtype=np.float32)
    m = {}
    m["xin"] = f(np.concatenate([inp["x_sample"][cid], inp["x_prompt"][2 * cid:2 * cid + 2].reshape(512, 1024)], 0))
    cv = np.stack([inp["c"][cid], inp["c_ctx"]], -1)
    m["cvec"] = f(cv.reshape(8, 128, 2).transpose(1, 0, 2))
    m["ckT"] = f(inp["cache_k"][cid].reshape(DEPTH, 512, 128).transpose(0, 2, 1))
    m["cv"] = f(inp["cache_v"][cid].reshape(DEPTH, 512, 128))
    st = inp["state_ssm"][cid].reshape(DEPTH, 2, 16, 2, 64, 2)
    m["h0"] = f(st.transpose(3, 4, 0, 1, 5, 2).reshape(128, DEPTH, 2, 2, 16))
    return m


IN_SHAPES = {
    "xin": (1536, 1024), "cvec": (128, 8, 2), "ckT": (2, 128, 512), "cv": (2, 512, 128), "h0": (128, 2, 2, 2, 16),
    "w_ada": (2, 1024, 6144), "w_in": (2, 1024, 4864), "w_glu": (2, 512, 512), "w_br_attn": (2, 512, 1024),
    "w_br_ssm": (2, 512, 1024), "w_br_four": (2, 512, 1024), "w_out": (2, 1024, 1024), "w_up": (2, 1024, 5632),
    "w_down": (2, 2816, 1024), "b_adaP": (128, 2, 48), "bgbc": (2, 128, 2, 1024), "lnbc": (2, 4, 128, 1024),
    "convp": (128, 2, 44, 4), "bglu": (128, 2, 4), "g640": (2, 128, 768), "lamin": (2, 2, 3, 128, 528),
    "BbdP": (2, 2, 2, 128, 16, 32), "BbdT": (2, 2, 2, 128, 4, 128), "CbdP": (2, 2, 2, 128, 16, 32),
    "Dblk": (2, 128, 4, 128), "ident": (128, 128), "maskP": (128, 4), "ropec": (128, 8, 32), "ropes": (128, 8, 32),
    "cs128": (128, 256), "cl1024": (128, 8, 1024), "sl1024": (128, 8, 1024), "cl256": (128, 2, 256), "sl256": (128, 2, 256),
}
OUT_SHAPES = {"y": (1536, 1024), "nk": (2, 2, 256, 128), "nv": (2, 2, 256, 128), "nst": (2, 2, 2, 32, 64, 2)}


def build(n_layers=DEPTH, stop=None, dbg_shapes=None):
    nc = bass.Bass("TRN2", target_bir_lowering=False)
    Din = {k: nc.dram_tensor(k, list(s), F32, kind="ExternalInput").ap() for k, s in IN_SHAPES.items()}
    Dout = {k: nc.dram_tensor(k, list(s), F32, kind="ExternalOutput").ap() for k, s in OUT_SHAPES.items()}
    Ddbg = {k: nc.dram_tensor(k, list(s), F32, kind="ExternalOutput").ap() for k, s in (dbg_shapes or {}).items()}
    es = contextlib.ExitStack()
    with es:
        fw = FW(nc, es)
        uid = [0]
        def sb(st, name, shape, dt=F32):
            uid[0] += 1
            return st.enter_context(nc.sbuf_tensor("%s_%d" % (name, uid[0]), list(shape), dt))
        OP = fw.op
        CH = [True]
        def MM(out, lhsT, rhs, start, stop_, r, w, tp=None, chain=None):
            if chain is None: chain = CH[0] and (tp is None)
            if tp is None:
                return fw.op("pe", lambda e: e.matmul(out, lhsT=lhsT, rhs=rhs, start=start, stop=stop_), r, w, chain=chain)
            return fw.op("pe", lambda e: e.matmul(out, lhsT=lhsT, rhs=rhs, start=start, stop=stop_, tile_position=tp), r, w, chain=chain)
        def TT(eng, out, in0, in1, op, r, w):
            return fw.op(eng, lambda e: e.tensor_tensor(out=out, in0=in0, in1=in1, op=op), r, w)
        def TS(eng, out, in0, s1, op0, r, w, s2=None, op1=None):
            if op1 is None:
                return fw.op(eng, lambda e: e.tensor_scalar(out=out, in0=in0, scalar1=s1, scalar2=None, op0=op0), r, w)
            return fw.op(eng, lambda e: e.tensor_scalar(out=out, in0=in0, scalar1=s1, scalar2=s2, op0=op0, op1=op1), r, w)
        def ACT(out, in_, func, r, w, scale=1.0, bias=0.0):
            return fw.op("act", lambda e: e.activation(out=out, in_=in_, func=func, bias=bias, scale=scale), r, w)
        def CP(eng, out, in_, r, w):
            if eng == "act":
                return fw.op("act", lambda e: e.copy(out=out, in_=in_), r, w)
            return fw.op(eng, lambda e: e.tensor_copy(out=out, in_=in_), r, w)
        Rdbg = Res("dbg")
        def DBG(name, ap, r):
            if name in Ddbg:
                fw.dma("pool", Ddbg[name], ap, reads=r, writes=[Rdbg])

        X = sb(es, "X", [128, NT, 1024]); RX = [Res("X%d" % i) for i in range(NT)]
        ident_f = sb(es, "ident_f", [128, 128]); ident_b = sb(es, "ident_b", [128, 128], BF16); Rid = Res("id")
        maskP = sb(es, "maskP", [128, 4]); Rmask = Res("mask")
        modP = sb(es, "modP", [128, 6, 8, 2]); opsc = sb(es, "opsc", [128, 2, 8, 2]); Rmod = Res("mod")
        csl = sb(es, "csl", [128, 8, 2]); sTb = sb(es, "sTb", [128, 8, 2], BF16); sRep = sb(es, "sRep", [128, 2, 8, 128], BF16)
        Rcs = Res("cs")
        stat = sb(es, "stat", [128, NT, 2, 6]); mv = sb(es, "mv", [128, NT, 2]); rs = sb(es, "rs", [128, NT, 2]); Rstat = Res("stat")
        ps = es.enter_context(nc.psum_tensor("ps", [128, 3584], F32)); PB = [Res("pb%d" % i) for i in range(7)]
        pst = es.enter_context(nc.psum_tensor("pst", [128, 1024], BF16)); PT = Res("pt")
        wslot = [None, None]; Rws = [None, None]
        wctr = [0]
        def alloc_ws(st):
            for i_ in range(2):
                wslot[i_] = sb(st, "wslot%d" % i_, [128, 4096], BF16); Rws[i_] = Res("ws%d" % i_)
        def load_w(views):
            s_ = wctr[0] % 2; wctr[0] += 1
            for dstf, src in views:
                fw.dma("pool", dstf(wslot[s_]), src, writes=[Rws[s_]])
            return wslot[s_], Rws[s_]
        def bank(i, n=1):
            return ps[:, i * 512:(i + n) * 512]

        fw.dma("sp", ident_f[:], Din["ident"], writes=[Rid])
        fw.dma("pool", ident_b[:], Din["ident"], writes=[Rid])
        fw.dma("sp", maskP[:], Din["maskP"], writes=[Rmask])
        for i in range(NT):
            fw.dma("sp", X[:, i, :], Din["xin"][i * 128:(i + 1) * 128, :], writes=[RX[i]])
        fw.dma("sp", csl[:], Din["cvec"], writes=[Rcs])
        ACT(csl[:], csl[:], AF.Silu, [Rcs], [Rcs])
        CP("dve", sTb[:], csl[:], [Rcs], [Rcs])
        for v in range(2):
            CP("dve", sRep[:, v, :, :], bl(csl[:, :, v], 128), [Rcs], [Rcs])

        def build_uT(uT, RuT, sub):
            sh_sec = 0 if sub == 0 else 3
            for i in range(NT):
                v = TILE_VEC[i]
                for kk in range(2):
                    bk = kk
                    for k4 in range(4):
                        k = kk * 4 + k4
                        OP("pe", lambda e: e.transpose(bank(bk)[:, k4 * 128:(k4 + 1) * 128], X[:, i, k * 128:(k + 1) * 128], ident_f[:]),
                           [RX[i], Rid], [PB[bk]])
                    for k4 in range(4):
                        k = kk * 4 + k4
                        ACT(uT[:, k, i * 128:(i + 1) * 128], bank(bk)[:, k4 * 128:(k4 + 1) * 128], AF.Identity,
                            [PB[bk], Rmod], [RuT[i]], scale=opsc[:, sub, k, v:v + 1], bias=modP[:, sh_sec, k, v:v + 1])

        def layer_norm_residual(l, which, kc, lhsT_fn, lres_fn, wsrc3):
            with contextlib.ExitStack() as ph:
                ch_save = CH[0]; CH[0] = 'L' in os.environ.get('CHP', 'mabfFGMLUS')
                alloc_ws(ph)
                lnb = sb(ph, "lnb", [128, 2, 1024]); Rln = Res("lnb")
                tmp = [sb(ph, "lntmp%d" % j, [128, 512]) for j in range(2)]; Rtmp = [Res("lntmp0"), Res("lntmp1")]
                Wd = sb(ph, "Wd", [128, kc, 512], BF16); RWd = Res("Wd")
                fw.dma("sp", lnb[:, 0, :], Din["lnbc"][l, 2 * which], writes=[Rln])
                fw.dma("sp", lnb[:, 1, :], Din["lnbc"][l, 2 * which + 1], writes=[Rln])
                gbc = sb(ph, "gbc", [128, 2, 1024]); Rgbc = Res("gbc")
                bg = sb(ph, "bg", [128, 1024]); Rbg = Res("bg")
                fw.dma("sp", bg[:], Din["bgbc"][l, :, which, :], writes=[Rbg])
                sec = 2 if which == 0 else 5
                wsrc = Din["w_ada"][l].rearrange("(k p) n -> p k n", p=128)
                for half in range(2):
                    Wg, RWg = load_w([(lambda t: t[:, 0:4096].rearrange("p (k n) -> p k n", k=8), wsrc[:, :, sec * 1024 + half * 512:sec * 1024 + (half + 1) * 512])])
                    Wgv = Wg[:, 0:4096].rearrange("p (k n) -> p k n", k=8)
                    for v in range(2):
                        for k in range(8):
                            MM(bank(v), sRep[:, v, k, :], Wgv[:, k, :], k == 0, k == 7, [RWg, Rcs], [PB[v]])
                        TT("dve", gbc[:, v, half * 512:(half + 1) * 512], bank(v), bg[:, half * 512:(half + 1) * 512], ALU.add, [PB[v], Rbg], [Rgbc])
                cnt = 0
                for nb in range(2):
                    cs = slice(nb * 512, (nb + 1) * 512)
                    for k0 in range(0, kc, 8):
                        k1 = min(kc, k0 + 8)
                        fw.dma("pool", Wd[:, k0:k1, :], wsrc3[:, k0:k1, cs], writes=[RWd])
                    for i in range(NT):
                        v = TILE_VEC[i]
                        bk = 2 + (cnt % 4); tj = cnt % 2; cnt += 1
                        for k in range(kc):
                            MM(bank(bk), lhsT_fn(i, k), Wd[:, k, :], k == 0, k == kc - 1, lres_fn(i, k) + [RWd], [PB[bk]])
                        TT("dve", tmp[tj][:], bank(bk), gbc[:, v, cs], ALU.mult, [PB[bk], Rgbc], [Rtmp[tj]])
                        OP("dve", lambda e: e.scalar_tensor_tensor(out=X[:, i, cs], in0=X[:, i, cs], scalar=float(ALPHA), in1=tmp[tj][:],
                                                                   op0=ALU.mult, op1=ALU.add), [RX[i], Rtmp[tj]], [RX[i]])
                for i in range(NT):
                    for hh in range(2):
                        OP("dve", lambda e: e.bn_stats(out=stat[:, i, hh, :], in_=X[:, i, hh * 512:(hh + 1) * 512]), [RX[i]], [Rstat])
                    OP("dve", lambda e: e.bn_aggr(out=mv[:, i, :], in_=stat[:, i, :, :].rearrange("p a b -> p (a b)")), [Rstat], [Rstat])
                TS("dve", rs[:, :, 0], mv[:, :, 1], EPS, ALU.add, [Rstat], [Rstat])
                ACT(rs[:, :, 0], rs[:, :, 0], AF.Sqrt, [Rstat], [Rstat])
                OP("dve", lambda e: e.reciprocal(out=rs[:, :, 0], in_=rs[:, :, 0]), [Rstat], [Rstat])
                OP("dve", lambda e: e.scalar_tensor_tensor(out=rs[:, :, 1], in0=mv[:, :, 0], scalar=-1.0, in1=rs[:, :, 0],
                                                           op0=ALU.mult, op1=ALU.mult), [Rstat], [Rstat])
                for i in range(NT):
                    ACT(X[:, i, :], X[:, i, :], AF.Identity, [RX[i], Rstat], [RX[i]], scale=rs[:, i, 0:1], bias=rs[:, i, 1:2])
                    TT("pool", X[:, i, :], X[:, i, :], lnb[:, 0, :], ALU.mult, [RX[i], Rln], [RX[i]])
                    TT("pool", X[:, i, :], X[:, i, :], lnb[:, 1, :], ALU.add, [RX[i], Rln], [RX[i]])
                fw.barrier()
                CH[0] = ch_save

        def layers():
          for l in range(n_layers):
              with contextlib.ExitStack() as ph:
                  wa = [sb(ph, "wa%d" % i, [128, 8, 512], BF16) for i in range(3)]; Rwa = [Res("wa%d" % i) for i in range(3)]
                  bP = sb(ph, "bP", [128, 48]); Rb = Res("bmod")
                  fw.dma("sp", bP[:], Din["b_adaP"][:, l, :], writes=[Rb])
                  wsrc = Din["w_ada"][l].rearrange("(k p) n -> p k n", p=128)
                  pm = bank(6)[:, 0:96]
                  CH[0] = 'm' in os.environ.get('CHP', 'mabfFGMLUS')
                  for cb in range(12):
                      s = cb % 3
                      sec, half = cb // 2, cb % 2
                      if sec in (2, 5):
                          continue
                      fw.dma("pool", wa[s][:], wsrc[:, :, cb * 512:(cb + 1) * 512], writes=[Rwa[s]])
                      for m in range(4):
                          col = (sec * 8 + half * 4 + m) * 2
                          for k in range(8):
                              MM(pm[:, col:col + 2], wa[s][:, k, m * 128:(m + 1) * 128], sTb[:, k, :], k == 0, k == 7,
                                 [Rwa[s], Rcs], [PB[6]])
                  for sec in (0, 1, 3, 4):
                      TT("dve", modP[:, sec, :, :], pm[:, sec * 16:(sec + 1) * 16].rearrange("p (c v) -> p c v", v=2),
                         bl(bP[:, sec * 8:(sec + 1) * 8], 2), ALU.add, [PB[6], Rb], [Rmod])
                  TS("dve", opsc[:, 0, :, :], modP[:, 1, :, :], 1.0, ALU.add, [Rmod], [Rmod])
                  TS("dve", opsc[:, 1, :, :], modP[:, 4, :, :], 1.0, ALU.add, [Rmod], [Rmod])
                  fw.barrier()
              if stop == "mod": fw.dead = True
              DBG("d_modP", modP[:].rearrange("p a b c -> p (a b c)"), [Rmod])

              with contextlib.ExitStack() as L1:
                  QT = sb(L1, "QT", [128, 4, TOK], BF16); RQ = [Res("QT%d" % j) for j in range(4)]; attnT = QT; RattnT = RQ
                  sfT = sb(L1, "sfT", [128, 8, TOK], BF16); RsfT = [Res("sfT%d" % j) for j in range(8)]
                  with contextlib.ExitStack() as L2:
                      KT = sb(L2, "KT", [128, 2, 2048], BF16); RKT = Res("KT")
                      Vsb = sb(L2, "Vsb", [128, 16, 2, 2, 128], BF16); RV = Res("V")
                      with contextlib.ExitStack() as ph:
                          alloc_ws(ph)
                          uT = sb(ph, "uT", [128, 8, TOK], BF16); RuT = [Res("uT%d" % i) for i in range(NT)]
                          qn = [sb(ph, "qn%d" % j, [128, 512]) for j in range(2)]; Rqn = [Res("qn0"), Res("qn1")]
                          vf = [sb(ph, "vf%d" % j, [128, 128]) for j in range(2)]; Rvf = [Res("vf0"), Res("vf1")]
                          qkb = [sb(ph, "qkb%d" % j, [128, 512], BF16) for j in range(2)]; Rqkb = [Res("qkb0"), Res("qkb1")]
                          sq = sb(ph, "sq", [128, 512]); Rsq = Res("sq")
                          kb2 = [sb(ph, "kb2%d" % j, [128, 128], BF16) for j in range(2)]; Rkb2 = [Res("kb20"), Res("kb21")]
                          ssq = sb(ph, "ssq", [128, 8]); Rssq = Res("ssq")
                          g640 = sb(ph, "g640", [128, 768]); rc = sb(ph, "rc", [128, 8, 32]); rsn = sb(ph, "rsn", [128, 8, 32]); Rcst = Res("cst")
                          rt = [sb(ph, "rt%d" % j, [128, 8, 32]) for j in range(2)]; Rrt = [Res("rt0"), Res("rt1")]
                          win = Din["w_in"][l].rearrange("(k p) n -> p k n", p=128)
                          fw.dma("sp", g640[:], Din["g640"][l], writes=[Rcst])
                          fw.dma("sp", rc[:], Din["ropec"], writes=[Rcst])
                          fw.dma("sp", rsn[:], Din["ropes"], writes=[Rcst])
                          fw.dma("pool", KT[:, 0, 1024:1536], Din["ckT"][l], writes=[RKT])
                          fw.dma("pool", KT[0:64, 1, 1024:1536], Din["ckT"][l, 64:128, :], writes=[RKT])
                          fw.dma("pool", KT[64:128, 1, 1024:1536], Din["ckT"][l, 0:64, :], writes=[RKT])
                          OP("dve", lambda e: e.memset(Vsb[:].rearrange("p a b c d -> p (a b c d)"), 1.0), [], [RV])
                          cvv = Din["cv"][l].rearrange("(t p) (h d) -> p t h d", p=128, h=2)
                          for lay in range(2):
                              for kvh_ in range(2):
                                  fw.dma("pool", Vsb[:, 8:12, kvh_, lay, lay * 64:(lay + 1) * 64], cvv[:, :, kvh_, :], writes=[RV])
                          build_uT(uT, RuT, 0)
                          if stop == "ut": fw.dead = True
                          DBG("d_uT", uT[:].rearrange("p a b -> p (a b)"), RuT)

                          def norm_rope(i, pa, pr, nh, gap, outb, Routb, slot):
                              w_ = nh * 64
                              ACT(sq[:, 0:w_], pa, AF.Square, pr, [Rsq])
                              OP("dve", lambda e: e.reduce_sum(out=ssq[:, 0:nh], in_=sq[:, 0:w_].rearrange("p (h d) -> p h d", d=64), axis=AX.X), [Rsq], [Rssq])
                              TS("dve", ssq[:, 0:nh], ssq[:, 0:nh], 1.0 / 64, ALU.mult, [Rssq], [Rssq], s2=EPS, op1=ALU.add)
                              ACT(ssq[:, 0:nh], ssq[:, 0:nh], AF.Sqrt, [Rssq], [Rssq])
                              OP("dve", lambda e: e.reciprocal(out=ssq[:, 0:nh], in_=ssq[:, 0:nh]), [Rssq], [Rssq])
                              qv = qn[slot][:, 0:w_]
                              TT("dve", qv.rearrange("p (h d) -> p h d", d=64), pa.rearrange("p (h d) -> p h d", d=64), bl(ssq[:, 0:nh], 64),
                                 ALU.mult, pr + [Rssq], [Rqn[slot]])
                              TT("dve", qv, qv, gap, ALU.mult, [Rqn[slot], Rcst], [Rqn[slot]])
                              if i >= 8:
                                  CP("act", outb, qv, [Rqn[slot]], [Routb])
                              else:
                                  x4 = qv.rearrange("p (h d two) -> p h d two", d=32, two=2)
                                  o4 = outb.rearrange("p (h d two) -> p h d two", d=32, two=2)
                                  x0, x1 = x4[:, :, :, 0], x4[:, :, :, 1]
                                  cc, ss_ = bm(rc[:, i, :], nh), bm(rsn[:, i, :], nh)
                                  r0, r1 = rt[0][:, 0:nh, :], rt[1][:, 0:nh, :]
                                  TT("dve", r0, x0, cc, ALU.mult, [Rqn[slot], Rcst], [Rrt[0]])
                                  TT("pool", r1, x1, ss_, ALU.mult, [Rqn[slot], Rcst], [Rrt[1]])
                                  TT("dve", o4[:, :, :, 0], r0, r1, ALU.subtract, Rrt, [Routb])
                                  TT("dve", r0, x0, ss_, ALU.mult, [Rqn[slot], Rcst], [Rrt[0]])
                                  TT("pool", r1, x1, cc, ALU.mult, [Rqn[slot], Rcst], [Rrt[1]])
                                  TT("dve", o4[:, :, :, 1], r0, r1, ALU.add, Rrt, [Routb])

                          CH[0] = 'a' in os.environ.get('CHP', 'mabfFGMLUS')
                          W, RW = load_w([(lambda t: t[:, 0:4096].rearrange("p (k n) -> p k n", k=8), win[:, :, 0:512])])
                          Wv = W[:, 0:4096].rearrange("p (k n) -> p k n", k=8)
                          for i in range(NT):
                              bk = 2 + (i % 2); sl = i % 2
                              for k in range(8):
                                  MM(bank(bk), uT[:, k, i * 128:(i + 1) * 128], Wv[:, k, :], k == 0, k == 7, [RuT[i], RW], [PB[bk]])
                              norm_rope(i, bank(bk), [PB[bk]], 8, g640[:, 0:512], qkb[sl][:, :], Rqkb[sl], sl)
                              for j in range(4):
                                  src = qkb[sl][:, j * 128:(j + 1) * 128]
                                  OP("pe", lambda e: e.transpose(pst[:, j * 128:(j + 1) * 128], src, ident_b[:]), [Rqkb[sl], Rid], [PT])
                              CP("act", QT[:, :, i * 128:(i + 1) * 128], pst[:, 0:512].rearrange("p (j c) -> p j c", j=4), [PT], RQ)
                          if stop == "passA": fw.dead = True
                          CH[0] = 'b' in os.environ.get('CHP', 'mabfFGMLUS')
                          W, RW = load_w([(lambda t: t[:, 0:2048].rearrange("p (k n) -> p k n", k=8), win[:, :, 512:768])])
                          Wv = W[:, 0:2048].rearrange("p (k n) -> p k n", k=8)
                          for i in range(NT):
                              bk = 2 + (i % 2); sl = i % 2
                              for k in range(8):
                                  MM(bank(bk)[:, 0:256], uT[:, k, i * 128:(i + 1) * 128], Wv[:, k, :], k == 0, k == 7, [RuT[i], RW], [PB[bk]])
                              norm_rope(i, bank(bk)[:, 0:256], [PB[bk]], 4, g640[:, 512:768], qkb[sl][:, 0:256], Rqkb[sl], sl)
                              kt = i if i < 8 else 12 + (i - 8)
                              for lay in range(2):
                                  CP("act", Vsb[:, kt, :, lay, lay * 64:(lay + 1) * 64], bank(bk)[:, 128:256].rearrange("p (h d) -> p h d", h=2), [PB[bk]], [RV])
                              if i >= 8:
                                  p_, t_ = (i - 8) // 2, (i - 8) % 2
                                  CP("act", vf[sl][:], bank(bk)[:, 128:256], [PB[bk]], [Rvf[sl]])
                                  fw.dma("sp", Dout["nk"][p_, l, t_ * 128:(t_ + 1) * 128, :], qn[sl][:, 0:128], reads=[Rqn[sl]], writes=[Rdbg])
                                  fw.dma("sp", Dout["nv"][p_, l, t_ * 128:(t_ + 1) * 128, :], vf[sl][:], reads=[Rvf[sl]], writes=[Rdbg])
                              CP("act", kb2[sl][:, 0:64], qkb[sl][:, 64:128], [Rqkb[sl]], [Rkb2[sl]])
                              CP("act", kb2[sl][:, 64:128], qkb[sl][:, 0:64], [Rqkb[sl]], [Rkb2[sl]])
                              OP("pe", lambda e: e.transpose(pst[:, 512:640], qkb[sl][:, 0:128], ident_b[:]), [Rqkb[sl], Rid], [PT])
                              OP("pe", lambda e: e.transpose(pst[:, 640:768], kb2[sl][:, :], ident_b[:]), [Rkb2[sl], Rid], [PT])
                              kc = i * 128 if i < 8 else 1536 + (i - 8) * 128
                              CP("act", KT[:, 0, kc:kc + 128], pst[:, 512:640], [PT], [RKT])
                              CP("act", KT[:, 1, kc:kc + 128], pst[:, 640:768], [PT], [RKT])
                          if stop == "passB": fw.dead = True
                          CH[0] = 'f' in os.environ.get('CHP', 'mabfFGMLUS')
                          for wb in range(2):
                              W, RW = load_w([(lambda t: t[:, 0:4096].rearrange("p (k n) -> p k n", k=8), win[:, :, 768 + wb * 512:1280 + wb * 512])])
                              Wv = W[:, 0:4096].rearrange("p (k n) -> p k n", k=8)
                              for m4 in range(4):
                                  m = wb * 4 + m4
                                  for tb, (c0, cn) in enumerate(BLKS):
                                      bk = (0, 1, 6)[(m * 3 + tb) % 3]
                                      for k in range(8):
                                          MM(bank(bk), Wv[:, k, m4 * 128:(m4 + 1) * 128], uT[:, k, c0:c0 + cn], k == 0, k == 7,
                                             RuT[c0 // 128:(c0 + cn) // 128] + [RW], [PB[bk]])
                                      CP("act", sfT[:, m, c0:c0 + cn], bank(bk), [PB[bk]], [RsfT[m]])
                          DBG("d_sfT", sfT[:].rearrange("p a b -> p (a b)"), RsfT)
                          fw.barrier()
                      if stop == "inproj": fw.dead = True
                      DBG("d_QT", QT[:].rearrange("p a b -> p (a b)"), RQ)
                      DBG("d_KT", KT[:].rearrange("p a b -> p (a b)"), [RKT])
                      CH[0] = 't' in os.environ.get('CHP', 'mabfFGMLUS')
                      with contextlib.ExitStack() as ph:
                          Eb = [sb(ph, "Eb%d" % j, [128, 512], BF16) for j in range(3)]; REb = [Res("Eb%d" % j) for j in range(3)]
                          rsb = [sb(ph, "rsb%d" % j, [128, 512]) for j in range(2)]; Rrsb = [Res("rsb0"), Res("rsb1")]
                          cnt = 0; cnt2 = 0
                          for si, (t0, Ls) in enumerate(SEQS):
                              if si == 0:
                                  qblks = [(0, 512), (512, 512)]; nkt = 12; kt0 = 0; kc0 = 0
                              else:
                                  qblks = [(t0, 256)]; nkt = 2; kt0 = 12 + 2 * (si - 1); kc0 = 1536 + 256 * (si - 1)
                              for h in range(8):
                                  j = h // 2; hh = h % 2; kvh = h // 4; var = 0 if kvh == hh else 1
                                  pv, sm = (slice(0, 64), slice(64, 128)) if hh == 0 else (slice(64, 128), slice(0, 64))
                                  for (q0, qn) in qblks:
                                      pvb = 3 + (cnt2 % 2); cnt2 += 1
                                      def S_mm(kt, sbk):
                                          MM(bank(sbk)[:, 0:qn], KT[hh * 64:(hh + 1) * 64, var, kc0 + kt * 128:kc0 + (kt + 1) * 128],
                                             QT[hh * 64:(hh + 1) * 64, j, q0:q0 + qn], True, True, [RKT, RQ[j]], [PB[sbk]], chain=False)
                                      slots = [(cnt + i_) % 3 for i_ in range(nkt)]; cnt += nkt
                                      S_mm(0, slots[0])
                                      for kt in range(nkt):
                                          sbk = slots[kt]; eb = sbk
                                          if kt + 1 < nkt:
                                              S_mm(kt + 1, slots[kt + 1])
                                          ACT(Eb[eb][:, 0:qn], bank(sbk)[:, 0:qn], AF.Exp, [PB[sbk]], [REb[eb]], scale=0.125)
                                          MM(bank(pvb)[:, 0:qn], Vsb[:, kt0 + kt, kvh, hh, :], Eb[eb][:, 0:qn], kt == 0, kt == nkt - 1,
                                             [RV, REb[eb]], [PB[pvb]], chain=False)
                                      rb = cnt2 % 2
                                      ACT(rsb[rb][sm, 0:qn], bank(pvb)[sm, 0:qn], AF.Ln, [PB[pvb]], [Rrsb[rb]])
                                      ACT(rsb[rb][sm, 0:qn], rsb[rb][sm, 0:qn], AF.Exp, [Rrsb[rb]], [Rrsb[rb]], scale=-1.0)
                                      TT("dve", attnT[pv, j, q0:q0 + qn], bank(pvb)[pv, 0:qn], rsb[rb][sm, 0:qn], ALU.mult,
                                         [PB[pvb], Rrsb[rb]], [RattnT[j]])
                          fw.barrier()
                  DBG("d_attnT", attnT[:].rearrange("p a b -> p (a b)"), RattnT)
                  CH[0] = 'F' in os.environ.get('CHP', 'mabfFGMLUS')
                  with contextlib.ExitStack() as L3:
                    fourT = sb(L3, "fourT", [128, 4, TOK], BF16); RfourT = [Res("fourT%d" % j) for j in range(4)]
                    with contextlib.ExitStack() as ph:
                        cs128 = sb(ph, "cs128", [128, 256], BF16); cl = sb(ph, "cl", [128, 8, 1024], BF16); sl_ = sb(ph, "sl", [128, 8, 1024], BF16)
                        clp = sb(ph, "clp", [128, 2, 256], BF16); slp = sb(ph, "slp", [128, 2, 256], BF16); Rft = Res("ftab")
                        Pfm = sb(ph, "Pfm", [128, NT, 1024], BF16); RPfm = [Res("Pfm%d" % i) for i in range(NT)]
                        fw.dma("pool", cs128[:], Din["cs128"], writes=[Rft])
                        for k in range(8):
                            fw.dma("pool", cl[:, k, :], Din["cl1024"][:, k, :], writes=[Rft])
                            fw.dma("pool", sl_[:, k, :], Din["sl1024"][:, k, :], writes=[Rft])
                        fw.dma("pool", clp[:], Din["cl256"], writes=[Rft])
                        fw.dma("pool", slp[:], Din["sl256"], writes=[Rft])
                        for i in range(NT):
                            b0 = 2 + 2 * (i % 2)
                            for g in range(4):
                                MM(bank(b0 + g // 2)[:, (g % 2) * 256:(g % 2 + 1) * 256], sfT[:, 4 + g, i * 128:(i + 1) * 128], cs128[:], True, True,
                                   [RsfT[4 + g], Rft], [PB[b0 + g // 2]])
                            CP("act", Pfm[:, i, :], bank(b0, 2), [PB[b0], PB[b0 + 1]], [RPfm[i]])
                        cnt = 0
                        for si, (t0, Ls) in enumerate(SEQS):
                            ntl = Ls // 128; tb0 = t0 // 128
                            tc_, ts_ = (cl, sl_) if si == 0 else (clp, slp)
                            for g in range(4):
                                for lb in range(0, Ls, 512):
                                    n = min(512, Ls - lb)
                                    bk = (0, 1, 6)[cnt % 3]; cnt += 1
                                    for k in range(ntl):
                                        MM(bank(bk)[:, 0:n], Pfm[:, tb0 + k, g * 256:g * 256 + 128], tc_[:, k, lb:lb + n], k == 0, False,
                                           [RPfm[tb0 + k], Rft], [PB[bk]])
                                        MM(bank(bk)[:, 0:n], Pfm[:, tb0 + k, g * 256 + 128:g * 256 + 256], ts_[:, k, lb:lb + n], False, k == ntl - 1,
                                           [RPfm[tb0 + k], Rft], [PB[bk]])
                                    CP("act", fourT[:, g, t0 + lb:t0 + lb + n], bank(bk)[:, 0:n], [PB[bk]], [RfourT[g]])
                        fw.barrier()
                    DBG("d_fourT", fourT[:].rearrange("p a b -> p (a b)"), RfourT)
                    if stop == "four": fw.dead = True
                    CH[0] = 'S' in os.environ.get('CHP', 'mabfFGMLUS')
                    ssmT = sb(L3, "ssmT", [128, 4, TOK], BF16); RssmT = [Res("ssmT%d" % j) for j in range(4)]
                    with contextlib.ExitStack() as S:
                        PowP = [sb(S, "PowP%d" % r, [128, 2, 16, 17]) for r in range(2)]
                        LH = [sb(S, "LH%d" % r, [128, 2, 16, 7]) for r in range(2)]
                        fP = [sb(S, "fP%d" % r, [128, 2, 16]) for r in range(2)]
                        Hb = sb(S, "Hb", [128, 2, 2, 16, NS], BF16)
                        h0 = sb(S, "h0", [128, 2, 2, 16]); stg = sb(S, "stg", [128, 16, 2])
                        RS = Res("ssm"); RHb = Res("Hb"); Rstg = Res("stg")
                        fw.dma("sp", h0[:], Din["h0"][:, l], writes=[RS])
                        def cmul(dr, di, ar, ai, br, bi, m1, m2):
                            TT("dve", m1, ar, br, ALU.mult, [RS], [RS]); TT("dve", m2, ai, bi, ALU.mult, [RS], [RS])
                            TT("dve", dr, m1, m2, ALU.subtract, [RS], [RS])
                            TT("dve", m1, ar, bi, ALU.mult, [RS], [RS]); TT("dve", m2, ai, br, ALU.mult, [RS], [RS])
                            TT("dve", di, m1, m2, ALU.add, [RS], [RS])
                        for d in range(2):
                            with contextlib.ExitStack() as SD:
                                lam = [sb(SD, "lam%d" % r, [128, 528]) for r in range(2)]
                                ff = [sb(SD, "ff%d" % r, [128, 528]) for r in range(2)]
                                A = [sb(SD, "A%d" % r, [128, 16, NS]) for r in range(2)]
                                with contextlib.ExitStack() as SP:
                                    are, aim, ldt, mag, cc, ss, t1, t2, t3 = [sb(SP, "sp%d" % j, [128, 528]) for j in range(9)]
                                    fw.dma("sp", are[:], Din["lamin"][l, d, 0], writes=[RS])
                                    fw.dma("sp", aim[:], Din["lamin"][l, d, 1], writes=[RS])
                                    fw.dma("sp", ldt[:], Din["lamin"][l, d, 2], writes=[RS])
                                    ACT(ldt[:], ldt[:], AF.Exp, [RS], [RS])
                                    TT("dve", t1[:], are[:], ldt[:], ALU.mult, [RS], [RS])
                                    ACT(mag[:], t1[:], AF.Exp, [RS], [RS])
                                    TT("dve", t2[:], aim[:], ldt[:], ALU.mult, [RS], [RS])
                                    ACT(ss[:], t2[:], AF.Sin, [RS], [RS], scale=1.0 / 16)
                                    ACT(t1[:], t2[:], AF.Sin, [RS], [RS], scale=1.0 / 32)
                                    TT("dve", t1[:], t1[:], t1[:], ALU.mult, [RS], [RS])
                                    TS("dve", cc[:], t1[:], -2.0, ALU.mult, [RS], [RS], s2=1.0, op1=ALU.add)
                                    for _ in range(4):
                                        TT("dve", t1[:], cc[:], cc[:], ALU.mult, [RS], [RS])
                                        TT("dve", t2[:], ss[:], ss[:], ALU.mult, [RS], [RS])
                                        TT("dve", t3[:], cc[:], ss[:], ALU.mult, [RS], [RS])
                                        TT("dve", cc[:], t1[:], t2[:], ALU.subtract, [RS], [RS])
                                        TS("dve", ss[:], t3[:], 2.0, ALU.mult, [RS], [RS])
                                    TT("dve", lam[0][:], mag[:], cc[:], ALU.mult, [RS], [RS])
                                    TT("dve", lam[1][:], mag[:], ss[:], ALU.mult, [RS], [RS])
                                    TT("dve", t1[:], are[:], are[:], ALU.mult, [RS], [RS])
                                    TT("dve", t2[:], aim[:], aim[:], ALU.mult, [RS], [RS])
                                    TT("dve", t1[:], t1[:], t2[:], ALU.add, [RS], [RS])
                                    OP("dve", lambda e: e.reciprocal(out=t1[:], in_=t1[:]), [RS], [RS])
                                    TS("dve", t2[:], lam[0][:], -1.0, ALU.add, [RS], [RS])
                                    TT("dve", t3[:], t2[:], are[:], ALU.mult, [RS], [RS])
                                    TT("dve", cc[:], lam[1][:], aim[:], ALU.mult, [RS], [RS])
                                    TT("dve", t3[:], t3[:], cc[:], ALU.add, [RS], [RS])
                                    TT("dve", ff[0][:], t3[:], t1[:], ALU.mult, [RS], [RS])
                                    TT("dve", t3[:], lam[1][:], are[:], ALU.mult, [RS], [RS])
                                    TT("dve", cc[:], t2[:], aim[:], ALU.mult, [RS], [RS])
                                    TT("dve", t3[:], t3[:], cc[:], ALU.subtract, [RS], [RS])
                                    TT("dve", ff[1][:], t3[:], t1[:], ALU.mult, [RS], [RS])
                                    pm1 = t1[:, 0:256].rearrange("p (g k) -> p g k", g=16); pm2 = t2[:, 0:256].rearrange("p (g k) -> p g k", g=16)
                                    OP("dve", lambda e: e.memset(PowP[0][:, d, :, 0:1], 1.0), [RS], [RS])
                                    OP("dve", lambda e: e.memset(PowP[1][:, d, :, 0:1], 0.0), [RS], [RS])
                                    for r in range(2):
                                        ACT(PowP[r][:, d, :, 1], lam[r][:, 0:16], AF.Identity, [RS], [RS])
                                        ACT(fP[r][:, d, :], ff[r][:, 0:16], AF.Identity, [RS], [RS])
                                    for w in (1, 2, 4, 8):
                                        br_ = PowP[0][:, d, :, w:w + 1].to_broadcast([128, 16, w]); bi_ = PowP[1][:, d, :, w:w + 1].to_broadcast([128, 16, w])
                                        cmul(PowP[0][:, d, :, w + 1:2 * w + 1], PowP[1][:, d, :, w + 1:2 * w + 1],
                                             PowP[0][:, d, :, 1:w + 1], PowP[1][:, d, :, 1:w + 1], br_, bi_, pm1[:, :, 0:w], pm2[:, :, 0:w])
                                    for r in range(2):
                                        ACT(LH[r][:, d, :, 0], PowP[r][:, d, :, 16], AF.Identity, [RS], [RS])
                                    for m in range(6):
                                        cmul(LH[0][:, d, :, m + 1], LH[1][:, d, :, m + 1], LH[0][:, d, :, m], LH[1][:, d, :, m],
                                             LH[0][:, d, :, m], LH[1][:, d, :, m], t1[:, 256:272], t2[:, 256:272])
                                for r in range(2):
                                    OP("dve", lambda e: e.memset(A[r][:].rearrange("p a b -> p (a b)"), 0.0), [RS], [RS])
                                with contextlib.ExitStack() as SW:
                                    W = [sb(SW, "W%d" % r, [128, 16, 128]) for r in range(2)]
                                    WT2 = [[sb(SW, "WT%d_%d" % (z, r), [128, 16, 128], BF16) for r in range(2)] for z in range(2)]
                                    wm = [sb(SW, "wm%d" % r, [128, 8, 128]) for r in range(3)]
                                    BT = [sb(SW, "BT%d" % r, [128, 128]) for r in range(2)]
                                    Lw = [sb(SW, "Lw%d" % r, [128, 128]) for r in range(2)]
                                    RWT2 = [Res("WTa"), Res("WTb")]
                                    def build_W(q):
                                        fs = slice(16 + q * 128, 16 + (q + 1) * 128)
                                        for r in range(2):
                                            fw.dma("sp", BT[r][:], Din["BbdT"][l, d, r, :, q, :], writes=[RS])
                                            ACT(Lw[r][:], lam[r][:, fs], AF.Identity, [RS], [RS])
                                        cmul(W[0][:, 0, :], W[1][:, 0, :], ff[0][:, fs], ff[1][:, fs], BT[0][:], BT[1][:], wm[0][:, 0, :], wm[1][:, 0, :])
                                        for w in (1, 2, 4, 8):
                                            cmul(W[0][:, w:2 * w, :], W[1][:, w:2 * w, :], bm(Lw[0][:], w), bm(Lw[1][:], w),
                                                 W[0][:, 0:w, :], W[1][:, 0:w, :], wm[0][:, 0:w, :], wm[1][:, 0:w, :])
                                            if w < 8:
                                                TT("dve", wm[0][:, 0, :], Lw[0][:], Lw[0][:], ALU.mult, [RS], [RS])
                                                TT("dve", wm[1][:, 0, :], Lw[1][:], Lw[1][:], ALU.mult, [RS], [RS])
                                                TT("dve", wm[2][:, 0, :], Lw[0][:], Lw[1][:], ALU.mult, [RS], [RS])
                                                TT("dve", Lw[0][:], wm[0][:, 0, :], wm[1][:, 0, :], ALU.subtract, [RS], [RS])
                                                TS("dve", Lw[1][:], wm[2][:, 0, :], 2.0, ALU.mult, [RS], [RS])
                                        for r in range(2):
                                            ACT(WT2[q % 2][r][:].rearrange("p a b -> p (a b)"), W[r][:].rearrange("p a b -> p (a b)"), AF.Identity, [RS], [RWT2[q % 2]])
                                    def run_S(q):
                                        WTq = WT2[q % 2]; RWTq = RWT2[q % 2]
                                        Sps = bank(0, 2).rearrange("p (b r c) -> p b r c", b=4, r=2)
                                        for b in range(4):
                                            sv = sfT[32 * b:32 * b + 32, q, :].rearrange("p (c j) -> p c j", j=16)
                                            for r in range(2):
                                                for k in range(16):
                                                    j = 15 - k if d == 0 else k
                                                    MM(Sps[:, b, r, 0:96], WTq[r][32 * b:32 * b + 32, k, :], sv[:, :, j], k == 0, k == 15,
                                                       [RWTq, RsfT[q]], [PB[0], PB[1]], tp=(32 * b, 0))
                                        for r in range(2):
                                            for s_i in range(3):
                                                n = SCH[s_i]; c0 = (0, 64, 80)[s_i]; lo = SOFF[s_i] + (1 if d == 0 else 0)
                                                ACT(A[r][:, 4 * q:4 * q + 4, lo:lo + n], Sps[:, :, r, c0:c0 + n], AF.Identity, [PB[0], PB[1]], [RS])
                                    build_W(0)
                                    for q in range(4):
                                        if q + 1 < 4:
                                            build_W(q + 1)
                                        run_S(q)
                                for r in range(2):
                                    ACT(A[r][:, :, 0 if d == 0 else 64], h0[:, d, r, :], AF.Identity, [RS], [RS])
                                with contextlib.ExitStack() as SC:
                                    sm = [sb(SC, "sm%d" % r, [128, 16, NS]) for r in range(3)]
                                    for m in range(7):
                                        dd = 1 << m
                                        groups = []
                                        if dd < 65:
                                            groups.append((lambda t_, lo, hi: t_[:, :, lo:hi], 65, None))
                                        if dd < 17:
                                            groups.append((lambda t_, lo, hi: t_[:, :, 65:99].rearrange("p g (s c) -> p g s c", s=2)[:, :, :, lo:hi], 17, 2))
                                        for view, n, ns in groups:
                                            cntn = n - dd
                                            (dlo, dhi, slo, shi) = (dd, n, 0, cntn) if d == 0 else (0, cntn, dd, n)
                                            if ns is None:
                                                Lr = bl(LH[0][:, d, :, m], cntn); Li = bl(LH[1][:, d, :, m], cntn)
                                                tv = lambda t_: t_[:, :, 0:cntn]
                                            else:
                                                Lr = LH[0][:, d, :, m].unsqueeze(2).unsqueeze(3).to_broadcast([128, 16, 2, cntn])
                                                Li = LH[1][:, d, :, m].unsqueeze(2).unsqueeze(3).to_broadcast([128, 16, 2, cntn])
                                                tv = lambda t_: t_[:, :, 0:2 * cntn].rearrange("p g (s c) -> p g s c", s=2)
                                            sr, si = view(A[0], slo, shi), view(A[1], slo, shi)
                                            dr, di = view(A[0], dlo, dhi), view(A[1], dlo, dhi)
                                            m1, m2, m3 = tv(sm[0]), tv(sm[1]), tv(sm[2])
                                            TT("dve", m1, sr, Lr, ALU.mult, [RS], [RS]); TT("dve", m2, si, Li, ALU.mult, [RS], [RS])
                                            TT("dve", m1, m1, m2, ALU.subtract, [RS], [RS])
                                            TT("dve", m2, si, Lr, ALU.mult, [RS], [RS]); TT("dve", m3, sr, Li, ALU.mult, [RS], [RS])
                                            TT("dve", m2, m2, m3, ALU.add, [RS], [RS])
                                            TT("dve", dr, dr, m1, ALU.add, [RS], [RS]); TT("dve", di, di, m2, ALU.add, [RS], [RS])
                                for r in range(2):
                                    ACT(Hb[:, d, r, :, :], A[r][:], AF.Identity, [RS], [RHb])
                                for s_i in (1, 2):
                                    idx = SOFF[s_i] + (16 if d == 0 else 0)
                                    for r in range(2):
                                        ACT(stg[:, :, r], A[r][:, :, idx], AF.Identity, [RS], [Rstg])
                                    fw.dma("sp", Dout["nst"][s_i - 1, l, d].rearrange("(pair g2) p r -> (g2 p) pair r", g2=2), stg[:], reads=[Rstg], writes=[Rdbg])
                                DBG("d_A%d" % d, A[0][:].rearrange("p a b -> p (a b)"), [RS])
                                fw.barrier()
                        with contextlib.ExitStack() as SQ:
                            CL = [[sb(SQ, "CL%d%d" % (d, r), [128, 4, 17, 32], BF16) for r in range(2)] for d in range(2)]
                            Kblk = [sb(SQ, "Kblk%d" % d, [128, 16, 128], BF16) for d in range(2)]
                            Bb = [[sb(SQ, "Bb%d%d" % (d, r), [128, 4, 32], BF16) for r in range(2)] for d in range(2)]
                            Dbl = sb(SQ, "Dbl", [128, 4, 128], BF16)
                            cm = [sb(SQ, "cm%d" % j, [128, 2, 17, 32]) for j in range(2)]
                            Cl = [sb(SQ, "Cl%d" % r, [128, 4, 32]) for r in range(2)]
                            Bl = [sb(SQ, "Bl%d" % r, [128, 4, 32]) for r in range(2)]
                            bt = [sb(SQ, "bt%d" % r, [128, 4, 32]) for r in range(2)]
                            gt = [sb(SQ, "gt%d" % r, [128, 512]) for r in range(2)]; Rgt = [Res("gt0"), Res("gt1")]
                            RCL = Res("CL"); RK = Res("Kblk")
                            ydb = sb(SQ, "ydb", [128, 512]); Rydb = Res("ydb")
                            fw.dma("pool", Dbl[:], Din["Dblk"][l], writes=[RCL])
                            for q in range(4):
                                ps4 = slice(4 * q, 4 * q + 4)
                                for d in range(2):
                                    for r in range(2):
                                        fw.dma("sp", Cl[r][:], Din["CbdP"][l, d, r, :, ps4, :], writes=[RS])
                                        fw.dma("sp", Bl[r][:], Din["BbdP"][l, d, r, :, ps4, :], writes=[RS])
                                    for hp in range(2):
                                        pp = slice(2 * hp, 2 * hp + 2); pq = slice(4 * q + 2 * hp, 4 * q + 2 * hp + 2)
                                        Pr = PowP[0][:, d, pq, :].unsqueeze(3).to_broadcast([128, 2, 17, 32])
                                        Pi = PowP[1][:, d, pq, :].unsqueeze(3).to_broadcast([128, 2, 17, 32])
                                        Cr = Cl[0][:, pp, :].unsqueeze(2).to_broadcast([128, 2, 17, 32])
                                        Ci = Cl[1][:, pp, :].unsqueeze(2).to_broadcast([128, 2, 17, 32])
                                        TT("dve", cm[0][:], Pr, Cr, ALU.mult, [RS], [RS]); TT("dve", cm[1][:], Pi, Ci, ALU.mult, [RS], [RS])
                                        TT("dve", CL[d][0][:, pp, :, :], cm[0][:], cm[1][:], ALU.subtract, [RS], [RCL])
                                        TT("dve", cm[0][:], Pr, Ci, ALU.mult, [RS], [RS]); TT("dve", cm[1][:], Pi, Cr, ALU.mult, [RS], [RS])
                                        TT("dve", cm[0][:], cm[0][:], cm[1][:], ALU.add, [RS], [RS])
                                        TS("dve", CL[d][1][:, pp, :, :], cm[0][:], -1.0, ALU.mult, [RS], [RCL])
                                    fr = bl(fP[0][:, d, ps4], 32); fi = bl(fP[1][:, d, ps4], 32)
                                    TT("dve", bt[0][:], fr, Bl[0][:], ALU.mult, [RS], [RS]); TT("dve", bt[1][:], fi, Bl[1][:], ALU.mult, [RS], [RS])
                                    TT("dve", Bb[d][0][:], bt[0][:], bt[1][:], ALU.subtract, [RS], [RCL])
                                    TT("dve", bt[0][:], fr, Bl[1][:], ALU.mult, [RS], [RS]); TT("dve", bt[1][:], fi, Bl[0][:], ALU.mult, [RS], [RS])
                                    TT("dve", Bb[d][1][:], bt[0][:], bt[1][:], ALU.add, [RS], [RCL])
                                    kb = 6
                                    for b in range(4):
                                        MM(bank(kb)[32 * b:32 * b + 32, :], Bb[d][0][:, b, :], CL[d][0][:, b, 0:16, :].rearrange("p t n -> p (t n)"), True, False,
                                           [RCL], [PB[kb]], tp=(0, 32 * b))
                                        MM(bank(kb)[32 * b:32 * b + 32, :], Bb[d][1][:, b, :], CL[d][1][:, b, 0:16, :].rearrange("p t n -> p (t n)"), False, True,
                                           [RCL], [PB[kb]], tp=(0, 32 * b))
                                    for cb in range(4):
                                        TS("dve", Kblk[d][:, :, 32 * cb:32 * cb + 32], bank(kb).rearrange("p (t n) -> p t n", n=32), maskP[:, cb:cb + 1], ALU.mult,
                                           [PB[kb], Rmask], [RK])
                                for tb, (c0, cn) in enumerate(BLKS):
                                    yb = 3 + tb
                                    Yb = bank(yb).rearrange("p (c j) -> p c j", j=16)
                                    sblk = sfT[:, q, c0:c0 + cn].rearrange("p (c j) -> p c j", j=16)
                                    MM(bank(yb), Dbl[:, q, :], sfT[:, q, c0:c0 + cn], True, False, [RCL, RsfT[q]], [PB[yb]])
                                    for d in range(2):
                                        for t in range(16):
                                            if d == 0:
                                                o_, r_ = Yb[:, :, t:16], sblk[:, :, 0:16 - t]
                                            else:
                                                o_, r_ = Yb[:, :, 0:16 - t], sblk[:, :, t:16]
                                            MM(o_, Kblk[d][:, t, :], r_, False, False, [RK, RsfT[q]], [PB[yb]])
                                    GC = os.environ.get('GC', '1') == '1'
                                    for d in range(2):
                                        for j in range(16):
                                            tt = j + 1 if d == 0 else 16 - j
                                            sh = 0 if d == 0 else 1
                                            for r in range(2):
                                                for b in range(4):
                                                    gfirst = (b == 0 and r == 0 and j == 0 and d == 0)
                                                    if tb < 2:
                                                        rhs = Hb[:, d, r, 4 * q + b, 32 * tb + sh:32 * tb + sh + 32]
                                                        o_ = Yb[32 * b:32 * b + 32, :, j]
                                                    else:
                                                        rhs = Hb[:, d, r, 4 * q + b, 65:99].rearrange("p (s c) -> p s c", s=2)[:, :, sh:sh + 16]
                                                        o_ = bank(yb).rearrange("p (s c j) -> p s c j", s=2, j=16)[32 * b:32 * b + 32, :, :, j]
                                                    last = (d == 1 and b == 3 and j == 15 and r == 1)
                                                    MM(o_, CL[d][r][:, b, tt, :], rhs, False, last, [RCL, RHb], [PB[yb]], tp=(0, 32 * b),
                                                       chain=(("gfirst" if gfirst else "gnext") if GC else False))
                                    if ("d_y%d_%d" % (q, tb)) in Ddbg:
                                        ACT(ydb[:], bank(yb), AF.Identity, [PB[yb]], [Rydb])
                                        DBG("d_y%d_%d" % (q, tb), ydb[:], [Rydb])
                                    g_ = tb % 2
                                    ACT(gt[g_][:], bank(yb), AF.Square, [PB[yb]], [Rgt[g_]])
                                    TS("dve", gt[g_][:], gt[g_][:], 0.044715, ALU.mult, [Rgt[g_]], [Rgt[g_]], s2=1.0, op1=ALU.add)
                                    TT("dve", gt[g_][:], gt[g_][:], bank(yb), ALU.mult, [Rgt[g_], PB[yb]], [Rgt[g_]])
                                    ACT(gt[g_][:], gt[g_][:], AF.Sigmoid, [Rgt[g_]], [Rgt[g_]], scale=1.5957691216)
                                    TT("dve", sfT[:, q, c0:c0 + cn], gt[g_][:], bank(yb), ALU.mult, [Rgt[g_], PB[yb]], [RsfT[q]])
                            fw.barrier()
                    CH[0] = 'G' in os.environ.get('CHP', 'mabfFGMLUS')
                    with contextlib.ExitStack() as ph:
                        alloc_ws(ph)
                        bgl = sb(ph, "bgl", [128, 4]); Rbgl = Res("bgl"); sg = [sb(ph, "sg%d" % j, [128, 512]) for j in range(2)]; Rsg = [Res("sg0"), Res("sg1")]
                        fw.dma("sp", bgl[:], Din["bglu"][:, l, :], writes=[Rbgl])
                        Wg, RWg = load_w([(lambda t: t[:, 0:2048].rearrange("p (k n) -> p k n", k=4), Din["w_glu"][l].rearrange("(k p) n -> p k n", p=128))])
                        Wgv = Wg[:, 0:2048].rearrange("p (k n) -> p k n", k=4)
                        cnt = 0
                        for m in range(4):
                            for tb, (c0, cn) in enumerate(BLKS):
                                bk = (0, 1, 2)[cnt % 3]; sj = cnt % 2; cnt += 1
                                for k in range(4):
                                    MM(bank(bk), Wgv[:, k, m * 128:(m + 1) * 128], sfT[:, k, c0:c0 + cn], k == 0, k == 3, [RWg, RsfT[k]], [PB[bk]])
                                ACT(sg[sj][:], bank(bk), AF.Sigmoid, [PB[bk], Rbgl], [Rsg[sj]], bias=bgl[:, m:m + 1])
                                TT("dve", ssmT[:, m, c0:c0 + cn], sg[sj][:], sfT[:, m, c0:c0 + cn], ALU.mult, [Rsg[sj], RsfT[m]], [RssmT[m]])
                        fw.barrier()
                    DBG("d_gy", sfT[:, 0:4, :].rearrange("p a b -> p (a b)"), RsfT[0:4])
                    DBG("d_ssmT", ssmT[:].rearrange("p a b -> p (a b)"), RssmT)
                    if stop == "ssm": fw.dead = True
                    CH[0] = 'M' in os.environ.get('CHP', 'mabfFGMLUS')
                    with contextlib.ExitStack() as M:
                        mergedT = sb(M, "mergedT", [128, 8, TOK], BF16); Rmg = [Res("mg%d" % j) for j in range(8)]
                        with contextlib.ExitStack() as ph:
                            alloc_ws(ph)
                            uT = sb(ph, "uT", [128, 8, TOK], BF16); RuT = [Res("uT%d" % i) for i in range(NT)]
                            gs = [sb(ph, "gs%d" % j, [128, 512]) for j in range(3)]; Rgs = [Res("gs%d" % j) for j in range(3)]
                            ac = [sb(ph, "ac%d" % j, [128, 512]) for j in range(2)]; Rac = [Res("ac0"), Res("ac1")]
                            build_uT(uT, RuT, 0)
                            win = Din["w_in"][l].rearrange("(k p) n -> p k n", p=128)
                            wbr = [Din[nm][l].rearrange("(k p) n -> p k n", p=128) for nm in ("w_br_attn", "w_br_ssm", "w_br_four")]
                            srcs = [(attnT, RattnT), (ssmT, RssmT), (fourT, RfourT)]
                            for c in range(8):
                                Wg, RWg = load_w([((lambda t, g=g: t[:, g * 1024:(g + 1) * 1024].rearrange("p (k n) -> p k n", k=8)),
                                                   win[:, :, 1792 + g * 1024 + c * 128:1792 + g * 1024 + (c + 1) * 128]) for g in range(3)])
                                Wb, RWb = load_w([((lambda t, x=x: t[:, x * 512:(x + 1) * 512].rearrange("p (k n) -> p k n", k=4)),
                                                   wbr[x][:, :, c * 128:(c + 1) * 128]) for x in range(3)])
                                for tb, (c0, cn) in enumerate(BLKS):
                                    for g in range(3):
                                        Wgv = Wg[:, g * 1024:(g + 1) * 1024].rearrange("p (k n) -> p k n", k=8)
                                        for k in range(8):
                                            MM(bank(g), Wgv[:, k, :], uT[:, k, c0:c0 + cn], k == 0, k == 7, RuT[c0 // 128:(c0 + cn) // 128] + [RWg], [PB[g]])
                                        ACT(gs[g][:], bank(g), AF.Sigmoid, [PB[g]], [Rgs[g]])
                                    for x in range(3):
                                        Wbv = Wb[:, x * 512:(x + 1) * 512].rearrange("p (k n) -> p k n", k=4)
                                        st_, Rst = srcs[x]
                                        for k in range(4):
                                            MM(bank(3 + x), Wbv[:, k, :], st_[:, k, c0:c0 + cn], k == 0, k == 3, [Rst[k], RWb], [PB[3 + x]])
                                    TT("dve", ac[0][:], gs[0][:], bank(3), ALU.mult, [Rgs[0], PB[3]], [Rac[0]])
                                    TT("dve", ac[1][:], gs[1][:], bank(4), ALU.mult, [Rgs[1], PB[4]], [Rac[1]])
                                    TT("dve", ac[0][:], ac[0][:], ac[1][:], ALU.add, Rac, [Rac[0]])
                                    TT("dve", ac[1][:], gs[2][:], bank(5), ALU.mult, [Rgs[2], PB[5]], [Rac[1]])
                                    TT("dve", mergedT[:, c, c0:c0 + cn], ac[0][:], ac[1][:], ALU.add, Rac, [Rmg[c]])
                            fw.barrier()
                        DBG("d_mergedT", mergedT[:].rearrange("p a b -> p (a b)"), Rmg)
                        if stop == "merge": fw.dead = True
                        layer_norm_residual(l, 0, 8, lambda i, k: mergedT[:, k, i * 128:(i + 1) * 128], lambda i, k: [Rmg[k]],
                                            Din["w_out"][l].rearrange("(k p) n -> p k n", p=128))
              if "d_x1" in Ddbg:
                  for i in range(NT):
                      fw.dma("sp", Ddbg["d_x1"][i * 128:(i + 1) * 128, :], X[:, i, :], reads=[RX[i]], writes=[Rdbg])
              if stop == "ln1": fw.dead = True
              CH[0] = 'U' in os.environ.get('CHP', 'mabfFGMLUS')
              with contextlib.ExitStack() as Fs:
                  actT = sb(Fs, "actT", [128, 22, TOK], BF16); Ract = [Res("act%d" % j) for j in range(22)]
                  with contextlib.ExitStack() as ph:
                      alloc_ws(ph)
                      uT = sb(ph, "uT", [128, 8, TOK], BF16); RuT = [Res("uT%d" % i) for i in range(NT)]
                      cvp = sb(ph, "cvp", [128, 44, 4]); Rcv = Res("cvp")
                      hc = [sb(ph, "hc%d" % j, [128, TOK]) for j in range(2)]; Rhc = [Res("hc0"), Res("hc1")]
                      fw.dma("sp", cvp[:], Din["convp"][:, l], writes=[Rcv])
                      build_uT(uT, RuT, 1)
                      wup = Din["w_up"][l].rearrange("(k p) n -> p k n", p=128)
                      for ch in range(44):
                          if ch % 4 == 0:
                              W, RW = load_w([(lambda t: t[:, 0:4096].rearrange("p (k n) -> p k n", k=8), wup[:, :, ch * 128:ch * 128 + 512])])
                              Wv = W[:, 0:4096].rearrange("p (k n) -> p k n", k=8)
                          cw = ch % 4; pb0 = 3 * (ch % 2); hj = ch % 2
                          prs = [PB[pb0], PB[pb0 + 1], PB[pb0 + 2]]
                          for tb, (c0, cn) in enumerate(BLKS):
                              for k in range(8):
                                  MM(bank(pb0 + tb), Wv[:, k, cw * 128:(cw + 1) * 128], uT[:, k, c0:c0 + cn], k == 0, k == 7,
                                     RuT[c0 // 128:(c0 + cn) // 128] + [RW], [prs[tb]])
                          hp = bank(pb0, 3); h_ = hc[hj]
                          ACT(h_[:], hp, AF.Identity, prs + [Rcv], [Rhc[hj]], scale=cvp[:, ch, 1:2], bias=cvp[:, ch, 3:4])
                          def stt(o_, i0, sc, i1):
                              OP("dve", lambda e: e.scalar_tensor_tensor(out=o_, in0=i0, scalar=sc, in1=i1, op0=ALU.mult, op1=ALU.add),
                                 prs + [Rcv, Rhc[hj]], [Rhc[hj]])
                          stt(h_[:, 1:1024], hp[:, 0:1023], cvp[:, ch, 0:1], h_[:, 1:1024])
                          stt(h_[:, 0:1023], hp[:, 1:1024], cvp[:, ch, 2:3], h_[:, 0:1023])
                          h3 = h_[:, 1024:1536].rearrange("p (s c) -> p s c", s=2); p3 = hp[:, 1024:1536].rearrange("p (s c) -> p s c", s=2)
                          stt(h3[:, :, 1:256], p3[:, :, 0:255], cvp[:, ch, 0:1], h3[:, :, 1:256])
                          stt(h3[:, :, 0:255], p3[:, :, 1:256], cvp[:, ch, 2:3], h3[:, :, 0:255])
                          if ch < 22:
                              ACT(actT[:, ch, :], h_[:], AF.Silu, [Rhc[hj]], [Ract[ch]])
                          else:
                              TT("dve", actT[:, ch - 22, :], actT[:, ch - 22, :], h_[:], ALU.mult, [Ract[ch - 22], Rhc[hj]], [Ract[ch - 22]])
                      fw.barrier()
                  DBG("d_actT", actT[:].rearrange("p a b -> p (a b)"), Ract)
                  if stop == "ffn": fw.dead = True
                  layer_norm_residual(l, 1, 22, lambda i, k: actT[:, k, i * 128:(i + 1) * 128], lambda i, k: [Ract[k]],
                                      Din["w_down"][l].rearrange("(k p) n -> p k n", p=128))
        layers()
        fw.dead = False
        for i in range(NT):
            fw.dma("sp", Dout["y"][i * 128:(i + 1) * 128, :], X[:, i, :], reads=[RX[i]], writes=[Rdbg])
        fw.finish()
    return nc


def kernel(**inputs):
    inp = {k: np.asarray(v) for k, v in inputs.items()}
    S = prep_shared(inp)
    nc = build()
    in_maps = []
    for cid in range(8):
        m = dict(S); m.update(prep_core(inp, cid))
        in_maps.append(m)
    res = run_bass_kernel_spmd(nc, in_maps, core_ids=list(range(8)))
    y_p = np.zeros((16, 256, 1024), np.float32); y_s = np.zeros((8, 1024, 1024), np.float32)
    nk = np.zeros((16, 2, 256, 2, 64), np.float32); nv = np.zeros((16, 2, 256, 2, 64), np.float32)
    nst = np.zeros((16, 2, 2, 32, 64, 2), np.float32)
    for cid in range(8):
        r = res.results[cid]
        y_s[cid] = r["y"][0:1024]
        y_p[2 * cid:2 * cid + 2] = r["y"][1024:1536].reshape(2, 256, 1024)
        nk[2 * cid:2 * cid + 2] = r["nk"].reshape(2, 2, 256, 2, 64)
        nv[2 * cid:2 * cid + 2] = r["nv"].reshape(2, 2, 256, 2, 64)
        nst[2 * cid:2 * cid + 2] = r["nst"]
    return (y_p, y_s, nk, nv, nst)
```

```python
import contextlib, os
SK = os.environ.get("SK", "")
import numpy as np
import concourse.bass as bass
import concourse.mybir as mybir
from concourse.bass_utils import run_bass_kernel_spmd

F32 = mybir.dt.float32; BF16 = mybir.dt.bfloat16
AF = mybir.ActivationFunctionType; ALU = mybir.AluOpType; AX = mybir.AxisListType

D_MODEL = 1024; DEPTH = 2; D_IN = 4864; D_FF = 2816
ALPHA = (2 * DEPTH) ** 0.25
EPS = 1e-6
NT = 12; TOK = 1536; T = 16
SEQS = [(0, 1024), (1024, 256), (1280, 256)]
BLKS = [(0, 512), (512, 512), (1024, 512)]
TILE_VEC = [0] * 8 + [1] * 4
NS = 99
SOFF = [0, 65, 82]
SCH = [64, 16, 16]


class StopBuild(Exception):
    pass


class Res:
    __slots__ = ("name", "w", "r")
    def __init__(self, name):
        self.name = name; self.w = None; self.r = {}


class Eng:
    def __init__(self, name, h, sem):
        self.name = name; self.h = h; self.sem = sem; self.count = 0; self.waited = {}


class FW:
    NDMA = 32
    def __init__(self, nc, es):
        self.nc = nc; self.sems = {}; self.eng = {}
        for name, h in (("pe", nc.tensor), ("act", nc.scalar), ("dve", nc.vector), ("pool", nc.gpsimd), ("sp", nc.sync)):
            s = es.enter_context(nc.semaphore("s_" + name))
            self.sems[name] = s; self.eng[name] = Eng(name, h, s)
        self.dma_sems = []
        for i in range(self.NDMA):
            s = es.enter_context(nc.semaphore("s_dma%d" % i))
            self.sems["dma%d" % i] = s; self.dma_sems.append(["dma%d" % i, 0])
        self.dma_i = 0; self.ninstr = 0; self.dead = False; self.prev_plain = False

    def _wait(self, e, key, val):
        if e.waited.get(key, 0) >= val: return
        e.h.wait_ge(self.sems[key], val); e.waited[key] = val

    def _deps(self, e, reads, writes, chain=False):
        deps = {}
        def add(kv):
            if kv is None: return
            k, v = kv
            if deps.get(k, 0) < v: deps[k] = v
        for r in reads: add(r.w)
        for w in writes:
            if w.w is not None and not (chain and w.w[0] == e.name):
                add(w.w)
            for k, v in w.r.items():
                if k == e.name: continue
                add((k, v))
        return deps

    def op(self, engname, fn, reads=(), writes=(), chain=False):
        if self.dead: return None
        e = self.eng[engname]
        if engname == "pe" and chain in ("gfirst", "gnext"):
            if chain == "gfirst" and e.count > 0:
                self._wait(e, "pe", e.count)
            chain = True; self.prev_plain = False
        elif engname == "pe":
            plain = chain
            chain = chain and self.prev_plain
            if plain and not chain and e.count > 0:
                self._wait(e, "pe", e.count)
            self.prev_plain = plain
        for k, v in self._deps(e, reads, writes, chain).items(): self._wait(e, k, v)
        ins = fn(e.h); e.count += 1; ins.then_inc(e.sem, 1)
        for r in reads: r.r[e.name] = e.count
        for w in writes: w.w = (e.name, e.count); w.r = {}
        self.ninstr += 1
        return ins

    def dma(self, qname, out, in_, reads=(), writes=()):
        if self.dead: return None
        e = self.eng[qname]
        deps = self._deps(e, reads, writes)
        slot = self.dma_sems[self.dma_i % self.NDMA]; self.dma_i += 1
        key = slot[0]
        if slot[1] > 0: deps[key] = max(deps.get(key, 0), slot[1])
        for k, v in deps.items(): self._wait(e, k, v)
        ins = e.h.dma_start(out=out, in_=in_)
        slot[1] += 16; ins.then_inc(self.sems[key], 16)
        for r in reads: r.r[key] = slot[1]
        for w in writes: w.w = (key, slot[1]); w.r = {}
        self.ninstr += 1
        return ins

    def barrier(self):
        if self.dead: return
        targets = {n: e.count for n, e in self.eng.items() if e.count > 0}
        for key, cnt in self.dma_sems:
            if cnt > 0: targets[key] = cnt
        for n, e in self.eng.items():
            for k, v in targets.items(): self._wait(e, k, v)

    def finish(self):
        self.dead = False
        self.barrier()


def bl(ap, n):
    return ap.unsqueeze(2).to_broadcast([ap.shape[0], ap.shape[1], n])


def bm(ap, n):
    return ap.unsqueeze(1).to_broadcast([ap.shape[0], n, ap.shape[1]])


def _rep(a, n=128):
    return np.ascontiguousarray(np.broadcast_to(a[None], (n,) + a.shape))


def prep_shared(inp):
    f = lambda a: np.ascontiguousarray(a, dtype=np.float32)
    L = DEPTH
    S = {}
    for k in ("w_ada", "w_in", "w_glu", "w_br_ssm", "w_br_four", "w_out", "w_up", "w_down"):
        S[k] = f(inp[k])
    S["w_br_attn"] = f(inp["w_br_attn"])
    S["b_adaP"] = f(inp["b_ada"].reshape(L, 48, 128).transpose(2, 0, 1))
    S["bgbc"] = f(np.stack([np.stack([_rep(inp["b_ada"][l, 2048:3072]), _rep(inp["b_ada"][l, 5120:6144])], 1) for l in range(L)]))
    S["lnbc"] = f(np.stack([np.stack([_rep(inp[k][l]) for k in ("ln1_g", "ln1_b", "ln2_g", "ln2_b")]) for l in range(L)]))
    cw = inp["conv_w"].reshape(L, 3, 44, 128).transpose(3, 0, 2, 1)
    cb = inp["conv_b"].reshape(L, 44, 128).transpose(2, 0, 1)[..., None]
    S["convp"] = f(np.concatenate([cw, cb], -1))
    S["bglu"] = f(inp["b_glu"].reshape(L, 4, 128).transpose(2, 0, 1))
    S["g640"] = f(np.stack([_rep(np.concatenate([np.tile(inp["q_norm_g"][l], 8), np.tile(inp["k_norm_g"][l], 4)])) for l in range(L)]))
    def lay(a):
        aP = a.reshape(L, 2, 16, 2, 64).transpose(0, 1, 3, 4, 2).reshape(L, 2, 128, 16)
        aF = a.reshape(L, 2, 4, 4, 2, 64).transpose(0, 1, 3, 2, 4, 5)
        aF = np.broadcast_to(aF[:, :, :, None], (L, 2, 4, 32, 4, 2, 64)).reshape(L, 2, 128, 512)
        return np.concatenate([aP, aF], -1)
    ldt = np.broadcast_to(inp["ssm_log_dt"][..., None], (L, 2, 32, 64))
    S["lamin"] = f(np.stack([lay(inp["ssm_a_re"]), lay(inp["ssm_a_im"]), lay(ldt)], 2))
    def bbdP(b):
        Bp = b.reshape(L, 2, 16, 2, 64, 16)
        out = np.zeros((L, 2, 2, 64, 16, 2, 16), np.float32)
        for g2 in range(2):
            out[:, :, g2, :, :, g2, :] = Bp[:, :, :, g2].transpose(0, 1, 3, 2, 4)
        return out.reshape(L, 2, 128, 16, 32)
    def bbdT(b):
        Bq = b.reshape(L, 2, 4, 4, 2, 64, 16)
        out = np.zeros((L, 2, 4, 2, 16, 4, 2, 64), np.float32)
        for g2 in range(2):
            out[:, :, :, g2, :, :, g2, :] = Bq[:, :, :, :, g2].transpose(0, 1, 3, 5, 2, 4)
        return out.reshape(L, 2, 128, 4, 128)
    def cbdP(c):
        Cp = c.reshape(L, 2, 16, 2, 16, 64)
        out = np.zeros((L, 2, 2, 64, 16, 2, 16), np.float32)
        for g2 in range(2):
            out[:, :, g2, :, :, g2, :] = Cp[:, :, :, g2].transpose(0, 1, 4, 2, 3)
        return out.reshape(L, 2, 128, 16, 32)
    S["BbdP"] = f(np.stack([bbdP(inp["ssm_b_re"]), bbdP(inp["ssm_b_im"])], 2))
    S["BbdT"] = f(np.stack([bbdT(inp["ssm_b_re"]), bbdT(inp["ssm_b_im"])], 2))
    S["CbdP"] = f(np.stack([cbdP(inp["ssm_c_re"]), cbdP(inp["ssm_c_im"])], 2))
    Dblk = np.zeros((L, 128, 4, 128), np.float32)
    for l in range(L):
        for q in range(4):
            Dblk[l, np.arange(128), q, np.arange(128)] = inp["ssm_d"][l, q * 128:(q + 1) * 128]
    S["Dblk"] = Dblk
    S["ident"] = np.eye(128, dtype=np.float32)
    mk = np.zeros((128, 4), np.float32); mk[np.arange(128), np.arange(128) // 32] = 1.0
    S["maskP"] = mk
    t = np.arange(1024)
    freqs = (10000.0 ** (-np.arange(16, dtype=np.float32) / 16)).astype(np.float32)
    ang = np.concatenate([(t // 64).astype(np.float32)[:, None] * freqs, (t % 64).astype(np.float32)[:, None] * freqs], -1)
    S["ropec"] = f(np.cos(ang).reshape(8, 128, 32).transpose(1, 0, 2))
    S["ropes"] = f(np.sin(ang).reshape(8, 128, 32).transpose(1, 0, 2))
    def dft(n):
        i = np.arange(n, dtype=np.int64)
        m = (i[:, None] * i[None, :]) % n
        a = 2.0 * np.pi * m.astype(np.float64) / n
        return np.cos(a) / np.sqrt(n), np.sin(a) / np.sqrt(n)
    c128, s128 = dft(128)
    S["cs128"] = f(np.concatenate([c128, s128], 1))
    for n in (1024, 256):
        c, s = dft(n)
        S["cl%d" % n] = f(c.reshape(n // 128, 128, n).transpose(1, 0, 2))
        S["sl%d" % n] = f((-s).reshape(n // 128, 128, n).transpose(1, 0, 2))
    return S


def prep_core(inp, cid):
    f = lambda a: np.ascontiguousarray(a, dtype=np.float32)
    m = {}
    m["xin"] = f(np.concatenate([inp["x_sample"][cid], inp["x_prompt"][2 * cid:2 * cid + 2].reshape(512, 1024)], 0))
    cv = np.stack([inp["c"][cid], inp["c_ctx"]], -1)
    m["cvec"] = f(cv.reshape(8, 128, 2).transpose(1, 0, 2))
    m["ckT"] = f(inp["cache_k"][cid].reshape(DEPTH, 512, 128).transpose(0, 2, 1))
    m["cv"] = f(inp["cache_v"][cid].reshape(DEPTH, 512, 128))
    st = inp["state_ssm"][cid].reshape(DEPTH, 2, 16, 2, 64, 2)
    m["h0"] = f(st.transpose(3, 4, 0, 1, 5, 2).reshape(128, DEPTH, 2, 2, 16))
    return m


IN_SHAPES = {
    "xin": (1536, 1024), "cvec": (128, 8, 2), "ckT": (2, 128, 512), "cv": (2, 512, 128), "h0": (128, 2, 2, 2, 16),
    "w_ada": (2, 1024, 6144), "w_in": (2, 1024, 4864), "w_glu": (2, 512, 512), "w_br_attn": (2, 512, 1024),
    "w_br_ssm": (2, 512, 1024), "w_br_four": (2, 512, 1024), "w_out": (2, 1024, 1024), "w_up": (2, 1024, 5632),
    "w_down": (2, 2816, 1024), "b_adaP": (128, 2, 48), "bgbc": (2, 128, 2, 1024), "lnbc": (2, 4, 128, 1024),
    "convp": (128, 2, 44, 4), "bglu": (128, 2, 4), "g640": (2, 128, 768), "lamin": (2, 2, 3, 128, 528),
    "BbdP": (2, 2, 2, 128, 16, 32), "BbdT": (2, 2, 2, 128, 4, 128), "CbdP": (2, 2, 2, 128, 16, 32),
    "Dblk": (2, 128, 4, 128), "ident": (128, 128), "maskP": (128, 4), "ropec": (128, 8, 32), "ropes": (128, 8, 32),
    "cs128": (128, 256), "cl1024": (128, 8, 1024), "sl1024": (128, 8, 1024), "cl256": (128, 2, 256), "sl256": (128, 2, 256),
}
OUT_SHAPES = {"y": (1536, 1024), "nk": (2, 2, 256, 128), "nv": (2, 2, 256, 128), "nst": (2, 2, 2, 32, 64, 2)}


def build(n_layers=DEPTH, stop=None, dbg_shapes=None):
    nc = bass.Bass("TRN2", target_bir_lowering=False)
    Din = {k: nc.dram_tensor(k, list(s), F32, kind="ExternalInput").ap() for k, s in IN_SHAPES.items()}
    Dout = {k: nc.dram_tensor(k, list(s), F32, kind="ExternalOutput").ap() for k, s in OUT_SHAPES.items()}
    Ddbg = {k: nc.dram_tensor(k, list(s), F32, kind="ExternalOutput").ap() for k, s in (dbg_shapes or {}).items()}
    es = contextlib.ExitStack()
    with es:
        fw = FW(nc, es)
        uid = [0]
        def sb(st, name, shape, dt=F32):
            uid[0] += 1
            return st.enter_context(nc.sbuf_tensor("%s_%d" % (name, uid[0]), list(shape), dt))
        OP = fw.op
        CH = [True]
        def MM(out, lhsT, rhs, start, stop_, r, w, tp=None, chain=None):
            if chain is None: chain = CH[0] and (tp is None)
            if tp is None:
                return fw.op("pe", lambda e: e.matmul(out, lhsT=lhsT, rhs=rhs, start=start, stop=stop_), r, w, chain=chain)
            return fw.op("pe", lambda e: e.matmul(out, lhsT=lhsT, rhs=rhs, start=start, stop=stop_, tile_position=tp), r, w, chain=chain)
        def TT(eng, out, in0, in1, op, r, w):
            return fw.op(eng, lambda e: e.tensor_tensor(out=out, in0=in0, in1=in1, op=op), r, w)
        def TS(eng, out, in0, s1, op0, r, w, s2=None, op1=None):
            if op1 is None:
                return fw.op(eng, lambda e: e.tensor_scalar(out=out, in0=in0, scalar1=s1, scalar2=None, op0=op0), r, w)
            return fw.op(eng, lambda e: e.tensor_scalar(out=out, in0=in0, scalar1=s1, scalar2=s2, op0=op0, op1=op1), r, w)
        def ACT(out, in_, func, r, w, scale=1.0, bias=0.0):
            return fw.op("act", lambda e: e.activation(out=out, in_=in_, func=func, bias=bias, scale=scale), r, w)
        def CP(eng, out, in_, r, w):
            if eng == "act":
                return fw.op("act", lambda e: e.copy(out=out, in_=in_), r, w)
            return fw.op(eng, lambda e: e.tensor_copy(out=out, in_=in_), r, w)
        Rdbg = Res("dbg")
        def DBG(name, ap, r):
            if name in Ddbg:
                fw.dma("pool", Ddbg[name], ap, reads=r, writes=[Rdbg])

        X = sb(es, "X", [128, NT, 1024]); RX = [Res("X%d" % i) for i in range(NT)]
        ident_f = sb(es, "ident_f", [128, 128]); ident_b = sb(es, "ident_b", [128, 128], BF16); Rid = Res("id")
        maskP = sb(es, "maskP", [128, 4]); Rmask = Res("mask")
        modP = sb(es, "modP", [128, 6, 8, 2]); opsc = sb(es, "opsc", [128, 2, 8, 2]); Rmod = Res("mod")
        csl = sb(es, "csl", [128, 8, 2]); sTb = sb(es, "sTb", [128, 8, 2], BF16); sRep = sb(es, "sRep", [128, 2, 8, 128], BF16)
        Rcs = Res("cs")
        stat = sb(es, "stat", [128, NT, 2, 6]); mv = sb(es, "mv", [128, NT, 2]); rs = sb(es, "rs", [128, NT, 2]); Rstat = Res("stat")
        ps = es.enter_context(nc.psum_tensor("ps", [128, 3584], F32)); PB = [Res("pb%d" % i) for i in range(7)]
        pst = es.enter_context(nc.psum_tensor("pst", [128, 1024], BF16)); PT = Res("pt")
        wslot = [None, None]; Rws = [None, None]
        wctr = [0]
        def alloc_ws(st):
            for i_ in range(2):
                wslot[i_] = sb(st, "wslot%d" % i_, [128, 4096], BF16); Rws[i_] = Res("ws%d" % i_)
        def load_w(views):
            s_ = wctr[0] % 2; wctr[0] += 1
            for dstf, src in views:
                fw.dma("pool", dstf(wslot[s_]), src, writes=[Rws[s_]])
            return wslot[s_], Rws[s_]
        def bank(i, n=1):
            return ps[:, i * 512:(i + n) * 512]

        fw.dma("sp", ident_f[:], Din["ident"], writes=[Rid])
        fw.dma("pool", ident_b[:], Din["ident"], writes=[Rid])
        fw.dma("sp", maskP[:], Din["maskP"], writes=[Rmask])
        for i in range(NT):
            fw.dma("sp", X[:, i, :], Din["xin"][i * 128:(i + 1) * 128, :], writes=[RX[i]])
        fw.dma("sp", csl[:], Din["cvec"], writes=[Rcs])
        ACT(csl[:], csl[:], AF.Silu, [Rcs], [Rcs])
        CP("dve", sTb[:], csl[:], [Rcs], [Rcs])
        for v in range(2):
            CP("dve", sRep[:, v, :, :], bl(csl[:, :, v], 128), [Rcs], [Rcs])

        def build_uT(uT, RuT, sub):
            sh_sec = 0 if sub == 0 else 3
            for i in range(NT):
                v = TILE_VEC[i]
                for kk in range(2):
                    bk = kk
                    for k4 in range(4):
                        k = kk * 4 + k4
                        OP("pe", lambda e: e.transpose(bank(bk)[:, k4 * 128:(k4 + 1) * 128], X[:, i, k * 128:(k + 1) * 128], ident_f[:]),
                           [RX[i], Rid], [PB[bk]])
                    for k4 in range(4):
                        k = kk * 4 + k4
                        ACT(uT[:, k, i * 128:(i + 1) * 128], bank(bk)[:, k4 * 128:(k4 + 1) * 128], AF.Identity,
                            [PB[bk], Rmod], [RuT[i]], scale=opsc[:, sub, k, v:v + 1], bias=modP[:, sh_sec, k, v:v + 1])

        def layer_norm_residual(l, which, kc, lhsT_fn, lres_fn, wsrc3):
            with contextlib.ExitStack() as ph:
                ch_save = CH[0]; CH[0] = 'L' in os.environ.get('CHP', 'mabfFGMLUS')
                alloc_ws(ph)
                lnb = sb(ph, "lnb", [128, 2, 1024]); Rln = Res("lnb")
                tmp = [sb(ph, "lntmp%d" % j, [128, 512]) for j in range(2)]; Rtmp = [Res("lntmp0"), Res("lntmp1")]
                Wd = sb(ph, "Wd", [128, kc, 512], BF16); RWd = Res("Wd")
                fw.dma("sp", lnb[:, 0, :], Din["lnbc"][l, 2 * which], writes=[Rln])
                fw.dma("sp", lnb[:, 1, :], Din["lnbc"][l, 2 * which + 1], writes=[Rln])
                gbc = sb(ph, "gbc", [128, 2, 1024]); Rgbc = Res("gbc")
                bg = sb(ph, "bg", [128, 1024]); Rbg = Res("bg")
                fw.dma("sp", bg[:], Din["bgbc"][l, :, which, :], writes=[Rbg])
                sec = 2 if which == 0 else 5
                wsrc = Din["w_ada"][l].rearrange("(k p) n -> p k n", p=128)
                for half in range(2):
                    Wg, RWg = load_w([(lambda t: t[:, 0:4096].rearrange("p (k n) -> p k n", k=8), wsrc[:, :, sec * 1024 + half * 512:sec * 1024 + (half + 1) * 512])])
                    Wgv = Wg[:, 0:4096].rearrange("p (k n) -> p k n", k=8)
                    for v in range(2):
                        for k in range(8):
                            MM(bank(v), sRep[:, v, k, :], Wgv[:, k, :], k == 0, k == 7, [RWg, Rcs], [PB[v]])
                        TT("dve", gbc[:, v, half * 512:(half + 1) * 512], bank(v), bg[:, half * 512:(half + 1) * 512], ALU.add, [PB[v], Rbg], [Rgbc])
                cnt = 0
                for nb in range(2):
                    cs = slice(nb * 512, (nb + 1) * 512)
                    for k0 in range(0, kc, 8):
                        k1 = min(kc, k0 + 8)
                        fw.dma("pool", Wd[:, k0:k1, :], wsrc3[:, k0:k1, cs], writes=[RWd])
                    for i in range(NT):
                        v = TILE_VEC[i]
                        bk = 2 + (cnt % 4); tj = cnt % 2; cnt += 1
                        for k in range(kc):
                            MM(bank(bk), lhsT_fn(i, k), Wd[:, k, :], k == 0, k == kc - 1, lres_fn(i, k) + [RWd], [PB[bk]])
                        TT("dve", tmp[tj][:], bank(bk), gbc[:, v, cs], ALU.mult, [PB[bk], Rgbc], [Rtmp[tj]])
                        OP("dve", lambda e: e.scalar_tensor_tensor(out=X[:, i, cs], in0=X[:, i, cs], scalar=float(ALPHA), in1=tmp[tj][:],
                                                                   op0=ALU.mult, op1=ALU.add), [RX[i], Rtmp[tj]], [RX[i]])
                for i in range(NT):
                    for hh in range(2):
                        OP("dve", lambda e: e.bn_stats(out=stat[:, i, hh, :], in_=X[:, i, hh * 512:(hh + 1) * 512]), [RX[i]], [Rstat])
                    OP("dve", lambda e: e.bn_aggr(out=mv[:, i, :], in_=stat[:, i, :, :].rearrange("p a b -> p (a b)")), [Rstat], [Rstat])
                TS("dve", rs[:, :, 0], mv[:, :, 1], EPS, ALU.add, [Rstat], [Rstat])
                ACT(rs[:, :, 0], rs[:, :, 0], AF.Sqrt, [Rstat], [Rstat])
                OP("dve", lambda e: e.reciprocal(out=rs[:, :, 0], in_=rs[:, :, 0]), [Rstat], [Rstat])
                OP("dve", lambda e: e.scalar_tensor_tensor(out=rs[:, :, 1], in0=mv[:, :, 0], scalar=-1.0, in1=rs[:, :, 0],
                                                           op0=ALU.mult, op1=ALU.mult), [Rstat], [Rstat])
                for i in range(NT):
                    ACT(X[:, i, :], X[:, i, :], AF.Identity, [RX[i], Rstat], [RX[i]], scale=rs[:, i, 0:1], bias=rs[:, i, 1:2])
                    TT("pool", X[:, i, :], X[:, i, :], lnb[:, 0, :], ALU.mult, [RX[i], Rln], [RX[i]])
                    TT("pool", X[:, i, :], X[:, i, :], lnb[:, 1, :], ALU.add, [RX[i], Rln], [RX[i]])
                fw.barrier()
                CH[0] = ch_save

        def layers():
          for l in range(n_layers):
              with contextlib.ExitStack() as ph:
                  wa = [sb(ph, "wa%d" % i, [128, 8, 512], BF16) for i in range(3)]; Rwa = [Res("wa%d" % i) for i in range(3)]
                  bP = sb(ph, "bP", [128, 48]); Rb = Res("bmod")
                  fw.dma("sp", bP[:], Din["b_adaP"][:, l, :], writes=[Rb])
                  wsrc = Din["w_ada"][l].rearrange("(k p) n -> p k n", p=128)
                  pm = bank(6)[:, 0:96]
                  CH[0] = 'm' in os.environ.get('CHP', 'mabfFGMLUS')
                  for cb in range(12):
                      s = cb % 3
                      sec, half = cb // 2, cb % 2
                      if sec in (2, 5):
                          continue
                      fw.dma("pool", wa[s][:], wsrc[:, :, cb * 512:(cb + 1) * 512], writes=[Rwa[s]])
                      for m in range(4):
                          col = (sec * 8 + half * 4 + m) * 2
                          for k in range(8):
                              MM(pm[:, col:col + 2], wa[s][:, k, m * 128:(m + 1) * 128], sTb[:, k, :], k == 0, k == 7,
                                 [Rwa[s], Rcs], [PB[6]])
                  for sec in (0, 1, 3, 4):
                      TT("dve", modP[:, sec, :, :], pm[:, sec * 16:(sec + 1) * 16].rearrange("p (c v) -> p c v", v=2),
                         bl(bP[:, sec * 8:(sec + 1) * 8], 2), ALU.add, [PB[6], Rb], [Rmod])
                  TS("dve", opsc[:, 0, :, :], modP[:, 1, :, :], 1.0, ALU.add, [Rmod], [Rmod])
                  TS("dve", opsc[:, 1, :, :], modP[:, 4, :, :], 1.0, ALU.add, [Rmod], [Rmod])
                  fw.barrier()
              if stop == "mod": fw.dead = True
              DBG("d_modP", modP[:].rearrange("p a b c -> p (a b c)"), [Rmod])

              with contextlib.ExitStack() as L1:
                  QT = sb(L1, "QT", [128, 4, TOK], BF16); RQ = [Res("QT%d" % j) for j in range(4)]; attnT = QT; RattnT = RQ
                  sfT = sb(L1, "sfT", [128, 8, TOK], BF16); RsfT = [Res("sfT%d" % j) for j in range(8)]
                  with contextlib.ExitStack() as L2:
                      KT = sb(L2, "KT", [128, 2, 2048], BF16); RKT = Res("KT")
                      Vsb = sb(L2, "Vsb", [128, 16, 2, 2, 128], BF16); RV = Res("V")
                      with contextlib.ExitStack() as ph:
                          alloc_ws(ph)
                          uT = sb(ph, "uT", [128, 8, TOK], BF16); RuT = [Res("uT%d" % i) for i in range(NT)]
                          qn = [sb(ph, "qn%d" % j, [128, 512]) for j in range(2)]; Rqn = [Res("qn0"), Res("qn1")]
                          vf = [sb(ph, "vf%d" % j, [128, 128]) for j in range(2)]; Rvf = [Res("vf0"), Res("vf1")]
                          qkb = [sb(ph, "qkb%d" % j, [128, 512], BF16) for j in range(2)]; Rqkb = [Res("qkb0"), Res("qkb1")]
                          sq = sb(ph, "sq", [128, 512]); Rsq = Res("sq")
                          kb2 = [sb(ph, "kb2%d" % j, [128, 128], BF16) for j in range(2)]; Rkb2 = [Res("kb20"), Res("kb21")]
                          ssq = sb(ph, "ssq", [128, 8]); Rssq = Res("ssq")
                          g640 = sb(ph, "g640", [128, 768]); rc = sb(ph, "rc", [128, 8, 32]); rsn = sb(ph, "rsn", [128, 8, 32]); Rcst = Res("cst")
                          rt = [sb(ph, "rt%d" % j, [128, 8, 32]) for j in range(2)]; Rrt = [Res("rt0"), Res("rt1")]
                          win = Din["w_in"][l].rearrange("(k p) n -> p k n", p=128)
                          fw.dma("sp", g640[:], Din["g640"][l], writes=[Rcst])
                          fw.dma("sp", rc[:], Din["ropec"], writes=[Rcst])
                          fw.dma("sp", rsn[:], Din["ropes"], writes=[Rcst])
                          fw.dma("pool", KT[:, 0, 1024:1536], Din["ckT"][l], writes=[RKT])
                          fw.dma("pool", KT[0:64, 1, 1024:1536], Din["ckT"][l, 64:128, :], writes=[RKT])
                          fw.dma("pool", KT[64:128, 1, 1024:1536], Din["ckT"][l, 0:64, :], writes=[RKT])
                          OP("dve", lambda e: e.memset(Vsb[:].rearrange("p a b c d -> p (a b c d)"), 1.0), [], [RV])
                          cvv = Din["cv"][l].rearrange("(t p) (h d) -> p t h d", p=128, h=2)
                          for lay in range(2):
                              for kvh_ in range(2):
                                  fw.dma("pool", Vsb[:, 8:12, kvh_, lay, lay * 64:(lay + 1) * 64], cvv[:, :, kvh_, :], writes=[RV])
                          build_uT(uT, RuT, 0)
                          if stop == "ut": fw.dead = True
                          DBG("d_uT", uT[:].rearrange("p a b -> p (a b)"), RuT)

                          def norm_rope(i, pa, pr, nh, gap, outb, Routb, slot):
                              w_ = nh * 64
                              ACT(sq[:, 0:w_], pa, AF.Square, pr, [Rsq])
                              OP("dve", lambda e: e.reduce_sum(out=ssq[:, 0:nh], in_=sq[:, 0:w_].rearrange("p (h d) -> p h d", d=64), axis=AX.X), [Rsq], [Rssq])
                              TS("dve", ssq[:, 0:nh], ssq[:, 0:nh], 1.0 / 64, ALU.mult, [Rssq], [Rssq], s2=EPS, op1=ALU.add)
                              ACT(ssq[:, 0:nh], ssq[:, 0:nh], AF.Sqrt, [Rssq], [Rssq])
                              OP("dve", lambda e: e.reciprocal(out=ssq[:, 0:nh], in_=ssq[:, 0:nh]), [Rssq], [Rssq])
                              qv = qn[slot][:, 0:w_]
                              TT("dve", qv.rearrange("p (h d) -> p h d", d=64), pa.rearrange("p (h d) -> p h d", d=64), bl(ssq[:, 0:nh], 64),
                                 ALU.mult, pr + [Rssq], [Rqn[slot]])
                              TT("dve", qv, qv, gap, ALU.mult, [Rqn[slot], Rcst], [Rqn[slot]])
                              if i >= 8:
                                  CP("act", outb, qv, [Rqn[slot]], [Routb])
                              else:
                                  x4 = qv.rearrange("p (h d two) -> p h d two", d=32, two=2)
                                  o4 = outb.rearrange("p (h d two) -> p h d two", d=32, two=2)
                                  x0, x1 = x4[:, :, :, 0], x4[:, :, :, 1]
                                  cc, ss_ = bm(rc[:, i, :], nh), bm(rsn[:, i, :], nh)
                                  r0, r1 = rt[0][:, 0:nh, :], rt[1][:, 0:nh, :]
                                  TT("dve", r0, x0, cc, ALU.mult, [Rqn[slot], Rcst], [Rrt[0]])
                                  TT("pool", r1, x1, ss_, ALU.mult, [Rqn[slot], Rcst], [Rrt[1]])
                                  TT("dve", o4[:, :, :, 0], r0, r1, ALU.subtract, Rrt, [Routb])
                                  TT("dve", r0, x0, ss_, ALU.mult, [Rqn[slot], Rcst], [Rrt[0]])
                                  TT("pool", r1, x1, cc, ALU.mult, [Rqn[slot], Rcst], [Rrt[1]])
                                  TT("dve", o4[:, :, :, 1], r0, r1, ALU.add, Rrt, [Routb])

                          CH[0] = 'a' in os.environ.get('CHP', 'mabfFGMLUS')
                          W, RW = load_w([(lambda t: t[:, 0:4096].rearrange("p (k n) -> p k n", k=8), win[:, :, 0:512])])
                          Wv = W[:, 0:4096].rearrange("p (k n) -> p k n", k=8)
                          for i in range(NT):
                              bk = 2 + (i % 2); sl = i % 2
                              for k in range(8):
                                  MM(bank(bk), uT[:, k, i * 128:(i + 1) * 128], Wv[:, k, :], k == 0, k == 7, [RuT[i], RW], [PB[bk]])
                              norm_rope(i, bank(bk), [PB[bk]], 8, g640[:, 0:512], qkb[sl][:, :], Rqkb[sl], sl)
                              for j in range(4):
                                  src = qkb[sl][:, j * 128:(j + 1) * 128]
                                  OP("pe", lambda e: e.transpose(pst[:, j * 128:(j + 1) * 128], src, ident_b[:]), [Rqkb[sl], Rid], [PT])
                              CP("act", QT[:, :, i * 128:(i + 1) * 128], pst[:, 0:512].rearrange("p (j c) -> p j c", j=4), [PT], RQ)
                          if stop == "passA": fw.dead = True
                          CH[0] = 'b' in os.environ.get('CHP', 'mabfFGMLUS')
                          W, RW = load_w([(lambda t: t[:, 0:2048].rearrange("p (k n) -> p k n", k=8), win[:, :, 512:768])])
                          Wv = W[:, 0:2048].rearrange("p (k n) -> p k n", k=8)
                          for i in range(NT):
                              bk = 2 + (i % 2); sl = i % 2
                              for k in range(8):
                                  MM(bank(bk)[:, 0:256], uT[:, k, i * 128:(i + 1) * 128], Wv[:, k, :], k == 0, k == 7, [RuT[i], RW], [PB[bk]])
                              norm_rope(i, bank(bk)[:, 0:256], [PB[bk]], 4, g640[:, 512:768], qkb[sl][:, 0:256], Rqkb[sl], sl)
                              kt = i if i < 8 else 12 + (i - 8)
                              for lay in range(2):
                                  CP("act", Vsb[:, kt, :, lay, lay * 64:(lay + 1) * 64], bank(bk)[:, 128:256].rearrange("p (h d) -> p h d", h=2), [PB[bk]], [RV])
                              if i >= 8:
                                  p_, t_ = (i - 8) // 2, (i - 8) % 2
                                  CP("act", vf[sl][:], bank(bk)[:, 128:256], [PB[bk]], [Rvf[sl]])
                                  fw.dma("sp", Dout["nk"][p_, l, t_ * 128:(t_ + 1) * 128, :], qn[sl][:, 0:128], reads=[Rqn[sl]], writes=[Rdbg])
                                  fw.dma("sp", Dout["nv"][p_, l, t_ * 128:(t_ + 1) * 128, :], vf[sl][:], reads=[Rvf[sl]], writes=[Rdbg])
                              CP("act", kb2[sl][:, 0:64], qkb[sl][:, 64:128], [Rqkb[sl]], [Rkb2[sl]])
                              CP("act", kb2[sl][:, 64:128], qkb[sl][:, 0:64], [Rqkb[sl]], [Rkb2[sl]])
                              OP("pe", lambda e: e.transpose(pst[:, 512:640], qkb[sl][:, 0:128], ident_b[:]), [Rqkb[sl], Rid], [PT])
                              OP("pe", lambda e: e.transpose(pst[:, 640:768], kb2[sl][:, :], ident_b[:]), [Rkb2[sl], Rid], [PT])
                              kc = i * 128 if i < 8 else 1536 + (i - 8) * 128
                              CP("act", KT[:, 0, kc:kc + 128], pst[:, 512:640], [PT], [RKT])
                              CP("act", KT[:, 1, kc:kc + 128], pst[:, 640:768], [PT], [RKT])
                          if stop == "passB": fw.dead = True
                          CH[0] = 'f' in os.environ.get('CHP', 'mabfFGMLUS')
                          for wb in range(2):
                              W, RW = load_w([(lambda t: t[:, 0:4096].rearrange("p (k n) -> p k n", k=8), win[:, :, 768 + wb * 512:1280 + wb * 512])])
                              Wv = W[:, 0:4096].rearrange("p (k n) -> p k n", k=8)
                              for m4 in range(4):
                                  m = wb * 4 + m4
                                  for tb, (c0, cn) in enumerate(BLKS):
                                      bk = (0, 1, 6)[(m * 3 + tb) % 3]
                                      for k in range(8):
                                          MM(bank(bk), Wv[:, k, m4 * 128:(m4 + 1) * 128], uT[:, k, c0:c0 + cn], k == 0, k == 7,
                                             RuT[c0 // 128:(c0 + cn) // 128] + [RW], [PB[bk]])
                                      CP("act", sfT[:, m, c0:c0 + cn], bank(bk), [PB[bk]], [RsfT[m]])
                          DBG("d_sfT", sfT[:].rearrange("p a b -> p (a b)"), RsfT)
                          fw.barrier()
                      if stop == "inproj": fw.dead = True
                      DBG("d_QT", QT[:].rearrange("p a b -> p (a b)"), RQ)
                      DBG("d_KT", KT[:].rearrange("p a b -> p (a b)"), [RKT])
                      CH[0] = 't' in os.environ.get('CHP', 'mabfFGMLUS')
                      with contextlib.ExitStack() as ph:
                          Eb = [sb(ph, "Eb%d" % j, [128, 512], BF16) for j in range(3)]; REb = [Res("Eb%d" % j) for j in range(3)]
                          rsb = [sb(ph, "rsb%d" % j, [128, 512]) for j in range(2)]; Rrsb = [Res("rsb0"), Res("rsb1")]
                          cnt = 0; cnt2 = 0
                          for si, (t0, Ls) in enumerate(SEQS):
                              if si == 0:
                                  qblks = [(0, 512), (512, 512)]; nkt = 12; kt0 = 0; kc0 = 0
                              else:
                                  qblks = [(t0, 256)]; nkt = 2; kt0 = 12 + 2 * (si - 1); kc0 = 1536 + 256 * (si - 1)
                              for h in range(8):
                                  j = h // 2; hh = h % 2; kvh = h // 4; var = 0 if kvh == hh else 1
                                  pv, sm = (slice(0, 64), slice(64, 128)) if hh == 0 else (slice(64, 128), slice(0, 64))
                                  for (q0, qn) in qblks:
                                      pvb = 3 + (cnt2 % 2); cnt2 += 1
                                      def S_mm(kt, sbk):
                                          MM(bank(sbk)[:, 0:qn], KT[hh * 64:(hh + 1) * 64, var, kc0 + kt * 128:kc0 + (kt + 1) * 128],
                                             QT[hh * 64:(hh + 1) * 64, j, q0:q0 + qn], True, True, [RKT, RQ[j]], [PB[sbk]], chain=False)
                                      slots = [(cnt + i_) % 3 for i_ in range(nkt)]; cnt += nkt
                                      S_mm(0, slots[0])
                                      for kt in range(nkt):
                                          sbk = slots[kt]; eb = sbk
                                          if kt + 1 < nkt:
                                              S_mm(kt + 1, slots[kt + 1])
                                          ACT(Eb[eb][:, 0:qn], bank(sbk)[:, 0:qn], AF.Exp, [PB[sbk]], [REb[eb]], scale=0.125)
                                          MM(bank(pvb)[:, 0:qn], Vsb[:, kt0 + kt, kvh, hh, :], Eb[eb][:, 0:qn], kt == 0, kt == nkt - 1,
                                             [RV, REb[eb]], [PB[pvb]], chain=False)
                                      rb = cnt2 % 2
                                      ACT(rsb[rb][sm, 0:qn], bank(pvb)[sm, 0:qn], AF.Ln, [PB[pvb]], [Rrsb[rb]])
                                      ACT(rsb[rb][sm, 0:qn], rsb[rb][sm, 0:qn], AF.Exp, [Rrsb[rb]], [Rrsb[rb]], scale=-1.0)
                                      TT("dve", attnT[pv, j, q0:q0 + qn], bank(pvb)[pv, 0:qn], rsb[rb][sm, 0:qn], ALU.mult,
                                         [PB[pvb], Rrsb[rb]], [RattnT[j]])
                          fw.barrier()
                  DBG("d_attnT", attnT[:].rearrange("p a b -> p (a b)"), RattnT)
                  CH[0] = 'F' in os.environ.get('CHP', 'mabfFGMLUS')
                  with contextlib.ExitStack() as L3:
                    fourT = sb(L3, "fourT", [128, 4, TOK], BF16); RfourT = [Res("fourT%d" % j) for j in range(4)]
                    with contextlib.ExitStack() as ph:
                        cs128 = sb(ph, "cs128", [128, 256], BF16); cl = sb(ph, "cl", [128, 8, 1024], BF16); sl_ = sb(ph, "sl", [128, 8, 1024], BF16)
                        clp = sb(ph, "clp", [128, 2, 256], BF16); slp = sb(ph, "slp", [128, 2, 256], BF16); Rft = Res("ftab")
                        Pfm = sb(ph, "Pfm", [128, NT, 1024], BF16); RPfm = [Res("Pfm%d" % i) for i in range(NT)]
                        fw.dma("pool", cs128[:], Din["cs128"], writes=[Rft])
                        for k in range(8):
                            fw.dma("pool", cl[:, k, :], Din["cl1024"][:, k, :], writes=[Rft])
                            fw.dma("pool", sl_[:, k, :], Din["sl1024"][:, k, :], writes=[Rft])
                        fw.dma("pool", clp[:], Din["cl256"], writes=[Rft])
                        fw.dma("pool", slp[:], Din["sl256"], writes=[Rft])
                        for i in range(NT):
                            b0 = 2 + 2 * (i % 2)
                            for g in range(4):
                                MM(bank(b0 + g // 2)[:, (g % 2) * 256:(g % 2 + 1) * 256], sfT[:, 4 + g, i * 128:(i + 1) * 128], cs128[:], True, True,
                                   [RsfT[4 + g], Rft], [PB[b0 + g // 2]])
                            CP("act", Pfm[:, i, :], bank(b0, 2), [PB[b0], PB[b0 + 1]], [RPfm[i]])
                        cnt = 0
                        for si, (t0, Ls) in enumerate(SEQS):
                            ntl = Ls // 128; tb0 = t0 // 128
                            tc_, ts_ = (cl, sl_) if si == 0 else (clp, slp)
                            for g in range(4):
                                for lb in range(0, Ls, 512):
                                    n = min(512, Ls - lb)
                                    bk = (0, 1, 6)[cnt % 3]; cnt += 1
                                    for k in range(ntl):
                                        MM(bank(bk)[:, 0:n], Pfm[:, tb0 + k, g * 256:g * 256 + 128], tc_[:, k, lb:lb + n], k == 0, False,
                                           [RPfm[tb0 + k], Rft], [PB[bk]])
                                        MM(bank(bk)[:, 0:n], Pfm[:, tb0 + k, g * 256 + 128:g * 256 + 256], ts_[:, k, lb:lb + n], False, k == ntl - 1,
                                           [RPfm[tb0 + k], Rft], [PB[bk]])
                                    CP("act", fourT[:, g, t0 + lb:t0 + lb + n], bank(bk)[:, 0:n], [PB[bk]], [RfourT[g]])
                        fw.barrier()
                    DBG("d_fourT", fourT[:].rearrange("p a b -> p (a b)"), RfourT)
                    if stop == "four": fw.dead = True
                    CH[0] = 'S' in os.environ.get('CHP', 'mabfFGMLUS')
                    ssmT = sb(L3, "ssmT", [128, 4, TOK], BF16); RssmT = [Res("ssmT%d" % j) for j in range(4)]
                    with contextlib.ExitStack() as S:
                        PowP = [sb(S, "PowP%d" % r, [128, 2, 16, 17]) for r in range(2)]
                        LH = [sb(S, "LH%d" % r, [128, 2, 16, 7]) for r in range(2)]
                        fP = [sb(S, "fP%d" % r, [128, 2, 16]) for r in range(2)]
                        Hb = sb(S, "Hb", [128, 2, 2, 16, NS], BF16)
                        h0 = sb(S, "h0", [128, 2, 2, 16]); stg = sb(S, "stg", [128, 16, 2])
                        RS = Res("ssm"); RHb = Res("Hb"); Rstg = Res("stg")
                        fw.dma("sp", h0[:], Din["h0"][:, l], writes=[RS])
                        def cmul(dr, di, ar, ai, br, bi, m1, m2):
                            TT("dve", m1, ar, br, ALU.mult, [RS], [RS]); TT("dve", m2, ai, bi, ALU.mult, [RS], [RS])
                            TT("dve", dr, m1, m2, ALU.subtract, [RS], [RS])
                            TT("dve", m1, ar, bi, ALU.mult, [RS], [RS]); TT("dve", m2, ai, br, ALU.mult, [RS], [RS])
                            TT("dve", di, m1, m2, ALU.add, [RS], [RS])
                        for d in range(2):
                            with contextlib.ExitStack() as SD:
                                lam = [sb(SD, "lam%d" % r, [128, 528]) for r in range(2)]
                                ff = [sb(SD, "ff%d" % r, [128, 528]) for r in range(2)]
                                A = [sb(SD, "A%d" % r, [128, 16, NS]) for r in range(2)]
                                with contextlib.ExitStack() as SP:
                                    are, aim, ldt, mag, cc, ss, t1, t2, t3 = [sb(SP, "sp%d" % j, [128, 528]) for j in range(9)]
                                    fw.dma("sp", are[:], Din["lamin"][l, d, 0], writes=[RS])
                                    fw.dma("sp", aim[:], Din["lamin"][l, d, 1], writes=[RS])
                                    fw.dma("sp", ldt[:], Din["lamin"][l, d, 2], writes=[RS])
                                    ACT(ldt[:], ldt[:], AF.Exp, [RS], [RS])
                                    TT("dve", t1[:], are[:], ldt[:], ALU.mult, [RS], [RS])
                                    ACT(mag[:], t1[:], AF.Exp, [RS], [RS])
                                    TT("dve", t2[:], aim[:], ldt[:], ALU.mult, [RS], [RS])
                                    ACT(ss[:], t2[:], AF.Sin, [RS], [RS], scale=1.0 / 16)
                                    ACT(t1[:], t2[:], AF.Sin, [RS], [RS], scale=1.0 / 32)
                                    TT("dve", t1[:], t1[:], t1[:], ALU.mult, [RS], [RS])
                                    TS("dve", cc[:], t1[:], -2.0, ALU.mult, [RS], [RS], s2=1.0, op1=ALU.add)
                                    for _ in range(4):
                                        TT("dve", t1[:], cc[:], cc[:], ALU.mult, [RS], [RS])
                                        TT("dve", t2[:], ss[:], ss[:], ALU.mult, [RS], [RS])
                                        TT("dve", t3[:], cc[:], ss[:], ALU.mult, [RS], [RS])
                                        TT("dve", cc[:], t1[:], t2[:], ALU.subtract, [RS], [RS])
                                        TS("dve", ss[:], t3[:], 2.0, ALU.mult, [RS], [RS])
                                    TT("dve", lam[0][:], mag[:], cc[:], ALU.mult, [RS], [RS])
                                    TT("dve", lam[1][:], mag[:], ss[:], ALU.mult, [RS], [RS])
                                    TT("dve", t1[:], are[:], are[:], ALU.mult, [RS], [RS])
                                    TT("dve", t2[:], aim[:], aim[:], ALU.mult, [RS], [RS])
                                    TT("dve", t1[:], t1[:], t2[:], ALU.add, [RS], [RS])
                                    OP("dve", lambda e: e.reciprocal(out=t1[:], in_=t1[:]), [RS], [RS])
                                    TS("dve", t2[:], lam[0][:], -1.0, ALU.add, [RS], [RS])
                                    TT("dve", t3[:], t2[:], are[:], ALU.mult, [RS], [RS])
                                    TT("dve", cc[:], lam[1][:], aim[:], ALU.mult, [RS], [RS])
                                    TT("dve", t3[:], t3[:], cc[:], ALU.add, [RS], [RS])
                                    TT("dve", ff[0][:], t3[:], t1[:], ALU.mult, [RS], [RS])
                                    TT("dve", t3[:], lam[1][:], are[:], ALU.mult, [RS], [RS])
                                    TT("dve", cc[:], t2[:], aim[:], ALU.mult, [RS], [RS])
                                    TT("dve", t3[:], t3[:], cc[:], ALU.subtract, [RS], [RS])
                                    TT("dve", ff[1][:], t3[:], t1[:], ALU.mult, [RS], [RS])
                                    pm1 = t1[:, 0:256].rearrange("p (g k) -> p g k", g=16); pm2 = t2[:, 0:256].rearrange("p (g k) -> p g k", g=16)
                                    OP("dve", lambda e: e.memset(PowP[0][:, d, :, 0:1], 1.0), [RS], [RS])
                                    OP("dve", lambda e: e.memset(PowP[1][:, d, :, 0:1], 0.0), [RS], [RS])
                                    for r in range(2):
                                        ACT(PowP[r][:, d, :, 1], lam[r][:, 0:16], AF.Identity, [RS], [RS])
                                        ACT(fP[r][:, d, :], ff[r][:, 0:16], AF.Identity, [RS], [RS])
                                    for w in (1, 2, 4, 8):
                                        br_ = PowP[0][:, d, :, w:w + 1].to_broadcast([128, 16, w]); bi_ = PowP[1][:, d, :, w:w + 1].to_broadcast([128, 16, w])
                                        cmul(PowP[0][:, d, :, w + 1:2 * w + 1], PowP[1][:, d, :, w + 1:2 * w + 1],
                                             PowP[0][:, d, :, 1:w + 1], PowP[1][:, d, :, 1:w + 1], br_, bi_, pm1[:, :, 0:w], pm2[:, :, 0:w])
                                    for r in range(2):
                                        ACT(LH[r][:, d, :, 0], PowP[r][:, d, :, 16], AF.Identity, [RS], [RS])
                                    for m in range(6):
                                        cmul(LH[0][:, d, :, m + 1], LH[1][:, d, :, m + 1], LH[0][:, d, :, m], LH[1][:, d, :, m],
                                             LH[0][:, d, :, m], LH[1][:, d, :, m], t1[:, 256:272], t2[:, 256:272])
                                for r in range(2):
                                    OP("dve", lambda e: e.memset(A[r][:].rearrange("p a b -> p (a b)"), 0.0), [RS], [RS])
                                with contextlib.ExitStack() as SW:
                                    W = [sb(SW, "W%d" % r, [128, 16, 128]) for r in range(2)]
                                    WT2 = [[sb(SW, "WT%d_%d" % (z, r), [128, 16, 128], BF16) for r in range(2)] for z in range(2)]
                                    wm = [sb(SW, "wm%d" % r, [128, 8, 128]) for r in range(3)]
                                    BT = [sb(SW, "BT%d" % r, [128, 128]) for r in range(2)]
                                    Lw = [sb(SW, "Lw%d" % r, [128, 128]) for r in range(2)]
                                    RWT2 = [Res("WTa"), Res("WTb")]
                                    def build_W(q):
                                        fs = slice(16 + q * 128, 16 + (q + 1) * 128)
                                        for r in range(2):
                                            fw.dma("sp", BT[r][:], Din["BbdT"][l, d, r, :, q, :], writes=[RS])
                                            ACT(Lw[r][:], lam[r][:, fs], AF.Identity, [RS], [RS])
                                        cmul(W[0][:, 0, :], W[1][:, 0, :], ff[0][:, fs], ff[1][:, fs], BT[0][:], BT[1][:], wm[0][:, 0, :], wm[1][:, 0, :])
                                        for w in (1, 2, 4, 8):
                                            cmul(W[0][:, w:2 * w, :], W[1][:, w:2 * w, :], bm(Lw[0][:], w), bm(Lw[1][:], w),
                                                 W[0][:, 0:w, :], W[1][:, 0:w, :], wm[0][:, 0:w, :], wm[1][:, 0:w, :])
                                            if w < 8:
                                                TT("dve", wm[0][:, 0, :], Lw[0][:], Lw[0][:], ALU.mult, [RS], [RS])
                                                TT("dve", wm[1][:, 0, :], Lw[1][:], Lw[1][:], ALU.mult, [RS], [RS])
                                                TT("dve", wm[2][:, 0, :], Lw[0][:], Lw[1][:], ALU.mult, [RS], [RS])
                                                TT("dve", Lw[0][:], wm[0][:, 0, :], wm[1][:, 0, :], ALU.subtract, [RS], [RS])
                                                TS("dve", Lw[1][:], wm[2][:, 0, :], 2.0, ALU.mult, [RS], [RS])
                                        for r in range(2):
                                            ACT(WT2[q % 2][r][:].rearrange("p a b -> p (a b)"), W[r][:].rearrange("p a b -> p (a b)"), AF.Identity, [RS], [RWT2[q % 2]])
                                    def run_S(q):
                                        WTq = WT2[q % 2]; RWTq = RWT2[q % 2]
                                        Sps = bank(0, 2).rearrange("p (b r c) -> p b r c", b=4, r=2)
                                        for b in range(4):
                                            sv = sfT[32 * b:32 * b + 32, q, :].rearrange("p (c j) -> p c j", j=16)
                                            for r in range(2):
                                                for k in range(16):
                                                    j = 15 - k if d == 0 else k
                                                    MM(Sps[:, b, r, 0:96], WTq[r][32 * b:32 * b + 32, k, :], sv[:, :, j], k == 0, k == 15,
                                                       [RWTq, RsfT[q]], [PB[0], PB[1]], tp=(32 * b, 0))
                                        for r in range(2):
                                            for s_i in range(3):
                                                n = SCH[s_i]; c0 = (0, 64, 80)[s_i]; lo = SOFF[s_i] + (1 if d == 0 else 0)
                                                ACT(A[r][:, 4 * q:4 * q + 4, lo:lo + n], Sps[:, :, r, c0:c0 + n], AF.Identity, [PB[0], PB[1]], [RS])
                                    build_W(0)
                                    for q in range(4):
                                        if q + 1 < 4:
                                            build_W(q + 1)
                                        run_S(q)
                                for r in range(2):
                                    ACT(A[r][:, :, 0 if d == 0 else 64], h0[:, d, r, :], AF.Identity, [RS], [RS])
                                with contextlib.ExitStack() as SC:
                                    sm = [sb(SC, "sm%d" % r, [128, 16, NS]) for r in range(3)]
                                    for m in range(7):
                                        dd = 1 << m
                                        groups = []
                                        if dd < 65:
                                            groups.append((lambda t_, lo, hi: t_[:, :, lo:hi], 65, None))
                                        if dd < 17:
                                            groups.append((lambda t_, lo, hi: t_[:, :, 65:99].rearrange("p g (s c) -> p g s c", s=2)[:, :, :, lo:hi], 17, 2))
                                        for view, n, ns in groups:
                                            cntn = n - dd
                                            (dlo, dhi, slo, shi) = (dd, n, 0, cntn) if d == 0 else (0, cntn, dd, n)
                                            if ns is None:
                                                Lr = bl(LH[0][:, d, :, m], cntn); Li = bl(LH[1][:, d, :, m], cntn)
                                                tv = lambda t_: t_[:, :, 0:cntn]
                                            else:
                                                Lr = LH[0][:, d, :, m].unsqueeze(2).unsqueeze(3).to_broadcast([128, 16, 2, cntn])
                                                Li = LH[1][:, d, :, m].unsqueeze(2).unsqueeze(3).to_broadcast([128, 16, 2, cntn])
                                                tv = lambda t_: t_[:, :, 0:2 * cntn].rearrange("p g (s c) -> p g s c", s=2)
                                            sr, si = view(A[0], slo, shi), view(A[1], slo, shi)
                                            dr, di = view(A[0], dlo, dhi), view(A[1], dlo, dhi)
                                            m1, m2, m3 = tv(sm[0]), tv(sm[1]), tv(sm[2])
                                            TT("dve", m1, sr, Lr, ALU.mult, [RS], [RS]); TT("dve", m2, si, Li, ALU.mult, [RS], [RS])
                                            TT("dve", m1, m1, m2, ALU.subtract, [RS], [RS])
                                            TT("dve", m2, si, Lr, ALU.mult, [RS], [RS]); TT("dve", m3, sr, Li, ALU.mult, [RS], [RS])
                                            TT("dve", m2, m2, m3, ALU.add, [RS], [RS])
                                            TT("dve", dr, dr, m1, ALU.add, [RS], [RS]); TT("dve", di, di, m2, ALU.add, [RS], [RS])
                                for r in range(2):
                                    ACT(Hb[:, d, r, :, :], A[r][:], AF.Identity, [RS], [RHb])
                                for s_i in (1, 2):
                                    idx = SOFF[s_i] + (16 if d == 0 else 0)
                                    for r in range(2):
                                        ACT(stg[:, :, r], A[r][:, :, idx], AF.Identity, [RS], [Rstg])
                                    fw.dma("sp", Dout["nst"][s_i - 1, l, d].rearrange("(pair g2) p r -> (g2 p) pair r", g2=2), stg[:], reads=[Rstg], writes=[Rdbg])
                                DBG("d_A%d" % d, A[0][:].rearrange("p a b -> p (a b)"), [RS])
                                fw.barrier()
                        with contextlib.ExitStack() as SQ:
                            CL2 = [[[sb(SQ, "CL%d%d%d" % (z, d, r), [128, 4, 17, 32], BF16) for r in range(2)] for d in range(2)] for z in range(2)]
                            Kblk = [sb(SQ, "Kblk%d" % d, [128, 16, 128], BF16) for d in range(2)]
                            Bb2 = [[[sb(SQ, "Bb%d%d%d" % (z, d, r), [128, 4, 32], BF16) for r in range(2)] for d in range(2)] for z in range(2)]
                            Dbl = sb(SQ, "Dbl", [128, 4, 128], BF16)
                            cm = [sb(SQ, "cm%d" % j, [128, 2, 17, 32]) for j in range(2)]
                            Cl = [sb(SQ, "Cl%d" % r, [128, 4, 32]) for r in range(2)]
                            Bl = [sb(SQ, "Bl%d" % r, [128, 4, 32]) for r in range(2)]
                            bt = [sb(SQ, "bt%d" % r, [128, 4, 32]) for r in range(2)]
                            gt = [sb(SQ, "gt%d" % r, [128, 512]) for r in range(2)]; Rgt = [Res("gt0"), Res("gt1")]
                            RCL2 = [Res("CLa"), Res("CLb")]; RK = Res("Kblk"); RDbl = Res("Dbl")
                            ydb = sb(SQ, "ydb", [128, 512]); Rydb = Res("ydb")
                            fw.dma("pool", Dbl[:], Din["Dblk"][l], writes=[RDbl])

                            def build_CL(q):
                                z = q % 2; CL = CL2[z]; Bb = Bb2[z]; RCL = RCL2[z]
                                ps4 = slice(4 * q, 4 * q + 4)
                                for d in range(2):
                                    for r in range(2):
                                        fw.dma("sp", Cl[r][:], Din["CbdP"][l, d, r, :, ps4, :], writes=[RS])
                                        fw.dma("sp", Bl[r][:], Din["BbdP"][l, d, r, :, ps4, :], writes=[RS])
                                    for hp in range(2):
                                        pp = slice(2 * hp, 2 * hp + 2); pq = slice(4 * q + 2 * hp, 4 * q + 2 * hp + 2)
                                        Pr = PowP[0][:, d, pq, :].unsqueeze(3).to_broadcast([128, 2, 17, 32])
                                        Pi = PowP[1][:, d, pq, :].unsqueeze(3).to_broadcast([128, 2, 17, 32])
                                        Cr = Cl[0][:, pp, :].unsqueeze(2).to_broadcast([128, 2, 17, 32])
                                        Ci = Cl[1][:, pp, :].unsqueeze(2).to_broadcast([128, 2, 17, 32])
                                        TT("dve", cm[0][:], Pr, Cr, ALU.mult, [RS], [RS]); TT("dve", cm[1][:], Pi, Ci, ALU.mult, [RS], [RS])
                                        TT("dve", CL[d][0][:, pp, :, :], cm[0][:], cm[1][:], ALU.subtract, [RS], [RCL])
                                        TT("dve", cm[0][:], Pr, Ci, ALU.mult, [RS], [RS]); TT("dve", cm[1][:], Pi, Cr, ALU.mult, [RS], [RS])
                                        TT("dve", cm[0][:], cm[0][:], cm[1][:], ALU.add, [RS], [RS])
                                        TS("dve", CL[d][1][:, pp, :, :], cm[0][:], -1.0, ALU.mult, [RS], [RCL])
                                    fr = bl(fP[0][:, d, ps4], 32); fi = bl(fP[1][:, d, ps4], 32)
                                    TT("dve", bt[0][:], fr, Bl[0][:], ALU.mult, [RS], [RS]); TT("dve", bt[1][:], fi, Bl[1][:], ALU.mult, [RS], [RS])
                                    TT("dve", Bb[d][0][:], bt[0][:], bt[1][:], ALU.subtract, [RS], [RCL])
                                    TT("dve", bt[0][:], fr, Bl[1][:], ALU.mult, [RS], [RS]); TT("dve", bt[1][:], fi, Bl[0][:], ALU.mult, [RS], [RS])
                                    TT("dve", Bb[d][1][:], bt[0][:], bt[1][:], ALU.add, [RS], [RCL])

                            def build_K(q):
                                z = q % 2; CL = CL2[z]; Bb = Bb2[z]; RCL = RCL2[z]
                                for d in range(2):
                                    kb = d
                                    for b in range(4):
                                        MM(bank(kb)[32 * b:32 * b + 32, :], Bb[d][0][:, b, :], CL[d][0][:, b, 0:16, :].rearrange("p t n -> p (t n)"), True, False,
                                           [RCL], [PB[kb]], tp=(0, 32 * b))
                                        MM(bank(kb)[32 * b:32 * b + 32, :], Bb[d][1][:, b, :], CL[d][1][:, b, 0:16, :].rearrange("p t n -> p (t n)"), False, True,
                                           [RCL], [PB[kb]], tp=(0, 32 * b))
                                    for cb in range(4):
                                        TS("dve", Kblk[d][:, :, 32 * cb:32 * cb + 32], bank(kb).rearrange("p (t n) -> p t n", n=32), maskP[:, cb:cb + 1], ALU.mult,
                                           [PB[kb], Rmask], [RK])

                            def run_y(q):
                                z = q % 2; CL = CL2[z]; RCL = RCL2[z]
                                for tb, (c0, cn) in enumerate(BLKS):
                                    yb = 3 + tb
                                    Yb = bank(yb).rearrange("p (c j) -> p c j", j=16)
                                    sblk = sfT[:, q, c0:c0 + cn].rearrange("p (c j) -> p c j", j=16)
                                    MM(bank(yb), Dbl[:, q, :], sfT[:, q, c0:c0 + cn], True, False, [RDbl, RsfT[q]], [PB[yb]])
                                    for d in range(2):
                                        for t in range(16):
                                            if d == 0:
                                                o_, r_ = Yb[:, :, t:16], sblk[:, :, 0:16 - t]
                                            else:
                                                o_, r_ = Yb[:, :, 0:16 - t], sblk[:, :, t:16]
                                            MM(o_, Kblk[d][:, t, :], r_, False, False, [RK, RsfT[q]], [PB[yb]])
                                    GC = os.environ.get('GC', '1') == '1'
                                    for d in range(2):
                                        for j in range(16):
                                            tt = j + 1 if d == 0 else 16 - j
                                            sh = 0 if d == 0 else 1
                                            for r in range(2):
                                                for b in range(4):
                                                    gfirst = (b == 0 and r == 0 and j == 0 and d == 0)
                                                    if tb < 2:
                                                        rhs = Hb[:, d, r, 4 * q + b, 32 * tb + sh:32 * tb + sh + 32]
                                                        o_ = Yb[32 * b:32 * b + 32, :, j]
                                                    else:
                                                        rhs = Hb[:, d, r, 4 * q + b, 65:99].rearrange("p (s c) -> p s c", s=2)[:, :, sh:sh + 16]
                                                        o_ = bank(yb).rearrange("p (s c j) -> p s c j", s=2, j=16)[32 * b:32 * b + 32, :, :, j]
                                                    last = (d == 1 and b == 3 and j == 15 and r == 1)
                                                    MM(o_, CL[d][r][:, b, tt, :], rhs, False, last, [RCL, RHb], [PB[yb]], tp=(0, 32 * b),
                                                       chain=(("gfirst" if gfirst else "gnext") if GC else False))
                                    if ("d_y%d_%d" % (q, tb)) in Ddbg:
                                        ACT(ydb[:], bank(yb), AF.Identity, [PB[yb]], [Rydb])
                                        DBG("d_y%d_%d" % (q, tb), ydb[:], [Rydb])
                                    g_ = tb % 2
                                    ACT(gt[g_][:], bank(yb), AF.Square, [PB[yb]], [Rgt[g_]])
                                    TS("dve", gt[g_][:], gt[g_][:], 0.044715, ALU.mult, [Rgt[g_]], [Rgt[g_]], s2=1.0, op1=ALU.add)
                                    TT("dve", gt[g_][:], gt[g_][:], bank(yb), ALU.mult, [Rgt[g_], PB[yb]], [Rgt[g_]])
                                    ACT(gt[g_][:], gt[g_][:], AF.Sigmoid, [Rgt[g_]], [Rgt[g_]], scale=1.5957691216)
                                    TT("dve", sfT[:, q, c0:c0 + cn], gt[g_][:], bank(yb), ALU.mult, [Rgt[g_], PB[yb]], [RsfT[q]])
                            build_CL(0); build_K(0)
                            for q in range(4):
                                if q + 1 < 4:
                                    build_CL(q + 1)
                                run_y(q)
                                if q + 1 < 4:
                                    build_K(q + 1)
                            fw.barrier()
                    CH[0] = 'G' in os.environ.get('CHP', 'mabfFGMLUS')
                    with contextlib.ExitStack() as ph:
                        alloc_ws(ph)
                        bgl = sb(ph, "bgl", [128, 4]); Rbgl = Res("bgl"); sg = [sb(ph, "sg%d" % j, [128, 512]) for j in range(2)]; Rsg = [Res("sg0"), Res("sg1")]
                        fw.dma("sp", bgl[:], Din["bglu"][:, l, :], writes=[Rbgl])
                        Wg, RWg = load_w([(lambda t: t[:, 0:2048].rearrange("p (k n) -> p k n", k=4), Din["w_glu"][l].rearrange("(k p) n -> p k n", p=128))])
                        Wgv = Wg[:, 0:2048].rearrange("p (k n) -> p k n", k=4)
                        cnt = 0
                        for m in range(4):
                            for tb, (c0, cn) in enumerate(BLKS):
                                bk = (0, 1, 2)[cnt % 3]; sj = cnt % 2; cnt += 1
                                for k in range(4):
                                    MM(bank(bk), Wgv[:, k, m * 128:(m + 1) * 128], sfT[:, k, c0:c0 + cn], k == 0, k == 3, [RWg, RsfT[k]], [PB[bk]])
                                ACT(sg[sj][:], bank(bk), AF.Sigmoid, [PB[bk], Rbgl], [Rsg[sj]], bias=bgl[:, m:m + 1])
                                TT("dve", ssmT[:, m, c0:c0 + cn], sg[sj][:], sfT[:, m, c0:c0 + cn], ALU.mult, [Rsg[sj], RsfT[m]], [RssmT[m]])
                        fw.barrier()
                    DBG("d_gy", sfT[:, 0:4, :].rearrange("p a b -> p (a b)"), RsfT[0:4])
                    DBG("d_ssmT", ssmT[:].rearrange("p a b -> p (a b)"), RssmT)
                    if stop == "ssm": fw.dead = True
                    CH[0] = 'M' in os.environ.get('CHP', 'mabfFGMLUS')
                    with contextlib.ExitStack() as M:
                        mergedT = sb(M, "mergedT", [128, 8, TOK], BF16); Rmg = [Res("mg%d" % j) for j in range(8)]
                        with contextlib.ExitStack() as ph:
                            alloc_ws(ph)
                            uT = sb(ph, "uT", [128, 8, TOK], BF16); RuT = [Res("uT%d" % i) for i in range(NT)]
                            gs = [sb(ph, "gs%d" % j, [128, 512]) for j in range(3)]; Rgs = [Res("gs%d" % j) for j in range(3)]
                            ac = [sb(ph, "ac%d" % j, [128, 512]) for j in range(2)]; Rac = [Res("ac0"), Res("ac1")]
                            build_uT(uT, RuT, 0)
                            win = Din["w_in"][l].rearrange("(k p) n -> p k n", p=128)
                            wbr = [Din[nm][l].rearrange("(k p) n -> p k n", p=128) for nm in ("w_br_attn", "w_br_ssm", "w_br_four")]
                            srcs = [(attnT, RattnT), (ssmT, RssmT), (fourT, RfourT)]
                            for c in range(8):
                                Wg, RWg = load_w([((lambda t, g=g: t[:, g * 1024:(g + 1) * 1024].rearrange("p (k n) -> p k n", k=8)),
                                                   win[:, :, 1792 + g * 1024 + c * 128:1792 + g * 1024 + (c + 1) * 128]) for g in range(3)])
                                Wb, RWb = load_w([((lambda t, x=x: t[:, x * 512:(x + 1) * 512].rearrange("p (k n) -> p k n", k=4)),
                                                   wbr[x][:, :, c * 128:(c + 1) * 128]) for x in range(3)])
                                for tb, (c0, cn) in enumerate(BLKS):
                                    for g in range(3):
                                        Wgv = Wg[:, g * 1024:(g + 1) * 1024].rearrange("p (k n) -> p k n", k=8)
                                        for k in range(8):
                                            MM(bank(g), Wgv[:, k, :], uT[:, k, c0:c0 + cn], k == 0, k == 7, RuT[c0 // 128:(c0 + cn) // 128] + [RWg], [PB[g]])
                                        ACT(gs[g][:], bank(g), AF.Sigmoid, [PB[g]], [Rgs[g]])
                                    for x in range(3):
                                        Wbv = Wb[:, x * 512:(x + 1) * 512].rearrange("p (k n) -> p k n", k=4)
                                        st_, Rst = srcs[x]
                                        for k in range(4):
                                            MM(bank(3 + x), Wbv[:, k, :], st_[:, k, c0:c0 + cn], k == 0, k == 3, [Rst[k], RWb], [PB[3 + x]])
                                    TT("dve", ac[0][:], gs[0][:], bank(3), ALU.mult, [Rgs[0], PB[3]], [Rac[0]])
                                    TT("dve", ac[1][:], gs[1][:], bank(4), ALU.mult, [Rgs[1], PB[4]], [Rac[1]])
                                    TT("dve", ac[0][:], ac[0][:], ac[1][:], ALU.add, Rac, [Rac[0]])
                                    TT("dve", ac[1][:], gs[2][:], bank(5), ALU.mult, [Rgs[2], PB[5]], [Rac[1]])
                                    TT("dve", mergedT[:, c, c0:c0 + cn], ac[0][:], ac[1][:], ALU.add, Rac, [Rmg[c]])
                            fw.barrier()
                        DBG("d_mergedT", mergedT[:].rearrange("p a b -> p (a b)"), Rmg)
                        if stop == "merge": fw.dead = True
                        layer_norm_residual(l, 0, 8, lambda i, k: mergedT[:, k, i * 128:(i + 1) * 128], lambda i, k: [Rmg[k]],
                                            Din["w_out"][l].rearrange("(k p) n -> p k n", p=128))
              if "d_x1" in Ddbg:
                  for i in range(NT):
                      fw.dma("sp", Ddbg["d_x1"][i * 128:(i + 1) * 128, :], X[:, i, :], reads=[RX[i]], writes=[Rdbg])
              if stop == "ln1": fw.dead = True
              CH[0] = 'U' in os.environ.get('CHP', 'mabfFGMLUS')
              with contextlib.ExitStack() as Fs:
                  actT = sb(Fs, "actT", [128, 22, TOK], BF16); Ract = [Res("act%d" % j) for j in range(22)]
                  with contextlib.ExitStack() as ph:
                      alloc_ws(ph)
                      uT = sb(ph, "uT", [128, 8, TOK], BF16); RuT = [Res("uT%d" % i) for i in range(NT)]
                      cvp = sb(ph, "cvp", [128, 44, 4]); Rcv = Res("cvp")
                      hc = [sb(ph, "hc%d" % j, [128, TOK]) for j in range(2)]; Rhc = [Res("hc0"), Res("hc1")]
                      fw.dma("sp", cvp[:], Din["convp"][:, l], writes=[Rcv])
                      build_uT(uT, RuT, 1)
                      wup = Din["w_up"][l].rearrange("(k p) n -> p k n", p=128)
                      for ch in range(44):
                          if ch % 4 == 0:
                              W, RW = load_w([(lambda t: t[:, 0:4096].rearrange("p (k n) -> p k n", k=8), wup[:, :, ch * 128:ch * 128 + 512])])
                              Wv = W[:, 0:4096].rearrange("p (k n) -> p k n", k=8)
                          cw = ch % 4; pb0 = 3 * (ch % 2); hj = ch % 2
                          prs = [PB[pb0], PB[pb0 + 1], PB[pb0 + 2]]
                          for tb, (c0, cn) in enumerate(BLKS):
                              for k in range(8):
                                  MM(bank(pb0 + tb), Wv[:, k, cw * 128:(cw + 1) * 128], uT[:, k, c0:c0 + cn], k == 0, k == 7,
                                     RuT[c0 // 128:(c0 + cn) // 128] + [RW], [prs[tb]])
                          hp = bank(pb0, 3); h_ = hc[hj]
                          ACT(h_[:], hp, AF.Identity, prs + [Rcv], [Rhc[hj]], scale=cvp[:, ch, 1:2], bias=cvp[:, ch, 3:4])
                          def stt(o_, i0, sc, i1):
                              OP("dve", lambda e: e.scalar_tensor_tensor(out=o_, in0=i0, scalar=sc, in1=i1, op0=ALU.mult, op1=ALU.add),
                                 prs + [Rcv, Rhc[hj]], [Rhc[hj]])
                          stt(h_[:, 1:1024], hp[:, 0:1023], cvp[:, ch, 0:1], h_[:, 1:1024])
                          stt(h_[:, 0:1023], hp[:, 1:1024], cvp[:, ch, 2:3], h_[:, 0:1023])
                          h3 = h_[:, 1024:1536].rearrange("p (s c) -> p s c", s=2); p3 = hp[:, 1024:1536].rearrange("p (s c) -> p s c", s=2)
                          stt(h3[:, :, 1:256], p3[:, :, 0:255], cvp[:, ch, 0:1], h3[:, :, 1:256])
                          stt(h3[:, :, 0:255], p3[:, :, 1:256], cvp[:, ch, 2:3], h3[:, :, 0:255])
                          if ch < 22:
                              ACT(actT[:, ch, :], h_[:], AF.Silu, [Rhc[hj]], [Ract[ch]])
                          else:
                              TT("dve", actT[:, ch - 22, :], actT[:, ch - 22, :], h_[:], ALU.mult, [Ract[ch - 22], Rhc[hj]], [Ract[ch - 22]])
                      fw.barrier()
                  DBG("d_actT", actT[:].rearrange("p a b -> p (a b)"), Ract)
                  if stop == "ffn": fw.dead = True
                  layer_norm_residual(l, 1, 22, lambda i, k: actT[:, k, i * 128:(i + 1) * 128], lambda i, k: [Ract[k]],
                                      Din["w_down"][l].rearrange("(k p) n -> p k n", p=128))
        layers()
        fw.dead = False
        for i in range(NT):
            fw.dma("sp", Dout["y"][i * 128:(i + 1) * 128, :], X[:, i, :], reads=[RX[i]], writes=[Rdbg])
        fw.finish()
    return nc


def kernel(**inputs):
    inp = {k: np.asarray(v) for k, v in inputs.items()}
    S = prep_shared(inp)
    nc = build()
    in_maps = []
    for cid in range(8):
        m = dict(S); m.update(prep_core(inp, cid))
        in_maps.append(m)
    res = run_bass_kernel_spmd(nc, in_maps, core_ids=list(range(8)))
    y_p = np.zeros((16, 256, 1024), np.float32); y_s = np.zeros((8, 1024, 1024), np.float32)
    nk = np.zeros((16, 2, 256, 2, 64), np.float32); nv = np.zeros((16, 2, 256, 2, 64), np.float32)
    nst = np.zeros((16, 2, 2, 32, 64, 2), np.float32)
    for cid in range(8):
        r = res.results[cid]
        y_s[cid] = r["y"][0:1024]
        y_p[2 * cid:2 * cid + 2] = r["y"][1024:1536].reshape(2, 256, 1024)
        nk[2 * cid:2 * cid + 2] = r["nk"].reshape(2, 2, 256, 2, 64)
        nv[2 * cid:2 * cid + 2] = r["nv"].reshape(2, 2, 256, 2, 64)
        nst[2 * cid:2 * cid + 2] = r["nst"]
    return (y_p, y_s, nk, nv, nst)
```

```python
import contextlib, os
SK = os.environ.get("SK", "")
import numpy as np
import concourse.bass as bass
import concourse.mybir as mybir
from concourse.bass_utils import run_bass_kernel_spmd

F32 = mybir.dt.float32; BF16 = mybir.dt.bfloat16
AF = mybir.ActivationFunctionType; ALU = mybir.AluOpType; AX = mybir.AxisListType

D_MODEL = 1024; DEPTH = 2; D_IN = 4864; D_FF = 2816
ALPHA = (2 * DEPTH) ** 0.25
EPS = 1e-6
NT = 12; TOK = 1536; T = 16
SEQS = [(0, 1024), (1024, 256), (1280, 256)]
BLKS = [(0, 512), (512, 512), (1024, 512)]
TILE_VEC = [0] * 8 + [1] * 4
NS = 99
SOFF = [0, 65, 82]
SCH = [64, 16, 16]


class StopBuild(Exception):
    pass


class Res:
    __slots__ = ("name", "w", "r")
    def __init__(self, name):
        self.name = name; self.w = None; self.r = {}


class Eng:
    def __init__(self, name, h, sem):
        self.name = name; self.h = h; self.sem = sem; self.count = 0; self.waited = {}


class FW:
    NDMA = 32
    def __init__(self, nc, es):
        self.nc = nc; self.sems = {}; self.eng = {}
        for name, h in (("pe", nc.tensor), ("act", nc.scalar), ("dve", nc.vector), ("pool", nc.gpsimd), ("sp", nc.sync)):
            s = es.enter_context(nc.semaphore("s_" + name))
            self.sems[name] = s; self.eng[name] = Eng(name, h, s)
        self.dma_sems = []
        for i in range(self.NDMA):
            s = es.enter_context(nc.semaphore("s_dma%d" % i))
            self.sems["dma%d" % i] = s; self.dma_sems.append(["dma%d" % i, 0])
        self.dma_i = 0; self.ninstr = 0; self.dead = False; self.prev_plain = False; self.rec = None

    def _wait(self, e, key, val):
        if e.waited.get(key, 0) >= val: return
        e.h.wait_ge(self.sems[key], val); e.waited[key] = val

    def _deps(self, e, reads, writes, chain=False):
        deps = {}
        def add(kv):
            if kv is None: return
            k, v = kv
            if deps.get(k, 0) < v: deps[k] = v
        for r in reads: add(r.w)
        for w in writes:
            if w.w is not None and not (chain and w.w[0] == e.name):
                add(w.w)
            for k, v in w.r.items():
                if k == e.name: continue
                add((k, v))
        return deps

    def op(self, engname, fn, reads=(), writes=(), chain=False):
        if self.dead: return None
        if self.rec is not None:
            self.rec.append(("op", engname, fn, tuple(reads), tuple(writes), chain)); return None
        e = self.eng[engname]
        if engname == "pe" and chain in ("gfirst", "gnext"):
            if chain == "gfirst" and e.count > 0:
                self._wait(e, "pe", e.count)
            chain = True; self.prev_plain = False
        elif engname == "pe":
            plain = chain
            chain = chain and self.prev_plain
            if plain and not chain and e.count > 0:
                self._wait(e, "pe", e.count)
            self.prev_plain = plain
        for k, v in self._deps(e, reads, writes, chain).items(): self._wait(e, k, v)
        ins = fn(e.h); e.count += 1; ins.then_inc(e.sem, 1)
        for r in reads: r.r[e.name] = e.count
        for w in writes: w.w = (e.name, e.count); w.r = {}
        self.ninstr += 1
        return ins

    def dma(self, qname, out, in_, reads=(), writes=()):
        if self.dead: return None
        if self.rec is not None:
            self.rec.append(("dma", qname, out, in_, tuple(reads), tuple(writes))); return None
        e = self.eng[qname]
        deps = self._deps(e, reads, writes)
        slot = self.dma_sems[self.dma_i % self.NDMA]; self.dma_i += 1
        key = slot[0]
        if slot[1] > 0: deps[key] = max(deps.get(key, 0), slot[1])
        for k, v in deps.items(): self._wait(e, k, v)
        ins = e.h.dma_start(out=out, in_=in_)
        slot[1] += 16; ins.then_inc(self.sems[key], 16)
        for r in reads: r.r[key] = slot[1]
        for w in writes: w.w = (key, slot[1]); w.r = {}
        self.ninstr += 1
        return ins

    def replay(self, items):
        for it in items:
            if it[0] == "op": self.op(it[1], it[2], it[3], it[4], it[5])
            else: self.dma(it[1], it[2], it[3], reads=it[4], writes=it[5])

    def barrier(self):
        if self.dead: return
        targets = {n: e.count for n, e in self.eng.items() if e.count > 0}
        for key, cnt in self.dma_sems:
            if cnt > 0: targets[key] = cnt
        for n, e in self.eng.items():
            for k, v in targets.items(): self._wait(e, k, v)

    def finish(self):
        self.dead = False
        self.barrier()


def bl(ap, n):
    return ap.unsqueeze(2).to_broadcast([ap.shape[0], ap.shape[1], n])


def bm(ap, n):
    return ap.unsqueeze(1).to_broadcast([ap.shape[0], n, ap.shape[1]])


def _rep(a, n=128):
    return np.ascontiguousarray(np.broadcast_to(a[None], (n,) + a.shape))


def prep_shared(inp):
    f = lambda a: np.ascontiguousarray(a, dtype=np.float32)
    L = DEPTH
    S = {}
    for k in ("w_ada", "w_in", "w_glu", "w_br_ssm", "w_br_four", "w_out", "w_up", "w_down"):
        S[k] = f(inp[k])
    S["w_br_attn"] = f(inp["w_br_attn"])
    S["b_adaP"] = f(inp["b_ada"].reshape(L, 48, 128).transpose(2, 0, 1))
    S["bgbc"] = f(np.stack([np.stack([_rep(inp["b_ada"][l, 2048:3072]), _rep(inp["b_ada"][l, 5120:6144])], 1) for l in range(L)]))
    S["lnbc"] = f(np.stack([np.stack([_rep(inp[k][l]) for k in ("ln1_g", "ln1_b", "ln2_g", "ln2_b")]) for l in range(L)]))
    cw = inp["conv_w"].reshape(L, 3, 44, 128).transpose(3, 0, 2, 1)
    cb = inp["conv_b"].reshape(L, 44, 128).transpose(2, 0, 1)[..., None]
    S["convp"] = f(np.concatenate([cw, cb], -1))
    S["bglu"] = f(inp["b_glu"].reshape(L, 4, 128).transpose(2, 0, 1))
    S["g640"] = f(np.stack([_rep(np.concatenate([np.tile(inp["q_norm_g"][l], 8), np.tile(inp["k_norm_g"][l], 4)])) for l in range(L)]))
    def lay(a):
        aP = a.reshape(L, 2, 16, 2, 64).transpose(0, 1, 3, 4, 2).reshape(L, 2, 128, 16)
        aF = a.reshape(L, 2, 4, 4, 2, 64).transpose(0, 1, 3, 2, 4, 5)
        aF = np.broadcast_to(aF[:, :, :, None], (L, 2, 4, 32, 4, 2, 64)).reshape(L, 2, 128, 512)
        return np.concatenate([aP, aF], -1)
    ldt = np.broadcast_to(inp["ssm_log_dt"][..., None], (L, 2, 32, 64))
    S["lamin"] = f(np.stack([lay(inp["ssm_a_re"]), lay(inp["ssm_a_im"]), lay(ldt)], 2))
    def bbdP(b):
        Bp = b.reshape(L, 2, 16, 2, 64, 16)
        out = np.zeros((L, 2, 2, 64, 16, 2, 16), np.float32)
        for g2 in range(2):
            out[:, :, g2, :, :, g2, :] = Bp[:, :, :, g2].transpose(0, 1, 3, 2, 4)
        return out.reshape(L, 2, 128, 16, 32)
    def bbdT(b):
        Bq = b.reshape(L, 2, 4, 4, 2, 64, 16)
        out = np.zeros((L, 2, 4, 2, 16, 4, 2, 64), np.float32)
        for g2 in range(2):
            out[:, :, :, g2, :, :, g2, :] = Bq[:, :, :, :, g2].transpose(0, 1, 3, 5, 2, 4)
        return out.reshape(L, 2, 128, 4, 128)
    def cbdP(c):
        Cp = c.reshape(L, 2, 16, 2, 16, 64)
        out = np.zeros((L, 2, 2, 64, 16, 2, 16), np.float32)
        for g2 in range(2):
            out[:, :, g2, :, :, g2, :] = Cp[:, :, :, g2].transpose(0, 1, 4, 2, 3)
        return out.reshape(L, 2, 128, 16, 32)
    S["BbdP"] = f(np.stack([bbdP(inp["ssm_b_re"]), bbdP(inp["ssm_b_im"])], 2))
    S["BbdT"] = f(np.stack([bbdT(inp["ssm_b_re"]), bbdT(inp["ssm_b_im"])], 2))
    S["CbdP"] = f(np.stack([cbdP(inp["ssm_c_re"]), cbdP(inp["ssm_c_im"])], 2))
    Dblk = np.zeros((L, 128, 4, 128), np.float32)
    for l in range(L):
        for q in range(4):
            Dblk[l, np.arange(128), q, np.arange(128)] = inp["ssm_d"][l, q * 128:(q + 1) * 128]
    S["Dblk"] = Dblk
    S["ident"] = np.eye(128, dtype=np.float32)
    mk = np.zeros((128, 4), np.float32); mk[np.arange(128), np.arange(128) // 32] = 1.0
    S["maskP"] = mk
    t = np.arange(1024)
    freqs = (10000.0 ** (-np.arange(16, dtype=np.float32) / 16)).astype(np.float32)
    ang = np.concatenate([(t // 64).astype(np.float32)[:, None] * freqs, (t % 64).astype(np.float32)[:, None] * freqs], -1)
    S["ropec"] = f(np.cos(ang).reshape(8, 128, 32).transpose(1, 0, 2))
    S["ropes"] = f(np.sin(ang).reshape(8, 128, 32).transpose(1, 0, 2))
    def dft(n):
        i = np.arange(n, dtype=np.int64)
        m = (i[:, None] * i[None, :]) % n
        a = 2.0 * np.pi * m.astype(np.float64) / n
        return np.cos(a) / np.sqrt(n), np.sin(a) / np.sqrt(n)
    c128, s128 = dft(128)
    S["cs128"] = f(np.concatenate([c128, s128], 1))
    for n in (1024, 256):
        c, s = dft(n)
        S["cl%d" % n] = f(c.reshape(n // 128, 128, n).transpose(1, 0, 2))
        S["sl%d" % n] = f((-s).reshape(n // 128, 128, n).transpose(1, 0, 2))
    return S


def prep_core(inp, cid):
    f = lambda a: np.ascontiguousarray(a, dtype=np.float32)
    m = {}
    m["xin"] = f(np.concatenate([inp["x_sample"][cid], inp["x_prompt"][2 * cid:2 * cid + 2].reshape(512, 1024)], 0))
    cv = np.stack([inp["c"][cid], inp["c_ctx"]], -1)
    m["cvec"] = f(cv.reshape(8, 128, 2).transpose(1, 0, 2))
    m["ckT"] = f(inp["cache_k"][cid].reshape(DEPTH, 512, 128).transpose(0, 2, 1))
    m["cv"] = f(inp["cache_v"][cid].reshape(DEPTH, 512, 128))
    st = inp["state_ssm"][cid].reshape(DEPTH, 2, 16, 2, 64, 2)
    m["h0"] = f(st.transpose(3, 4, 0, 1, 5, 2).reshape(128, DEPTH, 2, 2, 16))
    return m


IN_SHAPES = {
    "xin": (1536, 1024), "cvec": (128, 8, 2), "ckT": (2, 128, 512), "cv": (2, 512, 128), "h0": (128, 2, 2, 2, 16),
    "w_ada": (2, 1024, 6144), "w_in": (2, 1024, 4864), "w_glu": (2, 512, 512), "w_br_attn": (2, 512, 1024),
    "w_br_ssm": (2, 512, 1024), "w_br_four": (2, 512, 1024), "w_out": (2, 1024, 1024), "w_up": (2, 1024, 5632),
    "w_down": (2, 2816, 1024), "b_adaP": (128, 2, 48), "bgbc": (2, 128, 2, 1024), "lnbc": (2, 4, 128, 1024),
    "convp": (128, 2, 44, 4), "bglu": (128, 2, 4), "g640": (2, 128, 768), "lamin": (2, 2, 3, 128, 528),
    "BbdP": (2, 2, 2, 128, 16, 32), "BbdT": (2, 2, 2, 128, 4, 128), "CbdP": (2, 2, 2, 128, 16, 32),
    "Dblk": (2, 128, 4, 128), "ident": (128, 128), "maskP": (128, 4), "ropec": (128, 8, 32), "ropes": (128, 8, 32),
    "cs128": (128, 256), "cl1024": (128, 8, 1024), "sl1024": (128, 8, 1024), "cl256": (128, 2, 256), "sl256": (128, 2, 256),
}
OUT_SHAPES = {"y": (1536, 1024), "nk": (2, 2, 256, 128), "nv": (2, 2, 256, 128), "nst": (2, 2, 2, 32, 64, 2)}


def build(n_layers=DEPTH, stop=None, dbg_shapes=None):
    nc = bass.Bass("TRN2", target_bir_lowering=False)
    Din = {k: nc.dram_tensor(k, list(s), F32, kind="ExternalInput").ap() for k, s in IN_SHAPES.items()}
    Dout = {k: nc.dram_tensor(k, list(s), F32, kind="ExternalOutput").ap() for k, s in OUT_SHAPES.items()}
    Ddbg = {k: nc.dram_tensor(k, list(s), F32, kind="ExternalOutput").ap() for k, s in (dbg_shapes or {}).items()}
    es = contextlib.ExitStack()
    with es:
        fw = FW(nc, es)
        uid = [0]
        def sb(st, name, shape, dt=F32):
            uid[0] += 1
            return st.enter_context(nc.sbuf_tensor("%s_%d" % (name, uid[0]), list(shape), dt))
        OP = fw.op
        CH = [True]
        def MM(out, lhsT, rhs, start, stop_, r, w, tp=None, chain=None):
            if chain is None: chain = CH[0] and (tp is None)
            if tp is None:
                return fw.op("pe", lambda e: e.matmul(out, lhsT=lhsT, rhs=rhs, start=start, stop=stop_), r, w, chain=chain)
            return fw.op("pe", lambda e: e.matmul(out, lhsT=lhsT, rhs=rhs, start=start, stop=stop_, tile_position=tp), r, w, chain=chain)
        def TT(eng, out, in0, in1, op, r, w):
            return fw.op(eng, lambda e: e.tensor_tensor(out=out, in0=in0, in1=in1, op=op), r, w)
        def TS(eng, out, in0, s1, op0, r, w, s2=None, op1=None):
            if op1 is None:
                return fw.op(eng, lambda e: e.tensor_scalar(out=out, in0=in0, scalar1=s1, scalar2=None, op0=op0), r, w)
            return fw.op(eng, lambda e: e.tensor_scalar(out=out, in0=in0, scalar1=s1, scalar2=s2, op0=op0, op1=op1), r, w)
        def ACT(out, in_, func, r, w, scale=1.0, bias=0.0):
            return fw.op("act", lambda e: e.activation(out=out, in_=in_, func=func, bias=bias, scale=scale), r, w)
        def CP(eng, out, in_, r, w):
            if eng == "act":
                return fw.op("act", lambda e: e.copy(out=out, in_=in_), r, w)
            return fw.op(eng, lambda e: e.tensor_copy(out=out, in_=in_), r, w)
        Rdbg = Res("dbg")
        def DBG(name, ap, r):
            if name in Ddbg:
                fw.dma("pool", Ddbg[name], ap, reads=r, writes=[Rdbg])

        X = sb(es, "X", [128, NT, 1024]); RX = [Res("X%d" % i) for i in range(NT)]
        ident_f = sb(es, "ident_f", [128, 128]); ident_b = sb(es, "ident_b", [128, 128], BF16); Rid = Res("id")
        maskP = sb(es, "maskP", [128, 4]); Rmask = Res("mask")
        modP = sb(es, "modP", [128, 6, 8, 2]); opsc = sb(es, "opsc", [128, 2, 8, 2]); Rmod = Res("mod")
        csl = sb(es, "csl", [128, 8, 2]); sTb = sb(es, "sTb", [128, 8, 2], BF16); sRep = sb(es, "sRep", [128, 2, 8, 128], BF16)
        Rcs = Res("cs")
        stat = sb(es, "stat", [128, NT, 2, 6]); mv = sb(es, "mv", [128, NT, 2]); rs = sb(es, "rs", [128, NT, 2]); Rstat = Res("stat")
        ps = es.enter_context(nc.psum_tensor("ps", [128, 3584], F32)); PB = [Res("pb%d" % i) for i in range(7)]
        pst = es.enter_context(nc.psum_tensor("pst", [128, 1024], BF16)); PT = Res("pt")
        wslot = [None, None]; Rws = [None, None]
        wctr = [0]
        def alloc_ws(st):
            for i_ in range(2):
                wslot[i_] = sb(st, "wslot%d" % i_, [128, 4096], BF16); Rws[i_] = Res("ws%d" % i_)
        def load_w(views):
            s_ = wctr[0] % 2; wctr[0] += 1
            for dstf, src in views:
                fw.dma("pool", dstf(wslot[s_]), src, writes=[Rws[s_]])
            return wslot[s_], Rws[s_]
        def bank(i, n=1):
            return ps[:, i * 512:(i + n) * 512]

        fw.dma("sp", ident_f[:], Din["ident"], writes=[Rid])
        fw.dma("pool", ident_b[:], Din["ident"], writes=[Rid])
        fw.dma("sp", maskP[:], Din["maskP"], writes=[Rmask])
        for i in range(NT):
            fw.dma("sp", X[:, i, :], Din["xin"][i * 128:(i + 1) * 128, :], writes=[RX[i]])
        fw.dma("sp", csl[:], Din["cvec"], writes=[Rcs])
        ACT(csl[:], csl[:], AF.Silu, [Rcs], [Rcs])
        CP("dve", sTb[:], csl[:], [Rcs], [Rcs])
        for v in range(2):
            CP("dve", sRep[:, v, :, :], bl(csl[:, :, v], 128), [Rcs], [Rcs])

        def build_uT(uT, RuT, sub):
            sh_sec = 0 if sub == 0 else 3
            for i in range(NT):
                v = TILE_VEC[i]
                for kk in range(2):
                    bk = kk
                    for k4 in range(4):
                        k = kk * 4 + k4
                        OP("pe", lambda e: e.transpose(bank(bk)[:, k4 * 128:(k4 + 1) * 128], X[:, i, k * 128:(k + 1) * 128], ident_f[:]),
                           [RX[i], Rid], [PB[bk]])
                    for k4 in range(4):
                        k = kk * 4 + k4
                        ACT(uT[:, k, i * 128:(i + 1) * 128], bank(bk)[:, k4 * 128:(k4 + 1) * 128], AF.Identity,
                            [PB[bk], Rmod], [RuT[i]], scale=opsc[:, sub, k, v:v + 1], bias=modP[:, sh_sec, k, v:v + 1])

        def layer_norm_residual(l, which, kc, lhsT_fn, lres_fn, wsrc3):
            with contextlib.ExitStack() as ph:
                ch_save = CH[0]; CH[0] = 'L' in os.environ.get('CHP', 'mabfFGMLUS')
                alloc_ws(ph)
                lnb = sb(ph, "lnb", [128, 2, 1024]); Rln = Res("lnb")
                tmp = [sb(ph, "lntmp%d" % j, [128, 512]) for j in range(2)]; Rtmp = [Res("lntmp0"), Res("lntmp1")]
                Wd = sb(ph, "Wd", [128, kc, 512], BF16); RWd = Res("Wd")
                fw.dma("sp", lnb[:, 0, :], Din["lnbc"][l, 2 * which], writes=[Rln])
                fw.dma("sp", lnb[:, 1, :], Din["lnbc"][l, 2 * which + 1], writes=[Rln])
                gbc = sb(ph, "gbc", [128, 2, 1024]); Rgbc = Res("gbc")
                bg = sb(ph, "bg", [128, 1024]); Rbg = Res("bg")
                fw.dma("sp", bg[:], Din["bgbc"][l, :, which, :], writes=[Rbg])
                sec = 2 if which == 0 else 5
                wsrc = Din["w_ada"][l].rearrange("(k p) n -> p k n", p=128)
                for half in range(2):
                    Wg, RWg = load_w([(lambda t: t[:, 0:4096].rearrange("p (k n) -> p k n", k=8), wsrc[:, :, sec * 1024 + half * 512:sec * 1024 + (half + 1) * 512])])
                    Wgv = Wg[:, 0:4096].rearrange("p (k n) -> p k n", k=8)
                    for v in range(2):
                        for k in range(8):
                            MM(bank(v), sRep[:, v, k, :], Wgv[:, k, :], k == 0, k == 7, [RWg, Rcs], [PB[v]])
                        TT("dve", gbc[:, v, half * 512:(half + 1) * 512], bank(v), bg[:, half * 512:(half + 1) * 512], ALU.add, [PB[v], Rbg], [Rgbc])
                cnt = 0
                for nb in range(2):
                    cs = slice(nb * 512, (nb + 1) * 512)
                    for k0 in range(0, kc, 8):
                        k1 = min(kc, k0 + 8)
                        fw.dma("pool", Wd[:, k0:k1, :], wsrc3[:, k0:k1, cs], writes=[RWd])
                    for i in range(NT):
                        v = TILE_VEC[i]
                        bk = 2 + (cnt % 4); tj = cnt % 2; cnt += 1
                        for k in range(kc):
                            MM(bank(bk), lhsT_fn(i, k), Wd[:, k, :], k == 0, k == kc - 1, lres_fn(i, k) + [RWd], [PB[bk]])
                        TT("dve", tmp[tj][:], bank(bk), gbc[:, v, cs], ALU.mult, [PB[bk], Rgbc], [Rtmp[tj]])
                        OP("dve", lambda e: e.scalar_tensor_tensor(out=X[:, i, cs], in0=X[:, i, cs], scalar=float(ALPHA), in1=tmp[tj][:],
                                                                   op0=ALU.mult, op1=ALU.add), [RX[i], Rtmp[tj]], [RX[i]])
                for i in range(NT):
                    for hh in range(2):
                        OP("dve", lambda e: e.bn_stats(out=stat[:, i, hh, :], in_=X[:, i, hh * 512:(hh + 1) * 512]), [RX[i]], [Rstat])
                    OP("dve", lambda e: e.bn_aggr(out=mv[:, i, :], in_=stat[:, i, :, :].rearrange("p a b -> p (a b)")), [Rstat], [Rstat])
                TS("dve", rs[:, :, 0], mv[:, :, 1], EPS, ALU.add, [Rstat], [Rstat])
                ACT(rs[:, :, 0], rs[:, :, 0], AF.Sqrt, [Rstat], [Rstat])
                OP("dve", lambda e: e.reciprocal(out=rs[:, :, 0], in_=rs[:, :, 0]), [Rstat], [Rstat])
                OP("dve", lambda e: e.scalar_tensor_tensor(out=rs[:, :, 1], in0=mv[:, :, 0], scalar=-1.0, in1=rs[:, :, 0],
                                                           op0=ALU.mult, op1=ALU.mult), [Rstat], [Rstat])
                for i in range(NT):
                    ACT(X[:, i, :], X[:, i, :], AF.Identity, [RX[i], Rstat], [RX[i]], scale=rs[:, i, 0:1], bias=rs[:, i, 1:2])
                    TT("pool", X[:, i, :], X[:, i, :], lnb[:, 0, :], ALU.mult, [RX[i], Rln], [RX[i]])
                    TT("pool", X[:, i, :], X[:, i, :], lnb[:, 1, :], ALU.add, [RX[i], Rln], [RX[i]])
                fw.barrier()
                CH[0] = ch_save

        def layers():
          for l in range(n_layers):
              with contextlib.ExitStack() as ph:
                  wa = [sb(ph, "wa%d" % i, [128, 8, 512], BF16) for i in range(3)]; Rwa = [Res("wa%d" % i) for i in range(3)]
                  bP = sb(ph, "bP", [128, 48]); Rb = Res("bmod")
                  fw.dma("sp", bP[:], Din["b_adaP"][:, l, :], writes=[Rb])
                  wsrc = Din["w_ada"][l].rearrange("(k p) n -> p k n", p=128)
                  pm = bank(6)[:, 0:96]
                  CH[0] = 'm' in os.environ.get('CHP', 'mabfFGMLUS')
                  for cb in range(12):
                      s = cb % 3
                      sec, half = cb // 2, cb % 2
                      if sec in (2, 5):
                          continue
                      fw.dma("pool", wa[s][:], wsrc[:, :, cb * 512:(cb + 1) * 512], writes=[Rwa[s]])
                      for m in range(4):
                          col = (sec * 8 + half * 4 + m) * 2
                          for k in range(8):
                              MM(pm[:, col:col + 2], wa[s][:, k, m * 128:(m + 1) * 128], sTb[:, k, :], k == 0, k == 7,
                                 [Rwa[s], Rcs], [PB[6]])
                  for sec in (0, 1, 3, 4):
                      TT("dve", modP[:, sec, :, :], pm[:, sec * 16:(sec + 1) * 16].rearrange("p (c v) -> p c v", v=2),
                         bl(bP[:, sec * 8:(sec + 1) * 8], 2), ALU.add, [PB[6], Rb], [Rmod])
                  TS("dve", opsc[:, 0, :, :], modP[:, 1, :, :], 1.0, ALU.add, [Rmod], [Rmod])
                  TS("dve", opsc[:, 1, :, :], modP[:, 4, :, :], 1.0, ALU.add, [Rmod], [Rmod])
                  fw.barrier()
              if stop == "mod": fw.dead = True
              DBG("d_modP", modP[:].rearrange("p a b c -> p (a b c)"), [Rmod])

              with contextlib.ExitStack() as L1:
                  QT = sb(L1, "QT", [128, 4, TOK], BF16); RQ = [Res("QT%d" % j) for j in range(4)]; attnT = QT; RattnT = RQ
                  sfT = sb(L1, "sfT", [128, 8, TOK], BF16); RsfT = [Res("sfT%d" % j) for j in range(8)]
                  fourT = sb(L1, "fourT", [128, 4, TOK], BF16); RfourT = [Res("fourT%d" % j) for j in range(4)]
                  Lp = contextlib.ExitStack()
                  lamD = [[sb(Lp, "lam%d%d" % (d_, r), [128, 528]) for r in range(2)] for d_ in range(2)]
                  ffD = [[sb(Lp, "ff%d%d" % (d_, r), [128, 528]) for r in range(2)] for d_ in range(2)]
                  PowP = [sb(Lp, "PowP%d" % r, [128, 2, 16, 17]) for r in range(2)]
                  LH = [sb(Lp, "LH%d" % r, [128, 2, 16, 7]) for r in range(2)]
                  fP = [sb(Lp, "fP%d" % r, [128, 2, 16]) for r in range(2)]
                  RS = Res("ssm")
                  def cmul(dr, di, ar, ai, br, bi, m1, m2):
                      TT("dve", m1, ar, br, ALU.mult, [RS], [RS]); TT("dve", m2, ai, bi, ALU.mult, [RS], [RS])
                      TT("dve", dr, m1, m2, ALU.subtract, [RS], [RS])
                      TT("dve", m1, ar, bi, ALU.mult, [RS], [RS]); TT("dve", m2, ai, br, ALU.mult, [RS], [RS])
                      TT("dve", di, m1, m2, ALU.add, [RS], [RS])
                  def ssm_param_pipeline(st):
                      are, aim, ldt, mag, cc, ss, t1, t2, t3 = [sb(st, "sp%d" % j_, [128, 528]) for j_ in range(9)]
                      for d in range(2):
                          lam = lamD[d]; ff = ffD[d]
                          fw.dma("sp", are[:], Din["lamin"][l, d, 0], writes=[RS])
                          fw.dma("sp", aim[:], Din["lamin"][l, d, 1], writes=[RS])
                          fw.dma("sp", ldt[:], Din["lamin"][l, d, 2], writes=[RS])
                          ACT(ldt[:], ldt[:], AF.Exp, [RS], [RS])
                          TT("dve", t1[:], are[:], ldt[:], ALU.mult, [RS], [RS])
                          ACT(mag[:], t1[:], AF.Exp, [RS], [RS])
                          TT("dve", t2[:], aim[:], ldt[:], ALU.mult, [RS], [RS])
                          ACT(ss[:], t2[:], AF.Sin, [RS], [RS], scale=1.0 / 16)
                          ACT(t1[:], t2[:], AF.Sin, [RS], [RS], scale=1.0 / 32)
                          TT("dve", t1[:], t1[:], t1[:], ALU.mult, [RS], [RS])
                          TS("dve", cc[:], t1[:], -2.0, ALU.mult, [RS], [RS], s2=1.0, op1=ALU.add)
                          for _ in range(4):
                              TT("dve", t1[:], cc[:], cc[:], ALU.mult, [RS], [RS])
                              TT("dve", t2[:], ss[:], ss[:], ALU.mult, [RS], [RS])
                              TT("dve", t3[:], cc[:], ss[:], ALU.mult, [RS], [RS])
                              TT("dve", cc[:], t1[:], t2[:], ALU.subtract, [RS], [RS])
                              TS("dve", ss[:], t3[:], 2.0, ALU.mult, [RS], [RS])
                          TT("dve", lam[0][:], mag[:], cc[:], ALU.mult, [RS], [RS])
                          TT("dve", lam[1][:], mag[:], ss[:], ALU.mult, [RS], [RS])
                          TT("dve", t1[:], are[:], are[:], ALU.mult, [RS], [RS])
                          TT("dve", t2[:], aim[:], aim[:], ALU.mult, [RS], [RS])
                          TT("dve", t1[:], t1[:], t2[:], ALU.add, [RS], [RS])
                          OP("dve", lambda e: e.reciprocal(out=t1[:], in_=t1[:]), [RS], [RS])
                          TS("dve", t2[:], lam[0][:], -1.0, ALU.add, [RS], [RS])
                          TT("dve", t3[:], t2[:], are[:], ALU.mult, [RS], [RS])
                          TT("dve", cc[:], lam[1][:], aim[:], ALU.mult, [RS], [RS])
                          TT("dve", t3[:], t3[:], cc[:], ALU.add, [RS], [RS])
                          TT("dve", ff[0][:], t3[:], t1[:], ALU.mult, [RS], [RS])
                          TT("dve", t3[:], lam[1][:], are[:], ALU.mult, [RS], [RS])
                          TT("dve", cc[:], t2[:], aim[:], ALU.mult, [RS], [RS])
                          TT("dve", t3[:], t3[:], cc[:], ALU.subtract, [RS], [RS])
                          TT("dve", ff[1][:], t3[:], t1[:], ALU.mult, [RS], [RS])
                          pm1 = t1[:, 0:256].rearrange("p (g k) -> p g k", g=16); pm2 = t2[:, 0:256].rearrange("p (g k) -> p g k", g=16)
                          OP("dve", lambda e, d=d: e.memset(PowP[0][:, d, :, 0:1], 1.0), [RS], [RS])
                          OP("dve", lambda e, d=d: e.memset(PowP[1][:, d, :, 0:1], 0.0), [RS], [RS])
                          for r in range(2):
                              TS("dve", PowP[r][:, d, :, 1], lam[r][:, 0:16], 1.0, ALU.mult, [RS], [RS])
                              TS("dve", fP[r][:, d, :], ff[r][:, 0:16], 1.0, ALU.mult, [RS], [RS])
                          for w in (1, 2, 4, 8):
                              br_ = PowP[0][:, d, :, w:w + 1].to_broadcast([128, 16, w]); bi_ = PowP[1][:, d, :, w:w + 1].to_broadcast([128, 16, w])
                              cmul(PowP[0][:, d, :, w + 1:2 * w + 1], PowP[1][:, d, :, w + 1:2 * w + 1],
                                   PowP[0][:, d, :, 1:w + 1], PowP[1][:, d, :, 1:w + 1], br_, bi_, pm1[:, :, 0:w], pm2[:, :, 0:w])
                          for r in range(2):
                              TS("dve", LH[r][:, d, :, 0], PowP[r][:, d, :, 16], 1.0, ALU.mult, [RS], [RS])
                          for m in range(6):
                              cmul(LH[0][:, d, :, m + 1], LH[1][:, d, :, m + 1], LH[0][:, d, :, m], LH[1][:, d, :, m],
                                   LH[0][:, d, :, m], LH[1][:, d, :, m], t1[:, 256:272], t2[:, 256:272])
                  with contextlib.ExitStack() as L2:
                      KT = sb(L2, "KT", [128, 2, 2048], BF16); RKT = Res("KT")
                      Vsb = sb(L2, "Vsb", [128, 16, 2, 2, 128], BF16); RV = Res("V")
                      with contextlib.ExitStack() as ph:
                          alloc_ws(ph)
                          uT = sb(ph, "uT", [128, 8, TOK], BF16); RuT = [Res("uT%d" % i) for i in range(NT)]
                          qn = [sb(ph, "qn%d" % j, [128, 512]) for j in range(2)]; Rqn = [Res("qn0"), Res("qn1")]
                          vf = [sb(ph, "vf%d" % j, [128, 128]) for j in range(2)]; Rvf = [Res("vf0"), Res("vf1")]
                          qkb = [sb(ph, "qkb%d" % j, [128, 512], BF16) for j in range(2)]; Rqkb = [Res("qkb0"), Res("qkb1")]
                          sq = sb(ph, "sq", [128, 512]); Rsq = Res("sq")
                          kb2 = [sb(ph, "kb2%d" % j, [128, 128], BF16) for j in range(2)]; Rkb2 = [Res("kb20"), Res("kb21")]
                          ssq = sb(ph, "ssq", [128, 8]); Rssq = Res("ssq")
                          g640 = sb(ph, "g640", [128, 768]); rc = sb(ph, "rc", [128, 8, 32]); rsn = sb(ph, "rsn", [128, 8, 32]); Rcst = Res("cst")
                          rt = [sb(ph, "rt%d" % j, [128, 8, 32]) for j in range(2)]; Rrt = [Res("rt0"), Res("rt1")]
                          win = Din["w_in"][l].rearrange("(k p) n -> p k n", p=128)
                          fw.dma("sp", g640[:], Din["g640"][l], writes=[Rcst])
                          fw.dma("sp", rc[:], Din["ropec"], writes=[Rcst])
                          fw.dma("sp", rsn[:], Din["ropes"], writes=[Rcst])
                          fw.dma("pool", KT[:, 0, 1024:1536], Din["ckT"][l], writes=[RKT])
                          fw.dma("pool", KT[0:64, 1, 1024:1536], Din["ckT"][l, 64:128, :], writes=[RKT])
                          fw.dma("pool", KT[64:128, 1, 1024:1536], Din["ckT"][l, 0:64, :], writes=[RKT])
                          OP("dve", lambda e: e.memset(Vsb[:].rearrange("p a b c d -> p (a b c d)"), 1.0), [], [RV])
                          cvv = Din["cv"][l].rearrange("(t p) (h d) -> p t h d", p=128, h=2)
                          for lay in range(2):
                              for kvh_ in range(2):
                                  fw.dma("pool", Vsb[:, 8:12, kvh_, lay, lay * 64:(lay + 1) * 64], cvv[:, :, kvh_, :], writes=[RV])
                          build_uT(uT, RuT, 0)
                          if stop == "ut": fw.dead = True
                          DBG("d_uT", uT[:].rearrange("p a b -> p (a b)"), RuT)

                          def norm_rope(i, pa, pr, nh, gap, outb, Routb, slot):
                              w_ = nh * 64
                              ACT(sq[:, 0:w_], pa, AF.Square, pr, [Rsq])
                              OP("dve", lambda e: e.reduce_sum(out=ssq[:, 0:nh], in_=sq[:, 0:w_].rearrange("p (h d) -> p h d", d=64), axis=AX.X), [Rsq], [Rssq])
                              TS("dve", ssq[:, 0:nh], ssq[:, 0:nh], 1.0 / 64, ALU.mult, [Rssq], [Rssq], s2=EPS, op1=ALU.add)
                              ACT(ssq[:, 0:nh], ssq[:, 0:nh], AF.Sqrt, [Rssq], [Rssq])
                              OP("dve", lambda e: e.reciprocal(out=ssq[:, 0:nh], in_=ssq[:, 0:nh]), [Rssq], [Rssq])
                              qv = qn[slot][:, 0:w_]
                              TT("dve", qv.rearrange("p (h d) -> p h d", d=64), pa.rearrange("p (h d) -> p h d", d=64), bl(ssq[:, 0:nh], 64),
                                 ALU.mult, pr + [Rssq], [Rqn[slot]])
                              TT("dve", qv, qv, gap, ALU.mult, [Rqn[slot], Rcst], [Rqn[slot]])
                              if i >= 8:
                                  CP("act", outb, qv, [Rqn[slot]], [Routb])
                              else:
                                  x4 = qv.rearrange("p (h d two) -> p h d two", d=32, two=2)
                                  o4 = outb.rearrange("p (h d two) -> p h d two", d=32, two=2)
                                  x0, x1 = x4[:, :, :, 0], x4[:, :, :, 1]
                                  cc, ss_ = bm(rc[:, i, :], nh), bm(rsn[:, i, :], nh)
                                  r0, r1 = rt[0][:, 0:nh, :], rt[1][:, 0:nh, :]
                                  TT("dve", r0, x0, cc, ALU.mult, [Rqn[slot], Rcst], [Rrt[0]])
                                  TT("pool", r1, x1, ss_, ALU.mult, [Rqn[slot], Rcst], [Rrt[1]])
                                  TT("dve", o4[:, :, :, 0], r0, r1, ALU.subtract, Rrt, [Routb])
                                  TT("dve", r0, x0, ss_, ALU.mult, [Rqn[slot], Rcst], [Rrt[0]])
                                  TT("pool", r1, x1, cc, ALU.mult, [Rqn[slot], Rcst], [Rrt[1]])
                                  TT("dve", o4[:, :, :, 1], r0, r1, ALU.add, Rrt, [Routb])

                          CH[0] = 'a' in os.environ.get('CHP', 'mabfFGMLUS')
                          W, RW = load_w([(lambda t: t[:, 0:4096].rearrange("p (k n) -> p k n", k=8), win[:, :, 0:512])])
                          Wv = W[:, 0:4096].rearrange("p (k n) -> p k n", k=8)
                          for i in range(NT):
                              bk = 2 + (i % 2); sl = i % 2
                              for k in range(8):
                                  MM(bank(bk), uT[:, k, i * 128:(i + 1) * 128], Wv[:, k, :], k == 0, k == 7, [RuT[i], RW], [PB[bk]])
                              norm_rope(i, bank(bk), [PB[bk]], 8, g640[:, 0:512], qkb[sl][:, :], Rqkb[sl], sl)
                              for j in range(4):
                                  src = qkb[sl][:, j * 128:(j + 1) * 128]
                                  OP("pe", lambda e: e.transpose(pst[:, j * 128:(j + 1) * 128], src, ident_b[:]), [Rqkb[sl], Rid], [PT])
                              CP("act", QT[:, :, i * 128:(i + 1) * 128], pst[:, 0:512].rearrange("p (j c) -> p j c", j=4), [PT], RQ)
                          if stop == "passA": fw.dead = True
                          CH[0] = 'b' in os.environ.get('CHP', 'mabfFGMLUS')
                          W, RW = load_w([(lambda t: t[:, 0:2048].rearrange("p (k n) -> p k n", k=8), win[:, :, 512:768])])
                          Wv = W[:, 0:2048].rearrange("p (k n) -> p k n", k=8)
                          for i in range(NT):
                              bk = 2 + (i % 2); sl = i % 2
                              for k in range(8):
                                  MM(bank(bk)[:, 0:256], uT[:, k, i * 128:(i + 1) * 128], Wv[:, k, :], k == 0, k == 7, [RuT[i], RW], [PB[bk]])
                              norm_rope(i, bank(bk)[:, 0:256], [PB[bk]], 4, g640[:, 512:768], qkb[sl][:, 0:256], Rqkb[sl], sl)
                              kt = i if i < 8 else 12 + (i - 8)
                              for lay in range(2):
                                  CP("act", Vsb[:, kt, :, lay, lay * 64:(lay + 1) * 64], bank(bk)[:, 128:256].rearrange("p (h d) -> p h d", h=2), [PB[bk]], [RV])
                              if i >= 8:
                                  p_, t_ = (i - 8) // 2, (i - 8) % 2
                                  CP("act", vf[sl][:], bank(bk)[:, 128:256], [PB[bk]], [Rvf[sl]])
                                  fw.dma("sp", Dout["nk"][p_, l, t_ * 128:(t_ + 1) * 128, :], qn[sl][:, 0:128], reads=[Rqn[sl]], writes=[Rdbg])
                                  fw.dma("sp", Dout["nv"][p_, l, t_ * 128:(t_ + 1) * 128, :], vf[sl][:], reads=[Rvf[sl]], writes=[Rdbg])
                              CP("act", kb2[sl][:, 0:64], qkb[sl][:, 64:128], [Rqkb[sl]], [Rkb2[sl]])
                              CP("act", kb2[sl][:, 64:128], qkb[sl][:, 0:64], [Rqkb[sl]], [Rkb2[sl]])
                              OP("pe", lambda e: e.transpose(pst[:, 512:640], qkb[sl][:, 0:128], ident_b[:]), [Rqkb[sl], Rid], [PT])
                              OP("pe", lambda e: e.transpose(pst[:, 640:768], kb2[sl][:, :], ident_b[:]), [Rkb2[sl], Rid], [PT])
                              kc = i * 128 if i < 8 else 1536 + (i - 8) * 128
                              CP("act", KT[:, 0, kc:kc + 128], pst[:, 512:640], [PT], [RKT])
                              CP("act", KT[:, 1, kc:kc + 128], pst[:, 640:768], [PT], [RKT])
                          if stop == "passB": fw.dead = True
                          CH[0] = 'f' in os.environ.get('CHP', 'mabfFGMLUS')
                          for wb in range(2):
                              W, RW = load_w([(lambda t: t[:, 0:4096].rearrange("p (k n) -> p k n", k=8), win[:, :, 768 + wb * 512:1280 + wb * 512])])
                              Wv = W[:, 0:4096].rearrange("p (k n) -> p k n", k=8)
                              for m4 in range(4):
                                  m = wb * 4 + m4
                                  for tb, (c0, cn) in enumerate(BLKS):
                                      bk = (0, 1, 6)[(m * 3 + tb) % 3]
                                      for k in range(8):
                                          MM(bank(bk), Wv[:, k, m4 * 128:(m4 + 1) * 128], uT[:, k, c0:c0 + cn], k == 0, k == 7,
                                             RuT[c0 // 128:(c0 + cn) // 128] + [RW], [PB[bk]])
                                      CP("act", sfT[:, m, c0:c0 + cn], bank(bk), [PB[bk]], [RsfT[m]])
                          DBG("d_sfT", sfT[:].rearrange("p a b -> p (a b)"), RsfT)
                          fw.barrier()
                      if stop == "inproj": fw.dead = True
                      DBG("d_QT", QT[:].rearrange("p a b -> p (a b)"), RQ)
                      DBG("d_KT", KT[:].rearrange("p a b -> p (a b)"), [RKT])
                      CH[0] = 't' in os.environ.get('CHP', 'mabfFGMLUS')
                      with contextlib.ExitStack() as ph:
                          fw.rec = []
                          ssm_param_pipeline(ph)
                          pipe_items = fw.rec if fw.rec is not None else []; fw.rec = None
                          per_unit = (len(pipe_items) + 23) // 24
                          Eb = [sb(ph, "Eb%d" % j, [128, 512], BF16) for j in range(3)]; REb = [Res("Eb%d" % j) for j in range(3)]
                          rsb = [sb(ph, "rsb%d" % j, [128, 512]) for j in range(2)]; Rrsb = [Res("rsb0"), Res("rsb1")]
                          cnt = 0; cnt2 = 0
                          for si, (t0, Ls) in enumerate(SEQS):
                              if si == 0:
                                  qblks = [(0, 512), (512, 512)]; nkt = 12; kt0 = 0; kc0 = 0
                              else:
                                  qblks = [(t0, 256)]; nkt = 2; kt0 = 12 + 2 * (si - 1); kc0 = 1536 + 256 * (si - 1)
                              for h in range(8):
                                  j = h // 2; hh = h % 2; kvh = h // 4; var = 0 if kvh == hh else 1
                                  pv, sm = (slice(0, 64), slice(64, 128)) if hh == 0 else (slice(64, 128), slice(0, 64))
                                  for (q0, qn) in qblks:
                                      pvb = 3 + (cnt2 % 2); cnt2 += 1
                                      def S_mm(kt, sbk):
                                          MM(bank(sbk)[:, 0:qn], KT[hh * 64:(hh + 1) * 64, var, kc0 + kt * 128:kc0 + (kt + 1) * 128],
                                             QT[hh * 64:(hh + 1) * 64, j, q0:q0 + qn], True, True, [RKT, RQ[j]], [PB[sbk]], chain=False)
                                      slots = [(cnt + i_) % 3 for i_ in range(nkt)]; cnt += nkt
                                      S_mm(0, slots[0])
                                      for kt in range(nkt):
                                          sbk = slots[kt]; eb = sbk
                                          if kt + 1 < nkt:
                                              S_mm(kt + 1, slots[kt + 1])
                                          ACT(Eb[eb][:, 0:qn], bank(sbk)[:, 0:qn], AF.Exp, [PB[sbk]], [REb[eb]], scale=0.125)
                                          MM(bank(pvb)[:, 0:qn], Vsb[:, kt0 + kt, kvh, hh, :], Eb[eb][:, 0:qn], kt == 0, kt == nkt - 1,
                                             [RV, REb[eb]], [PB[pvb]], chain=False)
                                      rb = cnt2 % 2
                                      ACT(rsb[rb][sm, 0:qn], bank(pvb)[sm, 0:qn], AF.Ln, [PB[pvb]], [Rrsb[rb]])
                                      ACT(rsb[rb][sm, 0:qn], rsb[rb][sm, 0:qn], AF.Exp, [Rrsb[rb]], [Rrsb[rb]], scale=-1.0)
                                      TT("dve", attnT[pv, j, q0:q0 + qn], bank(pvb)[pv, 0:qn], rsb[rb][sm, 0:qn], ALU.mult,
                                         [PB[pvb], Rrsb[rb]], [RattnT[j]])
                                      fw.replay(pipe_items[:per_unit]); del pipe_items[:per_unit]
                          fw.replay(pipe_items); del pipe_items[:]
                          fw.barrier()
                  DBG("d_attnT", attnT[:].rearrange("p a b -> p (a b)"), RattnT)
                  CH[0] = 'F' in os.environ.get('CHP', 'mabfFGMLUS')
                  with contextlib.ExitStack() as L3:
                    with contextlib.ExitStack() as ph:
                        cs128 = sb(ph, "cs128", [128, 256], BF16); cl = sb(ph, "cl", [128, 8, 1024], BF16); sl_ = sb(ph, "sl", [128, 8, 1024], BF16)
                        clp = sb(ph, "clp", [128, 2, 256], BF16); slp = sb(ph, "slp", [128, 2, 256], BF16); Rft = Res("ftab")
                        Pfm = sb(ph, "Pfm", [128, NT, 1024], BF16); RPfm = [Res("Pfm%d" % i) for i in range(NT)]
                        fw.dma("pool", cs128[:], Din["cs128"], writes=[Rft])
                        for k in range(8):
                            fw.dma("pool", cl[:, k, :], Din["cl1024"][:, k, :], writes=[Rft])
                            fw.dma("pool", sl_[:, k, :], Din["sl1024"][:, k, :], writes=[Rft])
                        fw.dma("pool", clp[:], Din["cl256"], writes=[Rft])
                        fw.dma("pool", slp[:], Din["sl256"], writes=[Rft])
                        for i in range(NT):
                            b0 = 2 + 2 * (i % 2)
                            for g in range(4):
                                MM(bank(b0 + g // 2)[:, (g % 2) * 256:(g % 2 + 1) * 256], sfT[:, 4 + g, i * 128:(i + 1) * 128], cs128[:], True, True,
                                   [RsfT[4 + g], Rft], [PB[b0 + g // 2]])
                            CP("act", Pfm[:, i, :], bank(b0, 2), [PB[b0], PB[b0 + 1]], [RPfm[i]])
                        cnt = 0
                        for si, (t0, Ls) in enumerate(SEQS):
                            ntl = Ls // 128; tb0 = t0 // 128
                            tc_, ts_ = (cl, sl_) if si == 0 else (clp, slp)
                            for g in range(4):
                                for lb in range(0, Ls, 512):
                                    n = min(512, Ls - lb)
                                    bk = (0, 1, 6)[cnt % 3]; cnt += 1
                                    for k in range(ntl):
                                        MM(bank(bk)[:, 0:n], Pfm[:, tb0 + k, g * 256:g * 256 + 128], tc_[:, k, lb:lb + n], k == 0, False,
                                           [RPfm[tb0 + k], Rft], [PB[bk]])
                                        MM(bank(bk)[:, 0:n], Pfm[:, tb0 + k, g * 256 + 128:g * 256 + 256], ts_[:, k, lb:lb + n], False, k == ntl - 1,
                                           [RPfm[tb0 + k], Rft], [PB[bk]])
                                    CP("act", fourT[:, g, t0 + lb:t0 + lb + n], bank(bk)[:, 0:n], [PB[bk]], [RfourT[g]])
                        fw.barrier()
                    DBG("d_fourT", fourT[:].rearrange("p a b -> p (a b)"), RfourT)
                    if stop == "four": fw.dead = True
                    CH[0] = 'S' in os.environ.get('CHP', 'mabfFGMLUS')
                    with contextlib.ExitStack() as S:
                        Hb = sb(S, "Hb", [128, 2, 2, 16, NS], BF16)
                        h0 = sb(S, "h0", [128, 2, 2, 16]); stg = sb(S, "stg", [128, 16, 2])
                        RHb = Res("Hb"); Rstg = Res("stg")
                        fw.dma("sp", h0[:], Din["h0"][:, l], writes=[RS])
                        for d in range(2):
                            with contextlib.ExitStack() as SD:
                                lam = lamD[d]; ff = ffD[d]
                                A = [sb(SD, "A%d" % r, [128, 16, NS]) for r in range(2)]
                                for r in range(2):
                                    OP("dve", lambda e: e.memset(A[r][:].rearrange("p a b -> p (a b)"), 0.0), [RS], [RS])
                                with contextlib.ExitStack() as SW:
                                    W = [sb(SW, "W%d" % r, [128, 16, 128]) for r in range(2)]
                                    WT2 = [[sb(SW, "WT%d_%d" % (z, r), [128, 16, 128], BF16) for r in range(2)] for z in range(2)]
                                    wm = [sb(SW, "wm%d" % r, [128, 8, 128]) for r in range(3)]
                                    BT = [sb(SW, "BT%d" % r, [128, 128]) for r in range(2)]
                                    Lw = [sb(SW, "Lw%d" % r, [128, 128]) for r in range(2)]
                                    RWT2 = [Res("WTa"), Res("WTb")]
                                    def build_W(q):
                                        fs = slice(16 + q * 128, 16 + (q + 1) * 128)
                                        for r in range(2):
                                            fw.dma("sp", BT[r][:], Din["BbdT"][l, d, r, :, q, :], writes=[RS])
                                            ACT(Lw[r][:], lam[r][:, fs], AF.Identity, [RS], [RS])
                                        cmul(W[0][:, 0, :], W[1][:, 0, :], ff[0][:, fs], ff[1][:, fs], BT[0][:], BT[1][:], wm[0][:, 0, :], wm[1][:, 0, :])
                                        for w in (1, 2, 4, 8):
                                            cmul(W[0][:, w:2 * w, :], W[1][:, w:2 * w, :], bm(Lw[0][:], w), bm(Lw[1][:], w),
                                                 W[0][:, 0:w, :], W[1][:, 0:w, :], wm[0][:, 0:w, :], wm[1][:, 0:w, :])
                                            if w < 8:
                                                TT("dve", wm[0][:, 0, :], Lw[0][:], Lw[0][:], ALU.mult, [RS], [RS])
                                                TT("dve", wm[1][:, 0, :], Lw[1][:], Lw[1][:], ALU.mult, [RS], [RS])
                                                TT("dve", wm[2][:, 0, :], Lw[0][:], Lw[1][:], ALU.mult, [RS], [RS])
                                                TT("dve", Lw[0][:], wm[0][:, 0, :], wm[1][:, 0, :], ALU.subtract, [RS], [RS])
                                                TS("dve", Lw[1][:], wm[2][:, 0, :], 2.0, ALU.mult, [RS], [RS])
                                        for r in range(2):
                                            ACT(WT2[q % 2][r][:].rearrange("p a b -> p (a b)"), W[r][:].rearrange("p a b -> p (a b)"), AF.Identity, [RS], [RWT2[q % 2]])
                                    def run_S(q):
                                        WTq = WT2[q % 2]; RWTq = RWT2[q % 2]
                                        Sps = bank(0, 2).rearrange("p (b r c) -> p b r c", b=4, r=2)
                                        for b in range(4):
                                            sv = sfT[32 * b:32 * b + 32, q, :].rearrange("p (c j) -> p c j", j=16)
                                            for r in range(2):
                                                for k in range(16):
                                                    j = 15 - k if d == 0 else k
                                                    MM(Sps[:, b, r, 0:96], WTq[r][32 * b:32 * b + 32, k, :], sv[:, :, j], k == 0, k == 15,
                                                       [RWTq, RsfT[q]], [PB[0], PB[1]], tp=(32 * b, 0))
                                        for r in range(2):
                                            for s_i in range(3):
                                                n = SCH[s_i]; c0 = (0, 64, 80)[s_i]; lo = SOFF[s_i] + (1 if d == 0 else 0)
                                                ACT(A[r][:, 4 * q:4 * q + 4, lo:lo + n], Sps[:, :, r, c0:c0 + n], AF.Identity, [PB[0], PB[1]], [RS])
                                    build_W(0)
                                    for q in range(4):
                                        if q + 1 < 4:
                                            build_W(q + 1)
                                        run_S(q)
                                for r in range(2):
                                    ACT(A[r][:, :, 0 if d == 0 else 64], h0[:, d, r, :], AF.Identity, [RS], [RS])
                                with contextlib.ExitStack() as SC:
                                    sm = [sb(SC, "sm%d" % r, [128, 16, NS]) for r in range(3)]
                                    for m in range(7):
                                        dd = 1 << m
                                        groups = []
                                        if dd < 65:
                                            groups.append((lambda t_, lo, hi: t_[:, :, lo:hi], 65, None))
                                        if dd < 17:
                                            groups.append((lambda t_, lo, hi: t_[:, :, 65:99].rearrange("p g (s c) -> p g s c", s=2)[:, :, :, lo:hi], 17, 2))
                                        for view, n, ns in groups:
                                            cntn = n - dd
                                            (dlo, dhi, slo, shi) = (dd, n, 0, cntn) if d == 0 else (0, cntn, dd, n)
                                            if ns is None:
                                                Lr = bl(LH[0][:, d, :, m], cntn); Li = bl(LH[1][:, d, :, m], cntn)
                                                tv = lambda t_: t_[:, :, 0:cntn]
                                            else:
                                                Lr = LH[0][:, d, :, m].unsqueeze(2).unsqueeze(3).to_broadcast([128, 16, 2, cntn])
                                                Li = LH[1][:, d, :, m].unsqueeze(2).unsqueeze(3).to_broadcast([128, 16, 2, cntn])
                                                tv = lambda t_: t_[:, :, 0:2 * cntn].rearrange("p g (s c) -> p g s c", s=2)
                                            sr, si = view(A[0], slo, shi), view(A[1], slo, shi)
                                            dr, di = view(A[0], dlo, dhi), view(A[1], dlo, dhi)
                                            m1, m2, m3 = tv(sm[0]), tv(sm[1]), tv(sm[2])
                                            TT("dve", m1, sr, Lr, ALU.mult, [RS], [RS]); TT("dve", m2, si, Li, ALU.mult, [RS], [RS])
                                            TT("dve", m1, m1, m2, ALU.subtract, [RS], [RS])
                                            TT("dve", m2, si, Lr, ALU.mult, [RS], [RS]); TT("dve", m3, sr, Li, ALU.mult, [RS], [RS])
                                            TT("dve", m2, m2, m3, ALU.add, [RS], [RS])
                                            TT("dve", dr, dr, m1, ALU.add, [RS], [RS]); TT("dve", di, di, m2, ALU.add, [RS], [RS])
                                for r in range(2):
                                    ACT(Hb[:, d, r, :, :], A[r][:], AF.Identity, [RS], [RHb])
                                for s_i in (1, 2):
                                    idx = SOFF[s_i] + (16 if d == 0 else 0)
                                    for r in range(2):
                                        ACT(stg[:, :, r], A[r][:, :, idx], AF.Identity, [RS], [Rstg])
                                    fw.dma("sp", Dout["nst"][s_i - 1, l, d].rearrange("(pair g2) p r -> (g2 p) pair r", g2=2), stg[:], reads=[Rstg], writes=[Rdbg])
                                DBG("d_A%d" % d, A[0][:].rearrange("p a b -> p (a b)"), [RS])
                                fw.barrier()
                        with contextlib.ExitStack() as SQ:
                            CL2 = [[[sb(SQ, "CL%d%d%d" % (z, d, r), [128, 4, 17, 32], BF16) for r in range(2)] for d in range(2)] for z in range(2)]
                            Kblk = [sb(SQ, "Kblk%d" % d, [128, 16, 128], BF16) for d in range(2)]
                            Bb2 = [[[sb(SQ, "Bb%d%d%d" % (z, d, r), [128, 4, 32], BF16) for r in range(2)] for d in range(2)] for z in range(2)]
                            Dbl = sb(SQ, "Dbl", [128, 4, 128], BF16)
                            cm = [sb(SQ, "cm%d" % j, [128, 2, 17, 32]) for j in range(2)]
                            Cl = [sb(SQ, "Cl%d" % r, [128, 4, 32]) for r in range(2)]
                            Bl = [sb(SQ, "Bl%d" % r, [128, 4, 32]) for r in range(2)]
                            bt = [sb(SQ, "bt%d" % r, [128, 4, 32]) for r in range(2)]
                            gt = [sb(SQ, "gt%d" % r, [128, 512]) for r in range(2)]; Rgt = [Res("gt0"), Res("gt1")]
                            RCL2 = [Res("CLa"), Res("CLb")]; RK = Res("Kblk"); RDbl = Res("Dbl")
                            ydb = sb(SQ, "ydb", [128, 512]); Rydb = Res("ydb")
                            fw.dma("pool", Dbl[:], Din["Dblk"][l], writes=[RDbl])

                            def build_CL(q):
                                z = q % 2; CL = CL2[z]; Bb = Bb2[z]; RCL = RCL2[z]
                                ps4 = slice(4 * q, 4 * q + 4)
                                for d in range(2):
                                    for r in range(2):
                                        fw.dma("sp", Cl[r][:], Din["CbdP"][l, d, r, :, ps4, :], writes=[RS])
                                        fw.dma("sp", Bl[r][:], Din["BbdP"][l, d, r, :, ps4, :], writes=[RS])
                                    for hp in range(2):
                                        pp = slice(2 * hp, 2 * hp + 2); pq = slice(4 * q + 2 * hp, 4 * q + 2 * hp + 2)
                                        Pr = PowP[0][:, d, pq, :].unsqueeze(3).to_broadcast([128, 2, 17, 32])
                                        Pi = PowP[1][:, d, pq, :].unsqueeze(3).to_broadcast([128, 2, 17, 32])
                                        Cr = Cl[0][:, pp, :].unsqueeze(2).to_broadcast([128, 2, 17, 32])
                                        Ci = Cl[1][:, pp, :].unsqueeze(2).to_broadcast([128, 2, 17, 32])
                                        TT("dve", cm[0][:], Pr, Cr, ALU.mult, [RS], [RS]); TT("dve", cm[1][:], Pi, Ci, ALU.mult, [RS], [RS])
                                        TT("dve", CL[d][0][:, pp, :, :], cm[0][:], cm[1][:], ALU.subtract, [RS], [RCL])
                                        TT("dve", cm[0][:], Pr, Ci, ALU.mult, [RS], [RS]); TT("dve", cm[1][:], Pi, Cr, ALU.mult, [RS], [RS])
                                        TT("dve", cm[0][:], cm[0][:], cm[1][:], ALU.add, [RS], [RS])
                                        TS("dve", CL[d][1][:, pp, :, :], cm[0][:], -1.0, ALU.mult, [RS], [RCL])
                                    fr = bl(fP[0][:, d, ps4], 32); fi = bl(fP[1][:, d, ps4], 32)
                                    TT("dve", bt[0][:], fr, Bl[0][:], ALU.mult, [RS], [RS]); TT("dve", bt[1][:], fi, Bl[1][:], ALU.mult, [RS], [RS])
                                    TT("dve", Bb[d][0][:], bt[0][:], bt[1][:], ALU.subtract, [RS], [RCL])
                                    TT("dve", bt[0][:], fr, Bl[1][:], ALU.mult, [RS], [RS]); TT("dve", bt[1][:], fi, Bl[0][:], ALU.mult, [RS], [RS])
                                    TT("dve", Bb[d][1][:], bt[0][:], bt[1][:], ALU.add, [RS], [RCL])

                            def build_K(q):
                                z = q % 2; CL = CL2[z]; Bb = Bb2[z]; RCL = RCL2[z]
                                for d in range(2):
                                    kb = d
                                    for b in range(4):
                                        MM(bank(kb)[32 * b:32 * b + 32, :], Bb[d][0][:, b, :], CL[d][0][:, b, 0:16, :].rearrange("p t n -> p (t n)"), True, False,
                                           [RCL], [PB[kb]], tp=(0, 32 * b))
                                        MM(bank(kb)[32 * b:32 * b + 32, :], Bb[d][1][:, b, :], CL[d][1][:, b, 0:16, :].rearrange("p t n -> p (t n)"), False, True,
                                           [RCL], [PB[kb]], tp=(0, 32 * b))
                                    for cb in range(4):
                                        TS("dve", Kblk[d][:, :, 32 * cb:32 * cb + 32], bank(kb).rearrange("p (t n) -> p t n", n=32), maskP[:, cb:cb + 1], ALU.mult,
                                           [PB[kb], Rmask], [RK])

                            def run_y(q):
                                z = q % 2; CL = CL2[z]; RCL = RCL2[z]
                                for tb, (c0, cn) in enumerate(BLKS):
                                    yb = 3 + tb
                                    Yb = bank(yb).rearrange("p (c j) -> p c j", j=16)
                                    sblk = sfT[:, q, c0:c0 + cn].rearrange("p (c j) -> p c j", j=16)
                                    MM(bank(yb), Dbl[:, q, :], sfT[:, q, c0:c0 + cn], True, False, [RDbl, RsfT[q]], [PB[yb]])
                                    for d in range(2):
                                        for t in range(16):
                                            if d == 0:
                                                o_, r_ = Yb[:, :, t:16], sblk[:, :, 0:16 - t]
                                            else:
                                                o_, r_ = Yb[:, :, 0:16 - t], sblk[:, :, t:16]
                                            MM(o_, Kblk[d][:, t, :], r_, False, False, [RK, RsfT[q]], [PB[yb]])
                                    GC = os.environ.get('GC', '1') == '1'
                                    for d in range(2):
                                        for j in range(16):
                                            tt = j + 1 if d == 0 else 16 - j
                                            sh = 0 if d == 0 else 1
                                            for r in range(2):
                                                for b in range(4):
                                                    gfirst = (b == 0 and r == 0 and j == 0 and d == 0)
                                                    if tb < 2:
                                                        rhs = Hb[:, d, r, 4 * q + b, 32 * tb + sh:32 * tb + sh + 32]
                                                        o_ = Yb[32 * b:32 * b + 32, :, j]
                                                    else:
                                                        rhs = Hb[:, d, r, 4 * q + b, 65:99].rearrange("p (s c) -> p s c", s=2)[:, :, sh:sh + 16]
                                                        o_ = bank(yb).rearrange("p (s c j) -> p s c j", s=2, j=16)[32 * b:32 * b + 32, :, :, j]
                                                    last = (d == 1 and b == 3 and j == 15 and r == 1)
                                                    MM(o_, CL[d][r][:, b, tt, :], rhs, False, last, [RCL, RHb], [PB[yb]], tp=(0, 32 * b),
                                                       chain=(("gfirst" if gfirst else "gnext") if GC else False))
                                    if ("d_y%d_%d" % (q, tb)) in Ddbg:
                                        ACT(ydb[:], bank(yb), AF.Identity, [PB[yb]], [Rydb])
                                        DBG("d_y%d_%d" % (q, tb), ydb[:], [Rydb])
                                    g_ = tb % 2
                                    ACT(gt[g_][:], bank(yb), AF.Square, [PB[yb]], [Rgt[g_]])
                                    TS("dve", gt[g_][:], gt[g_][:], 0.044715, ALU.mult, [Rgt[g_]], [Rgt[g_]], s2=1.0, op1=ALU.add)
                                    TT("dve", gt[g_][:], gt[g_][:], bank(yb), ALU.mult, [Rgt[g_], PB[yb]], [Rgt[g_]])
                                    ACT(gt[g_][:], gt[g_][:], AF.Sigmoid, [Rgt[g_]], [Rgt[g_]], scale=1.5957691216)
                                    TT("dve", sfT[:, q, c0:c0 + cn], gt[g_][:], bank(yb), ALU.mult, [Rgt[g_], PB[yb]], [RsfT[q]])
                            build_CL(0); build_K(0)
                            for q in range(4):
                                if q + 1 < 4:
                                    build_CL(q + 1)
                                run_y(q)
                                if q + 1 < 4:
                                    build_K(q + 1)
                            fw.barrier()
                    Lp.close()
                    ssmT = sb(L3, "ssmT", [128, 4, TOK], BF16); RssmT = [Res("ssmT%d" % j) for j in range(4)]
                    CH[0] = 'G' in os.environ.get('CHP', 'mabfFGMLUS')
                    with contextlib.ExitStack() as ph:
                        alloc_ws(ph)
                        bgl = sb(ph, "bgl", [128, 4]); Rbgl = Res("bgl"); sg = [sb(ph, "sg%d" % j, [128, 512]) for j in range(2)]; Rsg = [Res("sg0"), Res("sg1")]
                        fw.dma("sp", bgl[:], Din["bglu"][:, l, :], writes=[Rbgl])
                        Wg, RWg = load_w([(lambda t: t[:, 0:2048].rearrange("p (k n) -> p k n", k=4), Din["w_glu"][l].rearrange("(k p) n -> p k n", p=128))])
                        Wgv = Wg[:, 0:2048].rearrange("p (k n) -> p k n", k=4)
                        cnt = 0
                        for m in range(4):
                            for tb, (c0, cn) in enumerate(BLKS):
                                bk = (0, 1, 2)[cnt % 3]; sj = cnt % 2; cnt += 1
                                for k in range(4):
                                    MM(bank(bk), Wgv[:, k, m * 128:(m + 1) * 128], sfT[:, k, c0:c0 + cn], k == 0, k == 3, [RWg, RsfT[k]], [PB[bk]])
                                ACT(sg[sj][:], bank(bk), AF.Sigmoid, [PB[bk], Rbgl], [Rsg[sj]], bias=bgl[:, m:m + 1])
                                TT("dve", ssmT[:, m, c0:c0 + cn], sg[sj][:], sfT[:, m, c0:c0 + cn], ALU.mult, [Rsg[sj], RsfT[m]], [RssmT[m]])
                        fw.barrier()
                    DBG("d_gy", sfT[:, 0:4, :].rearrange("p a b -> p (a b)"), RsfT[0:4])
                    DBG("d_ssmT", ssmT[:].rearrange("p a b -> p (a b)"), RssmT)
                    if stop == "ssm": fw.dead = True
                    CH[0] = 'M' in os.environ.get('CHP', 'mabfFGMLUS')
                    with contextlib.ExitStack() as M:
                        mergedT = sb(M, "mergedT", [128, 8, TOK], BF16); Rmg = [Res("mg%d" % j) for j in range(8)]
                        with contextlib.ExitStack() as ph:
                            alloc_ws(ph)
                            uT = sb(ph, "uT", [128, 8, TOK], BF16); RuT = [Res("uT%d" % i) for i in range(NT)]
                            gs = [sb(ph, "gs%d" % j, [128, 512]) for j in range(3)]; Rgs = [Res("gs%d" % j) for j in range(3)]
                            ac = [sb(ph, "ac%d" % j, [128, 512]) for j in range(2)]; Rac = [Res("ac0"), Res("ac1")]
                            build_uT(uT, RuT, 0)
                            win = Din["w_in"][l].rearrange("(k p) n -> p k n", p=128)
                            wbr = [Din[nm][l].rearrange("(k p) n -> p k n", p=128) for nm in ("w_br_attn", "w_br_ssm", "w_br_four")]
                            srcs = [(attnT, RattnT), (ssmT, RssmT), (fourT, RfourT)]
                            for c in range(8):
                                Wg, RWg = load_w([((lambda t, g=g: t[:, g * 1024:(g + 1) * 1024].rearrange("p (k n) -> p k n", k=8)),
                                                   win[:, :, 1792 + g * 1024 + c * 128:1792 + g * 1024 + (c + 1) * 128]) for g in range(3)])
                                Wb, RWb = load_w([((lambda t, x=x: t[:, x * 512:(x + 1) * 512].rearrange("p (k n) -> p k n", k=4)),
                                                   wbr[x][:, :, c * 128:(c + 1) * 128]) for x in range(3)])
                                for tb, (c0, cn) in enumerate(BLKS):
                                    for g in range(3):
                                        Wgv = Wg[:, g * 1024:(g + 1) * 1024].rearrange("p (k n) -> p k n", k=8)
                                        for k in range(8):
                                            MM(bank(g), Wgv[:, k, :], uT[:, k, c0:c0 + cn], k == 0, k == 7, RuT[c0 // 128:(c0 + cn) // 128] + [RWg], [PB[g]])
                                        ACT(gs[g][:], bank(g), AF.Sigmoid, [PB[g]], [Rgs[g]])
                                    for x in range(3):
                                        Wbv = Wb[:, x * 512:(x + 1) * 512].rearrange("p (k n) -> p k n", k=4)
                                        st_, Rst = srcs[x]
                                        for k in range(4):
                                            MM(bank(3 + x), Wbv[:, k, :], st_[:, k, c0:c0 + cn], k == 0, k == 3, [Rst[k], RWb], [PB[3 + x]])
                                    TT("dve", ac[0][:], gs[0][:], bank(3), ALU.mult, [Rgs[0], PB[3]], [Rac[0]])
                                    TT("dve", ac[1][:], gs[1][:], bank(4), ALU.mult, [Rgs[1], PB[4]], [Rac[1]])
                                    TT("dve", ac[0][:], ac[0][:], ac[1][:], ALU.add, Rac, [Rac[0]])
                                    TT("dve", ac[1][:], gs[2][:], bank(5), ALU.mult, [Rgs[2], PB[5]], [Rac[1]])
                                    TT("dve", mergedT[:, c, c0:c0 + cn], ac[0][:], ac[1][:], ALU.add, Rac, [Rmg[c]])
                            fw.barrier()
                        DBG("d_mergedT", mergedT[:].rearrange("p a b -> p (a b)"), Rmg)
                        if stop == "merge": fw.dead = True
                        layer_norm_residual(l, 0, 8, lambda i, k: mergedT[:, k, i * 128:(i + 1) * 128], lambda i, k: [Rmg[k]],
                                            Din["w_out"][l].rearrange("(k p) n -> p k n", p=128))
              if "d_x1" in Ddbg:
                  for i in range(NT):
                      fw.dma("sp", Ddbg["d_x1"][i * 128:(i + 1) * 128, :], X[:, i, :], reads=[RX[i]], writes=[Rdbg])
              if stop == "ln1": fw.dead = True
              CH[0] = 'U' in os.environ.get('CHP', 'mabfFGMLUS')
              with contextlib.ExitStack() as Fs:
                  actT = sb(Fs, "actT", [128, 22, TOK], BF16); Ract = [Res("act%d" % j) for j in range(22)]
                  with contextlib.ExitStack() as ph:
                      alloc_ws(ph)
                      uT = sb(ph, "uT", [128, 8, TOK], BF16); RuT = [Res("uT%d" % i) for i in range(NT)]
                      cvp = sb(ph, "cvp", [128, 44, 4]); Rcv = Res("cvp")
                      hc = [sb(ph, "hc%d" % j, [128, TOK]) for j in range(2)]; Rhc = [Res("hc0"), Res("hc1")]
                      fw.dma("sp", cvp[:], Din["convp"][:, l], writes=[Rcv])
                      build_uT(uT, RuT, 1)
                      wup = Din["w_up"][l].rearrange("(k p) n -> p k n", p=128)
                      for ch in range(44):
                          if ch % 4 == 0:
                              W, RW = load_w([(lambda t: t[:, 0:4096].rearrange("p (k n) -> p k n", k=8), wup[:, :, ch * 128:ch * 128 + 512])])
                              Wv = W[:, 0:4096].rearrange("p (k n) -> p k n", k=8)
                          cw = ch % 4; pb0 = 3 * (ch % 2); hj = ch % 2
                          prs = [PB[pb0], PB[pb0 + 1], PB[pb0 + 2]]
                          for tb, (c0, cn) in enumerate(BLKS):
                              for k in range(8):
                                  MM(bank(pb0 + tb), Wv[:, k, cw * 128:(cw + 1) * 128], uT[:, k, c0:c0 + cn], k == 0, k == 7,
                                     RuT[c0 // 128:(c0 + cn) // 128] + [RW], [prs[tb]])
                          hp = bank(pb0, 3); h_ = hc[hj]
                          ACT(h_[:], hp, AF.Identity, prs + [Rcv], [Rhc[hj]], scale=cvp[:, ch, 1:2], bias=cvp[:, ch, 3:4])
                          def stt(o_, i0, sc, i1):
                              OP("dve", lambda e: e.scalar_tensor_tensor(out=o_, in0=i0, scalar=sc, in1=i1, op0=ALU.mult, op1=ALU.add),
                                 prs + [Rcv, Rhc[hj]], [Rhc[hj]])
                          stt(h_[:, 1:1024], hp[:, 0:1023], cvp[:, ch, 0:1], h_[:, 1:1024])
                          stt(h_[:, 0:1023], hp[:, 1:1024], cvp[:, ch, 2:3], h_[:, 0:1023])
                          h3 = h_[:, 1024:1536].rearrange("p (s c) -> p s c", s=2); p3 = hp[:, 1024:1536].rearrange("p (s c) -> p s c", s=2)
                          stt(h3[:, :, 1:256], p3[:, :, 0:255], cvp[:, ch, 0:1], h3[:, :, 1:256])
                          stt(h3[:, :, 0:255], p3[:, :, 1:256], cvp[:, ch, 2:3], h3[:, :, 0:255])
                          if ch < 22:
                              ACT(actT[:, ch, :], h_[:], AF.Silu, [Rhc[hj]], [Ract[ch]])
                          else:
                              TT("dve", actT[:, ch - 22, :], actT[:, ch - 22, :], h_[:], ALU.mult, [Ract[ch - 22], Rhc[hj]], [Ract[ch - 22]])
                      fw.barrier()
                  DBG("d_actT", actT[:].rearrange("p a b -> p (a b)"), Ract)
                  if stop == "ffn": fw.dead = True
                  layer_norm_residual(l, 1, 22, lambda i, k: actT[:, k, i * 128:(i + 1) * 128], lambda i, k: [Ract[k]],
                                      Din["w_down"][l].rearrange("(k p) n -> p k n", p=128))
        layers()
        fw.dead = False
        for i in range(NT):
            fw.dma("sp", Dout["y"][i * 128:(i + 1) * 128, :], X[:, i, :], reads=[RX[i]], writes=[Rdbg])
        fw.finish()
    return nc


def kernel(**inputs):
    inp = {k: np.asarray(v) for k, v in inputs.items()}
    S = prep_shared(inp)
    nc = build()
    in_maps = []
    for cid in range(8):
        m = dict(S); m.update(prep_core(inp, cid))
        in_maps.append(m)
    res = run_bass_kernel_spmd(nc, in_maps, core_ids=list(range(8)))
    y_p = np.zeros((16, 256, 1024), np.float32); y_s = np.zeros((8, 1024, 1024), np.float32)
    nk = np.zeros((16, 2, 256, 2, 64), np.float32); nv = np.zeros((16, 2, 256, 2, 64), np.float32)
    nst = np.zeros((16, 2, 2, 32, 64, 2), np.float32)
    for cid in range(8):
        r = res.results[cid]
        y_s[cid] = r["y"][0:1024]
        y_p[2 * cid:2 * cid + 2] = r["y"][1024:1536].reshape(2, 256, 1024)
        nk[2 * cid:2 * cid + 2] = r["nk"].reshape(2, 2, 256, 2, 64)
        nv[2 * cid:2 * cid + 2] = r["nv"].reshape(2, 2, 256, 2, 64)
        nst[2 * cid:2 * cid + 2] = r["nst"]
    return (y_p, y_s, nk, nv, nst)
```

```python
import contextlib, os
SK = os.environ.get("SK", "")
import numpy as np
import concourse.bass as bass
import concourse.mybir as mybir
from concourse.bass_utils import run_bass_kernel_spmd

F32 = mybir.dt.float32; BF16 = mybir.dt.bfloat16
AF = mybir.ActivationFunctionType; ALU = mybir.AluOpType; AX = mybir.AxisListType

D_MODEL = 1024; DEPTH = 2; D_IN = 4864; D_FF = 2816
ALPHA = (2 * DEPTH) ** 0.25
EPS = 1e-6
NT = 12; TOK = 1536; T = 16
SEQS = [(0, 1024), (1024, 256), (1280, 256)]
BLKS = [(0, 512), (512, 512), (1024, 512)]
TILE_VEC = [0] * 8 + [1] * 4
NS = 99
SOFF = [0, 65, 82]
SCH = [64, 16, 16]


class StopBuild(Exception):
    pass


class Res:
    __slots__ = ("name", "w", "r")
    def __init__(self, name):
        self.name = name; self.w = None; self.r = {}


class Eng:
    def __init__(self, name, h, sem):
        self.name = name; self.h = h; self.sem = sem; self.count = 0; self.waited = {}


class FW:
    NDMA = 32
    def __init__(self, nc, es):
        self.nc = nc; self.sems = {}; self.eng = {}
        for name, h in (("pe", nc.tensor), ("act", nc.scalar), ("dve", nc.vector), ("pool", nc.gpsimd), ("sp", nc.sync)):
            s = es.enter_context(nc.semaphore("s_" + name))
            self.sems[name] = s; self.eng[name] = Eng(name, h, s)
        self.dma_sems = []
        for i in range(self.NDMA):
            s = es.enter_context(nc.semaphore("s_dma%d" % i))
            self.sems["dma%d" % i] = s; self.dma_sems.append(["dma%d" % i, 0])
        self.dma_i = 0; self.ninstr = 0; self.dead = False; self.prev_plain = False; self.rec = None

    def _wait(self, e, key, val):
        if e.waited.get(key, 0) >= val: return
        e.h.wait_ge(self.sems[key], val); e.waited[key] = val

    def _deps(self, e, reads, writes, chain=False):
        deps = {}
        def add(kv):
            if kv is None: return
            k, v = kv
            if deps.get(k, 0) < v: deps[k] = v
        for r in reads: add(r.w)
        for w in writes:
            if w.w is not None and not (chain and w.w[0] == e.name):
                add(w.w)
            for k, v in w.r.items():
                if k == e.name: continue
                add((k, v))
        return deps

    def op(self, engname, fn, reads=(), writes=(), chain=False):
        if self.dead: return None
        if self.rec is not None:
            self.rec.append(("op", engname, fn, tuple(reads), tuple(writes), chain)); return None
        e = self.eng[engname]
        if engname == "pe" and chain in ("gfirst", "gnext"):
            if chain == "gfirst" and e.count > 0:
                self._wait(e, "pe", e.count)
            chain = True; self.prev_plain = False
        elif engname == "pe":
            plain = chain
            chain = chain and self.prev_plain
            if plain and not chain and e.count > 0:
                self._wait(e, "pe", e.count)
            self.prev_plain = plain
        for k, v in self._deps(e, reads, writes, chain).items(): self._wait(e, k, v)
        ins = fn(e.h); e.count += 1; ins.then_inc(e.sem, 1)
        for r in reads: r.r[e.name] = e.count
        for w in writes: w.w = (e.name, e.count); w.r = {}
        self.ninstr += 1
        return ins

    def dma(self, qname, out, in_, reads=(), writes=()):
        if self.dead: return None
        if self.rec is not None:
            self.rec.append(("dma", qname, out, in_, tuple(reads), tuple(writes))); return None
        e = self.eng[qname]
        deps = self._deps(e, reads, writes)
        slot = self.dma_sems[self.dma_i % self.NDMA]; self.dma_i += 1
        key = slot[0]
        if slot[1] > 0: deps[key] = max(deps.get(key, 0), slot[1])
        for k, v in deps.items(): self._wait(e, k, v)
        ins = e.h.dma_start(out=out, in_=in_)
        slot[1] += 16; ins.then_inc(self.sems[key], 16)
        for r in reads: r.r[key] = slot[1]
        for w in writes: w.w = (key, slot[1]); w.r = {}
        self.ninstr += 1
        return ins

    def replay(self, items):
        for it in items:
            if it[0] == "op": self.op(it[1], it[2], it[3], it[4], it[5])
            else: self.dma(it[1], it[2], it[3], reads=it[4], writes=it[5])

    def barrier(self):
        if self.dead: return
        targets = {n: e.count for n, e in self.eng.items() if e.count > 0}
        for key, cnt in self.dma_sems:
            if cnt > 0: targets[key] = cnt
        for n, e in self.eng.items():
            for k, v in targets.items(): self._wait(e, k, v)

    def finish(self):
        self.dead = False
        self.barrier()


def bl(ap, n):
    return ap.unsqueeze(2).to_broadcast([ap.shape[0], ap.shape[1], n])


def bm(ap, n):
    return ap.unsqueeze(1).to_broadcast([ap.shape[0], n, ap.shape[1]])


def _rep(a, n=128):
    return np.ascontiguousarray(np.broadcast_to(a[None], (n,) + a.shape))


def prep_shared(inp):
    f = lambda a: np.ascontiguousarray(a, dtype=np.float32)
    L = DEPTH
    S = {}
    for k in ("w_ada", "w_in", "w_glu", "w_br_ssm", "w_br_four", "w_out", "w_up", "w_down"):
        S[k] = f(inp[k])
    S["w_br_attn"] = f(inp["w_br_attn"])
    S["b_adaP"] = f(inp["b_ada"].reshape(L, 48, 128).transpose(2, 0, 1))
    S["bgbc"] = f(np.stack([np.stack([_rep(inp["b_ada"][l, 2048:3072]), _rep(inp["b_ada"][l, 5120:6144])], 1) for l in range(L)]))
    S["lnbc"] = f(np.stack([np.stack([_rep(inp[k][l]) for k in ("ln1_g", "ln1_b", "ln2_g", "ln2_b")]) for l in range(L)]))
    cw = inp["conv_w"].reshape(L, 3, 44, 128).transpose(3, 0, 2, 1)
    cb = inp["conv_b"].reshape(L, 44, 128).transpose(2, 0, 1)[..., None]
    S["convp"] = f(np.concatenate([cw, cb], -1))
    S["bglu"] = f(inp["b_glu"].reshape(L, 4, 128).transpose(2, 0, 1))
    S["g640"] = f(np.stack([_rep(np.concatenate([np.tile(inp["q_norm_g"][l], 8), np.tile(inp["k_norm_g"][l], 4)])) for l in range(L)]))
    def lay(a):
        aP = a.reshape(L, 2, 16, 2, 64).transpose(0, 1, 3, 4, 2).reshape(L, 2, 128, 16)
        aF = a.reshape(L, 2, 4, 4, 2, 64).transpose(0, 1, 3, 2, 4, 5)
        aF = np.broadcast_to(aF[:, :, :, None], (L, 2, 4, 32, 4, 2, 64)).reshape(L, 2, 128, 512)
        return np.concatenate([aP, aF], -1)
    ldt = np.broadcast_to(inp["ssm_log_dt"][..., None], (L, 2, 32, 64))
    S["lamin"] = f(np.stack([lay(inp["ssm_a_re"]), lay(inp["ssm_a_im"]), lay(ldt)], 2))
    def bbdP(b):
        Bp = b.reshape(L, 2, 16, 2, 64, 16)
        out = np.zeros((L, 2, 2, 64, 16, 2, 16), np.float32)
        for g2 in range(2):
            out[:, :, g2, :, :, g2, :] = Bp[:, :, :, g2].transpose(0, 1, 3, 2, 4)
        return out.reshape(L, 2, 128, 16, 32)
    def bbdT(b):
        Bq = b.reshape(L, 2, 4, 4, 2, 64, 16)
        out = np.zeros((L, 2, 4, 2, 16, 4, 2, 64), np.float32)
        for g2 in range(2):
            out[:, :, :, g2, :, :, g2, :] = Bq[:, :, :, :, g2].transpose(0, 1, 3, 5, 2, 4)
        return out.reshape(L, 2, 128, 4, 128)
    def cbdP(c):
        Cp = c.reshape(L, 2, 16, 2, 16, 64)
        out = np.zeros((L, 2, 2, 64, 16, 2, 16), np.float32)
        for g2 in range(2):
            out[:, :, g2, :, :, g2, :] = Cp[:, :, :, g2].transpose(0, 1, 4, 2, 3)
        return out.reshape(L, 2, 128, 16, 32)
    S["BbdP"] = f(np.stack([bbdP(inp["ssm_b_re"]), bbdP(inp["ssm_b_im"])], 2))
    S["BbdT"] = f(np.stack([bbdT(inp["ssm_b_re"]), bbdT(inp["ssm_b_im"])], 2))
    S["CbdP"] = f(np.stack([cbdP(inp["ssm_c_re"]), cbdP(inp["ssm_c_im"])], 2))
    Dblk = np.zeros((L, 128, 4, 128), np.float32)
    for l in range(L):
        for q in range(4):
            Dblk[l, np.arange(128), q, np.arange(128)] = inp["ssm_d"][l, q * 128:(q + 1) * 128]
    S["Dblk"] = Dblk
    S["ident"] = np.eye(128, dtype=np.float32)
    mk = np.zeros((128, 4), np.float32); mk[np.arange(128), np.arange(128) // 32] = 1.0
    S["maskP"] = mk
    t = np.arange(1024)
    freqs = (10000.0 ** (-np.arange(16, dtype=np.float32) / 16)).astype(np.float32)
    ang = np.concatenate([(t // 64).astype(np.float32)[:, None] * freqs, (t % 64).astype(np.float32)[:, None] * freqs], -1)
    S["ropec"] = f(np.cos(ang).reshape(8, 128, 32).transpose(1, 0, 2))
    S["ropes"] = f(np.sin(ang).reshape(8, 128, 32).transpose(1, 0, 2))
    def dft(n):
        i = np.arange(n, dtype=np.int64)
        m = (i[:, None] * i[None, :]) % n
        a = 2.0 * np.pi * m.astype(np.float64) / n
        return np.cos(a) / np.sqrt(n), np.sin(a) / np.sqrt(n)
    c128, s128 = dft(128)
    S["cs128"] = f(np.concatenate([c128, s128], 1))
    for n in (1024, 256):
        c, s = dft(n)
        S["cl%d" % n] = f(c.reshape(n // 128, 128, n).transpose(1, 0, 2))
        S["sl%d" % n] = f((-s).reshape(n // 128, 128, n).transpose(1, 0, 2))
    return S


def prep_core(inp, cid):
    f = lambda a: np.ascontiguousarray(a, dtype=np.float32)
    m = {}
    m["xin"] = f(np.concatenate([inp["x_sample"][cid], inp["x_prompt"][2 * cid:2 * cid + 2].reshape(512, 1024)], 0))
    cv = np.stack([inp["c"][cid], inp["c_ctx"]], -1)
    m["cvec"] = f(cv.reshape(8, 128, 2).transpose(1, 0, 2))
    m["ckT"] = f(inp["cache_k"][cid].reshape(DEPTH, 512, 128).transpose(0, 2, 1))
    m["cv"] = f(inp["cache_v"][cid].reshape(DEPTH, 512, 128))
    st = inp["state_ssm"][cid].reshape(DEPTH, 2, 16, 2, 64, 2)
    m["h0"] = f(st.transpose(3, 4, 0, 1, 5, 2).reshape(128, DEPTH, 2, 2, 16))
    return m


IN_SHAPES = {
    "xin": (1536, 1024), "cvec": (128, 8, 2), "ckT": (2, 128, 512), "cv": (2, 512, 128), "h0": (128, 2, 2, 2, 16),
    "w_ada": (2, 1024, 6144), "w_in": (2, 1024, 4864), "w_glu": (2, 512, 512), "w_br_attn": (2, 512, 1024),
    "w_br_ssm": (2, 512, 1024), "w_br_four": (2, 512, 1024), "w_out": (2, 1024, 1024), "w_up": (2, 1024, 5632),
    "w_down": (2, 2816, 1024), "b_adaP": (128, 2, 48), "bgbc": (2, 128, 2, 1024), "lnbc": (2, 4, 128, 1024),
    "convp": (128, 2, 44, 4), "bglu": (128, 2, 4), "g640": (2, 128, 768), "lamin": (2, 2, 3, 128, 528),
    "BbdP": (2, 2, 2, 128, 16, 32), "BbdT": (2, 2, 2, 128, 4, 128), "CbdP": (2, 2, 2, 128, 16, 32),
    "Dblk": (2, 128, 4, 128), "ident": (128, 128), "maskP": (128, 4), "ropec": (128, 8, 32), "ropes": (128, 8, 32),
    "cs128": (128, 256), "cl1024": (128, 8, 1024), "sl1024": (128, 8, 1024), "cl256": (128, 2, 256), "sl256": (128, 2, 256),
}
OUT_SHAPES = {"y": (1536, 1024), "nk": (2, 2, 256, 128), "nv": (2, 2, 256, 128), "nst": (2, 2, 2, 32, 64, 2)}


def build(n_layers=DEPTH, stop=None, dbg_shapes=None):
    nc = bass.Bass("TRN2", target_bir_lowering=False)
    Din = {k: nc.dram_tensor(k, list(s), F32, kind="ExternalInput").ap() for k, s in IN_SHAPES.items()}
    Dout = {k: nc.dram_tensor(k, list(s), F32, kind="ExternalOutput").ap() for k, s in OUT_SHAPES.items()}
    Ddbg = {k: nc.dram_tensor(k, list(s), F32, kind="ExternalOutput").ap() for k, s in (dbg_shapes or {}).items()}
    es = contextlib.ExitStack()
    with es:
        fw = FW(nc, es)
        uid = [0]
        def sb(st, name, shape, dt=F32):
            uid[0] += 1
            return st.enter_context(nc.sbuf_tensor("%s_%d" % (name, uid[0]), list(shape), dt))
        OP = fw.op
        CH = [True]
        def MM(out, lhsT, rhs, start, stop_, r, w, tp=None, chain=None):
            if chain is None: chain = CH[0] and (tp is None)
            if tp is None:
                return fw.op("pe", lambda e: e.matmul(out, lhsT=lhsT, rhs=rhs, start=start, stop=stop_), r, w, chain=chain)
            return fw.op("pe", lambda e: e.matmul(out, lhsT=lhsT, rhs=rhs, start=start, stop=stop_, tile_position=tp), r, w, chain=chain)
        def TT(eng, out, in0, in1, op, r, w):
            return fw.op(eng, lambda e: e.tensor_tensor(out=out, in0=in0, in1=in1, op=op), r, w)
        def TS(eng, out, in0, s1, op0, r, w, s2=None, op1=None):
            if op1 is None:
                return fw.op(eng, lambda e: e.tensor_scalar(out=out, in0=in0, scalar1=s1, scalar2=None, op0=op0), r, w)
            return fw.op(eng, lambda e: e.tensor_scalar(out=out, in0=in0, scalar1=s1, scalar2=s2, op0=op0, op1=op1), r, w)
        def ACT(out, in_, func, r, w, scale=1.0, bias=0.0):
            return fw.op("act", lambda e: e.activation(out=out, in_=in_, func=func, bias=bias, scale=scale), r, w)
        def CP(eng, out, in_, r, w):
            if eng == "act":
                return fw.op("act", lambda e: e.copy(out=out, in_=in_), r, w)
            return fw.op(eng, lambda e: e.tensor_copy(out=out, in_=in_), r, w)
        Rdbg = Res("dbg")
        def DBG(name, ap, r):
            if name in Ddbg:
                fw.dma("pool", Ddbg[name], ap, reads=r, writes=[Rdbg])

        X = sb(es, "X", [128, NT, 1024]); RX = [Res("X%d" % i) for i in range(NT)]
        ident_f = sb(es, "ident_f", [128, 128]); ident_b = sb(es, "ident_b", [128, 128], BF16); Rid = Res("id")
        maskP = sb(es, "maskP", [128, 4]); Rmask = Res("mask")
        modP = sb(es, "modP", [128, 6, 8, 2]); opsc = sb(es, "opsc", [128, 2, 8, 2]); Rmod = Res("mod")
        csl = sb(es, "csl", [128, 8, 2]); sTb = sb(es, "sTb", [128, 8, 2], BF16); sRep = sb(es, "sRep", [128, 2, 8, 128], BF16)
        Rcs = Res("cs")
        stat = sb(es, "stat", [128, NT, 2, 6]); mv = sb(es, "mv", [128, NT, 2]); rs = sb(es, "rs", [128, NT, 2]); Rstat = Res("stat")
        ps = es.enter_context(nc.psum_tensor("ps", [128, 3584], F32)); PB = [Res("pb%d" % i) for i in range(7)]
        pst = es.enter_context(nc.psum_tensor("pst", [128, 1024], BF16)); PT = Res("pt")
        wslot = [None, None]; Rws = [None, None]
        wctr = [0]
        def alloc_ws(st):
            for i_ in range(2):
                wslot[i_] = sb(st, "wslot%d" % i_, [128, 4096], BF16); Rws[i_] = Res("ws%d" % i_)
        def load_w(views):
            s_ = wctr[0] % 2; wctr[0] += 1
            for dstf, src in views:
                fw.dma("pool", dstf(wslot[s_]), src, writes=[Rws[s_]])
            return wslot[s_], Rws[s_]
        def bank(i, n=1):
            return ps[:, i * 512:(i + n) * 512]

        fw.dma("sp", ident_f[:], Din["ident"], writes=[Rid])
        fw.dma("pool", ident_b[:], Din["ident"], writes=[Rid])
        fw.dma("sp", maskP[:], Din["maskP"], writes=[Rmask])
        for i in range(NT):
            fw.dma("sp", X[:, i, :], Din["xin"][i * 128:(i + 1) * 128, :], writes=[RX[i]])
        fw.dma("sp", csl[:], Din["cvec"], writes=[Rcs])
        ACT(csl[:], csl[:], AF.Silu, [Rcs], [Rcs])
        CP("dve", sTb[:], csl[:], [Rcs], [Rcs])
        for v in range(2):
            CP("dve", sRep[:, v, :, :], bl(csl[:, :, v], 128), [Rcs], [Rcs])

        def build_uT(uT, RuT, sub):
            sh_sec = 0 if sub == 0 else 3
            for i in range(NT):
                v = TILE_VEC[i]
                for kk in range(2):
                    bk = kk
                    for k4 in range(4):
                        k = kk * 4 + k4
                        OP("pe", lambda e: e.transpose(bank(bk)[:, k4 * 128:(k4 + 1) * 128], X[:, i, k * 128:(k + 1) * 128], ident_f[:]),
                           [RX[i], Rid], [PB[bk]])
                    for k4 in range(4):
                        k = kk * 4 + k4
                        ACT(uT[:, k, i * 128:(i + 1) * 128], bank(bk)[:, k4 * 128:(k4 + 1) * 128], AF.Identity,
                            [PB[bk], Rmod], [RuT[i]], scale=opsc[:, sub, k, v:v + 1], bias=modP[:, sh_sec, k, v:v + 1])

        def layer_norm_residual(l, which, kc, lhsT_fn, lres_fn, wsrc3):
            with contextlib.ExitStack() as ph:
                ch_save = CH[0]; CH[0] = 'L' in os.environ.get('CHP', 'mabfFGMLUS')
                alloc_ws(ph)
                lnb = sb(ph, "lnb", [128, 2, 1024]); Rln = Res("lnb")
                tmp = [sb(ph, "lntmp%d" % j, [128, 512]) for j in range(2)]; Rtmp = [Res("lntmp0"), Res("lntmp1")]
                Wd = sb(ph, "Wd", [128, kc, 512], BF16); RWd = Res("Wd")
                fw.dma("sp", lnb[:, 0, :], Din["lnbc"][l, 2 * which], writes=[Rln])
                fw.dma("sp", lnb[:, 1, :], Din["lnbc"][l, 2 * which + 1], writes=[Rln])
                gbc = sb(ph, "gbc", [128, 2, 1024]); Rgbc = Res("gbc")
                bg = sb(ph, "bg", [128, 1024]); Rbg = Res("bg")
                fw.dma("sp", bg[:], Din["bgbc"][l, :, which, :], writes=[Rbg])
                sec = 2 if which == 0 else 5
                wsrc = Din["w_ada"][l].rearrange("(k p) n -> p k n", p=128)
                for half in range(2):
                    Wg, RWg = load_w([(lambda t: t[:, 0:4096].rearrange("p (k n) -> p k n", k=8), wsrc[:, :, sec * 1024 + half * 512:sec * 1024 + (half + 1) * 512])])
                    Wgv = Wg[:, 0:4096].rearrange("p (k n) -> p k n", k=8)
                    for v in range(2):
                        for k in range(8):
                            MM(bank(v), sRep[:, v, k, :], Wgv[:, k, :], k == 0, k == 7, [RWg, Rcs], [PB[v]])
                        TT("dve", gbc[:, v, half * 512:(half + 1) * 512], bank(v), bg[:, half * 512:(half + 1) * 512], ALU.add, [PB[v], Rbg], [Rgbc])
                cnt = 0
                for nb in range(2):
                    cs = slice(nb * 512, (nb + 1) * 512)
                    for k0 in range(0, kc, 8):
                        k1 = min(kc, k0 + 8)
                        fw.dma("pool", Wd[:, k0:k1, :], wsrc3[:, k0:k1, cs], writes=[RWd])
                    for i in range(NT):
                        v = TILE_VEC[i]
                        bk = 2 + (cnt % 4); tj = cnt % 2; cnt += 1
                        for k in range(kc):
                            MM(bank(bk), lhsT_fn(i, k), Wd[:, k, :], k == 0, k == kc - 1, lres_fn(i, k) + [RWd], [PB[bk]])
                        TT("dve", tmp[tj][:], bank(bk), gbc[:, v, cs], ALU.mult, [PB[bk], Rgbc], [Rtmp[tj]])
                        OP("dve", lambda e: e.scalar_tensor_tensor(out=X[:, i, cs], in0=X[:, i, cs], scalar=float(ALPHA), in1=tmp[tj][:],
                                                                   op0=ALU.mult, op1=ALU.add), [RX[i], Rtmp[tj]], [RX[i]])
                for i in range(NT):
                    for hh in range(2):
                        OP("dve", lambda e: e.bn_stats(out=stat[:, i, hh, :], in_=X[:, i, hh * 512:(hh + 1) * 512]), [RX[i]], [Rstat])
                    OP("dve", lambda e: e.bn_aggr(out=mv[:, i, :], in_=stat[:, i, :, :].rearrange("p a b -> p (a b)")), [Rstat], [Rstat])
                TS("dve", rs[:, :, 0], mv[:, :, 1], EPS, ALU.add, [Rstat], [Rstat])
                ACT(rs[:, :, 0], rs[:, :, 0], AF.Sqrt, [Rstat], [Rstat])
                OP("dve", lambda e: e.reciprocal(out=rs[:, :, 0], in_=rs[:, :, 0]), [Rstat], [Rstat])
                OP("dve", lambda e: e.scalar_tensor_tensor(out=rs[:, :, 1], in0=mv[:, :, 0], scalar=-1.0, in1=rs[:, :, 0],
                                                           op0=ALU.mult, op1=ALU.mult), [Rstat], [Rstat])
                for i in range(NT):
                    ACT(X[:, i, :], X[:, i, :], AF.Identity, [RX[i], Rstat], [RX[i]], scale=rs[:, i, 0:1], bias=rs[:, i, 1:2])
                    TT("pool", X[:, i, :], X[:, i, :], lnb[:, 0, :], ALU.mult, [RX[i], Rln], [RX[i]])
                    TT("pool", X[:, i, :], X[:, i, :], lnb[:, 1, :], ALU.add, [RX[i], Rln], [RX[i]])
                fw.barrier()
                CH[0] = ch_save

        def layers():
          for l in range(n_layers):
              with contextlib.ExitStack() as ph:
                  wa = [sb(ph, "wa%d" % i, [128, 8, 512], BF16) for i in range(3)]; Rwa = [Res("wa%d" % i) for i in range(3)]
                  bP = sb(ph, "bP", [128, 48]); Rb = Res("bmod")
                  fw.dma("sp", bP[:], Din["b_adaP"][:, l, :], writes=[Rb])
                  wsrc = Din["w_ada"][l].rearrange("(k p) n -> p k n", p=128)
                  pm = bank(6)[:, 0:96]
                  CH[0] = 'm' in os.environ.get('CHP', 'mabfFGMLUS')
                  for cb in range(12):
                      s = cb % 3
                      sec, half = cb // 2, cb % 2
                      if sec in (2, 5):
                          continue
                      fw.dma("pool", wa[s][:], wsrc[:, :, cb * 512:(cb + 1) * 512], writes=[Rwa[s]])
                      for m in range(4):
                          col = (sec * 8 + half * 4 + m) * 2
                          for k in range(8):
                              MM(pm[:, col:col + 2], wa[s][:, k, m * 128:(m + 1) * 128], sTb[:, k, :], k == 0, k == 7,
                                 [Rwa[s], Rcs], [PB[6]])
                  for sec in (0, 1, 3, 4):
                      TT("dve", modP[:, sec, :, :], pm[:, sec * 16:(sec + 1) * 16].rearrange("p (c v) -> p c v", v=2),
                         bl(bP[:, sec * 8:(sec + 1) * 8], 2), ALU.add, [PB[6], Rb], [Rmod])
                  TS("dve", opsc[:, 0, :, :], modP[:, 1, :, :], 1.0, ALU.add, [Rmod], [Rmod])
                  TS("dve", opsc[:, 1, :, :], modP[:, 4, :, :], 1.0, ALU.add, [Rmod], [Rmod])
                  fw.barrier()
              if stop == "mod": fw.dead = True
              DBG("d_modP", modP[:].rearrange("p a b c -> p (a b c)"), [Rmod])

              with contextlib.ExitStack() as L1:
                  QT = sb(L1, "QT", [128, 4, TOK], BF16); RQ = [Res("QT%d" % j) for j in range(4)]; attnT = QT; RattnT = RQ
                  sfT = sb(L1, "sfT", [128, 8, TOK], BF16); RsfT = [Res("sfT%d" % j) for j in range(8)]
                  fourT = sb(L1, "fourT", [128, 4, TOK], BF16); RfourT = [Res("fourT%d" % j) for j in range(4)]
                  Lp = contextlib.ExitStack()
                  lamD = [[sb(Lp, "lam%d%d" % (d_, r), [128, 528]) for r in range(2)] for d_ in range(2)]
                  ffD = [[sb(Lp, "ff%d%d" % (d_, r), [128, 528]) for r in range(2)] for d_ in range(2)]
                  PowP = [sb(Lp, "PowP%d" % r, [128, 2, 16, 17]) for r in range(2)]
                  LH = [sb(Lp, "LH%d" % r, [128, 2, 16, 7]) for r in range(2)]
                  fP = [sb(Lp, "fP%d" % r, [128, 2, 16]) for r in range(2)]
                  RS = Res("ssm")
                  def cmul(dr, di, ar, ai, br, bi, m1, m2):
                      TT("dve", m1, ar, br, ALU.mult, [RS], [RS]); TT("dve", m2, ai, bi, ALU.mult, [RS], [RS])
                      TT("dve", dr, m1, m2, ALU.subtract, [RS], [RS])
                      TT("dve", m1, ar, bi, ALU.mult, [RS], [RS]); TT("dve", m2, ai, br, ALU.mult, [RS], [RS])
                      TT("dve", di, m1, m2, ALU.add, [RS], [RS])
                  def ssm_param_pipeline(st):
                      are, aim, ldt, mag, cc, ss, t1, t2, t3 = [sb(st, "sp%d" % j_, [128, 528]) for j_ in range(9)]
                      for d in range(2):
                          lam = lamD[d]; ff = ffD[d]
                          fw.dma("sp", are[:], Din["lamin"][l, d, 0], writes=[RS])
                          fw.dma("sp", aim[:], Din["lamin"][l, d, 1], writes=[RS])
                          fw.dma("sp", ldt[:], Din["lamin"][l, d, 2], writes=[RS])
                          ACT(ldt[:], ldt[:], AF.Exp, [RS], [RS])
                          TT("dve", t1[:], are[:], ldt[:], ALU.mult, [RS], [RS])
                          ACT(mag[:], t1[:], AF.Exp, [RS], [RS])
                          TT("dve", t2[:], aim[:], ldt[:], ALU.mult, [RS], [RS])
                          ACT(ss[:], t2[:], AF.Sin, [RS], [RS], scale=1.0 / 16)
                          ACT(t1[:], t2[:], AF.Sin, [RS], [RS], scale=1.0 / 32)
                          TT("dve", t1[:], t1[:], t1[:], ALU.mult, [RS], [RS])
                          TS("dve", cc[:], t1[:], -2.0, ALU.mult, [RS], [RS], s2=1.0, op1=ALU.add)
                          for _ in range(4):
                              TT("dve", t1[:], cc[:], cc[:], ALU.mult, [RS], [RS])
                              TT("dve", t2[:], ss[:], ss[:], ALU.mult, [RS], [RS])
                              TT("dve", t3[:], cc[:], ss[:], ALU.mult, [RS], [RS])
                              TT("dve", cc[:], t1[:], t2[:], ALU.subtract, [RS], [RS])
                              TS("dve", ss[:], t3[:], 2.0, ALU.mult, [RS], [RS])
                          TT("dve", lam[0][:], mag[:], cc[:], ALU.mult, [RS], [RS])
                          TT("dve", lam[1][:], mag[:], ss[:], ALU.mult, [RS], [RS])
                          TT("dve", t1[:], are[:], are[:], ALU.mult, [RS], [RS])
                          TT("dve", t2[:], aim[:], aim[:], ALU.mult, [RS], [RS])
                          TT("dve", t1[:], t1[:], t2[:], ALU.add, [RS], [RS])
                          OP("dve", lambda e: e.reciprocal(out=t1[:], in_=t1[:]), [RS], [RS])
                          TS("dve", t2[:], lam[0][:], -1.0, ALU.add, [RS], [RS])
                          TT("dve", t3[:], t2[:], are[:], ALU.mult, [RS], [RS])
                          TT("dve", cc[:], lam[1][:], aim[:], ALU.mult, [RS], [RS])
                          TT("dve", t3[:], t3[:], cc[:], ALU.add, [RS], [RS])
                          TT("dve", ff[0][:], t3[:], t1[:], ALU.mult, [RS], [RS])
                          TT("dve", t3[:], lam[1][:], are[:], ALU.mult, [RS], [RS])
                          TT("dve", cc[:], t2[:], aim[:], ALU.mult, [RS], [RS])
                          TT("dve", t3[:], t3[:], cc[:], ALU.subtract, [RS], [RS])
                          TT("dve", ff[1][:], t3[:], t1[:], ALU.mult, [RS], [RS])
                          pm1 = t1[:, 0:256].rearrange("p (g k) -> p g k", g=16); pm2 = t2[:, 0:256].rearrange("p (g k) -> p g k", g=16)
                          OP("dve", lambda e, d=d: e.memset(PowP[0][:, d, :, 0:1], 1.0), [RS], [RS])
                          OP("dve", lambda e, d=d: e.memset(PowP[1][:, d, :, 0:1], 0.0), [RS], [RS])
                          for r in range(2):
                              TS("dve", PowP[r][:, d, :, 1], lam[r][:, 0:16], 1.0, ALU.mult, [RS], [RS])
                              TS("dve", fP[r][:, d, :], ff[r][:, 0:16], 1.0, ALU.mult, [RS], [RS])
                          for w in (1, 2, 4, 8):
                              br_ = PowP[0][:, d, :, w:w + 1].to_broadcast([128, 16, w]); bi_ = PowP[1][:, d, :, w:w + 1].to_broadcast([128, 16, w])
                              cmul(PowP[0][:, d, :, w + 1:2 * w + 1], PowP[1][:, d, :, w + 1:2 * w + 1],
                                   PowP[0][:, d, :, 1:w + 1], PowP[1][:, d, :, 1:w + 1], br_, bi_, pm1[:, :, 0:w], pm2[:, :, 0:w])
                          for r in range(2):
                              TS("dve", LH[r][:, d, :, 0], PowP[r][:, d, :, 16], 1.0, ALU.mult, [RS], [RS])
                          for m in range(6):
                              cmul(LH[0][:, d, :, m + 1], LH[1][:, d, :, m + 1], LH[0][:, d, :, m], LH[1][:, d, :, m],
                                   LH[0][:, d, :, m], LH[1][:, d, :, m], t1[:, 256:272], t2[:, 256:272])
                  with contextlib.ExitStack() as L2:
                      KT = sb(L2, "KT", [128, 2, 2048], BF16); RKT = Res("KT")
                      Vsb = sb(L2, "Vsb", [128, 16, 2, 2, 128], BF16); RV = Res("V")
                      with contextlib.ExitStack() as ph:
                          alloc_ws(ph)
                          uT = sb(ph, "uT", [128, 8, TOK], BF16); RuT = [Res("uT%d" % i) for i in range(NT)]
                          qn = [sb(ph, "qn%d" % j, [128, 512]) for j in range(2)]; Rqn = [Res("qn0"), Res("qn1")]
                          vf = [sb(ph, "vf%d" % j, [128, 128]) for j in range(2)]; Rvf = [Res("vf0"), Res("vf1")]
                          qkb = [sb(ph, "qkb%d" % j, [128, 512], BF16) for j in range(2)]; Rqkb = [Res("qkb0"), Res("qkb1")]
                          sq = sb(ph, "sq", [128, 512]); Rsq = Res("sq")
                          kb2 = [sb(ph, "kb2%d" % j, [128, 128], BF16) for j in range(2)]; Rkb2 = [Res("kb20"), Res("kb21")]
                          ssq = sb(ph, "ssq", [128, 8]); Rssq = Res("ssq")
                          g640 = sb(ph, "g640", [128, 768]); rc = sb(ph, "rc", [128, 8, 32]); rsn = sb(ph, "rsn", [128, 8, 32]); Rcst = Res("cst")
                          rt = [sb(ph, "rt%d" % j, [128, 8, 32]) for j in range(2)]; Rrt = [Res("rt0"), Res("rt1")]
                          win = Din["w_in"][l].rearrange("(k p) n -> p k n", p=128)
                          fw.dma("sp", g640[:], Din["g640"][l], writes=[Rcst])
                          fw.dma("sp", rc[:], Din["ropec"], writes=[Rcst])
                          fw.dma("sp", rsn[:], Din["ropes"], writes=[Rcst])
                          fw.dma("pool", KT[:, 0, 1024:1536], Din["ckT"][l], writes=[RKT])
                          fw.dma("pool", KT[0:64, 1, 1024:1536], Din["ckT"][l, 64:128, :], writes=[RKT])
                          fw.dma("pool", KT[64:128, 1, 1024:1536], Din["ckT"][l, 0:64, :], writes=[RKT])
                          OP("dve", lambda e: e.memset(Vsb[:].rearrange("p a b c d -> p (a b c d)"), 1.0), [], [RV])
                          cvv = Din["cv"][l].rearrange("(t p) (h d) -> p t h d", p=128, h=2)
                          for lay in range(2):
                              for kvh_ in range(2):
                                  fw.dma("pool", Vsb[:, 8:12, kvh_, lay, lay * 64:(lay + 1) * 64], cvv[:, :, kvh_, :], writes=[RV])
                          build_uT(uT, RuT, 0)
                          if stop == "ut": fw.dead = True
                          DBG("d_uT", uT[:].rearrange("p a b -> p (a b)"), RuT)

                          def norm_rope(i, pa, pr, nh, gap, outb, Routb, slot):
                              w_ = nh * 64
                              ACT(sq[:, 0:w_], pa, AF.Square, pr, [Rsq])
                              OP("dve", lambda e: e.reduce_sum(out=ssq[:, 0:nh], in_=sq[:, 0:w_].rearrange("p (h d) -> p h d", d=64), axis=AX.X), [Rsq], [Rssq])
                              TS("dve", ssq[:, 0:nh], ssq[:, 0:nh], 1.0 / 64, ALU.mult, [Rssq], [Rssq], s2=EPS, op1=ALU.add)
                              ACT(ssq[:, 0:nh], ssq[:, 0:nh], AF.Sqrt, [Rssq], [Rssq])
                              OP("dve", lambda e: e.reciprocal(out=ssq[:, 0:nh], in_=ssq[:, 0:nh]), [Rssq], [Rssq])
                              qv = qn[slot][:, 0:w_]
                              TT("dve", qv.rearrange("p (h d) -> p h d", d=64), pa.rearrange("p (h d) -> p h d", d=64), bl(ssq[:, 0:nh], 64),
                                 ALU.mult, pr + [Rssq], [Rqn[slot]])
                              TT("dve", qv, qv, gap, ALU.mult, [Rqn[slot], Rcst], [Rqn[slot]])
                              if i >= 8:
                                  CP("act", outb, qv, [Rqn[slot]], [Routb])
                              else:
                                  x4 = qv.rearrange("p (h d two) -> p h d two", d=32, two=2)
                                  o4 = outb.rearrange("p (h d two) -> p h d two", d=32, two=2)
                                  x0, x1 = x4[:, :, :, 0], x4[:, :, :, 1]
                                  cc, ss_ = bm(rc[:, i, :], nh), bm(rsn[:, i, :], nh)
                                  r0, r1 = rt[0][:, 0:nh, :], rt[1][:, 0:nh, :]
                                  TT("dve", r0, x0, cc, ALU.mult, [Rqn[slot], Rcst], [Rrt[0]])
                                  TT("pool", r1, x1, ss_, ALU.mult, [Rqn[slot], Rcst], [Rrt[1]])
                                  TT("dve", o4[:, :, :, 0], r0, r1, ALU.subtract, Rrt, [Routb])
                                  TT("dve", r0, x0, ss_, ALU.mult, [Rqn[slot], Rcst], [Rrt[0]])
                                  TT("pool", r1, x1, cc, ALU.mult, [Rqn[slot], Rcst], [Rrt[1]])
                                  TT("dve", o4[:, :, :, 1], r0, r1, ALU.add, Rrt, [Routb])

                          CH[0] = 'a' in os.environ.get('CHP', 'mabfFGMLUS')
                          W, RW = load_w([(lambda t: t[:, 0:4096].rearrange("p (k n) -> p k n", k=8), win[:, :, 0:512])])
                          Wv = W[:, 0:4096].rearrange("p (k n) -> p k n", k=8)
                          for i in range(NT):
                              bk = 2 + (i % 2); sl = i % 2
                              for k in range(8):
                                  MM(bank(bk), uT[:, k, i * 128:(i + 1) * 128], Wv[:, k, :], k == 0, k == 7, [RuT[i], RW], [PB[bk]])
                              norm_rope(i, bank(bk), [PB[bk]], 8, g640[:, 0:512], qkb[sl][:, :], Rqkb[sl], sl)
                              for j in range(4):
                                  src = qkb[sl][:, j * 128:(j + 1) * 128]
                                  OP("pe", lambda e: e.transpose(pst[:, j * 128:(j + 1) * 128], src, ident_b[:]), [Rqkb[sl], Rid], [PT])
                              CP("act", QT[:, :, i * 128:(i + 1) * 128], pst[:, 0:512].rearrange("p (j c) -> p j c", j=4), [PT], RQ)
                          if stop == "passA": fw.dead = True
                          CH[0] = 'b' in os.environ.get('CHP', 'mabfFGMLUS')
                          W, RW = load_w([(lambda t: t[:, 0:2048].rearrange("p (k n) -> p k n", k=8), win[:, :, 512:768])])
                          Wv = W[:, 0:2048].rearrange("p (k n) -> p k n", k=8)
                          for i in range(NT):
                              bk = 2 + (i % 2); sl = i % 2
                              for k in range(8):
                                  MM(bank(bk)[:, 0:256], uT[:, k, i * 128:(i + 1) * 128], Wv[:, k, :], k == 0, k == 7, [RuT[i], RW], [PB[bk]])
                              norm_rope(i, bank(bk)[:, 0:256], [PB[bk]], 4, g640[:, 512:768], qkb[sl][:, 0:256], Rqkb[sl], sl)
                              kt = i if i < 8 else 12 + (i - 8)
                              for lay in range(2):
                                  CP("act", Vsb[:, kt, :, lay, lay * 64:(lay + 1) * 64], bank(bk)[:, 128:256].rearrange("p (h d) -> p h d", h=2), [PB[bk]], [RV])
                              if i >= 8:
                                  p_, t_ = (i - 8) // 2, (i - 8) % 2
                                  CP("act", vf[sl][:], bank(bk)[:, 128:256], [PB[bk]], [Rvf[sl]])
                                  fw.dma("sp", Dout["nk"][p_, l, t_ * 128:(t_ + 1) * 128, :], qn[sl][:, 0:128], reads=[Rqn[sl]], writes=[Rdbg])
                                  fw.dma("sp", Dout["nv"][p_, l, t_ * 128:(t_ + 1) * 128, :], vf[sl][:], reads=[Rvf[sl]], writes=[Rdbg])
                              CP("act", kb2[sl][:, 0:64], qkb[sl][:, 64:128], [Rqkb[sl]], [Rkb2[sl]])
                              CP("act", kb2[sl][:, 64:128], qkb[sl][:, 0:64], [Rqkb[sl]], [Rkb2[sl]])
                              OP("pe", lambda e: e.transpose(pst[:, 512:640], qkb[sl][:, 0:128], ident_b[:]), [Rqkb[sl], Rid], [PT])
                              OP("pe", lambda e: e.transpose(pst[:, 640:768], kb2[sl][:, :], ident_b[:]), [Rkb2[sl], Rid], [PT])
                              kc = i * 128 if i < 8 else 1536 + (i - 8) * 128
                              CP("act", KT[:, 0, kc:kc + 128], pst[:, 512:640], [PT], [RKT])
                              CP("act", KT[:, 1, kc:kc + 128], pst[:, 640:768], [PT], [RKT])
                          if stop == "passB": fw.dead = True
                          CH[0] = 'f' in os.environ.get('CHP', 'mabfFGMLUS')
                          for wb in range(2):
                              W, RW = load_w([(lambda t: t[:, 0:4096].rearrange("p (k n) -> p k n", k=8), win[:, :, 768 + wb * 512:1280 + wb * 512])])
                              Wv = W[:, 0:4096].rearrange("p (k n) -> p k n", k=8)
                              for m4 in range(4):
                                  m = wb * 4 + m4
                                  for tb, (c0, cn) in enumerate(BLKS):
                                      bk = (0, 1, 6)[(m * 3 + tb) % 3]
                                      for k in range(8):
                                          MM(bank(bk), Wv[:, k, m4 * 128:(m4 + 1) * 128], uT[:, k, c0:c0 + cn], k == 0, k == 7,
                                             RuT[c0 // 128:(c0 + cn) // 128] + [RW], [PB[bk]])
                                      CP("act", sfT[:, m, c0:c0 + cn], bank(bk), [PB[bk]], [RsfT[m]])
                          DBG("d_sfT", sfT[:].rearrange("p a b -> p (a b)"), RsfT)
                          fw.barrier()
                      if stop == "inproj": fw.dead = True
                      DBG("d_QT", QT[:].rearrange("p a b -> p (a b)"), RQ)
                      DBG("d_KT", KT[:].rearrange("p a b -> p (a b)"), [RKT])
                      CH[0] = 't' in os.environ.get('CHP', 'mabfFGMLUS')
                      with contextlib.ExitStack() as ph:
                          fw.rec = []
                          ssm_param_pipeline(ph)
                          pipe_items = fw.rec if fw.rec is not None else []; fw.rec = None
                          per_unit = (len(pipe_items) + 23) // 24
                          Eb = [sb(ph, "Eb%d" % j, [128, 512], BF16) for j in range(3)]; REb = [Res("Eb%d" % j) for j in range(3)]
                          rsb = [sb(ph, "rsb%d" % j, [128, 512]) for j in range(2)]; Rrsb = [Res("rsb0"), Res("rsb1")]
                          cnt = 0; cnt2 = 0
                          for si, (t0, Ls) in enumerate(SEQS):
                              if si == 0:
                                  qblks = [(0, 512), (512, 512)]; nkt = 12; kt0 = 0; kc0 = 0
                              else:
                                  qblks = [(t0, 256)]; nkt = 2; kt0 = 12 + 2 * (si - 1); kc0 = 1536 + 256 * (si - 1)
                              for h in range(8):
                                  j = h // 2; hh = h % 2; kvh = h // 4; var = 0 if kvh == hh else 1
                                  pv, sm = (slice(0, 64), slice(64, 128)) if hh == 0 else (slice(64, 128), slice(0, 64))
                                  for (q0, qn) in qblks:
                                      pvb = 3 + (cnt2 % 2); cnt2 += 1
                                      def S_mm(kt, sbk):
                                          MM(bank(sbk)[:, 0:qn], KT[hh * 64:(hh + 1) * 64, var, kc0 + kt * 128:kc0 + (kt + 1) * 128],
                                             QT[hh * 64:(hh + 1) * 64, j, q0:q0 + qn], True, True, [RKT, RQ[j]], [PB[sbk]], chain=False)
                                      slots = [(cnt + i_) % 3 for i_ in range(nkt)]; cnt += nkt
                                      S_mm(0, slots[0])
                                      for kt in range(nkt):
                                          sbk = slots[kt]; eb = sbk
                                          if kt + 1 < nkt:
                                              S_mm(kt + 1, slots[kt + 1])
                                          ACT(Eb[eb][:, 0:qn], bank(sbk)[:, 0:qn], AF.Exp, [PB[sbk]], [REb[eb]], scale=0.125)
                                          MM(bank(pvb)[:, 0:qn], Vsb[:, kt0 + kt, kvh, hh, :], Eb[eb][:, 0:qn], kt == 0, kt == nkt - 1,
                                             [RV, REb[eb]], [PB[pvb]], chain=False)
                                      rb = cnt2 % 2
                                      ACT(rsb[rb][sm, 0:qn], bank(pvb)[sm, 0:qn], AF.Ln, [PB[pvb]], [Rrsb[rb]])
                                      ACT(rsb[rb][sm, 0:qn], rsb[rb][sm, 0:qn], AF.Exp, [Rrsb[rb]], [Rrsb[rb]], scale=-1.0)
                                      TT("dve", attnT[pv, j, q0:q0 + qn], bank(pvb)[pv, 0:qn], rsb[rb][sm, 0:qn], ALU.mult,
                                         [PB[pvb], Rrsb[rb]], [RattnT[j]])
                                      fw.replay(pipe_items[:per_unit]); del pipe_items[:per_unit]
                          fw.replay(pipe_items); del pipe_items[:]
                          fw.barrier()
                  DBG("d_attnT", attnT[:].rearrange("p a b -> p (a b)"), RattnT)
                  CH[0] = 'F' in os.environ.get('CHP', 'mabfFGMLUS')
                  with contextlib.ExitStack() as L3:
                    with contextlib.ExitStack() as ph:
                        cs128 = sb(ph, "cs128", [128, 256], BF16); cl = sb(ph, "cl", [128, 8, 1024], BF16); sl_ = sb(ph, "sl", [128, 8, 1024], BF16)
                        clp = sb(ph, "clp", [128, 2, 256], BF16); slp = sb(ph, "slp", [128, 2, 256], BF16); Rft = Res("ftab")
                        Pfm = sb(ph, "Pfm", [128, NT, 1024], BF16); RPfm = [Res("Pfm%d" % i) for i in range(NT)]
                        fw.dma("pool", cs128[:], Din["cs128"], writes=[Rft])
                        for k in range(8):
                            fw.dma("pool", cl[:, k, :], Din["cl1024"][:, k, :], writes=[Rft])
                            fw.dma("pool", sl_[:, k, :], Din["sl1024"][:, k, :], writes=[Rft])
                        fw.dma("pool", clp[:], Din["cl256"], writes=[Rft])
                        fw.dma("pool", slp[:], Din["sl256"], writes=[Rft])
                        for i in range(NT):
                            b0 = 2 + 2 * (i % 2)
                            for g in range(4):
                                MM(bank(b0 + g // 2)[:, (g % 2) * 256:(g % 2 + 1) * 256], sfT[:, 4 + g, i * 128:(i + 1) * 128], cs128[:], True, True,
                                   [RsfT[4 + g], Rft], [PB[b0 + g // 2]])
                            CP("act", Pfm[:, i, :], bank(b0, 2), [PB[b0], PB[b0 + 1]], [RPfm[i]])
                        cnt = 0
                        for si, (t0, Ls) in enumerate(SEQS):
                            ntl = Ls // 128; tb0 = t0 // 128
                            tc_, ts_ = (cl, sl_) if si == 0 else (clp, slp)
                            for g in range(4):
                                for lb in range(0, Ls, 512):
                                    n = min(512, Ls - lb)
                                    bk = (0, 1, 6)[cnt % 3]; cnt += 1
                                    for k in range(ntl):
                                        MM(bank(bk)[:, 0:n], Pfm[:, tb0 + k, g * 256:g * 256 + 128], tc_[:, k, lb:lb + n], k == 0, False,
                                           [RPfm[tb0 + k], Rft], [PB[bk]])
                                        MM(bank(bk)[:, 0:n], Pfm[:, tb0 + k, g * 256 + 128:g * 256 + 256], ts_[:, k, lb:lb + n], False, k == ntl - 1,
                                           [RPfm[tb0 + k], Rft], [PB[bk]])
                                    CP("act", fourT[:, g, t0 + lb:t0 + lb + n], bank(bk)[:, 0:n], [PB[bk]], [RfourT[g]])
                        fw.barrier()
                    DBG("d_fourT", fourT[:].rearrange("p a b -> p (a b)"), RfourT)
                    if stop == "four": fw.dead = True
                    CH[0] = 'S' in os.environ.get('CHP', 'mabfFGMLUS')
                    with contextlib.ExitStack() as S:
                        Hb = sb(S, "Hb", [128, 2, 2, 16, NS], BF16)
                        h0 = sb(S, "h0", [128, 2, 2, 16]); stg = sb(S, "stg", [128, 16, 2])
                        RHb = Res("Hb"); Rstg = Res("stg")
                        fw.dma("sp", h0[:], Din["h0"][:, l], writes=[RS])
                        for d in range(2):
                            with contextlib.ExitStack() as SD:
                                lam = lamD[d]; ff = ffD[d]
                                A = [sb(SD, "A%d" % r, [128, 16, NS]) for r in range(2)]
                                for r in range(2):
                                    OP("dve", lambda e: e.memset(A[r][:].rearrange("p a b -> p (a b)"), 0.0), [RS], [RS])
                                with contextlib.ExitStack() as SW:
                                    W = [sb(SW, "W%d" % r, [128, 16, 128]) for r in range(2)]
                                    WT2 = [[sb(SW, "WT%d_%d" % (z, r), [128, 16, 128], BF16) for r in range(2)] for z in range(2)]
                                    wm = [sb(SW, "wm%d" % r, [128, 8, 128]) for r in range(3)]
                                    BT = [sb(SW, "BT%d" % r, [128, 128]) for r in range(2)]
                                    Lw = [sb(SW, "Lw%d" % r, [128, 128]) for r in range(2)]
                                    RWT2 = [Res("WTa"), Res("WTb")]
                                    def build_W(q):
                                        fs = slice(16 + q * 128, 16 + (q + 1) * 128)
                                        for r in range(2):
                                            fw.dma("sp", BT[r][:], Din["BbdT"][l, d, r, :, q, :], writes=[RS])
                                            ACT(Lw[r][:], lam[r][:, fs], AF.Identity, [RS], [RS])
                                        cmul(W[0][:, 0, :], W[1][:, 0, :], ff[0][:, fs], ff[1][:, fs], BT[0][:], BT[1][:], wm[0][:, 0, :], wm[1][:, 0, :])
                                        for w in (1, 2, 4, 8):
                                            cmul(W[0][:, w:2 * w, :], W[1][:, w:2 * w, :], bm(Lw[0][:], w), bm(Lw[1][:], w),
                                                 W[0][:, 0:w, :], W[1][:, 0:w, :], wm[0][:, 0:w, :], wm[1][:, 0:w, :])
                                            if w < 8:
                                                TT("dve", wm[0][:, 0, :], Lw[0][:], Lw[0][:], ALU.mult, [RS], [RS])
                                                TT("dve", wm[1][:, 0, :], Lw[1][:], Lw[1][:], ALU.mult, [RS], [RS])
                                                TT("dve", wm[2][:, 0, :], Lw[0][:], Lw[1][:], ALU.mult, [RS], [RS])
                                                TT("dve", Lw[0][:], wm[0][:, 0, :], wm[1][:, 0, :], ALU.subtract, [RS], [RS])
                                                TS("dve", Lw[1][:], wm[2][:, 0, :], 2.0, ALU.mult, [RS], [RS])
                                        for r in range(2):
                                            ACT(WT2[q % 2][r][:].rearrange("p a b -> p (a b)"), W[r][:].rearrange("p a b -> p (a b)"), AF.Identity, [RS], [RWT2[q % 2]])
                                    def run_S(q):
                                        WTq = WT2[q % 2]; RWTq = RWT2[q % 2]
                                        Sps = bank(0, 2).rearrange("p (b r c) -> p b r c", b=4, r=2)
                                        for b in range(4):
                                            sv = sfT[32 * b:32 * b + 32, q, :].rearrange("p (c j) -> p c j", j=16)
                                            for r in range(2):
                                                for k in range(16):
                                                    j = 15 - k if d == 0 else k
                                                    MM(Sps[:, b, r, 0:96], WTq[r][32 * b:32 * b + 32, k, :], sv[:, :, j], k == 0, k == 15,
                                                       [RWTq, RsfT[q]], [PB[0], PB[1]], tp=(32 * b, 0))
                                        for r in range(2):
                                            for s_i in range(3):
                                                n = SCH[s_i]; c0 = (0, 64, 80)[s_i]; lo = SOFF[s_i] + (1 if d == 0 else 0)
                                                ACT(A[r][:, 4 * q:4 * q + 4, lo:lo + n], Sps[:, :, r, c0:c0 + n], AF.Identity, [PB[0], PB[1]], [RS])
                                    build_W(0)
                                    for q in range(4):
                                        if q + 1 < 4:
                                            build_W(q + 1)
                                        run_S(q)
                                RSp = Res("ssm_pool"); NPD = 10
                                for r in range(2):
                                    ACT(A[r][:, :, 0 if d == 0 else 64], h0[:, d, r, :], AF.Identity, [RS], [RS, RSp])
                                with contextlib.ExitStack() as SC:
                                    sm = [sb(SC, "sm%d" % r, [128, 16, NS]) for r in range(3)]
                                    for m in range(7):
                                        dd = 1 << m
                                        groups = []
                                        if dd < 65:
                                            groups.append((lambda t_, ps_, lo, hi: t_[:, ps_, lo:hi], 65, None))
                                        if dd < 17:
                                            groups.append((lambda t_, ps_, lo, hi: t_[:, ps_, 65:99].rearrange("p g (s c) -> p g s c", s=2)[:, :, :, lo:hi], 17, 2))
                                        for view, n, ns in groups:
                                            cntn = n - dd
                                            (dlo, dhi, slo, shi) = (dd, n, 0, cntn) if d == 0 else (0, cntn, dd, n)
                                            for eng_, ps_, Rr in (("dve", slice(0, NPD), RS), ("pool", slice(NPD, 16), RSp)):
                                                npz = ps_.stop - ps_.start
                                                if ns is None:
                                                    Lr = bl(LH[0][:, d, ps_, m], cntn); Li = bl(LH[1][:, d, ps_, m], cntn)
                                                    tv = lambda t_: t_[:, ps_, 0:cntn]
                                                else:
                                                    Lr = LH[0][:, d, ps_, m].unsqueeze(2).unsqueeze(3).to_broadcast([128, npz, 2, cntn])
                                                    Li = LH[1][:, d, ps_, m].unsqueeze(2).unsqueeze(3).to_broadcast([128, npz, 2, cntn])
                                                    tv = lambda t_: t_[:, ps_, 0:2 * cntn].rearrange("p g (s c) -> p g s c", s=2)
                                                sr, si = view(A[0], ps_, slo, shi), view(A[1], ps_, slo, shi)
                                                dr, di = view(A[0], ps_, dlo, dhi), view(A[1], ps_, dlo, dhi)
                                                m1, m2, m3 = tv(sm[0]), tv(sm[1]), tv(sm[2])
                                                TT(eng_, m1, sr, Lr, ALU.mult, [Rr], [Rr]); TT(eng_, m2, si, Li, ALU.mult, [Rr], [Rr])
                                                TT(eng_, m1, m1, m2, ALU.subtract, [Rr], [Rr])
                                                TT(eng_, m2, si, Lr, ALU.mult, [Rr], [Rr]); TT(eng_, m3, sr, Li, ALU.mult, [Rr], [Rr])
                                                TT(eng_, m2, m2, m3, ALU.add, [Rr], [Rr])
                                                TT(eng_, dr, dr, m1, ALU.add, [Rr], [Rr]); TT(eng_, di, di, m2, ALU.add, [Rr], [Rr])
                                for r in range(2):
                                    ACT(Hb[:, d, r, :, :], A[r][:], AF.Identity, [RS, RSp], [RHb])
                                for s_i in (1, 2):
                                    idx = SOFF[s_i] + (16 if d == 0 else 0)
                                    for r in range(2):
                                        ACT(stg[:, :, r], A[r][:, :, idx], AF.Identity, [RS, RSp], [Rstg])
                                    fw.dma("sp", Dout["nst"][s_i - 1, l, d].rearrange("(pair g2) p r -> (g2 p) pair r", g2=2), stg[:], reads=[Rstg], writes=[Rdbg])
                                DBG("d_A%d" % d, A[0][:].rearrange("p a b -> p (a b)"), [RS, RSp])
                                fw.barrier()
                        with contextlib.ExitStack() as SQ:
                            CL2 = [[[sb(SQ, "CL%d%d%d" % (z, d, r), [128, 4, 17, 32], BF16) for r in range(2)] for d in range(2)] for z in range(2)]
                            Kblk = [sb(SQ, "Kblk%d" % d, [128, 16, 128], BF16) for d in range(2)]
                            Bb2 = [[[sb(SQ, "Bb%d%d%d" % (z, d, r), [128, 4, 32], BF16) for r in range(2)] for d in range(2)] for z in range(2)]
                            Dbl = sb(SQ, "Dbl", [128, 4, 128], BF16)
                            cm = [sb(SQ, "cm%d" % j, [128, 2, 17, 32]) for j in range(2)]
                            Cl = [sb(SQ, "Cl%d" % r, [128, 4, 32]) for r in range(2)]
                            Bl = [sb(SQ, "Bl%d" % r, [128, 4, 32]) for r in range(2)]
                            bt = [sb(SQ, "bt%d" % r, [128, 4, 32]) for r in range(2)]
                            gt = [sb(SQ, "gt%d" % r, [128, 512]) for r in range(2)]; Rgt = [Res("gt0"), Res("gt1")]
                            RCL2 = [Res("CLa"), Res("CLb")]; RK = Res("Kblk"); RDbl = Res("Dbl")
                            ydb = sb(SQ, "ydb", [128, 512]); Rydb = Res("ydb")
                            fw.dma("pool", Dbl[:], Din["Dblk"][l], writes=[RDbl])

                            def build_CL(q):
                                z = q % 2; CL = CL2[z]; Bb = Bb2[z]; RCL = RCL2[z]
                                ps4 = slice(4 * q, 4 * q + 4)
                                for d in range(2):
                                    for r in range(2):
                                        fw.dma("sp", Cl[r][:], Din["CbdP"][l, d, r, :, ps4, :], writes=[RS])
                                        fw.dma("sp", Bl[r][:], Din["BbdP"][l, d, r, :, ps4, :], writes=[RS])
                                    for hp in range(2):
                                        pp = slice(2 * hp, 2 * hp + 2); pq = slice(4 * q + 2 * hp, 4 * q + 2 * hp + 2)
                                        Pr = PowP[0][:, d, pq, :].unsqueeze(3).to_broadcast([128, 2, 17, 32])
                                        Pi = PowP[1][:, d, pq, :].unsqueeze(3).to_broadcast([128, 2, 17, 32])
                                        Cr = Cl[0][:, pp, :].unsqueeze(2).to_broadcast([128, 2, 17, 32])
                                        Ci = Cl[1][:, pp, :].unsqueeze(2).to_broadcast([128, 2, 17, 32])
                                        TT("dve", cm[0][:], Pr, Cr, ALU.mult, [RS], [RS]); TT("dve", cm[1][:], Pi, Ci, ALU.mult, [RS], [RS])
                                        TT("dve", CL[d][0][:, pp, :, :], cm[0][:], cm[1][:], ALU.subtract, [RS], [RCL])
                                        TT("dve", cm[0][:], Pr, Ci, ALU.mult, [RS], [RS]); TT("dve", cm[1][:], Pi, Cr, ALU.mult, [RS], [RS])
                                        TT("dve", cm[0][:], cm[0][:], cm[1][:], ALU.add, [RS], [RS])
                                        TS("dve", CL[d][1][:, pp, :, :], cm[0][:], -1.0, ALU.mult, [RS], [RCL])
                                    fr = bl(fP[0][:, d, ps4], 32); fi = bl(fP[1][:, d, ps4], 32)
                                    TT("dve", bt[0][:], fr, Bl[0][:], ALU.mult, [RS], [RS]); TT("dve", bt[1][:], fi, Bl[1][:], ALU.mult, [RS], [RS])
                                    TT("dve", Bb[d][0][:], bt[0][:], bt[1][:], ALU.subtract, [RS], [RCL])
                                    TT("dve", bt[0][:], fr, Bl[1][:], ALU.mult, [RS], [RS]); TT("dve", bt[1][:], fi, Bl[0][:], ALU.mult, [RS], [RS])
                                    TT("dve", Bb[d][1][:], bt[0][:], bt[1][:], ALU.add, [RS], [RCL])

                            def build_K(q):
                                z = q % 2; CL = CL2[z]; Bb = Bb2[z]; RCL = RCL2[z]
                                for d in range(2):
                                    kb = d
                                    for b in range(4):
                                        MM(bank(kb)[32 * b:32 * b + 32, :], Bb[d][0][:, b, :], CL[d][0][:, b, 0:16, :].rearrange("p t n -> p (t n)"), True, False,
                                           [RCL], [PB[kb]], tp=(0, 32 * b))
                                        MM(bank(kb)[32 * b:32 * b + 32, :], Bb[d][1][:, b, :], CL[d][1][:, b, 0:16, :].rearrange("p t n -> p (t n)"), False, True,
                                           [RCL], [PB[kb]], tp=(0, 32 * b))
                                    for cb in range(4):
                                        TS("dve", Kblk[d][:, :, 32 * cb:32 * cb + 32], bank(kb).rearrange("p (t n) -> p t n", n=32), maskP[:, cb:cb + 1], ALU.mult,
                                           [PB[kb], Rmask], [RK])

                            def run_y(q):
                                z = q % 2; CL = CL2[z]; RCL = RCL2[z]
                                for tb, (c0, cn) in enumerate(BLKS):
                                    yb = 3 + tb
                                    Yb = bank(yb).rearrange("p (c j) -> p c j", j=16)
                                    sblk = sfT[:, q, c0:c0 + cn].rearrange("p (c j) -> p c j", j=16)
                                    MM(bank(yb), Dbl[:, q, :], sfT[:, q, c0:c0 + cn], True, False, [RDbl, RsfT[q]], [PB[yb]])
                                    for d in range(2):
                                        for t in range(16):
                                            if d == 0:
                                                o_, r_ = Yb[:, :, t:16], sblk[:, :, 0:16 - t]
                                            else:
                                                o_, r_ = Yb[:, :, 0:16 - t], sblk[:, :, t:16]
                                            MM(o_, Kblk[d][:, t, :], r_, False, False, [RK, RsfT[q]], [PB[yb]])
                                    GC = os.environ.get('GC', '1') == '1'
                                    for d in range(2):
                                        for j in range(16):
                                            tt = j + 1 if d == 0 else 16 - j
                                            sh = 0 if d == 0 else 1
                                            for r in range(2):
                                                for b in range(4):
                                                    gfirst = (b == 0 and r == 0 and j == 0 and d == 0)
                                                    if tb < 2:
                                                        rhs = Hb[:, d, r, 4 * q + b, 32 * tb + sh:32 * tb + sh + 32]
                                                        o_ = Yb[32 * b:32 * b + 32, :, j]
                                                    else:
                                                        rhs = Hb[:, d, r, 4 * q + b, 65:99].rearrange("p (s c) -> p s c", s=2)[:, :, sh:sh + 16]
                                                        o_ = bank(yb).rearrange("p (s c j) -> p s c j", s=2, j=16)[32 * b:32 * b + 32, :, :, j]
                                                    last = (d == 1 and b == 3 and j == 15 and r == 1)
                                                    MM(o_, CL[d][r][:, b, tt, :], rhs, False, last, [RCL, RHb], [PB[yb]], tp=(0, 32 * b),
                                                       chain=(("gfirst" if gfirst else "gnext") if GC else False))
                                    if ("d_y%d_%d" % (q, tb)) in Ddbg:
                                        ACT(ydb[:], bank(yb), AF.Identity, [PB[yb]], [Rydb])
                                        DBG("d_y%d_%d" % (q, tb), ydb[:], [Rydb])
                                    g_ = tb % 2
                                    ACT(gt[g_][:], bank(yb), AF.Square, [PB[yb]], [Rgt[g_]])
                                    TS("dve", gt[g_][:], gt[g_][:], 0.044715, ALU.mult, [Rgt[g_]], [Rgt[g_]], s2=1.0, op1=ALU.add)
                                    TT("dve", gt[g_][:], gt[g_][:], bank(yb), ALU.mult, [Rgt[g_], PB[yb]], [Rgt[g_]])
                                    ACT(gt[g_][:], gt[g_][:], AF.Sigmoid, [Rgt[g_]], [Rgt[g_]], scale=1.5957691216)
                                    TT("dve", sfT[:, q, c0:c0 + cn], gt[g_][:], bank(yb), ALU.mult, [Rgt[g_], PB[yb]], [RsfT[q]])
                            build_CL(0); build_K(0)
                            for q in range(4):
                                if q + 1 < 4:
                                    build_CL(q + 1)
                                run_y(q)
                                if q + 1 < 4:
                                    build_K(q + 1)
                            fw.barrier()
                    Lp.close()
                    ssmT = sb(L3, "ssmT", [128, 4, TOK], BF16); RssmT = [Res("ssmT%d" % j) for j in range(4)]
                    CH[0] = 'G' in os.environ.get('CHP', 'mabfFGMLUS')
                    with contextlib.ExitStack() as ph:
                        alloc_ws(ph)
                        bgl = sb(ph, "bgl", [128, 4]); Rbgl = Res("bgl"); sg = [sb(ph, "sg%d" % j, [128, 512]) for j in range(2)]; Rsg = [Res("sg0"), Res("sg1")]
                        fw.dma("sp", bgl[:], Din["bglu"][:, l, :], writes=[Rbgl])
                        Wg, RWg = load_w([(lambda t: t[:, 0:2048].rearrange("p (k n) -> p k n", k=4), Din["w_glu"][l].rearrange("(k p) n -> p k n", p=128))])
                        Wgv = Wg[:, 0:2048].rearrange("p (k n) -> p k n", k=4)
                        cnt = 0
                        for m in range(4):
                            for tb, (c0, cn) in enumerate(BLKS):
                                bk = (0, 1, 2)[cnt % 3]; sj = cnt % 2; cnt += 1
                                for k in range(4):
                                    MM(bank(bk), Wgv[:, k, m * 128:(m + 1) * 128], sfT[:, k, c0:c0 + cn], k == 0, k == 3, [RWg, RsfT[k]], [PB[bk]])
                                ACT(sg[sj][:], bank(bk), AF.Sigmoid, [PB[bk], Rbgl], [Rsg[sj]], bias=bgl[:, m:m + 1])
                                TT("dve", ssmT[:, m, c0:c0 + cn], sg[sj][:], sfT[:, m, c0:c0 + cn], ALU.mult, [Rsg[sj], RsfT[m]], [RssmT[m]])
                        fw.barrier()
                    DBG("d_gy", sfT[:, 0:4, :].rearrange("p a b -> p (a b)"), RsfT[0:4])
                    DBG("d_ssmT", ssmT[:].rearrange("p a b -> p (a b)"), RssmT)
                    if stop == "ssm": fw.dead = True
                    CH[0] = 'M' in os.environ.get('CHP', 'mabfFGMLUS')
                    with contextlib.ExitStack() as M:
                        mergedT = sb(M, "mergedT", [128, 8, TOK], BF16); Rmg = [Res("mg%d" % j) for j in range(8)]
                        with contextlib.ExitStack() as ph:
                            alloc_ws(ph)
                            uT = sb(ph, "uT", [128, 8, TOK], BF16); RuT = [Res("uT%d" % i) for i in range(NT)]
                            gs = [sb(ph, "gs%d" % j, [128, 512]) for j in range(3)]; Rgs = [Res("gs%d" % j) for j in range(3)]
                            ac = [sb(ph, "ac%d" % j, [128, 512]) for j in range(2)]; Rac = [Res("ac0"), Res("ac1")]
                            build_uT(uT, RuT, 0)
                            win = Din["w_in"][l].rearrange("(k p) n -> p k n", p=128)
                            wbr = [Din[nm][l].rearrange("(k p) n -> p k n", p=128) for nm in ("w_br_attn", "w_br_ssm", "w_br_four")]
                            srcs = [(attnT, RattnT), (ssmT, RssmT), (fourT, RfourT)]
                            Wg2 = [sb(ph, "Wg2_%d" % z, [128, 3072], BF16) for z in range(2)]; RWg2 = [Res("Wg2a"), Res("Wg2b")]
                            Wb2 = [sb(ph, "Wb2_%d" % z, [128, 1536], BF16) for z in range(2)]; RWb2 = [Res("Wb2a"), Res("Wb2b")]
                            def load_c(c):
                                z = c % 2
                                for g in range(3):
                                    fw.dma("pool", Wg2[z][:, g * 1024:(g + 1) * 1024].rearrange("p (k n) -> p k n", k=8),
                                           win[:, :, 1792 + g * 1024 + c * 128:1792 + g * 1024 + (c + 1) * 128], writes=[RWg2[z]])
                                for x in range(3):
                                    fw.dma("pool", Wb2[z][:, x * 512:(x + 1) * 512].rearrange("p (k n) -> p k n", k=4),
                                           wbr[x][:, :, c * 128:(c + 1) * 128], writes=[RWb2[z]])
                            load_c(0)
                            for c in range(8):
                                if c + 1 < 8:
                                    load_c(c + 1)
                                Wg, RWg, Wb, RWb = Wg2[c % 2], RWg2[c % 2], Wb2[c % 2], RWb2[c % 2]
                                for tb, (c0, cn) in enumerate(BLKS):
                                    for g in range(3):
                                        Wgv = Wg[:, g * 1024:(g + 1) * 1024].rearrange("p (k n) -> p k n", k=8)
                                        for k in range(8):
                                            MM(bank(g), Wgv[:, k, :], uT[:, k, c0:c0 + cn], k == 0, k == 7, RuT[c0 // 128:(c0 + cn) // 128] + [RWg], [PB[g]])
                                        ACT(gs[g][:], bank(g), AF.Sigmoid, [PB[g]], [Rgs[g]])
                                    for x in range(3):
                                        Wbv = Wb[:, x * 512:(x + 1) * 512].rearrange("p (k n) -> p k n", k=4)
                                        st_, Rst = srcs[x]
                                        for k in range(4):
                                            MM(bank(3 + x), Wbv[:, k, :], st_[:, k, c0:c0 + cn], k == 0, k == 3, [Rst[k], RWb], [PB[3 + x]])
                                    TT("dve", ac[0][:], gs[0][:], bank(3), ALU.mult, [Rgs[0], PB[3]], [Rac[0]])
                                    TT("dve", ac[1][:], gs[1][:], bank(4), ALU.mult, [Rgs[1], PB[4]], [Rac[1]])
                                    TT("dve", ac[0][:], ac[0][:], ac[1][:], ALU.add, Rac, [Rac[0]])
                                    TT("dve", ac[1][:], gs[2][:], bank(5), ALU.mult, [Rgs[2], PB[5]], [Rac[1]])
                                    TT("dve", mergedT[:, c, c0:c0 + cn], ac[0][:], ac[1][:], ALU.add, Rac, [Rmg[c]])
                            fw.barrier()
                        DBG("d_mergedT", mergedT[:].rearrange("p a b -> p (a b)"), Rmg)
                        if stop == "merge": fw.dead = True
                        layer_norm_residual(l, 0, 8, lambda i, k: mergedT[:, k, i * 128:(i + 1) * 128], lambda i, k: [Rmg[k]],
                                            Din["w_out"][l].rearrange("(k p) n -> p k n", p=128))
              if "d_x1" in Ddbg:
                  for i in range(NT):
                      fw.dma("sp", Ddbg["d_x1"][i * 128:(i + 1) * 128, :], X[:, i, :], reads=[RX[i]], writes=[Rdbg])
              if stop == "ln1": fw.dead = True
              CH[0] = 'U' in os.environ.get('CHP', 'mabfFGMLUS')
              with contextlib.ExitStack() as Fs:
                  actT = sb(Fs, "actT", [128, 22, TOK], BF16); Ract = [Res("act%d" % j) for j in range(22)]
                  with contextlib.ExitStack() as ph:
                      alloc_ws(ph)
                      uT = sb(ph, "uT", [128, 8, TOK], BF16); RuT = [Res("uT%d" % i) for i in range(NT)]
                      cvp = sb(ph, "cvp", [128, 44, 4]); Rcv = Res("cvp")
                      hc = [sb(ph, "hc%d" % j, [128, TOK]) for j in range(2)]; Rhc = [Res("hc0"), Res("hc1")]
                      fw.dma("sp", cvp[:], Din["convp"][:, l], writes=[Rcv])
                      build_uT(uT, RuT, 1)
                      wup = Din["w_up"][l].rearrange("(k p) n -> p k n", p=128)
                      for ch in range(44):
                          if ch % 4 == 0:
                              W, RW = load_w([(lambda t: t[:, 0:4096].rearrange("p (k n) -> p k n", k=8), wup[:, :, ch * 128:ch * 128 + 512])])
                              Wv = W[:, 0:4096].rearrange("p (k n) -> p k n", k=8)
                          cw = ch % 4; pb0 = 3 * (ch % 2); hj = ch % 2
                          prs = [PB[pb0], PB[pb0 + 1], PB[pb0 + 2]]
                          for tb, (c0, cn) in enumerate(BLKS):
                              for k in range(8):
                                  MM(bank(pb0 + tb), Wv[:, k, cw * 128:(cw + 1) * 128], uT[:, k, c0:c0 + cn], k == 0, k == 7,
                                     RuT[c0 // 128:(c0 + cn) // 128] + [RW], [prs[tb]])
                          hp = bank(pb0, 3); h_ = hc[hj]
                          ACT(h_[:], hp, AF.Identity, prs + [Rcv], [Rhc[hj]], scale=cvp[:, ch, 1:2], bias=cvp[:, ch, 3:4])
                          def stt(o_, i0, sc, i1):
                              OP("dve", lambda e: e.scalar_tensor_tensor(out=o_, in0=i0, scalar=sc, in1=i1, op0=ALU.mult, op1=ALU.add),
                                 prs + [Rcv, Rhc[hj]], [Rhc[hj]])
                          stt(h_[:, 1:1024], hp[:, 0:1023], cvp[:, ch, 0:1], h_[:, 1:1024])
                          stt(h_[:, 0:1023], hp[:, 1:1024], cvp[:, ch, 2:3], h_[:, 0:1023])
                          h3 = h_[:, 1024:1536].rearrange("p (s c) -> p s c", s=2); p3 = hp[:, 1024:1536].rearrange("p (s c) -> p s c", s=2)
                          stt(h3[:, :, 1:256], p3[:, :, 0:255], cvp[:, ch, 0:1], h3[:, :, 1:256])
                          stt(h3[:, :, 0:255], p3[:, :, 1:256], cvp[:, ch, 2:3], h3[:, :, 0:255])
                          if ch < 22:
                              ACT(actT[:, ch, :], h_[:], AF.Silu, [Rhc[hj]], [Ract[ch]])
                          else:
                              TT("dve", actT[:, ch - 22, :], actT[:, ch - 22, :], h_[:], ALU.mult, [Ract[ch - 22], Rhc[hj]], [Ract[ch - 22]])
                      fw.barrier()
                  DBG("d_actT", actT[:].rearrange("p a b -> p (a b)"), Ract)
                  if stop == "ffn": fw.dead = True
                  layer_norm_residual(l, 1, 22, lambda i, k: actT[:, k, i * 128:(i + 1) * 128], lambda i, k: [Ract[k]],
                                      Din["w_down"][l].rearrange("(k p) n -> p k n", p=128))
        layers()
        fw.dead = False
        for i in range(NT):
            fw.dma("sp", Dout["y"][i * 128:(i + 1) * 128, :], X[:, i, :], reads=[RX[i]], writes=[Rdbg])
        fw.finish()
    return nc


def kernel(**inputs):
    inp = {k: np.asarray(v) for k, v in inputs.items()}
    S = prep_shared(inp)
    nc = build()
    in_maps = []
    for cid in range(8):
        m = dict(S); m.update(prep_core(inp, cid))
        in_maps.append(m)
    res = run_bass_kernel_spmd(nc, in_maps, core_ids=list(range(8)))
    y_p = np.zeros((16, 256, 1024), np.float32); y_s = np.zeros((8, 1024, 1024), np.float32)
    nk = np.zeros((16, 2, 256, 2, 64), np.float32); nv = np.zeros((16, 2, 256, 2, 64), np.float32)
    nst = np.zeros((16, 2, 2, 32, 64, 2), np.float32)
    for cid in range(8):
        r = res.results[cid]
        y_s[cid] = r["y"][0:1024]
        y_p[2 * cid:2 * cid + 2] = r["y"][1024:1536].reshape(2, 256, 1024)
        nk[2 * cid:2 * cid + 2] = r["nk"].reshape(2, 2, 256, 2, 64)
        nv[2 * cid:2 * cid + 2] = r["nv"].reshape(2, 2, 256, 2, 64)
        nst[2 * cid:2 * cid + 2] = r["nst"]
    return (y_p, y_s, nk, nv, nst)
```

```python
import contextlib, os
SK = os.environ.get("SK", "")
import numpy as np
import concourse.bass as bass
import concourse.mybir as mybir
from concourse.bass_utils import run_bass_kernel_spmd

F32 = mybir.dt.float32; BF16 = mybir.dt.bfloat16
AF = mybir.ActivationFunctionType; ALU = mybir.AluOpType; AX = mybir.AxisListType

D_MODEL = 1024; DEPTH = 2; D_IN = 4864; D_FF = 2816
ALPHA = (2 * DEPTH) ** 0.25
EPS = 1e-6
NT = 12; TOK = 1536; T = 16
SEQS = [(0, 1024), (1024, 256), (1280, 256)]
BLKS = [(0, 512), (512, 512), (1024, 512)]
TILE_VEC = [0] * 8 + [1] * 4
NS = 99
SOFF = [0, 65, 82]
SCH = [64, 16, 16]


class StopBuild(Exception):
    pass


class Res:
    __slots__ = ("name", "w", "r")
    def __init__(self, name):
        self.name = name; self.w = None; self.r = {}


class Eng:
    def __init__(self, name, h, sem):
        self.name = name; self.h = h; self.sem = sem; self.count = 0; self.waited = {}


class FW:
    NDMA = 32
    def __init__(self, nc, es):
        self.nc = nc; self.sems = {}; self.eng = {}
        for name, h in (("pe", nc.tensor), ("act", nc.scalar), ("dve", nc.vector), ("pool", nc.gpsimd), ("sp", nc.sync)):
            s = es.enter_context(nc.semaphore("s_" + name))
            self.sems[name] = s; self.eng[name] = Eng(name, h, s)
        self.dma_sems = []
        for i in range(self.NDMA):
            s = es.enter_context(nc.semaphore("s_dma%d" % i))
            self.sems["dma%d" % i] = s; self.dma_sems.append(["dma%d" % i, 0])
        self.dma_i = 0; self.ninstr = 0; self.dead = False; self.prev_plain = False; self.rec = None

    def _wait(self, e, key, val):
        if e.waited.get(key, 0) >= val: return
        e.h.wait_ge(self.sems[key], val); e.waited[key] = val

    def _deps(self, e, reads, writes, chain=False):
        deps = {}
        def add(kv):
            if kv is None: return
            k, v = kv
            if deps.get(k, 0) < v: deps[k] = v
        for r in reads: add(r.w)
        for w in writes:
            if w.w is not None and not (chain and w.w[0] == e.name):
                add(w.w)
            for k, v in w.r.items():
                if k == e.name: continue
                add((k, v))
        return deps

    def op(self, engname, fn, reads=(), writes=(), chain=False):
        if self.dead: return None
        if self.rec is not None:
            self.rec.append(("op", engname, fn, tuple(reads), tuple(writes), chain)); return None
        e = self.eng[engname]
        if engname == "pe" and chain in ("gfirst", "gnext"):
            if chain == "gfirst" and e.count > 0:
                self._wait(e, "pe", e.count)
            chain = True; self.prev_plain = False
        elif engname == "pe":
            plain = chain
            chain = chain and self.prev_plain
            if plain and not chain and e.count > 0:
                self._wait(e, "pe", e.count)
            self.prev_plain = plain
        for k, v in self._deps(e, reads, writes, chain).items(): self._wait(e, k, v)
        ins = fn(e.h); e.count += 1; ins.then_inc(e.sem, 1)
        for r in reads: r.r[e.name] = e.count
        for w in writes: w.w = (e.name, e.count); w.r = {}
        self.ninstr += 1
        return ins

    def dma(self, qname, out, in_, reads=(), writes=()):
        if self.dead: return None
        if self.rec is not None:
            self.rec.append(("dma", qname, out, in_, tuple(reads), tuple(writes))); return None
        e = self.eng[qname]
        deps = self._deps(e, reads, writes)
        slot = self.dma_sems[self.dma_i % self.NDMA]; self.dma_i += 1
        key = slot[0]
        if slot[1] > 0: deps[key] = max(deps.get(key, 0), slot[1])
        for k, v in deps.items(): self._wait(e, k, v)
        ins = e.h.dma_start(out=out, in_=in_)
        slot[1] += 16; ins.then_inc(self.sems[key], 16)
        for r in reads: r.r[key] = slot[1]
        for w in writes: w.w = (key, slot[1]); w.r = {}
        self.ninstr += 1
        return ins

    def replay(self, items):
        for it in items:
            if it[0] == "op": self.op(it[1], it[2], it[3], it[4], it[5])
            else: self.dma(it[1], it[2], it[3], reads=it[4], writes=it[5])

    def barrier(self):
        if self.dead: return
        targets = {n: e.count for n, e in self.eng.items() if e.count > 0}
        for key, cnt in self.dma_sems:
            if cnt > 0: targets[key] = cnt
        for n, e in self.eng.items():
            for k, v in targets.items(): self._wait(e, k, v)

    def finish(self):
        self.dead = False
        self.barrier()


def bl(ap, n):
    return ap.unsqueeze(2).to_broadcast([ap.shape[0], ap.shape[1], n])


def bm(ap, n):
    return ap.unsqueeze(1).to_broadcast([ap.shape[0], n, ap.shape[1]])


def _rep(a, n=128):
    return np.ascontiguousarray(np.broadcast_to(a[None], (n,) + a.shape))


def prep_shared(inp):
    f = lambda a: np.ascontiguousarray(a, dtype=np.float32)
    L = DEPTH
    S = {}
    for k in ("w_ada", "w_in", "w_glu", "w_br_ssm", "w_br_four", "w_out", "w_up", "w_down"):
        S[k] = f(inp[k])
    S["w_br_attn"] = f(inp["w_br_attn"])
    S["b_adaP"] = f(inp["b_ada"].reshape(L, 48, 128).transpose(2, 0, 1))
    S["bgbc"] = f(np.stack([np.stack([_rep(inp["b_ada"][l, 2048:3072]), _rep(inp["b_ada"][l, 5120:6144])], 1) for l in range(L)]))
    S["lnbc"] = f(np.stack([np.stack([_rep(inp[k][l]) for k in ("ln1_g", "ln1_b", "ln2_g", "ln2_b")]) for l in range(L)]))
    cw = inp["conv_w"].reshape(L, 3, 44, 128).transpose(3, 0, 2, 1)
    cb = inp["conv_b"].reshape(L, 44, 128).transpose(2, 0, 1)[..., None]
    S["convp"] = f(np.concatenate([cw, cb], -1))
    S["bglu"] = f(inp["b_glu"].reshape(L, 4, 128).transpose(2, 0, 1))
    S["g640"] = f(np.stack([_rep(np.concatenate([np.tile(inp["q_norm_g"][l], 8), np.tile(inp["k_norm_g"][l], 4)])) for l in range(L)]))
    def lay(a):
        aP = a.reshape(L, 2, 16, 2, 64).transpose(0, 1, 3, 4, 2).reshape(L, 2, 128, 16)
        aF = a.reshape(L, 2, 4, 4, 2, 64).transpose(0, 1, 3, 2, 4, 5)
        aF = np.broadcast_to(aF[:, :, :, None], (L, 2, 4, 32, 4, 2, 64)).reshape(L, 2, 128, 512)
        return np.concatenate([aP, aF], -1)
    ldt = np.broadcast_to(inp["ssm_log_dt"][..., None], (L, 2, 32, 64))
    S["lamin"] = f(np.stack([lay(inp["ssm_a_re"]), lay(inp["ssm_a_im"]), lay(ldt)], 2))
    def bbdP(b):
        Bp = b.reshape(L, 2, 16, 2, 64, 16)
        out = np.zeros((L, 2, 2, 64, 16, 2, 16), np.float32)
        for g2 in range(2):
            out[:, :, g2, :, :, g2, :] = Bp[:, :, :, g2].transpose(0, 1, 3, 2, 4)
        return out.reshape(L, 2, 128, 16, 32)
    def bbdT(b):
        Bq = b.reshape(L, 2, 4, 4, 2, 64, 16)
        out = np.zeros((L, 2, 4, 2, 16, 4, 2, 64), np.float32)
        for g2 in range(2):
            out[:, :, :, g2, :, :, g2, :] = Bq[:, :, :, :, g2].transpose(0, 1, 3, 5, 2, 4)
        return out.reshape(L, 2, 128, 4, 128)
    def cbdP(c):
        Cp = c.reshape(L, 2, 16, 2, 16, 64)
        out = np.zeros((L, 2, 2, 64, 16, 2, 16), np.float32)
        for g2 in range(2):
            out[:, :, g2, :, :, g2, :] = Cp[:, :, :, g2].transpose(0, 1, 4, 2, 3)
        return out.reshape(L, 2, 128, 16, 32)
    S["BbdP"] = f(np.stack([bbdP(inp["ssm_b_re"]), bbdP(inp["ssm_b_im"])], 2))
    S["BbdT"] = f(np.stack([bbdT(inp["ssm_b_re"]), bbdT(inp["ssm_b_im"])], 2))
    S["CbdP"] = f(np.stack([cbdP(inp["ssm_c_re"]), cbdP(inp["ssm_c_im"])], 2))
    Dblk = np.zeros((L, 128, 4, 128), np.float32)
    for l in range(L):
        for q in range(4):
            Dblk[l, np.arange(128), q, np.arange(128)] = inp["ssm_d"][l, q * 128:(q + 1) * 128]
    S["Dblk"] = Dblk
    S["ident"] = np.eye(128, dtype=np.float32)
    mk = np.zeros((128, 4), np.float32); mk[np.arange(128), np.arange(128) // 32] = 1.0
    S["maskP"] = mk
    t = np.arange(1024)
    freqs = (10000.0 ** (-np.arange(16, dtype=np.float32) / 16)).astype(np.float32)
    ang = np.concatenate([(t // 64).astype(np.float32)[:, None] * freqs, (t % 64).astype(np.float32)[:, None] * freqs], -1)
    S["ropec"] = f(np.cos(ang).reshape(8, 128, 32).transpose(1, 0, 2))
    S["ropes"] = f(np.sin(ang).reshape(8, 128, 32).transpose(1, 0, 2))
    def dft(n):
        i = np.arange(n, dtype=np.int64)
        m = (i[:, None] * i[None, :]) % n
        a = 2.0 * np.pi * m.astype(np.float64) / n
        return np.cos(a) / np.sqrt(n), np.sin(a) / np.sqrt(n)
    c128, s128 = dft(128)
    S["cs128"] = f(np.concatenate([c128, s128], 1))
    for n in (1024, 256):
        c, s = dft(n)
        S["cl%d" % n] = f(c.reshape(n // 128, 128, n).transpose(1, 0, 2))
        S["sl%d" % n] = f((-s).reshape(n // 128, 128, n).transpose(1, 0, 2))
    return S


def prep_core(inp, cid):
    f = lambda a: np.ascontiguousarray(a, dtype=np.float32)
    m = {}
    m["xin"] = f(np.concatenate([inp["x_sample"][cid], inp["x_prompt"][2 * cid:2 * cid + 2].reshape(512, 1024)], 0))
    cv = np.stack([inp["c"][cid], inp["c_ctx"]], -1)
    m["cvec"] = f(cv.reshape(8, 128, 2).transpose(1, 0, 2))
    m["ckT"] = f(inp["cache_k"][cid].reshape(DEPTH, 512, 128).transpose(0, 2, 1))
    m["cv"] = f(inp["cache_v"][cid].reshape(DEPTH, 512, 128))
    st = inp["state_ssm"][cid].reshape(DEPTH, 2, 16, 2, 64, 2)
    m["h0"] = f(st.transpose(3, 4, 0, 1, 5, 2).reshape(128, DEPTH, 2, 2, 16))
    return m


IN_SHAPES = {
    "xin": (1536, 1024), "cvec": (128, 8, 2), "ckT": (2, 128, 512), "cv": (2, 512, 128), "h0": (128, 2, 2, 2, 16),
    "w_ada": (2, 1024, 6144), "w_in": (2, 1024, 4864), "w_glu": (2, 512, 512), "w_br_attn": (2, 512, 1024),
    "w_br_ssm": (2, 512, 1024), "w_br_four": (2, 512, 1024), "w_out": (2, 1024, 1024), "w_up": (2, 1024, 5632),
    "w_down": (2, 2816, 1024), "b_adaP": (128, 2, 48), "bgbc": (2, 128, 2, 1024), "lnbc": (2, 4, 128, 1024),
    "convp": (128, 2, 44, 4), "bglu": (128, 2, 4), "g640": (2, 128, 768), "lamin": (2, 2, 3, 128, 528),
    "BbdP": (2, 2, 2, 128, 16, 32), "BbdT": (2, 2, 2, 128, 4, 128), "CbdP": (2, 2, 2, 128, 16, 32),
    "Dblk": (2, 128, 4, 128), "ident": (128, 128), "maskP": (128, 4), "ropec": (128, 8, 32), "ropes": (128, 8, 32),
    "cs128": (128, 256), "cl1024": (128, 8, 1024), "sl1024": (128, 8, 1024), "cl256": (128, 2, 256), "sl256": (128, 2, 256),
}
OUT_SHAPES = {"y": (1536, 1024), "nk": (2, 2, 256, 128), "nv": (2, 2, 256, 128), "nst": (2, 2, 2, 32, 64, 2)}


def build(n_layers=DEPTH, stop=None, dbg_shapes=None):
    nc = bass.Bass("TRN2", target_bir_lowering=False)
    Din = {k: nc.dram_tensor(k, list(s), F32, kind="ExternalInput").ap() for k, s in IN_SHAPES.items()}
    Dout = {k: nc.dram_tensor(k, list(s), F32, kind="ExternalOutput").ap() for k, s in OUT_SHAPES.items()}
    Ddbg = {k: nc.dram_tensor(k, list(s), F32, kind="ExternalOutput").ap() for k, s in (dbg_shapes or {}).items()}
    es = contextlib.ExitStack()
    with es:
        fw = FW(nc, es)
        uid = [0]
        def sb(st, name, shape, dt=F32):
            uid[0] += 1
            return st.enter_context(nc.sbuf_tensor("%s_%d" % (name, uid[0]), list(shape), dt))
        OP = fw.op
        CH = [True]
        def MM(out, lhsT, rhs, start, stop_, r, w, tp=None, chain=None):
            if chain is None: chain = CH[0] and (tp is None)
            if tp is None:
                return fw.op("pe", lambda e: e.matmul(out, lhsT=lhsT, rhs=rhs, start=start, stop=stop_), r, w, chain=chain)
            return fw.op("pe", lambda e: e.matmul(out, lhsT=lhsT, rhs=rhs, start=start, stop=stop_, tile_position=tp), r, w, chain=chain)
        def TT(eng, out, in0, in1, op, r, w):
            return fw.op(eng, lambda e: e.tensor_tensor(out=out, in0=in0, in1=in1, op=op), r, w)
        def TS(eng, out, in0, s1, op0, r, w, s2=None, op1=None):
            if op1 is None:
                return fw.op(eng, lambda e: e.tensor_scalar(out=out, in0=in0, scalar1=s1, scalar2=None, op0=op0), r, w)
            return fw.op(eng, lambda e: e.tensor_scalar(out=out, in0=in0, scalar1=s1, scalar2=s2, op0=op0, op1=op1), r, w)
        def ACT(out, in_, func, r, w, scale=1.0, bias=0.0):
            return fw.op("act", lambda e: e.activation(out=out, in_=in_, func=func, bias=bias, scale=scale), r, w)
        def CP(eng, out, in_, r, w):
            if eng == "act":
                return fw.op("act", lambda e: e.copy(out=out, in_=in_), r, w)
            return fw.op(eng, lambda e: e.tensor_copy(out=out, in_=in_), r, w)
        Rdbg = Res("dbg")
        def DBG(name, ap, r):
            if name in Ddbg:
                fw.dma("pool", Ddbg[name], ap, reads=r, writes=[Rdbg])

        X = sb(es, "X", [128, NT, 1024]); RX = [Res("X%d" % i) for i in range(NT)]
        ident_f = sb(es, "ident_f", [128, 128]); ident_b = sb(es, "ident_b", [128, 128], BF16); Rid = Res("id")
        maskP = sb(es, "maskP", [128, 4]); Rmask = Res("mask")
        modP = sb(es, "modP", [128, 6, 8, 2]); opsc = sb(es, "opsc", [128, 2, 8, 2]); Rmod = Res("mod")
        csl = sb(es, "csl", [128, 8, 2]); sTb = sb(es, "sTb", [128, 8, 2], BF16); sRep = sb(es, "sRep", [128, 2, 8, 128], BF16)
        Rcs = Res("cs")
        stat = sb(es, "stat", [128, NT, 2, 6]); mv = sb(es, "mv", [128, NT, 2]); rs = sb(es, "rs", [128, NT, 2]); Rstat = Res("stat")
        ps = es.enter_context(nc.psum_tensor("ps", [128, 3584], F32)); PB = [Res("pb%d" % i) for i in range(7)]
        pst = es.enter_context(nc.psum_tensor("pst", [128, 1024], BF16)); PT = Res("pt")
        wslot = [None, None]; Rws = [None, None]
        wctr = [0]
        def alloc_ws(st):
            for i_ in range(2):
                wslot[i_] = sb(st, "wslot%d" % i_, [128, 4096], BF16); Rws[i_] = Res("ws%d" % i_)
        def load_w(views):
            s_ = wctr[0] % 2; wctr[0] += 1
            for dstf, src in views:
                fw.dma("pool", dstf(wslot[s_]), src, writes=[Rws[s_]])
            return wslot[s_], Rws[s_]
        def bank(i, n=1):
            return ps[:, i * 512:(i + n) * 512]

        fw.dma("sp", ident_f[:], Din["ident"], writes=[Rid])
        fw.dma("pool", ident_b[:], Din["ident"], writes=[Rid])
        fw.dma("sp", maskP[:], Din["maskP"], writes=[Rmask])
        for i in range(NT):
            fw.dma("sp", X[:, i, :], Din["xin"][i * 128:(i + 1) * 128, :], writes=[RX[i]])
        fw.dma("sp", csl[:], Din["cvec"], writes=[Rcs])
        ACT(csl[:], csl[:], AF.Silu, [Rcs], [Rcs])
        CP("dve", sTb[:], csl[:], [Rcs], [Rcs])
        for v in range(2):
            CP("dve", sRep[:, v, :, :], bl(csl[:, :, v], 128), [Rcs], [Rcs])

        def build_uT(uT, RuT, sub):
            sh_sec = 0 if sub == 0 else 3
            for i in range(NT):
                v = TILE_VEC[i]
                for kk in range(2):
                    bk = kk
                    for k4 in range(4):
                        k = kk * 4 + k4
                        OP("pe", lambda e: e.transpose(bank(bk)[:, k4 * 128:(k4 + 1) * 128], X[:, i, k * 128:(k + 1) * 128], ident_f[:]),
                           [RX[i], Rid], [PB[bk]])
                    for k4 in range(4):
                        k = kk * 4 + k4
                        ACT(uT[:, k, i * 128:(i + 1) * 128], bank(bk)[:, k4 * 128:(k4 + 1) * 128], AF.Identity,
                            [PB[bk], Rmod], [RuT[i]], scale=opsc[:, sub, k, v:v + 1], bias=modP[:, sh_sec, k, v:v + 1])

        def layer_norm_residual(l, which, kc, lhsT_fn, lres_fn, wsrc3):
            with contextlib.ExitStack() as ph:
                ch_save = CH[0]; CH[0] = 'L' in os.environ.get('CHP', 'mabfFGMLUS')
                alloc_ws(ph)
                lnb = sb(ph, "lnb", [128, 2, 1024]); Rln = Res("lnb")
                tmp = [sb(ph, "lntmp%d" % j, [128, 512]) for j in range(2)]; Rtmp = [Res("lntmp0"), Res("lntmp1")]
                nWd = 2 if kc <= 8 else 1
                Wds = [sb(ph, "Wd%d" % z, [128, kc, 512], BF16) for z in range(nWd)]; RWds = [Res("Wd%d" % z) for z in range(nWd)]
                def load_half(nb):
                    z = nb % nWd
                    for k0 in range(0, kc, 8):
                        k1 = min(kc, k0 + 8)
                        fw.dma("pool", Wds[z][:, k0:k1, :], wsrc3[:, k0:k1, nb * 512:(nb + 1) * 512], writes=[RWds[z]])
                load_half(0)
                if nWd == 2:
                    load_half(1)
                fw.dma("sp", lnb[:, 0, :], Din["lnbc"][l, 2 * which], writes=[Rln])
                fw.dma("sp", lnb[:, 1, :], Din["lnbc"][l, 2 * which + 1], writes=[Rln])
                gbc = sb(ph, "gbc", [128, 2, 1024]); Rgbc = Res("gbc")
                bg = sb(ph, "bg", [128, 1024]); Rbg = Res("bg")
                fw.dma("sp", bg[:], Din["bgbc"][l, :, which, :], writes=[Rbg])
                sec = 2 if which == 0 else 5
                wsrc = Din["w_ada"][l].rearrange("(k p) n -> p k n", p=128)
                for half in range(2):
                    Wg, RWg = load_w([(lambda t: t[:, 0:4096].rearrange("p (k n) -> p k n", k=8), wsrc[:, :, sec * 1024 + half * 512:sec * 1024 + (half + 1) * 512])])
                    Wgv = Wg[:, 0:4096].rearrange("p (k n) -> p k n", k=8)
                    for v in range(2):
                        for k in range(8):
                            MM(bank(v), sRep[:, v, k, :], Wgv[:, k, :], k == 0, k == 7, [RWg, Rcs], [PB[v]])
                        TT("dve", gbc[:, v, half * 512:(half + 1) * 512], bank(v), bg[:, half * 512:(half + 1) * 512], ALU.add, [PB[v], Rbg], [Rgbc])
                cnt = 0
                for nb in range(2):
                    cs = slice(nb * 512, (nb + 1) * 512)
                    Wd = Wds[nb % nWd]; RWd = RWds[nb % nWd]
                    if nWd == 1 and nb == 1:
                        load_half(1)
                    for i in range(NT):
                        v = TILE_VEC[i]
                        bk = 2 + (cnt % 4); tj = cnt % 2; cnt += 1
                        for k in range(kc):
                            MM(bank(bk), lhsT_fn(i, k), Wd[:, k, :], k == 0, k == kc - 1, lres_fn(i, k) + [RWd], [PB[bk]])
                        TT("dve", tmp[tj][:], bank(bk), gbc[:, v, cs], ALU.mult, [PB[bk], Rgbc], [Rtmp[tj]])
                        OP("dve", lambda e: e.scalar_tensor_tensor(out=X[:, i, cs], in0=X[:, i, cs], scalar=float(ALPHA), in1=tmp[tj][:],
                                                                   op0=ALU.mult, op1=ALU.add), [RX[i], Rtmp[tj]], [RX[i]])
                for i in range(NT):
                    for hh in range(2):
                        OP("dve", lambda e: e.bn_stats(out=stat[:, i, hh, :], in_=X[:, i, hh * 512:(hh + 1) * 512]), [RX[i]], [Rstat])
                    OP("dve", lambda e: e.bn_aggr(out=mv[:, i, :], in_=stat[:, i, :, :].rearrange("p a b -> p (a b)")), [Rstat], [Rstat])
                TS("dve", rs[:, :, 0], mv[:, :, 1], EPS, ALU.add, [Rstat], [Rstat])
                ACT(rs[:, :, 0], rs[:, :, 0], AF.Sqrt, [Rstat], [Rstat])
                OP("dve", lambda e: e.reciprocal(out=rs[:, :, 0], in_=rs[:, :, 0]), [Rstat], [Rstat])
                OP("dve", lambda e: e.scalar_tensor_tensor(out=rs[:, :, 1], in0=mv[:, :, 0], scalar=-1.0, in1=rs[:, :, 0],
                                                           op0=ALU.mult, op1=ALU.mult), [Rstat], [Rstat])
                for i in range(NT):
                    ACT(X[:, i, :], X[:, i, :], AF.Identity, [RX[i], Rstat], [RX[i]], scale=rs[:, i, 0:1], bias=rs[:, i, 1:2])
                    TT("pool", X[:, i, :], X[:, i, :], lnb[:, 0, :], ALU.mult, [RX[i], Rln], [RX[i]])
                    TT("pool", X[:, i, :], X[:, i, :], lnb[:, 1, :], ALU.add, [RX[i], Rln], [RX[i]])
                fw.barrier()
                CH[0] = ch_save

        def layers():
          for l in range(n_layers):
              with contextlib.ExitStack() as ph:
                  wa = [sb(ph, "wa%d" % i, [128, 8, 512], BF16) for i in range(3)]; Rwa = [Res("wa%d" % i) for i in range(3)]
                  bP = sb(ph, "bP", [128, 48]); Rb = Res("bmod")
                  fw.dma("sp", bP[:], Din["b_adaP"][:, l, :], writes=[Rb])
                  wsrc = Din["w_ada"][l].rearrange("(k p) n -> p k n", p=128)
                  pm = bank(6)[:, 0:96]
                  CH[0] = 'm' in os.environ.get('CHP', 'mabfFGMLUS')
                  cbs = [cb for cb in range(12) if cb // 2 not in (2, 5)]
                  for i_ in range(min(3, len(cbs))):
                      fw.dma("pool", wa[i_][:], wsrc[:, :, cbs[i_] * 512:(cbs[i_] + 1) * 512], writes=[Rwa[i_]])
                  for i_, cb in enumerate(cbs):
                      s = i_ % 3
                      sec, half = cb // 2, cb % 2
                      if i_ >= 3 and False:
                          pass
                      for m in range(4):
                          col = (sec * 8 + half * 4 + m) * 2
                          for k in range(8):
                              MM(pm[:, col:col + 2], wa[s][:, k, m * 128:(m + 1) * 128], sTb[:, k, :], k == 0, k == 7,
                                 [Rwa[s], Rcs], [PB[6]])
                      if i_ + 3 < len(cbs):
                          fw.dma("pool", wa[s][:], wsrc[:, :, cbs[i_ + 3] * 512:(cbs[i_ + 3] + 1) * 512], writes=[Rwa[s]])
                  for sec in (0, 1, 3, 4):
                      TT("dve", modP[:, sec, :, :], pm[:, sec * 16:(sec + 1) * 16].rearrange("p (c v) -> p c v", v=2),
                         bl(bP[:, sec * 8:(sec + 1) * 8], 2), ALU.add, [PB[6], Rb], [Rmod])
                  TS("dve", opsc[:, 0, :, :], modP[:, 1, :, :], 1.0, ALU.add, [Rmod], [Rmod])
                  TS("dve", opsc[:, 1, :, :], modP[:, 4, :, :], 1.0, ALU.add, [Rmod], [Rmod])
                  fw.barrier()
              if stop == "mod": fw.dead = True
              DBG("d_modP", modP[:].rearrange("p a b c -> p (a b c)"), [Rmod])

              with contextlib.ExitStack() as L1:
                  QT = sb(L1, "QT", [128, 4, TOK], BF16); RQ = [Res("QT%d" % j) for j in range(4)]; attnT = QT; RattnT = RQ
                  sfT = sb(L1, "sfT", [128, 8, TOK], BF16); RsfT = [Res("sfT%d" % j) for j in range(8)]
                  fourT = sb(L1, "fourT", [128, 4, TOK], BF16); RfourT = [Res("fourT%d" % j) for j in range(4)]
                  Lp = contextlib.ExitStack()
                  lamD = [[sb(Lp, "lam%d%d" % (d_, r), [128, 528]) for r in range(2)] for d_ in range(2)]
                  ffD = [[sb(Lp, "ff%d%d" % (d_, r), [128, 528]) for r in range(2)] for d_ in range(2)]
                  PowP = [sb(Lp, "PowP%d" % r, [128, 2, 16, 17]) for r in range(2)]
                  LH = [sb(Lp, "LH%d" % r, [128, 2, 16, 7]) for r in range(2)]
                  fP = [sb(Lp, "fP%d" % r, [128, 2, 16]) for r in range(2)]
                  RS = Res("ssm")
                  def cmul(dr, di, ar, ai, br, bi, m1, m2):
                      TT("dve", m1, ar, br, ALU.mult, [RS], [RS]); TT("dve", m2, ai, bi, ALU.mult, [RS], [RS])
                      TT("dve", dr, m1, m2, ALU.subtract, [RS], [RS])
                      TT("dve", m1, ar, bi, ALU.mult, [RS], [RS]); TT("dve", m2, ai, br, ALU.mult, [RS], [RS])
                      TT("dve", di, m1, m2, ALU.add, [RS], [RS])
                  def ssm_param_pipeline(st):
                      are, aim, ldt, mag, cc, ss, t1, t2, t3 = [sb(st, "sp%d" % j_, [128, 528]) for j_ in range(9)]
                      for d in range(2):
                          lam = lamD[d]; ff = ffD[d]
                          fw.dma("sp", are[:], Din["lamin"][l, d, 0], writes=[RS])
                          fw.dma("sp", aim[:], Din["lamin"][l, d, 1], writes=[RS])
                          fw.dma("sp", ldt[:], Din["lamin"][l, d, 2], writes=[RS])
                          ACT(ldt[:], ldt[:], AF.Exp, [RS], [RS])
                          TT("dve", t1[:], are[:], ldt[:], ALU.mult, [RS], [RS])
                          ACT(mag[:], t1[:], AF.Exp, [RS], [RS])
                          TT("dve", t2[:], aim[:], ldt[:], ALU.mult, [RS], [RS])
                          ACT(ss[:], t2[:], AF.Sin, [RS], [RS], scale=1.0 / 16)
                          ACT(t1[:], t2[:], AF.Sin, [RS], [RS], scale=1.0 / 32)
                          TT("dve", t1[:], t1[:], t1[:], ALU.mult, [RS], [RS])
                          TS("dve", cc[:], t1[:], -2.0, ALU.mult, [RS], [RS], s2=1.0, op1=ALU.add)
                          for _ in range(4):
                              TT("dve", t1[:], cc[:], cc[:], ALU.mult, [RS], [RS])
                              TT("dve", t2[:], ss[:], ss[:], ALU.mult, [RS], [RS])
                              TT("dve", t3[:], cc[:], ss[:], ALU.mult, [RS], [RS])
                              TT("dve", cc[:], t1[:], t2[:], ALU.subtract, [RS], [RS])
                              TS("dve", ss[:], t3[:], 2.0, ALU.mult, [RS], [RS])
                          TT("dve", lam[0][:], mag[:], cc[:], ALU.mult, [RS], [RS])
                          TT("dve", lam[1][:], mag[:], ss[:], ALU.mult, [RS], [RS])
                          TT("dve", t1[:], are[:], are[:], ALU.mult, [RS], [RS])
                          TT("dve", t2[:], aim[:], aim[:], ALU.mult, [RS], [RS])
                          TT("dve", t1[:], t1[:], t2[:], ALU.add, [RS], [RS])
                          OP("dve", lambda e: e.reciprocal(out=t1[:], in_=t1[:]), [RS], [RS])
                          TS("dve", t2[:], lam[0][:], -1.0, ALU.add, [RS], [RS])
                          TT("dve", t3[:], t2[:], are[:], ALU.mult, [RS], [RS])
                          TT("dve", cc[:], lam[1][:], aim[:], ALU.mult, [RS], [RS])
                          TT("dve", t3[:], t3[:], cc[:], ALU.add, [RS], [RS])
                          TT("dve", ff[0][:], t3[:], t1[:], ALU.mult, [RS], [RS])
                          TT("dve", t3[:], lam[1][:], are[:], ALU.mult, [RS], [RS])
                          TT("dve", cc[:], t2[:], aim[:], ALU.mult, [RS], [RS])
                          TT("dve", t3[:], t3[:], cc[:], ALU.subtract, [RS], [RS])
                          TT("dve", ff[1][:], t3[:], t1[:], ALU.mult, [RS], [RS])
                          pm1 = t1[:, 0:256].rearrange("p (g k) -> p g k", g=16); pm2 = t2[:, 0:256].rearrange("p (g k) -> p g k", g=16)
                          OP("dve", lambda e, d=d: e.memset(PowP[0][:, d, :, 0:1], 1.0), [RS], [RS])
                          OP("dve", lambda e, d=d: e.memset(PowP[1][:, d, :, 0:1], 0.0), [RS], [RS])
                          for r in range(2):
                              TS("dve", PowP[r][:, d, :, 1], lam[r][:, 0:16], 1.0, ALU.mult, [RS], [RS])
                              TS("dve", fP[r][:, d, :], ff[r][:, 0:16], 1.0, ALU.mult, [RS], [RS])
                          for w in (1, 2, 4, 8):
                              br_ = PowP[0][:, d, :, w:w + 1].to_broadcast([128, 16, w]); bi_ = PowP[1][:, d, :, w:w + 1].to_broadcast([128, 16, w])
                              cmul(PowP[0][:, d, :, w + 1:2 * w + 1], PowP[1][:, d, :, w + 1:2 * w + 1],
                                   PowP[0][:, d, :, 1:w + 1], PowP[1][:, d, :, 1:w + 1], br_, bi_, pm1[:, :, 0:w], pm2[:, :, 0:w])
                          for r in range(2):
                              TS("dve", LH[r][:, d, :, 0], PowP[r][:, d, :, 16], 1.0, ALU.mult, [RS], [RS])
                          for m in range(6):
                              cmul(LH[0][:, d, :, m + 1], LH[1][:, d, :, m + 1], LH[0][:, d, :, m], LH[1][:, d, :, m],
                                   LH[0][:, d, :, m], LH[1][:, d, :, m], t1[:, 256:272], t2[:, 256:272])
                  with contextlib.ExitStack() as L2:
                      KT = sb(L2, "KT", [128, 2, 2048], BF16); RKT = Res("KT")
                      Vsb = sb(L2, "Vsb", [128, 16, 2, 2, 128], BF16); RV = Res("V")
                      with contextlib.ExitStack() as ph:
                          alloc_ws(ph)
                          uT = sb(ph, "uT", [128, 8, TOK], BF16); RuT = [Res("uT%d" % i) for i in range(NT)]
                          qn = [sb(ph, "qn%d" % j, [128, 512]) for j in range(2)]; Rqn = [Res("qn0"), Res("qn1")]
                          vf = [sb(ph, "vf%d" % j, [128, 128]) for j in range(2)]; Rvf = [Res("vf0"), Res("vf1")]
                          qkb = [sb(ph, "qkb%d" % j, [128, 512], BF16) for j in range(2)]; Rqkb = [Res("qkb0"), Res("qkb1")]
                          sq = sb(ph, "sq", [128, 512]); Rsq = Res("sq")
                          kb2 = [sb(ph, "kb2%d" % j, [128, 128], BF16) for j in range(2)]; Rkb2 = [Res("kb20"), Res("kb21")]
                          ssq = sb(ph, "ssq", [128, 8]); Rssq = Res("ssq")
                          g640 = sb(ph, "g640", [128, 768]); rc = sb(ph, "rc", [128, 8, 32]); rsn = sb(ph, "rsn", [128, 8, 32]); Rcst = Res("cst")
                          rt = [sb(ph, "rt%d" % j, [128, 8, 32]) for j in range(2)]; Rrt = [Res("rt0"), Res("rt1")]
                          win = Din["w_in"][l].rearrange("(k p) n -> p k n", p=128)
                          WA_pre = load_w([(lambda t: t[:, 0:4096].rearrange("p (k n) -> p k n", k=8), win[:, :, 0:512])])
                          WB_pre = load_w([(lambda t: t[:, 0:2048].rearrange("p (k n) -> p k n", k=8), win[:, :, 512:768])])
                          fw.dma("sp", g640[:], Din["g640"][l], writes=[Rcst])
                          fw.dma("sp", rc[:], Din["ropec"], writes=[Rcst])
                          fw.dma("sp", rsn[:], Din["ropes"], writes=[Rcst])
                          fw.dma("pool", KT[:, 0, 1024:1536], Din["ckT"][l], writes=[RKT])
                          fw.dma("pool", KT[0:64, 1, 1024:1536], Din["ckT"][l, 64:128, :], writes=[RKT])
                          fw.dma("pool", KT[64:128, 1, 1024:1536], Din["ckT"][l, 0:64, :], writes=[RKT])
                          OP("dve", lambda e: e.memset(Vsb[:].rearrange("p a b c d -> p (a b c d)"), 1.0), [], [RV])
                          cvv = Din["cv"][l].rearrange("(t p) (h d) -> p t h d", p=128, h=2)
                          for lay in range(2):
                              for kvh_ in range(2):
                                  fw.dma("pool", Vsb[:, 8:12, kvh_, lay, lay * 64:(lay + 1) * 64], cvv[:, :, kvh_, :], writes=[RV])
                          build_uT(uT, RuT, 0)
                          if stop == "ut": fw.dead = True
                          DBG("d_uT", uT[:].rearrange("p a b -> p (a b)"), RuT)

                          def norm_rope(i, pa, pr, nh, gap, outb, Routb, slot):
                              w_ = nh * 64
                              ACT(sq[:, 0:w_], pa, AF.Square, pr, [Rsq])
                              OP("dve", lambda e: e.reduce_sum(out=ssq[:, 0:nh], in_=sq[:, 0:w_].rearrange("p (h d) -> p h d", d=64), axis=AX.X), [Rsq], [Rssq])
                              TS("dve", ssq[:, 0:nh], ssq[:, 0:nh], 1.0 / 64, ALU.mult, [Rssq], [Rssq], s2=EPS, op1=ALU.add)
                              ACT(ssq[:, 0:nh], ssq[:, 0:nh], AF.Sqrt, [Rssq], [Rssq])
                              OP("dve", lambda e: e.reciprocal(out=ssq[:, 0:nh], in_=ssq[:, 0:nh]), [Rssq], [Rssq])
                              qv = qn[slot][:, 0:w_]
                              TT("dve", qv.rearrange("p (h d) -> p h d", d=64), pa.rearrange("p (h d) -> p h d", d=64), bl(ssq[:, 0:nh], 64),
                                 ALU.mult, pr + [Rssq], [Rqn[slot]])
                              TT("dve", qv, qv, gap, ALU.mult, [Rqn[slot], Rcst], [Rqn[slot]])
                              if i >= 8:
                                  CP("act", outb, qv, [Rqn[slot]], [Routb])
                              else:
                                  x4 = qv.rearrange("p (h d two) -> p h d two", d=32, two=2)
                                  o4 = outb.rearrange("p (h d two) -> p h d two", d=32, two=2)
                                  x0, x1 = x4[:, :, :, 0], x4[:, :, :, 1]
                                  cc, ss_ = bm(rc[:, i, :], nh), bm(rsn[:, i, :], nh)
                                  r0, r1 = rt[0][:, 0:nh, :], rt[1][:, 0:nh, :]
                                  TT("dve", r0, x0, cc, ALU.mult, [Rqn[slot], Rcst], [Rrt[0]])
                                  TT("pool", r1, x1, ss_, ALU.mult, [Rqn[slot], Rcst], [Rrt[1]])
                                  TT("dve", o4[:, :, :, 0], r0, r1, ALU.subtract, Rrt, [Routb])
                                  TT("dve", r0, x0, ss_, ALU.mult, [Rqn[slot], Rcst], [Rrt[0]])
                                  TT("pool", r1, x1, cc, ALU.mult, [Rqn[slot], Rcst], [Rrt[1]])
                                  TT("dve", o4[:, :, :, 1], r0, r1, ALU.add, Rrt, [Routb])

                          CH[0] = 'a' in os.environ.get('CHP', 'mabfFGMLUS')
                          W, RW = WA_pre
                          Wv = W[:, 0:4096].rearrange("p (k n) -> p k n", k=8)
                          for i in range(NT):
                              bk = 2 + (i % 2); sl = i % 2
                              for k in range(8):
                                  MM(bank(bk), uT[:, k, i * 128:(i + 1) * 128], Wv[:, k, :], k == 0, k == 7, [RuT[i], RW], [PB[bk]])
                              norm_rope(i, bank(bk), [PB[bk]], 8, g640[:, 0:512], qkb[sl][:, :], Rqkb[sl], sl)
                              for j in range(4):
                                  src = qkb[sl][:, j * 128:(j + 1) * 128]
                                  OP("pe", lambda e: e.transpose(pst[:, j * 128:(j + 1) * 128], src, ident_b[:]), [Rqkb[sl], Rid], [PT])
                              CP("act", QT[:, :, i * 128:(i + 1) * 128], pst[:, 0:512].rearrange("p (j c) -> p j c", j=4), [PT], RQ)
                          if stop == "passA": fw.dead = True
                          CH[0] = 'b' in os.environ.get('CHP', 'mabfFGMLUS')
                          W, RW = WB_pre
                          WF_pre = load_w([(lambda t: t[:, 0:4096].rearrange("p (k n) -> p k n", k=8), win[:, :, 768:1280])])
                          Wv = W[:, 0:2048].rearrange("p (k n) -> p k n", k=8)
                          for i in range(NT):
                              bk = 2 + (i % 2); sl = i % 2
                              for k in range(8):
                                  MM(bank(bk)[:, 0:256], uT[:, k, i * 128:(i + 1) * 128], Wv[:, k, :], k == 0, k == 7, [RuT[i], RW], [PB[bk]])
                              norm_rope(i, bank(bk)[:, 0:256], [PB[bk]], 4, g640[:, 512:768], qkb[sl][:, 0:256], Rqkb[sl], sl)
                              kt = i if i < 8 else 12 + (i - 8)
                              for lay in range(2):
                                  CP("act", Vsb[:, kt, :, lay, lay * 64:(lay + 1) * 64], bank(bk)[:, 128:256].rearrange("p (h d) -> p h d", h=2), [PB[bk]], [RV])
                              if i >= 8:
                                  p_, t_ = (i - 8) // 2, (i - 8) % 2
                                  CP("act", vf[sl][:], bank(bk)[:, 128:256], [PB[bk]], [Rvf[sl]])
                                  fw.dma("sp", Dout["nk"][p_, l, t_ * 128:(t_ + 1) * 128, :], qn[sl][:, 0:128], reads=[Rqn[sl]], writes=[Rdbg])
                                  fw.dma("sp", Dout["nv"][p_, l, t_ * 128:(t_ + 1) * 128, :], vf[sl][:], reads=[Rvf[sl]], writes=[Rdbg])
                              CP("act", kb2[sl][:, 0:64], qkb[sl][:, 64:128], [Rqkb[sl]], [Rkb2[sl]])
                              CP("act", kb2[sl][:, 64:128], qkb[sl][:, 0:64], [Rqkb[sl]], [Rkb2[sl]])
                              OP("pe", lambda e: e.transpose(pst[:, 512:640], qkb[sl][:, 0:128], ident_b[:]), [Rqkb[sl], Rid], [PT])
                              OP("pe", lambda e: e.transpose(pst[:, 640:768], kb2[sl][:, :], ident_b[:]), [Rkb2[sl], Rid], [PT])
                              kc = i * 128 if i < 8 else 1536 + (i - 8) * 128
                              CP("act", KT[:, 0, kc:kc + 128], pst[:, 512:640], [PT], [RKT])
                              CP("act", KT[:, 1, kc:kc + 128], pst[:, 640:768], [PT], [RKT])
                          if stop == "passB": fw.dead = True
                          CH[0] = 'f' in os.environ.get('CHP', 'mabfFGMLUS')
                          for wb in range(2):
                              W, RW = WF_pre
                              if wb == 0:
                                  WF_pre = load_w([(lambda t: t[:, 0:4096].rearrange("p (k n) -> p k n", k=8), win[:, :, 1280:1792])])
                              Wv = W[:, 0:4096].rearrange("p (k n) -> p k n", k=8)
                              for m4 in range(4):
                                  m = wb * 4 + m4
                                  for tb, (c0, cn) in enumerate(BLKS):
                                      bk = (0, 1, 6)[(m * 3 + tb) % 3]
                                      for k in range(8):
                                          MM(bank(bk), Wv[:, k, m4 * 128:(m4 + 1) * 128], uT[:, k, c0:c0 + cn], k == 0, k == 7,
                                             RuT[c0 // 128:(c0 + cn) // 128] + [RW], [PB[bk]])
                                      CP("act", sfT[:, m, c0:c0 + cn], bank(bk), [PB[bk]], [RsfT[m]])
                          DBG("d_sfT", sfT[:].rearrange("p a b -> p (a b)"), RsfT)
                          fw.barrier()
                      if stop == "inproj": fw.dead = True
                      DBG("d_QT", QT[:].rearrange("p a b -> p (a b)"), RQ)
                      DBG("d_KT", KT[:].rearrange("p a b -> p (a b)"), [RKT])
                      CH[0] = 't' in os.environ.get('CHP', 'mabfFGMLUS')
                      with contextlib.ExitStack() as ph:
                          fw.rec = []
                          ssm_param_pipeline(ph)
                          pipe_items = fw.rec if fw.rec is not None else []; fw.rec = None
                          per_unit = (len(pipe_items) + 23) // 24
                          Eb = [sb(ph, "Eb%d" % j, [128, 512], BF16) for j in range(3)]; REb = [Res("Eb%d" % j) for j in range(3)]
                          rsb = [sb(ph, "rsb%d" % j, [128, 512]) for j in range(2)]; Rrsb = [Res("rsb0"), Res("rsb1")]
                          cnt = 0; cnt2 = 0
                          for si, (t0, Ls) in enumerate(SEQS):
                              if si == 0:
                                  qblks = [(0, 512), (512, 512)]; nkt = 12; kt0 = 0; kc0 = 0
                              else:
                                  qblks = [(t0, 256)]; nkt = 2; kt0 = 12 + 2 * (si - 1); kc0 = 1536 + 256 * (si - 1)
                              for h in range(8):
                                  j = h // 2; hh = h % 2; kvh = h // 4; var = 0 if kvh == hh else 1
                                  pv, sm = (slice(0, 64), slice(64, 128)) if hh == 0 else (slice(64, 128), slice(0, 64))
                                  for (q0, qn) in qblks:
                                      pvb = 3 + (cnt2 % 2); cnt2 += 1
                                      def S_mm(kt, sbk):
                                          MM(bank(sbk)[:, 0:qn], KT[hh * 64:(hh + 1) * 64, var, kc0 + kt * 128:kc0 + (kt + 1) * 128],
                                             QT[hh * 64:(hh + 1) * 64, j, q0:q0 + qn], True, True, [RKT, RQ[j]], [PB[sbk]], chain=False)
                                      slots = [(cnt + i_) % 3 for i_ in range(nkt)]; cnt += nkt
                                      S_mm(0, slots[0])
                                      for kt in range(nkt):
                                          sbk = slots[kt]; eb = sbk
                                          if kt + 1 < nkt:
                                              S_mm(kt + 1, slots[kt + 1])
                                          ACT(Eb[eb][:, 0:qn], bank(sbk)[:, 0:qn], AF.Exp, [PB[sbk]], [REb[eb]], scale=0.125)
                                          MM(bank(pvb)[:, 0:qn], Vsb[:, kt0 + kt, kvh, hh, :], Eb[eb][:, 0:qn], kt == 0, kt == nkt - 1,
                                             [RV, REb[eb]], [PB[pvb]], chain=False)
                                      rb = cnt2 % 2
                                      ACT(rsb[rb][sm, 0:qn], bank(pvb)[sm, 0:qn], AF.Ln, [PB[pvb]], [Rrsb[rb]])
                                      ACT(rsb[rb][sm, 0:qn], rsb[rb][sm, 0:qn], AF.Exp, [Rrsb[rb]], [Rrsb[rb]], scale=-1.0)
                                      TT("dve", attnT[pv, j, q0:q0 + qn], bank(pvb)[pv, 0:qn], rsb[rb][sm, 0:qn], ALU.mult,
                                         [PB[pvb], Rrsb[rb]], [RattnT[j]])
                                      fw.replay(pipe_items[:per_unit]); del pipe_items[:per_unit]
                          fw.replay(pipe_items); del pipe_items[:]
                          fw.barrier()
                  DBG("d_attnT", attnT[:].rearrange("p a b -> p (a b)"), RattnT)
                  CH[0] = 'F' in os.environ.get('CHP', 'mabfFGMLUS')
                  with contextlib.ExitStack() as L3:
                    with contextlib.ExitStack() as ph:
                        cs128 = sb(ph, "cs128", [128, 256], BF16); cl = sb(ph, "cl", [128, 8, 1024], BF16); sl_ = sb(ph, "sl", [128, 8, 1024], BF16)
                        clp = sb(ph, "clp", [128, 2, 256], BF16); slp = sb(ph, "slp", [128, 2, 256], BF16); Rft = Res("ftab")
                        Pfm = sb(ph, "Pfm", [128, NT, 1024], BF16); RPfm = [Res("Pfm%d" % i) for i in range(NT)]
                        fw.dma("pool", cs128[:], Din["cs128"], writes=[Rft])
                        for k in range(8):
                            fw.dma("pool", cl[:, k, :], Din["cl1024"][:, k, :], writes=[Rft])
                            fw.dma("pool", sl_[:, k, :], Din["sl1024"][:, k, :], writes=[Rft])
                        fw.dma("pool", clp[:], Din["cl256"], writes=[Rft])
                        fw.dma("pool", slp[:], Din["sl256"], writes=[Rft])
                        for i in range(NT):
                            b0 = 2 + 2 * (i % 2)
                            for g in range(4):
                                MM(bank(b0 + g // 2)[:, (g % 2) * 256:(g % 2 + 1) * 256], sfT[:, 4 + g, i * 128:(i + 1) * 128], cs128[:], True, True,
                                   [RsfT[4 + g], Rft], [PB[b0 + g // 2]])
                            CP("act", Pfm[:, i, :], bank(b0, 2), [PB[b0], PB[b0 + 1]], [RPfm[i]])
                        cnt = 0
                        for si, (t0, Ls) in enumerate(SEQS):
                            ntl = Ls // 128; tb0 = t0 // 128
                            tc_, ts_ = (cl, sl_) if si == 0 else (clp, slp)
                            for g in range(4):
                                for lb in range(0, Ls, 512):
                                    n = min(512, Ls - lb)
                                    bk = (0, 1, 6)[cnt % 3]; cnt += 1
                                    for k in range(ntl):
                                        MM(bank(bk)[:, 0:n], Pfm[:, tb0 + k, g * 256:g * 256 + 128], tc_[:, k, lb:lb + n], k == 0, False,
                                           [RPfm[tb0 + k], Rft], [PB[bk]])
                                        MM(bank(bk)[:, 0:n], Pfm[:, tb0 + k, g * 256 + 128:g * 256 + 256], ts_[:, k, lb:lb + n], False, k == ntl - 1,
                                           [RPfm[tb0 + k], Rft], [PB[bk]])
                                    CP("act", fourT[:, g, t0 + lb:t0 + lb + n], bank(bk)[:, 0:n], [PB[bk]], [RfourT[g]])
                        fw.barrier()
                    DBG("d_fourT", fourT[:].rearrange("p a b -> p (a b)"), RfourT)
                    if stop == "four": fw.dead = True
                    CH[0] = 'S' in os.environ.get('CHP', 'mabfFGMLUS')
                    with contextlib.ExitStack() as S:
                        Hb = sb(S, "Hb", [128, 2, 2, 16, NS], BF16)
                        h0 = sb(S, "h0", [128, 2, 2, 16]); stg = sb(S, "stg", [128, 16, 2])
                        RHb = Res("Hb"); Rstg = Res("stg")
                        fw.dma("sp", h0[:], Din["h0"][:, l], writes=[RS])
                        for d in range(2):
                            with contextlib.ExitStack() as SD:
                                lam = lamD[d]; ff = ffD[d]
                                A = [sb(SD, "A%d" % r, [128, 16, NS]) for r in range(2)]
                                for r in range(2):
                                    OP("dve", lambda e: e.memset(A[r][:].rearrange("p a b -> p (a b)"), 0.0), [RS], [RS])
                                with contextlib.ExitStack() as SW:
                                    W = [sb(SW, "W%d" % r, [128, 16, 128]) for r in range(2)]
                                    WT2 = [[sb(SW, "WT%d_%d" % (z, r), [128, 16, 128], BF16) for r in range(2)] for z in range(2)]
                                    wm = [sb(SW, "wm%d" % r, [128, 8, 128]) for r in range(3)]
                                    BT = [sb(SW, "BT%d" % r, [128, 128]) for r in range(2)]
                                    Lw = [sb(SW, "Lw%d" % r, [128, 128]) for r in range(2)]
                                    RWT2 = [Res("WTa"), Res("WTb")]
                                    def build_W(q):
                                        fs = slice(16 + q * 128, 16 + (q + 1) * 128)
                                        for r in range(2):
                                            fw.dma("sp", BT[r][:], Din["BbdT"][l, d, r, :, q, :], writes=[RS])
                                            ACT(Lw[r][:], lam[r][:, fs], AF.Identity, [RS], [RS])
                                        cmul(W[0][:, 0, :], W[1][:, 0, :], ff[0][:, fs], ff[1][:, fs], BT[0][:], BT[1][:], wm[0][:, 0, :], wm[1][:, 0, :])
                                        for w in (1, 2, 4, 8):
                                            cmul(W[0][:, w:2 * w, :], W[1][:, w:2 * w, :], bm(Lw[0][:], w), bm(Lw[1][:], w),
                                                 W[0][:, 0:w, :], W[1][:, 0:w, :], wm[0][:, 0:w, :], wm[1][:, 0:w, :])
                                            if w < 8:
                                                TT("dve", wm[0][:, 0, :], Lw[0][:], Lw[0][:], ALU.mult, [RS], [RS])
                                                TT("dve", wm[1][:, 0, :], Lw[1][:], Lw[1][:], ALU.mult, [RS], [RS])
                                                TT("dve", wm[2][:, 0, :], Lw[0][:], Lw[1][:], ALU.mult, [RS], [RS])
                                                TT("dve", Lw[0][:], wm[0][:, 0, :], wm[1][:, 0, :], ALU.subtract, [RS], [RS])
                                                TS("dve", Lw[1][:], wm[2][:, 0, :], 2.0, ALU.mult, [RS], [RS])
                                        for r in range(2):
                                            ACT(WT2[q % 2][r][:].rearrange("p a b -> p (a b)"), W[r][:].rearrange("p a b -> p (a b)"), AF.Identity, [RS], [RWT2[q % 2]])
                                    def run_S(q):
                                        WTq = WT2[q % 2]; RWTq = RWT2[q % 2]
                                        Sps = bank(0, 2).rearrange("p (b r c) -> p b r c", b=4, r=2)
                                        for b in range(4):
                                            sv = sfT[32 * b:32 * b + 32, q, :].rearrange("p (c j) -> p c j", j=16)
                                            for r in range(2):
                                                for k in range(16):
                                                    j = 15 - k if d == 0 else k
                                                    MM(Sps[:, b, r, 0:96], WTq[r][32 * b:32 * b + 32, k, :], sv[:, :, j], k == 0, k == 15,
                                                       [RWTq, RsfT[q]], [PB[0], PB[1]], tp=(32 * b, 0))
                                        for r in range(2):
                                            for s_i in range(3):
                                                n = SCH[s_i]; c0 = (0, 64, 80)[s_i]; lo = SOFF[s_i] + (1 if d == 0 else 0)
                                                ACT(A[r][:, 4 * q:4 * q + 4, lo:lo + n], Sps[:, :, r, c0:c0 + n], AF.Identity, [PB[0], PB[1]], [RS])
                                    build_W(0)
                                    for q in range(4):
                                        if q + 1 < 4:
                                            build_W(q + 1)
                                        run_S(q)
                                RSp = Res("ssm_pool"); NPD = 10
                                for r in range(2):
                                    ACT(A[r][:, :, 0 if d == 0 else 64], h0[:, d, r, :], AF.Identity, [RS], [RS, RSp])
                                with contextlib.ExitStack() as SC:
                                    sm = [sb(SC, "sm%d" % r, [128, 16, NS]) for r in range(3)]
                                    for m in range(7):
                                        dd = 1 << m
                                        groups = []
                                        if dd < 65:
                                            groups.append((lambda t_, ps_, lo, hi: t_[:, ps_, lo:hi], 65, None))
                                        if dd < 17:
                                            groups.append((lambda t_, ps_, lo, hi: t_[:, ps_, 65:99].rearrange("p g (s c) -> p g s c", s=2)[:, :, :, lo:hi], 17, 2))
                                        for view, n, ns in groups:
                                            cntn = n - dd
                                            (dlo, dhi, slo, shi) = (dd, n, 0, cntn) if d == 0 else (0, cntn, dd, n)
                                            for eng_, ps_, Rr in (("dve", slice(0, NPD), RS), ("pool", slice(NPD, 16), RSp)):
                                                npz = ps_.stop - ps_.start
                                                if ns is None:
                                                    Lr = bl(LH[0][:, d, ps_, m], cntn); Li = bl(LH[1][:, d, ps_, m], cntn)
                                                    tv = lambda t_: t_[:, ps_, 0:cntn]
                                                else:
                                                    Lr = LH[0][:, d, ps_, m].unsqueeze(2).unsqueeze(3).to_broadcast([128, npz, 2, cntn])
                                                    Li = LH[1][:, d, ps_, m].unsqueeze(2).unsqueeze(3).to_broadcast([128, npz, 2, cntn])
                                                    tv = lambda t_: t_[:, ps_, 0:2 * cntn].rearrange("p g (s c) -> p g s c", s=2)
                                                sr, si = view(A[0], ps_, slo, shi), view(A[1], ps_, slo, shi)
                                                dr, di = view(A[0], ps_, dlo, dhi), view(A[1], ps_, dlo, dhi)
                                                m1, m2, m3 = tv(sm[0]), tv(sm[1]), tv(sm[2])
                                                TT(eng_, m1, sr, Lr, ALU.mult, [Rr], [Rr]); TT(eng_, m2, si, Li, ALU.mult, [Rr], [Rr])
                                                TT(eng_, m1, m1, m2, ALU.subtract, [Rr], [Rr])
                                                TT(eng_, m2, si, Lr, ALU.mult, [Rr], [Rr]); TT(eng_, m3, sr, Li, ALU.mult, [Rr], [Rr])
                                                TT(eng_, m2, m2, m3, ALU.add, [Rr], [Rr])
                                                TT(eng_, dr, dr, m1, ALU.add, [Rr], [Rr]); TT(eng_, di, di, m2, ALU.add, [Rr], [Rr])
                                for r in range(2):
                                    ACT(Hb[:, d, r, :, :], A[r][:], AF.Identity, [RS, RSp], [RHb])
                                for s_i in (1, 2):
                                    idx = SOFF[s_i] + (16 if d == 0 else 0)
                                    for r in range(2):
                                        ACT(stg[:, :, r], A[r][:, :, idx], AF.Identity, [RS, RSp], [Rstg])
                                    fw.dma("sp", Dout["nst"][s_i - 1, l, d].rearrange("(pair g2) p r -> (g2 p) pair r", g2=2), stg[:], reads=[Rstg], writes=[Rdbg])
                                DBG("d_A%d" % d, A[0][:].rearrange("p a b -> p (a b)"), [RS, RSp])
                                fw.barrier()
                        with contextlib.ExitStack() as SQ:
                            CL2 = [[[sb(SQ, "CL%d%d%d" % (z, d, r), [128, 4, 17, 32], BF16) for r in range(2)] for d in range(2)] for z in range(2)]
                            Kblk = [sb(SQ, "Kblk%d" % d, [128, 16, 128], BF16) for d in range(2)]
                            Bb2 = [[[sb(SQ, "Bb%d%d%d" % (z, d, r), [128, 4, 32], BF16) for r in range(2)] for d in range(2)] for z in range(2)]
                            Dbl = sb(SQ, "Dbl", [128, 4, 128], BF16)
                            cm = [sb(SQ, "cm%d" % j, [128, 2, 17, 32]) for j in range(2)]
                            Cl = [sb(SQ, "Cl%d" % r, [128, 4, 32]) for r in range(2)]
                            Bl = [sb(SQ, "Bl%d" % r, [128, 4, 32]) for r in range(2)]
                            bt = [sb(SQ, "bt%d" % r, [128, 4, 32]) for r in range(2)]
                            gt = [sb(SQ, "gt%d" % r, [128, 512]) for r in range(2)]; Rgt = [Res("gt0"), Res("gt1")]
                            RCL2 = [Res("CLa"), Res("CLb")]; RK = Res("Kblk"); RDbl = Res("Dbl")
                            ydb = sb(SQ, "ydb", [128, 512]); Rydb = Res("ydb")
                            fw.dma("pool", Dbl[:], Din["Dblk"][l], writes=[RDbl])

                            def build_CL(q):
                                z = q % 2; CL = CL2[z]; Bb = Bb2[z]; RCL = RCL2[z]
                                ps4 = slice(4 * q, 4 * q + 4)
                                for d in range(2):
                                    for r in range(2):
                                        fw.dma("sp", Cl[r][:], Din["CbdP"][l, d, r, :, ps4, :], writes=[RS])
                                        fw.dma("sp", Bl[r][:], Din["BbdP"][l, d, r, :, ps4, :], writes=[RS])
                                    for hp in range(2):
                                        pp = slice(2 * hp, 2 * hp + 2); pq = slice(4 * q + 2 * hp, 4 * q + 2 * hp + 2)
                                        Pr = PowP[0][:, d, pq, :].unsqueeze(3).to_broadcast([128, 2, 17, 32])
                                        Pi = PowP[1][:, d, pq, :].unsqueeze(3).to_broadcast([128, 2, 17, 32])
                                        Cr = Cl[0][:, pp, :].unsqueeze(2).to_broadcast([128, 2, 17, 32])
                                        Ci = Cl[1][:, pp, :].unsqueeze(2).to_broadcast([128, 2, 17, 32])
                                        TT("dve", cm[0][:], Pr, Cr, ALU.mult, [RS], [RS]); TT("dve", cm[1][:], Pi, Ci, ALU.mult, [RS], [RS])
                                        TT("dve", CL[d][0][:, pp, :, :], cm[0][:], cm[1][:], ALU.subtract, [RS], [RCL])
                                        TT("dve", cm[0][:], Pr, Ci, ALU.mult, [RS], [RS]); TT("dve", cm[1][:], Pi, Cr, ALU.mult, [RS], [RS])
                                        TT("dve", cm[0][:], cm[0][:], cm[1][:], ALU.add, [RS], [RS])
                                        TS("dve", CL[d][1][:, pp, :, :], cm[0][:], -1.0, ALU.mult, [RS], [RCL])
                                    fr = bl(fP[0][:, d, ps4], 32); fi = bl(fP[1][:, d, ps4], 32)
                                    TT("dve", bt[0][:], fr, Bl[0][:], ALU.mult, [RS], [RS]); TT("dve", bt[1][:], fi, Bl[1][:], ALU.mult, [RS], [RS])
                                    TT("dve", Bb[d][0][:], bt[0][:], bt[1][:], ALU.subtract, [RS], [RCL])
                                    TT("dve", bt[0][:], fr, Bl[1][:], ALU.mult, [RS], [RS]); TT("dve", bt[1][:], fi, Bl[0][:], ALU.mult, [RS], [RS])
                                    TT("dve", Bb[d][1][:], bt[0][:], bt[1][:], ALU.add, [RS], [RCL])

                            def build_K(q):
                                z = q % 2; CL = CL2[z]; Bb = Bb2[z]; RCL = RCL2[z]
                                for d in range(2):
                                    kb = d
                                    for b in range(4):
                                        MM(bank(kb)[32 * b:32 * b + 32, :], Bb[d][0][:, b, :], CL[d][0][:, b, 0:16, :].rearrange("p t n -> p (t n)"), True, False,
                                           [RCL], [PB[kb]], tp=(0, 32 * b))
                                        MM(bank(kb)[32 * b:32 * b + 32, :], Bb[d][1][:, b, :], CL[d][1][:, b, 0:16, :].rearrange("p t n -> p (t n)"), False, True,
                                           [RCL], [PB[kb]], tp=(0, 32 * b))
                                    for cb in range(4):
                                        TS("dve", Kblk[d][:, :, 32 * cb:32 * cb + 32], bank(kb).rearrange("p (t n) -> p t n", n=32), maskP[:, cb:cb + 1], ALU.mult,
                                           [PB[kb], Rmask], [RK])

                            def run_y(q):
                                z = q % 2; CL = CL2[z]; RCL = RCL2[z]
                                for tb, (c0, cn) in enumerate(BLKS):
                                    yb = 3 + tb
                                    Yb = bank(yb).rearrange("p (c j) -> p c j", j=16)
                                    sblk = sfT[:, q, c0:c0 + cn].rearrange("p (c j) -> p c j", j=16)
                                    MM(bank(yb), Dbl[:, q, :], sfT[:, q, c0:c0 + cn], True, False, [RDbl, RsfT[q]], [PB[yb]])
                                    for d in range(2):
                                        for t in range(16):
                                            if d == 0:
                                                o_, r_ = Yb[:, :, t:16], sblk[:, :, 0:16 - t]
                                            else:
                                                o_, r_ = Yb[:, :, 0:16 - t], sblk[:, :, t:16]
                                            MM(o_, Kblk[d][:, t, :], r_, False, False, [RK, RsfT[q]], [PB[yb]])
                                    GC = os.environ.get('GC', '1') == '1'
                                    for d in range(2):
                                        for j in range(16):
                                            tt = j + 1 if d == 0 else 16 - j
                                            sh = 0 if d == 0 else 1
                                            for r in range(2):
                                                for b in range(4):
                                                    gfirst = (b == 0 and r == 0 and j == 0 and d == 0)
                                                    if tb < 2:
                                                        rhs = Hb[:, d, r, 4 * q + b, 32 * tb + sh:32 * tb + sh + 32]
                                                        o_ = Yb[32 * b:32 * b + 32, :, j]
                                                    else:
                                                        rhs = Hb[:, d, r, 4 * q + b, 65:99].rearrange("p (s c) -> p s c", s=2)[:, :, sh:sh + 16]
                                                        o_ = bank(yb).rearrange("p (s c j) -> p s c j", s=2, j=16)[32 * b:32 * b + 32, :, :, j]
                                                    last = (d == 1 and b == 3 and j == 15 and r == 1)
                                                    MM(o_, CL[d][r][:, b, tt, :], rhs, False, last, [RCL, RHb], [PB[yb]], tp=(0, 32 * b),
                                                       chain=(("gfirst" if gfirst else "gnext") if GC else False))
                                    if ("d_y%d_%d" % (q, tb)) in Ddbg:
                                        ACT(ydb[:], bank(yb), AF.Identity, [PB[yb]], [Rydb])
                                        DBG("d_y%d_%d" % (q, tb), ydb[:], [Rydb])
                                    g_ = tb % 2
                                    ACT(gt[g_][:], bank(yb), AF.Square, [PB[yb]], [Rgt[g_]])
                                    TS("dve", gt[g_][:], gt[g_][:], 0.044715, ALU.mult, [Rgt[g_]], [Rgt[g_]], s2=1.0, op1=ALU.add)
                                    TT("dve", gt[g_][:], gt[g_][:], bank(yb), ALU.mult, [Rgt[g_], PB[yb]], [Rgt[g_]])
                                    ACT(gt[g_][:], gt[g_][:], AF.Sigmoid, [Rgt[g_]], [Rgt[g_]], scale=1.5957691216)
                                    TT("dve", sfT[:, q, c0:c0 + cn], gt[g_][:], bank(yb), ALU.mult, [Rgt[g_], PB[yb]], [RsfT[q]])
                            build_CL(0); build_K(0)
                            for q in range(4):
                                if q + 1 < 4:
                                    build_CL(q + 1)
                                run_y(q)
                                if q + 1 < 4:
                                    build_K(q + 1)
                            fw.barrier()
                    Lp.close()
                    ssmT = sb(L3, "ssmT", [128, 4, TOK], BF16); RssmT = [Res("ssmT%d" % j) for j in range(4)]
                    CH[0] = 'G' in os.environ.get('CHP', 'mabfFGMLUS')
                    with contextlib.ExitStack() as ph:
                        alloc_ws(ph)
                        bgl = sb(ph, "bgl", [128, 4]); Rbgl = Res("bgl"); sg = [sb(ph, "sg%d" % j, [128, 512]) for j in range(2)]; Rsg = [Res("sg0"), Res("sg1")]
                        fw.dma("sp", bgl[:], Din["bglu"][:, l, :], writes=[Rbgl])
                        Wg, RWg = load_w([(lambda t: t[:, 0:2048].rearrange("p (k n) -> p k n", k=4), Din["w_glu"][l].rearrange("(k p) n -> p k n", p=128))])
                        Wgv = Wg[:, 0:2048].rearrange("p (k n) -> p k n", k=4)
                        cnt = 0
                        for m in range(4):
                            for tb, (c0, cn) in enumerate(BLKS):
                                bk = (0, 1, 2)[cnt % 3]; sj = cnt % 2; cnt += 1
                                for k in range(4):
                                    MM(bank(bk), Wgv[:, k, m * 128:(m + 1) * 128], sfT[:, k, c0:c0 + cn], k == 0, k == 3, [RWg, RsfT[k]], [PB[bk]])
                                ACT(sg[sj][:], bank(bk), AF.Sigmoid, [PB[bk], Rbgl], [Rsg[sj]], bias=bgl[:, m:m + 1])
                                TT("dve", ssmT[:, m, c0:c0 + cn], sg[sj][:], sfT[:, m, c0:c0 + cn], ALU.mult, [Rsg[sj], RsfT[m]], [RssmT[m]])
                        fw.barrier()
                    DBG("d_gy", sfT[:, 0:4, :].rearrange("p a b -> p (a b)"), RsfT[0:4])
                    DBG("d_ssmT", ssmT[:].rearrange("p a b -> p (a b)"), RssmT)
                    if stop == "ssm": fw.dead = True
                    CH[0] = 'M' in os.environ.get('CHP', 'mabfFGMLUS')
                    with contextlib.ExitStack() as M:
                        mergedT = sb(M, "mergedT", [128, 8, TOK], BF16); Rmg = [Res("mg%d" % j) for j in range(8)]
                        with contextlib.ExitStack() as ph:
                            alloc_ws(ph)
                            uT = sb(ph, "uT", [128, 8, TOK], BF16); RuT = [Res("uT%d" % i) for i in range(NT)]
                            gs = [sb(ph, "gs%d" % j, [128, 512]) for j in range(3)]; Rgs = [Res("gs%d" % j) for j in range(3)]
                            ac = [sb(ph, "ac%d" % j, [128, 512]) for j in range(2)]; Rac = [Res("ac0"), Res("ac1")]
                            build_uT(uT, RuT, 0)
                            win = Din["w_in"][l].rearrange("(k p) n -> p k n", p=128)
                            wbr = [Din[nm][l].rearrange("(k p) n -> p k n", p=128) for nm in ("w_br_attn", "w_br_ssm", "w_br_four")]
                            srcs = [(attnT, RattnT), (ssmT, RssmT), (fourT, RfourT)]
                            Wg2 = [sb(ph, "Wg2_%d" % z, [128, 3072], BF16) for z in range(2)]; RWg2 = [Res("Wg2a"), Res("Wg2b")]
                            Wb2 = [sb(ph, "Wb2_%d" % z, [128, 1536], BF16) for z in range(2)]; RWb2 = [Res("Wb2a"), Res("Wb2b")]
                            def load_c(c):
                                z = c % 2
                                for g in range(3):
                                    fw.dma("pool", Wg2[z][:, g * 1024:(g + 1) * 1024].rearrange("p (k n) -> p k n", k=8),
                                           win[:, :, 1792 + g * 1024 + c * 128:1792 + g * 1024 + (c + 1) * 128], writes=[RWg2[z]])
                                for x in range(3):
                                    fw.dma("pool", Wb2[z][:, x * 512:(x + 1) * 512].rearrange("p (k n) -> p k n", k=4),
                                           wbr[x][:, :, c * 128:(c + 1) * 128], writes=[RWb2[z]])
                            load_c(0)
                            for c in range(8):
                                if c + 1 < 8:
                                    load_c(c + 1)
                                Wg, RWg, Wb, RWb = Wg2[c % 2], RWg2[c % 2], Wb2[c % 2], RWb2[c % 2]
                                for tb, (c0, cn) in enumerate(BLKS):
                                    for g in range(3):
                                        Wgv = Wg[:, g * 1024:(g + 1) * 1024].rearrange("p (k n) -> p k n", k=8)
                                        for k in range(8):
                                            MM(bank(g), Wgv[:, k, :], uT[:, k, c0:c0 + cn], k == 0, k == 7, RuT[c0 // 128:(c0 + cn) // 128] + [RWg], [PB[g]])
                                        ACT(gs[g][:], bank(g), AF.Sigmoid, [PB[g]], [Rgs[g]])
                                    for x in range(3):
                                        Wbv = Wb[:, x * 512:(x + 1) * 512].rearrange("p (k n) -> p k n", k=4)
                                        st_, Rst = srcs[x]
                                        for k in range(4):
                                            MM(bank(3 + x), Wbv[:, k, :], st_[:, k, c0:c0 + cn], k == 0, k == 3, [Rst[k], RWb], [PB[3 + x]])
                                    TT("dve", ac[0][:], gs[0][:], bank(3), ALU.mult, [Rgs[0], PB[3]], [Rac[0]])
                                    TT("dve", ac[1][:], gs[1][:], bank(4), ALU.mult, [Rgs[1], PB[4]], [Rac[1]])
                                    TT("dve", ac[0][:], ac[0][:], ac[1][:], ALU.add, Rac, [Rac[0]])
                                    TT("dve", ac[1][:], gs[2][:], bank(5), ALU.mult, [Rgs[2], PB[5]], [Rac[1]])
                                    TT("dve", mergedT[:, c, c0:c0 + cn], ac[0][:], ac[1][:], ALU.add, Rac, [Rmg[c]])
                            fw.barrier()
                        DBG("d_mergedT", mergedT[:].rearrange("p a b -> p (a b)"), Rmg)
                        if stop == "merge": fw.dead = True
                        layer_norm_residual(l, 0, 8, lambda i, k: mergedT[:, k, i * 128:(i + 1) * 128], lambda i, k: [Rmg[k]],
                                            Din["w_out"][l].rearrange("(k p) n -> p k n", p=128))
              if "d_x1" in Ddbg:
                  for i in range(NT):
                      fw.dma("sp", Ddbg["d_x1"][i * 128:(i + 1) * 128, :], X[:, i, :], reads=[RX[i]], writes=[Rdbg])
              if stop == "ln1": fw.dead = True
              CH[0] = 'U' in os.environ.get('CHP', 'mabfFGMLUS')
              with contextlib.ExitStack() as Fs:
                  actT = sb(Fs, "actT", [128, 22, TOK], BF16); Ract = [Res("act%d" % j) for j in range(22)]
                  with contextlib.ExitStack() as ph:
                      alloc_ws(ph)
                      uT = sb(ph, "uT", [128, 8, TOK], BF16); RuT = [Res("uT%d" % i) for i in range(NT)]
                      cvp = sb(ph, "cvp", [128, 44, 4]); Rcv = Res("cvp")
                      hc = [sb(ph, "hc%d" % j, [128, TOK]) for j in range(2)]; Rhc = [Res("hc0"), Res("hc1")]
                      fw.dma("sp", cvp[:], Din["convp"][:, l], writes=[Rcv])
                      wup = Din["w_up"][l].rearrange("(k p) n -> p k n", p=128)
                      def load_up(bi):
                          return load_w([(lambda t: t[:, 0:4096].rearrange("p (k n) -> p k n", k=8), wup[:, :, bi * 512:(bi + 1) * 512])])
                      nxt_up = load_up(0)
                      build_uT(uT, RuT, 1)
                      for ch in range(44):
                          if ch % 4 == 0:
                              W, RW = nxt_up
                              if ch // 4 + 1 < 11:
                                  nxt_up = load_up(ch // 4 + 1)
                              Wv = W[:, 0:4096].rearrange("p (k n) -> p k n", k=8)
                          cw = ch % 4; pb0 = 3 * (ch % 2); hj = ch % 2
                          prs = [PB[pb0], PB[pb0 + 1], PB[pb0 + 2]]
                          for tb, (c0, cn) in enumerate(BLKS):
                              for k in range(8):
                                  MM(bank(pb0 + tb), Wv[:, k, cw * 128:(cw + 1) * 128], uT[:, k, c0:c0 + cn], k == 0, k == 7,
                                     RuT[c0 // 128:(c0 + cn) // 128] + [RW], [prs[tb]])
                          hp = bank(pb0, 3); h_ = hc[hj]
                          ACT(h_[:], hp, AF.Identity, prs + [Rcv], [Rhc[hj]], scale=cvp[:, ch, 1:2], bias=cvp[:, ch, 3:4])
                          def stt(o_, i0, sc, i1):
                              OP("dve", lambda e: e.scalar_tensor_tensor(out=o_, in0=i0, scalar=sc, in1=i1, op0=ALU.mult, op1=ALU.add),
                                 prs + [Rcv, Rhc[hj]], [Rhc[hj]])
                          stt(h_[:, 1:1024], hp[:, 0:1023], cvp[:, ch, 0:1], h_[:, 1:1024])
                          stt(h_[:, 0:1023], hp[:, 1:1024], cvp[:, ch, 2:3], h_[:, 0:1023])
                          h3 = h_[:, 1024:1536].rearrange("p (s c) -> p s c", s=2); p3 = hp[:, 1024:1536].rearrange("p (s c) -> p s c", s=2)
                          stt(h3[:, :, 1:256], p3[:, :, 0:255], cvp[:, ch, 0:1], h3[:, :, 1:256])
                          stt(h3[:, :, 0:255], p3[:, :, 1:256], cvp[:, ch, 2:3], h3[:, :, 0:255])
                          if ch < 22:
                              ACT(actT[:, ch, :], h_[:], AF.Silu, [Rhc[hj]], [Ract[ch]])
                          else:
                              TT("dve", actT[:, ch - 22, :], actT[:, ch - 22, :], h_[:], ALU.mult, [Ract[ch - 22], Rhc[hj]], [Ract[ch - 22]])
                      fw.barrier()
                  DBG("d_actT", actT[:].rearrange("p a b -> p (a b)"), Ract)
                  if stop == "ffn": fw.dead = True
                  layer_norm_residual(l, 1, 22, lambda i, k: actT[:, k, i * 128:(i + 1) * 128], lambda i, k: [Ract[k]],
                                      Din["w_down"][l].rearrange("(k p) n -> p k n", p=128))
        layers()
        fw.dead = False
        for i in range(NT):
            fw.dma("sp", Dout["y"][i * 128:(i + 1) * 128, :], X[:, i, :], reads=[RX[i]], writes=[Rdbg])
        fw.finish()
    return nc


def kernel(**inputs):
    inp = {k: np.asarray(v) for k, v in inputs.items()}
    S = prep_shared(inp)
    nc = build()
    in_maps = []
    for cid in range(8):
        m = dict(S); m.update(prep_core(inp, cid))
        in_maps.append(m)
    res = run_bass_kernel_spmd(nc, in_maps, core_ids=list(range(8)))
    y_p = np.zeros((16, 256, 1024), np.float32); y_s = np.zeros((8, 1024, 1024), np.float32)
    nk = np.zeros((16, 2, 256, 2, 64), np.float32); nv = np.zeros((16, 2, 256, 2, 64), np.float32)
    nst = np.zeros((16, 2, 2, 32, 64, 2), np.float32)
    for cid in range(8):
        r = res.results[cid]
        y_s[cid] = r["y"][0:1024]
        y_p[2 * cid:2 * cid + 2] = r["y"][1024:1536].reshape(2, 256, 1024)
        nk[2 * cid:2 * cid + 2] = r["nk"].reshape(2, 2, 256, 2, 64)
        nv[2 * cid:2 * cid + 2] = r["nv"].reshape(2, 2, 256, 2, 64)
        nst[2 * cid:2 * cid + 2] = r["nst"]
    return (y_p, y_s, nk, nv, nst)
```

```python
import contextlib, os
SK = os.environ.get("SK", "")
import numpy as np
import concourse.bass as bass
import concourse.mybir as mybir
from concourse.bass_utils import run_bass_kernel_spmd

F32 = mybir.dt.float32; BF16 = mybir.dt.bfloat16
AF = mybir.ActivationFunctionType; ALU = mybir.AluOpType; AX = mybir.AxisListType

D_MODEL = 1024; DEPTH = 2; D_IN = 4864; D_FF = 2816
ALPHA = (2 * DEPTH) ** 0.25
EPS = 1e-6
NT = 12; TOK = 1536; T = 16
SEQS = [(0, 1024), (1024, 256), (1280, 256)]
BLKS = [(0, 512), (512, 512), (1024, 512)]
TILE_VEC = [0] * 8 + [1] * 4
NS = 99
SOFF = [0, 65, 82]
SCH = [64, 16, 16]


class StopBuild(Exception):
    pass


class Res:
    __slots__ = ("name", "w", "r")
    def __init__(self, name):
        self.name = name; self.w = None; self.r = {}


class Eng:
    def __init__(self, name, h, sem):
        self.name = name; self.h = h; self.sem = sem; self.count = 0; self.waited = {}


class FW:
    NDMA = 32
    def __init__(self, nc, es):
        self.nc = nc; self.sems = {}; self.eng = {}
        for name, h in (("pe", nc.tensor), ("act", nc.scalar), ("dve", nc.vector), ("pool", nc.gpsimd), ("sp", nc.sync)):
            s = es.enter_context(nc.semaphore("s_" + name))
            self.sems[name] = s; self.eng[name] = Eng(name, h, s)
        self.dma_sems = []
        for i in range(self.NDMA):
            s = es.enter_context(nc.semaphore("s_dma%d" % i))
            self.sems["dma%d" % i] = s; self.dma_sems.append(["dma%d" % i, 0])
        self.dma_i = 0; self.ninstr = 0; self.dead = False; self.prev_plain = False; self.rec = None

    def _wait(self, e, key, val):
        if e.waited.get(key, 0) >= val: return
        e.h.wait_ge(self.sems[key], val); e.waited[key] = val

    def _deps(self, e, reads, writes, chain=False):
        deps = {}
        def add(kv):
            if kv is None: return
            k, v = kv
            if deps.get(k, 0) < v: deps[k] = v
        for r in reads: add(r.w)
        for w in writes:
            if w.w is not None and not (chain and w.w[0] == e.name):
                add(w.w)
            for k, v in w.r.items():
                if k == e.name: continue
                add((k, v))
        return deps

    def op(self, engname, fn, reads=(), writes=(), chain=False):
        if self.dead: return None
        if self.rec is not None:
            self.rec.append(("op", engname, fn, tuple(reads), tuple(writes), chain)); return None
        e = self.eng[engname]
        if engname == "pe" and chain in ("gfirst", "gnext"):
            if chain == "gfirst" and e.count > 0:
                self._wait(e, "pe", e.count)
            chain = True; self.prev_plain = False
        elif engname == "pe":
            plain = chain
            chain = chain and self.prev_plain
            if plain and not chain and e.count > 0:
                self._wait(e, "pe", e.count)
            self.prev_plain = plain
        for k, v in self._deps(e, reads, writes, chain).items(): self._wait(e, k, v)
        ins = fn(e.h); e.count += 1; ins.then_inc(e.sem, 1)
        for r in reads: r.r[e.name] = e.count
        for w in writes: w.w = (e.name, e.count); w.r = {}
        self.ninstr += 1
        return ins

    def dma(self, qname, out, in_, reads=(), writes=()):
        if self.dead: return None
        if self.rec is not None:
            self.rec.append(("dma", qname, out, in_, tuple(reads), tuple(writes))); return None
        e = self.eng[qname]
        deps = self._deps(e, reads, writes)
        slot = self.dma_sems[self.dma_i % self.NDMA]; self.dma_i += 1
        key = slot[0]
        if slot[1] > 0: deps[key] = max(deps.get(key, 0), slot[1])
        for k, v in deps.items(): self._wait(e, k, v)
        ins = e.h.dma_start(out=out, in_=in_)
        slot[1] += 16; ins.then_inc(self.sems[key], 16)
        for r in reads: r.r[key] = slot[1]
        for w in writes: w.w = (key, slot[1]); w.r = {}
        self.ninstr += 1
        return ins

    def replay(self, items):
        for it in items:
            if it[0] == "op": self.op(it[1], it[2], it[3], it[4], it[5])
            else: self.dma(it[1], it[2], it[3], reads=it[4], writes=it[5])

    def barrier(self):
        if self.dead: return
        targets = {n: e.count for n, e in self.eng.items() if e.count > 0}
        for key, cnt in self.dma_sems:
            if cnt > 0: targets[key] = cnt
        for n, e in self.eng.items():
            for k, v in targets.items(): self._wait(e, k, v)

    def finish(self):
        self.dead = False
        self.barrier()


def bl(ap, n):
    return ap.unsqueeze(2).to_broadcast([ap.shape[0], ap.shape[1], n])


def bm(ap, n):
    return ap.unsqueeze(1).to_broadcast([ap.shape[0], n, ap.shape[1]])


def _rep(a, n=128):
    return np.ascontiguousarray(np.broadcast_to(a[None], (n,) + a.shape))


def prep_shared(inp):
    f = lambda a: np.ascontiguousarray(a, dtype=np.float32)
    L = DEPTH
    S = {}
    for k in ("w_ada", "w_in", "w_glu", "w_br_ssm", "w_br_four", "w_out", "w_up", "w_down"):
        S[k] = f(inp[k])
    S["w_br_attn"] = f(inp["w_br_attn"])
    S["b_adaP"] = f(inp["b_ada"].reshape(L, 48, 128).transpose(2, 0, 1))
    S["bgbc"] = f(np.stack([np.stack([_rep(inp["b_ada"][l, 2048:3072]), _rep(inp["b_ada"][l, 5120:6144])], 1) for l in range(L)]))
    S["lnbc"] = f(np.stack([np.stack([_rep(inp[k][l]) for k in ("ln1_g", "ln1_b", "ln2_g", "ln2_b")]) for l in range(L)]))
    cw = inp["conv_w"].reshape(L, 3, 44, 128).transpose(3, 0, 2, 1)
    cb = inp["conv_b"].reshape(L, 44, 128).transpose(2, 0, 1)[..., None]
    S["convp"] = f(np.concatenate([cw, cb], -1))
    S["bglu"] = f(inp["b_glu"].reshape(L, 4, 128).transpose(2, 0, 1))
    S["g640"] = f(np.stack([_rep(np.concatenate([np.tile(inp["q_norm_g"][l], 8), np.tile(inp["k_norm_g"][l], 4)])) for l in range(L)]))
    def lay(a):
        aP = a.reshape(L, 2, 16, 2, 64).transpose(0, 1, 3, 4, 2).reshape(L, 2, 128, 16)
        aF = a.reshape(L, 2, 4, 4, 2, 64).transpose(0, 1, 3, 2, 4, 5)
        aF = np.broadcast_to(aF[:, :, :, None], (L, 2, 4, 32, 4, 2, 64)).reshape(L, 2, 128, 512)
        return np.concatenate([aP, aF], -1)
    ldt = np.broadcast_to(inp["ssm_log_dt"][..., None], (L, 2, 32, 64))
    S["lamin"] = f(np.stack([lay(inp["ssm_a_re"]), lay(inp["ssm_a_im"]), lay(ldt)], 2))
    def bbdP(b):
        Bp = b.reshape(L, 2, 16, 2, 64, 16)
        out = np.zeros((L, 2, 2, 64, 16, 2, 16), np.float32)
        for g2 in range(2):
            out[:, :, g2, :, :, g2, :] = Bp[:, :, :, g2].transpose(0, 1, 3, 2, 4)
        return out.reshape(L, 2, 128, 16, 32)
    def bbdT(b):
        Bq = b.reshape(L, 2, 4, 4, 2, 64, 16)
        out = np.zeros((L, 2, 4, 2, 16, 4, 2, 64), np.float32)
        for g2 in range(2):
            out[:, :, :, g2, :, :, g2, :] = Bq[:, :, :, :, g2].transpose(0, 1, 3, 5, 2, 4)
        return out.reshape(L, 2, 128, 4, 128)
    def cbdP(c):
        Cp = c.reshape(L, 2, 16, 2, 16, 64)
        out = np.zeros((L, 2, 2, 64, 16, 2, 16), np.float32)
        for g2 in range(2):
            out[:, :, g2, :, :, g2, :] = Cp[:, :, :, g2].transpose(0, 1, 4, 2, 3)
        return out.reshape(L, 2, 128, 16, 32)
    S["BbdP"] = f(np.stack([bbdP(inp["ssm_b_re"]), bbdP(inp["ssm_b_im"])], 2))
    S["BbdT"] = f(np.stack([bbdT(inp["ssm_b_re"]), bbdT(inp["ssm_b_im"])], 2))
    S["CbdP"] = f(np.stack([cbdP(inp["ssm_c_re"]), cbdP(inp["ssm_c_im"])], 2))
    Dblk = np.zeros((L, 128, 4, 128), np.float32)
    for l in range(L):
        for q in range(4):
            Dblk[l, np.arange(128), q, np.arange(128)] = inp["ssm_d"][l, q * 128:(q + 1) * 128]
    S["Dblk"] = Dblk
    S["ident"] = np.eye(128, dtype=np.float32)
    mk = np.zeros((128, 4), np.float32); mk[np.arange(128), np.arange(128) // 32] = 1.0
    S["maskP"] = mk
    t = np.arange(1024)
    freqs = (10000.0 ** (-np.arange(16, dtype=np.float32) / 16)).astype(np.float32)
    ang = np.concatenate([(t // 64).astype(np.float32)[:, None] * freqs, (t % 64).astype(np.float32)[:, None] * freqs], -1)
    S["ropec"] = f(np.cos(ang).reshape(8, 128, 32).transpose(1, 0, 2))
    S["ropes"] = f(np.sin(ang).reshape(8, 128, 32).transpose(1, 0, 2))
    def dft(n):
        i = np.arange(n, dtype=np.int64)
        m = (i[:, None] * i[None, :]) % n
        a = 2.0 * np.pi * m.astype(np.float64) / n
        return np.cos(a) / np.sqrt(n), np.sin(a) / np.sqrt(n)
    c128, s128 = dft(128)
    S["cs128"] = f(np.concatenate([c128, s128], 1))
    for n in (1024, 256):
        c, s = dft(n)
        S["cl%d" % n] = f(c.reshape(n // 128, 128, n).transpose(1, 0, 2))
        S["sl%d" % n] = f((-s).reshape(n // 128, 128, n).transpose(1, 0, 2))
    return S


def prep_core(inp, cid):
    f = lambda a: np.ascontiguousarray(a, dtype=np.float32)
    m = {}
    m["xin"] = f(np.concatenate([inp["x_sample"][cid], inp["x_prompt"][2 * cid:2 * cid + 2].reshape(512, 1024)], 0))
    cv = np.stack([inp["c"][cid], inp["c_ctx"]], -1)
    m["cvec"] = f(cv.reshape(8, 128, 2).transpose(1, 0, 2))
    m["ckT"] = f(inp["cache_k"][cid].reshape(DEPTH, 512, 128).transpose(0, 2, 1))
    m["cv"] = f(inp["cache_v"][cid].reshape(DEPTH, 512, 128))
    st = inp["state_ssm"][cid].reshape(DEPTH, 2, 16, 2, 64, 2)
    m["h0"] = f(st.transpose(3, 4, 0, 1, 5, 2).reshape(128, DEPTH, 2, 2, 16))
    return m


IN_SHAPES = {
    "xin": (1536, 1024), "cvec": (128, 8, 2), "ckT": (2, 128, 512), "cv": (2, 512, 128), "h0": (128, 2, 2, 2, 16),
    "w_ada": (2, 1024, 6144), "w_in": (2, 1024, 4864), "w_glu": (2, 512, 512), "w_br_attn": (2, 512, 1024),
    "w_br_ssm": (2, 512, 1024), "w_br_four": (2, 512, 1024), "w_out": (2, 1024, 1024), "w_up": (2, 1024, 5632),
    "w_down": (2, 2816, 1024), "b_adaP": (128, 2, 48), "bgbc": (2, 128, 2, 1024), "lnbc": (2, 4, 128, 1024),
    "convp": (128, 2, 44, 4), "bglu": (128, 2, 4), "g640": (2, 128, 768), "lamin": (2, 2, 3, 128, 528),
    "BbdP": (2, 2, 2, 128, 16, 32), "BbdT": (2, 2, 2, 128, 4, 128), "CbdP": (2, 2, 2, 128, 16, 32),
    "Dblk": (2, 128, 4, 128), "ident": (128, 128), "maskP": (128, 4), "ropec": (128, 8, 32), "ropes": (128, 8, 32),
    "cs128": (128, 256), "cl1024": (128, 8, 1024), "sl1024": (128, 8, 1024), "cl256": (128, 2, 256), "sl256": (128, 2, 256),
}
OUT_SHAPES = {"y": (1536, 1024), "nk": (2, 2, 256, 128), "nv": (2, 2, 256, 128), "nst": (2, 2, 2, 32, 64, 2)}


def build(n_layers=DEPTH, stop=None, dbg_shapes=None):
    nc = bass.Bass("TRN2", target_bir_lowering=False)
    Din = {k: nc.dram_tensor(k, list(s), F32, kind="ExternalInput").ap() for k, s in IN_SHAPES.items()}
    Dout = {k: nc.dram_tensor(k, list(s), F32, kind="ExternalOutput").ap() for k, s in OUT_SHAPES.items()}
    Ddbg = {k: nc.dram_tensor(k, list(s), F32, kind="ExternalOutput").ap() for k, s in (dbg_shapes or {}).items()}
    es = contextlib.ExitStack()
    with es:
        fw = FW(nc, es)
        uid = [0]
        def sb(st, name, shape, dt=F32):
            uid[0] += 1
            return st.enter_context(nc.sbuf_tensor("%s_%d" % (name, uid[0]), list(shape), dt))
        OP = fw.op
        CH = [True]
        def MM(out, lhsT, rhs, start, stop_, r, w, tp=None, chain=None):
            if chain is None: chain = CH[0] and (tp is None)
            if tp is None:
                return fw.op("pe", lambda e: e.matmul(out, lhsT=lhsT, rhs=rhs, start=start, stop=stop_), r, w, chain=chain)
            return fw.op("pe", lambda e: e.matmul(out, lhsT=lhsT, rhs=rhs, start=start, stop=stop_, tile_position=tp), r, w, chain=chain)
        def TT(eng, out, in0, in1, op, r, w):
            return fw.op(eng, lambda e: e.tensor_tensor(out=out, in0=in0, in1=in1, op=op), r, w)
        def TS(eng, out, in0, s1, op0, r, w, s2=None, op1=None):
            if op1 is None:
                return fw.op(eng, lambda e: e.tensor_scalar(out=out, in0=in0, scalar1=s1, scalar2=None, op0=op0), r, w)
            return fw.op(eng, lambda e: e.tensor_scalar(out=out, in0=in0, scalar1=s1, scalar2=s2, op0=op0, op1=op1), r, w)
        def ACT(out, in_, func, r, w, scale=1.0, bias=0.0):
            return fw.op("act", lambda e: e.activation(out=out, in_=in_, func=func, bias=bias, scale=scale), r, w)
        def CP(eng, out, in_, r, w):
            if eng == "act":
                return fw.op("act", lambda e: e.copy(out=out, in_=in_), r, w)
            return fw.op(eng, lambda e: e.tensor_copy(out=out, in_=in_), r, w)
        Rdbg = Res("dbg")
        def DBG(name, ap, r):
            if name in Ddbg:
                fw.dma("pool", Ddbg[name], ap, reads=r, writes=[Rdbg])

        X = sb(es, "X", [128, NT, 1024]); RX = [Res("X%d" % i) for i in range(NT)]
        ident_f = sb(es, "ident_f", [128, 128]); ident_b = sb(es, "ident_b", [128, 128], BF16); Rid = Res("id")
        maskP = sb(es, "maskP", [128, 4]); Rmask = Res("mask")
        modP = sb(es, "modP", [128, 6, 8, 2]); opsc = sb(es, "opsc", [128, 2, 8, 2]); Rmod = Res("mod")
        csl = sb(es, "csl", [128, 8, 2]); sTb = sb(es, "sTb", [128, 8, 2], BF16); sRep = sb(es, "sRep", [128, 2, 8, 128], BF16)
        Rcs = Res("cs")
        stat = sb(es, "stat", [128, NT, 2, 6]); mv = sb(es, "mv", [128, NT, 2]); rs = sb(es, "rs", [128, NT, 2]); Rstat = Res("stat")
        ps = es.enter_context(nc.psum_tensor("ps", [128, 3584], F32)); PB = [Res("pb%d" % i) for i in range(7)]
        pst = es.enter_context(nc.psum_tensor("pst", [128, 1024], BF16)); PT = Res("pt")
        wslot = [None, None]; Rws = [None, None]
        wctr = [0]
        def alloc_ws(st):
            for i_ in range(2):
                wslot[i_] = sb(st, "wslot%d" % i_, [128, 4096], BF16); Rws[i_] = Res("ws%d" % i_)
        def load_w(views):
            s_ = wctr[0] % 2; wctr[0] += 1
            for dstf, src in views:
                fw.dma("pool", dstf(wslot[s_]), src, writes=[Rws[s_]])
            return wslot[s_], Rws[s_]
        def bank(i, n=1):
            return ps[:, i * 512:(i + n) * 512]

        fw.dma("sp", ident_f[:], Din["ident"], writes=[Rid])
        fw.dma("pool", ident_b[:], Din["ident"], writes=[Rid])
        fw.dma("sp", maskP[:], Din["maskP"], writes=[Rmask])
        for i in range(NT):
            fw.dma("sp", X[:, i, :], Din["xin"][i * 128:(i + 1) * 128, :], writes=[RX[i]])
        fw.dma("sp", csl[:], Din["cvec"], writes=[Rcs])
        ACT(csl[:], csl[:], AF.Silu, [Rcs], [Rcs])
        CP("dve", sTb[:], csl[:], [Rcs], [Rcs])
        for v in range(2):
            CP("dve", sRep[:, v, :, :], bl(csl[:, :, v], 128), [Rcs], [Rcs])

        def build_uT(uT, RuT, sub):
            sh_sec = 0 if sub == 0 else 3
            for i in range(NT):
                v = TILE_VEC[i]
                for kk in range(2):
                    bk = kk
                    for k4 in range(4):
                        k = kk * 4 + k4
                        OP("pe", lambda e: e.transpose(bank(bk)[:, k4 * 128:(k4 + 1) * 128], X[:, i, k * 128:(k + 1) * 128], ident_f[:]),
                           [RX[i], Rid], [PB[bk]])
                    for k4 in range(4):
                        k = kk * 4 + k4
                        ACT(uT[:, k, i * 128:(i + 1) * 128], bank(bk)[:, k4 * 128:(k4 + 1) * 128], AF.Identity,
                            [PB[bk], Rmod], [RuT[i]], scale=opsc[:, sub, k, v:v + 1], bias=modP[:, sh_sec, k, v:v + 1])

        def layer_norm_residual(l, which, kc, lhsT_fn, lres_fn, wsrc3):
            with contextlib.ExitStack() as ph:
                ch_save = CH[0]; CH[0] = 'L' in os.environ.get('CHP', 'mabfFGMLUS')
                alloc_ws(ph)
                lnb = sb(ph, "lnb", [128, 2, 1024]); Rln = Res("lnb")
                tmp = [sb(ph, "lntmp%d" % j, [128, 512]) for j in range(2)]; Rtmp = [Res("lntmp0"), Res("lntmp1")]
                nWd = 2 if kc <= 8 else 1
                Wds = [sb(ph, "Wd%d" % z, [128, kc, 512], BF16) for z in range(nWd)]; RWds = [Res("Wd%d" % z) for z in range(nWd)]
                def load_half(nb):
                    z = nb % nWd
                    for k0 in range(0, kc, 8):
                        k1 = min(kc, k0 + 8)
                        fw.dma("pool", Wds[z][:, k0:k1, :], wsrc3[:, k0:k1, nb * 512:(nb + 1) * 512], writes=[RWds[z]])
                load_half(0)
                if nWd == 2:
                    load_half(1)
                fw.dma("sp", lnb[:, 0, :], Din["lnbc"][l, 2 * which], writes=[Rln])
                fw.dma("sp", lnb[:, 1, :], Din["lnbc"][l, 2 * which + 1], writes=[Rln])
                gbc = sb(ph, "gbc", [128, 2, 1024]); Rgbc = Res("gbc")
                bg = sb(ph, "bg", [128, 1024]); Rbg = Res("bg")
                fw.dma("sp", bg[:], Din["bgbc"][l, :, which, :], writes=[Rbg])
                sec = 2 if which == 0 else 5
                wsrc = Din["w_ada"][l].rearrange("(k p) n -> p k n", p=128)
                for half in range(2):
                    Wg, RWg = load_w([(lambda t: t[:, 0:4096].rearrange("p (k n) -> p k n", k=8), wsrc[:, :, sec * 1024 + half * 512:sec * 1024 + (half + 1) * 512])])
                    Wgv = Wg[:, 0:4096].rearrange("p (k n) -> p k n", k=8)
                    for v in range(2):
                        for k in range(8):
                            MM(bank(v), sRep[:, v, k, :], Wgv[:, k, :], k == 0, k == 7, [RWg, Rcs], [PB[v]])
                        TT("dve", gbc[:, v, half * 512:(half + 1) * 512], bank(v), bg[:, half * 512:(half + 1) * 512], ALU.add, [PB[v], Rbg], [Rgbc])
                cnt = 0
                for nb in range(2):
                    cs = slice(nb * 512, (nb + 1) * 512)
                    Wd = Wds[nb % nWd]; RWd = RWds[nb % nWd]
                    if nWd == 1 and nb == 1:
                        load_half(1)
                    for i in range(NT):
                        v = TILE_VEC[i]
                        bk = 2 + (cnt % 4); tj = cnt % 2; cnt += 1
                        for k in range(kc):
                            MM(bank(bk), lhsT_fn(i, k), Wd[:, k, :], k == 0, k == kc - 1, lres_fn(i, k) + [RWd], [PB[bk]])
                        TT("dve", tmp[tj][:], bank(bk), gbc[:, v, cs], ALU.mult, [PB[bk], Rgbc], [Rtmp[tj]])
                        OP("dve", lambda e: e.scalar_tensor_tensor(out=X[:, i, cs], in0=X[:, i, cs], scalar=float(ALPHA), in1=tmp[tj][:],
                                                                   op0=ALU.mult, op1=ALU.add), [RX[i], Rtmp[tj]], [RX[i]])
                for i in range(NT):
                    for hh in range(2):
                        OP("dve", lambda e: e.bn_stats(out=stat[:, i, hh, :], in_=X[:, i, hh * 512:(hh + 1) * 512]), [RX[i]], [Rstat])
                    OP("dve", lambda e: e.bn_aggr(out=mv[:, i, :], in_=stat[:, i, :, :].rearrange("p a b -> p (a b)")), [Rstat], [Rstat])
                TS("dve", rs[:, :, 0], mv[:, :, 1], EPS, ALU.add, [Rstat], [Rstat])
                ACT(rs[:, :, 0], rs[:, :, 0], AF.Sqrt, [Rstat], [Rstat])
                OP("dve", lambda e: e.reciprocal(out=rs[:, :, 0], in_=rs[:, :, 0]), [Rstat], [Rstat])
                OP("dve", lambda e: e.scalar_tensor_tensor(out=rs[:, :, 1], in0=mv[:, :, 0], scalar=-1.0, in1=rs[:, :, 0],
                                                           op0=ALU.mult, op1=ALU.mult), [Rstat], [Rstat])
                for i in range(NT):
                    ACT(X[:, i, :], X[:, i, :], AF.Identity, [RX[i], Rstat], [RX[i]], scale=rs[:, i, 0:1], bias=rs[:, i, 1:2])
                    TT("pool", X[:, i, :], X[:, i, :], lnb[:, 0, :], ALU.mult, [RX[i], Rln], [RX[i]])
                    TT("pool", X[:, i, :], X[:, i, :], lnb[:, 1, :], ALU.add, [RX[i], Rln], [RX[i]])
                fw.barrier()
                CH[0] = ch_save

        def layers():
          for l in range(n_layers):
              with contextlib.ExitStack() as ph:
                  wa = [sb(ph, "wa%d" % i, [128, 8, 512], BF16) for i in range(3)]; Rwa = [Res("wa%d" % i) for i in range(3)]
                  bP = sb(ph, "bP", [128, 48]); Rb = Res("bmod")
                  fw.dma("sp", bP[:], Din["b_adaP"][:, l, :], writes=[Rb])
                  wsrc = Din["w_ada"][l].rearrange("(k p) n -> p k n", p=128)
                  pm = bank(6)[:, 0:96]
                  CH[0] = 'm' in os.environ.get('CHP', 'mabfFGMLUS')
                  cbs = [cb for cb in range(12) if cb // 2 not in (2, 5)]
                  for i_ in range(min(3, len(cbs))):
                      fw.dma("pool", wa[i_][:], wsrc[:, :, cbs[i_] * 512:(cbs[i_] + 1) * 512], writes=[Rwa[i_]])
                  for i_, cb in enumerate(cbs):
                      s = i_ % 3
                      sec, half = cb // 2, cb % 2
                      if i_ >= 3 and False:
                          pass
                      for m in range(4):
                          col = (sec * 8 + half * 4 + m) * 2
                          for k in range(8):
                              MM(pm[:, col:col + 2], wa[s][:, k, m * 128:(m + 1) * 128], sTb[:, k, :], k == 0, k == 7,
                                 [Rwa[s], Rcs], [PB[6]])
                      if i_ + 3 < len(cbs):
                          fw.dma("pool", wa[s][:], wsrc[:, :, cbs[i_ + 3] * 512:(cbs[i_ + 3] + 1) * 512], writes=[Rwa[s]])
                  for sec in (0, 1, 3, 4):
                      TT("dve", modP[:, sec, :, :], pm[:, sec * 16:(sec + 1) * 16].rearrange("p (c v) -> p c v", v=2),
                         bl(bP[:, sec * 8:(sec + 1) * 8], 2), ALU.add, [PB[6], Rb], [Rmod])
                  TS("dve", opsc[:, 0, :, :], modP[:, 1, :, :], 1.0, ALU.add, [Rmod], [Rmod])
                  TS("dve", opsc[:, 1, :, :], modP[:, 4, :, :], 1.0, ALU.add, [Rmod], [Rmod])
                  fw.barrier()
              if stop == "mod": fw.dead = True
              DBG("d_modP", modP[:].rearrange("p a b c -> p (a b c)"), [Rmod])

              with contextlib.ExitStack() as L1:
                  QT = sb(L1, "QT", [128, 4, TOK], BF16); RQ = [Res("QT%d" % j) for j in range(4)]; attnT = QT; RattnT = RQ
                  sfT = sb(L1, "sfT", [128, 8, TOK], BF16); RsfT = [Res("sfT%d" % j) for j in range(8)]
                  fourT = sb(L1, "fourT", [128, 4, TOK], BF16); RfourT = [Res("fourT%d" % j) for j in range(4)]
                  Lp = contextlib.ExitStack()
                  lamD = [[sb(Lp, "lam%d%d" % (d_, r), [128, 528]) for r in range(2)] for d_ in range(2)]
                  ffD = [[sb(Lp, "ff%d%d" % (d_, r), [128, 528]) for r in range(2)] for d_ in range(2)]
                  PowP = [sb(Lp, "PowP%d" % r, [128, 2, 16, 17]) for r in range(2)]
                  LH = [sb(Lp, "LH%d" % r, [128, 2, 16, 7]) for r in range(2)]
                  fP = [sb(Lp, "fP%d" % r, [128, 2, 16]) for r in range(2)]
                  RS = Res("ssm")
                  def cmul(dr, di, ar, ai, br, bi, m1, m2):
                      TT("dve", m1, ar, br, ALU.mult, [RS], [RS]); TT("dve", m2, ai, bi, ALU.mult, [RS], [RS])
                      TT("dve", dr, m1, m2, ALU.subtract, [RS], [RS])
                      TT("dve", m1, ar, bi, ALU.mult, [RS], [RS]); TT("dve", m2, ai, br, ALU.mult, [RS], [RS])
                      TT("dve", di, m1, m2, ALU.add, [RS], [RS])
                  def ssm_param_pipeline(st):
                      are, aim, ldt, mag, cc, ss, t1, t2, t3 = [sb(st, "sp%d" % j_, [128, 528]) for j_ in range(9)]
                      for d in range(2):
                          lam = lamD[d]; ff = ffD[d]
                          fw.dma("sp", are[:], Din["lamin"][l, d, 0], writes=[RS])
                          fw.dma("sp", aim[:], Din["lamin"][l, d, 1], writes=[RS])
                          fw.dma("sp", ldt[:], Din["lamin"][l, d, 2], writes=[RS])
                          ACT(ldt[:], ldt[:], AF.Exp, [RS], [RS])
                          TT("dve", t1[:], are[:], ldt[:], ALU.mult, [RS], [RS])
                          ACT(mag[:], t1[:], AF.Exp, [RS], [RS])
                          TT("dve", t2[:], aim[:], ldt[:], ALU.mult, [RS], [RS])
                          ACT(ss[:], t2[:], AF.Sin, [RS], [RS], scale=1.0 / 16)
                          ACT(t1[:], t2[:], AF.Sin, [RS], [RS], scale=1.0 / 32)
                          TT("dve", t1[:], t1[:], t1[:], ALU.mult, [RS], [RS])
                          TS("dve", cc[:], t1[:], -2.0, ALU.mult, [RS], [RS], s2=1.0, op1=ALU.add)
                          for _ in range(4):
                              TT("dve", t1[:], cc[:], cc[:], ALU.mult, [RS], [RS])
                              TT("dve", t2[:], ss[:], ss[:], ALU.mult, [RS], [RS])
                              TT("dve", t3[:], cc[:], ss[:], ALU.mult, [RS], [RS])
                              TT("dve", cc[:], t1[:], t2[:], ALU.subtract, [RS], [RS])
                              TS("dve", ss[:], t3[:], 2.0, ALU.mult, [RS], [RS])
                          TT("dve", lam[0][:], mag[:], cc[:], ALU.mult, [RS], [RS])
                          TT("dve", lam[1][:], mag[:], ss[:], ALU.mult, [RS], [RS])
                          TT("dve", t1[:], are[:], are[:], ALU.mult, [RS], [RS])
                          TT("dve", t2[:], aim[:], aim[:], ALU.mult, [RS], [RS])
                          TT("dve", t1[:], t1[:], t2[:], ALU.add, [RS], [RS])
                          OP("dve", lambda e: e.reciprocal(out=t1[:], in_=t1[:]), [RS], [RS])
                          TS("dve", t2[:], lam[0][:], -1.0, ALU.add, [RS], [RS])
                          TT("dve", t3[:], t2[:], are[:], ALU.mult, [RS], [RS])
                          TT("dve", cc[:], lam[1][:], aim[:], ALU.mult, [RS], [RS])
                          TT("dve", t3[:], t3[:], cc[:], ALU.add, [RS], [RS])
                          TT("dve", ff[0][:], t3[:], t1[:], ALU.mult, [RS], [RS])
                          TT("dve", t3[:], lam[1][:], are[:], ALU.mult, [RS], [RS])
                          TT("dve", cc[:], t2[:], aim[:], ALU.mult, [RS], [RS])
                          TT("dve", t3[:], t3[:], cc[:], ALU.subtract, [RS], [RS])
                          TT("dve", ff[1][:], t3[:], t1[:], ALU.mult, [RS], [RS])
                          pm1 = t1[:, 0:256].rearrange("p (g k) -> p g k", g=16); pm2 = t2[:, 0:256].rearrange("p (g k) -> p g k", g=16)
                          OP("dve", lambda e, d=d: e.memset(PowP[0][:, d, :, 0:1], 1.0), [RS], [RS])
                          OP("dve", lambda e, d=d: e.memset(PowP[1][:, d, :, 0:1], 0.0), [RS], [RS])
                          for r in range(2):
                              TS("dve", PowP[r][:, d, :, 1], lam[r][:, 0:16], 1.0, ALU.mult, [RS], [RS])
                              TS("dve", fP[r][:, d, :], ff[r][:, 0:16], 1.0, ALU.mult, [RS], [RS])
                          for w in (1, 2, 4, 8):
                              br_ = PowP[0][:, d, :, w:w + 1].to_broadcast([128, 16, w]); bi_ = PowP[1][:, d, :, w:w + 1].to_broadcast([128, 16, w])
                              cmul(PowP[0][:, d, :, w + 1:2 * w + 1], PowP[1][:, d, :, w + 1:2 * w + 1],
                                   PowP[0][:, d, :, 1:w + 1], PowP[1][:, d, :, 1:w + 1], br_, bi_, pm1[:, :, 0:w], pm2[:, :, 0:w])
                          for r in range(2):
                              TS("dve", LH[r][:, d, :, 0], PowP[r][:, d, :, 16], 1.0, ALU.mult, [RS], [RS])
                          for m in range(6):
                              cmul(LH[0][:, d, :, m + 1], LH[1][:, d, :, m + 1], LH[0][:, d, :, m], LH[1][:, d, :, m],
                                   LH[0][:, d, :, m], LH[1][:, d, :, m], t1[:, 256:272], t2[:, 256:272])
                  with contextlib.ExitStack() as L2:
                      KT = sb(L2, "KT", [128, 2, 2048], BF16); RKT = Res("KT")
                      Vsb = sb(L2, "Vsb", [128, 16, 2, 2, 128], BF16); RV = Res("V")
                      with contextlib.ExitStack() as ph:
                          alloc_ws(ph)
                          uT = sb(ph, "uT", [128, 8, TOK], BF16); RuT = [Res("uT%d" % i) for i in range(NT)]
                          qn = [sb(ph, "qn%d" % j, [128, 768]) for j in range(2)]; Rqn = [Res("qn0"), Res("qn1")]
                          vf = [sb(ph, "vf%d" % j, [128, 128]) for j in range(2)]; Rvf = [Res("vf0"), Res("vf1")]
                          qkb = [sb(ph, "qkb%d" % j, [128, 768], BF16) for j in range(2)]; Rqkb = [Res("qkb0"), Res("qkb1")]
                          kb2 = [sb(ph, "kb2%d" % j, [128, 128], BF16) for j in range(2)]; Rkb2 = [Res("kb20"), Res("kb21")]
                          ssq = sb(ph, "ssq", [128, 12]); Rssq = Res("ssq")
                          g640 = sb(ph, "g640", [128, 768]); rc = sb(ph, "rc", [128, 8, 32]); rsn = sb(ph, "rsn", [128, 8, 32]); Rcst = Res("cst")
                          rt = [sb(ph, "rt%d" % j, [128, 12, 32]) for j in range(2)]; Rrt = [Res("rt0"), Res("rt1")]
                          win = Din["w_in"][l].rearrange("(k p) n -> p k n", p=128)
                          WA_pre = load_w([(lambda t: t[:, 0:4096].rearrange("p (k n) -> p k n", k=8), win[:, :, 0:512])])
                          WB_pre = load_w([(lambda t: t[:, 0:2048].rearrange("p (k n) -> p k n", k=8), win[:, :, 512:768])])
                          fw.dma("sp", g640[:], Din["g640"][l], writes=[Rcst])
                          fw.dma("sp", rc[:], Din["ropec"], writes=[Rcst])
                          fw.dma("sp", rsn[:], Din["ropes"], writes=[Rcst])
                          fw.dma("pool", KT[:, 0, 1024:1536], Din["ckT"][l], writes=[RKT])
                          fw.dma("pool", KT[0:64, 1, 1024:1536], Din["ckT"][l, 64:128, :], writes=[RKT])
                          fw.dma("pool", KT[64:128, 1, 1024:1536], Din["ckT"][l, 0:64, :], writes=[RKT])
                          OP("dve", lambda e: e.memset(Vsb[:].rearrange("p a b c d -> p (a b c d)"), 1.0), [], [RV])
                          cvv = Din["cv"][l].rearrange("(t p) (h d) -> p t h d", p=128, h=2)
                          for lay in range(2):
                              for kvh_ in range(2):
                                  fw.dma("pool", Vsb[:, 8:12, kvh_, lay, lay * 64:(lay + 1) * 64], cvv[:, :, kvh_, :], writes=[RV])
                          build_uT(uT, RuT, 0)
                          if stop == "ut": fw.dead = True
                          DBG("d_uT", uT[:].rearrange("p a b -> p (a b)"), RuT)

                          def norm_rope(i, pa, pr, nh, gap, outb, Routb, slot):
                              w_ = nh * 64
                              sqv = qn[slot][:, 0:w_]
                              ACT(sqv, pa, AF.Square, pr, [Rqn[slot]])
                              OP("dve", lambda e: e.reduce_sum(out=ssq[:, 0:nh], in_=sqv.rearrange("p (h d) -> p h d", d=64), axis=AX.X), [Rqn[slot]], [Rssq])
                              TS("dve", ssq[:, 0:nh], ssq[:, 0:nh], 1.0 / 64, ALU.mult, [Rssq], [Rssq], s2=EPS, op1=ALU.add)
                              ACT(ssq[:, 0:nh], ssq[:, 0:nh], AF.Sqrt, [Rssq], [Rssq])
                              OP("dve", lambda e: e.reciprocal(out=ssq[:, 0:nh], in_=ssq[:, 0:nh]), [Rssq], [Rssq])
                              qv = qn[slot][:, 0:w_]
                              TT("dve", qv.rearrange("p (h d) -> p h d", d=64), pa.rearrange("p (h d) -> p h d", d=64), bl(ssq[:, 0:nh], 64),
                                 ALU.mult, pr + [Rssq], [Rqn[slot]])
                              TT("dve", qv, qv, gap, ALU.mult, [Rqn[slot], Rcst], [Rqn[slot]])
                              if i >= 8:
                                  CP("act", outb, qv, [Rqn[slot]], [Routb])
                              else:
                                  x4 = qv.rearrange("p (h d two) -> p h d two", d=32, two=2)
                                  o4 = outb.rearrange("p (h d two) -> p h d two", d=32, two=2)
                                  x0, x1 = x4[:, :, :, 0], x4[:, :, :, 1]
                                  cc, ss_ = bm(rc[:, i, :], nh), bm(rsn[:, i, :], nh)
                                  r0, r1 = rt[0][:, 0:nh, :], rt[1][:, 0:nh, :]
                                  TT("dve", r0, x0, cc, ALU.mult, [Rqn[slot], Rcst], [Rrt[0]])
                                  TT("pool", r1, x1, ss_, ALU.mult, [Rqn[slot], Rcst], [Rrt[1]])
                                  TT("dve", o4[:, :, :, 0], r0, r1, ALU.subtract, Rrt, [Routb])
                                  TT("dve", r0, x0, ss_, ALU.mult, [Rqn[slot], Rcst], [Rrt[0]])
                                  TT("pool", r1, x1, cc, ALU.mult, [Rqn[slot], Rcst], [Rrt[1]])
                                  TT("dve", o4[:, :, :, 1], r0, r1, ALU.add, Rrt, [Routb])

                          CH[0] = 'a' in os.environ.get('CHP', 'mabfFGMLUS')
                          WA, RWA = WA_pre; WB, RWB = WB_pre
                          WvA = WA[:, 0:4096].rearrange("p (k n) -> p k n", k=8)
                          WvB = WB[:, 0:2048].rearrange("p (k n) -> p k n", k=8)
                          for i in range(NT):
                              b0 = 2 + 2 * (i % 2); sl = i % 2
                              prs = [PB[b0], PB[b0 + 1]]
                              for k in range(8):
                                  MM(bank(b0), uT[:, k, i * 128:(i + 1) * 128], WvA[:, k, :], k == 0, k == 7, [RuT[i], RWA], [PB[b0]])
                              for k in range(8):
                                  MM(bank(b0 + 1)[:, 0:256], uT[:, k, i * 128:(i + 1) * 128], WvB[:, k, :], k == 0, k == 7, [RuT[i], RWB], [PB[b0 + 1]])
                              vps = bank(b0 + 1)[:, 128:256]
                              norm_rope(i, bank(b0, 2)[:, 0:768], prs, 12, g640[:, 0:768], qkb[sl][:, 0:768], Rqkb[sl], sl)
                              for j in range(4):
                                  src = qkb[sl][:, j * 128:(j + 1) * 128]
                                  OP("pe", lambda e: e.transpose(pst[:, j * 128:(j + 1) * 128], src, ident_b[:]), [Rqkb[sl], Rid], [PT])
                              CP("act", QT[:, :, i * 128:(i + 1) * 128], pst[:, 0:512].rearrange("p (j c) -> p j c", j=4), [PT], RQ)
                              kt = i if i < 8 else 12 + (i - 8)
                              for lay in range(2):
                                  CP("act", Vsb[:, kt, :, lay, lay * 64:(lay + 1) * 64], vps.rearrange("p (h d) -> p h d", h=2), [PB[b0 + 1]], [RV])
                              if i >= 8:
                                  p_, t_ = (i - 8) // 2, (i - 8) % 2
                                  CP("act", vf[sl][:], vps, [PB[b0 + 1]], [Rvf[sl]])
                                  fw.dma("sp", Dout["nk"][p_, l, t_ * 128:(t_ + 1) * 128, :], qn[sl][:, 512:640], reads=[Rqn[sl]], writes=[Rdbg])
                                  fw.dma("sp", Dout["nv"][p_, l, t_ * 128:(t_ + 1) * 128, :], vf[sl][:], reads=[Rvf[sl]], writes=[Rdbg])
                              CP("act", kb2[sl][:, 0:64], qkb[sl][:, 576:640], [Rqkb[sl]], [Rkb2[sl]])
                              CP("act", kb2[sl][:, 64:128], qkb[sl][:, 512:576], [Rqkb[sl]], [Rkb2[sl]])
                              OP("pe", lambda e: e.transpose(pst[:, 512:640], qkb[sl][:, 512:640], ident_b[:]), [Rqkb[sl], Rid], [PT])
                              OP("pe", lambda e: e.transpose(pst[:, 640:768], kb2[sl][:, :], ident_b[:]), [Rkb2[sl], Rid], [PT])
                              kc = i * 128 if i < 8 else 1536 + (i - 8) * 128
                              CP("act", KT[:, 0, kc:kc + 128], pst[:, 512:640], [PT], [RKT])
                              CP("act", KT[:, 1, kc:kc + 128], pst[:, 640:768], [PT], [RKT])
                          WF_pre = load_w([(lambda t: t[:, 0:4096].rearrange("p (k n) -> p k n", k=8), win[:, :, 768:1280])])
                          if stop in ("passA", "passB"): fw.dead = True
                          CH[0] = 'f' in os.environ.get('CHP', 'mabfFGMLUS')
                          for wb in range(2):
                              W, RW = WF_pre
                              if wb == 0:
                                  WF_pre = load_w([(lambda t: t[:, 0:4096].rearrange("p (k n) -> p k n", k=8), win[:, :, 1280:1792])])
                              Wv = W[:, 0:4096].rearrange("p (k n) -> p k n", k=8)
                              for m4 in range(4):
                                  m = wb * 4 + m4
                                  for tb, (c0, cn) in enumerate(BLKS):
                                      bk = (0, 1, 6)[(m * 3 + tb) % 3]
                                      for k in range(8):
                                          MM(bank(bk), Wv[:, k, m4 * 128:(m4 + 1) * 128], uT[:, k, c0:c0 + cn], k == 0, k == 7,
                                             RuT[c0 // 128:(c0 + cn) // 128] + [RW], [PB[bk]])
                                      CP("act", sfT[:, m, c0:c0 + cn], bank(bk), [PB[bk]], [RsfT[m]])
                          DBG("d_sfT", sfT[:].rearrange("p a b -> p (a b)"), RsfT)
                          fw.barrier()
                      if stop == "inproj": fw.dead = True
                      DBG("d_QT", QT[:].rearrange("p a b -> p (a b)"), RQ)
                      DBG("d_KT", KT[:].rearrange("p a b -> p (a b)"), [RKT])
                      CH[0] = 't' in os.environ.get('CHP', 'mabfFGMLUS')
                      with contextlib.ExitStack() as ph:
                          fw.rec = []
                          ssm_param_pipeline(ph)
                          pipe_items = fw.rec if fw.rec is not None else []; fw.rec = None
                          per_unit = (len(pipe_items) + 23) // 24
                          Eb = [sb(ph, "Eb%d" % j, [128, 512], BF16) for j in range(3)]; REb = [Res("Eb%d" % j) for j in range(3)]
                          rsb = [sb(ph, "rsb%d" % j, [128, 512]) for j in range(2)]; Rrsb = [Res("rsb0"), Res("rsb1")]
                          cnt = 0; cnt2 = 0
                          for si, (t0, Ls) in enumerate(SEQS):
                              if si == 0:
                                  qblks = [(0, 512), (512, 512)]; nkt = 12; kt0 = 0; kc0 = 0
                              else:
                                  qblks = [(t0, 256)]; nkt = 2; kt0 = 12 + 2 * (si - 1); kc0 = 1536 + 256 * (si - 1)
                              for h in range(8):
                                  j = h // 2; hh = h % 2; kvh = h // 4; var = 0 if kvh == hh else 1
                                  pv, sm = (slice(0, 64), slice(64, 128)) if hh == 0 else (slice(64, 128), slice(0, 64))
                                  for (q0, qn) in qblks:
                                      pvb = 3 + (cnt2 % 2); cnt2 += 1
                                      def S_mm(kt, sbk):
                                          MM(bank(sbk)[:, 0:qn], KT[hh * 64:(hh + 1) * 64, var, kc0 + kt * 128:kc0 + (kt + 1) * 128],
                                             QT[hh * 64:(hh + 1) * 64, j, q0:q0 + qn], True, True, [RKT, RQ[j]], [PB[sbk]], chain=False)
                                      slots = [(cnt + i_) % 3 for i_ in range(nkt)]; cnt += nkt
                                      S_mm(0, slots[0])
                                      for kt in range(nkt):
                                          sbk = slots[kt]; eb = sbk
                                          if kt + 1 < nkt:
                                              S_mm(kt + 1, slots[kt + 1])
                                          ACT(Eb[eb][:, 0:qn], bank(sbk)[:, 0:qn], AF.Exp, [PB[sbk]], [REb[eb]], scale=0.125)
                                          MM(bank(pvb)[:, 0:qn], Vsb[:, kt0 + kt, kvh, hh, :], Eb[eb][:, 0:qn], kt == 0, kt == nkt - 1,
                                             [RV, REb[eb]], [PB[pvb]], chain=False)
                                      rb = cnt2 % 2
                                      ACT(rsb[rb][sm, 0:qn], bank(pvb)[sm, 0:qn], AF.Ln, [PB[pvb]], [Rrsb[rb]])
                                      ACT(rsb[rb][sm, 0:qn], rsb[rb][sm, 0:qn], AF.Exp, [Rrsb[rb]], [Rrsb[rb]], scale=-1.0)
                                      TT("dve", attnT[pv, j, q0:q0 + qn], bank(pvb)[pv, 0:qn], rsb[rb][sm, 0:qn], ALU.mult,
                                         [PB[pvb], Rrsb[rb]], [RattnT[j]])
                                      fw.replay(pipe_items[:per_unit]); del pipe_items[:per_unit]
                          fw.replay(pipe_items); del pipe_items[:]
                          fw.barrier()
                  DBG("d_attnT", attnT[:].rearrange("p a b -> p (a b)"), RattnT)
                  CH[0] = 'F' in os.environ.get('CHP', 'mabfFGMLUS')
                  with contextlib.ExitStack() as L3:
                    with contextlib.ExitStack() as ph:
                        cs128 = sb(ph, "cs128", [128, 256], BF16); cl = sb(ph, "cl", [128, 8, 1024], BF16); sl_ = sb(ph, "sl", [128, 8, 1024], BF16)
                        clp = sb(ph, "clp", [128, 2, 256], BF16); slp = sb(ph, "slp", [128, 2, 256], BF16); Rft = Res("ftab")
                        Pfm = sb(ph, "Pfm", [128, NT, 1024], BF16); RPfm = [Res("Pfm%d" % i) for i in range(NT)]
                        fw.dma("pool", cs128[:], Din["cs128"], writes=[Rft])
                        for k in range(8):
                            fw.dma("pool", cl[:, k, :], Din["cl1024"][:, k, :], writes=[Rft])
                            fw.dma("pool", sl_[:, k, :], Din["sl1024"][:, k, :], writes=[Rft])
                        fw.dma("pool", clp[:], Din["cl256"], writes=[Rft])
                        fw.dma("pool", slp[:], Din["sl256"], writes=[Rft])
                        for i in range(NT):
                            b0 = 2 + 2 * (i % 2)
                            for g in range(4):
                                MM(bank(b0 + g // 2)[:, (g % 2) * 256:(g % 2 + 1) * 256], sfT[:, 4 + g, i * 128:(i + 1) * 128], cs128[:], True, True,
                                   [RsfT[4 + g], Rft], [PB[b0 + g // 2]])
                            CP("act", Pfm[:, i, :], bank(b0, 2), [PB[b0], PB[b0 + 1]], [RPfm[i]])
                        cnt = 0
                        for si, (t0, Ls) in enumerate(SEQS):
                            ntl = Ls // 128; tb0 = t0 // 128
                            tc_, ts_ = (cl, sl_) if si == 0 else (clp, slp)
                            for g in range(4):
                                for lb in range(0, Ls, 512):
                                    n = min(512, Ls - lb)
                                    bk = (0, 1, 6)[cnt % 3]; cnt += 1
                                    for k in range(ntl):
                                        MM(bank(bk)[:, 0:n], Pfm[:, tb0 + k, g * 256:g * 256 + 128], tc_[:, k, lb:lb + n], k == 0, False,
                                           [RPfm[tb0 + k], Rft], [PB[bk]])
                                        MM(bank(bk)[:, 0:n], Pfm[:, tb0 + k, g * 256 + 128:g * 256 + 256], ts_[:, k, lb:lb + n], False, k == ntl - 1,
                                           [RPfm[tb0 + k], Rft], [PB[bk]])
                                    CP("act", fourT[:, g, t0 + lb:t0 + lb + n], bank(bk)[:, 0:n], [PB[bk]], [RfourT[g]])
                        fw.barrier()
                    DBG("d_fourT", fourT[:].rearrange("p a b -> p (a b)"), RfourT)
                    if stop == "four": fw.dead = True
                    CH[0] = 'S' in os.environ.get('CHP', 'mabfFGMLUS')
                    with contextlib.ExitStack() as S:
                        Hb = sb(S, "Hb", [128, 2, 2, 16, NS], BF16)
                        h0 = sb(S, "h0", [128, 2, 2, 16]); stg = sb(S, "stg", [128, 16, 2])
                        RHb = Res("Hb"); Rstg = Res("stg")
                        fw.dma("sp", h0[:], Din["h0"][:, l], writes=[RS])
                        for d in range(2):
                            with contextlib.ExitStack() as SD:
                                lam = lamD[d]; ff = ffD[d]
                                A = [sb(SD, "A%d" % r, [128, 16, NS]) for r in range(2)]
                                for r in range(2):
                                    OP("dve", lambda e: e.memset(A[r][:].rearrange("p a b -> p (a b)"), 0.0), [RS], [RS])
                                with contextlib.ExitStack() as SW:
                                    W = [sb(SW, "W%d" % r, [128, 16, 128]) for r in range(2)]
                                    WT2 = [[sb(SW, "WT%d_%d" % (z, r), [128, 16, 128], BF16) for r in range(2)] for z in range(2)]
                                    wm = [sb(SW, "wm%d" % r, [128, 8, 128]) for r in range(3)]
                                    BT = [sb(SW, "BT%d" % r, [128, 128]) for r in range(2)]
                                    Lw = [sb(SW, "Lw%d" % r, [128, 128]) for r in range(2)]
                                    RWT2 = [Res("WTa"), Res("WTb")]
                                    def build_W(q):
                                        fs = slice(16 + q * 128, 16 + (q + 1) * 128)
                                        for r in range(2):
                                            fw.dma("sp", BT[r][:], Din["BbdT"][l, d, r, :, q, :], writes=[RS])
                                            ACT(Lw[r][:], lam[r][:, fs], AF.Identity, [RS], [RS])
                                        cmul(W[0][:, 0, :], W[1][:, 0, :], ff[0][:, fs], ff[1][:, fs], BT[0][:], BT[1][:], wm[0][:, 0, :], wm[1][:, 0, :])
                                        for w in (1, 2, 4, 8):
                                            cmul(W[0][:, w:2 * w, :], W[1][:, w:2 * w, :], bm(Lw[0][:], w), bm(Lw[1][:], w),
                                                 W[0][:, 0:w, :], W[1][:, 0:w, :], wm[0][:, 0:w, :], wm[1][:, 0:w, :])
                                            if w < 8:
                                                TT("dve", wm[0][:, 0, :], Lw[0][:], Lw[0][:], ALU.mult, [RS], [RS])
                                                TT("dve", wm[1][:, 0, :], Lw[1][:], Lw[1][:], ALU.mult, [RS], [RS])
                                                TT("dve", wm[2][:, 0, :], Lw[0][:], Lw[1][:], ALU.mult, [RS], [RS])
                                                TT("dve", Lw[0][:], wm[0][:, 0, :], wm[1][:, 0, :], ALU.subtract, [RS], [RS])
                                                TS("dve", Lw[1][:], wm[2][:, 0, :], 2.0, ALU.mult, [RS], [RS])
                                        for r in range(2):
                                            ACT(WT2[q % 2][r][:].rearrange("p a b -> p (a b)"), W[r][:].rearrange("p a b -> p (a b)"), AF.Identity, [RS], [RWT2[q % 2]])
                                    def run_S(q):
                                        WTq = WT2[q % 2]; RWTq = RWT2[q % 2]
                                        Sps = bank(0, 2).rearrange("p (b r c) -> p b r c", b=4, r=2)
                                        for b in range(4):
                                            sv = sfT[32 * b:32 * b + 32, q, :].rearrange("p (c j) -> p c j", j=16)
                                            for r in range(2):
                                                for k in range(16):
                                                    j = 15 - k if d == 0 else k
                                                    MM(Sps[:, b, r, 0:96], WTq[r][32 * b:32 * b + 32, k, :], sv[:, :, j], k == 0, k == 15,
                                                       [RWTq, RsfT[q]], [PB[0], PB[1]], tp=(32 * b, 0))
                                        for r in range(2):
                                            for s_i in range(3):
                                                n = SCH[s_i]; c0 = (0, 64, 80)[s_i]; lo = SOFF[s_i] + (1 if d == 0 else 0)
                                                ACT(A[r][:, 4 * q:4 * q + 4, lo:lo + n], Sps[:, :, r, c0:c0 + n], AF.Identity, [PB[0], PB[1]], [RS])
                                    build_W(0)
                                    for q in range(4):
                                        if q + 1 < 4:
                                            build_W(q + 1)
                                        run_S(q)
                                RSp = Res("ssm_pool"); NPD = 10
                                for r in range(2):
                                    ACT(A[r][:, :, 0 if d == 0 else 64], h0[:, d, r, :], AF.Identity, [RS], [RS, RSp])
                                with contextlib.ExitStack() as SC:
                                    sm = [sb(SC, "sm%d" % r, [128, 16, NS]) for r in range(3)]
                                    for m in range(7):
                                        dd = 1 << m
                                        groups = []
                                        if dd < 65:
                                            groups.append((lambda t_, ps_, lo, hi: t_[:, ps_, lo:hi], 65, None))
                                        if dd < 17:
                                            groups.append((lambda t_, ps_, lo, hi: t_[:, ps_, 65:99].rearrange("p g (s c) -> p g s c", s=2)[:, :, :, lo:hi], 17, 2))
                                        for view, n, ns in groups:
                                            cntn = n - dd
                                            (dlo, dhi, slo, shi) = (dd, n, 0, cntn) if d == 0 else (0, cntn, dd, n)
                                            for eng_, ps_, Rr in (("dve", slice(0, NPD), RS), ("pool", slice(NPD, 16), RSp)):
                                                npz = ps_.stop - ps_.start
                                                if ns is None:
                                                    Lr = bl(LH[0][:, d, ps_, m], cntn); Li = bl(LH[1][:, d, ps_, m], cntn)
                                                    tv = lambda t_: t_[:, ps_, 0:cntn]
                                                else:
                                                    Lr = LH[0][:, d, ps_, m].unsqueeze(2).unsqueeze(3).to_broadcast([128, npz, 2, cntn])
                                                    Li = LH[1][:, d, ps_, m].unsqueeze(2).unsqueeze(3).to_broadcast([128, npz, 2, cntn])
                                                    tv = lambda t_: t_[:, ps_, 0:2 * cntn].rearrange("p g (s c) -> p g s c", s=2)
                                                sr, si = view(A[0], ps_, slo, shi), view(A[1], ps_, slo, shi)
                                                dr, di = view(A[0], ps_, dlo, dhi), view(A[1], ps_, dlo, dhi)
                                                m1, m2, m3 = tv(sm[0]), tv(sm[1]), tv(sm[2])
                                                TT(eng_, m1, sr, Lr, ALU.mult, [Rr], [Rr]); TT(eng_, m2, si, Li, ALU.mult, [Rr], [Rr])
                                                TT(eng_, m1, m1, m2, ALU.subtract, [Rr], [Rr])
                                                TT(eng_, m2, si, Lr, ALU.mult, [Rr], [Rr]); TT(eng_, m3, sr, Li, ALU.mult, [Rr], [Rr])
                                                TT(eng_, m2, m2, m3, ALU.add, [Rr], [Rr])
                                                TT(eng_, dr, dr, m1, ALU.add, [Rr], [Rr]); TT(eng_, di, di, m2, ALU.add, [Rr], [Rr])
                                for r in range(2):
                                    ACT(Hb[:, d, r, :, :], A[r][:], AF.Identity, [RS, RSp], [RHb])
                                for s_i in (1, 2):
                                    idx = SOFF[s_i] + (16 if d == 0 else 0)
                                    for r in range(2):
                                        ACT(stg[:, :, r], A[r][:, :, idx], AF.Identity, [RS, RSp], [Rstg])
                                    fw.dma("sp", Dout["nst"][s_i - 1, l, d].rearrange("(pair g2) p r -> (g2 p) pair r", g2=2), stg[:], reads=[Rstg], writes=[Rdbg])
                                DBG("d_A%d" % d, A[0][:].rearrange("p a b -> p (a b)"), [RS, RSp])
                                fw.barrier()
                        with contextlib.ExitStack() as SQ:
                            CL2 = [[[sb(SQ, "CL%d%d%d" % (z, d, r), [128, 4, 17, 32], BF16) for r in range(2)] for d in range(2)] for z in range(2)]
                            Kblk = [sb(SQ, "Kblk%d" % d, [128, 16, 128], BF16) for d in range(2)]
                            Bb2 = [[[sb(SQ, "Bb%d%d%d" % (z, d, r), [128, 4, 32], BF16) for r in range(2)] for d in range(2)] for z in range(2)]
                            Dbl = sb(SQ, "Dbl", [128, 4, 128], BF16)
                            cm = [sb(SQ, "cm%d" % j, [128, 2, 17, 32]) for j in range(2)]
                            Cl = [sb(SQ, "Cl%d" % r, [128, 4, 32]) for r in range(2)]
                            Bl = [sb(SQ, "Bl%d" % r, [128, 4, 32]) for r in range(2)]
                            bt = [sb(SQ, "bt%d" % r, [128, 4, 32]) for r in range(2)]
                            gt = [sb(SQ, "gt%d" % r, [128, 512]) for r in range(2)]; Rgt = [Res("gt0"), Res("gt1")]
                            RCL2 = [Res("CLa"), Res("CLb")]; RK = Res("Kblk"); RDbl = Res("Dbl")
                            ydb = sb(SQ, "ydb", [128, 512]); Rydb = Res("ydb")
                            fw.dma("pool", Dbl[:], Din["Dblk"][l], writes=[RDbl])

                            def build_CL(q):
                                z = q % 2; CL = CL2[z]; Bb = Bb2[z]; RCL = RCL2[z]
                                ps4 = slice(4 * q, 4 * q + 4)
                                for d in range(2):
                                    for r in range(2):
                                        fw.dma("sp", Cl[r][:], Din["CbdP"][l, d, r, :, ps4, :], writes=[RS])
                                        fw.dma("sp", Bl[r][:], Din["BbdP"][l, d, r, :, ps4, :], writes=[RS])
                                    for hp in range(2):
                                        pp = slice(2 * hp, 2 * hp + 2); pq = slice(4 * q + 2 * hp, 4 * q + 2 * hp + 2)
                                        Pr = PowP[0][:, d, pq, :].unsqueeze(3).to_broadcast([128, 2, 17, 32])
                                        Pi = PowP[1][:, d, pq, :].unsqueeze(3).to_broadcast([128, 2, 17, 32])
                                        Cr = Cl[0][:, pp, :].unsqueeze(2).to_broadcast([128, 2, 17, 32])
                                        Ci = Cl[1][:, pp, :].unsqueeze(2).to_broadcast([128, 2, 17, 32])
                                        TT("dve", cm[0][:], Pr, Cr, ALU.mult, [RS], [RS]); TT("dve", cm[1][:], Pi, Ci, ALU.mult, [RS], [RS])
                                        TT("dve", CL[d][0][:, pp, :, :], cm[0][:], cm[1][:], ALU.subtract, [RS], [RCL])
                                        TT("dve", cm[0][:], Pr, Ci, ALU.mult, [RS], [RS]); TT("dve", cm[1][:], Pi, Cr, ALU.mult, [RS], [RS])
                                        TT("dve", cm[0][:], cm[0][:], cm[1][:], ALU.add, [RS], [RS])
                                        TS("dve", CL[d][1][:, pp, :, :], cm[0][:], -1.0, ALU.mult, [RS], [RCL])
                                    fr = bl(fP[0][:, d, ps4], 32); fi = bl(fP[1][:, d, ps4], 32)
                                    TT("dve", bt[0][:], fr, Bl[0][:], ALU.mult, [RS], [RS]); TT("dve", bt[1][:], fi, Bl[1][:], ALU.mult, [RS], [RS])
                                    TT("dve", Bb[d][0][:], bt[0][:], bt[1][:], ALU.subtract, [RS], [RCL])
                                    TT("dve", bt[0][:], fr, Bl[1][:], ALU.mult, [RS], [RS]); TT("dve", bt[1][:], fi, Bl[0][:], ALU.mult, [RS], [RS])
                                    TT("dve", Bb[d][1][:], bt[0][:], bt[1][:], ALU.add, [RS], [RCL])

                            def build_K(q):
                                z = q % 2; CL = CL2[z]; Bb = Bb2[z]; RCL = RCL2[z]
                                for d in range(2):
                                    kb = d
                                    for b in range(4):
                                        MM(bank(kb)[32 * b:32 * b + 32, :], Bb[d][0][:, b, :], CL[d][0][:, b, 0:16, :].rearrange("p t n -> p (t n)"), True, False,
                                           [RCL], [PB[kb]], tp=(0, 32 * b))
                                        MM(bank(kb)[32 * b:32 * b + 32, :], Bb[d][1][:, b, :], CL[d][1][:, b, 0:16, :].rearrange("p t n -> p (t n)"), False, True,
                                           [RCL], [PB[kb]], tp=(0, 32 * b))
                                    for cb in range(4):
                                        TS("dve", Kblk[d][:, :, 32 * cb:32 * cb + 32], bank(kb).rearrange("p (t n) -> p t n", n=32), maskP[:, cb:cb + 1], ALU.mult,
                                           [PB[kb], Rmask], [RK])

                            def run_y(q):
                                z = q % 2; CL = CL2[z]; RCL = RCL2[z]
                                for tb, (c0, cn) in enumerate(BLKS):
                                    yb = 3 + tb
                                    Yb = bank(yb).rearrange("p (c j) -> p c j", j=16)
                                    sblk = sfT[:, q, c0:c0 + cn].rearrange("p (c j) -> p c j", j=16)
                                    MM(bank(yb), Dbl[:, q, :], sfT[:, q, c0:c0 + cn], True, False, [RDbl, RsfT[q]], [PB[yb]])
                                    for d in range(2):
                                        for t in range(16):
                                            if d == 0:
                                                o_, r_ = Yb[:, :, t:16], sblk[:, :, 0:16 - t]
                                            else:
                                                o_, r_ = Yb[:, :, 0:16 - t], sblk[:, :, t:16]
                                            MM(o_, Kblk[d][:, t, :], r_, False, False, [RK, RsfT[q]], [PB[yb]])
                                    GC = os.environ.get('GC', '1') == '1'
                                    for d in range(2):
                                        for j in range(16):
                                            tt = j + 1 if d == 0 else 16 - j
                                            sh = 0 if d == 0 else 1
                                            for r in range(2):
                                                for b in range(4):
                                                    gfirst = (b == 0 and r == 0 and j == 0 and d == 0)
                                                    if tb < 2:
                                                        rhs = Hb[:, d, r, 4 * q + b, 32 * tb + sh:32 * tb + sh + 32]
                                                        o_ = Yb[32 * b:32 * b + 32, :, j]
                                                    else:
                                                        rhs = Hb[:, d, r, 4 * q + b, 65:99].rearrange("p (s c) -> p s c", s=2)[:, :, sh:sh + 16]
                                                        o_ = bank(yb).rearrange("p (s c j) -> p s c j", s=2, j=16)[32 * b:32 * b + 32, :, :, j]
                                                    last = (d == 1 and b == 3 and j == 15 and r == 1)
                                                    MM(o_, CL[d][r][:, b, tt, :], rhs, False, last, [RCL, RHb], [PB[yb]], tp=(0, 32 * b),
                                                       chain=(("gfirst" if gfirst else "gnext") if GC else False))
                                    if ("d_y%d_%d" % (q, tb)) in Ddbg:
                                        ACT(ydb[:], bank(yb), AF.Identity, [PB[yb]], [Rydb])
                                        DBG("d_y%d_%d" % (q, tb), ydb[:], [Rydb])
                                    g_ = tb % 2
                                    ACT(gt[g_][:], bank(yb), AF.Square, [PB[yb]], [Rgt[g_]])
                                    TS("dve", gt[g_][:], gt[g_][:], 0.044715, ALU.mult, [Rgt[g_]], [Rgt[g_]], s2=1.0, op1=ALU.add)
                                    TT("dve", gt[g_][:], gt[g_][:], bank(yb), ALU.mult, [Rgt[g_], PB[yb]], [Rgt[g_]])
                                    ACT(gt[g_][:], gt[g_][:], AF.Sigmoid, [Rgt[g_]], [Rgt[g_]], scale=1.5957691216)
                                    TT("dve", sfT[:, q, c0:c0 + cn], gt[g_][:], bank(yb), ALU.mult, [Rgt[g_], PB[yb]], [RsfT[q]])
                            build_CL(0); build_K(0)
                            for q in range(4):
                                if q + 1 < 4:
                                    build_CL(q + 1)
                                run_y(q)
                                if q + 1 < 4:
                                    build_K(q + 1)
                            fw.barrier()
                    Lp.close()
                    ssmT = sb(L3, "ssmT", [128, 4, TOK], BF16); RssmT = [Res("ssmT%d" % j) for j in range(4)]
                    CH[0] = 'G' in os.environ.get('CHP', 'mabfFGMLUS')
                    with contextlib.ExitStack() as ph:
                        alloc_ws(ph)
                        bgl = sb(ph, "bgl", [128, 4]); Rbgl = Res("bgl"); sg = [sb(ph, "sg%d" % j, [128, 512]) for j in range(2)]; Rsg = [Res("sg0"), Res("sg1")]
                        fw.dma("sp", bgl[:], Din["bglu"][:, l, :], writes=[Rbgl])
                        Wg, RWg = load_w([(lambda t: t[:, 0:2048].rearrange("p (k n) -> p k n", k=4), Din["w_glu"][l].rearrange("(k p) n -> p k n", p=128))])
                        Wgv = Wg[:, 0:2048].rearrange("p (k n) -> p k n", k=4)
                        cnt = 0
                        for m in range(4):
                            for tb, (c0, cn) in enumerate(BLKS):
                                bk = (0, 1, 2)[cnt % 3]; sj = cnt % 2; cnt += 1
                                for k in range(4):
                                    MM(bank(bk), Wgv[:, k, m * 128:(m + 1) * 128], sfT[:, k, c0:c0 + cn], k == 0, k == 3, [RWg, RsfT[k]], [PB[bk]])
                                ACT(sg[sj][:], bank(bk), AF.Sigmoid, [PB[bk], Rbgl], [Rsg[sj]], bias=bgl[:, m:m + 1])
                                TT("dve", ssmT[:, m, c0:c0 + cn], sg[sj][:], sfT[:, m, c0:c0 + cn], ALU.mult, [Rsg[sj], RsfT[m]], [RssmT[m]])
                        fw.barrier()
                    DBG("d_gy", sfT[:, 0:4, :].rearrange("p a b -> p (a b)"), RsfT[0:4])
                    DBG("d_ssmT", ssmT[:].rearrange("p a b -> p (a b)"), RssmT)
                    if stop == "ssm": fw.dead = True
                    CH[0] = 'M' in os.environ.get('CHP', 'mabfFGMLUS')
                    with contextlib.ExitStack() as M:
                        mergedT = sb(M, "mergedT", [128, 8, TOK], BF16); Rmg = [Res("mg%d" % j) for j in range(8)]
                        with contextlib.ExitStack() as ph:
                            alloc_ws(ph)
                            uT = sb(ph, "uT", [128, 8, TOK], BF16); RuT = [Res("uT%d" % i) for i in range(NT)]
                            gs = [sb(ph, "gs%d" % j, [128, 512]) for j in range(3)]; Rgs = [Res("gs%d" % j) for j in range(3)]
                            ac = [sb(ph, "ac%d" % j, [128, 512]) for j in range(2)]; Rac = [Res("ac0"), Res("ac1")]
                            build_uT(uT, RuT, 0)
                            win = Din["w_in"][l].rearrange("(k p) n -> p k n", p=128)
                            wbr = [Din[nm][l].rearrange("(k p) n -> p k n", p=128) for nm in ("w_br_attn", "w_br_ssm", "w_br_four")]
                            srcs = [(attnT, RattnT), (ssmT, RssmT), (fourT, RfourT)]
                            Wg2 = [sb(ph, "Wg2_%d" % z, [128, 3072], BF16) for z in range(2)]; RWg2 = [Res("Wg2a"), Res("Wg2b")]
                            Wb2 = [sb(ph, "Wb2_%d" % z, [128, 1536], BF16) for z in range(2)]; RWb2 = [Res("Wb2a"), Res("Wb2b")]
                            def load_c(c):
                                z = c % 2
                                for g in range(3):
                                    fw.dma("pool", Wg2[z][:, g * 1024:(g + 1) * 1024].rearrange("p (k n) -> p k n", k=8),
                                           win[:, :, 1792 + g * 1024 + c * 128:1792 + g * 1024 + (c + 1) * 128], writes=[RWg2[z]])
                                for x in range(3):
                                    fw.dma("pool", Wb2[z][:, x * 512:(x + 1) * 512].rearrange("p (k n) -> p k n", k=4),
                                           wbr[x][:, :, c * 128:(c + 1) * 128], writes=[RWb2[z]])
                            load_c(0)
                            for c in range(8):
                                if c + 1 < 8:
                                    load_c(c + 1)
                                Wg, RWg, Wb, RWb = Wg2[c % 2], RWg2[c % 2], Wb2[c % 2], RWb2[c % 2]
                                for tb, (c0, cn) in enumerate(BLKS):
                                    for g in range(3):
                                        Wgv = Wg[:, g * 1024:(g + 1) * 1024].rearrange("p (k n) -> p k n", k=8)
                                        for k in range(8):
                                            MM(bank(g), Wgv[:, k, :], uT[:, k, c0:c0 + cn], k == 0, k == 7, RuT[c0 // 128:(c0 + cn) // 128] + [RWg], [PB[g]])
                                        ACT(gs[g][:], bank(g), AF.Sigmoid, [PB[g]], [Rgs[g]])
                                    for x in range(3):
                                        Wbv = Wb[:, x * 512:(x + 1) * 512].rearrange("p (k n) -> p k n", k=4)
                                        st_, Rst = srcs[x]
                                        for k in range(4):
                                            MM(bank(3 + x), Wbv[:, k, :], st_[:, k, c0:c0 + cn], k == 0, k == 3, [Rst[k], RWb], [PB[3 + x]])
                                    TT("dve", ac[0][:], gs[0][:], bank(3), ALU.mult, [Rgs[0], PB[3]], [Rac[0]])
                                    TT("dve", ac[1][:], gs[1][:], bank(4), ALU.mult, [Rgs[1], PB[4]], [Rac[1]])
                                    TT("dve", ac[0][:], ac[0][:], ac[1][:], ALU.add, Rac, [Rac[0]])
                                    TT("dve", ac[1][:], gs[2][:], bank(5), ALU.mult, [Rgs[2], PB[5]], [Rac[1]])
                                    TT("dve", mergedT[:, c, c0:c0 + cn], ac[0][:], ac[1][:], ALU.add, Rac, [Rmg[c]])
                            fw.barrier()
                        DBG("d_mergedT", mergedT[:].rearrange("p a b -> p (a b)"), Rmg)
                        if stop == "merge": fw.dead = True
                        layer_norm_residual(l, 0, 8, lambda i, k: mergedT[:, k, i * 128:(i + 1) * 128], lambda i, k: [Rmg[k]],
                                            Din["w_out"][l].rearrange("(k p) n -> p k n", p=128))
              if "d_x1" in Ddbg:
                  for i in range(NT):
                      fw.dma("sp", Ddbg["d_x1"][i * 128:(i + 1) * 128, :], X[:, i, :], reads=[RX[i]], writes=[Rdbg])
              if stop == "ln1": fw.dead = True
              CH[0] = 'U' in os.environ.get('CHP', 'mabfFGMLUS')
              with contextlib.ExitStack() as Fs:
                  actT = sb(Fs, "actT", [128, 22, TOK], BF16); Ract = [Res("act%d" % j) for j in range(22)]
                  with contextlib.ExitStack() as ph:
                      alloc_ws(ph)
                      uT = sb(ph, "uT", [128, 8, TOK], BF16); RuT = [Res("uT%d" % i) for i in range(NT)]
                      cvp = sb(ph, "cvp", [128, 44, 4]); Rcv = Res("cvp")
                      hc = [sb(ph, "hc%d" % j, [128, TOK]) for j in range(2)]; Rhc = [Res("hc0"), Res("hc1")]
                      fw.dma("sp", cvp[:], Din["convp"][:, l], writes=[Rcv])
                      wup = Din["w_up"][l].rearrange("(k p) n -> p k n", p=128)
                      def load_up(bi):
                          return load_w([(lambda t: t[:, 0:4096].rearrange("p (k n) -> p k n", k=8), wup[:, :, bi * 512:(bi + 1) * 512])])
                      nxt_up = load_up(0)
                      build_uT(uT, RuT, 1)
                      for ch in range(44):
                          if ch % 4 == 0:
                              W, RW = nxt_up
                              if ch // 4 + 1 < 11:
                                  nxt_up = load_up(ch // 4 + 1)
                              Wv = W[:, 0:4096].rearrange("p (k n) -> p k n", k=8)
                          cw = ch % 4; pb0 = 3 * (ch % 2); hj = ch % 2
                          prs = [PB[pb0], PB[pb0 + 1], PB[pb0 + 2]]
                          for tb, (c0, cn) in enumerate(BLKS):
                              for k in range(8):
                                  MM(bank(pb0 + tb), Wv[:, k, cw * 128:(cw + 1) * 128], uT[:, k, c0:c0 + cn], k == 0, k == 7,
                                     RuT[c0 // 128:(c0 + cn) // 128] + [RW], [prs[tb]])
                          hp = bank(pb0, 3); h_ = hc[hj]
                          ACT(h_[:], hp, AF.Identity, prs + [Rcv], [Rhc[hj]], scale=cvp[:, ch, 1:2], bias=cvp[:, ch, 3:4])
                          def stt(o_, i0, sc, i1):
                              OP("dve", lambda e: e.scalar_tensor_tensor(out=o_, in0=i0, scalar=sc, in1=i1, op0=ALU.mult, op1=ALU.add),
                                 prs + [Rcv, Rhc[hj]], [Rhc[hj]])
                          stt(h_[:, 1:1024], hp[:, 0:1023], cvp[:, ch, 0:1], h_[:, 1:1024])
                          stt(h_[:, 0:1023], hp[:, 1:1024], cvp[:, ch, 2:3], h_[:, 0:1023])
                          h3 = h_[:, 1024:1536].rearrange("p (s c) -> p s c", s=2); p3 = hp[:, 1024:1536].rearrange("p (s c) -> p s c", s=2)
                          stt(h3[:, :, 1:256], p3[:, :, 0:255], cvp[:, ch, 0:1], h3[:, :, 1:256])
                          stt(h3[:, :, 0:255], p3[:, :, 1:256], cvp[:, ch, 2:3], h3[:, :, 0:255])
                          if ch < 22:
                              ACT(actT[:, ch, :], h_[:], AF.Silu, [Rhc[hj]], [Ract[ch]])
                          else:
                              TT("dve", actT[:, ch - 22, :], actT[:, ch - 22, :], h_[:], ALU.mult, [Ract[ch - 22], Rhc[hj]], [Ract[ch - 22]])
                      fw.barrier()
                  DBG("d_actT", actT[:].rearrange("p a b -> p (a b)"), Ract)
                  if stop == "ffn": fw.dead = True
                  layer_norm_residual(l, 1, 22, lambda i, k: actT[:, k, i * 128:(i + 1) * 128], lambda i, k: [Ract[k]],
                                      Din["w_down"][l].rearrange("(k p) n -> p k n", p=128))
        layers()
        fw.dead = False
        for i in range(NT):
            fw.dma("sp", Dout["y"][i * 128:(i + 1) * 128, :], X[:, i, :], reads=[RX[i]], writes=[Rdbg])
        fw.finish()
    return nc


def kernel(**inputs):
    inp = {k: np.asarray(v) for k, v in inputs.items()}
    S = prep_shared(inp)
    nc = build()
    in_maps = []
    for cid in range(8):
        m = dict(S); m.update(prep_core(inp, cid))
        in_maps.append(m)
    res = run_bass_kernel_spmd(nc, in_maps, core_ids=list(range(8)))
    y_p = np.zeros((16, 256, 1024), np.float32); y_s = np.zeros((8, 1024, 1024), np.float32)
    nk = np.zeros((16, 2, 256, 2, 64), np.float32); nv = np.zeros((16, 2, 256, 2, 64), np.float32)
    nst = np.zeros((16, 2, 2, 32, 64, 2), np.float32)
    for cid in range(8):
        r = res.results[cid]
        y_s[cid] = r["y"][0:1024]
        y_p[2 * cid:2 * cid + 2] = r["y"][1024:1536].reshape(2, 256, 1024)
        nk[2 * cid:2 * cid + 2] = r["nk"].reshape(2, 2, 256, 2, 64)
        nv[2 * cid:2 * cid + 2] = r["nv"].reshape(2, 2, 256, 2, 64)
        nst[2 * cid:2 * cid + 2] = r["nst"]
    return (y_p, y_s, nk, nv, nst)
```

```python
import contextlib, os
SK = os.environ.get("SK", "")
import numpy as np
import concourse.bass as bass
import concourse.mybir as mybir
from concourse.bass_utils import run_bass_kernel_spmd

F32 = mybir.dt.float32; BF16 = mybir.dt.bfloat16
AF = mybir.ActivationFunctionType; ALU = mybir.AluOpType; AX = mybir.AxisListType

D_MODEL = 1024; DEPTH = 2; D_IN = 4864; D_FF = 2816
ALPHA = (2 * DEPTH) ** 0.25
EPS = 1e-6
NT = 12; TOK = 1536; T = 16
SEQS = [(0, 1024), (1024, 256), (1280, 256)]
BLKS = [(0, 512), (512, 512), (1024, 512)]
TILE_VEC = [0] * 8 + [1] * 4
NS = 99
SOFF = [0, 65, 82]
SCH = [64, 16, 16]


class StopBuild(Exception):
    pass


class Res:
    __slots__ = ("name", "w", "r")
    def __init__(self, name):
        self.name = name; self.w = None; self.r = {}


class Eng:
    def __init__(self, name, h, sem):
        self.name = name; self.h = h; self.sem = sem; self.count = 0; self.waited = {}


class FW:
    NDMA = 32
    def __init__(self, nc, es):
        self.nc = nc; self.sems = {}; self.eng = {}
        for name, h in (("pe", nc.tensor), ("act", nc.scalar), ("dve", nc.vector), ("pool", nc.gpsimd), ("sp", nc.sync)):
            s = es.enter_context(nc.semaphore("s_" + name))
            self.sems[name] = s; self.eng[name] = Eng(name, h, s)
        self.dma_sems = []
        for i in range(self.NDMA):
            s = es.enter_context(nc.semaphore("s_dma%d" % i))
            self.sems["dma%d" % i] = s; self.dma_sems.append(["dma%d" % i, 0])
        self.dma_i = 0; self.ninstr = 0; self.dead = False; self.prev_plain = False; self.rec = None

    def _wait(self, e, key, val):
        if e.waited.get(key, 0) >= val: return
        e.h.wait_ge(self.sems[key], val); e.waited[key] = val

    def _deps(self, e, reads, writes, chain=False):
        deps = {}
        def add(kv):
            if kv is None: return
            k, v = kv
            if deps.get(k, 0) < v: deps[k] = v
        for r in reads: add(r.w)
        for w in writes:
            if w.w is not None and not (chain and w.w[0] == e.name):
                add(w.w)
            for k, v in w.r.items():
                if k == e.name: continue
                add((k, v))
        return deps

    def op(self, engname, fn, reads=(), writes=(), chain=False):
        if self.dead: return None
        if self.rec is not None:
            self.rec.append(("op", engname, fn, tuple(reads), tuple(writes), chain)); return None
        e = self.eng[engname]
        if engname == "pe" and chain in ("gfirst", "gnext"):
            if chain == "gfirst" and e.count > 0:
                self._wait(e, "pe", e.count)
            chain = True; self.prev_plain = False
        elif engname == "pe":
            plain = chain
            chain = chain and self.prev_plain
            if plain and not chain and e.count > 0:
                self._wait(e, "pe", e.count)
            self.prev_plain = plain
        for k, v in self._deps(e, reads, writes, chain).items(): self._wait(e, k, v)
        ins = fn(e.h); e.count += 1; ins.then_inc(e.sem, 1)
        for r in reads: r.r[e.name] = e.count
        for w in writes: w.w = (e.name, e.count); w.r = {}
        self.ninstr += 1
        return ins

    def dma(self, qname, out, in_, reads=(), writes=()):
        if self.dead: return None
        if self.rec is not None:
            self.rec.append(("dma", qname, out, in_, tuple(reads), tuple(writes))); return None
        e = self.eng[qname]
        deps = self._deps(e, reads, writes)
        slot = self.dma_sems[self.dma_i % self.NDMA]; self.dma_i += 1
        key = slot[0]
        if slot[1] > 0: deps[key] = max(deps.get(key, 0), slot[1])
        for k, v in deps.items(): self._wait(e, k, v)
        ins = e.h.dma_start(out=out, in_=in_)
        slot[1] += 16; ins.then_inc(self.sems[key], 16)
        for r in reads: r.r[key] = slot[1]
        for w in writes: w.w = (key, slot[1]); w.r = {}
        self.ninstr += 1
        return ins

    def replay(self, items):
        for it in items:
            if it[0] == "op": self.op(it[1], it[2], it[3], it[4], it[5])
            else: self.dma(it[1], it[2], it[3], reads=it[4], writes=it[5])

    def barrier(self):
        if self.dead: return
        targets = {n: e.count for n, e in self.eng.items() if e.count > 0}
        for key, cnt in self.dma_sems:
            if cnt > 0: targets[key] = cnt
        for n, e in self.eng.items():
            for k, v in targets.items(): self._wait(e, k, v)

    def finish(self):
        self.dead = False
        self.barrier()


def bl(ap, n):
    return ap.unsqueeze(2).to_broadcast([ap.shape[0], ap.shape[1], n])


def bm(ap, n):
    return ap.unsqueeze(1).to_broadcast([ap.shape[0], n, ap.shape[1]])


def _rep(a, n=128):
    return np.ascontiguousarray(np.broadcast_to(a[None], (n,) + a.shape))


def prep_shared(inp):
    f = lambda a: np.ascontiguousarray(a, dtype=np.float32)
    L = DEPTH
    S = {}
    for k in ("w_ada", "w_in", "w_glu", "w_br_ssm", "w_br_four", "w_out", "w_up", "w_down"):
        S[k] = f(inp[k])
    S["w_br_attn"] = f(inp["w_br_attn"])
    S["b_adaP"] = f(inp["b_ada"].reshape(L, 48, 128).transpose(2, 0, 1))
    S["bgbc"] = f(np.stack([np.stack([_rep(inp["b_ada"][l, 2048:3072]), _rep(inp["b_ada"][l, 5120:6144])], 1) for l in range(L)]))
    S["lnbc"] = f(np.stack([np.stack([_rep(inp[k][l]) for k in ("ln1_g", "ln1_b", "ln2_g", "ln2_b")]) for l in range(L)]))
    cw = inp["conv_w"].reshape(L, 3, 44, 128).transpose(3, 0, 2, 1)
    cb = inp["conv_b"].reshape(L, 44, 128).transpose(2, 0, 1)[..., None]
    S["convp"] = f(np.concatenate([cw, cb], -1))
    S["bglu"] = f(inp["b_glu"].reshape(L, 4, 128).transpose(2, 0, 1))
    S["g640"] = f(np.stack([_rep(np.concatenate([np.tile(inp["q_norm_g"][l], 8), np.tile(inp["k_norm_g"][l], 4)])) for l in range(L)]))
    def lay(a):
        aP = a.reshape(L, 2, 16, 2, 64).transpose(0, 1, 3, 4, 2).reshape(L, 2, 128, 16)
        aF = a.reshape(L, 2, 4, 4, 2, 64).transpose(0, 1, 3, 2, 4, 5)
        aF = np.broadcast_to(aF[:, :, :, None], (L, 2, 4, 32, 4, 2, 64)).reshape(L, 2, 128, 512)
        return np.concatenate([aP, aF], -1)
    ldt = np.broadcast_to(inp["ssm_log_dt"][..., None], (L, 2, 32, 64))
    S["lamin"] = f(np.stack([lay(inp["ssm_a_re"]), lay(inp["ssm_a_im"]), lay(ldt)], 2))
    def bbdP(b):
        Bp = b.reshape(L, 2, 16, 2, 64, 16)
        out = np.zeros((L, 2, 2, 64, 16, 2, 16), np.float32)
        for g2 in range(2):
            out[:, :, g2, :, :, g2, :] = Bp[:, :, :, g2].transpose(0, 1, 3, 2, 4)
        return out.reshape(L, 2, 128, 16, 32)
    def bbdT(b):
        Bq = b.reshape(L, 2, 4, 4, 2, 64, 16)
        out = np.zeros((L, 2, 4, 2, 16, 4, 2, 64), np.float32)
        for g2 in range(2):
            out[:, :, :, g2, :, :, g2, :] = Bq[:, :, :, :, g2].transpose(0, 1, 3, 5, 2, 4)
        return out.reshape(L, 2, 128, 4, 128)
    def cbdP(c):
        Cp = c.reshape(L, 2, 16, 2, 16, 64)
        out = np.zeros((L, 2, 2, 64, 16, 2, 16), np.float32)
        for g2 in range(2):
            out[:, :, g2, :, :, g2, :] = Cp[:, :, :, g2].transpose(0, 1, 4, 2, 3)
        return out.reshape(L, 2, 128, 16, 32)
    S["BbdP"] = f(np.stack([bbdP(inp["ssm_b_re"]), bbdP(inp["ssm_b_im"])], 2))
    S["BbdT"] = f(np.stack([bbdT(inp["ssm_b_re"]), bbdT(inp["ssm_b_im"])], 2))
    S["CbdP"] = f(np.stack([cbdP(inp["ssm_c_re"]), cbdP(inp["ssm_c_im"])], 2))
    Dblk = np.zeros((L, 128, 4, 128), np.float32)
    for l in range(L):
        for q in range(4):
            Dblk[l, np.arange(128), q, np.arange(128)] = inp["ssm_d"][l, q * 128:(q + 1) * 128]
    S["Dblk"] = Dblk
    S["ident"] = np.eye(128, dtype=np.float32)
    mk = np.zeros((128, 4), np.float32); mk[np.arange(128), np.arange(128) // 32] = 1.0
    S["maskP"] = mk
    t = np.arange(1024)
    freqs = (10000.0 ** (-np.arange(16, dtype=np.float32) / 16)).astype(np.float32)
    ang = np.concatenate([(t // 64).astype(np.float32)[:, None] * freqs, (t % 64).astype(np.float32)[:, None] * freqs], -1)
    S["ropec"] = f(np.cos(ang).reshape(8, 128, 32).transpose(1, 0, 2))
    S["ropes"] = f(np.sin(ang).reshape(8, 128, 32).transpose(1, 0, 2))
    def dft(n):
        i = np.arange(n, dtype=np.int64)
        m = (i[:, None] * i[None, :]) % n
        a = 2.0 * np.pi * m.astype(np.float64) / n
        return np.cos(a) / np.sqrt(n), np.sin(a) / np.sqrt(n)
    c128, s128 = dft(128)
    S["cs128"] = f(np.concatenate([c128, s128], 1))
    for n in (1024, 256):
        c, s = dft(n)
        S["cl%d" % n] = f(c.reshape(n // 128, 128, n).transpose(1, 0, 2))
        S["sl%d" % n] = f((-s).reshape(n // 128, 128, n).transpose(1, 0, 2))
    return S


def prep_core(inp, cid):
    f = lambda a: np.ascontiguousarray(a, dtype=np.float32)
    m = {}
    m["xin"] = f(np.concatenate([inp["x_sample"][cid], inp["x_prompt"][2 * cid:2 * cid + 2].reshape(512, 1024)], 0))
    cv = np.stack([inp["c"][cid], inp["c_ctx"]], -1)
    m["cvec"] = f(cv.reshape(8, 128, 2).transpose(1, 0, 2))
    m["ckT"] = f(inp["cache_k"][cid].reshape(DEPTH, 512, 128).transpose(0, 2, 1))
    m["cv"] = f(inp["cache_v"][cid].reshape(DEPTH, 512, 128))
    st = inp["state_ssm"][cid].reshape(DEPTH, 2, 16, 2, 64, 2)
    m["h0"] = f(st.transpose(3, 4, 0, 1, 5, 2).reshape(128, DEPTH, 2, 2, 16))
    return m


IN_SHAPES = {
    "xin": (1536, 1024), "cvec": (128, 8, 2), "ckT": (2, 128, 512), "cv": (2, 512, 128), "h0": (128, 2, 2, 2, 16),
    "w_ada": (2, 1024, 6144), "w_in": (2, 1024, 4864), "w_glu": (2, 512, 512), "w_br_attn": (2, 512, 1024),
    "w_br_ssm": (2, 512, 1024), "w_br_four": (2, 512, 1024), "w_out": (2, 1024, 1024), "w_up": (2, 1024, 5632),
    "w_down": (2, 2816, 1024), "b_adaP": (128, 2, 48), "bgbc": (2, 128, 2, 1024), "lnbc": (2, 4, 128, 1024),
    "convp": (128, 2, 44, 4), "bglu": (128, 2, 4), "g640": (2, 128, 768), "lamin": (2, 2, 3, 128, 528),
    "BbdP": (2, 2, 2, 128, 16, 32), "BbdT": (2, 2, 2, 128, 4, 128), "CbdP": (2, 2, 2, 128, 16, 32),
    "Dblk": (2, 128, 4, 128), "ident": (128, 128), "maskP": (128, 4), "ropec": (128, 8, 32), "ropes": (128, 8, 32),
    "cs128": (128, 256), "cl1024": (128, 8, 1024), "sl1024": (128, 8, 1024), "cl256": (128, 2, 256), "sl256": (128, 2, 256),
}
OUT_SHAPES = {"y": (1536, 1024), "nk": (2, 2, 256, 128), "nv": (2, 2, 256, 128), "nst": (2, 2, 2, 32, 64, 2)}


def build(n_layers=DEPTH, stop=None, dbg_shapes=None):
    nc = bass.Bass("TRN2", target_bir_lowering=False)
    Din = {k: nc.dram_tensor(k, list(s), F32, kind="ExternalInput").ap() for k, s in IN_SHAPES.items()}
    Dout = {k: nc.dram_tensor(k, list(s), F32, kind="ExternalOutput").ap() for k, s in OUT_SHAPES.items()}
    Ddbg = {k: nc.dram_tensor(k, list(s), F32, kind="ExternalOutput").ap() for k, s in (dbg_shapes or {}).items()}
    es = contextlib.ExitStack()
    with es:
        fw = FW(nc, es)
        uid = [0]
        def sb(st, name, shape, dt=F32):
            uid[0] += 1
            return st.enter_context(nc.sbuf_tensor("%s_%d" % (name, uid[0]), list(shape), dt))
        OP = fw.op
        CH = [True]
        def MM(out, lhsT, rhs, start, stop_, r, w, tp=None, chain=None):
            if chain is None: chain = CH[0] and (tp is None)
            if tp is None:
                return fw.op("pe", lambda e: e.matmul(out, lhsT=lhsT, rhs=rhs, start=start, stop=stop_), r, w, chain=chain)
            return fw.op("pe", lambda e: e.matmul(out, lhsT=lhsT, rhs=rhs, start=start, stop=stop_, tile_position=tp), r, w, chain=chain)
        def TT(eng, out, in0, in1, op, r, w):
            return fw.op(eng, lambda e: e.tensor_tensor(out=out, in0=in0, in1=in1, op=op), r, w)
        def TS(eng, out, in0, s1, op0, r, w, s2=None, op1=None):
            if op1 is None:
                return fw.op(eng, lambda e: e.tensor_scalar(out=out, in0=in0, scalar1=s1, scalar2=None, op0=op0), r, w)
            return fw.op(eng, lambda e: e.tensor_scalar(out=out, in0=in0, scalar1=s1, scalar2=s2, op0=op0, op1=op1), r, w)
        def ACT(out, in_, func, r, w, scale=1.0, bias=0.0):
            return fw.op("act", lambda e: e.activation(out=out, in_=in_, func=func, bias=bias, scale=scale), r, w)
        def CP(eng, out, in_, r, w):
            if eng == "act":
                return fw.op("act", lambda e: e.copy(out=out, in_=in_), r, w)
            return fw.op(eng, lambda e: e.tensor_copy(out=out, in_=in_), r, w)
        Rdbg = Res("dbg")
        def DBG(name, ap, r):
            if name in Ddbg:
                fw.dma("pool", Ddbg[name], ap, reads=r, writes=[Rdbg])

        X = sb(es, "X", [128, NT, 1024]); RX = [Res("X%d" % i) for i in range(NT)]
        ident_f = sb(es, "ident_f", [128, 128]); ident_b = sb(es, "ident_b", [128, 128], BF16); Rid = Res("id")
        maskP = sb(es, "maskP", [128, 4]); Rmask = Res("mask")
        modP = sb(es, "modP", [128, 6, 8, 2]); opsc = sb(es, "opsc", [128, 2, 8, 2]); Rmod = Res("mod")
        csl = sb(es, "csl", [128, 8, 2]); sTb = sb(es, "sTb", [128, 8, 2], BF16); sRep = sb(es, "sRep", [128, 2, 8, 128], BF16)
        Rcs = Res("cs")
        stat = sb(es, "stat", [128, NT, 2, 6]); mv = sb(es, "mv", [128, NT, 2]); rs = sb(es, "rs", [128, NT, 2]); Rstat = Res("stat")
        ps = es.enter_context(nc.psum_tensor("ps", [128, 3584], F32)); PB = [Res("pb%d" % i) for i in range(7)]
        pst = es.enter_context(nc.psum_tensor("pst", [128, 1024], BF16)); PT = Res("pt")
        wslot = [None, None]; Rws = [None, None]
        wctr = [0]
        def alloc_ws(st):
            for i_ in range(2):
                wslot[i_] = sb(st, "wslot%d" % i_, [128, 4096], BF16); Rws[i_] = Res("ws%d" % i_)
        def load_w(views):
            s_ = wctr[0] % 2; wctr[0] += 1
            for dstf, src in views:
                fw.dma("pool", dstf(wslot[s_]), src, writes=[Rws[s_]])
            return wslot[s_], Rws[s_]
        def bank(i, n=1):
            return ps[:, i * 512:(i + n) * 512]

        fw.dma("sp", ident_f[:], Din["ident"], writes=[Rid])
        fw.dma("pool", ident_b[:], Din["ident"], writes=[Rid])
        fw.dma("sp", maskP[:], Din["maskP"], writes=[Rmask])
        for i in range(NT):
            fw.dma("sp", X[:, i, :], Din["xin"][i * 128:(i + 1) * 128, :], writes=[RX[i]])
        fw.dma("sp", csl[:], Din["cvec"], writes=[Rcs])
        ACT(csl[:], csl[:], AF.Silu, [Rcs], [Rcs])
        CP("dve", sTb[:], csl[:], [Rcs], [Rcs])
        for v in range(2):
            CP("dve", sRep[:, v, :, :], bl(csl[:, :, v], 128), [Rcs], [Rcs])

        def build_uT(uT, RuT, sub):
            sh_sec = 0 if sub == 0 else 3
            for i in range(NT):
                v = TILE_VEC[i]
                for kk in range(2):
                    bk = kk
                    for k4 in range(4):
                        k = kk * 4 + k4
                        OP("pe", lambda e: e.transpose(bank(bk)[:, k4 * 128:(k4 + 1) * 128], X[:, i, k * 128:(k + 1) * 128], ident_f[:]),
                           [RX[i], Rid], [PB[bk]])
                    for k4 in range(4):
                        k = kk * 4 + k4
                        ACT(uT[:, k, i * 128:(i + 1) * 128], bank(bk)[:, k4 * 128:(k4 + 1) * 128], AF.Identity,
                            [PB[bk], Rmod], [RuT[i]], scale=opsc[:, sub, k, v:v + 1], bias=modP[:, sh_sec, k, v:v + 1])

        def layer_norm_residual(l, which, kc, lhsT_fn, lres_fn, wsrc3, out_tile=None):
            with contextlib.ExitStack() as ph:
                ch_save = CH[0]; CH[0] = 'L' in os.environ.get('CHP', 'mabfFGMLUS')
                alloc_ws(ph)
                lnb = sb(ph, "lnb", [128, 2, 1024]); Rln = Res("lnb")
                tmp = [sb(ph, "lntmp%d" % j, [128, 512]) for j in range(2)]; Rtmp = [Res("lntmp0"), Res("lntmp1")]
                nWd = 2 if kc <= 8 else 1
                Wds = [sb(ph, "Wd%d" % z, [128, kc, 512], BF16) for z in range(nWd)]; RWds = [Res("Wd%d" % z) for z in range(nWd)]
                def load_half(nb):
                    z = nb % nWd
                    for k0 in range(0, kc, 8):
                        k1 = min(kc, k0 + 8)
                        fw.dma("pool", Wds[z][:, k0:k1, :], wsrc3[:, k0:k1, nb * 512:(nb + 1) * 512], writes=[RWds[z]])
                load_half(0)
                if nWd == 2:
                    load_half(1)
                fw.dma("sp", lnb[:, 0, :], Din["lnbc"][l, 2 * which], writes=[Rln])
                fw.dma("sp", lnb[:, 1, :], Din["lnbc"][l, 2 * which + 1], writes=[Rln])
                gbc = sb(ph, "gbc", [128, 2, 1024]); Rgbc = Res("gbc")
                bg = sb(ph, "bg", [128, 1024]); Rbg = Res("bg")
                fw.dma("sp", bg[:], Din["bgbc"][l, :, which, :], writes=[Rbg])
                sec = 2 if which == 0 else 5
                wsrc = Din["w_ada"][l].rearrange("(k p) n -> p k n", p=128)
                for half in range(2):
                    Wg, RWg = load_w([(lambda t: t[:, 0:4096].rearrange("p (k n) -> p k n", k=8), wsrc[:, :, sec * 1024 + half * 512:sec * 1024 + (half + 1) * 512])])
                    Wgv = Wg[:, 0:4096].rearrange("p (k n) -> p k n", k=8)
                    for v in range(2):
                        for k in range(8):
                            MM(bank(v), sRep[:, v, k, :], Wgv[:, k, :], k == 0, k == 7, [RWg, Rcs], [PB[v]])
                        TT("dve", gbc[:, v, half * 512:(half + 1) * 512], bank(v), bg[:, half * 512:(half + 1) * 512], ALU.add, [PB[v], Rbg], [Rgbc])
                cnt = 0
                for nb in range(2):
                    cs = slice(nb * 512, (nb + 1) * 512)
                    Wd = Wds[nb % nWd]; RWd = RWds[nb % nWd]
                    if nWd == 1 and nb == 1:
                        load_half(1)
                    for i in range(NT):
                        v = TILE_VEC[i]
                        bk = 2 + (cnt % 4); tj = cnt % 2; cnt += 1
                        for k in range(kc):
                            MM(bank(bk), lhsT_fn(i, k), Wd[:, k, :], k == 0, k == kc - 1, lres_fn(i, k) + [RWd], [PB[bk]])
                        TT("dve", tmp[tj][:], bank(bk), gbc[:, v, cs], ALU.mult, [PB[bk], Rgbc], [Rtmp[tj]])
                        OP("dve", lambda e: e.scalar_tensor_tensor(out=X[:, i, cs], in0=X[:, i, cs], scalar=float(ALPHA), in1=tmp[tj][:],
                                                                   op0=ALU.mult, op1=ALU.add), [RX[i], Rtmp[tj]], [RX[i]])
                for i in range(NT):
                    for hh in range(2):
                        OP("dve", lambda e: e.bn_stats(out=stat[:, i, hh, :], in_=X[:, i, hh * 512:(hh + 1) * 512]), [RX[i]], [Rstat])
                    OP("dve", lambda e: e.bn_aggr(out=mv[:, i, :], in_=stat[:, i, :, :].rearrange("p a b -> p (a b)")), [Rstat], [Rstat])
                TS("dve", rs[:, :, 0], mv[:, :, 1], EPS, ALU.add, [Rstat], [Rstat])
                ACT(rs[:, :, 0], rs[:, :, 0], AF.Sqrt, [Rstat], [Rstat])
                OP("dve", lambda e: e.reciprocal(out=rs[:, :, 0], in_=rs[:, :, 0]), [Rstat], [Rstat])
                OP("dve", lambda e: e.scalar_tensor_tensor(out=rs[:, :, 1], in0=mv[:, :, 0], scalar=-1.0, in1=rs[:, :, 0],
                                                           op0=ALU.mult, op1=ALU.mult), [Rstat], [Rstat])
                for i in range(NT):
                    ACT(X[:, i, :], X[:, i, :], AF.Identity, [RX[i], Rstat], [RX[i]], scale=rs[:, i, 0:1], bias=rs[:, i, 1:2])
                    TT("pool", X[:, i, :], X[:, i, :], lnb[:, 0, :], ALU.mult, [RX[i], Rln], [RX[i]])
                    TT("pool", X[:, i, :], X[:, i, :], lnb[:, 1, :], ALU.add, [RX[i], Rln], [RX[i]])
                    if out_tile is not None:
                        out_tile(i)
                fw.barrier()
                CH[0] = ch_save

        def layers():
          for l in range(n_layers):
              with contextlib.ExitStack() as ph:
                  wa = [sb(ph, "wa%d" % i, [128, 8, 512], BF16) for i in range(3)]; Rwa = [Res("wa%d" % i) for i in range(3)]
                  bP = sb(ph, "bP", [128, 48]); Rb = Res("bmod")
                  fw.dma("sp", bP[:], Din["b_adaP"][:, l, :], writes=[Rb])
                  wsrc = Din["w_ada"][l].rearrange("(k p) n -> p k n", p=128)
                  pm = bank(6)[:, 0:96]
                  CH[0] = 'm' in os.environ.get('CHP', 'mabfFGMLUS')
                  cbs = [cb for cb in range(12) if cb // 2 not in (2, 5)]
                  for i_ in range(min(3, len(cbs))):
                      fw.dma("pool", wa[i_][:], wsrc[:, :, cbs[i_] * 512:(cbs[i_] + 1) * 512], writes=[Rwa[i_]])
                  for i_, cb in enumerate(cbs):
                      s = i_ % 3
                      sec, half = cb // 2, cb % 2
                      if i_ >= 3 and False:
                          pass
                      for m in range(4):
                          col = (sec * 8 + half * 4 + m) * 2
                          for k in range(8):
                              MM(pm[:, col:col + 2], wa[s][:, k, m * 128:(m + 1) * 128], sTb[:, k, :], k == 0, k == 7,
                                 [Rwa[s], Rcs], [PB[6]])
                      if i_ + 3 < len(cbs):
                          fw.dma("pool", wa[s][:], wsrc[:, :, cbs[i_ + 3] * 512:(cbs[i_ + 3] + 1) * 512], writes=[Rwa[s]])
                  for sec in (0, 1, 3, 4):
                      TT("dve", modP[:, sec, :, :], pm[:, sec * 16:(sec + 1) * 16].rearrange("p (c v) -> p c v", v=2),
                         bl(bP[:, sec * 8:(sec + 1) * 8], 2), ALU.add, [PB[6], Rb], [Rmod])
                  TS("dve", opsc[:, 0, :, :], modP[:, 1, :, :], 1.0, ALU.add, [Rmod], [Rmod])
                  TS("dve", opsc[:, 1, :, :], modP[:, 4, :, :], 1.0, ALU.add, [Rmod], [Rmod])
                  fw.barrier()
              if stop == "mod": fw.dead = True
              DBG("d_modP", modP[:].rearrange("p a b c -> p (a b c)"), [Rmod])

              with contextlib.ExitStack() as L1:
                  QT = sb(L1, "QT", [128, 4, TOK], BF16); RQ = [Res("QT%d" % j) for j in range(4)]; attnT = QT; RattnT = RQ
                  sfT = sb(L1, "sfT", [128, 8, TOK], BF16); RsfT = [Res("sfT%d" % j) for j in range(8)]
                  fourT = sb(L1, "fourT", [128, 4, TOK], BF16); RfourT = [Res("fourT%d" % j) for j in range(4)]
                  Lp = contextlib.ExitStack()
                  lamD = [[sb(Lp, "lam%d%d" % (d_, r), [128, 528]) for r in range(2)] for d_ in range(2)]
                  ffD = [[sb(Lp, "ff%d%d" % (d_, r), [128, 528]) for r in range(2)] for d_ in range(2)]
                  PowP = [sb(Lp, "PowP%d" % r, [128, 2, 16, 17]) for r in range(2)]
                  LH = [sb(Lp, "LH%d" % r, [128, 2, 16, 7]) for r in range(2)]
                  fP = [sb(Lp, "fP%d" % r, [128, 2, 16]) for r in range(2)]
                  RS = Res("ssm")
                  def cmul(dr, di, ar, ai, br, bi, m1, m2):
                      TT("dve", m1, ar, br, ALU.mult, [RS], [RS]); TT("dve", m2, ai, bi, ALU.mult, [RS], [RS])
                      TT("dve", dr, m1, m2, ALU.subtract, [RS], [RS])
                      TT("dve", m1, ar, bi, ALU.mult, [RS], [RS]); TT("dve", m2, ai, br, ALU.mult, [RS], [RS])
                      TT("dve", di, m1, m2, ALU.add, [RS], [RS])
                  def ssm_param_pipeline(st):
                      are, aim, ldt, mag, cc, ss, t1, t2, t3 = [sb(st, "sp%d" % j_, [128, 528]) for j_ in range(9)]
                      for d in range(2):
                          lam = lamD[d]; ff = ffD[d]
                          fw.dma("sp", are[:], Din["lamin"][l, d, 0], writes=[RS])
                          fw.dma("sp", aim[:], Din["lamin"][l, d, 1], writes=[RS])
                          fw.dma("sp", ldt[:], Din["lamin"][l, d, 2], writes=[RS])
                          ACT(ldt[:], ldt[:], AF.Exp, [RS], [RS])
                          TT("dve", t1[:], are[:], ldt[:], ALU.mult, [RS], [RS])
                          ACT(mag[:], t1[:], AF.Exp, [RS], [RS])
                          TT("dve", t2[:], aim[:], ldt[:], ALU.mult, [RS], [RS])
                          ACT(ss[:], t2[:], AF.Sin, [RS], [RS], scale=1.0 / 16)
                          ACT(t1[:], t2[:], AF.Sin, [RS], [RS], scale=1.0 / 32)
                          TT("dve", t1[:], t1[:], t1[:], ALU.mult, [RS], [RS])
                          TS("dve", cc[:], t1[:], -2.0, ALU.mult, [RS], [RS], s2=1.0, op1=ALU.add)
                          for _ in range(4):
                              TT("dve", t1[:], cc[:], cc[:], ALU.mult, [RS], [RS])
                              TT("dve", t2[:], ss[:], ss[:], ALU.mult, [RS], [RS])
                              TT("dve", t3[:], cc[:], ss[:], ALU.mult, [RS], [RS])
                              TT("dve", cc[:], t1[:], t2[:], ALU.subtract, [RS], [RS])
                              TS("dve", ss[:], t3[:], 2.0, ALU.mult, [RS], [RS])
                          TT("dve", lam[0][:], mag[:], cc[:], ALU.mult, [RS], [RS])
                          TT("dve", lam[1][:], mag[:], ss[:], ALU.mult, [RS], [RS])
                          TT("dve", t1[:], are[:], are[:], ALU.mult, [RS], [RS])
                          TT("dve", t2[:], aim[:], aim[:], ALU.mult, [RS], [RS])
                          TT("dve", t1[:], t1[:], t2[:], ALU.add, [RS], [RS])
                          OP("dve", lambda e: e.reciprocal(out=t1[:], in_=t1[:]), [RS], [RS])
                          TS("dve", t2[:], lam[0][:], -1.0, ALU.add, [RS], [RS])
                          TT("dve", t3[:], t2[:], are[:], ALU.mult, [RS], [RS])
                          TT("dve", cc[:], lam[1][:], aim[:], ALU.mult, [RS], [RS])
                          TT("dve", t3[:], t3[:], cc[:], ALU.add, [RS], [RS])
                          TT("dve", ff[0][:], t3[:], t1[:], ALU.mult, [RS], [RS])
                          TT("dve", t3[:], lam[1][:], are[:], ALU.mult, [RS], [RS])
                          TT("dve", cc[:], t2[:], aim[:], ALU.mult, [RS], [RS])
                          TT("dve", t3[:], t3[:], cc[:], ALU.subtract, [RS], [RS])
                          TT("dve", ff[1][:], t3[:], t1[:], ALU.mult, [RS], [RS])
                          pm1 = t1[:, 0:256].rearrange("p (g k) -> p g k", g=16); pm2 = t2[:, 0:256].rearrange("p (g k) -> p g k", g=16)
                          OP("dve", lambda e, d=d: e.memset(PowP[0][:, d, :, 0:1], 1.0), [RS], [RS])
                          OP("dve", lambda e, d=d: e.memset(PowP[1][:, d, :, 0:1], 0.0), [RS], [RS])
                          for r in range(2):
                              TS("dve", PowP[r][:, d, :, 1], lam[r][:, 0:16], 1.0, ALU.mult, [RS], [RS])
                              TS("dve", fP[r][:, d, :], ff[r][:, 0:16], 1.0, ALU.mult, [RS], [RS])
                          for w in (1, 2, 4, 8):
                              br_ = PowP[0][:, d, :, w:w + 1].to_broadcast([128, 16, w]); bi_ = PowP[1][:, d, :, w:w + 1].to_broadcast([128, 16, w])
                              cmul(PowP[0][:, d, :, w + 1:2 * w + 1], PowP[1][:, d, :, w + 1:2 * w + 1],
                                   PowP[0][:, d, :, 1:w + 1], PowP[1][:, d, :, 1:w + 1], br_, bi_, pm1[:, :, 0:w], pm2[:, :, 0:w])
                          for r in range(2):
                              TS("dve", LH[r][:, d, :, 0], PowP[r][:, d, :, 16], 1.0, ALU.mult, [RS], [RS])
                          for m in range(6):
                              cmul(LH[0][:, d, :, m + 1], LH[1][:, d, :, m + 1], LH[0][:, d, :, m], LH[1][:, d, :, m],
                                   LH[0][:, d, :, m], LH[1][:, d, :, m], t1[:, 256:272], t2[:, 256:272])
                  with contextlib.ExitStack() as L2:
                      KT = sb(L2, "KT", [128, 2, 2048], BF16); RKT = Res("KT")
                      Vsb = sb(L2, "Vsb", [128, 16, 2, 2, 128], BF16); RV = Res("V")
                      with contextlib.ExitStack() as ph:
                          alloc_ws(ph)
                          uT = sb(ph, "uT", [128, 8, TOK], BF16); RuT = [Res("uT%d" % i) for i in range(NT)]
                          qn = [sb(ph, "qn%d" % j, [128, 768]) for j in range(2)]; Rqn = [Res("qn0"), Res("qn1")]
                          vf = [sb(ph, "vf%d" % j, [128, 128]) for j in range(2)]; Rvf = [Res("vf0"), Res("vf1")]
                          qkb = [sb(ph, "qkb%d" % j, [128, 768], BF16) for j in range(2)]; Rqkb = [Res("qkb0"), Res("qkb1")]
                          kb2 = [sb(ph, "kb2%d" % j, [128, 128], BF16) for j in range(2)]; Rkb2 = [Res("kb20"), Res("kb21")]
                          ssq = sb(ph, "ssq", [128, 12]); Rssq = Res("ssq")
                          g640 = sb(ph, "g640", [128, 768]); rc = sb(ph, "rc", [128, 8, 32]); rsn = sb(ph, "rsn", [128, 8, 32]); Rcst = Res("cst")
                          rt = [sb(ph, "rt%d" % j, [128, 12, 32]) for j in range(2)]; Rrt = [Res("rt0"), Res("rt1")]
                          win = Din["w_in"][l].rearrange("(k p) n -> p k n", p=128)
                          WA_pre = load_w([(lambda t: t[:, 0:4096].rearrange("p (k n) -> p k n", k=8), win[:, :, 0:512])])
                          WB_pre = load_w([(lambda t: t[:, 0:2048].rearrange("p (k n) -> p k n", k=8), win[:, :, 512:768])])
                          fw.dma("sp", g640[:], Din["g640"][l], writes=[Rcst])
                          fw.dma("sp", rc[:], Din["ropec"], writes=[Rcst])
                          fw.dma("sp", rsn[:], Din["ropes"], writes=[Rcst])
                          fw.dma("pool", KT[:, 0, 1024:1536], Din["ckT"][l], writes=[RKT])
                          fw.dma("pool", KT[0:64, 1, 1024:1536], Din["ckT"][l, 64:128, :], writes=[RKT])
                          fw.dma("pool", KT[64:128, 1, 1024:1536], Din["ckT"][l, 0:64, :], writes=[RKT])
                          OP("dve", lambda e: e.memset(Vsb[:].rearrange("p a b c d -> p (a b c d)"), 1.0), [], [RV])
                          cvv = Din["cv"][l].rearrange("(t p) (h d) -> p t h d", p=128, h=2)
                          for lay in range(2):
                              for kvh_ in range(2):
                                  fw.dma("pool", Vsb[:, 8:12, kvh_, lay, lay * 64:(lay + 1) * 64], cvv[:, :, kvh_, :], writes=[RV])
                          build_uT(uT, RuT, 0)
                          if stop == "ut": fw.dead = True
                          DBG("d_uT", uT[:].rearrange("p a b -> p (a b)"), RuT)

                          def norm_rope(i, pa, pr, nh, gap, outb, Routb, slot):
                              w_ = nh * 64
                              sqv = qn[slot][:, 0:w_]
                              ACT(sqv, pa, AF.Square, pr, [Rqn[slot]])
                              OP("dve", lambda e: e.reduce_sum(out=ssq[:, 0:nh], in_=sqv.rearrange("p (h d) -> p h d", d=64), axis=AX.X), [Rqn[slot]], [Rssq])
                              TS("dve", ssq[:, 0:nh], ssq[:, 0:nh], 1.0 / 64, ALU.mult, [Rssq], [Rssq], s2=EPS, op1=ALU.add)
                              ACT(ssq[:, 0:nh], ssq[:, 0:nh], AF.Sqrt, [Rssq], [Rssq])
                              OP("dve", lambda e: e.reciprocal(out=ssq[:, 0:nh], in_=ssq[:, 0:nh]), [Rssq], [Rssq])
                              qv = qn[slot][:, 0:w_]
                              TT("dve", qv.rearrange("p (h d) -> p h d", d=64), pa.rearrange("p (h d) -> p h d", d=64), bl(ssq[:, 0:nh], 64),
                                 ALU.mult, pr + [Rssq], [Rqn[slot]])
                              TT("dve", qv, qv, gap, ALU.mult, [Rqn[slot], Rcst], [Rqn[slot]])
                              if i >= 8:
                                  CP("act", outb, qv, [Rqn[slot]], [Routb])
                              else:
                                  x4 = qv.rearrange("p (h d two) -> p h d two", d=32, two=2)
                                  o4 = outb.rearrange("p (h d two) -> p h d two", d=32, two=2)
                                  x0, x1 = x4[:, :, :, 0], x4[:, :, :, 1]
                                  cc, ss_ = bm(rc[:, i, :], nh), bm(rsn[:, i, :], nh)
                                  r0, r1 = rt[0][:, 0:nh, :], rt[1][:, 0:nh, :]
                                  TT("dve", r0, x0, cc, ALU.mult, [Rqn[slot], Rcst], [Rrt[0]])
                                  TT("pool", r1, x1, ss_, ALU.mult, [Rqn[slot], Rcst], [Rrt[1]])
                                  TT("dve", o4[:, :, :, 0], r0, r1, ALU.subtract, Rrt, [Routb])
                                  TT("dve", r0, x0, ss_, ALU.mult, [Rqn[slot], Rcst], [Rrt[0]])
                                  TT("pool", r1, x1, cc, ALU.mult, [Rqn[slot], Rcst], [Rrt[1]])
                                  TT("dve", o4[:, :, :, 1], r0, r1, ALU.add, Rrt, [Routb])

                          CH[0] = 'a' in os.environ.get('CHP', 'mabfFGMLUS')
                          WA, RWA = WA_pre; WB, RWB = WB_pre
                          WvA = WA[:, 0:4096].rearrange("p (k n) -> p k n", k=8)
                          WvB = WB[:, 0:2048].rearrange("p (k n) -> p k n", k=8)
                          for i in range(NT):
                              b0 = 2 + 2 * (i % 2); sl = i % 2
                              prs = [PB[b0], PB[b0 + 1]]
                              for k in range(8):
                                  MM(bank(b0), uT[:, k, i * 128:(i + 1) * 128], WvA[:, k, :], k == 0, k == 7, [RuT[i], RWA], [PB[b0]])
                              for k in range(8):
                                  MM(bank(b0 + 1)[:, 0:256], uT[:, k, i * 128:(i + 1) * 128], WvB[:, k, :], k == 0, k == 7, [RuT[i], RWB], [PB[b0 + 1]])
                              vps = bank(b0 + 1)[:, 128:256]
                              norm_rope(i, bank(b0, 2)[:, 0:768], prs, 12, g640[:, 0:768], qkb[sl][:, 0:768], Rqkb[sl], sl)
                              for j in range(4):
                                  src = qkb[sl][:, j * 128:(j + 1) * 128]
                                  OP("pe", lambda e: e.transpose(pst[:, j * 128:(j + 1) * 128], src, ident_b[:]), [Rqkb[sl], Rid], [PT])
                              CP("act", QT[:, :, i * 128:(i + 1) * 128], pst[:, 0:512].rearrange("p (j c) -> p j c", j=4), [PT], RQ)
                              kt = i if i < 8 else 12 + (i - 8)
                              for lay in range(2):
                                  CP("act", Vsb[:, kt, :, lay, lay * 64:(lay + 1) * 64], vps.rearrange("p (h d) -> p h d", h=2), [PB[b0 + 1]], [RV])
                              if i >= 8:
                                  p_, t_ = (i - 8) // 2, (i - 8) % 2
                                  CP("act", vf[sl][:], vps, [PB[b0 + 1]], [Rvf[sl]])
                                  fw.dma("sp", Dout["nk"][p_, l, t_ * 128:(t_ + 1) * 128, :], qn[sl][:, 512:640], reads=[Rqn[sl]], writes=[Rdbg])
                                  fw.dma("sp", Dout["nv"][p_, l, t_ * 128:(t_ + 1) * 128, :], vf[sl][:], reads=[Rvf[sl]], writes=[Rdbg])
                              CP("act", kb2[sl][:, 0:64], qkb[sl][:, 576:640], [Rqkb[sl]], [Rkb2[sl]])
                              CP("act", kb2[sl][:, 64:128], qkb[sl][:, 512:576], [Rqkb[sl]], [Rkb2[sl]])
                              OP("pe", lambda e: e.transpose(pst[:, 512:640], qkb[sl][:, 512:640], ident_b[:]), [Rqkb[sl], Rid], [PT])
                              OP("pe", lambda e: e.transpose(pst[:, 640:768], kb2[sl][:, :], ident_b[:]), [Rkb2[sl], Rid], [PT])
                              kc = i * 128 if i < 8 else 1536 + (i - 8) * 128
                              CP("act", KT[:, 0, kc:kc + 128], pst[:, 512:640], [PT], [RKT])
                              CP("act", KT[:, 1, kc:kc + 128], pst[:, 640:768], [PT], [RKT])
                          WF_pre = load_w([(lambda t: t[:, 0:4096].rearrange("p (k n) -> p k n", k=8), win[:, :, 768:1280])])
                          if stop in ("passA", "passB"): fw.dead = True
                          CH[0] = 'f' in os.environ.get('CHP', 'mabfFGMLUS')
                          for wb in range(2):
                              W, RW = WF_pre
                              if wb == 0:
                                  WF_pre = load_w([(lambda t: t[:, 0:4096].rearrange("p (k n) -> p k n", k=8), win[:, :, 1280:1792])])
                              Wv = W[:, 0:4096].rearrange("p (k n) -> p k n", k=8)
                              for m4 in range(4):
                                  m = wb * 4 + m4
                                  for tb, (c0, cn) in enumerate(BLKS):
                                      bk = (0, 1, 6)[(m * 3 + tb) % 3]
                                      for k in range(8):
                                          MM(bank(bk), Wv[:, k, m4 * 128:(m4 + 1) * 128], uT[:, k, c0:c0 + cn], k == 0, k == 7,
                                             RuT[c0 // 128:(c0 + cn) // 128] + [RW], [PB[bk]])
                                      CP("act", sfT[:, m, c0:c0 + cn], bank(bk), [PB[bk]], [RsfT[m]])
                          DBG("d_sfT", sfT[:].rearrange("p a b -> p (a b)"), RsfT)
                          fw.barrier()
                      if stop == "inproj": fw.dead = True
                      DBG("d_QT", QT[:].rearrange("p a b -> p (a b)"), RQ)
                      DBG("d_KT", KT[:].rearrange("p a b -> p (a b)"), [RKT])
                      CH[0] = 't' in os.environ.get('CHP', 'mabfFGMLUS')
                      with contextlib.ExitStack() as ph:
                          fw.rec = []
                          ssm_param_pipeline(ph)
                          pipe_items = fw.rec if fw.rec is not None else []; fw.rec = None
                          per_unit = (len(pipe_items) + 23) // 24
                          Eb = [sb(ph, "Eb%d" % j, [128, 512], BF16) for j in range(3)]; REb = [Res("Eb%d" % j) for j in range(3)]
                          rsb = [sb(ph, "rsb%d" % j, [128, 512]) for j in range(2)]; Rrsb = [Res("rsb0"), Res("rsb1")]
                          cnt = 0; cnt2 = 0
                          for si, (t0, Ls) in enumerate(SEQS):
                              if si == 0:
                                  qblks = [(0, 512), (512, 512)]; nkt = 12; kt0 = 0; kc0 = 0
                              else:
                                  qblks = [(t0, 256)]; nkt = 2; kt0 = 12 + 2 * (si - 1); kc0 = 1536 + 256 * (si - 1)
                              for h in range(8):
                                  j = h // 2; hh = h % 2; kvh = h // 4; var = 0 if kvh == hh else 1
                                  pv, sm = (slice(0, 64), slice(64, 128)) if hh == 0 else (slice(64, 128), slice(0, 64))
                                  for (q0, qn) in qblks:
                                      pvb = 3 + (cnt2 % 2); cnt2 += 1
                                      def S_mm(kt, sbk):
                                          MM(bank(sbk)[:, 0:qn], KT[hh * 64:(hh + 1) * 64, var, kc0 + kt * 128:kc0 + (kt + 1) * 128],
                                             QT[hh * 64:(hh + 1) * 64, j, q0:q0 + qn], True, True, [RKT, RQ[j]], [PB[sbk]], chain=False)
                                      slots = [(cnt + i_) % 3 for i_ in range(nkt)]; cnt += nkt
                                      S_mm(0, slots[0])
                                      for kt in range(nkt):
                                          sbk = slots[kt]; eb = sbk
                                          if kt + 1 < nkt:
                                              S_mm(kt + 1, slots[kt + 1])
                                          ACT(Eb[eb][:, 0:qn], bank(sbk)[:, 0:qn], AF.Exp, [PB[sbk]], [REb[eb]], scale=0.125)
                                          MM(bank(pvb)[:, 0:qn], Vsb[:, kt0 + kt, kvh, hh, :], Eb[eb][:, 0:qn], kt == 0, kt == nkt - 1,
                                             [RV, REb[eb]], [PB[pvb]], chain=False)
                                      rb = cnt2 % 2
                                      ACT(rsb[rb][sm, 0:qn], bank(pvb)[sm, 0:qn], AF.Ln, [PB[pvb]], [Rrsb[rb]])
                                      ACT(rsb[rb][sm, 0:qn], rsb[rb][sm, 0:qn], AF.Exp, [Rrsb[rb]], [Rrsb[rb]], scale=-1.0)
                                      TT("dve", attnT[pv, j, q0:q0 + qn], bank(pvb)[pv, 0:qn], rsb[rb][sm, 0:qn], ALU.mult,
                                         [PB[pvb], Rrsb[rb]], [RattnT[j]])
                                      fw.replay(pipe_items[:per_unit]); del pipe_items[:per_unit]
                          fw.replay(pipe_items); del pipe_items[:]
                          fw.barrier()
                  DBG("d_attnT", attnT[:].rearrange("p a b -> p (a b)"), RattnT)
                  CH[0] = 'F' in os.environ.get('CHP', 'mabfFGMLUS')
                  with contextlib.ExitStack() as L3:
                    with contextlib.ExitStack() as ph:
                        cs128 = sb(ph, "cs128", [128, 256], BF16); cl = sb(ph, "cl", [128, 8, 1024], BF16); sl_ = sb(ph, "sl", [128, 8, 1024], BF16)
                        clp = sb(ph, "clp", [128, 2, 256], BF16); slp = sb(ph, "slp", [128, 2, 256], BF16); Rft = Res("ftab")
                        Pfm = sb(ph, "Pfm", [128, NT, 1024], BF16); RPfm = [Res("Pfm%d" % i) for i in range(NT)]
                        fw.dma("pool", cs128[:], Din["cs128"], writes=[Rft])
                        for k in range(8):
                            fw.dma("pool", cl[:, k, :], Din["cl1024"][:, k, :], writes=[Rft])
                            fw.dma("pool", sl_[:, k, :], Din["sl1024"][:, k, :], writes=[Rft])
                        fw.dma("pool", clp[:], Din["cl256"], writes=[Rft])
                        fw.dma("pool", slp[:], Din["sl256"], writes=[Rft])
                        for i in range(NT):
                            b0 = 2 + 2 * (i % 2)
                            for g in range(4):
                                MM(bank(b0 + g // 2)[:, (g % 2) * 256:(g % 2 + 1) * 256], sfT[:, 4 + g, i * 128:(i + 1) * 128], cs128[:], True, True,
                                   [RsfT[4 + g], Rft], [PB[b0 + g // 2]])
                            CP("act", Pfm[:, i, :], bank(b0, 2), [PB[b0], PB[b0 + 1]], [RPfm[i]])
                        cnt = 0
                        for si, (t0, Ls) in enumerate(SEQS):
                            ntl = Ls // 128; tb0 = t0 // 128
                            tc_, ts_ = (cl, sl_) if si == 0 else (clp, slp)
                            for g in range(4):
                                for lb in range(0, Ls, 512):
                                    n = min(512, Ls - lb)
                                    bk = (0, 1, 6)[cnt % 3]; cnt += 1
                                    for k in range(ntl):
                                        MM(bank(bk)[:, 0:n], Pfm[:, tb0 + k, g * 256:g * 256 + 128], tc_[:, k, lb:lb + n], k == 0, False,
                                           [RPfm[tb0 + k], Rft], [PB[bk]])
                                        MM(bank(bk)[:, 0:n], Pfm[:, tb0 + k, g * 256 + 128:g * 256 + 256], ts_[:, k, lb:lb + n], False, k == ntl - 1,
                                           [RPfm[tb0 + k], Rft], [PB[bk]])
                                    CP("act", fourT[:, g, t0 + lb:t0 + lb + n], bank(bk)[:, 0:n], [PB[bk]], [RfourT[g]])
                        fw.barrier()
                    DBG("d_fourT", fourT[:].rearrange("p a b -> p (a b)"), RfourT)
                    if stop == "four": fw.dead = True
                    CH[0] = 'S' in os.environ.get('CHP', 'mabfFGMLUS')
                    with contextlib.ExitStack() as S:
                        Hb = sb(S, "Hb", [128, 2, 2, 16, NS], BF16)
                        h0 = sb(S, "h0", [128, 2, 2, 16]); stg = sb(S, "stg", [128, 16, 2])
                        RHb = Res("Hb"); Rstg = Res("stg")
                        fw.dma("sp", h0[:], Din["h0"][:, l], writes=[RS])
                        for d in range(2):
                            with contextlib.ExitStack() as SD:
                                lam = lamD[d]; ff = ffD[d]
                                A = [sb(SD, "A%d" % r, [128, 16, NS]) for r in range(2)]
                                for r in range(2):
                                    OP("dve", lambda e: e.memset(A[r][:].rearrange("p a b -> p (a b)"), 0.0), [RS], [RS])
                                with contextlib.ExitStack() as SW:
                                    W = [sb(SW, "W%d" % r, [128, 16, 128]) for r in range(2)]
                                    WT2 = [[sb(SW, "WT%d_%d" % (z, r), [128, 16, 128], BF16) for r in range(2)] for z in range(2)]
                                    wm = [sb(SW, "wm%d" % r, [128, 8, 128]) for r in range(3)]
                                    BT = [sb(SW, "BT%d" % r, [128, 128]) for r in range(2)]
                                    Lw = [sb(SW, "Lw%d" % r, [128, 128]) for r in range(2)]
                                    RWT2 = [Res("WTa"), Res("WTb")]
                                    def build_W(q):
                                        fs = slice(16 + q * 128, 16 + (q + 1) * 128)
                                        for r in range(2):
                                            fw.dma("sp", BT[r][:], Din["BbdT"][l, d, r, :, q, :], writes=[RS])
                                            ACT(Lw[r][:], lam[r][:, fs], AF.Identity, [RS], [RS])
                                        cmul(W[0][:, 0, :], W[1][:, 0, :], ff[0][:, fs], ff[1][:, fs], BT[0][:], BT[1][:], wm[0][:, 0, :], wm[1][:, 0, :])
                                        for w in (1, 2, 4, 8):
                                            cmul(W[0][:, w:2 * w, :], W[1][:, w:2 * w, :], bm(Lw[0][:], w), bm(Lw[1][:], w),
                                                 W[0][:, 0:w, :], W[1][:, 0:w, :], wm[0][:, 0:w, :], wm[1][:, 0:w, :])
                                            if w < 8:
                                                TT("dve", wm[0][:, 0, :], Lw[0][:], Lw[0][:], ALU.mult, [RS], [RS])
                                                TT("dve", wm[1][:, 0, :], Lw[1][:], Lw[1][:], ALU.mult, [RS], [RS])
                                                TT("dve", wm[2][:, 0, :], Lw[0][:], Lw[1][:], ALU.mult, [RS], [RS])
                                                TT("dve", Lw[0][:], wm[0][:, 0, :], wm[1][:, 0, :], ALU.subtract, [RS], [RS])
                                                TS("dve", Lw[1][:], wm[2][:, 0, :], 2.0, ALU.mult, [RS], [RS])
                                        for r in range(2):
                                            ACT(WT2[q % 2][r][:].rearrange("p a b -> p (a b)"), W[r][:].rearrange("p a b -> p (a b)"), AF.Identity, [RS], [RWT2[q % 2]])
                                    def run_S(q):
                                        WTq = WT2[q % 2]; RWTq = RWT2[q % 2]
                                        Sps = bank(0, 2).rearrange("p (b r c) -> p b r c", b=4, r=2)
                                        for b in range(4):
                                            sv = sfT[32 * b:32 * b + 32, q, :].rearrange("p (c j) -> p c j", j=16)
                                            for r in range(2):
                                                for k in range(16):
                                                    j = 15 - k if d == 0 else k
                                                    MM(Sps[:, b, r, 0:96], WTq[r][32 * b:32 * b + 32, k, :], sv[:, :, j], k == 0, k == 15,
                                                       [RWTq, RsfT[q]], [PB[0], PB[1]], tp=(32 * b, 0))
                                        for r in range(2):
                                            for s_i in range(3):
                                                n = SCH[s_i]; c0 = (0, 64, 80)[s_i]; lo = SOFF[s_i] + (1 if d == 0 else 0)
                                                ACT(A[r][:, 4 * q:4 * q + 4, lo:lo + n], Sps[:, :, r, c0:c0 + n], AF.Identity, [PB[0], PB[1]], [RS])
                                    build_W(0)
                                    for q in range(4):
                                        if q + 1 < 4:
                                            build_W(q + 1)
                                        run_S(q)
                                RSp = Res("ssm_pool"); NPD = 10
                                for r in range(2):
                                    ACT(A[r][:, :, 0 if d == 0 else 64], h0[:, d, r, :], AF.Identity, [RS], [RS, RSp])
                                with contextlib.ExitStack() as SC:
                                    sm = [sb(SC, "sm%d" % r, [128, 16, NS]) for r in range(3)]
                                    for m in range(7):
                                        dd = 1 << m
                                        groups = []
                                        if dd < 65:
                                            groups.append((lambda t_, ps_, lo, hi: t_[:, ps_, lo:hi], 65, None))
                                        if dd < 17:
                                            groups.append((lambda t_, ps_, lo, hi: t_[:, ps_, 65:99].rearrange("p g (s c) -> p g s c", s=2)[:, :, :, lo:hi], 17, 2))
                                        for view, n, ns in groups:
                                            cntn = n - dd
                                            (dlo, dhi, slo, shi) = (dd, n, 0, cntn) if d == 0 else (0, cntn, dd, n)
                                            for eng_, ps_, Rr in (("dve", slice(0, NPD), RS), ("pool", slice(NPD, 16), RSp)):
                                                npz = ps_.stop - ps_.start
                                                if ns is None:
                                                    Lr = bl(LH[0][:, d, ps_, m], cntn); Li = bl(LH[1][:, d, ps_, m], cntn)
                                                    tv = lambda t_: t_[:, ps_, 0:cntn]
                                                else:
                                                    Lr = LH[0][:, d, ps_, m].unsqueeze(2).unsqueeze(3).to_broadcast([128, npz, 2, cntn])
                                                    Li = LH[1][:, d, ps_, m].unsqueeze(2).unsqueeze(3).to_broadcast([128, npz, 2, cntn])
                                                    tv = lambda t_: t_[:, ps_, 0:2 * cntn].rearrange("p g (s c) -> p g s c", s=2)
                                                sr, si = view(A[0], ps_, slo, shi), view(A[1], ps_, slo, shi)
                                                dr, di = view(A[0], ps_, dlo, dhi), view(A[1], ps_, dlo, dhi)
                                                m1, m2, m3 = tv(sm[0]), tv(sm[1]), tv(sm[2])
                                                TT(eng_, m1, sr, Lr, ALU.mult, [Rr], [Rr]); TT(eng_, m2, si, Li, ALU.mult, [Rr], [Rr])
                                                TT(eng_, m1, m1, m2, ALU.subtract, [Rr], [Rr])
                                                TT(eng_, m2, si, Lr, ALU.mult, [Rr], [Rr]); TT(eng_, m3, sr, Li, ALU.mult, [Rr], [Rr])
                                                TT(eng_, m2, m2, m3, ALU.add, [Rr], [Rr])
                                                TT(eng_, dr, dr, m1, ALU.add, [Rr], [Rr]); TT(eng_, di, di, m2, ALU.add, [Rr], [Rr])
                                for r in range(2):
                                    ACT(Hb[:, d, r, :, :], A[r][:], AF.Identity, [RS, RSp], [RHb])
                                for s_i in (1, 2):
                                    idx = SOFF[s_i] + (16 if d == 0 else 0)
                                    for r in range(2):
                                        ACT(stg[:, :, r], A[r][:, :, idx], AF.Identity, [RS, RSp], [Rstg])
                                    fw.dma("sp", Dout["nst"][s_i - 1, l, d].rearrange("(pair g2) p r -> (g2 p) pair r", g2=2), stg[:], reads=[Rstg], writes=[Rdbg])
                                DBG("d_A%d" % d, A[0][:].rearrange("p a b -> p (a b)"), [RS, RSp])
                                fw.barrier()
                        with contextlib.ExitStack() as SQ:
                            CL2 = [[[sb(SQ, "CL%d%d%d" % (z, d, r), [128, 4, 17, 32], BF16) for r in range(2)] for d in range(2)] for z in range(2)]
                            Kblk = [sb(SQ, "Kblk%d" % d, [128, 16, 128], BF16) for d in range(2)]
                            Bb2 = [[[sb(SQ, "Bb%d%d%d" % (z, d, r), [128, 4, 32], BF16) for r in range(2)] for d in range(2)] for z in range(2)]
                            Dbl = sb(SQ, "Dbl", [128, 4, 128], BF16)
                            cm = [sb(SQ, "cm%d" % j, [128, 2, 17, 32]) for j in range(2)]
                            Cl = [sb(SQ, "Cl%d" % r, [128, 4, 32]) for r in range(2)]
                            Bl = [sb(SQ, "Bl%d" % r, [128, 4, 32]) for r in range(2)]
                            bt = [sb(SQ, "bt%d" % r, [128, 4, 32]) for r in range(2)]
                            gt = [sb(SQ, "gt%d" % r, [128, 512]) for r in range(2)]; Rgt = [Res("gt0"), Res("gt1")]
                            RCL2 = [Res("CLa"), Res("CLb")]; RK = Res("Kblk"); RDbl = Res("Dbl")
                            ydb = sb(SQ, "ydb", [128, 512]); Rydb = Res("ydb")
                            fw.dma("pool", Dbl[:], Din["Dblk"][l], writes=[RDbl])

                            def build_CL(q):
                                z = q % 2; CL = CL2[z]; Bb = Bb2[z]; RCL = RCL2[z]
                                ps4 = slice(4 * q, 4 * q + 4)
                                for d in range(2):
                                    for r in range(2):
                                        fw.dma("sp", Cl[r][:], Din["CbdP"][l, d, r, :, ps4, :], writes=[RS])
                                        fw.dma("sp", Bl[r][:], Din["BbdP"][l, d, r, :, ps4, :], writes=[RS])
                                    for hp in range(2):
                                        pp = slice(2 * hp, 2 * hp + 2); pq = slice(4 * q + 2 * hp, 4 * q + 2 * hp + 2)
                                        Pr = PowP[0][:, d, pq, :].unsqueeze(3).to_broadcast([128, 2, 17, 32])
                                        Pi = PowP[1][:, d, pq, :].unsqueeze(3).to_broadcast([128, 2, 17, 32])
                                        Cr = Cl[0][:, pp, :].unsqueeze(2).to_broadcast([128, 2, 17, 32])
                                        Ci = Cl[1][:, pp, :].unsqueeze(2).to_broadcast([128, 2, 17, 32])
                                        TT("dve", cm[0][:], Pr, Cr, ALU.mult, [RS], [RS]); TT("dve", cm[1][:], Pi, Ci, ALU.mult, [RS], [RS])
                                        TT("dve", CL[d][0][:, pp, :, :], cm[0][:], cm[1][:], ALU.subtract, [RS], [RCL])
                                        TT("dve", cm[0][:], Pr, Ci, ALU.mult, [RS], [RS]); TT("dve", cm[1][:], Pi, Cr, ALU.mult, [RS], [RS])
                                        TT("dve", cm[0][:], cm[0][:], cm[1][:], ALU.add, [RS], [RS])
                                        TS("dve", CL[d][1][:, pp, :, :], cm[0][:], -1.0, ALU.mult, [RS], [RCL])
                                    fr = bl(fP[0][:, d, ps4], 32); fi = bl(fP[1][:, d, ps4], 32)
                                    TT("dve", bt[0][:], fr, Bl[0][:], ALU.mult, [RS], [RS]); TT("dve", bt[1][:], fi, Bl[1][:], ALU.mult, [RS], [RS])
                                    TT("dve", Bb[d][0][:], bt[0][:], bt[1][:], ALU.subtract, [RS], [RCL])
                                    TT("dve", bt[0][:], fr, Bl[1][:], ALU.mult, [RS], [RS]); TT("dve", bt[1][:], fi, Bl[0][:], ALU.mult, [RS], [RS])
                                    TT("dve", Bb[d][1][:], bt[0][:], bt[1][:], ALU.add, [RS], [RCL])

                            def build_K(q):
                                z = q % 2; CL = CL2[z]; Bb = Bb2[z]; RCL = RCL2[z]
                                for d in range(2):
                                    kb = d
                                    for b in range(4):
                                        MM(bank(kb)[32 * b:32 * b + 32, :], Bb[d][0][:, b, :], CL[d][0][:, b, 0:16, :].rearrange("p t n -> p (t n)"), True, False,
                                           [RCL], [PB[kb]], tp=(0, 32 * b))
                                        MM(bank(kb)[32 * b:32 * b + 32, :], Bb[d][1][:, b, :], CL[d][1][:, b, 0:16, :].rearrange("p t n -> p (t n)"), False, True,
                                           [RCL], [PB[kb]], tp=(0, 32 * b))
                                    for cb in range(4):
                                        ACT(Kblk[d][:, :, 32 * cb:32 * cb + 32], bank(kb).rearrange("p (t n) -> p t n", n=32), AF.Identity,
                                            [PB[kb], Rmask], [RK], scale=maskP[:, cb:cb + 1])

                            def run_y(q):
                                z = q % 2; CL = CL2[z]; RCL = RCL2[z]
                                for tb, (c0, cn) in enumerate(BLKS):
                                    yb = 3 + tb
                                    Yb = bank(yb).rearrange("p (c j) -> p c j", j=16)
                                    sblk = sfT[:, q, c0:c0 + cn].rearrange("p (c j) -> p c j", j=16)
                                    MM(bank(yb), Dbl[:, q, :], sfT[:, q, c0:c0 + cn], True, False, [RDbl, RsfT[q]], [PB[yb]])
                                    for d in range(2):
                                        for t in range(16):
                                            if d == 0:
                                                o_, r_ = Yb[:, :, t:16], sblk[:, :, 0:16 - t]
                                            else:
                                                o_, r_ = Yb[:, :, 0:16 - t], sblk[:, :, t:16]
                                            MM(o_, Kblk[d][:, t, :], r_, False, False, [RK, RsfT[q]], [PB[yb]])
                                    GC = os.environ.get('GC', '1') == '1'
                                    for d in range(2):
                                        for j in range(16):
                                            tt = j + 1 if d == 0 else 16 - j
                                            sh = 0 if d == 0 else 1
                                            for r in range(2):
                                                for b in range(4):
                                                    gfirst = (b == 0 and r == 0 and j == 0 and d == 0)
                                                    if tb < 2:
                                                        rhs = Hb[:, d, r, 4 * q + b, 32 * tb + sh:32 * tb + sh + 32]
                                                        o_ = Yb[32 * b:32 * b + 32, :, j]
                                                    else:
                                                        rhs = Hb[:, d, r, 4 * q + b, 65:99].rearrange("p (s c) -> p s c", s=2)[:, :, sh:sh + 16]
                                                        o_ = bank(yb).rearrange("p (s c j) -> p s c j", s=2, j=16)[32 * b:32 * b + 32, :, :, j]
                                                    last = (d == 1 and b == 3 and j == 15 and r == 1)
                                                    MM(o_, CL[d][r][:, b, tt, :], rhs, False, last, [RCL, RHb], [PB[yb]], tp=(0, 32 * b),
                                                       chain=(("gfirst" if gfirst else "gnext") if GC else False))
                                    if ("d_y%d_%d" % (q, tb)) in Ddbg:
                                        ACT(ydb[:], bank(yb), AF.Identity, [PB[yb]], [Rydb])
                                        DBG("d_y%d_%d" % (q, tb), ydb[:], [Rydb])
                                    g_ = tb % 2
                                    ACT(gt[g_][:], bank(yb), AF.Square, [PB[yb]], [Rgt[g_]])
                                    TS("dve", gt[g_][:], gt[g_][:], 0.044715, ALU.mult, [Rgt[g_]], [Rgt[g_]], s2=1.0, op1=ALU.add)
                                    TT("dve", gt[g_][:], gt[g_][:], bank(yb), ALU.mult, [Rgt[g_], PB[yb]], [Rgt[g_]])
                                    ACT(gt[g_][:], gt[g_][:], AF.Sigmoid, [Rgt[g_]], [Rgt[g_]], scale=1.5957691216)
                                    TT("dve", sfT[:, q, c0:c0 + cn], gt[g_][:], bank(yb), ALU.mult, [Rgt[g_], PB[yb]], [RsfT[q]])
                            build_CL(0); build_K(0)
                            for q in range(4):
                                if q + 1 < 4:
                                    build_CL(q + 1)
                                run_y(q)
                                if q + 1 < 4:
                                    build_K(q + 1)
                            fw.barrier()
                    Lp.close()
                    ssmT = sb(L3, "ssmT", [128, 4, TOK], BF16); RssmT = [Res("ssmT%d" % j) for j in range(4)]
                    CH[0] = 'G' in os.environ.get('CHP', 'mabfFGMLUS')
                    with contextlib.ExitStack() as ph:
                        alloc_ws(ph)
                        bgl = sb(ph, "bgl", [128, 4]); Rbgl = Res("bgl"); sg = [sb(ph, "sg%d" % j, [128, 512]) for j in range(2)]; Rsg = [Res("sg0"), Res("sg1")]
                        fw.dma("sp", bgl[:], Din["bglu"][:, l, :], writes=[Rbgl])
                        Wg, RWg = load_w([(lambda t: t[:, 0:2048].rearrange("p (k n) -> p k n", k=4), Din["w_glu"][l].rearrange("(k p) n -> p k n", p=128))])
                        Wgv = Wg[:, 0:2048].rearrange("p (k n) -> p k n", k=4)
                        cnt = 0
                        for m in range(4):
                            for tb, (c0, cn) in enumerate(BLKS):
                                bk = (0, 1, 2)[cnt % 3]; sj = cnt % 2; cnt += 1
                                for k in range(4):
                                    MM(bank(bk), Wgv[:, k, m * 128:(m + 1) * 128], sfT[:, k, c0:c0 + cn], k == 0, k == 3, [RWg, RsfT[k]], [PB[bk]])
                                ACT(sg[sj][:], bank(bk), AF.Sigmoid, [PB[bk], Rbgl], [Rsg[sj]], bias=bgl[:, m:m + 1])
                                TT("dve", ssmT[:, m, c0:c0 + cn], sg[sj][:], sfT[:, m, c0:c0 + cn], ALU.mult, [Rsg[sj], RsfT[m]], [RssmT[m]])
                        fw.barrier()
                    DBG("d_gy", sfT[:, 0:4, :].rearrange("p a b -> p (a b)"), RsfT[0:4])
                    DBG("d_ssmT", ssmT[:].rearrange("p a b -> p (a b)"), RssmT)
                    if stop == "ssm": fw.dead = True
                    CH[0] = 'M' in os.environ.get('CHP', 'mabfFGMLUS')
                    with contextlib.ExitStack() as M:
                        mergedT = sb(M, "mergedT", [128, 8, TOK], BF16); Rmg = [Res("mg%d" % j) for j in range(8)]
                        with contextlib.ExitStack() as ph:
                            alloc_ws(ph)
                            uT = sb(ph, "uT", [128, 8, TOK], BF16); RuT = [Res("uT%d" % i) for i in range(NT)]
                            gs = [sb(ph, "gs%d" % j, [128, 512]) for j in range(3)]; Rgs = [Res("gs%d" % j) for j in range(3)]
                            ac = [sb(ph, "ac%d" % j, [128, 512]) for j in range(2)]; Rac = [Res("ac0"), Res("ac1")]
                            build_uT(uT, RuT, 0)
                            win = Din["w_in"][l].rearrange("(k p) n -> p k n", p=128)
                            wbr = [Din[nm][l].rearrange("(k p) n -> p k n", p=128) for nm in ("w_br_attn", "w_br_ssm", "w_br_four")]
                            srcs = [(attnT, RattnT), (ssmT, RssmT), (fourT, RfourT)]
                            Wg2 = [sb(ph, "Wg2_%d" % z, [128, 3072], BF16) for z in range(2)]; RWg2 = [Res("Wg2a"), Res("Wg2b")]
                            Wb2 = [sb(ph, "Wb2_%d" % z, [128, 1536], BF16) for z in range(2)]; RWb2 = [Res("Wb2a"), Res("Wb2b")]
                            def load_c(c):
                                z = c % 2
                                for g in range(3):
                                    fw.dma("pool", Wg2[z][:, g * 1024:(g + 1) * 1024].rearrange("p (k n) -> p k n", k=8),
                                           win[:, :, 1792 + g * 1024 + c * 128:1792 + g * 1024 + (c + 1) * 128], writes=[RWg2[z]])
                                for x in range(3):
                                    fw.dma("pool", Wb2[z][:, x * 512:(x + 1) * 512].rearrange("p (k n) -> p k n", k=4),
                                           wbr[x][:, :, c * 128:(c + 1) * 128], writes=[RWb2[z]])
                            load_c(0)
                            for c in range(8):
                                if c + 1 < 8:
                                    load_c(c + 1)
                                Wg, RWg, Wb, RWb = Wg2[c % 2], RWg2[c % 2], Wb2[c % 2], RWb2[c % 2]
                                for tb, (c0, cn) in enumerate(BLKS):
                                    for g in range(3):
                                        Wgv = Wg[:, g * 1024:(g + 1) * 1024].rearrange("p (k n) -> p k n", k=8)
                                        for k in range(8):
                                            MM(bank(g), Wgv[:, k, :], uT[:, k, c0:c0 + cn], k == 0, k == 7, RuT[c0 // 128:(c0 + cn) // 128] + [RWg], [PB[g]])
                                        ACT(gs[g][:], bank(g), AF.Sigmoid, [PB[g]], [Rgs[g]])
                                    for x in range(3):
                                        Wbv = Wb[:, x * 512:(x + 1) * 512].rearrange("p (k n) -> p k n", k=4)
                                        st_, Rst = srcs[x]
                                        for k in range(4):
                                            MM(bank(3 + x), Wbv[:, k, :], st_[:, k, c0:c0 + cn], k == 0, k == 3, [Rst[k], RWb], [PB[3 + x]])
                                    TT("dve", ac[0][:], gs[0][:], bank(3), ALU.mult, [Rgs[0], PB[3]], [Rac[0]])
                                    TT("dve", ac[1][:], gs[1][:], bank(4), ALU.mult, [Rgs[1], PB[4]], [Rac[1]])
                                    TT("dve", ac[0][:], ac[0][:], ac[1][:], ALU.add, Rac, [Rac[0]])
                                    TT("dve", ac[1][:], gs[2][:], bank(5), ALU.mult, [Rgs[2], PB[5]], [Rac[1]])
                                    TT("dve", mergedT[:, c, c0:c0 + cn], ac[0][:], ac[1][:], ALU.add, Rac, [Rmg[c]])
                            fw.barrier()
                        DBG("d_mergedT", mergedT[:].rearrange("p a b -> p (a b)"), Rmg)
                        if stop == "merge": fw.dead = True
                        layer_norm_residual(l, 0, 8, lambda i, k: mergedT[:, k, i * 128:(i + 1) * 128], lambda i, k: [Rmg[k]],
                                            Din["w_out"][l].rearrange("(k p) n -> p k n", p=128))
              if "d_x1" in Ddbg:
                  for i in range(NT):
                      fw.dma("sp", Ddbg["d_x1"][i * 128:(i + 1) * 128, :], X[:, i, :], reads=[RX[i]], writes=[Rdbg])
              if stop == "ln1": fw.dead = True
              CH[0] = 'U' in os.environ.get('CHP', 'mabfFGMLUS')
              with contextlib.ExitStack() as Fs:
                  actT = sb(Fs, "actT", [128, 22, TOK], BF16); Ract = [Res("act%d" % j) for j in range(22)]
                  with contextlib.ExitStack() as ph:
                      alloc_ws(ph)
                      uT = sb(ph, "uT", [128, 8, TOK], BF16); RuT = [Res("uT%d" % i) for i in range(NT)]
                      cvp = sb(ph, "cvp", [128, 44, 4]); Rcv = Res("cvp")
                      hc = [sb(ph, "hc%d" % j, [128, TOK]) for j in range(2)]; Rhc = [Res("hc0"), Res("hc1")]
                      fw.dma("sp", cvp[:], Din["convp"][:, l], writes=[Rcv])
                      wup = Din["w_up"][l].rearrange("(k p) n -> p k n", p=128)
                      def load_up(bi):
                          return load_w([(lambda t: t[:, 0:4096].rearrange("p (k n) -> p k n", k=8), wup[:, :, bi * 512:(bi + 1) * 512])])
                      nxt_up = load_up(0)
                      build_uT(uT, RuT, 1)
                      for ch in range(44):
                          if ch % 4 == 0:
                              W, RW = nxt_up
                              if ch // 4 + 1 < 11:
                                  nxt_up = load_up(ch // 4 + 1)
                              Wv = W[:, 0:4096].rearrange("p (k n) -> p k n", k=8)
                          cw = ch % 4; pb0 = 3 * (ch % 2); hj = ch % 2
                          prs = [PB[pb0], PB[pb0 + 1], PB[pb0 + 2]]
                          for tb, (c0, cn) in enumerate(BLKS):
                              for k in range(8):
                                  MM(bank(pb0 + tb), Wv[:, k, cw * 128:(cw + 1) * 128], uT[:, k, c0:c0 + cn], k == 0, k == 7,
                                     RuT[c0 // 128:(c0 + cn) // 128] + [RW], [prs[tb]])
                          hp = bank(pb0, 3); h_ = hc[hj]
                          ACT(h_[:], hp, AF.Identity, prs + [Rcv], [Rhc[hj]], scale=cvp[:, ch, 1:2], bias=cvp[:, ch, 3:4])
                          def stt(o_, i0, sc, i1):
                              OP("dve", lambda e: e.scalar_tensor_tensor(out=o_, in0=i0, scalar=sc, in1=i1, op0=ALU.mult, op1=ALU.add),
                                 prs + [Rcv, Rhc[hj]], [Rhc[hj]])
                          stt(h_[:, 1:1024], hp[:, 0:1023], cvp[:, ch, 0:1], h_[:, 1:1024])
                          stt(h_[:, 0:1023], hp[:, 1:1024], cvp[:, ch, 2:3], h_[:, 0:1023])
                          h3 = h_[:, 1024:1536].rearrange("p (s c) -> p s c", s=2); p3 = hp[:, 1024:1536].rearrange("p (s c) -> p s c", s=2)
                          stt(h3[:, :, 1:256], p3[:, :, 0:255], cvp[:, ch, 0:1], h3[:, :, 1:256])
                          stt(h3[:, :, 0:255], p3[:, :, 1:256], cvp[:, ch, 2:3], h3[:, :, 0:255])
                          if ch < 22:
                              ACT(actT[:, ch, :], h_[:], AF.Silu, [Rhc[hj]], [Ract[ch]])
                          else:
                              TT("dve", actT[:, ch - 22, :], actT[:, ch - 22, :], h_[:], ALU.mult, [Ract[ch - 22], Rhc[hj]], [Ract[ch - 22]])
                      fw.barrier()
                  DBG("d_actT", actT[:].rearrange("p a b -> p (a b)"), Ract)
                  if stop == "ffn": fw.dead = True
                  def y_out(i):
                      fw.dma("sp", Dout["y"][i * 128:(i + 1) * 128, :], X[:, i, :], reads=[RX[i]], writes=[Rdbg])
                  layer_norm_residual(l, 1, 22, lambda i, k: actT[:, k, i * 128:(i + 1) * 128], lambda i, k: [Ract[k]],
                                      Din["w_down"][l].rearrange("(k p) n -> p k n", p=128),
                                      out_tile=(y_out if l == n_layers - 1 else None))
        layers()
        if fw.dead:
            fw.dead = False
            for i in range(NT):
                fw.dma("sp", Dout["y"][i * 128:(i + 1) * 128, :], X[:, i, :], reads=[RX[i]], writes=[Rdbg])
        fw.finish()
    return nc


def kernel(**inputs):
    inp = {k: np.asarray(v) for k, v in inputs.items()}
    S = prep_shared(inp)
    nc = build()
    in_maps = []
    for cid in range(8):
        m = dict(S); m.update(prep_core(inp, cid))
        in_maps.append(m)
    res = run_bass_kernel_spmd(nc, in_maps, core_ids=list(range(8)))
    y_p = np.zeros((16, 256, 1024), np.float32); y_s = np.zeros((8, 1024, 1024), np.float32)
    nk = np.zeros((16, 2, 256, 2, 64), np.float32); nv = np.zeros((16, 2, 256, 2, 64), np.float32)
    nst = np.zeros((16, 2, 2, 32, 64, 2), np.float32)
    for cid in range(8):
        r = res.results[cid]
        y_s[cid] = r["y"][0:1024]
        y_p[2 * cid:2 * cid + 2] = r["y"][1024:1536].reshape(2, 256, 1024)
        nk[2 * cid:2 * cid + 2] = r["nk"].reshape(2, 2, 256, 2, 64)
        nv[2 * cid:2 * cid + 2] = r["nv"].reshape(2, 2, 256, 2, 64)
        nst[2 * cid:2 * cid + 2] = r["nst"]
    return (y_p, y_s, nk, nv, nst)
```

```python
import contextlib, os
SK = os.environ.get("SK", "")
import numpy as np
import concourse.bass as bass
import concourse.mybir as mybir
from concourse.bass_utils import run_bass_kernel_spmd

F32 = mybir.dt.float32; BF16 = mybir.dt.bfloat16
AF = mybir.ActivationFunctionType; ALU = mybir.AluOpType; AX = mybir.AxisListType

D_MODEL = 1024; DEPTH = 2; D_IN = 4864; D_FF = 2816
ALPHA = (2 * DEPTH) ** 0.25
EPS = 1e-6
NT = 12; TOK = 1536; T = 16
SEQS = [(0, 1024), (1024, 256), (1280, 256)]
BLKS = [(0, 512), (512, 512), (1024, 512)]
TILE_VEC = [0] * 8 + [1] * 4
NS = 99
SOFF = [0, 65, 82]
SCH = [64, 16, 16]


class StopBuild(Exception):
    pass


class Res:
    __slots__ = ("name", "w", "r")
    def __init__(self, name):
        self.name = name; self.w = None; self.r = {}


class Eng:
    def __init__(self, name, h, sem):
        self.name = name; self.h = h; self.sem = sem; self.count = 0; self.waited = {}


class FW:
    NDMA = 32
    def __init__(self, nc, es):
        self.nc = nc; self.sems = {}; self.eng = {}
        for name, h in (("pe", nc.tensor), ("act", nc.scalar), ("dve", nc.vector), ("pool", nc.gpsimd), ("sp", nc.sync)):
            s = es.enter_context(nc.semaphore("s_" + name))
            self.sems[name] = s; self.eng[name] = Eng(name, h, s)
        self.dma_sems = []
        for i in range(self.NDMA):
            s = es.enter_context(nc.semaphore("s_dma%d" % i))
            self.sems["dma%d" % i] = s; self.dma_sems.append(["dma%d" % i, 0])
        self.dma_i = 0; self.ninstr = 0; self.dead = False; self.prev_plain = False; self.rec = None

    def _wait(self, e, key, val):
        if e.waited.get(key, 0) >= val: return
        e.h.wait_ge(self.sems[key], val); e.waited[key] = val

    def _deps(self, e, reads, writes, chain=False):
        deps = {}
        def add(kv):
            if kv is None: return
            k, v = kv
            if deps.get(k, 0) < v: deps[k] = v
        for r in reads: add(r.w)
        for w in writes:
            if w.w is not None and not (chain and w.w[0] == e.name):
                add(w.w)
            for k, v in w.r.items():
                if k == e.name: continue
                add((k, v))
        return deps

    def op(self, engname, fn, reads=(), writes=(), chain=False):
        if self.dead: return None
        if self.rec is not None:
            self.rec.append(("op", engname, fn, tuple(reads), tuple(writes), chain)); return None
        e = self.eng[engname]
        if engname == "pe" and chain in ("gfirst", "gnext"):
            if chain == "gfirst" and e.count > 0:
                self._wait(e, "pe", e.count)
            chain = True; self.prev_plain = False
        elif engname == "pe":
            plain = chain
            chain = chain and self.prev_plain
            if plain and not chain and e.count > 0:
                self._wait(e, "pe", e.count)
            self.prev_plain = plain
        for k, v in self._deps(e, reads, writes, chain).items(): self._wait(e, k, v)
        ins = fn(e.h); e.count += 1; ins.then_inc(e.sem, 1)
        for r in reads: r.r[e.name] = e.count
        for w in writes: w.w = (e.name, e.count); w.r = {}
        self.ninstr += 1
        return ins

    def dma(self, qname, out, in_, reads=(), writes=()):
        if self.dead: return None
        if self.rec is not None:
            self.rec.append(("dma", qname, out, in_, tuple(reads), tuple(writes))); return None
        e = self.eng[qname]
        deps = self._deps(e, reads, writes)
        slot = self.dma_sems[self.dma_i % self.NDMA]; self.dma_i += 1
        key = slot[0]
        if slot[1] > 0: deps[key] = max(deps.get(key, 0), slot[1])
        for k, v in deps.items(): self._wait(e, k, v)
        ins = e.h.dma_start(out=out, in_=in_)
        slot[1] += 16; ins.then_inc(self.sems[key], 16)
        for r in reads: r.r[key] = slot[1]
        for w in writes: w.w = (key, slot[1]); w.r = {}
        self.ninstr += 1
        return ins

    def replay(self, items):
        for it in items:
            if it[0] == "op": self.op(it[1], it[2], it[3], it[4], it[5])
            else: self.dma(it[1], it[2], it[3], reads=it[4], writes=it[5])

    def barrier(self):
        if self.dead: return
        targets = {n: e.count for n, e in self.eng.items() if e.count > 0}
        for key, cnt in self.dma_sems:
            if cnt > 0: targets[key] = cnt
        for n, e in self.eng.items():
            for k, v in targets.items(): self._wait(e, k, v)

    def finish(self):
        self.dead = False
        self.barrier()


def bl(ap, n):
    return ap.unsqueeze(2).to_broadcast([ap.shape[0], ap.shape[1], n])


def bm(ap, n):
    return ap.unsqueeze(1).to_broadcast([ap.shape[0], n, ap.shape[1]])


def _rep(a, n=128):
    return np.ascontiguousarray(np.broadcast_to(a[None], (n,) + a.shape))


def prep_shared(inp):
    f = lambda a: np.ascontiguousarray(a, dtype=np.float32)
    L = DEPTH
    S = {}
    for k in ("w_ada", "w_in", "w_glu", "w_br_ssm", "w_br_four", "w_out", "w_up", "w_down"):
        S[k] = f(inp[k])
    S["w_br_attn"] = f(inp["w_br_attn"])
    S["b_adaP"] = f(inp["b_ada"].reshape(L, 48, 128).transpose(2, 0, 1))
    S["bgbc"] = f(np.stack([np.stack([_rep(inp["b_ada"][l, 2048:3072]), _rep(inp["b_ada"][l, 5120:6144])], 1) for l in range(L)]))
    S["lnbc"] = f(np.stack([np.stack([_rep(inp[k][l]) for k in ("ln1_g", "ln1_b", "ln2_g", "ln2_b")]) for l in range(L)]))
    cw = inp["conv_w"].reshape(L, 3, 44, 128).transpose(3, 0, 2, 1)
    cb = inp["conv_b"].reshape(L, 44, 128).transpose(2, 0, 1)[..., None]
    S["convp"] = f(np.concatenate([cw, cb], -1))
    S["bglu"] = f(inp["b_glu"].reshape(L, 4, 128).transpose(2, 0, 1))
    S["g640"] = f(np.stack([_rep(np.concatenate([np.tile(inp["q_norm_g"][l], 8), np.tile(inp["k_norm_g"][l], 4)])) for l in range(L)]))
    def lay(a):
        aP = a.reshape(L, 2, 16, 2, 64).transpose(0, 1, 3, 4, 2).reshape(L, 2, 128, 16)
        aF = a.reshape(L, 2, 4, 4, 2, 64).transpose(0, 1, 3, 2, 4, 5)
        aF = np.broadcast_to(aF[:, :, :, None], (L, 2, 4, 32, 4, 2, 64)).reshape(L, 2, 128, 512)
        return np.concatenate([aP, aF], -1)
    ldt = np.broadcast_to(inp["ssm_log_dt"][..., None], (L, 2, 32, 64))
    S["lamin"] = f(np.stack([lay(inp["ssm_a_re"]), lay(inp["ssm_a_im"]), lay(ldt)], 2))
    def bbdP(b):
        Bp = b.reshape(L, 2, 16, 2, 64, 16)
        out = np.zeros((L, 2, 2, 64, 16, 2, 16), np.float32)
        for g2 in range(2):
            out[:, :, g2, :, :, g2, :] = Bp[:, :, :, g2].transpose(0, 1, 3, 2, 4)
        return out.reshape(L, 2, 128, 16, 32)
    def bbdT(b):
        Bq = b.reshape(L, 2, 4, 4, 2, 64, 16)
        out = np.zeros((L, 2, 4, 2, 16, 4, 2, 64), np.float32)
        for g2 in range(2):
            out[:, :, :, g2, :, :, g2, :] = Bq[:, :, :, :, g2].transpose(0, 1, 3, 5, 2, 4)
        return out.reshape(L, 2, 128, 4, 128)
    def cbdP(c):
        Cp = c.reshape(L, 2, 16, 2, 16, 64)
        out = np.zeros((L, 2, 2, 64, 16, 2, 16), np.float32)
        for g2 in range(2):
            out[:, :, g2, :, :, g2, :] = Cp[:, :, :, g2].transpose(0, 1, 4, 2, 3)
        return out.reshape(L, 2, 128, 16, 32)
    S["BbdP"] = f(np.stack([bbdP(inp["ssm_b_re"]), bbdP(inp["ssm_b_im"])], 2))
    S["BbdT"] = f(np.stack([bbdT(inp["ssm_b_re"]), bbdT(inp["ssm_b_im"])], 2))
    S["CbdP"] = f(np.stack([cbdP(inp["ssm_c_re"]), cbdP(inp["ssm_c_im"])], 2))
    Dblk = np.zeros((L, 128, 4, 128), np.float32)
    for l in range(L):
        for q in range(4):
            Dblk[l, np.arange(128), q, np.arange(128)] = inp["ssm_d"][l, q * 128:(q + 1) * 128]
    S["Dblk"] = Dblk
    S["ident"] = np.eye(128, dtype=np.float32)
    mk = np.zeros((128, 4), np.float32); mk[np.arange(128), np.arange(128) // 32] = 1.0
    S["maskP"] = mk
    t = np.arange(1024)
    freqs = (10000.0 ** (-np.arange(16, dtype=np.float32) / 16)).astype(np.float32)
    ang = np.concatenate([(t // 64).astype(np.float32)[:, None] * freqs, (t % 64).astype(np.float32)[:, None] * freqs], -1)
    S["ropec"] = f(np.cos(ang).reshape(8, 128, 32).transpose(1, 0, 2))
    S["ropes"] = f(np.sin(ang).reshape(8, 128, 32).transpose(1, 0, 2))
    def dft(n):
        i = np.arange(n, dtype=np.int64)
        m = (i[:, None] * i[None, :]) % n
        a = 2.0 * np.pi * m.astype(np.float64) / n
        return np.cos(a) / np.sqrt(n), np.sin(a) / np.sqrt(n)
    c128, s128 = dft(128)
    S["cs128"] = f(np.concatenate([c128, s128], 1))
    for n in (1024, 256):
        c, s = dft(n)
        S["cl%d" % n] = f(c.reshape(n // 128, 128, n).transpose(1, 0, 2))
        S["sl%d" % n] = f((-s).reshape(n // 128, 128, n).transpose(1, 0, 2))
    return S


def prep_core(inp, cid):
    f = lambda a: np.ascontiguousarray(a, dtype=np.float32)
    m = {}
    m["xin"] = f(np.concatenate([inp["x_sample"][cid], inp["x_prompt"][2 * cid:2 * cid + 2].reshape(512, 1024)], 0))
    cv = np.stack([inp["c"][cid], inp["c_ctx"]], -1)
    m["cvec"] = f(cv.reshape(8, 128, 2).transpose(1, 0, 2))
    m["ckT"] = f(inp["cache_k"][cid].reshape(DEPTH, 512, 128).transpose(0, 2, 1))
    m["cv"] = f(inp["cache_v"][cid].reshape(DEPTH, 512, 128))
    st = inp["state_ssm"][cid].reshape(DEPTH, 2, 16, 2, 64, 2)
    m["h0"] = f(st.transpose(3, 4, 0, 1, 5, 2).reshape(128, DEPTH, 2, 2, 16))
    return m


IN_SHAPES = {
    "xin": (1536, 1024), "cvec": (128, 8, 2), "ckT": (2, 128, 512), "cv": (2, 512, 128), "h0": (128, 2, 2, 2, 16),
    "w_ada": (2, 1024, 6144), "w_in": (2, 1024, 4864), "w_glu": (2, 512, 512), "w_br_attn": (2, 512, 1024),
    "w_br_ssm": (2, 512, 1024), "w_br_four": (2, 512, 1024), "w_out": (2, 1024, 1024), "w_up": (2, 1024, 5632),
    "w_down": (2, 2816, 1024), "b_adaP": (128, 2, 48), "bgbc": (2, 128, 2, 1024), "lnbc": (2, 4, 128, 1024),
    "convp": (128, 2, 44, 4), "bglu": (128, 2, 4), "g640": (2, 128, 768), "lamin": (2, 2, 3, 128, 528),
    "BbdP": (2, 2, 2, 128, 16, 32), "BbdT": (2, 2, 2, 128, 4, 128), "CbdP": (2, 2, 2, 128, 16, 32),
    "Dblk": (2, 128, 4, 128), "ident": (128, 128), "maskP": (128, 4), "ropec": (128, 8, 32), "ropes": (128, 8, 32),
    "cs128": (128, 256), "cl1024": (128, 8, 1024), "sl1024": (128, 8, 1024), "cl256": (128, 2, 256), "sl256": (128, 2, 256),
}
OUT_SHAPES = {"y": (1536, 1024), "nk": (2, 2, 256, 128), "nv": (2, 2, 256, 128), "nst": (2, 2, 2, 32, 64, 2)}


def build(n_layers=DEPTH, stop=None, dbg_shapes=None):
    nc = bass.Bass("TRN2", target_bir_lowering=False)
    Din = {k: nc.dram_tensor(k, list(s), F32, kind="ExternalInput").ap() for k, s in IN_SHAPES.items()}
    Dout = {k: nc.dram_tensor(k, list(s), F32, kind="ExternalOutput").ap() for k, s in OUT_SHAPES.items()}
    Ddbg = {k: nc.dram_tensor(k, list(s), F32, kind="ExternalOutput").ap() for k, s in (dbg_shapes or {}).items()}
    es = contextlib.ExitStack()
    with es:
        fw = FW(nc, es)
        uid = [0]
        def sb(st, name, shape, dt=F32):
            uid[0] += 1
            return st.enter_context(nc.sbuf_tensor("%s_%d" % (name, uid[0]), list(shape), dt))
        OP = fw.op
        CH = [True]
        def MM(out, lhsT, rhs, start, stop_, r, w, tp=None, chain=None):
            if chain is None: chain = CH[0] and (tp is None)
            if tp is None:
                return fw.op("pe", lambda e: e.matmul(out, lhsT=lhsT, rhs=rhs, start=start, stop=stop_), r, w, chain=chain)
            return fw.op("pe", lambda e: e.matmul(out, lhsT=lhsT, rhs=rhs, start=start, stop=stop_, tile_position=tp), r, w, chain=chain)
        def TT(eng, out, in0, in1, op, r, w):
            return fw.op(eng, lambda e: e.tensor_tensor(out=out, in0=in0, in1=in1, op=op), r, w)
        def TS(eng, out, in0, s1, op0, r, w, s2=None, op1=None):
            if op1 is None:
                return fw.op(eng, lambda e: e.tensor_scalar(out=out, in0=in0, scalar1=s1, scalar2=None, op0=op0), r, w)
            return fw.op(eng, lambda e: e.tensor_scalar(out=out, in0=in0, scalar1=s1, scalar2=s2, op0=op0, op1=op1), r, w)
        def ACT(out, in_, func, r, w, scale=1.0, bias=0.0):
            return fw.op("act", lambda e: e.activation(out=out, in_=in_, func=func, bias=bias, scale=scale), r, w)
        def CP(eng, out, in_, r, w):
            if eng == "act":
                return fw.op("act", lambda e: e.copy(out=out, in_=in_), r, w)
            return fw.op(eng, lambda e: e.tensor_copy(out=out, in_=in_), r, w)
        Rdbg = Res("dbg")
        def DBG(name, ap, r):
            if name in Ddbg:
                fw.dma("pool", Ddbg[name], ap, reads=r, writes=[Rdbg])

        X = sb(es, "X", [128, NT, 1024]); RX = [Res("X%d" % i) for i in range(NT)]
        ident_f = sb(es, "ident_f", [128, 128]); ident_b = sb(es, "ident_b", [128, 128], BF16); Rid = Res("id")
        maskP = sb(es, "maskP", [128, 4]); Rmask = Res("mask")
        modP = sb(es, "modP", [128, 6, 8, 2]); opsc = sb(es, "opsc", [128, 2, 8, 2]); Rmod = Res("mod")
        csl = sb(es, "csl", [128, 8, 2]); sTb = sb(es, "sTb", [128, 8, 2], BF16); sRep = sb(es, "sRep", [128, 2, 8, 128], BF16)
        Rcs = Res("cs")
        stat = sb(es, "stat", [128, NT, 2, 6]); mv = sb(es, "mv", [128, NT, 2]); rs = sb(es, "rs", [128, NT, 2]); Rstat = Res("stat")
        ps = es.enter_context(nc.psum_tensor("ps", [128, 3584], F32)); PB = [Res("pb%d" % i) for i in range(7)]
        pst = es.enter_context(nc.psum_tensor("pst", [128, 1024], BF16)); PT = Res("pt")
        wslot = [None, None]; Rws = [None, None]
        wctr = [0]
        def alloc_ws(st):
            for i_ in range(2):
                wslot[i_] = sb(st, "wslot%d" % i_, [128, 4096], BF16); Rws[i_] = Res("ws%d" % i_)
        def load_w(views):
            s_ = wctr[0] % 2; wctr[0] += 1
            for dstf, src in views:
                fw.dma("pool", dstf(wslot[s_]), src, writes=[Rws[s_]])
            return wslot[s_], Rws[s_]
        def bank(i, n=1):
            return ps[:, i * 512:(i + n) * 512]

        fw.dma("sp", ident_f[:], Din["ident"], writes=[Rid])
        fw.dma("pool", ident_b[:], Din["ident"], writes=[Rid])
        fw.dma("sp", maskP[:], Din["maskP"], writes=[Rmask])
        for i in range(NT):
            fw.dma("sp", X[:, i, :], Din["xin"][i * 128:(i + 1) * 128, :], writes=[RX[i]])
        fw.dma("sp", csl[:], Din["cvec"], writes=[Rcs])
        ACT(csl[:], csl[:], AF.Silu, [Rcs], [Rcs])
        CP("dve", sTb[:], csl[:], [Rcs], [Rcs])
        for v in range(2):
            CP("dve", sRep[:, v, :, :], bl(csl[:, :, v], 128), [Rcs], [Rcs])

        def build_uT(uT, RuT, sub):
            sh_sec = 0 if sub == 0 else 3
            for i in range(NT):
                v = TILE_VEC[i]
                for kk in range(2):
                    bk = kk
                    for k4 in range(4):
                        k = kk * 4 + k4
                        OP("pe", lambda e: e.transpose(bank(bk)[:, k4 * 128:(k4 + 1) * 128], X[:, i, k * 128:(k + 1) * 128], ident_f[:]),
                           [RX[i], Rid], [PB[bk]])
                    for k4 in range(4):
                        k = kk * 4 + k4
                        ACT(uT[:, k, i * 128:(i + 1) * 128], bank(bk)[:, k4 * 128:(k4 + 1) * 128], AF.Identity,
                            [PB[bk], Rmod], [RuT[i]], scale=opsc[:, sub, k, v:v + 1], bias=modP[:, sh_sec, k, v:v + 1])

        def layer_norm_residual(l, which, kc, lhsT_fn, lres_fn, wsrc3, out_tile=None):
            with contextlib.ExitStack() as ph:
                ch_save = CH[0]; CH[0] = 'L' in os.environ.get('CHP', 'mabfFGMLUS')
                alloc_ws(ph)
                lnb = sb(ph, "lnb", [128, 2, 1024]); Rln = Res("lnb")
                tmp = [sb(ph, "lntmp%d" % j, [128, 512]) for j in range(2)]; Rtmp = [Res("lntmp0"), Res("lntmp1")]
                nWd = 2 if kc <= 8 else 1
                Wds = [sb(ph, "Wd%d" % z, [128, kc, 512], BF16) for z in range(nWd)]; RWds = [Res("Wd%d" % z) for z in range(nWd)]
                def load_half(nb):
                    z = nb % nWd
                    for k0 in range(0, kc, 8):
                        k1 = min(kc, k0 + 8)
                        fw.dma("pool", Wds[z][:, k0:k1, :], wsrc3[:, k0:k1, nb * 512:(nb + 1) * 512], writes=[RWds[z]])
                load_half(0)
                if nWd == 2:
                    load_half(1)
                fw.dma("sp", lnb[:, 0, :], Din["lnbc"][l, 2 * which], writes=[Rln])
                fw.dma("sp", lnb[:, 1, :], Din["lnbc"][l, 2 * which + 1], writes=[Rln])
                gbc = sb(ph, "gbc", [128, 2, 1024]); Rgbc = Res("gbc")
                bg = sb(ph, "bg", [128, 1024]); Rbg = Res("bg")
                fw.dma("sp", bg[:], Din["bgbc"][l, :, which, :], writes=[Rbg])
                sec = 2 if which == 0 else 5
                wsrc = Din["w_ada"][l].rearrange("(k p) n -> p k n", p=128)
                for half in range(2):
                    Wg, RWg = load_w([(lambda t: t[:, 0:4096].rearrange("p (k n) -> p k n", k=8), wsrc[:, :, sec * 1024 + half * 512:sec * 1024 + (half + 1) * 512])])
                    Wgv = Wg[:, 0:4096].rearrange("p (k n) -> p k n", k=8)
                    for v in range(2):
                        for k in range(8):
                            MM(bank(v), sRep[:, v, k, :], Wgv[:, k, :], k == 0, k == 7, [RWg, Rcs], [PB[v]])
                        TT("dve", gbc[:, v, half * 512:(half + 1) * 512], bank(v), bg[:, half * 512:(half + 1) * 512], ALU.add, [PB[v], Rbg], [Rgbc])
                cnt = 0
                for nb in range(2):
                    cs = slice(nb * 512, (nb + 1) * 512)
                    Wd = Wds[nb % nWd]; RWd = RWds[nb % nWd]
                    if nWd == 1 and nb == 1:
                        load_half(1)
                    for i in range(NT):
                        v = TILE_VEC[i]
                        bk = 2 + (cnt % 4); tj = cnt % 2; cnt += 1
                        for k in range(kc):
                            MM(bank(bk), lhsT_fn(i, k), Wd[:, k, :], k == 0, k == kc - 1, lres_fn(i, k) + [RWd], [PB[bk]])
                        TT("dve", tmp[tj][:], bank(bk), gbc[:, v, cs], ALU.mult, [PB[bk], Rgbc], [Rtmp[tj]])
                        OP("dve", lambda e: e.scalar_tensor_tensor(out=X[:, i, cs], in0=X[:, i, cs], scalar=float(ALPHA), in1=tmp[tj][:],
                                                                   op0=ALU.mult, op1=ALU.add), [RX[i], Rtmp[tj]], [RX[i]])
                for i in range(NT):
                    for hh in range(2):
                        OP("dve", lambda e: e.bn_stats(out=stat[:, i, hh, :], in_=X[:, i, hh * 512:(hh + 1) * 512]), [RX[i]], [Rstat])
                    OP("dve", lambda e: e.bn_aggr(out=mv[:, i, :], in_=stat[:, i, :, :].rearrange("p a b -> p (a b)")), [Rstat], [Rstat])
                TS("dve", rs[:, :, 0], mv[:, :, 1], EPS, ALU.add, [Rstat], [Rstat])
                ACT(rs[:, :, 0], rs[:, :, 0], AF.Sqrt, [Rstat], [Rstat])
                OP("dve", lambda e: e.reciprocal(out=rs[:, :, 0], in_=rs[:, :, 0]), [Rstat], [Rstat])
                OP("dve", lambda e: e.scalar_tensor_tensor(out=rs[:, :, 1], in0=mv[:, :, 0], scalar=-1.0, in1=rs[:, :, 0],
                                                           op0=ALU.mult, op1=ALU.mult), [Rstat], [Rstat])
                for i in range(NT):
                    ACT(X[:, i, :], X[:, i, :], AF.Identity, [RX[i], Rstat], [RX[i]], scale=rs[:, i, 0:1], bias=rs[:, i, 1:2])
                    TT("pool", X[:, i, :], X[:, i, :], lnb[:, 0, :], ALU.mult, [RX[i], Rln], [RX[i]])
                    TT("pool", X[:, i, :], X[:, i, :], lnb[:, 1, :], ALU.add, [RX[i], Rln], [RX[i]])
                    if out_tile is not None:
                        out_tile(i)
                fw.barrier()
                CH[0] = ch_save

        def layers():
          for l in range(n_layers):
              with contextlib.ExitStack() as ph:
                  wa = [sb(ph, "wa%d" % i, [128, 8, 512], BF16) for i in range(3)]; Rwa = [Res("wa%d" % i) for i in range(3)]
                  bP = sb(ph, "bP", [128, 48]); Rb = Res("bmod")
                  fw.dma("sp", bP[:], Din["b_adaP"][:, l, :], writes=[Rb])
                  wsrc = Din["w_ada"][l].rearrange("(k p) n -> p k n", p=128)
                  pm = bank(6)[:, 0:96]
                  CH[0] = 'm' in os.environ.get('CHP', 'mabfFGMLUS')
                  cbs = [cb for cb in range(12) if cb // 2 not in (2, 5)]
                  for i_ in range(min(3, len(cbs))):
                      fw.dma("pool", wa[i_][:], wsrc[:, :, cbs[i_] * 512:(cbs[i_] + 1) * 512], writes=[Rwa[i_]])
                  for i_, cb in enumerate(cbs):
                      s = i_ % 3
                      sec, half = cb // 2, cb % 2
                      if i_ >= 3 and False:
                          pass
                      for m in range(4):
                          col = (sec * 8 + half * 4 + m) * 2
                          for k in range(8):
                              MM(pm[:, col:col + 2], wa[s][:, k, m * 128:(m + 1) * 128], sTb[:, k, :], k == 0, k == 7,
                                 [Rwa[s], Rcs], [PB[6]])
                      if i_ + 3 < len(cbs):
                          fw.dma("pool", wa[s][:], wsrc[:, :, cbs[i_ + 3] * 512:(cbs[i_ + 3] + 1) * 512], writes=[Rwa[s]])
                  for sec in (0, 1, 3, 4):
                      TT("dve", modP[:, sec, :, :], pm[:, sec * 16:(sec + 1) * 16].rearrange("p (c v) -> p c v", v=2),
                         bl(bP[:, sec * 8:(sec + 1) * 8], 2), ALU.add, [PB[6], Rb], [Rmod])
                  TS("dve", opsc[:, 0, :, :], modP[:, 1, :, :], 1.0, ALU.add, [Rmod], [Rmod])
                  TS("dve", opsc[:, 1, :, :], modP[:, 4, :, :], 1.0, ALU.add, [Rmod], [Rmod])
                  fw.barrier()
              if stop == "mod": fw.dead = True
              DBG("d_modP", modP[:].rearrange("p a b c -> p (a b c)"), [Rmod])

              with contextlib.ExitStack() as L1:
                  QT = sb(L1, "QT", [128, 4, TOK], BF16); RQ = [Res("QT%d" % j) for j in range(4)]; attnT = QT; RattnT = RQ
                  sfT = sb(L1, "sfT", [128, 8, TOK], BF16); RsfT = [Res("sfT%d" % j) for j in range(8)]
                  fourT = sb(L1, "fourT", [128, 4, TOK], BF16); RfourT = [Res("fourT%d" % j) for j in range(4)]
                  Lp = contextlib.ExitStack()
                  lamD = [[sb(Lp, "lam%d%d" % (d_, r), [128, 528]) for r in range(2)] for d_ in range(2)]
                  ffD = [[sb(Lp, "ff%d%d" % (d_, r), [128, 528]) for r in range(2)] for d_ in range(2)]
                  PowP = [sb(Lp, "PowP%d" % r, [128, 2, 16, 17]) for r in range(2)]
                  LH = [sb(Lp, "LH%d" % r, [128, 2, 16, 7]) for r in range(2)]
                  fP = [sb(Lp, "fP%d" % r, [128, 2, 16]) for r in range(2)]
                  RS = Res("ssm")
                  def cmul(dr, di, ar, ai, br, bi, m1, m2):
                      TT("dve", m1, ar, br, ALU.mult, [RS], [RS]); TT("dve", m2, ai, bi, ALU.mult, [RS], [RS])
                      TT("dve", dr, m1, m2, ALU.subtract, [RS], [RS])
                      TT("dve", m1, ar, bi, ALU.mult, [RS], [RS]); TT("dve", m2, ai, br, ALU.mult, [RS], [RS])
                      TT("dve", di, m1, m2, ALU.add, [RS], [RS])
                  def ssm_param_pipeline(st):
                      are, aim, ldt, mag, cc, ss, t1, t2, t3 = [sb(st, "sp%d" % j_, [128, 528]) for j_ in range(9)]
                      for d in range(2):
                          lam = lamD[d]; ff = ffD[d]
                          fw.dma("sp", are[:], Din["lamin"][l, d, 0], writes=[RS])
                          fw.dma("sp", aim[:], Din["lamin"][l, d, 1], writes=[RS])
                          fw.dma("sp", ldt[:], Din["lamin"][l, d, 2], writes=[RS])
                          ACT(ldt[:], ldt[:], AF.Exp, [RS], [RS])
                          TT("dve", t1[:], are[:], ldt[:], ALU.mult, [RS], [RS])
                          ACT(mag[:], t1[:], AF.Exp, [RS], [RS])
                          TT("dve", t2[:], aim[:], ldt[:], ALU.mult, [RS], [RS])
                          ACT(ss[:], t2[:], AF.Sin, [RS], [RS], scale=1.0 / 16)
                          ACT(t1[:], t2[:], AF.Sin, [RS], [RS], scale=1.0 / 32)
                          TT("dve", t1[:], t1[:], t1[:], ALU.mult, [RS], [RS])
                          TS("dve", cc[:], t1[:], -2.0, ALU.mult, [RS], [RS], s2=1.0, op1=ALU.add)
                          for _ in range(4):
                              TT("dve", t1[:], cc[:], cc[:], ALU.mult, [RS], [RS])
                              TT("dve", t2[:], ss[:], ss[:], ALU.mult, [RS], [RS])
                              TT("dve", t3[:], cc[:], ss[:], ALU.mult, [RS], [RS])
                              TT("dve", cc[:], t1[:], t2[:], ALU.subtract, [RS], [RS])
                              TS("dve", ss[:], t3[:], 2.0, ALU.mult, [RS], [RS])
                          TT("dve", lam[0][:], mag[:], cc[:], ALU.mult, [RS], [RS])
                          TT("dve", lam[1][:], mag[:], ss[:], ALU.mult, [RS], [RS])
                          TT("dve", t1[:], are[:], are[:], ALU.mult, [RS], [RS])
                          TT("dve", t2[:], aim[:], aim[:], ALU.mult, [RS], [RS])
                          TT("dve", t1[:], t1[:], t2[:], ALU.add, [RS], [RS])
                          OP("dve", lambda e: e.reciprocal(out=t1[:], in_=t1[:]), [RS], [RS])
                          TS("dve", t2[:], lam[0][:], -1.0, ALU.add, [RS], [RS])
                          TT("dve", t3[:], t2[:], are[:], ALU.mult, [RS], [RS])
                          TT("dve", cc[:], lam[1][:], aim[:], ALU.mult, [RS], [RS])
                          TT("dve", t3[:], t3[:], cc[:], ALU.add, [RS], [RS])
                          TT("dve", ff[0][:], t3[:], t1[:], ALU.mult, [RS], [RS])
                          TT("dve", t3[:], lam[1][:], are[:], ALU.mult, [RS], [RS])
                          TT("dve", cc[:], t2[:], aim[:], ALU.mult, [RS], [RS])
                          TT("dve", t3[:], t3[:], cc[:], ALU.subtract, [RS], [RS])
                          TT("dve", ff[1][:], t3[:], t1[:], ALU.mult, [RS], [RS])
                          pm1 = t1[:, 0:256].rearrange("p (g k) -> p g k", g=16); pm2 = t2[:, 0:256].rearrange("p (g k) -> p g k", g=16)
                          OP("dve", lambda e, d=d: e.memset(PowP[0][:, d, :, 0:1], 1.0), [RS], [RS])
                          OP("dve", lambda e, d=d: e.memset(PowP[1][:, d, :, 0:1], 0.0), [RS], [RS])
                          for r in range(2):
                              TS("dve", PowP[r][:, d, :, 1], lam[r][:, 0:16], 1.0, ALU.mult, [RS], [RS])
                              TS("dve", fP[r][:, d, :], ff[r][:, 0:16], 1.0, ALU.mult, [RS], [RS])
                          for w in (1, 2, 4, 8):
                              br_ = PowP[0][:, d, :, w:w + 1].to_broadcast([128, 16, w]); bi_ = PowP[1][:, d, :, w:w + 1].to_broadcast([128, 16, w])
                              cmul(PowP[0][:, d, :, w + 1:2 * w + 1], PowP[1][:, d, :, w + 1:2 * w + 1],
                                   PowP[0][:, d, :, 1:w + 1], PowP[1][:, d, :, 1:w + 1], br_, bi_, pm1[:, :, 0:w], pm2[:, :, 0:w])
                          for r in range(2):
                              TS("dve", LH[r][:, d, :, 0], PowP[r][:, d, :, 16], 1.0, ALU.mult, [RS], [RS])
                          for m in range(6):
                              cmul(LH[0][:, d, :, m + 1], LH[1][:, d, :, m + 1], LH[0][:, d, :, m], LH[1][:, d, :, m],
                                   LH[0][:, d, :, m], LH[1][:, d, :, m], t1[:, 256:272], t2[:, 256:272])
                  with contextlib.ExitStack() as L2:
                      KT = sb(L2, "KT", [128, 2, 2048], BF16); RKT = Res("KT")
                      Vsb = sb(L2, "Vsb", [128, 16, 2, 2, 128], BF16); RV = Res("V")
                      with contextlib.ExitStack() as ph:
                          alloc_ws(ph)
                          uT = sb(ph, "uT", [128, 8, TOK], BF16); RuT = [Res("uT%d" % i) for i in range(NT)]
                          qn = [sb(ph, "qn%d" % j, [128, 768]) for j in range(2)]; Rqn = [Res("qn0"), Res("qn1")]
                          vf = [sb(ph, "vf%d" % j, [128, 128]) for j in range(2)]; Rvf = [Res("vf0"), Res("vf1")]
                          qkb = [sb(ph, "qkb%d" % j, [128, 768], BF16) for j in range(2)]; Rqkb = [Res("qkb0"), Res("qkb1")]
                          kb2 = [sb(ph, "kb2%d" % j, [128, 128], BF16) for j in range(2)]; Rkb2 = [Res("kb20"), Res("kb21")]
                          ssq = sb(ph, "ssq", [128, 12]); Rssq = Res("ssq")
                          g640 = sb(ph, "g640", [128, 768]); rc = sb(ph, "rc", [128, 8, 32]); rsn = sb(ph, "rsn", [128, 8, 32]); Rcst = Res("cst")
                          rt = [sb(ph, "rt%d" % j, [128, 12, 32]) for j in range(2)]; Rrt = [Res("rt0"), Res("rt1")]
                          win = Din["w_in"][l].rearrange("(k p) n -> p k n", p=128)
                          WA_pre = load_w([(lambda t: t[:, 0:4096].rearrange("p (k n) -> p k n", k=8), win[:, :, 0:512])])
                          WB_pre = load_w([(lambda t: t[:, 0:2048].rearrange("p (k n) -> p k n", k=8), win[:, :, 512:768])])
                          fw.dma("sp", g640[:], Din["g640"][l], writes=[Rcst])
                          fw.dma("sp", rc[:], Din["ropec"], writes=[Rcst])
                          fw.dma("sp", rsn[:], Din["ropes"], writes=[Rcst])
                          fw.dma("pool", KT[:, 0, 1024:1536], Din["ckT"][l], writes=[RKT])
                          fw.dma("pool", KT[0:64, 1, 1024:1536], Din["ckT"][l, 64:128, :], writes=[RKT])
                          fw.dma("pool", KT[64:128, 1, 1024:1536], Din["ckT"][l, 0:64, :], writes=[RKT])
                          OP("dve", lambda e: e.memset(Vsb[:].rearrange("p a b c d -> p (a b c d)"), 1.0), [], [RV])
                          cvv = Din["cv"][l].rearrange("(t p) (h d) -> p t h d", p=128, h=2)
                          for lay in range(2):
                              for kvh_ in range(2):
                                  fw.dma("pool", Vsb[:, 8:12, kvh_, lay, lay * 64:(lay + 1) * 64], cvv[:, :, kvh_, :], writes=[RV])
                          build_uT(uT, RuT, 0)
                          if stop == "ut": fw.dead = True
                          DBG("d_uT", uT[:].rearrange("p a b -> p (a b)"), RuT)

                          def norm_rope(i, pa, pr, nh, gap, outb, Routb, slot):
                              w_ = nh * 64
                              sqv = qn[slot][:, 0:w_]
                              ACT(sqv, pa, AF.Square, pr, [Rqn[slot]])
                              OP("dve", lambda e: e.reduce_sum(out=ssq[:, 0:nh], in_=sqv.rearrange("p (h d) -> p h d", d=64), axis=AX.X), [Rqn[slot]], [Rssq])
                              TS("dve", ssq[:, 0:nh], ssq[:, 0:nh], 1.0 / 64, ALU.mult, [Rssq], [Rssq], s2=EPS, op1=ALU.add)
                              ACT(ssq[:, 0:nh], ssq[:, 0:nh], AF.Sqrt, [Rssq], [Rssq])
                              OP("dve", lambda e: e.reciprocal(out=ssq[:, 0:nh], in_=ssq[:, 0:nh]), [Rssq], [Rssq])
                              qv = qn[slot][:, 0:w_]
                              TT("dve", qv.rearrange("p (h d) -> p h d", d=64), pa.rearrange("p (h d) -> p h d", d=64), bl(ssq[:, 0:nh], 64),
                                 ALU.mult, pr + [Rssq], [Rqn[slot]])
                              TT("dve", qv, qv, gap, ALU.mult, [Rqn[slot], Rcst], [Rqn[slot]])
                              if i >= 8:
                                  CP("act", outb, qv, [Rqn[slot]], [Routb])
                              else:
                                  x4 = qv.rearrange("p (h d two) -> p h d two", d=32, two=2)
                                  o4 = outb.rearrange("p (h d two) -> p h d two", d=32, two=2)
                                  x0, x1 = x4[:, :, :, 0], x4[:, :, :, 1]
                                  cc, ss_ = bm(rc[:, i, :], nh), bm(rsn[:, i, :], nh)
                                  r0, r1 = rt[0][:, 0:nh, :], rt[1][:, 0:nh, :]
                                  TT("dve", r0, x0, cc, ALU.mult, [Rqn[slot], Rcst], [Rrt[0]])
                                  TT("pool", r1, x1, ss_, ALU.mult, [Rqn[slot], Rcst], [Rrt[1]])
                                  TT("dve", o4[:, :, :, 0], r0, r1, ALU.subtract, Rrt, [Routb])
                                  TT("dve", r0, x0, ss_, ALU.mult, [Rqn[slot], Rcst], [Rrt[0]])
                                  TT("pool", r1, x1, cc, ALU.mult, [Rqn[slot], Rcst], [Rrt[1]])
                                  TT("dve", o4[:, :, :, 1], r0, r1, ALU.add, Rrt, [Routb])

                          CH[0] = 'a' in os.environ.get('CHP', 'mabfFGMLUS')
                          WA, RWA = WA_pre; WB, RWB = WB_pre
                          WvA = WA[:, 0:4096].rearrange("p (k n) -> p k n", k=8)
                          WvB = WB[:, 0:2048].rearrange("p (k n) -> p k n", k=8)
                          for i in range(NT):
                              b0 = 2 + 2 * (i % 2); sl = i % 2
                              prs = [PB[b0], PB[b0 + 1]]
                              for k in range(8):
                                  MM(bank(b0), uT[:, k, i * 128:(i + 1) * 128], WvA[:, k, :], k == 0, k == 7, [RuT[i], RWA], [PB[b0]])
                              for k in range(8):
                                  MM(bank(b0 + 1)[:, 0:256], uT[:, k, i * 128:(i + 1) * 128], WvB[:, k, :], k == 0, k == 7, [RuT[i], RWB], [PB[b0 + 1]])
                              vps = bank(b0 + 1)[:, 128:256]
                              norm_rope(i, bank(b0, 2)[:, 0:768], prs, 12, g640[:, 0:768], qkb[sl][:, 0:768], Rqkb[sl], sl)
                              for j in range(4):
                                  src = qkb[sl][:, j * 128:(j + 1) * 128]
                                  OP("pe", lambda e: e.transpose(pst[:, j * 128:(j + 1) * 128], src, ident_b[:]), [Rqkb[sl], Rid], [PT])
                              CP("act", QT[:, :, i * 128:(i + 1) * 128], pst[:, 0:512].rearrange("p (j c) -> p j c", j=4), [PT], RQ)
                              kt = i if i < 8 else 12 + (i - 8)
                              for lay in range(2):
                                  CP("act", Vsb[:, kt, :, lay, lay * 64:(lay + 1) * 64], vps.rearrange("p (h d) -> p h d", h=2), [PB[b0 + 1]], [RV])
                              if i >= 8:
                                  p_, t_ = (i - 8) // 2, (i - 8) % 2
                                  CP("act", vf[sl][:], vps, [PB[b0 + 1]], [Rvf[sl]])
                                  fw.dma("sp", Dout["nk"][p_, l, t_ * 128:(t_ + 1) * 128, :], qn[sl][:, 512:640], reads=[Rqn[sl]], writes=[Rdbg])
                                  fw.dma("sp", Dout["nv"][p_, l, t_ * 128:(t_ + 1) * 128, :], vf[sl][:], reads=[Rvf[sl]], writes=[Rdbg])
                              CP("act", kb2[sl][:, 0:64], qkb[sl][:, 576:640], [Rqkb[sl]], [Rkb2[sl]])
                              CP("act", kb2[sl][:, 64:128], qkb[sl][:, 512:576], [Rqkb[sl]], [Rkb2[sl]])
                              OP("pe", lambda e: e.transpose(pst[:, 512:640], qkb[sl][:, 512:640], ident_b[:]), [Rqkb[sl], Rid], [PT])
                              OP("pe", lambda e: e.transpose(pst[:, 640:768], kb2[sl][:, :], ident_b[:]), [Rkb2[sl], Rid], [PT])
                              kc = i * 128 if i < 8 else 1536 + (i - 8) * 128
                              CP("act", KT[:, 0, kc:kc + 128], pst[:, 512:640], [PT], [RKT])
                              CP("act", KT[:, 1, kc:kc + 128], pst[:, 640:768], [PT], [RKT])
                          WF_pre = load_w([(lambda t: t[:, 0:4096].rearrange("p (k n) -> p k n", k=8), win[:, :, 768:1280])])
                          if stop in ("passA", "passB"): fw.dead = True
                          CH[0] = 'f' in os.environ.get('CHP', 'mabfFGMLUS')
                          for wb in range(2):
                              W, RW = WF_pre
                              if wb == 0:
                                  WF_pre = load_w([(lambda t: t[:, 0:4096].rearrange("p (k n) -> p k n", k=8), win[:, :, 1280:1792])])
                              Wv = W[:, 0:4096].rearrange("p (k n) -> p k n", k=8)
                              for m4 in range(4):
                                  m = wb * 4 + m4
                                  for tb, (c0, cn) in enumerate(BLKS):
                                      bk = (0, 1, 6)[(m * 3 + tb) % 3]
                                      for k in range(8):
                                          MM(bank(bk), Wv[:, k, m4 * 128:(m4 + 1) * 128], uT[:, k, c0:c0 + cn], k == 0, k == 7,
                                             RuT[c0 // 128:(c0 + cn) // 128] + [RW], [PB[bk]])
                                      CP("act", sfT[:, m, c0:c0 + cn], bank(bk), [PB[bk]], [RsfT[m]])
                          DBG("d_sfT", sfT[:].rearrange("p a b -> p (a b)"), RsfT)
                          fw.barrier()
                      if stop == "inproj": fw.dead = True
                      DBG("d_QT", QT[:].rearrange("p a b -> p (a b)"), RQ)
                      DBG("d_KT", KT[:].rearrange("p a b -> p (a b)"), [RKT])
                      CH[0] = 't' in os.environ.get('CHP', 'mabfFGMLUS')
                      with contextlib.ExitStack() as ph:
                          fw.rec = []
                          ssm_param_pipeline(ph)
                          pipe_items = fw.rec if fw.rec is not None else []; fw.rec = None
                          per_unit = (len(pipe_items) + 23) // 24
                          Eb = [sb(ph, "Eb%d" % j, [128, 512], BF16) for j in range(3)]; REb = [Res("Eb%d" % j) for j in range(3)]
                          rsb = [sb(ph, "rsb%d" % j, [128, 512]) for j in range(2)]; Rrsb = [Res("rsb0"), Res("rsb1")]
                          cnt = 0; cnt2 = 0
                          for si, (t0, Ls) in enumerate(SEQS):
                              if si == 0:
                                  qblks = [(0, 512), (512, 512)]; nkt = 12; kt0 = 0; kc0 = 0
                              else:
                                  qblks = [(t0, 256)]; nkt = 2; kt0 = 12 + 2 * (si - 1); kc0 = 1536 + 256 * (si - 1)
                              for h in range(8):
                                  j = h // 2; hh = h % 2; kvh = h // 4; var = 0 if kvh == hh else 1
                                  pv, sm = (slice(0, 64), slice(64, 128)) if hh == 0 else (slice(64, 128), slice(0, 64))
                                  for (q0, qn) in qblks:
                                      pvb = 3 + (cnt2 % 2); cnt2 += 1
                                      def S_mm(kt, sbk):
                                          MM(bank(sbk)[:, 0:qn], KT[hh * 64:(hh + 1) * 64, var, kc0 + kt * 128:kc0 + (kt + 1) * 128],
                                             QT[hh * 64:(hh + 1) * 64, j, q0:q0 + qn], True, True, [RKT, RQ[j]], [PB[sbk]], chain=False)
                                      slots = [(cnt + i_) % 3 for i_ in range(nkt)]; cnt += nkt
                                      S_mm(0, slots[0])
                                      for kt in range(nkt):
                                          sbk = slots[kt]; eb = sbk
                                          if kt + 1 < nkt:
                                              S_mm(kt + 1, slots[kt + 1])
                                          ACT(Eb[eb][:, 0:qn], bank(sbk)[:, 0:qn], AF.Exp, [PB[sbk]], [REb[eb]], scale=0.125)
                                          MM(bank(pvb)[:, 0:qn], Vsb[:, kt0 + kt, kvh, hh, :], Eb[eb][:, 0:qn], kt == 0, kt == nkt - 1,
                                             [RV, REb[eb]], [PB[pvb]], chain=False)
                                      rb = cnt2 % 2
                                      ACT(rsb[rb][sm, 0:qn], bank(pvb)[sm, 0:qn], AF.Ln, [PB[pvb]], [Rrsb[rb]])
                                      ACT(rsb[rb][sm, 0:qn], rsb[rb][sm, 0:qn], AF.Exp, [Rrsb[rb]], [Rrsb[rb]], scale=-1.0)
                                      TT("dve", attnT[pv, j, q0:q0 + qn], bank(pvb)[pv, 0:qn], rsb[rb][sm, 0:qn], ALU.mult,
                                         [PB[pvb], Rrsb[rb]], [RattnT[j]])
                                      fw.replay(pipe_items[:per_unit]); del pipe_items[:per_unit]
                          fw.replay(pipe_items); del pipe_items[:]
                          fw.barrier()
                  DBG("d_attnT", attnT[:].rearrange("p a b -> p (a b)"), RattnT)
                  CH[0] = 'F' in os.environ.get('CHP', 'mabfFGMLUS')
                  with contextlib.ExitStack() as L3:
                    with contextlib.ExitStack() as ph:
                        cs128 = sb(ph, "cs128", [128, 256], BF16); cl = sb(ph, "cl", [128, 8, 1024], BF16); sl_ = sb(ph, "sl", [128, 8, 1024], BF16)
                        clp = sb(ph, "clp", [128, 2, 256], BF16); slp = sb(ph, "slp", [128, 2, 256], BF16); Rft = Res("ftab")
                        Pfm = sb(ph, "Pfm", [128, NT, 1024], BF16); RPfm = [Res("Pfm%d" % i) for i in range(NT)]
                        fw.dma("pool", cs128[:], Din["cs128"], writes=[Rft])
                        for k in range(8):
                            fw.dma("pool", cl[:, k, :], Din["cl1024"][:, k, :], writes=[Rft])
                            fw.dma("pool", sl_[:, k, :], Din["sl1024"][:, k, :], writes=[Rft])
                        fw.dma("pool", clp[:], Din["cl256"], writes=[Rft])
                        fw.dma("pool", slp[:], Din["sl256"], writes=[Rft])
                        for i in range(NT):
                            b0 = 2 + 2 * (i % 2)
                            for g in range(4):
                                MM(bank(b0 + g // 2)[:, (g % 2) * 256:(g % 2 + 1) * 256], sfT[:, 4 + g, i * 128:(i + 1) * 128], cs128[:], True, True,
                                   [RsfT[4 + g], Rft], [PB[b0 + g // 2]])
                            CP("act", Pfm[:, i, :], bank(b0, 2), [PB[b0], PB[b0 + 1]], [RPfm[i]])
                        cnt = 0
                        for si, (t0, Ls) in enumerate(SEQS):
                            ntl = Ls // 128; tb0 = t0 // 128
                            tc_, ts_ = (cl, sl_) if si == 0 else (clp, slp)
                            for g in range(4):
                                for lb in range(0, Ls, 512):
                                    n = min(512, Ls - lb)
                                    bk = (0, 1, 6)[cnt % 3]; cnt += 1
                                    for k in range(ntl):
                                        MM(bank(bk)[:, 0:n], Pfm[:, tb0 + k, g * 256:g * 256 + 128], tc_[:, k, lb:lb + n], k == 0, False,
                                           [RPfm[tb0 + k], Rft], [PB[bk]])
                                        MM(bank(bk)[:, 0:n], Pfm[:, tb0 + k, g * 256 + 128:g * 256 + 256], ts_[:, k, lb:lb + n], False, k == ntl - 1,
                                           [RPfm[tb0 + k], Rft], [PB[bk]])
                                    CP("act", fourT[:, g, t0 + lb:t0 + lb + n], bank(bk)[:, 0:n], [PB[bk]], [RfourT[g]])
                        fw.barrier()
                    DBG("d_fourT", fourT[:].rearrange("p a b -> p (a b)"), RfourT)
                    if stop == "four": fw.dead = True
                    CH[0] = 'S' in os.environ.get('CHP', 'mabfFGMLUS')
                    with contextlib.ExitStack() as S:
                        Hb = sb(S, "Hb", [128, 2, 2, 16, NS], BF16)
                        h0 = sb(S, "h0", [128, 2, 2, 16]); stg = sb(S, "stg", [128, 16, 2])
                        RHb = Res("Hb"); Rstg = Res("stg")
                        fw.dma("sp", h0[:], Din["h0"][:, l], writes=[RS])
                        for d in range(2):
                            with contextlib.ExitStack() as SD:
                                lam = lamD[d]; ff = ffD[d]
                                A = [sb(SD, "A%d" % r, [128, 16, NS]) for r in range(2)]
                                for r in range(2):
                                    OP("dve", lambda e: e.memset(A[r][:].rearrange("p a b -> p (a b)"), 0.0), [RS], [RS])
                                with contextlib.ExitStack() as SW:
                                    W = [sb(SW, "W%d" % r, [128, 16, 128]) for r in range(2)]
                                    WT2 = [[sb(SW, "WT%d_%d" % (z, r), [128, 16, 128], BF16) for r in range(2)] for z in range(2)]
                                    wm = [sb(SW, "wm%d" % r, [128, 8, 128]) for r in range(3)]
                                    BT = [sb(SW, "BT%d" % r, [128, 128]) for r in range(2)]
                                    Lw = [sb(SW, "Lw%d" % r, [128, 128]) for r in range(2)]
                                    RWT2 = [Res("WTa"), Res("WTb")]
                                    def build_W(q):
                                        fs = slice(16 + q * 128, 16 + (q + 1) * 128)
                                        for r in range(2):
                                            fw.dma("sp", BT[r][:], Din["BbdT"][l, d, r, :, q, :], writes=[RS])
                                            ACT(Lw[r][:], lam[r][:, fs], AF.Identity, [RS], [RS])
                                        cmul(W[0][:, 0, :], W[1][:, 0, :], ff[0][:, fs], ff[1][:, fs], BT[0][:], BT[1][:], wm[0][:, 0, :], wm[1][:, 0, :])
                                        for w in (1, 2, 4, 8):
                                            cmul(W[0][:, w:2 * w, :], W[1][:, w:2 * w, :], bm(Lw[0][:], w), bm(Lw[1][:], w),
                                                 W[0][:, 0:w, :], W[1][:, 0:w, :], wm[0][:, 0:w, :], wm[1][:, 0:w, :])
                                            if w < 8:
                                                TT("dve", wm[0][:, 0, :], Lw[0][:], Lw[0][:], ALU.mult, [RS], [RS])
                                                TT("dve", wm[1][:, 0, :], Lw[1][:], Lw[1][:], ALU.mult, [RS], [RS])
                                                TT("dve", wm[2][:, 0, :], Lw[0][:], Lw[1][:], ALU.mult, [RS], [RS])
                                                TT("dve", Lw[0][:], wm[0][:, 0, :], wm[1][:, 0, :], ALU.subtract, [RS], [RS])
                                                TS("dve", Lw[1][:], wm[2][:, 0, :], 2.0, ALU.mult, [RS], [RS])
                                        for r in range(2):
                                            ACT(WT2[q % 2][r][:].rearrange("p a b -> p (a b)"), W[r][:].rearrange("p a b -> p (a b)"), AF.Identity, [RS], [RWT2[q % 2]])
                                    def run_S(q):
                                        WTq = WT2[q % 2]; RWTq = RWT2[q % 2]
                                        Sps = bank(0, 2).rearrange("p (b r c) -> p b r c", b=4, r=2)
                                        for b in range(4):
                                            sv = sfT[32 * b:32 * b + 32, q, :].rearrange("p (c j) -> p c j", j=16)
                                            for r in range(2):
                                                for k in range(16):
                                                    j = 15 - k if d == 0 else k
                                                    MM(Sps[:, b, r, 0:96], WTq[r][32 * b:32 * b + 32, k, :], sv[:, :, j], k == 0, k == 15,
                                                       [RWTq, RsfT[q]], [PB[0], PB[1]], tp=(32 * b, 0))
                                        for r in range(2):
                                            for s_i in range(3):
                                                n = SCH[s_i]; c0 = (0, 64, 80)[s_i]; lo = SOFF[s_i] + (1 if d == 0 else 0)
                                                ACT(A[r][:, 4 * q:4 * q + 4, lo:lo + n], Sps[:, :, r, c0:c0 + n], AF.Identity, [PB[0], PB[1]], [RS])
                                    build_W(0)
                                    for q in range(4):
                                        if q + 1 < 4:
                                            build_W(q + 1)
                                        run_S(q)
                                RSp = Res("ssm_pool"); NPD = 10
                                for r in range(2):
                                    ACT(A[r][:, :, 0 if d == 0 else 64], h0[:, d, r, :], AF.Identity, [RS], [RS, RSp])
                                with contextlib.ExitStack() as SC:
                                    sm = [sb(SC, "sm%d" % r, [128, 16, NS]) for r in range(3)]
                                    for m in range(7):
                                        dd = 1 << m
                                        groups = []
                                        if dd < 65:
                                            groups.append((lambda t_, ps_, lo, hi: t_[:, ps_, lo:hi], 65, None))
                                        if dd < 17:
                                            groups.append((lambda t_, ps_, lo, hi: t_[:, ps_, 65:99].rearrange("p g (s c) -> p g s c", s=2)[:, :, :, lo:hi], 17, 2))
                                        for view, n, ns in groups:
                                            cntn = n - dd
                                            (dlo, dhi, slo, shi) = (dd, n, 0, cntn) if d == 0 else (0, cntn, dd, n)
                                            for eng_, ps_, Rr in (("dve", slice(0, NPD), RS), ("pool", slice(NPD, 16), RSp)):
                                                npz = ps_.stop - ps_.start
                                                if ns is None:
                                                    Lr = bl(LH[0][:, d, ps_, m], cntn); Li = bl(LH[1][:, d, ps_, m], cntn)
                                                    tv = lambda t_: t_[:, ps_, 0:cntn]
                                                else:
                                                    Lr = LH[0][:, d, ps_, m].unsqueeze(2).unsqueeze(3).to_broadcast([128, npz, 2, cntn])
                                                    Li = LH[1][:, d, ps_, m].unsqueeze(2).unsqueeze(3).to_broadcast([128, npz, 2, cntn])
                                                    tv = lambda t_: t_[:, ps_, 0:2 * cntn].rearrange("p g (s c) -> p g s c", s=2)
                                                sr, si = view(A[0], ps_, slo, shi), view(A[1], ps_, slo, shi)
                                                dr, di = view(A[0], ps_, dlo, dhi), view(A[1], ps_, dlo, dhi)
                                                m1, m2, m3 = tv(sm[0]), tv(sm[1]), tv(sm[2])
                                                TT(eng_, m1, sr, Lr, ALU.mult, [Rr], [Rr]); TT(eng_, m2, si, Li, ALU.mult, [Rr], [Rr])
                                                TT(eng_, m1, m1, m2, ALU.subtract, [Rr], [Rr])
                                                TT(eng_, m2, si, Lr, ALU.mult, [Rr], [Rr]); TT(eng_, m3, sr, Li, ALU.mult, [Rr], [Rr])
                                                TT(eng_, m2, m2, m3, ALU.add, [Rr], [Rr])
                                                TT(eng_, dr, dr, m1, ALU.add, [Rr], [Rr]); TT(eng_, di, di, m2, ALU.add, [Rr], [Rr])
                                for r in range(2):
                                    ACT(Hb[:, d, r, :, :], A[r][:], AF.Identity, [RS, RSp], [RHb])
                                for s_i in (1, 2):
                                    idx = SOFF[s_i] + (16 if d == 0 else 0)
                                    for r in range(2):
                                        ACT(stg[:, :, r], A[r][:, :, idx], AF.Identity, [RS, RSp], [Rstg])
                                    fw.dma("sp", Dout["nst"][s_i - 1, l, d].rearrange("(pair g2) p r -> (g2 p) pair r", g2=2), stg[:], reads=[Rstg], writes=[Rdbg])
                                DBG("d_A%d" % d, A[0][:].rearrange("p a b -> p (a b)"), [RS, RSp])
                                fw.barrier()
                        with contextlib.ExitStack() as SQ:
                            CL2 = [[[sb(SQ, "CL%d%d%d" % (z, d, r), [128, 4, 17, 32], BF16) for r in range(2)] for d in range(2)] for z in range(2)]
                            Kblk = [sb(SQ, "Kblk%d" % d, [128, 16, 128], BF16) for d in range(2)]
                            Bb2 = [[[sb(SQ, "Bb%d%d%d" % (z, d, r), [128, 4, 32], BF16) for r in range(2)] for d in range(2)] for z in range(2)]
                            Dbl = sb(SQ, "Dbl", [128, 4, 128], BF16)
                            cm = [sb(SQ, "cm%d" % j, [128, 2, 17, 32]) for j in range(2)]
                            Cl = [sb(SQ, "Cl%d" % r, [128, 4, 32]) for r in range(2)]
                            nCl = sb(SQ, "nCl", [128, 4, 32])
                            Bl = [sb(SQ, "Bl%d" % r, [128, 4, 32]) for r in range(2)]
                            bt = [sb(SQ, "bt%d" % r, [128, 4, 32]) for r in range(2)]
                            gt = [sb(SQ, "gt%d" % r, [128, 512]) for r in range(2)]; Rgt = [Res("gt0"), Res("gt1")]
                            RCL2 = [Res("CLa"), Res("CLb")]; RK = Res("Kblk"); RDbl = Res("Dbl")
                            ydb = sb(SQ, "ydb", [128, 512]); Rydb = Res("ydb")
                            fw.dma("pool", Dbl[:], Din["Dblk"][l], writes=[RDbl])

                            def build_CL(q):
                                z = q % 2; CL = CL2[z]; Bb = Bb2[z]; RCL = RCL2[z]
                                ps4 = slice(4 * q, 4 * q + 4)
                                for d in range(2):
                                    for r in range(2):
                                        fw.dma("sp", Cl[r][:], Din["CbdP"][l, d, r, :, ps4, :], writes=[RS])
                                        fw.dma("sp", Bl[r][:], Din["BbdP"][l, d, r, :, ps4, :], writes=[RS])
                                    TS("dve", nCl[:], Cl[0][:], -1.0, ALU.mult, [RS], [RS])
                                    for hp in range(2):
                                        pp = slice(2 * hp, 2 * hp + 2); pq = slice(4 * q + 2 * hp, 4 * q + 2 * hp + 2)
                                        Pr = PowP[0][:, d, pq, :].unsqueeze(3).to_broadcast([128, 2, 17, 32])
                                        Pi = PowP[1][:, d, pq, :].unsqueeze(3).to_broadcast([128, 2, 17, 32])
                                        Cr = Cl[0][:, pp, :].unsqueeze(2).to_broadcast([128, 2, 17, 32])
                                        Ci = Cl[1][:, pp, :].unsqueeze(2).to_broadcast([128, 2, 17, 32])
                                        nCr = nCl[:, pp, :].unsqueeze(2).to_broadcast([128, 2, 17, 32])
                                        TT("dve", cm[0][:], Pr, Cr, ALU.mult, [RS], [RS]); TT("dve", cm[1][:], Pi, Ci, ALU.mult, [RS], [RS])
                                        TT("dve", CL[d][0][:, pp, :, :], cm[0][:], cm[1][:], ALU.subtract, [RS], [RCL])
                                        TT("dve", cm[0][:], Pr, Ci, ALU.mult, [RS], [RS]); TT("dve", cm[1][:], Pi, nCr, ALU.mult, [RS], [RS])
                                        TT("dve", CL[d][1][:, pp, :, :], cm[1][:], cm[0][:], ALU.subtract, [RS], [RCL])
                                    fr = bl(fP[0][:, d, ps4], 32); fi = bl(fP[1][:, d, ps4], 32)
                                    TT("dve", bt[0][:], fr, Bl[0][:], ALU.mult, [RS], [RS]); TT("dve", bt[1][:], fi, Bl[1][:], ALU.mult, [RS], [RS])
                                    TT("dve", Bb[d][0][:], bt[0][:], bt[1][:], ALU.subtract, [RS], [RCL])
                                    TT("dve", bt[0][:], fr, Bl[1][:], ALU.mult, [RS], [RS]); TT("dve", bt[1][:], fi, Bl[0][:], ALU.mult, [RS], [RS])
                                    TT("dve", Bb[d][1][:], bt[0][:], bt[1][:], ALU.add, [RS], [RCL])

                            def build_K(q):
                                z = q % 2; CL = CL2[z]; Bb = Bb2[z]; RCL = RCL2[z]
                                for d in range(2):
                                    kb = d
                                    for b in range(4):
                                        MM(bank(kb)[32 * b:32 * b + 32, :], Bb[d][0][:, b, :], CL[d][0][:, b, 0:16, :].rearrange("p t n -> p (t n)"), True, False,
                                           [RCL], [PB[kb]], tp=(0, 32 * b))
                                        MM(bank(kb)[32 * b:32 * b + 32, :], Bb[d][1][:, b, :], CL[d][1][:, b, 0:16, :].rearrange("p t n -> p (t n)"), False, True,
                                           [RCL], [PB[kb]], tp=(0, 32 * b))
                                    for cb in range(4):
                                        ACT(Kblk[d][:, :, 32 * cb:32 * cb + 32], bank(kb).rearrange("p (t n) -> p t n", n=32), AF.Identity,
                                            [PB[kb], Rmask], [RK], scale=maskP[:, cb:cb + 1])

                            def run_y(q):
                                z = q % 2; CL = CL2[z]; RCL = RCL2[z]
                                for tb, (c0, cn) in enumerate(BLKS):
                                    yb = 3 + tb
                                    Yb = bank(yb).rearrange("p (c j) -> p c j", j=16)
                                    sblk = sfT[:, q, c0:c0 + cn].rearrange("p (c j) -> p c j", j=16)
                                    MM(bank(yb), Dbl[:, q, :], sfT[:, q, c0:c0 + cn], True, False, [RDbl, RsfT[q]], [PB[yb]])
                                    for d in range(2):
                                        for t in range(16):
                                            if d == 0:
                                                o_, r_ = Yb[:, :, t:16], sblk[:, :, 0:16 - t]
                                            else:
                                                o_, r_ = Yb[:, :, 0:16 - t], sblk[:, :, t:16]
                                            MM(o_, Kblk[d][:, t, :], r_, False, False, [RK, RsfT[q]], [PB[yb]])
                                    GC = os.environ.get('GC', '1') == '1'
                                    for d in range(2):
                                        for j in range(16):
                                            tt = j + 1 if d == 0 else 16 - j
                                            sh = 0 if d == 0 else 1
                                            for r in range(2):
                                                for b in range(4):
                                                    gfirst = (b == 0 and r == 0 and j == 0 and d == 0)
                                                    if tb < 2:
                                                        rhs = Hb[:, d, r, 4 * q + b, 32 * tb + sh:32 * tb + sh + 32]
                                                        o_ = Yb[32 * b:32 * b + 32, :, j]
                                                    else:
                                                        rhs = Hb[:, d, r, 4 * q + b, 65:99].rearrange("p (s c) -> p s c", s=2)[:, :, sh:sh + 16]
                                                        o_ = bank(yb).rearrange("p (s c j) -> p s c j", s=2, j=16)[32 * b:32 * b + 32, :, :, j]
                                                    last = (d == 1 and b == 3 and j == 15 and r == 1)
                                                    MM(o_, CL[d][r][:, b, tt, :], rhs, False, last, [RCL, RHb], [PB[yb]], tp=(0, 32 * b),
                                                       chain=(("gfirst" if gfirst else "gnext") if GC else False))
                                    if ("d_y%d_%d" % (q, tb)) in Ddbg:
                                        ACT(ydb[:], bank(yb), AF.Identity, [PB[yb]], [Rydb])
                                        DBG("d_y%d_%d" % (q, tb), ydb[:], [Rydb])
                                    g_ = tb % 2
                                    ACT(gt[g_][:], bank(yb), AF.Square, [PB[yb]], [Rgt[g_]], scale=float(0.044715 ** 0.5))
                                    OP("dve", lambda e: e.scalar_tensor_tensor(out=gt[g_][:], in0=gt[g_][:], scalar=1.0, in1=bank(yb), op0=ALU.add, op1=ALU.mult),
                                       [Rgt[g_], PB[yb]], [Rgt[g_]])
                                    ACT(gt[g_][:], gt[g_][:], AF.Sigmoid, [Rgt[g_]], [Rgt[g_]], scale=1.5957691216)
                                    TT("dve", sfT[:, q, c0:c0 + cn], gt[g_][:], bank(yb), ALU.mult, [Rgt[g_], PB[yb]], [RsfT[q]])
                            build_CL(0); build_K(0)
                            for q in range(4):
                                if q + 1 < 4:
                                    build_CL(q + 1)
                                run_y(q)
                                if q + 1 < 4:
                                    build_K(q + 1)
                            fw.barrier()
                    Lp.close()
                    ssmT = sb(L3, "ssmT", [128, 4, TOK], BF16); RssmT = [Res("ssmT%d" % j) for j in range(4)]
                    CH[0] = 'G' in os.environ.get('CHP', 'mabfFGMLUS')
                    with contextlib.ExitStack() as ph:
                        alloc_ws(ph)
                        bgl = sb(ph, "bgl", [128, 4]); Rbgl = Res("bgl"); sg = [sb(ph, "sg%d" % j, [128, 512]) for j in range(2)]; Rsg = [Res("sg0"), Res("sg1")]
                        fw.dma("sp", bgl[:], Din["bglu"][:, l, :], writes=[Rbgl])
                        Wg, RWg = load_w([(lambda t: t[:, 0:2048].rearrange("p (k n) -> p k n", k=4), Din["w_glu"][l].rearrange("(k p) n -> p k n", p=128))])
                        Wgv = Wg[:, 0:2048].rearrange("p (k n) -> p k n", k=4)
                        cnt = 0
                        for m in range(4):
                            for tb, (c0, cn) in enumerate(BLKS):
                                bk = (0, 1, 2)[cnt % 3]; sj = cnt % 2; cnt += 1
                                for k in range(4):
                                    MM(bank(bk), Wgv[:, k, m * 128:(m + 1) * 128], sfT[:, k, c0:c0 + cn], k == 0, k == 3, [RWg, RsfT[k]], [PB[bk]])
                                ACT(sg[sj][:], bank(bk), AF.Sigmoid, [PB[bk], Rbgl], [Rsg[sj]], bias=bgl[:, m:m + 1])
                                TT("dve", ssmT[:, m, c0:c0 + cn], sg[sj][:], sfT[:, m, c0:c0 + cn], ALU.mult, [Rsg[sj], RsfT[m]], [RssmT[m]])
                        fw.barrier()
                    DBG("d_gy", sfT[:, 0:4, :].rearrange("p a b -> p (a b)"), RsfT[0:4])
                    DBG("d_ssmT", ssmT[:].rearrange("p a b -> p (a b)"), RssmT)
                    if stop == "ssm": fw.dead = True
                    CH[0] = 'M' in os.environ.get('CHP', 'mabfFGMLUS')
                    with contextlib.ExitStack() as M:
                        mergedT = sb(M, "mergedT", [128, 8, TOK], BF16); Rmg = [Res("mg%d" % j) for j in range(8)]
                        with contextlib.ExitStack() as ph:
                            alloc_ws(ph)
                            uT = sb(ph, "uT", [128, 8, TOK], BF16); RuT = [Res("uT%d" % i) for i in range(NT)]
                            gs = [sb(ph, "gs%d" % j, [128, 512]) for j in range(3)]; Rgs = [Res("gs%d" % j) for j in range(3)]
                            ac = [sb(ph, "ac%d" % j, [128, 512]) for j in range(2)]; Rac = [Res("ac0"), Res("ac1")]
                            build_uT(uT, RuT, 0)
                            win = Din["w_in"][l].rearrange("(k p) n -> p k n", p=128)
                            wbr = [Din[nm][l].rearrange("(k p) n -> p k n", p=128) for nm in ("w_br_attn", "w_br_ssm", "w_br_four")]
                            srcs = [(attnT, RattnT), (ssmT, RssmT), (fourT, RfourT)]
                            Wg2 = [sb(ph, "Wg2_%d" % z, [128, 3072], BF16) for z in range(2)]; RWg2 = [Res("Wg2a"), Res("Wg2b")]
                            Wb2 = [sb(ph, "Wb2_%d" % z, [128, 1536], BF16) for z in range(2)]; RWb2 = [Res("Wb2a"), Res("Wb2b")]
                            def load_c(c):
                                z = c % 2
                                for g in range(3):
                                    fw.dma("pool", Wg2[z][:, g * 1024:(g + 1) * 1024].rearrange("p (k n) -> p k n", k=8),
                                           win[:, :, 1792 + g * 1024 + c * 128:1792 + g * 1024 + (c + 1) * 128], writes=[RWg2[z]])
                                for x in range(3):
                                    fw.dma("pool", Wb2[z][:, x * 512:(x + 1) * 512].rearrange("p (k n) -> p k n", k=4),
                                           wbr[x][:, :, c * 128:(c + 1) * 128], writes=[RWb2[z]])
                            load_c(0)
                            for c in range(8):
                                if c + 1 < 8:
                                    load_c(c + 1)
                                Wg, RWg, Wb, RWb = Wg2[c % 2], RWg2[c % 2], Wb2[c % 2], RWb2[c % 2]
                                for tb, (c0, cn) in enumerate(BLKS):
                                    for g in range(3):
                                        Wgv = Wg[:, g * 1024:(g + 1) * 1024].rearrange("p (k n) -> p k n", k=8)
                                        for k in range(8):
                                            MM(bank(g), Wgv[:, k, :], uT[:, k, c0:c0 + cn], k == 0, k == 7, RuT[c0 // 128:(c0 + cn) // 128] + [RWg], [PB[g]])
                                        ACT(gs[g][:], bank(g), AF.Sigmoid, [PB[g]], [Rgs[g]])
                                    for x in range(3):
                                        Wbv = Wb[:, x * 512:(x + 1) * 512].rearrange("p (k n) -> p k n", k=4)
                                        st_, Rst = srcs[x]
                                        for k in range(4):
                                            MM(bank(3 + x), Wbv[:, k, :], st_[:, k, c0:c0 + cn], k == 0, k == 3, [Rst[k], RWb], [PB[3 + x]])
                                    TT("dve", ac[0][:], gs[0][:], bank(3), ALU.mult, [Rgs[0], PB[3]], [Rac[0]])
                                    TT("dve", ac[1][:], gs[1][:], bank(4), ALU.mult, [Rgs[1], PB[4]], [Rac[1]])
                                    TT("dve", ac[0][:], ac[0][:], ac[1][:], ALU.add, Rac, [Rac[0]])
                                    TT("dve", ac[1][:], gs[2][:], bank(5), ALU.mult, [Rgs[2], PB[5]], [Rac[1]])
                                    TT("dve", mergedT[:, c, c0:c0 + cn], ac[0][:], ac[1][:], ALU.add, Rac, [Rmg[c]])
                            fw.barrier()
                        DBG("d_mergedT", mergedT[:].rearrange("p a b -> p (a b)"), Rmg)
                        if stop == "merge": fw.dead = True
                        layer_norm_residual(l, 0, 8, lambda i, k: mergedT[:, k, i * 128:(i + 1) * 128], lambda i, k: [Rmg[k]],
                                            Din["w_out"][l].rearrange("(k p) n -> p k n", p=128))
              if "d_x1" in Ddbg:
                  for i in range(NT):
                      fw.dma("sp", Ddbg["d_x1"][i * 128:(i + 1) * 128, :], X[:, i, :], reads=[RX[i]], writes=[Rdbg])
              if stop == "ln1": fw.dead = True
              CH[0] = 'U' in os.environ.get('CHP', 'mabfFGMLUS')
              with contextlib.ExitStack() as Fs:
                  actT = sb(Fs, "actT", [128, 22, TOK], BF16); Ract = [Res("act%d" % j) for j in range(22)]
                  with contextlib.ExitStack() as ph:
                      alloc_ws(ph)
                      uT = sb(ph, "uT", [128, 8, TOK], BF16); RuT = [Res("uT%d" % i) for i in range(NT)]
                      cvp = sb(ph, "cvp", [128, 44, 4]); Rcv = Res("cvp")
                      hc = [sb(ph, "hc%d" % j, [128, TOK]) for j in range(2)]; Rhc = [Res("hc0"), Res("hc1")]
                      fw.dma("sp", cvp[:], Din["convp"][:, l], writes=[Rcv])
                      wup = Din["w_up"][l].rearrange("(k p) n -> p k n", p=128)
                      def load_up(bi):
                          return load_w([(lambda t: t[:, 0:4096].rearrange("p (k n) -> p k n", k=8), wup[:, :, bi * 512:(bi + 1) * 512])])
                      nxt_up = load_up(0)
                      build_uT(uT, RuT, 1)
                      for ch in range(44):
                          if ch % 4 == 0:
                              W, RW = nxt_up
                              if ch // 4 + 1 < 11:
                                  nxt_up = load_up(ch // 4 + 1)
                              Wv = W[:, 0:4096].rearrange("p (k n) -> p k n", k=8)
                          cw = ch % 4; pb0 = 3 * (ch % 2); hj = ch % 2
                          prs = [PB[pb0], PB[pb0 + 1], PB[pb0 + 2]]
                          for tb, (c0, cn) in enumerate(BLKS):
                              for k in range(8):
                                  MM(bank(pb0 + tb), Wv[:, k, cw * 128:(cw + 1) * 128], uT[:, k, c0:c0 + cn], k == 0, k == 7,
                                     RuT[c0 // 128:(c0 + cn) // 128] + [RW], [prs[tb]])
                          hp = bank(pb0, 3); h_ = hc[hj]
                          ACT(h_[:], hp, AF.Identity, prs + [Rcv], [Rhc[hj]], scale=cvp[:, ch, 1:2], bias=cvp[:, ch, 3:4])
                          def stt(o_, i0, sc, i1):
                              OP("dve", lambda e: e.scalar_tensor_tensor(out=o_, in0=i0, scalar=sc, in1=i1, op0=ALU.mult, op1=ALU.add),
                                 prs + [Rcv, Rhc[hj]], [Rhc[hj]])
                          stt(h_[:, 1:1024], hp[:, 0:1023], cvp[:, ch, 0:1], h_[:, 1:1024])
                          stt(h_[:, 0:1023], hp[:, 1:1024], cvp[:, ch, 2:3], h_[:, 0:1023])
                          h3 = h_[:, 1024:1536].rearrange("p (s c) -> p s c", s=2); p3 = hp[:, 1024:1536].rearrange("p (s c) -> p s c", s=2)
                          stt(h3[:, :, 1:256], p3[:, :, 0:255], cvp[:, ch, 0:1], h3[:, :, 1:256])
                          stt(h3[:, :, 0:255], p3[:, :, 1:256], cvp[:, ch, 2:3], h3[:, :, 0:255])
                          if ch < 22:
                              ACT(actT[:, ch, :], h_[:], AF.Silu, [Rhc[hj]], [Ract[ch]])
                          else:
                              TT("dve", actT[:, ch - 22, :], actT[:, ch - 22, :], h_[:], ALU.mult, [Ract[ch - 22], Rhc[hj]], [Ract[ch - 22]])
                      fw.barrier()
                  DBG("d_actT", actT[:].rearrange("p a b -> p (a b)"), Ract)
                  if stop == "ffn": fw.dead = True
                  def y_out(i):
                      fw.dma("sp", Dout["y"][i * 128:(i + 1) * 128, :], X[:, i, :], reads=[RX[i]], writes=[Rdbg])
                  layer_norm_residual(l, 1, 22, lambda i, k: actT[:, k, i * 128:(i + 1) * 128], lambda i, k: [Ract[k]],
                                      Din["w_down"][l].rearrange("(k p) n -> p k n", p=128),
                                      out_tile=(y_out if l == n_layers - 1 else None))
        layers()
        if fw.dead:
            fw.dead = False
            for i in range(NT):
                fw.dma("sp", Dout["y"][i * 128:(i + 1) * 128, :], X[:, i, :], reads=[RX[i]], writes=[Rdbg])
        fw.finish()
    return nc


def kernel(**inputs):
    inp = {k: np.asarray(v) for k, v in inputs.items()}
    S = prep_shared(inp)
    nc = build()
    in_maps = []
    for cid in range(8):
        m = dict(S); m.update(prep_core(inp, cid))
        in_maps.append(m)
    res = run_bass_kernel_spmd(nc, in_maps, core_ids=list(range(8)))
    y_p = np.zeros((16, 256, 1024), np.float32); y_s = np.zeros((8, 1024, 1024), np.float32)
    nk = np.zeros((16, 2, 256, 2, 64), np.float32); nv = np.zeros((16, 2, 256, 2, 64), np.float32)
    nst = np.zeros((16, 2, 2, 32, 64, 2), np.float32)
    for cid in range(8):
        r = res.results[cid]
        y_s[cid] = r["y"][0:1024]
        y_p[2 * cid:2 * cid + 2] = r["y"][1024:1536].reshape(2, 256, 1024)
        nk[2 * cid:2 * cid + 2] = r["nk"].reshape(2, 2, 256, 2, 64)
        nv[2 * cid:2 * cid + 2] = r["nv"].reshape(2, 2, 256, 2, 64)
        nst[2 * cid:2 * cid + 2] = r["nst"]
    return (y_p, y_s, nk, nv, nst)
```

```python
import contextlib, os
SK = os.environ.get("SK", "")
import numpy as np
import concourse.bass as bass
import concourse.mybir as mybir
from concourse.bass_utils import run_bass_kernel_spmd

F32 = mybir.dt.float32; BF16 = mybir.dt.bfloat16
AF = mybir.ActivationFunctionType; ALU = mybir.AluOpType; AX = mybir.AxisListType

D_MODEL = 1024; DEPTH = 2; D_IN = 4864; D_FF = 2816
ALPHA = (2 * DEPTH) ** 0.25
EPS = 1e-6
NT = 12; TOK = 1536; T = 16
SEQS = [(0, 1024), (1024, 256), (1280, 256)]
BLKS = [(0, 512), (512, 512), (1024, 512)]
TILE_VEC = [0] * 8 + [1] * 4
NS = 99
SOFF = [0, 65, 82]
SCH = [64, 16, 16]


class StopBuild(Exception):
    pass


class Res:
    __slots__ = ("name", "w", "r")
    def __init__(self, name):
        self.name = name; self.w = None; self.r = {}


class Eng:
    def __init__(self, name, h, sem):
        self.name = name; self.h = h; self.sem = sem; self.count = 0; self.waited = {}


class FW:
    NDMA = 32
    def __init__(self, nc, es):
        self.nc = nc; self.sems = {}; self.eng = {}
        for name, h in (("pe", nc.tensor), ("act", nc.scalar), ("dve", nc.vector), ("pool", nc.gpsimd), ("sp", nc.sync)):
            s = es.enter_context(nc.semaphore("s_" + name))
            self.sems[name] = s; self.eng[name] = Eng(name, h, s)
        self.dma_sems = []
        for i in range(self.NDMA):
            s = es.enter_context(nc.semaphore("s_dma%d" % i))
            self.sems["dma%d" % i] = s; self.dma_sems.append(["dma%d" % i, 0])
        self.dma_i = 0; self.dma_ip = 0; self.ninstr = 0; self.dead = False; self.prev_plain = False; self.rec = None

    def _wait(self, e, key, val):
        if e.waited.get(key, 0) >= val: return
        e.h.wait_ge(self.sems[key], val); e.waited[key] = val

    def _deps(self, e, reads, writes, chain=False):
        deps = {}
        def add(kv):
            if kv is None: return
            k, v = kv
            if deps.get(k, 0) < v: deps[k] = v
        for r in reads: add(r.w)
        for w in writes:
            if w.w is not None and not (chain and w.w[0] == e.name):
                add(w.w)
            for k, v in w.r.items():
                if k == e.name: continue
                add((k, v))
        return deps

    def op(self, engname, fn, reads=(), writes=(), chain=False):
        if self.dead: return None
        if self.rec is not None:
            self.rec.append(("op", engname, fn, tuple(reads), tuple(writes), chain)); return None
        e = self.eng[engname]
        if engname == "pe" and chain in ("gfirst", "gnext"):
            if chain == "gfirst" and e.count > 0:
                self._wait(e, "pe", e.count)
            chain = True; self.prev_plain = False
        elif engname == "pe":
            plain = chain
            chain = chain and self.prev_plain
            if plain and not chain and e.count > 0:
                self._wait(e, "pe", e.count)
            self.prev_plain = plain
        for k, v in self._deps(e, reads, writes, chain).items(): self._wait(e, k, v)
        ins = fn(e.h); e.count += 1; ins.then_inc(e.sem, 1)
        for r in reads: r.r[e.name] = e.count
        for w in writes: w.w = (e.name, e.count); w.r = {}
        self.ninstr += 1
        return ins

    def dma(self, qname, out, in_, reads=(), writes=()):
        if self.dead: return None
        if self.rec is not None:
            self.rec.append(("dma", qname, out, in_, tuple(reads), tuple(writes))); return None
        e = self.eng[qname]
        deps = self._deps(e, reads, writes)
        half = self.NDMA // 2
        if qname == "pool":
            slot = self.dma_sems[half + (self.dma_ip % half)]; self.dma_ip += 1
        else:
            slot = self.dma_sems[self.dma_i % half]; self.dma_i += 1
        key = slot[0]
        if slot[1] > 0: deps[key] = max(deps.get(key, 0), slot[1])
        for k, v in deps.items(): self._wait(e, k, v)
        ins = e.h.dma_start(out=out, in_=in_)
        slot[1] += 16; ins.then_inc(self.sems[key], 16)
        for r in reads: r.r[key] = slot[1]
        for w in writes: w.w = (key, slot[1]); w.r = {}
        self.ninstr += 1
        return ins

    def replay(self, items):
        for it in items:
            if it[0] == "op": self.op(it[1], it[2], it[3], it[4], it[5])
            else: self.dma(it[1], it[2], it[3], reads=it[4], writes=it[5])

    def barrier(self):
        if self.dead: return
        targets = {n: e.count for n, e in self.eng.items() if e.count > 0}
        for key, cnt in self.dma_sems:
            if cnt > 0: targets[key] = cnt
        for n, e in self.eng.items():
            for k, v in targets.items(): self._wait(e, k, v)

    def finish(self):
        self.dead = False
        self.barrier()


def bl(ap, n):
    return ap.unsqueeze(2).to_broadcast([ap.shape[0], ap.shape[1], n])


def bm(ap, n):
    return ap.unsqueeze(1).to_broadcast([ap.shape[0], n, ap.shape[1]])


def _rep(a, n=128):
    return np.ascontiguousarray(np.broadcast_to(a[None], (n,) + a.shape))


def prep_shared(inp):
    f = lambda a: np.ascontiguousarray(a, dtype=np.float32)
    L = DEPTH
    S = {}
    for k in ("w_ada", "w_in", "w_glu", "w_br_ssm", "w_br_four", "w_out", "w_up", "w_down"):
        S[k] = f(inp[k])
    S["w_br_attn"] = f(inp["w_br_attn"])
    S["b_adaP"] = f(inp["b_ada"].reshape(L, 48, 128).transpose(2, 0, 1))
    S["bgbc"] = f(np.stack([np.stack([_rep(inp["b_ada"][l, 2048:3072]), _rep(inp["b_ada"][l, 5120:6144])], 1) for l in range(L)]))
    S["lnbc"] = f(np.stack([np.stack([_rep(inp[k][l]) for k in ("ln1_g", "ln1_b", "ln2_g", "ln2_b")]) for l in range(L)]))
    cw = inp["conv_w"].reshape(L, 3, 44, 128).transpose(3, 0, 2, 1)
    cb = inp["conv_b"].reshape(L, 44, 128).transpose(2, 0, 1)[..., None]
    S["convp"] = f(np.concatenate([cw, cb], -1))
    S["bglu"] = f(inp["b_glu"].reshape(L, 4, 128).transpose(2, 0, 1))
    S["g640"] = f(np.stack([_rep(np.concatenate([np.tile(inp["q_norm_g"][l], 8), np.tile(inp["k_norm_g"][l], 4)])) for l in range(L)]))
    def lay(a):
        aP = a.reshape(L, 2, 16, 2, 64).transpose(0, 1, 3, 4, 2).reshape(L, 2, 128, 16)
        aF = a.reshape(L, 2, 4, 4, 2, 64).transpose(0, 1, 3, 2, 4, 5)
        aF = np.broadcast_to(aF[:, :, :, None], (L, 2, 4, 32, 4, 2, 64)).reshape(L, 2, 128, 512)
        return np.concatenate([aP, aF], -1)
    ldt = np.broadcast_to(inp["ssm_log_dt"][..., None], (L, 2, 32, 64))
    S["lamin"] = f(np.stack([lay(inp["ssm_a_re"]), lay(inp["ssm_a_im"]), lay(ldt)], 2))
    def bbdP(b):
        Bp = b.reshape(L, 2, 16, 2, 64, 16)
        out = np.zeros((L, 2, 2, 64, 16, 2, 16), np.float32)
        for g2 in range(2):
            out[:, :, g2, :, :, g2, :] = Bp[:, :, :, g2].transpose(0, 1, 3, 2, 4)
        return out.reshape(L, 2, 128, 16, 32)
    def bbdT(b):
        Bq = b.reshape(L, 2, 4, 4, 2, 64, 16)
        out = np.zeros((L, 2, 4, 2, 16, 4, 2, 64), np.float32)
        for g2 in range(2):
            out[:, :, :, g2, :, :, g2, :] = Bq[:, :, :, :, g2].transpose(0, 1, 3, 5, 2, 4)
        return out.reshape(L, 2, 128, 4, 128)
    def cbdP(c):
        Cp = c.reshape(L, 2, 16, 2, 16, 64)
        out = np.zeros((L, 2, 2, 64, 16, 2, 16), np.float32)
        for g2 in range(2):
            out[:, :, g2, :, :, g2, :] = Cp[:, :, :, g2].transpose(0, 1, 4, 2, 3)
        return out.reshape(L, 2, 128, 16, 32)
    S["BbdP"] = f(np.stack([bbdP(inp["ssm_b_re"]), bbdP(inp["ssm_b_im"])], 2))
    S["BbdT"] = f(np.stack([bbdT(inp["ssm_b_re"]), bbdT(inp["ssm_b_im"])], 2))
    S["CbdP"] = f(np.stack([cbdP(inp["ssm_c_re"]), cbdP(inp["ssm_c_im"])], 2))
    Dblk = np.zeros((L, 128, 4, 128), np.float32)
    for l in range(L):
        for q in range(4):
            Dblk[l, np.arange(128), q, np.arange(128)] = inp["ssm_d"][l, q * 128:(q + 1) * 128]
    S["Dblk"] = Dblk
    S["ident"] = np.eye(128, dtype=np.float32)
    mk = np.zeros((128, 4), np.float32); mk[np.arange(128), np.arange(128) // 32] = 1.0
    S["maskP"] = mk
    t = np.arange(1024)
    freqs = (10000.0 ** (-np.arange(16, dtype=np.float32) / 16)).astype(np.float32)
    ang = np.concatenate([(t // 64).astype(np.float32)[:, None] * freqs, (t % 64).astype(np.float32)[:, None] * freqs], -1)
    S["ropec"] = f(np.cos(ang).reshape(8, 128, 32).transpose(1, 0, 2))
    S["ropes"] = f(np.sin(ang).reshape(8, 128, 32).transpose(1, 0, 2))
    def dft(n):
        i = np.arange(n, dtype=np.int64)
        m = (i[:, None] * i[None, :]) % n
        a = 2.0 * np.pi * m.astype(np.float64) / n
        return np.cos(a) / np.sqrt(n), np.sin(a) / np.sqrt(n)
    c128, s128 = dft(128)
    S["cs128"] = f(np.concatenate([c128, s128], 1))
    for n in (1024, 256):
        c, s = dft(n)
        S["cl%d" % n] = f(c.reshape(n // 128, 128, n).transpose(1, 0, 2))
        S["sl%d" % n] = f((-s).reshape(n // 128, 128, n).transpose(1, 0, 2))
    return S


def prep_core(inp, cid):
    f = lambda a: np.ascontiguousarray(a, dtype=np.float32)
    m = {}
    m["xin"] = f(np.concatenate([inp["x_sample"][cid], inp["x_prompt"][2 * cid:2 * cid + 2].reshape(512, 1024)], 0))
    cv = np.stack([inp["c"][cid], inp["c_ctx"]], -1)
    m["cvec"] = f(cv.reshape(8, 128, 2).transpose(1, 0, 2))
    m["ckT"] = f(inp["cache_k"][cid].reshape(DEPTH, 512, 128).transpose(0, 2, 1))
    m["cv"] = f(inp["cache_v"][cid].reshape(DEPTH, 512, 128))
    st = inp["state_ssm"][cid].reshape(DEPTH, 2, 16, 2, 64, 2)
    m["h0"] = f(st.transpose(3, 4, 0, 1, 5, 2).reshape(128, DEPTH, 2, 2, 16))
    return m


IN_SHAPES = {
    "xin": (1536, 1024), "cvec": (128, 8, 2), "ckT": (2, 128, 512), "cv": (2, 512, 128), "h0": (128, 2, 2, 2, 16),
    "w_ada": (2, 1024, 6144), "w_in": (2, 1024, 4864), "w_glu": (2, 512, 512), "w_br_attn": (2, 512, 1024),
    "w_br_ssm": (2, 512, 1024), "w_br_four": (2, 512, 1024), "w_out": (2, 1024, 1024), "w_up": (2, 1024, 5632),
    "w_down": (2, 2816, 1024), "b_adaP": (128, 2, 48), "bgbc": (2, 128, 2, 1024), "lnbc": (2, 4, 128, 1024),
    "convp": (128, 2, 44, 4), "bglu": (128, 2, 4), "g640": (2, 128, 768), "lamin": (2, 2, 3, 128, 528),
    "BbdP": (2, 2, 2, 128, 16, 32), "BbdT": (2, 2, 2, 128, 4, 128), "CbdP": (2, 2, 2, 128, 16, 32),
    "Dblk": (2, 128, 4, 128), "ident": (128, 128), "maskP": (128, 4), "ropec": (128, 8, 32), "ropes": (128, 8, 32),
    "cs128": (128, 256), "cl1024": (128, 8, 1024), "sl1024": (128, 8, 1024), "cl256": (128, 2, 256), "sl256": (128, 2, 256),
}
OUT_SHAPES = {"y": (1536, 1024), "nk": (2, 2, 256, 128), "nv": (2, 2, 256, 128), "nst": (2, 2, 2, 32, 64, 2)}


def build(n_layers=DEPTH, stop=None, dbg_shapes=None):
    nc = bass.Bass("TRN2", target_bir_lowering=False)
    Din = {k: nc.dram_tensor(k, list(s), F32, kind="ExternalInput").ap() for k, s in IN_SHAPES.items()}
    Dout = {k: nc.dram_tensor(k, list(s), F32, kind="ExternalOutput").ap() for k, s in OUT_SHAPES.items()}
    Ddbg = {k: nc.dram_tensor(k, list(s), F32, kind="ExternalOutput").ap() for k, s in (dbg_shapes or {}).items()}
    es = contextlib.ExitStack()
    with es:
        fw = FW(nc, es)
        uid = [0]
        def sb(st, name, shape, dt=F32):
            uid[0] += 1
            return st.enter_context(nc.sbuf_tensor("%s_%d" % (name, uid[0]), list(shape), dt))
        OP = fw.op
        CH = [True]
        def MM(out, lhsT, rhs, start, stop_, r, w, tp=None, chain=None):
            if chain is None: chain = CH[0] and (tp is None)
            if tp is None:
                return fw.op("pe", lambda e: e.matmul(out, lhsT=lhsT, rhs=rhs, start=start, stop=stop_), r, w, chain=chain)
            return fw.op("pe", lambda e: e.matmul(out, lhsT=lhsT, rhs=rhs, start=start, stop=stop_, tile_position=tp), r, w, chain=chain)
        def TT(eng, out, in0, in1, op, r, w):
            return fw.op(eng, lambda e: e.tensor_tensor(out=out, in0=in0, in1=in1, op=op), r, w)
        def TS(eng, out, in0, s1, op0, r, w, s2=None, op1=None):
            if op1 is None:
                return fw.op(eng, lambda e: e.tensor_scalar(out=out, in0=in0, scalar1=s1, scalar2=None, op0=op0), r, w)
            return fw.op(eng, lambda e: e.tensor_scalar(out=out, in0=in0, scalar1=s1, scalar2=s2, op0=op0, op1=op1), r, w)
        def ACT(out, in_, func, r, w, scale=1.0, bias=0.0):
            return fw.op("act", lambda e: e.activation(out=out, in_=in_, func=func, bias=bias, scale=scale), r, w)
        def CP(eng, out, in_, r, w):
            if eng == "act":
                return fw.op("act", lambda e: e.copy(out=out, in_=in_), r, w)
            return fw.op(eng, lambda e: e.tensor_copy(out=out, in_=in_), r, w)
        Rdbg = Res("dbg")
        def DBG(name, ap, r):
            if name in Ddbg:
                fw.dma("pool", Ddbg[name], ap, reads=r, writes=[Rdbg])

        X = sb(es, "X", [128, NT, 1024]); RX = [Res("X%d" % i) for i in range(NT)]
        ident_f = sb(es, "ident_f", [128, 128]); ident_b = sb(es, "ident_b", [128, 128], BF16); Rid = Res("id")
        maskP = sb(es, "maskP", [128, 4]); Rmask = Res("mask")
        modP = sb(es, "modP", [128, 6, 8, 2]); opsc = sb(es, "opsc", [128, 2, 8, 2]); Rmod = Res("mod")
        csl = sb(es, "csl", [128, 8, 2]); sTb = sb(es, "sTb", [128, 8, 2], BF16); sRep = sb(es, "sRep", [128, 2, 8, 128], BF16)
        Rcs = Res("cs")
        stat = sb(es, "stat", [128, NT, 2, 6]); mv = sb(es, "mv", [128, NT, 2]); rs = sb(es, "rs", [128, NT, 2]); Rstat = Res("stat")
        ps = es.enter_context(nc.psum_tensor("ps", [128, 3584], F32)); PB = [Res("pb%d" % i) for i in range(7)]
        pst = es.enter_context(nc.psum_tensor("pst", [128, 1024], BF16)); PT = Res("pt")
        wslot = [None, None]; Rws = [None, None]
        wctr = [0]
        def alloc_ws(st):
            for i_ in range(2):
                wslot[i_] = sb(st, "wslot%d" % i_, [128, 4096], BF16); Rws[i_] = Res("ws%d" % i_)
        def load_w(views):
            s_ = wctr[0] % 2; wctr[0] += 1
            for dstf, src in views:
                fw.dma("pool", dstf(wslot[s_]), src, writes=[Rws[s_]])
            return wslot[s_], Rws[s_]
        def bank(i, n=1):
            return ps[:, i * 512:(i + n) * 512]

        fw.dma("sp", ident_f[:], Din["ident"], writes=[Rid])
        fw.dma("pool", ident_b[:], Din["ident"], writes=[Rid])
        fw.dma("sp", maskP[:], Din["maskP"], writes=[Rmask])
        for i in range(NT):
            fw.dma("sp", X[:, i, :], Din["xin"][i * 128:(i + 1) * 128, :], writes=[RX[i]])
        fw.dma("sp", csl[:], Din["cvec"], writes=[Rcs])
        ACT(csl[:], csl[:], AF.Silu, [Rcs], [Rcs])
        CP("dve", sTb[:], csl[:], [Rcs], [Rcs])
        for v in range(2):
            CP("dve", sRep[:, v, :, :], bl(csl[:, :, v], 128), [Rcs], [Rcs])

        def build_uT(uT, RuT, sub):
            sh_sec = 0 if sub == 0 else 3
            for i in range(NT):
                v = TILE_VEC[i]
                for kk in range(2):
                    bk = kk
                    for k4 in range(4):
                        k = kk * 4 + k4
                        OP("pe", lambda e: e.transpose(bank(bk)[:, k4 * 128:(k4 + 1) * 128], X[:, i, k * 128:(k + 1) * 128], ident_f[:]),
                           [RX[i], Rid], [PB[bk]])
                    for k4 in range(4):
                        k = kk * 4 + k4
                        ACT(uT[:, k, i * 128:(i + 1) * 128], bank(bk)[:, k4 * 128:(k4 + 1) * 128], AF.Identity,
                            [PB[bk], Rmod], [RuT[i]], scale=opsc[:, sub, k, v:v + 1], bias=modP[:, sh_sec, k, v:v + 1])

        def layer_norm_residual(l, which, kc, lhsT_fn, lres_fn, wsrc3, out_tile=None):
            with contextlib.ExitStack() as ph:
                ch_save = CH[0]; CH[0] = 'L' in os.environ.get('CHP', 'mabfFGMLUS')
                alloc_ws(ph)
                lnb = sb(ph, "lnb", [128, 2, 1024]); Rln = Res("lnb")
                tmp = [sb(ph, "lntmp%d" % j, [128, 512]) for j in range(2)]; Rtmp = [Res("lntmp0"), Res("lntmp1")]
                nWd = 2 if kc <= 8 else 1
                Wds = [sb(ph, "Wd%d" % z, [128, kc, 512], BF16) for z in range(nWd)]; RWds = [Res("Wd%d" % z) for z in range(nWd)]
                def load_half(nb):
                    z = nb % nWd
                    for k0 in range(0, kc, 8):
                        k1 = min(kc, k0 + 8)
                        fw.dma("pool", Wds[z][:, k0:k1, :], wsrc3[:, k0:k1, nb * 512:(nb + 1) * 512], writes=[RWds[z]])
                load_half(0)
                if nWd == 2:
                    load_half(1)
                fw.dma("sp", lnb[:, 0, :], Din["lnbc"][l, 2 * which], writes=[Rln])
                fw.dma("sp", lnb[:, 1, :], Din["lnbc"][l, 2 * which + 1], writes=[Rln])
                gbc = sb(ph, "gbc", [128, 2, 1024]); Rgbc = Res("gbc")
                bg = sb(ph, "bg", [128, 1024]); Rbg = Res("bg")
                fw.dma("sp", bg[:], Din["bgbc"][l, :, which, :], writes=[Rbg])
                sec = 2 if which == 0 else 5
                wsrc = Din["w_ada"][l].rearrange("(k p) n -> p k n", p=128)
                for half in range(2):
                    Wg, RWg = load_w([(lambda t: t[:, 0:4096].rearrange("p (k n) -> p k n", k=8), wsrc[:, :, sec * 1024 + half * 512:sec * 1024 + (half + 1) * 512])])
                    Wgv = Wg[:, 0:4096].rearrange("p (k n) -> p k n", k=8)
                    for v in range(2):
                        for k in range(8):
                            MM(bank(v), sRep[:, v, k, :], Wgv[:, k, :], k == 0, k == 7, [RWg, Rcs], [PB[v]])
                        TT("dve", gbc[:, v, half * 512:(half + 1) * 512], bank(v), bg[:, half * 512:(half + 1) * 512], ALU.add, [PB[v], Rbg], [Rgbc])
                cnt = 0
                for nb in range(2):
                    cs = slice(nb * 512, (nb + 1) * 512)
                    Wd = Wds[nb % nWd]; RWd = RWds[nb % nWd]
                    if nWd == 1 and nb == 1:
                        load_half(1)
                    for i in range(NT):
                        v = TILE_VEC[i]
                        bk = 2 + (cnt % 4); tj = cnt % 2; cnt += 1
                        for k in range(kc):
                            MM(bank(bk), lhsT_fn(i, k), Wd[:, k, :], k == 0, k == kc - 1, lres_fn(i, k) + [RWd], [PB[bk]])
                        TT("dve", tmp[tj][:], bank(bk), gbc[:, v, cs], ALU.mult, [PB[bk], Rgbc], [Rtmp[tj]])
                        OP("dve", lambda e: e.scalar_tensor_tensor(out=X[:, i, cs], in0=X[:, i, cs], scalar=float(ALPHA), in1=tmp[tj][:],
                                                                   op0=ALU.mult, op1=ALU.add), [RX[i], Rtmp[tj]], [RX[i]])
                for i in range(NT):
                    for hh in range(2):
                        OP("dve", lambda e: e.bn_stats(out=stat[:, i, hh, :], in_=X[:, i, hh * 512:(hh + 1) * 512]), [RX[i]], [Rstat])
                    OP("dve", lambda e: e.bn_aggr(out=mv[:, i, :], in_=stat[:, i, :, :].rearrange("p a b -> p (a b)")), [Rstat], [Rstat])
                TS("dve", rs[:, :, 0], mv[:, :, 1], EPS, ALU.add, [Rstat], [Rstat])
                ACT(rs[:, :, 0], rs[:, :, 0], AF.Sqrt, [Rstat], [Rstat])
                OP("dve", lambda e: e.reciprocal(out=rs[:, :, 0], in_=rs[:, :, 0]), [Rstat], [Rstat])
                OP("dve", lambda e: e.scalar_tensor_tensor(out=rs[:, :, 1], in0=mv[:, :, 0], scalar=-1.0, in1=rs[:, :, 0],
                                                           op0=ALU.mult, op1=ALU.mult), [Rstat], [Rstat])
                for i in range(NT):
                    ACT(X[:, i, :], X[:, i, :], AF.Identity, [RX[i], Rstat], [RX[i]], scale=rs[:, i, 0:1], bias=rs[:, i, 1:2])
                    TT("pool", X[:, i, :], X[:, i, :], lnb[:, 0, :], ALU.mult, [RX[i], Rln], [RX[i]])
                    TT("pool", X[:, i, :], X[:, i, :], lnb[:, 1, :], ALU.add, [RX[i], Rln], [RX[i]])
                    if out_tile is not None:
                        out_tile(i)
                fw.barrier()
                CH[0] = ch_save

        def layers():
          for l in range(n_layers):
              with contextlib.ExitStack() as ph:
                  wa = [sb(ph, "wa%d" % i, [128, 8, 512], BF16) for i in range(3)]; Rwa = [Res("wa%d" % i) for i in range(3)]
                  bP = sb(ph, "bP", [128, 48]); Rb = Res("bmod")
                  fw.dma("sp", bP[:], Din["b_adaP"][:, l, :], writes=[Rb])
                  wsrc = Din["w_ada"][l].rearrange("(k p) n -> p k n", p=128)
                  pm = bank(6)[:, 0:96]
                  CH[0] = 'm' in os.environ.get('CHP', 'mabfFGMLUS')
                  cbs = [cb for cb in range(12) if cb // 2 not in (2, 5)]
                  for i_ in range(min(3, len(cbs))):
                      fw.dma("pool", wa[i_][:], wsrc[:, :, cbs[i_] * 512:(cbs[i_] + 1) * 512], writes=[Rwa[i_]])
                  for i_, cb in enumerate(cbs):
                      s = i_ % 3
                      sec, half = cb // 2, cb % 2
                      if i_ >= 3 and False:
                          pass
                      for m in range(4):
                          col = (sec * 8 + half * 4 + m) * 2
                          for k in range(8):
                              MM(pm[:, col:col + 2], wa[s][:, k, m * 128:(m + 1) * 128], sTb[:, k, :], k == 0, k == 7,
                                 [Rwa[s], Rcs], [PB[6]])
                      if i_ + 3 < len(cbs):
                          fw.dma("pool", wa[s][:], wsrc[:, :, cbs[i_ + 3] * 512:(cbs[i_ + 3] + 1) * 512], writes=[Rwa[s]])
                  for sec in (0, 1, 3, 4):
                      TT("dve", modP[:, sec, :, :], pm[:, sec * 16:(sec + 1) * 16].rearrange("p (c v) -> p c v", v=2),
                         bl(bP[:, sec * 8:(sec + 1) * 8], 2), ALU.add, [PB[6], Rb], [Rmod])
                  TS("dve", opsc[:, 0, :, :], modP[:, 1, :, :], 1.0, ALU.add, [Rmod], [Rmod])
                  TS("dve", opsc[:, 1, :, :], modP[:, 4, :, :], 1.0, ALU.add, [Rmod], [Rmod])
                  fw.barrier()
              if stop == "mod": fw.dead = True
              DBG("d_modP", modP[:].rearrange("p a b c -> p (a b c)"), [Rmod])

              with contextlib.ExitStack() as L1:
                  QT = sb(L1, "QT", [128, 4, TOK], BF16); RQ = [Res("QT%d" % j) for j in range(4)]; attnT = QT; RattnT = RQ
                  sfT = sb(L1, "sfT", [128, 8, TOK], BF16); RsfT = [Res("sfT%d" % j) for j in range(8)]
                  fourT = sb(L1, "fourT", [128, 4, TOK], BF16); RfourT = [Res("fourT%d" % j) for j in range(4)]
                  Lp = contextlib.ExitStack()
                  lamD = [[sb(Lp, "lam%d%d" % (d_, r), [128, 528]) for r in range(2)] for d_ in range(2)]
                  ffD = [[sb(Lp, "ff%d%d" % (d_, r), [128, 528]) for r in range(2)] for d_ in range(2)]
                  PowP = [sb(Lp, "PowP%d" % r, [128, 2, 16, 17]) for r in range(2)]
                  LH = [sb(Lp, "LH%d" % r, [128, 2, 16, 7]) for r in range(2)]
                  fP = [sb(Lp, "fP%d" % r, [128, 2, 16]) for r in range(2)]
                  RS = Res("ssm")
                  def cmul(dr, di, ar, ai, br, bi, m1, m2):
                      TT("dve", m1, ar, br, ALU.mult, [RS], [RS]); TT("dve", m2, ai, bi, ALU.mult, [RS], [RS])
                      TT("dve", dr, m1, m2, ALU.subtract, [RS], [RS])
                      TT("dve", m1, ar, bi, ALU.mult, [RS], [RS]); TT("dve", m2, ai, br, ALU.mult, [RS], [RS])
                      TT("dve", di, m1, m2, ALU.add, [RS], [RS])
                  def ssm_param_pipeline(st):
                      are, aim, ldt, mag, cc, ss, t1, t2, t3 = [sb(st, "sp%d" % j_, [128, 528]) for j_ in range(9)]
                      for d in range(2):
                          lam = lamD[d]; ff = ffD[d]
                          fw.dma("sp", are[:], Din["lamin"][l, d, 0], writes=[RS])
                          fw.dma("sp", aim[:], Din["lamin"][l, d, 1], writes=[RS])
                          fw.dma("sp", ldt[:], Din["lamin"][l, d, 2], writes=[RS])
                          ACT(ldt[:], ldt[:], AF.Exp, [RS], [RS])
                          TT("dve", t1[:], are[:], ldt[:], ALU.mult, [RS], [RS])
                          ACT(mag[:], t1[:], AF.Exp, [RS], [RS])
                          TT("dve", t2[:], aim[:], ldt[:], ALU.mult, [RS], [RS])
                          ACT(ss[:], t2[:], AF.Sin, [RS], [RS], scale=1.0 / 16)
                          ACT(t1[:], t2[:], AF.Sin, [RS], [RS], scale=1.0 / 32)
                          TT("dve", t1[:], t1[:], t1[:], ALU.mult, [RS], [RS])
                          TS("dve", cc[:], t1[:], -2.0, ALU.mult, [RS], [RS], s2=1.0, op1=ALU.add)
                          for _ in range(4):
                              TT("dve", t1[:], cc[:], cc[:], ALU.mult, [RS], [RS])
                              TT("dve", t2[:], ss[:], ss[:], ALU.mult, [RS], [RS])
                              TT("dve", t3[:], cc[:], ss[:], ALU.mult, [RS], [RS])
                              TT("dve", cc[:], t1[:], t2[:], ALU.subtract, [RS], [RS])
                              TS("dve", ss[:], t3[:], 2.0, ALU.mult, [RS], [RS])
                          TT("dve", lam[0][:], mag[:], cc[:], ALU.mult, [RS], [RS])
                          TT("dve", lam[1][:], mag[:], ss[:], ALU.mult, [RS], [RS])
                          TT("dve", t1[:], are[:], are[:], ALU.mult, [RS], [RS])
                          TT("dve", t2[:], aim[:], aim[:], ALU.mult, [RS], [RS])
                          TT("dve", t1[:], t1[:], t2[:], ALU.add, [RS], [RS])
                          OP("dve", lambda e: e.reciprocal(out=t1[:], in_=t1[:]), [RS], [RS])
                          TS("dve", t2[:], lam[0][:], -1.0, ALU.add, [RS], [RS])
                          TT("dve", t3[:], t2[:], are[:], ALU.mult, [RS], [RS])
                          TT("dve", cc[:], lam[1][:], aim[:], ALU.mult, [RS], [RS])
                          TT("dve", t3[:], t3[:], cc[:], ALU.add, [RS], [RS])
                          TT("dve", ff[0][:], t3[:], t1[:], ALU.mult, [RS], [RS])
                          TT("dve", t3[:], lam[1][:], are[:], ALU.mult, [RS], [RS])
                          TT("dve", cc[:], t2[:], aim[:], ALU.mult, [RS], [RS])
                          TT("dve", t3[:], t3[:], cc[:], ALU.subtract, [RS], [RS])
                          TT("dve", ff[1][:], t3[:], t1[:], ALU.mult, [RS], [RS])
                          pm1 = t1[:, 0:256].rearrange("p (g k) -> p g k", g=16); pm2 = t2[:, 0:256].rearrange("p (g k) -> p g k", g=16)
                          OP("dve", lambda e, d=d: e.memset(PowP[0][:, d, :, 0:1], 1.0), [RS], [RS])
                          OP("dve", lambda e, d=d: e.memset(PowP[1][:, d, :, 0:1], 0.0), [RS], [RS])
                          for r in range(2):
                              TS("dve", PowP[r][:, d, :, 1], lam[r][:, 0:16], 1.0, ALU.mult, [RS], [RS])
                              TS("dve", fP[r][:, d, :], ff[r][:, 0:16], 1.0, ALU.mult, [RS], [RS])
                          for w in (1, 2, 4, 8):
                              br_ = PowP[0][:, d, :, w:w + 1].to_broadcast([128, 16, w]); bi_ = PowP[1][:, d, :, w:w + 1].to_broadcast([128, 16, w])
                              cmul(PowP[0][:, d, :, w + 1:2 * w + 1], PowP[1][:, d, :, w + 1:2 * w + 1],
                                   PowP[0][:, d, :, 1:w + 1], PowP[1][:, d, :, 1:w + 1], br_, bi_, pm1[:, :, 0:w], pm2[:, :, 0:w])
                          for r in range(2):
                              TS("dve", LH[r][:, d, :, 0], PowP[r][:, d, :, 16], 1.0, ALU.mult, [RS], [RS])
                          for m in range(6):
                              cmul(LH[0][:, d, :, m + 1], LH[1][:, d, :, m + 1], LH[0][:, d, :, m], LH[1][:, d, :, m],
                                   LH[0][:, d, :, m], LH[1][:, d, :, m], t1[:, 256:272], t2[:, 256:272])
                  with contextlib.ExitStack() as L2:
                      KT = sb(L2, "KT", [128, 2, 2048], BF16); RKT = Res("KT")
                      Vsb = sb(L2, "Vsb", [128, 16, 2, 2, 128], BF16); RV = Res("V")
                      with contextlib.ExitStack() as ph:
                          alloc_ws(ph)
                          uT = sb(ph, "uT", [128, 8, TOK], BF16); RuT = [Res("uT%d" % i) for i in range(NT)]
                          qn = [sb(ph, "qn%d" % j, [128, 768]) for j in range(2)]; Rqn = [Res("qn0"), Res("qn1")]
                          vf = [sb(ph, "vf%d" % j, [128, 128]) for j in range(2)]; Rvf = [Res("vf0"), Res("vf1")]
                          qkb = [sb(ph, "qkb%d" % j, [128, 768], BF16) for j in range(2)]; Rqkb = [Res("qkb0"), Res("qkb1")]
                          kb2 = [sb(ph, "kb2%d" % j, [128, 128], BF16) for j in range(2)]; Rkb2 = [Res("kb20"), Res("kb21")]
                          ssq = sb(ph, "ssq", [128, 12]); Rssq = Res("ssq")
                          g640 = sb(ph, "g640", [128, 768]); rc = sb(ph, "rc", [128, 8, 32]); rsn = sb(ph, "rsn", [128, 8, 32]); Rcst = Res("cst")
                          rt = [sb(ph, "rt%d" % j, [128, 12, 32]) for j in range(2)]; Rrt = [Res("rt0"), Res("rt1")]
                          win = Din["w_in"][l].rearrange("(k p) n -> p k n", p=128)
                          WA_pre = load_w([(lambda t: t[:, 0:4096].rearrange("p (k n) -> p k n", k=8), win[:, :, 0:512])])
                          WB_pre = load_w([(lambda t: t[:, 0:2048].rearrange("p (k n) -> p k n", k=8), win[:, :, 512:768])])
                          fw.dma("sp", g640[:], Din["g640"][l], writes=[Rcst])
                          fw.dma("sp", rc[:], Din["ropec"], writes=[Rcst])
                          fw.dma("sp", rsn[:], Din["ropes"], writes=[Rcst])
                          fw.dma("pool", KT[:, 0, 1024:1536], Din["ckT"][l], writes=[RKT])
                          fw.dma("pool", KT[0:64, 1, 1024:1536], Din["ckT"][l, 64:128, :], writes=[RKT])
                          fw.dma("pool", KT[64:128, 1, 1024:1536], Din["ckT"][l, 0:64, :], writes=[RKT])
                          OP("dve", lambda e: e.memset(Vsb[:].rearrange("p a b c d -> p (a b c d)"), 1.0), [], [RV])
                          cvv = Din["cv"][l].rearrange("(t p) (h d) -> p t h d", p=128, h=2)
                          for lay in range(2):
                              for kvh_ in range(2):
                                  fw.dma("pool", Vsb[:, 8:12, kvh_, lay, lay * 64:(lay + 1) * 64], cvv[:, :, kvh_, :], writes=[RV])
                          build_uT(uT, RuT, 0)
                          if stop == "ut": fw.dead = True
                          DBG("d_uT", uT[:].rearrange("p a b -> p (a b)"), RuT)

                          def norm_rope(i, pa, pr, nh, gap, outb, Routb, slot):
                              w_ = nh * 64
                              sqv = qn[slot][:, 0:w_]
                              ACT(sqv, pa, AF.Square, pr, [Rqn[slot]])
                              OP("dve", lambda e: e.reduce_sum(out=ssq[:, 0:nh], in_=sqv.rearrange("p (h d) -> p h d", d=64), axis=AX.X), [Rqn[slot]], [Rssq])
                              TS("dve", ssq[:, 0:nh], ssq[:, 0:nh], 1.0 / 64, ALU.mult, [Rssq], [Rssq], s2=EPS, op1=ALU.add)
                              ACT(ssq[:, 0:nh], ssq[:, 0:nh], AF.Sqrt, [Rssq], [Rssq])
                              OP("dve", lambda e: e.reciprocal(out=ssq[:, 0:nh], in_=ssq[:, 0:nh]), [Rssq], [Rssq])
                              qv = qn[slot][:, 0:w_]
                              TT("dve", qv.rearrange("p (h d) -> p h d", d=64), pa.rearrange("p (h d) -> p h d", d=64), bl(ssq[:, 0:nh], 64),
                                 ALU.mult, pr + [Rssq], [Rqn[slot]])
                              TT("dve", qv, qv, gap, ALU.mult, [Rqn[slot], Rcst], [Rqn[slot]])
                              if i >= 8:
                                  CP("act", outb, qv, [Rqn[slot]], [Routb])
                              else:
                                  x4 = qv.rearrange("p (h d two) -> p h d two", d=32, two=2)
                                  o4 = outb.rearrange("p (h d two) -> p h d two", d=32, two=2)
                                  x0, x1 = x4[:, :, :, 0], x4[:, :, :, 1]
                                  cc, ss_ = bm(rc[:, i, :], nh), bm(rsn[:, i, :], nh)
                                  r0, r1 = rt[0][:, 0:nh, :], rt[1][:, 0:nh, :]
                                  TT("dve", r0, x0, cc, ALU.mult, [Rqn[slot], Rcst], [Rrt[0]])
                                  TT("pool", r1, x1, ss_, ALU.mult, [Rqn[slot], Rcst], [Rrt[1]])
                                  TT("dve", o4[:, :, :, 0], r0, r1, ALU.subtract, Rrt, [Routb])
                                  TT("dve", r0, x0, ss_, ALU.mult, [Rqn[slot], Rcst], [Rrt[0]])
                                  TT("pool", r1, x1, cc, ALU.mult, [Rqn[slot], Rcst], [Rrt[1]])
                                  TT("dve", o4[:, :, :, 1], r0, r1, ALU.add, Rrt, [Routb])

                          CH[0] = 'a' in os.environ.get('CHP', 'mabfFGMLUS')
                          WA, RWA = WA_pre; WB, RWB = WB_pre
                          WvA = WA[:, 0:4096].rearrange("p (k n) -> p k n", k=8)
                          WvB = WB[:, 0:2048].rearrange("p (k n) -> p k n", k=8)
                          for i in range(NT):
                              b0 = 2 + 2 * (i % 2); sl = i % 2
                              prs = [PB[b0], PB[b0 + 1]]
                              for k in range(8):
                                  MM(bank(b0), uT[:, k, i * 128:(i + 1) * 128], WvA[:, k, :], k == 0, k == 7, [RuT[i], RWA], [PB[b0]])
                              for k in range(8):
                                  MM(bank(b0 + 1)[:, 0:256], uT[:, k, i * 128:(i + 1) * 128], WvB[:, k, :], k == 0, k == 7, [RuT[i], RWB], [PB[b0 + 1]])
                              vps = bank(b0 + 1)[:, 128:256]
                              norm_rope(i, bank(b0, 2)[:, 0:768], prs, 12, g640[:, 0:768], qkb[sl][:, 0:768], Rqkb[sl], sl)
                              for j in range(4):
                                  src = qkb[sl][:, j * 128:(j + 1) * 128]
                                  OP("pe", lambda e: e.transpose(pst[:, j * 128:(j + 1) * 128], src, ident_b[:]), [Rqkb[sl], Rid], [PT])
                              CP("act", QT[:, :, i * 128:(i + 1) * 128], pst[:, 0:512].rearrange("p (j c) -> p j c", j=4), [PT], RQ)
                              kt = i if i < 8 else 12 + (i - 8)
                              for lay in range(2):
                                  CP("act", Vsb[:, kt, :, lay, lay * 64:(lay + 1) * 64], vps.rearrange("p (h d) -> p h d", h=2), [PB[b0 + 1]], [RV])
                              if i >= 8:
                                  p_, t_ = (i - 8) // 2, (i - 8) % 2
                                  CP("act", vf[sl][:], vps, [PB[b0 + 1]], [Rvf[sl]])
                                  fw.dma("sp", Dout["nk"][p_, l, t_ * 128:(t_ + 1) * 128, :], qn[sl][:, 512:640], reads=[Rqn[sl]], writes=[Rdbg])
                                  fw.dma("sp", Dout["nv"][p_, l, t_ * 128:(t_ + 1) * 128, :], vf[sl][:], reads=[Rvf[sl]], writes=[Rdbg])
                              CP("act", kb2[sl][:, 0:64], qkb[sl][:, 576:640], [Rqkb[sl]], [Rkb2[sl]])
                              CP("act", kb2[sl][:, 64:128], qkb[sl][:, 512:576], [Rqkb[sl]], [Rkb2[sl]])
                              OP("pe", lambda e: e.transpose(pst[:, 512:640], qkb[sl][:, 512:640], ident_b[:]), [Rqkb[sl], Rid], [PT])
                              OP("pe", lambda e: e.transpose(pst[:, 640:768], kb2[sl][:, :], ident_b[:]), [Rkb2[sl], Rid], [PT])
                              kc = i * 128 if i < 8 else 1536 + (i - 8) * 128
                              CP("act", KT[:, 0, kc:kc + 128], pst[:, 512:640], [PT], [RKT])
                              CP("act", KT[:, 1, kc:kc + 128], pst[:, 640:768], [PT], [RKT])
                          WF_pre = load_w([(lambda t: t[:, 0:4096].rearrange("p (k n) -> p k n", k=8), win[:, :, 768:1280])])
                          if stop in ("passA", "passB"): fw.dead = True
                          CH[0] = 'f' in os.environ.get('CHP', 'mabfFGMLUS')
                          for wb in range(2):
                              W, RW = WF_pre
                              if wb == 0:
                                  WF_pre = load_w([(lambda t: t[:, 0:4096].rearrange("p (k n) -> p k n", k=8), win[:, :, 1280:1792])])
                              Wv = W[:, 0:4096].rearrange("p (k n) -> p k n", k=8)
                              for m4 in range(4):
                                  m = wb * 4 + m4
                                  for tb, (c0, cn) in enumerate(BLKS):
                                      bk = (0, 1, 6)[(m * 3 + tb) % 3]
                                      for k in range(8):
                                          MM(bank(bk), Wv[:, k, m4 * 128:(m4 + 1) * 128], uT[:, k, c0:c0 + cn], k == 0, k == 7,
                                             RuT[c0 // 128:(c0 + cn) // 128] + [RW], [PB[bk]])
                                      CP("act", sfT[:, m, c0:c0 + cn], bank(bk), [PB[bk]], [RsfT[m]])
                          DBG("d_sfT", sfT[:].rearrange("p a b -> p (a b)"), RsfT)
                          fw.barrier()
                      if stop == "inproj": fw.dead = True
                      DBG("d_QT", QT[:].rearrange("p a b -> p (a b)"), RQ)
                      DBG("d_KT", KT[:].rearrange("p a b -> p (a b)"), [RKT])
                      CH[0] = 't' in os.environ.get('CHP', 'mabfFGMLUS')
                      with contextlib.ExitStack() as ph:
                          fw.rec = []
                          ssm_param_pipeline(ph)
                          pipe_items = fw.rec if fw.rec is not None else []; fw.rec = None
                          per_unit = (len(pipe_items) + 23) // 24
                          Eb = [sb(ph, "Eb%d" % j, [128, 512], BF16) for j in range(3)]; REb = [Res("Eb%d" % j) for j in range(3)]
                          rsb = [sb(ph, "rsb%d" % j, [128, 512]) for j in range(2)]; Rrsb = [Res("rsb0"), Res("rsb1")]
                          cnt = 0; cnt2 = 0
                          for si, (t0, Ls) in enumerate(SEQS):
                              if si == 0:
                                  qblks = [(0, 512), (512, 512)]; nkt = 12; kt0 = 0; kc0 = 0
                              else:
                                  qblks = [(t0, 256)]; nkt = 2; kt0 = 12 + 2 * (si - 1); kc0 = 1536 + 256 * (si - 1)
                              for h in range(8):
                                  j = h // 2; hh = h % 2; kvh = h // 4; var = 0 if kvh == hh else 1
                                  pv, sm = (slice(0, 64), slice(64, 128)) if hh == 0 else (slice(64, 128), slice(0, 64))
                                  for (q0, qn) in qblks:
                                      pvb = 3 + (cnt2 % 2); cnt2 += 1
                                      def S_mm(kt, sbk):
                                          MM(bank(sbk)[:, 0:qn], KT[hh * 64:(hh + 1) * 64, var, kc0 + kt * 128:kc0 + (kt + 1) * 128],
                                             QT[hh * 64:(hh + 1) * 64, j, q0:q0 + qn], True, True, [RKT, RQ[j]], [PB[sbk]], chain=False)
                                      slots = [(cnt + i_) % 3 for i_ in range(nkt)]; cnt += nkt
                                      S_mm(0, slots[0])
                                      for kt in range(nkt):
                                          sbk = slots[kt]; eb = sbk
                                          if kt + 1 < nkt:
                                              S_mm(kt + 1, slots[kt + 1])
                                          ACT(Eb[eb][:, 0:qn], bank(sbk)[:, 0:qn], AF.Exp, [PB[sbk]], [REb[eb]], scale=0.125)
                                          MM(bank(pvb)[:, 0:qn], Vsb[:, kt0 + kt, kvh, hh, :], Eb[eb][:, 0:qn], kt == 0, kt == nkt - 1,
                                             [RV, REb[eb]], [PB[pvb]], chain=False)
                                      rb = cnt2 % 2
                                      ACT(rsb[rb][sm, 0:qn], bank(pvb)[sm, 0:qn], AF.Ln, [PB[pvb]], [Rrsb[rb]])
                                      ACT(rsb[rb][sm, 0:qn], rsb[rb][sm, 0:qn], AF.Exp, [Rrsb[rb]], [Rrsb[rb]], scale=-1.0)
                                      TT("dve", attnT[pv, j, q0:q0 + qn], bank(pvb)[pv, 0:qn], rsb[rb][sm, 0:qn], ALU.mult,
                                         [PB[pvb], Rrsb[rb]], [RattnT[j]])
                                      fw.replay(pipe_items[:per_unit]); del pipe_items[:per_unit]
                          fw.replay(pipe_items); del pipe_items[:]
                          fw.barrier()
                  DBG("d_attnT", attnT[:].rearrange("p a b -> p (a b)"), RattnT)
                  CH[0] = 'F' in os.environ.get('CHP', 'mabfFGMLUS')
                  with contextlib.ExitStack() as L3:
                    with contextlib.ExitStack() as ph:
                        cs128 = sb(ph, "cs128", [128, 256], BF16); cl = sb(ph, "cl", [128, 8, 1024], BF16); sl_ = sb(ph, "sl", [128, 8, 1024], BF16)
                        clp = sb(ph, "clp", [128, 2, 256], BF16); slp = sb(ph, "slp", [128, 2, 256], BF16); Rft = Res("ftab")
                        Pfm = sb(ph, "Pfm", [128, NT, 1024], BF16); RPfm = [Res("Pfm%d" % i) for i in range(NT)]
                        fw.dma("pool", cs128[:], Din["cs128"], writes=[Rft])
                        for k in range(8):
                            fw.dma("pool", cl[:, k, :], Din["cl1024"][:, k, :], writes=[Rft])
                            fw.dma("pool", sl_[:, k, :], Din["sl1024"][:, k, :], writes=[Rft])
                        fw.dma("pool", clp[:], Din["cl256"], writes=[Rft])
                        fw.dma("pool", slp[:], Din["sl256"], writes=[Rft])
                        for i in range(NT):
                            b0 = 2 + 2 * (i % 2)
                            for g in range(4):
                                MM(bank(b0 + g // 2)[:, (g % 2) * 256:(g % 2 + 1) * 256], sfT[:, 4 + g, i * 128:(i + 1) * 128], cs128[:], True, True,
                                   [RsfT[4 + g], Rft], [PB[b0 + g // 2]])
                            CP("act", Pfm[:, i, :], bank(b0, 2), [PB[b0], PB[b0 + 1]], [RPfm[i]])
                        cnt = 0
                        for si, (t0, Ls) in enumerate(SEQS):
                            ntl = Ls // 128; tb0 = t0 // 128
                            tc_, ts_ = (cl, sl_) if si == 0 else (clp, slp)
                            for g in range(4):
                                for lb in range(0, Ls, 512):
                                    n = min(512, Ls - lb)
                                    bk = (0, 1, 6)[cnt % 3]; cnt += 1
                                    for k in range(ntl):
                                        MM(bank(bk)[:, 0:n], Pfm[:, tb0 + k, g * 256:g * 256 + 128], tc_[:, k, lb:lb + n], k == 0, False,
                                           [RPfm[tb0 + k], Rft], [PB[bk]])
                                        MM(bank(bk)[:, 0:n], Pfm[:, tb0 + k, g * 256 + 128:g * 256 + 256], ts_[:, k, lb:lb + n], False, k == ntl - 1,
                                           [RPfm[tb0 + k], Rft], [PB[bk]])
                                    CP("act", fourT[:, g, t0 + lb:t0 + lb + n], bank(bk)[:, 0:n], [PB[bk]], [RfourT[g]])
                        fw.barrier()
                    DBG("d_fourT", fourT[:].rearrange("p a b -> p (a b)"), RfourT)
                    if stop == "four": fw.dead = True
                    CH[0] = 'S' in os.environ.get('CHP', 'mabfFGMLUS')
                    with contextlib.ExitStack() as S:
                        Hb = sb(S, "Hb", [128, 2, 2, 16, NS], BF16)
                        h0 = sb(S, "h0", [128, 2, 2, 16]); stg = sb(S, "stg", [128, 16, 2])
                        RHb = Res("Hb"); Rstg = Res("stg")
                        fw.dma("sp", h0[:], Din["h0"][:, l], writes=[RS])
                        for d in range(2):
                            with contextlib.ExitStack() as SD:
                                lam = lamD[d]; ff = ffD[d]
                                A = [sb(SD, "A%d" % r, [128, 16, NS]) for r in range(2)]
                                for r in range(2):
                                    OP("dve", lambda e: e.memset(A[r][:].rearrange("p a b -> p (a b)"), 0.0), [RS], [RS])
                                with contextlib.ExitStack() as SW:
                                    W = [sb(SW, "W%d" % r, [128, 16, 128]) for r in range(2)]
                                    WT2 = [[sb(SW, "WT%d_%d" % (z, r), [128, 16, 128], BF16) for r in range(2)] for z in range(2)]
                                    wm = [sb(SW, "wm%d" % r, [128, 8, 128]) for r in range(3)]
                                    BT = [sb(SW, "BT%d" % r, [128, 128]) for r in range(2)]
                                    Lw = [sb(SW, "Lw%d" % r, [128, 128]) for r in range(2)]
                                    RWT2 = [Res("WTa"), Res("WTb")]
                                    def build_W(q):
                                        fs = slice(16 + q * 128, 16 + (q + 1) * 128)
                                        for r in range(2):
                                            fw.dma("sp", BT[r][:], Din["BbdT"][l, d, r, :, q, :], writes=[RS])
                                            ACT(Lw[r][:], lam[r][:, fs], AF.Identity, [RS], [RS])
                                        cmul(W[0][:, 0, :], W[1][:, 0, :], ff[0][:, fs], ff[1][:, fs], BT[0][:], BT[1][:], wm[0][:, 0, :], wm[1][:, 0, :])
                                        for w in (1, 2, 4, 8):
                                            cmul(W[0][:, w:2 * w, :], W[1][:, w:2 * w, :], bm(Lw[0][:], w), bm(Lw[1][:], w),
                                                 W[0][:, 0:w, :], W[1][:, 0:w, :], wm[0][:, 0:w, :], wm[1][:, 0:w, :])
                                            if w < 8:
                                                TT("dve", wm[0][:, 0, :], Lw[0][:], Lw[0][:], ALU.mult, [RS], [RS])
                                                TT("dve", wm[1][:, 0, :], Lw[1][:], Lw[1][:], ALU.mult, [RS], [RS])
                                                TT("dve", wm[2][:, 0, :], Lw[0][:], Lw[1][:], ALU.mult, [RS], [RS])
                                                TT("dve", Lw[0][:], wm[0][:, 0, :], wm[1][:, 0, :], ALU.subtract, [RS], [RS])
                                                TS("dve", Lw[1][:], wm[2][:, 0, :], 2.0, ALU.mult, [RS], [RS])
                                        for r in range(2):
                                            ACT(WT2[q % 2][r][:].rearrange("p a b -> p (a b)"), W[r][:].rearrange("p a b -> p (a b)"), AF.Identity, [RS], [RWT2[q % 2]])
                                    def run_S(q):
                                        WTq = WT2[q % 2]; RWTq = RWT2[q % 2]
                                        Sps = bank(0, 2).rearrange("p (b r c) -> p b r c", b=4, r=2)
                                        for b in range(4):
                                            sv = sfT[32 * b:32 * b + 32, q, :].rearrange("p (c j) -> p c j", j=16)
                                            for r in range(2):
                                                for k in range(16):
                                                    j = 15 - k if d == 0 else k
                                                    MM(Sps[:, b, r, 0:96], WTq[r][32 * b:32 * b + 32, k, :], sv[:, :, j], k == 0, k == 15,
                                                       [RWTq, RsfT[q]], [PB[0], PB[1]], tp=(32 * b, 0))
                                        for r in range(2):
                                            for s_i in range(3):
                                                n = SCH[s_i]; c0 = (0, 64, 80)[s_i]; lo = SOFF[s_i] + (1 if d == 0 else 0)
                                                ACT(A[r][:, 4 * q:4 * q + 4, lo:lo + n], Sps[:, :, r, c0:c0 + n], AF.Identity, [PB[0], PB[1]], [RS])
                                    build_W(0)
                                    for q in range(4):
                                        if q + 1 < 4:
                                            build_W(q + 1)
                                        run_S(q)
                                RSp = Res("ssm_pool"); NPD = 10
                                for r in range(2):
                                    ACT(A[r][:, :, 0 if d == 0 else 64], h0[:, d, r, :], AF.Identity, [RS], [RS, RSp])
                                with contextlib.ExitStack() as SC:
                                    sm = [sb(SC, "sm%d" % r, [128, 16, NS]) for r in range(3)]
                                    for m in range(7):
                                        dd = 1 << m
                                        groups = []
                                        if dd < 65:
                                            groups.append((lambda t_, ps_, lo, hi: t_[:, ps_, lo:hi], 65, None))
                                        if dd < 17:
                                            groups.append((lambda t_, ps_, lo, hi: t_[:, ps_, 65:99].rearrange("p g (s c) -> p g s c", s=2)[:, :, :, lo:hi], 17, 2))
                                        for view, n, ns in groups:
                                            cntn = n - dd
                                            (dlo, dhi, slo, shi) = (dd, n, 0, cntn) if d == 0 else (0, cntn, dd, n)
                                            for eng_, ps_, Rr in (("dve", slice(0, NPD), RS), ("pool", slice(NPD, 16), RSp)):
                                                npz = ps_.stop - ps_.start
                                                if ns is None:
                                                    Lr = bl(LH[0][:, d, ps_, m], cntn); Li = bl(LH[1][:, d, ps_, m], cntn)
                                                    tv = lambda t_: t_[:, ps_, 0:cntn]
                                                else:
                                                    Lr = LH[0][:, d, ps_, m].unsqueeze(2).unsqueeze(3).to_broadcast([128, npz, 2, cntn])
                                                    Li = LH[1][:, d, ps_, m].unsqueeze(2).unsqueeze(3).to_broadcast([128, npz, 2, cntn])
                                                    tv = lambda t_: t_[:, ps_, 0:2 * cntn].rearrange("p g (s c) -> p g s c", s=2)
                                                sr, si = view(A[0], ps_, slo, shi), view(A[1], ps_, slo, shi)
                                                dr, di = view(A[0], ps_, dlo, dhi), view(A[1], ps_, dlo, dhi)
                                                m1, m2, m3 = tv(sm[0]), tv(sm[1]), tv(sm[2])
                                                TT(eng_, m1, sr, Lr, ALU.mult, [Rr], [Rr]); TT(eng_, m2, si, Li, ALU.mult, [Rr], [Rr])
                                                TT(eng_, m1, m1, m2, ALU.subtract, [Rr], [Rr])
                                                TT(eng_, m2, si, Lr, ALU.mult, [Rr], [Rr]); TT(eng_, m3, sr, Li, ALU.mult, [Rr], [Rr])
                                                TT(eng_, m2, m2, m3, ALU.add, [Rr], [Rr])
                                                TT(eng_, dr, dr, m1, ALU.add, [Rr], [Rr]); TT(eng_, di, di, m2, ALU.add, [Rr], [Rr])
                                for r in range(2):
                                    ACT(Hb[:, d, r, :, :], A[r][:], AF.Identity, [RS, RSp], [RHb])
                                for s_i in (1, 2):
                                    idx = SOFF[s_i] + (16 if d == 0 else 0)
                                    for r in range(2):
                                        ACT(stg[:, :, r], A[r][:, :, idx], AF.Identity, [RS, RSp], [Rstg])
                                    fw.dma("sp", Dout["nst"][s_i - 1, l, d].rearrange("(pair g2) p r -> (g2 p) pair r", g2=2), stg[:], reads=[Rstg], writes=[Rdbg])
                                DBG("d_A%d" % d, A[0][:].rearrange("p a b -> p (a b)"), [RS, RSp])
                                fw.barrier()
                        with contextlib.ExitStack() as SQ:
                            CL2 = [[[sb(SQ, "CL%d%d%d" % (z, d, r), [128, 4, 17, 32], BF16) for r in range(2)] for d in range(2)] for z in range(2)]
                            Kblk = [sb(SQ, "Kblk%d" % d, [128, 16, 128], BF16) for d in range(2)]
                            Bb2 = [[[sb(SQ, "Bb%d%d%d" % (z, d, r), [128, 4, 32], BF16) for r in range(2)] for d in range(2)] for z in range(2)]
                            Dbl = sb(SQ, "Dbl", [128, 4, 128], BF16)
                            cm = [sb(SQ, "cm%d" % j, [128, 2, 17, 32]) for j in range(2)]
                            Cl = [sb(SQ, "Cl%d" % r, [128, 4, 32]) for r in range(2)]
                            nCl = sb(SQ, "nCl", [128, 4, 32])
                            Bl = [sb(SQ, "Bl%d" % r, [128, 4, 32]) for r in range(2)]
                            bt = [sb(SQ, "bt%d" % r, [128, 4, 32]) for r in range(2)]
                            gt = [sb(SQ, "gt%d" % r, [128, 512]) for r in range(2)]; Rgt = [Res("gt0"), Res("gt1")]
                            RCL2 = [Res("CLa"), Res("CLb")]; RK = Res("Kblk"); RDbl = Res("Dbl")
                            ydb = sb(SQ, "ydb", [128, 512]); Rydb = Res("ydb")
                            fw.dma("pool", Dbl[:], Din["Dblk"][l], writes=[RDbl])

                            def build_CL(q):
                                z = q % 2; CL = CL2[z]; Bb = Bb2[z]; RCL = RCL2[z]
                                ps4 = slice(4 * q, 4 * q + 4)
                                for d in range(2):
                                    for r in range(2):
                                        fw.dma("sp", Cl[r][:], Din["CbdP"][l, d, r, :, ps4, :], writes=[RS])
                                        fw.dma("sp", Bl[r][:], Din["BbdP"][l, d, r, :, ps4, :], writes=[RS])
                                    TS("dve", nCl[:], Cl[0][:], -1.0, ALU.mult, [RS], [RS])
                                    for hp in range(2):
                                        pp = slice(2 * hp, 2 * hp + 2); pq = slice(4 * q + 2 * hp, 4 * q + 2 * hp + 2)
                                        Pr = PowP[0][:, d, pq, :].unsqueeze(3).to_broadcast([128, 2, 17, 32])
                                        Pi = PowP[1][:, d, pq, :].unsqueeze(3).to_broadcast([128, 2, 17, 32])
                                        Cr = Cl[0][:, pp, :].unsqueeze(2).to_broadcast([128, 2, 17, 32])
                                        Ci = Cl[1][:, pp, :].unsqueeze(2).to_broadcast([128, 2, 17, 32])
                                        nCr = nCl[:, pp, :].unsqueeze(2).to_broadcast([128, 2, 17, 32])
                                        TT("dve", cm[0][:], Pr, Cr, ALU.mult, [RS], [RS]); TT("dve", cm[1][:], Pi, Ci, ALU.mult, [RS], [RS])
                                        TT("dve", CL[d][0][:, pp, :, :], cm[0][:], cm[1][:], ALU.subtract, [RS], [RCL])
                                        TT("dve", cm[0][:], Pr, Ci, ALU.mult, [RS], [RS]); TT("dve", cm[1][:], Pi, nCr, ALU.mult, [RS], [RS])
                                        TT("dve", CL[d][1][:, pp, :, :], cm[1][:], cm[0][:], ALU.subtract, [RS], [RCL])
                                    fr = bl(fP[0][:, d, ps4], 32); fi = bl(fP[1][:, d, ps4], 32)
                                    TT("dve", bt[0][:], fr, Bl[0][:], ALU.mult, [RS], [RS]); TT("dve", bt[1][:], fi, Bl[1][:], ALU.mult, [RS], [RS])
                                    TT("dve", Bb[d][0][:], bt[0][:], bt[1][:], ALU.subtract, [RS], [RCL])
                                    TT("dve", bt[0][:], fr, Bl[1][:], ALU.mult, [RS], [RS]); TT("dve", bt[1][:], fi, Bl[0][:], ALU.mult, [RS], [RS])
                                    TT("dve", Bb[d][1][:], bt[0][:], bt[1][:], ALU.add, [RS], [RCL])

                            def build_K(q):
                                z = q % 2; CL = CL2[z]; Bb = Bb2[z]; RCL = RCL2[z]
                                for d in range(2):
                                    kb = d
                                    for b in range(4):
                                        MM(bank(kb)[32 * b:32 * b + 32, :], Bb[d][0][:, b, :], CL[d][0][:, b, 0:16, :].rearrange("p t n -> p (t n)"), True, False,
                                           [RCL], [PB[kb]], tp=(0, 32 * b))
                                        MM(bank(kb)[32 * b:32 * b + 32, :], Bb[d][1][:, b, :], CL[d][1][:, b, 0:16, :].rearrange("p t n -> p (t n)"), False, True,
                                           [RCL], [PB[kb]], tp=(0, 32 * b))
                                    for cb in range(4):
                                        ACT(Kblk[d][:, :, 32 * cb:32 * cb + 32], bank(kb).rearrange("p (t n) -> p t n", n=32), AF.Identity,
                                            [PB[kb], Rmask], [RK], scale=maskP[:, cb:cb + 1])

                            def run_y(q):
                                z = q % 2; CL = CL2[z]; RCL = RCL2[z]
                                for tb, (c0, cn) in enumerate(BLKS):
                                    yb = 3 + tb
                                    Yb = bank(yb).rearrange("p (c j) -> p c j", j=16)
                                    sblk = sfT[:, q, c0:c0 + cn].rearrange("p (c j) -> p c j", j=16)
                                    MM(bank(yb), Dbl[:, q, :], sfT[:, q, c0:c0 + cn], True, False, [RDbl, RsfT[q]], [PB[yb]])
                                    for d in range(2):
                                        for t in range(16):
                                            if d == 0:
                                                o_, r_ = Yb[:, :, t:16], sblk[:, :, 0:16 - t]
                                            else:
                                                o_, r_ = Yb[:, :, 0:16 - t], sblk[:, :, t:16]
                                            MM(o_, Kblk[d][:, t, :], r_, False, False, [RK, RsfT[q]], [PB[yb]])
                                    GC = os.environ.get('GC', '1') == '1'
                                    for d in range(2):
                                        for j in range(16):
                                            tt = j + 1 if d == 0 else 16 - j
                                            sh = 0 if d == 0 else 1
                                            for r in range(2):
                                                for b in range(4):
                                                    gfirst = (b == 0 and r == 0 and j == 0 and d == 0)
                                                    if tb < 2:
                                                        rhs = Hb[:, d, r, 4 * q + b, 32 * tb + sh:32 * tb + sh + 32]
                                                        o_ = Yb[32 * b:32 * b + 32, :, j]
                                                    else:
                                                        rhs = Hb[:, d, r, 4 * q + b, 65:99].rearrange("p (s c) -> p s c", s=2)[:, :, sh:sh + 16]
                                                        o_ = bank(yb).rearrange("p (s c j) -> p s c j", s=2, j=16)[32 * b:32 * b + 32, :, :, j]
                                                    last = (d == 1 and b == 3 and j == 15 and r == 1)
                                                    MM(o_, CL[d][r][:, b, tt, :], rhs, False, last, [RCL, RHb], [PB[yb]], tp=(0, 32 * b),
                                                       chain=(("gfirst" if gfirst else "gnext") if GC else False))
                                    if ("d_y%d_%d" % (q, tb)) in Ddbg:
                                        ACT(ydb[:], bank(yb), AF.Identity, [PB[yb]], [Rydb])
                                        DBG("d_y%d_%d" % (q, tb), ydb[:], [Rydb])
                                    g_ = tb % 2
                                    ACT(gt[g_][:], bank(yb), AF.Square, [PB[yb]], [Rgt[g_]], scale=float(0.044715 ** 0.5))
                                    OP("dve", lambda e: e.scalar_tensor_tensor(out=gt[g_][:], in0=gt[g_][:], scalar=1.0, in1=bank(yb), op0=ALU.add, op1=ALU.mult),
                                       [Rgt[g_], PB[yb]], [Rgt[g_]])
                                    ACT(gt[g_][:], gt[g_][:], AF.Sigmoid, [Rgt[g_]], [Rgt[g_]], scale=1.5957691216)
                                    TT("dve", sfT[:, q, c0:c0 + cn], gt[g_][:], bank(yb), ALU.mult, [Rgt[g_], PB[yb]], [RsfT[q]])
                            build_CL(0); build_K(0)
                            for q in range(4):
                                if q + 1 < 4:
                                    build_CL(q + 1)
                                run_y(q)
                                if q + 1 < 4:
                                    build_K(q + 1)
                            fw.barrier()
                    Lp.close()
                    ssmT = sb(L3, "ssmT", [128, 4, TOK], BF16); RssmT = [Res("ssmT%d" % j) for j in range(4)]
                    CH[0] = 'G' in os.environ.get('CHP', 'mabfFGMLUS')
                    with contextlib.ExitStack() as ph:
                        alloc_ws(ph)
                        bgl = sb(ph, "bgl", [128, 4]); Rbgl = Res("bgl"); sg = [sb(ph, "sg%d" % j, [128, 512]) for j in range(2)]; Rsg = [Res("sg0"), Res("sg1")]
                        fw.dma("sp", bgl[:], Din["bglu"][:, l, :], writes=[Rbgl])
                        Wg, RWg = load_w([(lambda t: t[:, 0:2048].rearrange("p (k n) -> p k n", k=4), Din["w_glu"][l].rearrange("(k p) n -> p k n", p=128))])
                        Wgv = Wg[:, 0:2048].rearrange("p (k n) -> p k n", k=4)
                        cnt = 0
                        for m in range(4):
                            for tb, (c0, cn) in enumerate(BLKS):
                                bk = (0, 1, 2)[cnt % 3]; sj = cnt % 2; cnt += 1
                                for k in range(4):
                                    MM(bank(bk), Wgv[:, k, m * 128:(m + 1) * 128], sfT[:, k, c0:c0 + cn], k == 0, k == 3, [RWg, RsfT[k]], [PB[bk]])
                                ACT(sg[sj][:], bank(bk), AF.Sigmoid, [PB[bk], Rbgl], [Rsg[sj]], bias=bgl[:, m:m + 1])
                                TT("dve", ssmT[:, m, c0:c0 + cn], sg[sj][:], sfT[:, m, c0:c0 + cn], ALU.mult, [Rsg[sj], RsfT[m]], [RssmT[m]])
                        fw.barrier()
                    DBG("d_gy", sfT[:, 0:4, :].rearrange("p a b -> p (a b)"), RsfT[0:4])
                    DBG("d_ssmT", ssmT[:].rearrange("p a b -> p (a b)"), RssmT)
                    if stop == "ssm": fw.dead = True
                    CH[0] = 'M' in os.environ.get('CHP', 'mabfFGMLUS')
                    with contextlib.ExitStack() as M:
                        mergedT = sb(M, "mergedT", [128, 8, TOK], BF16); Rmg = [Res("mg%d" % j) for j in range(8)]
                        with contextlib.ExitStack() as ph:
                            alloc_ws(ph)
                            uT = sb(ph, "uT", [128, 8, TOK], BF16); RuT = [Res("uT%d" % i) for i in range(NT)]
                            gs = [sb(ph, "gs%d" % j, [128, 512]) for j in range(3)]; Rgs = [Res("gs%d" % j) for j in range(3)]
                            ac = [sb(ph, "ac%d" % j, [128, 512]) for j in range(2)]; Rac = [Res("ac0"), Res("ac1")]
                            build_uT(uT, RuT, 0)
                            win = Din["w_in"][l].rearrange("(k p) n -> p k n", p=128)
                            wbr = [Din[nm][l].rearrange("(k p) n -> p k n", p=128) for nm in ("w_br_attn", "w_br_ssm", "w_br_four")]
                            srcs = [(attnT, RattnT), (ssmT, RssmT), (fourT, RfourT)]
                            Wg2 = [sb(ph, "Wg2_%d" % z, [128, 3072], BF16) for z in range(2)]; RWg2 = [Res("Wg2a"), Res("Wg2b")]
                            Wb2 = [sb(ph, "Wb2_%d" % z, [128, 1536], BF16) for z in range(2)]; RWb2 = [Res("Wb2a"), Res("Wb2b")]
                            def load_c(c):
                                z = c % 2
                                for g in range(3):
                                    fw.dma("pool", Wg2[z][:, g * 1024:(g + 1) * 1024].rearrange("p (k n) -> p k n", k=8),
                                           win[:, :, 1792 + g * 1024 + c * 128:1792 + g * 1024 + (c + 1) * 128], writes=[RWg2[z]])
                                for x in range(3):
                                    fw.dma("pool", Wb2[z][:, x * 512:(x + 1) * 512].rearrange("p (k n) -> p k n", k=4),
                                           wbr[x][:, :, c * 128:(c + 1) * 128], writes=[RWb2[z]])
                            load_c(0)
                            for c in range(8):
                                if c + 1 < 8:
                                    load_c(c + 1)
                                Wg, RWg, Wb, RWb = Wg2[c % 2], RWg2[c % 2], Wb2[c % 2], RWb2[c % 2]
                                for tb, (c0, cn) in enumerate(BLKS):
                                    for g in range(3):
                                        Wgv = Wg[:, g * 1024:(g + 1) * 1024].rearrange("p (k n) -> p k n", k=8)
                                        for k in range(8):
                                            MM(bank(g), Wgv[:, k, :], uT[:, k, c0:c0 + cn], k == 0, k == 7, RuT[c0 // 128:(c0 + cn) // 128] + [RWg], [PB[g]])
                                        ACT(gs[g][:], bank(g), AF.Sigmoid, [PB[g]], [Rgs[g]])
                                    for x in range(3):
                                        Wbv = Wb[:, x * 512:(x + 1) * 512].rearrange("p (k n) -> p k n", k=4)
                                        st_, Rst = srcs[x]
                                        for k in range(4):
                                            MM(bank(3 + x), Wbv[:, k, :], st_[:, k, c0:c0 + cn], k == 0, k == 3, [Rst[k], RWb], [PB[3 + x]])
                                    TT("dve", ac[0][:], gs[0][:], bank(3), ALU.mult, [Rgs[0], PB[3]], [Rac[0]])
                                    TT("dve", ac[1][:], gs[1][:], bank(4), ALU.mult, [Rgs[1], PB[4]], [Rac[1]])
                                    TT("dve", ac[0][:], ac[0][:], ac[1][:], ALU.add, Rac, [Rac[0]])
                                    TT("dve", ac[1][:], gs[2][:], bank(5), ALU.mult, [Rgs[2], PB[5]], [Rac[1]])
                                    TT("dve", mergedT[:, c, c0:c0 + cn], ac[0][:], ac[1][:], ALU.add, Rac, [Rmg[c]])
                            fw.barrier()
                        DBG("d_mergedT", mergedT[:].rearrange("p a b -> p (a b)"), Rmg)
                        if stop == "merge": fw.dead = True
                        layer_norm_residual(l, 0, 8, lambda i, k: mergedT[:, k, i * 128:(i + 1) * 128], lambda i, k: [Rmg[k]],
                                            Din["w_out"][l].rearrange("(k p) n -> p k n", p=128))
              if "d_x1" in Ddbg:
                  for i in range(NT):
                      fw.dma("sp", Ddbg["d_x1"][i * 128:(i + 1) * 128, :], X[:, i, :], reads=[RX[i]], writes=[Rdbg])
              if stop == "ln1": fw.dead = True
              CH[0] = 'U' in os.environ.get('CHP', 'mabfFGMLUS')
              with contextlib.ExitStack() as Fs:
                  actT = sb(Fs, "actT", [128, 22, TOK], BF16); Ract = [Res("act%d" % j) for j in range(22)]
                  with contextlib.ExitStack() as ph:
                      alloc_ws(ph)
                      uT = sb(ph, "uT", [128, 8, TOK], BF16); RuT = [Res("uT%d" % i) for i in range(NT)]
                      cvp = sb(ph, "cvp", [128, 44, 4]); Rcv = Res("cvp")
                      hc = [sb(ph, "hc%d" % j, [128, TOK]) for j in range(2)]; Rhc = [Res("hc0"), Res("hc1")]
                      fw.dma("sp", cvp[:], Din["convp"][:, l], writes=[Rcv])
                      wup = Din["w_up"][l].rearrange("(k p) n -> p k n", p=128)
                      def load_up(bi):
                          return load_w([(lambda t: t[:, 0:4096].rearrange("p (k n) -> p k n", k=8), wup[:, :, bi * 512:(bi + 1) * 512])])
                      nxt_up = load_up(0)
                      build_uT(uT, RuT, 1)
                      for ch in range(44):
                          if ch % 4 == 0:
                              W, RW = nxt_up
                              if ch // 4 + 1 < 11:
                                  nxt_up = load_up(ch // 4 + 1)
                              Wv = W[:, 0:4096].rearrange("p (k n) -> p k n", k=8)
                          cw = ch % 4; pb0 = 3 * (ch % 2); hj = ch % 2
                          prs = [PB[pb0], PB[pb0 + 1], PB[pb0 + 2]]
                          for tb, (c0, cn) in enumerate(BLKS):
                              for k in range(8):
                                  MM(bank(pb0 + tb), Wv[:, k, cw * 128:(cw + 1) * 128], uT[:, k, c0:c0 + cn], k == 0, k == 7,
                                     RuT[c0 // 128:(c0 + cn) // 128] + [RW], [prs[tb]])
                          hp = bank(pb0, 3); h_ = hc[hj]
                          ACT(h_[:], hp, AF.Identity, prs + [Rcv], [Rhc[hj]], scale=cvp[:, ch, 1:2], bias=cvp[:, ch, 3:4])
                          def stt(o_, i0, sc, i1):
                              OP("dve", lambda e: e.scalar_tensor_tensor(out=o_, in0=i0, scalar=sc, in1=i1, op0=ALU.mult, op1=ALU.add),
                                 prs + [Rcv, Rhc[hj]], [Rhc[hj]])
                          stt(h_[:, 1:1024], hp[:, 0:1023], cvp[:, ch, 0:1], h_[:, 1:1024])
                          stt(h_[:, 0:1023], hp[:, 1:1024], cvp[:, ch, 2:3], h_[:, 0:1023])
                          h3 = h_[:, 1024:1536].rearrange("p (s c) -> p s c", s=2); p3 = hp[:, 1024:1536].rearrange("p (s c) -> p s c", s=2)
                          stt(h3[:, :, 1:256], p3[:, :, 0:255], cvp[:, ch, 0:1], h3[:, :, 1:256])
                          stt(h3[:, :, 0:255], p3[:, :, 1:256], cvp[:, ch, 2:3], h3[:, :, 0:255])
                          if ch < 22:
                              ACT(actT[:, ch, :], h_[:], AF.Silu, [Rhc[hj]], [Ract[ch]])
                          else:
                              TT("dve", actT[:, ch - 22, :], actT[:, ch - 22, :], h_[:], ALU.mult, [Ract[ch - 22], Rhc[hj]], [Ract[ch - 22]])
                      fw.barrier()
                  DBG("d_actT", actT[:].rearrange("p a b -> p (a b)"), Ract)
                  if stop == "ffn": fw.dead = True
                  def y_out(i):
                      fw.dma("sp", Dout["y"][i * 128:(i + 1) * 128, :], X[:, i, :], reads=[RX[i]], writes=[Rdbg])
                  layer_norm_residual(l, 1, 22, lambda i, k: actT[:, k, i * 128:(i + 1) * 128], lambda i, k: [Ract[k]],
                                      Din["w_down"][l].rearrange("(k p) n -> p k n", p=128),
                                      out_tile=(y_out if l == n_layers - 1 else None))
        layers()
        if fw.dead:
            fw.dead = False
            for i in range(NT):
                fw.dma("sp", Dout["y"][i * 128:(i + 1) * 128, :], X[:, i, :], reads=[RX[i]], writes=[Rdbg])
        fw.finish()
    return nc


def kernel(**inputs):
    inp = {k: np.asarray(v) for k, v in inputs.items()}
    S = prep_shared(inp)
    nc = build()
    in_maps = []
    for cid in range(8):
        m = dict(S); m.update(prep_core(inp, cid))
        in_maps.append(m)
    res = run_bass_kernel_spmd(nc, in_maps, core_ids=list(range(8)))
    y_p = np.zeros((16, 256, 1024), np.float32); y_s = np.zeros((8, 1024, 1024), np.float32)
    nk = np.zeros((16, 2, 256, 2, 64), np.float32); nv = np.zeros((16, 2, 256, 2, 64), np.float32)
    nst = np.zeros((16, 2, 2, 32, 64, 2), np.float32)
    for cid in range(8):
        r = res.results[cid]
        y_s[cid] = r["y"][0:1024]
        y_p[2 * cid:2 * cid + 2] = r["y"][1024:1536].reshape(2, 256, 1024)
        nk[2 * cid:2 * cid + 2] = r["nk"].reshape(2, 2, 256, 2, 64)
        nv[2 * cid:2 * cid + 2] = r["nv"].reshape(2, 2, 256, 2, 64)
        nst[2 * cid:2 * cid + 2] = r["nst"]
    return (y_p, y_s, nk, nv, nst)
```

```python
import contextlib, os
SK = os.environ.get("SK", "")
import numpy as np
import concourse.bass as bass
import concourse.mybir as mybir
from concourse.bass_utils import run_bass_kernel_spmd

F32 = mybir.dt.float32; BF16 = mybir.dt.bfloat16
AF = mybir.ActivationFunctionType; ALU = mybir.AluOpType; AX = mybir.AxisListType

D_MODEL = 1024; DEPTH = 2; D_IN = 4864; D_FF = 2816
ALPHA = (2 * DEPTH) ** 0.25
EPS = 1e-6
NT = 12; TOK = 1536; T = 16
SEQS = [(0, 1024), (1024, 256), (1280, 256)]
BLKS = [(0, 512), (512, 512), (1024, 512)]
TILE_VEC = [0] * 8 + [1] * 4
NS = 99
SOFF = [0, 65, 82]
SCH = [64, 16, 16]


class StopBuild(Exception):
    pass


class Res:
    __slots__ = ("name", "w", "r")
    def __init__(self, name):
        self.name = name; self.w = None; self.r = {}


class Eng:
    def __init__(self, name, h, sem):
        self.name = name; self.h = h; self.sem = sem; self.count = 0; self.waited = {}


class FW:
    NDMA = 32
    def __init__(self, nc, es):
        self.nc = nc; self.sems = {}; self.eng = {}
        for name, h in (("pe", nc.tensor), ("act", nc.scalar), ("dve", nc.vector), ("pool", nc.gpsimd), ("sp", nc.sync)):
            s = es.enter_context(nc.semaphore("s_" + name))
            self.sems[name] = s; self.eng[name] = Eng(name, h, s)
        self.dma_sems = []
        for i in range(self.NDMA):
            s = es.enter_context(nc.semaphore("s_dma%d" % i))
            self.sems["dma%d" % i] = s; self.dma_sems.append(["dma%d" % i, 0])
        self.dma_i = 0; self.dma_ip = 0; self.ninstr = 0; self.dead = False; self.prev_plain = False; self.rec = None

    def _wait(self, e, key, val):
        if e.waited.get(key, 0) >= val: return
        e.h.wait_ge(self.sems[key], val); e.waited[key] = val

    def _deps(self, e, reads, writes, chain=False):
        deps = {}
        def add(kv):
            if kv is None: return
            k, v = kv
            if deps.get(k, 0) < v: deps[k] = v
        for r in reads: add(r.w)
        for w in writes:
            if w.w is not None and not (chain and w.w[0] == e.name):
                add(w.w)
            for k, v in w.r.items():
                if k == e.name: continue
                add((k, v))
        return deps

    def op(self, engname, fn, reads=(), writes=(), chain=False):
        if self.dead: return None
        if self.rec is not None:
            self.rec.append(("op", engname, fn, tuple(reads), tuple(writes), chain)); return None
        e = self.eng[engname]
        if engname == "pe" and chain in ("gfirst", "gnext"):
            if chain == "gfirst" and e.count > 0:
                self._wait(e, "pe", e.count)
            chain = True; self.prev_plain = False
        elif engname == "pe":
            plain = chain
            chain = chain and self.prev_plain
            if plain and not chain and e.count > 0:
                self._wait(e, "pe", e.count)
            self.prev_plain = plain
        for k, v in self._deps(e, reads, writes, chain).items(): self._wait(e, k, v)
        ins = fn(e.h); e.count += 1; ins.then_inc(e.sem, 1)
        for r in reads: r.r[e.name] = e.count
        for w in writes: w.w = (e.name, e.count); w.r = {}
        self.ninstr += 1
        return ins

    def dma(self, qname, out, in_, reads=(), writes=()):
        if self.dead: return None
        if self.rec is not None:
            self.rec.append(("dma", qname, out, in_, tuple(reads), tuple(writes))); return None
        e = self.eng[qname]
        deps = self._deps(e, reads, writes)
        half = self.NDMA // 2
        if qname == "pool":
            slot = self.dma_sems[half + (self.dma_ip % half)]; self.dma_ip += 1
        else:
            slot = self.dma_sems[self.dma_i % half]; self.dma_i += 1
        key = slot[0]
        if slot[1] > 0: deps[key] = max(deps.get(key, 0), slot[1])
        for k, v in deps.items(): self._wait(e, k, v)
        ins = e.h.dma_start(out=out, in_=in_)
        slot[1] += 16; ins.then_inc(self.sems[key], 16)
        for r in reads: r.r[key] = slot[1]
        for w in writes: w.w = (key, slot[1]); w.r = {}
        self.ninstr += 1
        return ins

    def replay(self, items):
        for it in items:
            if it[0] == "op": self.op(it[1], it[2], it[3], it[4], it[5])
            else: self.dma(it[1], it[2], it[3], reads=it[4], writes=it[5])

    def barrier(self):
        if self.dead: return
        targets = {n: e.count for n, e in self.eng.items() if e.count > 0}
        for key, cnt in self.dma_sems:
            if cnt > 0: targets[key] = cnt
        for n, e in self.eng.items():
            for k, v in targets.items(): self._wait(e, k, v)

    def finish(self):
        self.dead = False
        self.barrier()


def bl(ap, n):
    return ap.unsqueeze(2).to_broadcast([ap.shape[0], ap.shape[1], n])


def bm(ap, n):
    return ap.unsqueeze(1).to_broadcast([ap.shape[0], n, ap.shape[1]])


def _rep(a, n=128):
    return np.ascontiguousarray(np.broadcast_to(a[None], (n,) + a.shape))


def prep_shared(inp):
    f = lambda a: np.ascontiguousarray(a, dtype=np.float32)
    L = DEPTH
    S = {}
    for k in ("w_ada", "w_in", "w_glu", "w_br_ssm", "w_br_four", "w_out", "w_up", "w_down"):
        S[k] = f(inp[k])
    S["w_br_attn"] = f(inp["w_br_attn"])
    S["b_adaP"] = f(inp["b_ada"].reshape(L, 48, 128).transpose(2, 0, 1))
    S["bgbc"] = f(np.stack([np.stack([_rep(inp["b_ada"][l, 2048:3072]), _rep(inp["b_ada"][l, 5120:6144])], 1) for l in range(L)]))
    S["lnbc"] = f(np.stack([np.stack([_rep(inp[k][l]) for k in ("ln1_g", "ln1_b", "ln2_g", "ln2_b")]) for l in range(L)]))
    cw = inp["conv_w"].reshape(L, 3, 44, 128).transpose(3, 0, 2, 1)
    cb = inp["conv_b"].reshape(L, 44, 128).transpose(2, 0, 1)[..., None]
    S["convp"] = f(np.concatenate([cw, cb], -1))
    S["bglu"] = f(inp["b_glu"].reshape(L, 4, 128).transpose(2, 0, 1))
    S["g640"] = f(np.stack([_rep(np.concatenate([np.tile(inp["q_norm_g"][l], 8), np.tile(inp["k_norm_g"][l], 4)])) for l in range(L)]))
    def lay(a):
        aP = a.reshape(L, 2, 16, 2, 64).transpose(0, 1, 3, 4, 2).reshape(L, 2, 128, 16)
        aF = a.reshape(L, 2, 4, 4, 2, 64).transpose(0, 1, 3, 2, 4, 5)
        aF = np.broadcast_to(aF[:, :, :, None], (L, 2, 4, 32, 4, 2, 64)).reshape(L, 2, 128, 512)
        return np.concatenate([aP, aF], -1)
    ldt = np.broadcast_to(inp["ssm_log_dt"][..., None], (L, 2, 32, 64))
    S["lamin"] = f(np.stack([lay(inp["ssm_a_re"]), lay(inp["ssm_a_im"]), lay(ldt)], 2))
    def bbdP(b):
        Bp = b.reshape(L, 2, 16, 2, 64, 16)
        out = np.zeros((L, 2, 2, 64, 16, 2, 16), np.float32)
        for g2 in range(2):
            out[:, :, g2, :, :, g2, :] = Bp[:, :, :, g2].transpose(0, 1, 3, 2, 4)
        return out.reshape(L, 2, 128, 16, 32)
    def bbdT(b):
        Bq = b.reshape(L, 2, 4, 4, 2, 64, 16)
        out = np.zeros((L, 2, 4, 2, 16, 4, 2, 64), np.float32)
        for g2 in range(2):
            out[:, :, :, g2, :, :, g2, :] = Bq[:, :, :, :, g2].transpose(0, 1, 3, 5, 2, 4)
        return out.reshape(L, 2, 128, 4, 128)
    def cbdP(c):
        Cp = c.reshape(L, 2, 16, 2, 16, 64)
        out = np.zeros((L, 2, 2, 64, 16, 2, 16), np.float32)
        for g2 in range(2):
            out[:, :, g2, :, :, g2, :] = Cp[:, :, :, g2].transpose(0, 1, 4, 2, 3)
        return out.reshape(L, 2, 128, 16, 32)
    S["BbdP"] = f(np.stack([bbdP(inp["ssm_b_re"]), bbdP(inp["ssm_b_im"])], 2))
    S["BbdT"] = f(np.stack([bbdT(inp["ssm_b_re"]), bbdT(inp["ssm_b_im"])], 2))
    S["CbdP"] = f(np.stack([cbdP(inp["ssm_c_re"]), cbdP(inp["ssm_c_im"])], 2))
    Dblk = np.zeros((L, 128, 4, 128), np.float32)
    for l in range(L):
        for q in range(4):
            Dblk[l, np.arange(128), q, np.arange(128)] = inp["ssm_d"][l, q * 128:(q + 1) * 128]
    S["Dblk"] = Dblk
    S["ident"] = np.eye(128, dtype=np.float32)
    mk = np.zeros((128, 4), np.float32); mk[np.arange(128), np.arange(128) // 32] = 1.0
    S["maskP"] = mk
    t = np.arange(1024)
    freqs = (10000.0 ** (-np.arange(16, dtype=np.float32) / 16)).astype(np.float32)
    ang = np.concatenate([(t // 64).astype(np.float32)[:, None] * freqs, (t % 64).astype(np.float32)[:, None] * freqs], -1)
    S["ropec"] = f(np.cos(ang).reshape(8, 128, 32).transpose(1, 0, 2))
    S["ropes"] = f(np.sin(ang).reshape(8, 128, 32).transpose(1, 0, 2))
    def dft(n):
        i = np.arange(n, dtype=np.int64)
        m = (i[:, None] * i[None, :]) % n
        a = 2.0 * np.pi * m.astype(np.float64) / n
        return np.cos(a) / np.sqrt(n), np.sin(a) / np.sqrt(n)
    c128, s128 = dft(128)
    S["cs128"] = f(np.concatenate([c128, s128], 1))
    for n in (1024, 256):
        c, s = dft(n)
        S["cl%d" % n] = f(c.reshape(n // 128, 128, n).transpose(1, 0, 2))
        S["sl%d" % n] = f((-s).reshape(n // 128, 128, n).transpose(1, 0, 2))
    return S


def prep_core(inp, cid):
    f = lambda a: np.ascontiguousarray(a, dtype=np.float32)
    m = {}
    m["xin"] = f(np.concatenate([inp["x_sample"][cid], inp["x_prompt"][2 * cid:2 * cid + 2].reshape(512, 1024)], 0))
    cv = np.stack([inp["c"][cid], inp["c_ctx"]], -1)
    m["cvec"] = f(cv.reshape(8, 128, 2).transpose(1, 0, 2))
    m["ckT"] = f(inp["cache_k"][cid].reshape(DEPTH, 512, 128).transpose(0, 2, 1))
    m["cv"] = f(inp["cache_v"][cid].reshape(DEPTH, 512, 128))
    st = inp["state_ssm"][cid].reshape(DEPTH, 2, 16, 2, 64, 2)
    m["h0"] = f(st.transpose(3, 4, 0, 1, 5, 2).reshape(128, DEPTH, 2, 2, 16))
    return m


IN_SHAPES = {
    "xin": (1536, 1024), "cvec": (128, 8, 2), "ckT": (2, 128, 512), "cv": (2, 512, 128), "h0": (128, 2, 2, 2, 16),
    "w_ada": (2, 1024, 6144), "w_in": (2, 1024, 4864), "w_glu": (2, 512, 512), "w_br_attn": (2, 512, 1024),
    "w_br_ssm": (2, 512, 1024), "w_br_four": (2, 512, 1024), "w_out": (2, 1024, 1024), "w_up": (2, 1024, 5632),
    "w_down": (2, 2816, 1024), "b_adaP": (128, 2, 48), "bgbc": (2, 128, 2, 1024), "lnbc": (2, 4, 128, 1024),
    "convp": (128, 2, 44, 4), "bglu": (128, 2, 4), "g640": (2, 128, 768), "lamin": (2, 2, 3, 128, 528),
    "BbdP": (2, 2, 2, 128, 16, 32), "BbdT": (2, 2, 2, 128, 4, 128), "CbdP": (2, 2, 2, 128, 16, 32),
    "Dblk": (2, 128, 4, 128), "ident": (128, 128), "maskP": (128, 4), "ropec": (128, 8, 32), "ropes": (128, 8, 32),
    "cs128": (128, 256), "cl1024": (128, 8, 1024), "sl1024": (128, 8, 1024), "cl256": (128, 2, 256), "sl256": (128, 2, 256),
}
OUT_SHAPES = {"y": (1536, 1024), "nk": (2, 2, 256, 128), "nv": (2, 2, 256, 128), "nst": (2, 2, 2, 32, 64, 2)}


def build(n_layers=DEPTH, stop=None, dbg_shapes=None):
    nc = bass.Bass("TRN2", target_bir_lowering=False)
    Din = {k: nc.dram_tensor(k, list(s), F32, kind="ExternalInput").ap() for k, s in IN_SHAPES.items()}
    Dout = {k: nc.dram_tensor(k, list(s), F32, kind="ExternalOutput").ap() for k, s in OUT_SHAPES.items()}
    Ddbg = {k: nc.dram_tensor(k, list(s), F32, kind="ExternalOutput").ap() for k, s in (dbg_shapes or {}).items()}
    es = contextlib.ExitStack()
    with es:
        fw = FW(nc, es)
        uid = [0]
        def sb(st, name, shape, dt=F32):
            uid[0] += 1
            return st.enter_context(nc.sbuf_tensor("%s_%d" % (name, uid[0]), list(shape), dt))
        OP = fw.op
        CH = [True]
        def MM(out, lhsT, rhs, start, stop_, r, w, tp=None, chain=None):
            if chain is None: chain = CH[0] and (tp is None)
            if tp is None:
                return fw.op("pe", lambda e: e.matmul(out, lhsT=lhsT, rhs=rhs, start=start, stop=stop_), r, w, chain=chain)
            return fw.op("pe", lambda e: e.matmul(out, lhsT=lhsT, rhs=rhs, start=start, stop=stop_, tile_position=tp), r, w, chain=chain)
        def TT(eng, out, in0, in1, op, r, w):
            return fw.op(eng, lambda e: e.tensor_tensor(out=out, in0=in0, in1=in1, op=op), r, w)
        def TS(eng, out, in0, s1, op0, r, w, s2=None, op1=None):
            if op1 is None:
                return fw.op(eng, lambda e: e.tensor_scalar(out=out, in0=in0, scalar1=s1, scalar2=None, op0=op0), r, w)
            return fw.op(eng, lambda e: e.tensor_scalar(out=out, in0=in0, scalar1=s1, scalar2=s2, op0=op0, op1=op1), r, w)
        def ACT(out, in_, func, r, w, scale=1.0, bias=0.0):
            return fw.op("act", lambda e: e.activation(out=out, in_=in_, func=func, bias=bias, scale=scale), r, w)
        def CP(eng, out, in_, r, w):
            if eng == "act":
                return fw.op("act", lambda e: e.copy(out=out, in_=in_), r, w)
            return fw.op(eng, lambda e: e.tensor_copy(out=out, in_=in_), r, w)
        Rdbg = Res("dbg")
        def DBG(name, ap, r):
            if name in Ddbg:
                fw.dma("pool", Ddbg[name], ap, reads=r, writes=[Rdbg])

        X = sb(es, "X", [128, NT, 1024]); RX = [Res("X%d" % i) for i in range(NT)]
        ident_f = sb(es, "ident_f", [128, 128]); ident_b = sb(es, "ident_b", [128, 128], BF16); Rid = Res("id")
        maskP = sb(es, "maskP", [128, 4]); Rmask = Res("mask")
        modP = sb(es, "modP", [128, 6, 8, 2]); opsc = sb(es, "opsc", [128, 2, 8, 2]); Rmod = Res("mod")
        csl = sb(es, "csl", [128, 8, 2]); sTb = sb(es, "sTb", [128, 8, 2], BF16); sRep = sb(es, "sRep", [128, 2, 8, 128], BF16)
        Rcs = Res("cs")
        stat = sb(es, "stat", [128, NT, 2, 6]); mv = sb(es, "mv", [128, NT, 2]); rs = sb(es, "rs", [128, NT, 2]); Rstat = Res("stat")
        ps = es.enter_context(nc.psum_tensor("ps", [128, 3584], F32)); PB = [Res("pb%d" % i) for i in range(7)]
        pst = es.enter_context(nc.psum_tensor("pst", [128, 1024], BF16)); PT = Res("pt")
        wslot = [None, None]; Rws = [None, None]
        wctr = [0]
        def alloc_ws(st):
            for i_ in range(2):
                wslot[i_] = sb(st, "wslot%d" % i_, [128, 4096], BF16); Rws[i_] = Res("ws%d" % i_)
        def load_w(views):
            s_ = wctr[0] % 2; wctr[0] += 1
            for dstf, src in views:
                fw.dma("pool", dstf(wslot[s_]), src, writes=[Rws[s_]])
            return wslot[s_], Rws[s_]
        def bank(i, n=1):
            return ps[:, i * 512:(i + n) * 512]

        fw.dma("sp", ident_f[:], Din["ident"], writes=[Rid])
        fw.dma("pool", ident_b[:], Din["ident"], writes=[Rid])
        fw.dma("sp", maskP[:], Din["maskP"], writes=[Rmask])
        for i in range(NT):
            fw.dma("sp", X[:, i, :], Din["xin"][i * 128:(i + 1) * 128, :], writes=[RX[i]])
        fw.dma("sp", csl[:], Din["cvec"], writes=[Rcs])
        ACT(csl[:], csl[:], AF.Silu, [Rcs], [Rcs])
        CP("dve", sTb[:], csl[:], [Rcs], [Rcs])
        for v in range(2):
            CP("dve", sRep[:, v, :, :], bl(csl[:, :, v], 128), [Rcs], [Rcs])

        def build_uT(uT, RuT, sub):
            sh_sec = 0 if sub == 0 else 3
            for i in range(NT):
                v = TILE_VEC[i]
                for kk in range(2):
                    bk = kk
                    for k4 in range(4):
                        k = kk * 4 + k4
                        OP("pe", lambda e: e.transpose(bank(bk)[:, k4 * 128:(k4 + 1) * 128], X[:, i, k * 128:(k + 1) * 128], ident_f[:]),
                           [RX[i], Rid], [PB[bk]])
                    for k4 in range(4):
                        k = kk * 4 + k4
                        ACT(uT[:, k, i * 128:(i + 1) * 128], bank(bk)[:, k4 * 128:(k4 + 1) * 128], AF.Identity,
                            [PB[bk], Rmod], [RuT[i]], scale=opsc[:, sub, k, v:v + 1], bias=modP[:, sh_sec, k, v:v + 1])

        def layer_norm_residual(l, which, kc, lhsT_fn, lres_fn, wsrc3, out_tile=None):
            with contextlib.ExitStack() as ph:
                ch_save = CH[0]; CH[0] = 'L' in os.environ.get('CHP', 'mabfFGMLUS')
                alloc_ws(ph)
                lnb = sb(ph, "lnb", [128, 2, 1024]); Rln = Res("lnb")
                tmp = [sb(ph, "lntmp%d" % j, [128, 512]) for j in range(2)]; Rtmp = [Res("lntmp0"), Res("lntmp1")]
                nWd = 2 if kc <= 8 else 1
                Wds = [sb(ph, "Wd%d" % z, [128, kc, 512], BF16) for z in range(nWd)]; RWds = [Res("Wd%d" % z) for z in range(nWd)]
                def load_half(nb):
                    z = nb % nWd
                    for k0 in range(0, kc, 8):
                        k1 = min(kc, k0 + 8)
                        fw.dma("pool", Wds[z][:, k0:k1, :], wsrc3[:, k0:k1, nb * 512:(nb + 1) * 512], writes=[RWds[z]])
                load_half(0)
                if nWd == 2:
                    load_half(1)
                fw.dma("sp", lnb[:, 0, :], Din["lnbc"][l, 2 * which], writes=[Rln])
                fw.dma("sp", lnb[:, 1, :], Din["lnbc"][l, 2 * which + 1], writes=[Rln])
                gbc = sb(ph, "gbc", [128, 2, 1024]); Rgbc = Res("gbc")
                bg = sb(ph, "bg", [128, 1024]); Rbg = Res("bg")
                fw.dma("sp", bg[:], Din["bgbc"][l, :, which, :], writes=[Rbg])
                sec = 2 if which == 0 else 5
                wsrc = Din["w_ada"][l].rearrange("(k p) n -> p k n", p=128)
                for half in range(2):
                    Wg, RWg = load_w([(lambda t: t[:, 0:4096].rearrange("p (k n) -> p k n", k=8), wsrc[:, :, sec * 1024 + half * 512:sec * 1024 + (half + 1) * 512])])
                    Wgv = Wg[:, 0:4096].rearrange("p (k n) -> p k n", k=8)
                    for v in range(2):
                        for k in range(8):
                            MM(bank(v), sRep[:, v, k, :], Wgv[:, k, :], k == 0, k == 7, [RWg, Rcs], [PB[v]])
                        TT("dve", gbc[:, v, half * 512:(half + 1) * 512], bank(v), bg[:, half * 512:(half + 1) * 512], ALU.add, [PB[v], Rbg], [Rgbc])
                cnt = 0
                for nb in range(2):
                    cs = slice(nb * 512, (nb + 1) * 512)
                    Wd = Wds[nb % nWd]; RWd = RWds[nb % nWd]
                    if nWd == 1 and nb == 1:
                        load_half(1)
                    for i in range(NT):
                        v = TILE_VEC[i]
                        bk = 2 + (cnt % 4); tj = cnt % 2; cnt += 1
                        for k in range(kc):
                            MM(bank(bk), lhsT_fn(i, k), Wd[:, k, :], k == 0, k == kc - 1, lres_fn(i, k) + [RWd], [PB[bk]])
                        TT("dve", tmp[tj][:], bank(bk), gbc[:, v, cs], ALU.mult, [PB[bk], Rgbc], [Rtmp[tj]])
                        OP("dve", lambda e: e.scalar_tensor_tensor(out=X[:, i, cs], in0=X[:, i, cs], scalar=float(ALPHA), in1=tmp[tj][:],
                                                                   op0=ALU.mult, op1=ALU.add), [RX[i], Rtmp[tj]], [RX[i]])
                for i in range(NT):
                    for hh in range(2):
                        OP("dve", lambda e: e.bn_stats(out=stat[:, i, hh, :], in_=X[:, i, hh * 512:(hh + 1) * 512]), [RX[i]], [Rstat])
                    OP("dve", lambda e: e.bn_aggr(out=mv[:, i, :], in_=stat[:, i, :, :].rearrange("p a b -> p (a b)")), [Rstat], [Rstat])
                TS("dve", rs[:, :, 0], mv[:, :, 1], EPS, ALU.add, [Rstat], [Rstat])
                ACT(rs[:, :, 0], rs[:, :, 0], AF.Sqrt, [Rstat], [Rstat])
                OP("dve", lambda e: e.reciprocal(out=rs[:, :, 0], in_=rs[:, :, 0]), [Rstat], [Rstat])
                OP("dve", lambda e: e.scalar_tensor_tensor(out=rs[:, :, 1], in0=mv[:, :, 0], scalar=-1.0, in1=rs[:, :, 0],
                                                           op0=ALU.mult, op1=ALU.mult), [Rstat], [Rstat])
                for i in range(NT):
                    ACT(X[:, i, :], X[:, i, :], AF.Identity, [RX[i], Rstat], [RX[i]], scale=rs[:, i, 0:1], bias=rs[:, i, 1:2])
                    TT("dve", X[:, i, :], X[:, i, :], lnb[:, 0, :], ALU.mult, [RX[i], Rln], [RX[i]])
                    TT("pool", X[:, i, :], X[:, i, :], lnb[:, 1, :], ALU.add, [RX[i], Rln], [RX[i]])
                    if out_tile is not None:
                        out_tile(i)
                fw.barrier()
                CH[0] = ch_save

        def layers():
          for l in range(n_layers):
              with contextlib.ExitStack() as ph:
                  wa = [sb(ph, "wa%d" % i, [128, 8, 512], BF16) for i in range(3)]; Rwa = [Res("wa%d" % i) for i in range(3)]
                  bP = sb(ph, "bP", [128, 48]); Rb = Res("bmod")
                  fw.dma("sp", bP[:], Din["b_adaP"][:, l, :], writes=[Rb])
                  wsrc = Din["w_ada"][l].rearrange("(k p) n -> p k n", p=128)
                  pm = bank(6)[:, 0:96]
                  CH[0] = 'm' in os.environ.get('CHP', 'mabfFGMLUS')
                  cbs = [cb for cb in range(12) if cb // 2 not in (2, 5)]
                  for i_ in range(min(3, len(cbs))):
                      fw.dma("pool", wa[i_][:], wsrc[:, :, cbs[i_] * 512:(cbs[i_] + 1) * 512], writes=[Rwa[i_]])
                  for i_, cb in enumerate(cbs):
                      s = i_ % 3
                      sec, half = cb // 2, cb % 2
                      if i_ >= 3 and False:
                          pass
                      for m in range(4):
                          col = (sec * 8 + half * 4 + m) * 2
                          for k in range(8):
                              MM(pm[:, col:col + 2], wa[s][:, k, m * 128:(m + 1) * 128], sTb[:, k, :], k == 0, k == 7,
                                 [Rwa[s], Rcs], [PB[6]])
                      if i_ + 3 < len(cbs):
                          fw.dma("pool", wa[s][:], wsrc[:, :, cbs[i_ + 3] * 512:(cbs[i_ + 3] + 1) * 512], writes=[Rwa[s]])
                  for sec in (0, 1, 3, 4):
                      TT("dve", modP[:, sec, :, :], pm[:, sec * 16:(sec + 1) * 16].rearrange("p (c v) -> p c v", v=2),
                         bl(bP[:, sec * 8:(sec + 1) * 8], 2), ALU.add, [PB[6], Rb], [Rmod])
                  TS("dve", opsc[:, 0, :, :], modP[:, 1, :, :], 1.0, ALU.add, [Rmod], [Rmod])
                  TS("dve", opsc[:, 1, :, :], modP[:, 4, :, :], 1.0, ALU.add, [Rmod], [Rmod])
                  fw.barrier()
              if stop == "mod": fw.dead = True
              DBG("d_modP", modP[:].rearrange("p a b c -> p (a b c)"), [Rmod])

              with contextlib.ExitStack() as L1:
                  QT = sb(L1, "QT", [128, 4, TOK], BF16); RQ = [Res("QT%d" % j) for j in range(4)]; attnT = QT; RattnT = RQ
                  sfT = sb(L1, "sfT", [128, 8, TOK], BF16); RsfT = [Res("sfT%d" % j) for j in range(8)]
                  fourT = sb(L1, "fourT", [128, 4, TOK], BF16); RfourT = [Res("fourT%d" % j) for j in range(4)]
                  Lp = contextlib.ExitStack()
                  lamD = [[sb(Lp, "lam%d%d" % (d_, r), [128, 528]) for r in range(2)] for d_ in range(2)]
                  ffD = [[sb(Lp, "ff%d%d" % (d_, r), [128, 528]) for r in range(2)] for d_ in range(2)]
                  PowP = [sb(Lp, "PowP%d" % r, [128, 2, 16, 17]) for r in range(2)]
                  LH = [sb(Lp, "LH%d" % r, [128, 2, 16, 7]) for r in range(2)]
                  fP = [sb(Lp, "fP%d" % r, [128, 2, 16]) for r in range(2)]
                  RS = Res("ssm")
                  def cmul(dr, di, ar, ai, br, bi, m1, m2):
                      TT("dve", m1, ar, br, ALU.mult, [RS], [RS]); TT("dve", m2, ai, bi, ALU.mult, [RS], [RS])
                      TT("dve", dr, m1, m2, ALU.subtract, [RS], [RS])
                      TT("dve", m1, ar, bi, ALU.mult, [RS], [RS]); TT("dve", m2, ai, br, ALU.mult, [RS], [RS])
                      TT("dve", di, m1, m2, ALU.add, [RS], [RS])
                  def ssm_param_pipeline(st):
                      are, aim, ldt, mag, cc, ss, t1, t2, t3 = [sb(st, "sp%d" % j_, [128, 528]) for j_ in range(9)]
                      for d in range(2):
                          lam = lamD[d]; ff = ffD[d]
                          fw.dma("sp", are[:], Din["lamin"][l, d, 0], writes=[RS])
                          fw.dma("sp", aim[:], Din["lamin"][l, d, 1], writes=[RS])
                          fw.dma("sp", ldt[:], Din["lamin"][l, d, 2], writes=[RS])
                          ACT(ldt[:], ldt[:], AF.Exp, [RS], [RS])
                          TT("dve", t1[:], are[:], ldt[:], ALU.mult, [RS], [RS])
                          ACT(mag[:], t1[:], AF.Exp, [RS], [RS])
                          TT("dve", t2[:], aim[:], ldt[:], ALU.mult, [RS], [RS])
                          ACT(ss[:], t2[:], AF.Sin, [RS], [RS], scale=1.0 / 16)
                          ACT(t1[:], t2[:], AF.Sin, [RS], [RS], scale=1.0 / 32)
                          TT("dve", t1[:], t1[:], t1[:], ALU.mult, [RS], [RS])
                          TS("dve", cc[:], t1[:], -2.0, ALU.mult, [RS], [RS], s2=1.0, op1=ALU.add)
                          for _ in range(4):
                              TT("dve", t1[:], cc[:], cc[:], ALU.mult, [RS], [RS])
                              TT("dve", t2[:], ss[:], ss[:], ALU.mult, [RS], [RS])
                              TT("dve", t3[:], cc[:], ss[:], ALU.mult, [RS], [RS])
                              TT("dve", cc[:], t1[:], t2[:], ALU.subtract, [RS], [RS])
                              TS("dve", ss[:], t3[:], 2.0, ALU.mult, [RS], [RS])
                          TT("dve", lam[0][:], mag[:], cc[:], ALU.mult, [RS], [RS])
                          TT("dve", lam[1][:], mag[:], ss[:], ALU.mult, [RS], [RS])
                          TT("dve", t1[:], are[:], are[:], ALU.mult, [RS], [RS])
                          TT("dve", t2[:], aim[:], aim[:], ALU.mult, [RS], [RS])
                          TT("dve", t1[:], t1[:], t2[:], ALU.add, [RS], [RS])
                          OP("dve", lambda e: e.reciprocal(out=t1[:], in_=t1[:]), [RS], [RS])
                          TS("dve", t2[:], lam[0][:], -1.0, ALU.add, [RS], [RS])
                          TT("dve", t3[:], t2[:], are[:], ALU.mult, [RS], [RS])
                          TT("dve", cc[:], lam[1][:], aim[:], ALU.mult, [RS], [RS])
                          TT("dve", t3[:], t3[:], cc[:], ALU.add, [RS], [RS])
                          TT("dve", ff[0][:], t3[:], t1[:], ALU.mult, [RS], [RS])
                          TT("dve", t3[:], lam[1][:], are[:], ALU.mult, [RS], [RS])
                          TT("dve", cc[:], t2[:], aim[:], ALU.mult, [RS], [RS])
                          TT("dve", t3[:], t3[:], cc[:], ALU.subtract, [RS], [RS])
                          TT("dve", ff[1][:], t3[:], t1[:], ALU.mult, [RS], [RS])
                          pm1 = t1[:, 0:256].rearrange("p (g k) -> p g k", g=16); pm2 = t2[:, 0:256].rearrange("p (g k) -> p g k", g=16)
                          OP("dve", lambda e, d=d: e.memset(PowP[0][:, d, :, 0:1], 1.0), [RS], [RS])
                          OP("dve", lambda e, d=d: e.memset(PowP[1][:, d, :, 0:1], 0.0), [RS], [RS])
                          for r in range(2):
                              TS("dve", PowP[r][:, d, :, 1], lam[r][:, 0:16], 1.0, ALU.mult, [RS], [RS])
                              TS("dve", fP[r][:, d, :], ff[r][:, 0:16], 1.0, ALU.mult, [RS], [RS])
                          for w in (1, 2, 4, 8):
                              br_ = PowP[0][:, d, :, w:w + 1].to_broadcast([128, 16, w]); bi_ = PowP[1][:, d, :, w:w + 1].to_broadcast([128, 16, w])
                              cmul(PowP[0][:, d, :, w + 1:2 * w + 1], PowP[1][:, d, :, w + 1:2 * w + 1],
                                   PowP[0][:, d, :, 1:w + 1], PowP[1][:, d, :, 1:w + 1], br_, bi_, pm1[:, :, 0:w], pm2[:, :, 0:w])
                          for r in range(2):
                              TS("dve", LH[r][:, d, :, 0], PowP[r][:, d, :, 16], 1.0, ALU.mult, [RS], [RS])
                          for m in range(6):
                              cmul(LH[0][:, d, :, m + 1], LH[1][:, d, :, m + 1], LH[0][:, d, :, m], LH[1][:, d, :, m],
                                   LH[0][:, d, :, m], LH[1][:, d, :, m], t1[:, 256:272], t2[:, 256:272])
                  with contextlib.ExitStack() as L2:
                      KT = sb(L2, "KT", [128, 2, 2048], BF16); RKT = Res("KT")
                      Vsb = sb(L2, "Vsb", [128, 16, 2, 2, 128], BF16); RV = Res("V")
                      with contextlib.ExitStack() as ph:
                          alloc_ws(ph)
                          uT = sb(ph, "uT", [128, 8, TOK], BF16); RuT = [Res("uT%d" % i) for i in range(NT)]
                          qn = [sb(ph, "qn%d" % j, [128, 768]) for j in range(2)]; Rqn = [Res("qn0"), Res("qn1")]
                          vf = [sb(ph, "vf%d" % j, [128, 128]) for j in range(2)]; Rvf = [Res("vf0"), Res("vf1")]
                          qkb = [sb(ph, "qkb%d" % j, [128, 768], BF16) for j in range(2)]; Rqkb = [Res("qkb0"), Res("qkb1")]
                          kb2 = [sb(ph, "kb2%d" % j, [128, 128], BF16) for j in range(2)]; Rkb2 = [Res("kb20"), Res("kb21")]
                          ssq = sb(ph, "ssq", [128, 12]); Rssq = Res("ssq")
                          g640 = sb(ph, "g640", [128, 768]); rc = sb(ph, "rc", [128, 8, 32]); rsn = sb(ph, "rsn", [128, 8, 32]); Rcst = Res("cst")
                          rt = [sb(ph, "rt%d" % j, [128, 12, 32]) for j in range(2)]; Rrt = [Res("rt0"), Res("rt1")]
                          win = Din["w_in"][l].rearrange("(k p) n -> p k n", p=128)
                          WA_pre = load_w([(lambda t: t[:, 0:4096].rearrange("p (k n) -> p k n", k=8), win[:, :, 0:512])])
                          WB_pre = load_w([(lambda t: t[:, 0:2048].rearrange("p (k n) -> p k n", k=8), win[:, :, 512:768])])
                          fw.dma("sp", g640[:], Din["g640"][l], writes=[Rcst])
                          fw.dma("sp", rc[:], Din["ropec"], writes=[Rcst])
                          fw.dma("sp", rsn[:], Din["ropes"], writes=[Rcst])
                          fw.dma("pool", KT[:, 0, 1024:1536], Din["ckT"][l], writes=[RKT])
                          fw.dma("pool", KT[0:64, 1, 1024:1536], Din["ckT"][l, 64:128, :], writes=[RKT])
                          fw.dma("pool", KT[64:128, 1, 1024:1536], Din["ckT"][l, 0:64, :], writes=[RKT])
                          OP("dve", lambda e: e.memset(Vsb[:].rearrange("p a b c d -> p (a b c d)"), 1.0), [], [RV])
                          cvv = Din["cv"][l].rearrange("(t p) (h d) -> p t h d", p=128, h=2)
                          for lay in range(2):
                              for kvh_ in range(2):
                                  fw.dma("pool", Vsb[:, 8:12, kvh_, lay, lay * 64:(lay + 1) * 64], cvv[:, :, kvh_, :], writes=[RV])
                          build_uT(uT, RuT, 0)
                          if stop == "ut": fw.dead = True
                          DBG("d_uT", uT[:].rearrange("p a b -> p (a b)"), RuT)

                          def norm_rope(i, pa, pr, nh, gap, outb, Routb, slot):
                              w_ = nh * 64
                              sqv = qn[slot][:, 0:w_]
                              ACT(sqv, pa, AF.Square, pr, [Rqn[slot]])
                              OP("dve", lambda e: e.reduce_sum(out=ssq[:, 0:nh], in_=sqv.rearrange("p (h d) -> p h d", d=64), axis=AX.X), [Rqn[slot]], [Rssq])
                              TS("dve", ssq[:, 0:nh], ssq[:, 0:nh], 1.0 / 64, ALU.mult, [Rssq], [Rssq], s2=EPS, op1=ALU.add)
                              ACT(ssq[:, 0:nh], ssq[:, 0:nh], AF.Sqrt, [Rssq], [Rssq])
                              OP("dve", lambda e: e.reciprocal(out=ssq[:, 0:nh], in_=ssq[:, 0:nh]), [Rssq], [Rssq])
                              qv = qn[slot][:, 0:w_]
                              TT("dve", qv.rearrange("p (h d) -> p h d", d=64), pa.rearrange("p (h d) -> p h d", d=64), bl(ssq[:, 0:nh], 64),
                                 ALU.mult, pr + [Rssq], [Rqn[slot]])
                              TT("dve", qv, qv, gap, ALU.mult, [Rqn[slot], Rcst], [Rqn[slot]])
                              if i >= 8:
                                  CP("act", outb, qv, [Rqn[slot]], [Routb])
                              else:
                                  x4 = qv.rearrange("p (h d two) -> p h d two", d=32, two=2)
                                  o4 = outb.rearrange("p (h d two) -> p h d two", d=32, two=2)
                                  x0, x1 = x4[:, :, :, 0], x4[:, :, :, 1]
                                  cc, ss_ = bm(rc[:, i, :], nh), bm(rsn[:, i, :], nh)
                                  r0, r1 = rt[0][:, 0:nh, :], rt[1][:, 0:nh, :]
                                  TT("dve", r0, x0, cc, ALU.mult, [Rqn[slot], Rcst], [Rrt[0]])
                                  TT("pool", r1, x1, ss_, ALU.mult, [Rqn[slot], Rcst], [Rrt[1]])
                                  TT("dve", o4[:, :, :, 0], r0, r1, ALU.subtract, Rrt, [Routb])
                                  TT("dve", r0, x0, ss_, ALU.mult, [Rqn[slot], Rcst], [Rrt[0]])
                                  TT("pool", r1, x1, cc, ALU.mult, [Rqn[slot], Rcst], [Rrt[1]])
                                  TT("dve", o4[:, :, :, 1], r0, r1, ALU.add, Rrt, [Routb])

                          CH[0] = 'a' in os.environ.get('CHP', 'mabfFGMLUS')
                          WA, RWA = WA_pre; WB, RWB = WB_pre
                          WvA = WA[:, 0:4096].rearrange("p (k n) -> p k n", k=8)
                          WvB = WB[:, 0:2048].rearrange("p (k n) -> p k n", k=8)
                          for i in range(NT):
                              b0 = 2 + 2 * (i % 2); sl = i % 2
                              prs = [PB[b0], PB[b0 + 1]]
                              for k in range(8):
                                  MM(bank(b0), uT[:, k, i * 128:(i + 1) * 128], WvA[:, k, :], k == 0, k == 7, [RuT[i], RWA], [PB[b0]])
                              for k in range(8):
                                  MM(bank(b0 + 1)[:, 0:256], uT[:, k, i * 128:(i + 1) * 128], WvB[:, k, :], k == 0, k == 7, [RuT[i], RWB], [PB[b0 + 1]])
                              vps = bank(b0 + 1)[:, 128:256]
                              norm_rope(i, bank(b0, 2)[:, 0:768], prs, 12, g640[:, 0:768], qkb[sl][:, 0:768], Rqkb[sl], sl)
                              for j in range(4):
                                  src = qkb[sl][:, j * 128:(j + 1) * 128]
                                  OP("pe", lambda e: e.transpose(pst[:, j * 128:(j + 1) * 128], src, ident_b[:]), [Rqkb[sl], Rid], [PT])
                              CP("act", QT[:, :, i * 128:(i + 1) * 128], pst[:, 0:512].rearrange("p (j c) -> p j c", j=4), [PT], RQ)
                              kt = i if i < 8 else 12 + (i - 8)
                              for lay in range(2):
                                  CP("act", Vsb[:, kt, :, lay, lay * 64:(lay + 1) * 64], vps.rearrange("p (h d) -> p h d", h=2), [PB[b0 + 1]], [RV])
                              if i >= 8:
                                  p_, t_ = (i - 8) // 2, (i - 8) % 2
                                  CP("act", vf[sl][:], vps, [PB[b0 + 1]], [Rvf[sl]])
                                  fw.dma("sp", Dout["nk"][p_, l, t_ * 128:(t_ + 1) * 128, :], qn[sl][:, 512:640], reads=[Rqn[sl]], writes=[Rdbg])
                                  fw.dma("sp", Dout["nv"][p_, l, t_ * 128:(t_ + 1) * 128, :], vf[sl][:], reads=[Rvf[sl]], writes=[Rdbg])
                              CP("act", kb2[sl][:, 0:64], qkb[sl][:, 576:640], [Rqkb[sl]], [Rkb2[sl]])
                              CP("act", kb2[sl][:, 64:128], qkb[sl][:, 512:576], [Rqkb[sl]], [Rkb2[sl]])
                              OP("pe", lambda e: e.transpose(pst[:, 512:640], qkb[sl][:, 512:640], ident_b[:]), [Rqkb[sl], Rid], [PT])
                              OP("pe", lambda e: e.transpose(pst[:, 640:768], kb2[sl][:, :], ident_b[:]), [Rkb2[sl], Rid], [PT])
                              kc = i * 128 if i < 8 else 1536 + (i - 8) * 128
                              CP("act", KT[:, 0, kc:kc + 128], pst[:, 512:640], [PT], [RKT])
                              CP("act", KT[:, 1, kc:kc + 128], pst[:, 640:768], [PT], [RKT])
                          WF_pre = load_w([(lambda t: t[:, 0:4096].rearrange("p (k n) -> p k n", k=8), win[:, :, 768:1280])])
                          if stop in ("passA", "passB"): fw.dead = True
                          CH[0] = 'f' in os.environ.get('CHP', 'mabfFGMLUS')
                          for wb in range(2):
                              W, RW = WF_pre
                              if wb == 0:
                                  WF_pre = load_w([(lambda t: t[:, 0:4096].rearrange("p (k n) -> p k n", k=8), win[:, :, 1280:1792])])
                              Wv = W[:, 0:4096].rearrange("p (k n) -> p k n", k=8)
                              for m4 in range(4):
                                  m = wb * 4 + m4
                                  for tb, (c0, cn) in enumerate(BLKS):
                                      bk = (0, 1, 6)[(m * 3 + tb) % 3]
                                      for k in range(8):
                                          MM(bank(bk), Wv[:, k, m4 * 128:(m4 + 1) * 128], uT[:, k, c0:c0 + cn], k == 0, k == 7,
                                             RuT[c0 // 128:(c0 + cn) // 128] + [RW], [PB[bk]])
                                      CP("act", sfT[:, m, c0:c0 + cn], bank(bk), [PB[bk]], [RsfT[m]])
                          DBG("d_sfT", sfT[:].rearrange("p a b -> p (a b)"), RsfT)
                          fw.barrier()
                      if stop == "inproj": fw.dead = True
                      DBG("d_QT", QT[:].rearrange("p a b -> p (a b)"), RQ)
                      DBG("d_KT", KT[:].rearrange("p a b -> p (a b)"), [RKT])
                      CH[0] = 't' in os.environ.get('CHP', 'mabfFGMLUS')
                      with contextlib.ExitStack() as ph:
                          fw.rec = []
                          ssm_param_pipeline(ph)
                          pipe_items = fw.rec if fw.rec is not None else []; fw.rec = None
                          per_unit = (len(pipe_items) + 23) // 24
                          Eb = [sb(ph, "Eb%d" % j, [128, 512], BF16) for j in range(3)]; REb = [Res("Eb%d" % j) for j in range(3)]
                          rsb = [sb(ph, "rsb%d" % j, [128, 512]) for j in range(2)]; Rrsb = [Res("rsb0"), Res("rsb1")]
                          cnt = 0; cnt2 = 0
                          for si, (t0, Ls) in enumerate(SEQS):
                              if si == 0:
                                  qblks = [(0, 512), (512, 512)]; nkt = 12; kt0 = 0; kc0 = 0
                              else:
                                  qblks = [(t0, 256)]; nkt = 2; kt0 = 12 + 2 * (si - 1); kc0 = 1536 + 256 * (si - 1)
                              for h in range(8):
                                  j = h // 2; hh = h % 2; kvh = h // 4; var = 0 if kvh == hh else 1
                                  pv, sm = (slice(0, 64), slice(64, 128)) if hh == 0 else (slice(64, 128), slice(0, 64))
                                  for (q0, qn) in qblks:
                                      pvb = 3 + (cnt2 % 2); cnt2 += 1
                                      def S_mm(kt, sbk):
                                          MM(bank(sbk)[:, 0:qn], KT[hh * 64:(hh + 1) * 64, var, kc0 + kt * 128:kc0 + (kt + 1) * 128],
                                             QT[hh * 64:(hh + 1) * 64, j, q0:q0 + qn], True, True, [RKT, RQ[j]], [PB[sbk]], chain=False)
                                      slots = [(cnt + i_) % 3 for i_ in range(nkt)]; cnt += nkt
                                      S_mm(0, slots[0])
                                      for kt in range(nkt):
                                          sbk = slots[kt]; eb = sbk
                                          if kt + 1 < nkt:
                                              S_mm(kt + 1, slots[kt + 1])
                                          ACT(Eb[eb][:, 0:qn], bank(sbk)[:, 0:qn], AF.Exp, [PB[sbk]], [REb[eb]], scale=0.125)
                                          MM(bank(pvb)[:, 0:qn], Vsb[:, kt0 + kt, kvh, hh, :], Eb[eb][:, 0:qn], kt == 0, kt == nkt - 1,
                                             [RV, REb[eb]], [PB[pvb]], chain=False)
                                      rb = cnt2 % 2
                                      ACT(rsb[rb][sm, 0:qn], bank(pvb)[sm, 0:qn], AF.Ln, [PB[pvb]], [Rrsb[rb]])
                                      ACT(rsb[rb][sm, 0:qn], rsb[rb][sm, 0:qn], AF.Exp, [Rrsb[rb]], [Rrsb[rb]], scale=-1.0)
                                      TT("dve", attnT[pv, j, q0:q0 + qn], bank(pvb)[pv, 0:qn], rsb[rb][sm, 0:qn], ALU.mult,
                                         [PB[pvb], Rrsb[rb]], [RattnT[j]])
                                      fw.replay(pipe_items[:per_unit]); del pipe_items[:per_unit]
                          fw.replay(pipe_items); del pipe_items[:]
                          fw.barrier()
                  DBG("d_attnT", attnT[:].rearrange("p a b -> p (a b)"), RattnT)
                  CH[0] = 'F' in os.environ.get('CHP', 'mabfFGMLUS')
                  with contextlib.ExitStack() as L3:
                    with contextlib.ExitStack() as ph:
                        cs128 = sb(ph, "cs128", [128, 256], BF16); cl = sb(ph, "cl", [128, 8, 1024], BF16); sl_ = sb(ph, "sl", [128, 8, 1024], BF16)
                        clp = sb(ph, "clp", [128, 2, 256], BF16); slp = sb(ph, "slp", [128, 2, 256], BF16); Rft = Res("ftab")
                        Pfm = sb(ph, "Pfm", [128, NT, 1024], BF16); RPfm = [Res("Pfm%d" % i) for i in range(NT)]
                        fw.dma("pool", cs128[:], Din["cs128"], writes=[Rft])
                        for k in range(8):
                            fw.dma("pool", cl[:, k, :], Din["cl1024"][:, k, :], writes=[Rft])
                            fw.dma("pool", sl_[:, k, :], Din["sl1024"][:, k, :], writes=[Rft])
                        fw.dma("pool", clp[:], Din["cl256"], writes=[Rft])
                        fw.dma("pool", slp[:], Din["sl256"], writes=[Rft])
                        for i in range(NT):
                            b0 = 2 + 2 * (i % 2)
                            for g in range(4):
                                MM(bank(b0 + g // 2)[:, (g % 2) * 256:(g % 2 + 1) * 256], sfT[:, 4 + g, i * 128:(i + 1) * 128], cs128[:], True, True,
                                   [RsfT[4 + g], Rft], [PB[b0 + g // 2]])
                            CP("act", Pfm[:, i, :], bank(b0, 2), [PB[b0], PB[b0 + 1]], [RPfm[i]])
                        cnt = 0
                        for si, (t0, Ls) in enumerate(SEQS):
                            ntl = Ls // 128; tb0 = t0 // 128
                            tc_, ts_ = (cl, sl_) if si == 0 else (clp, slp)
                            for g in range(4):
                                for lb in range(0, Ls, 512):
                                    n = min(512, Ls - lb)
                                    bk = (0, 1, 6)[cnt % 3]; cnt += 1
                                    for k in range(ntl):
                                        MM(bank(bk)[:, 0:n], Pfm[:, tb0 + k, g * 256:g * 256 + 128], tc_[:, k, lb:lb + n], k == 0, False,
                                           [RPfm[tb0 + k], Rft], [PB[bk]])
                                        MM(bank(bk)[:, 0:n], Pfm[:, tb0 + k, g * 256 + 128:g * 256 + 256], ts_[:, k, lb:lb + n], False, k == ntl - 1,
                                           [RPfm[tb0 + k], Rft], [PB[bk]])
                                    CP("act", fourT[:, g, t0 + lb:t0 + lb + n], bank(bk)[:, 0:n], [PB[bk]], [RfourT[g]])
                        fw.barrier()
                    DBG("d_fourT", fourT[:].rearrange("p a b -> p (a b)"), RfourT)
                    if stop == "four": fw.dead = True
                    CH[0] = 'S' in os.environ.get('CHP', 'mabfFGMLUS')
                    with contextlib.ExitStack() as S:
                        Hb = sb(S, "Hb", [128, 2, 2, 16, NS], BF16)
                        h0 = sb(S, "h0", [128, 2, 2, 16]); stg = sb(S, "stg", [128, 16, 2])
                        RHb = Res("Hb"); Rstg = Res("stg")
                        fw.dma("sp", h0[:], Din["h0"][:, l], writes=[RS])
                        for d in range(2):
                            with contextlib.ExitStack() as SD:
                                lam = lamD[d]; ff = ffD[d]
                                A = [sb(SD, "A%d" % r, [128, 16, NS]) for r in range(2)]
                                for r in range(2):
                                    OP("dve", lambda e: e.memset(A[r][:].rearrange("p a b -> p (a b)"), 0.0), [RS], [RS])
                                with contextlib.ExitStack() as SW:
                                    W = [sb(SW, "W%d" % r, [128, 16, 128]) for r in range(2)]
                                    WT2 = [[sb(SW, "WT%d_%d" % (z, r), [128, 16, 128], BF16) for r in range(2)] for z in range(2)]
                                    wm = [sb(SW, "wm%d" % r, [128, 8, 128]) for r in range(3)]
                                    BT = [sb(SW, "BT%d" % r, [128, 128]) for r in range(2)]
                                    Lw = [sb(SW, "Lw%d" % r, [128, 128]) for r in range(2)]
                                    RWT2 = [Res("WTa"), Res("WTb")]
                                    def build_W(q):
                                        fs = slice(16 + q * 128, 16 + (q + 1) * 128)
                                        for r in range(2):
                                            fw.dma("sp", BT[r][:], Din["BbdT"][l, d, r, :, q, :], writes=[RS])
                                            ACT(Lw[r][:], lam[r][:, fs], AF.Identity, [RS], [RS])
                                        cmul(W[0][:, 0, :], W[1][:, 0, :], ff[0][:, fs], ff[1][:, fs], BT[0][:], BT[1][:], wm[0][:, 0, :], wm[1][:, 0, :])
                                        for w in (1, 2, 4, 8):
                                            cmul(W[0][:, w:2 * w, :], W[1][:, w:2 * w, :], bm(Lw[0][:], w), bm(Lw[1][:], w),
                                                 W[0][:, 0:w, :], W[1][:, 0:w, :], wm[0][:, 0:w, :], wm[1][:, 0:w, :])
                                            if w < 8:
                                                TT("dve", wm[0][:, 0, :], Lw[0][:], Lw[0][:], ALU.mult, [RS], [RS])
                                                TT("dve", wm[1][:, 0, :], Lw[1][:], Lw[1][:], ALU.mult, [RS], [RS])
                                                TT("dve", wm[2][:, 0, :], Lw[0][:], Lw[1][:], ALU.mult, [RS], [RS])
                                                TT("dve", Lw[0][:], wm[0][:, 0, :], wm[1][:, 0, :], ALU.subtract, [RS], [RS])
                                                TS("dve", Lw[1][:], wm[2][:, 0, :], 2.0, ALU.mult, [RS], [RS])
                                        for r in range(2):
                                            ACT(WT2[q % 2][r][:].rearrange("p a b -> p (a b)"), W[r][:].rearrange("p a b -> p (a b)"), AF.Identity, [RS], [RWT2[q % 2]])
                                    def run_S(q):
                                        WTq = WT2[q % 2]; RWTq = RWT2[q % 2]
                                        Sps = bank(0, 2).rearrange("p (b r c) -> p b r c", b=4, r=2)
                                        for b in range(4):
                                            sv = sfT[32 * b:32 * b + 32, q, :].rearrange("p (c j) -> p c j", j=16)
                                            for r in range(2):
                                                for k in range(16):
                                                    j = 15 - k if d == 0 else k
                                                    MM(Sps[:, b, r, 0:96], WTq[r][32 * b:32 * b + 32, k, :], sv[:, :, j], k == 0, k == 15,
                                                       [RWTq, RsfT[q]], [PB[0], PB[1]], tp=(32 * b, 0))
                                        for r in range(2):
                                            for s_i in range(3):
                                                n = SCH[s_i]; c0 = (0, 64, 80)[s_i]; lo = SOFF[s_i] + (1 if d == 0 else 0)
                                                ACT(A[r][:, 4 * q:4 * q + 4, lo:lo + n], Sps[:, :, r, c0:c0 + n], AF.Identity, [PB[0], PB[1]], [RS])
                                    build_W(0)
                                    for q in range(4):
                                        if q + 1 < 4:
                                            build_W(q + 1)
                                        run_S(q)
                                RSp = Res("ssm_pool"); NPD = 10
                                for r in range(2):
                                    ACT(A[r][:, :, 0 if d == 0 else 64], h0[:, d, r, :], AF.Identity, [RS], [RS, RSp])
                                with contextlib.ExitStack() as SC:
                                    sm = [sb(SC, "sm%d" % r, [128, 16, NS]) for r in range(3)]
                                    for m in range(7):
                                        dd = 1 << m
                                        groups = []
                                        if dd < 65:
                                            groups.append((lambda t_, ps_, lo, hi: t_[:, ps_, lo:hi], 65, None))
                                        if dd < 17:
                                            groups.append((lambda t_, ps_, lo, hi: t_[:, ps_, 65:99].rearrange("p g (s c) -> p g s c", s=2)[:, :, :, lo:hi], 17, 2))
                                        for view, n, ns in groups:
                                            cntn = n - dd
                                            (dlo, dhi, slo, shi) = (dd, n, 0, cntn) if d == 0 else (0, cntn, dd, n)
                                            for eng_, ps_, Rr in (("dve", slice(0, NPD), RS), ("pool", slice(NPD, 16), RSp)):
                                                npz = ps_.stop - ps_.start
                                                if ns is None:
                                                    Lr = bl(LH[0][:, d, ps_, m], cntn); Li = bl(LH[1][:, d, ps_, m], cntn)
                                                    tv = lambda t_: t_[:, ps_, 0:cntn]
                                                else:
                                                    Lr = LH[0][:, d, ps_, m].unsqueeze(2).unsqueeze(3).to_broadcast([128, npz, 2, cntn])
                                                    Li = LH[1][:, d, ps_, m].unsqueeze(2).unsqueeze(3).to_broadcast([128, npz, 2, cntn])
                                                    tv = lambda t_: t_[:, ps_, 0:2 * cntn].rearrange("p g (s c) -> p g s c", s=2)
                                                sr, si = view(A[0], ps_, slo, shi), view(A[1], ps_, slo, shi)
                                                dr, di = view(A[0], ps_, dlo, dhi), view(A[1], ps_, dlo, dhi)
                                                m1, m2, m3 = tv(sm[0]), tv(sm[1]), tv(sm[2])
                                                TT(eng_, m1, sr, Lr, ALU.mult, [Rr], [Rr]); TT(eng_, m2, si, Li, ALU.mult, [Rr], [Rr])
                                                TT(eng_, m1, m1, m2, ALU.subtract, [Rr], [Rr])
                                                TT(eng_, m2, si, Lr, ALU.mult, [Rr], [Rr]); TT(eng_, m3, sr, Li, ALU.mult, [Rr], [Rr])
                                                TT(eng_, m2, m2, m3, ALU.add, [Rr], [Rr])
                                                TT(eng_, dr, dr, m1, ALU.add, [Rr], [Rr]); TT(eng_, di, di, m2, ALU.add, [Rr], [Rr])
                                for r in range(2):
                                    ACT(Hb[:, d, r, :, :], A[r][:], AF.Identity, [RS, RSp], [RHb])
                                for s_i in (1, 2):
                                    idx = SOFF[s_i] + (16 if d == 0 else 0)
                                    for r in range(2):
                                        ACT(stg[:, :, r], A[r][:, :, idx], AF.Identity, [RS, RSp], [Rstg])
                                    fw.dma("sp", Dout["nst"][s_i - 1, l, d].rearrange("(pair g2) p r -> (g2 p) pair r", g2=2), stg[:], reads=[Rstg], writes=[Rdbg])
                                DBG("d_A%d" % d, A[0][:].rearrange("p a b -> p (a b)"), [RS, RSp])
                                fw.barrier()
                        with contextlib.ExitStack() as SQ:
                            CL2 = [[[sb(SQ, "CL%d%d%d" % (z, d, r), [128, 4, 17, 32], BF16) for r in range(2)] for d in range(2)] for z in range(2)]
                            Kblk = [sb(SQ, "Kblk%d" % d, [128, 16, 128], BF16) for d in range(2)]
                            Bb2 = [[[sb(SQ, "Bb%d%d%d" % (z, d, r), [128, 4, 32], BF16) for r in range(2)] for d in range(2)] for z in range(2)]
                            Dbl = sb(SQ, "Dbl", [128, 4, 128], BF16)
                            cm = [sb(SQ, "cm%d" % j, [128, 2, 17, 32]) for j in range(2)]
                            Cl = [sb(SQ, "Cl%d" % r, [128, 4, 32]) for r in range(2)]
                            nCl = sb(SQ, "nCl", [128, 4, 32])
                            Bl = [sb(SQ, "Bl%d" % r, [128, 4, 32]) for r in range(2)]
                            bt = [sb(SQ, "bt%d" % r, [128, 4, 32]) for r in range(2)]
                            gt = [sb(SQ, "gt%d" % r, [128, 512]) for r in range(2)]; Rgt = [Res("gt0"), Res("gt1")]
                            RCL2 = [Res("CLa"), Res("CLb")]; RK = Res("Kblk"); RDbl = Res("Dbl")
                            ydb = sb(SQ, "ydb", [128, 512]); Rydb = Res("ydb")
                            fw.dma("pool", Dbl[:], Din["Dblk"][l], writes=[RDbl])

                            def build_CL(q):
                                z = q % 2; CL = CL2[z]; Bb = Bb2[z]; RCL = RCL2[z]
                                ps4 = slice(4 * q, 4 * q + 4)
                                for d in range(2):
                                    for r in range(2):
                                        fw.dma("sp", Cl[r][:], Din["CbdP"][l, d, r, :, ps4, :], writes=[RS])
                                        fw.dma("sp", Bl[r][:], Din["BbdP"][l, d, r, :, ps4, :], writes=[RS])
                                    TS("dve", nCl[:], Cl[0][:], -1.0, ALU.mult, [RS], [RS])
                                    for hp in range(2):
                                        pp = slice(2 * hp, 2 * hp + 2); pq = slice(4 * q + 2 * hp, 4 * q + 2 * hp + 2)
                                        Pr = PowP[0][:, d, pq, :].unsqueeze(3).to_broadcast([128, 2, 17, 32])
                                        Pi = PowP[1][:, d, pq, :].unsqueeze(3).to_broadcast([128, 2, 17, 32])
                                        Cr = Cl[0][:, pp, :].unsqueeze(2).to_broadcast([128, 2, 17, 32])
                                        Ci = Cl[1][:, pp, :].unsqueeze(2).to_broadcast([128, 2, 17, 32])
                                        nCr = nCl[:, pp, :].unsqueeze(2).to_broadcast([128, 2, 17, 32])
                                        TT("dve", cm[0][:], Pr, Cr, ALU.mult, [RS], [RS]); TT("dve", cm[1][:], Pi, Ci, ALU.mult, [RS], [RS])
                                        TT("dve", CL[d][0][:, pp, :, :], cm[0][:], cm[1][:], ALU.subtract, [RS], [RCL])
                                        TT("dve", cm[0][:], Pr, Ci, ALU.mult, [RS], [RS]); TT("dve", cm[1][:], Pi, nCr, ALU.mult, [RS], [RS])
                                        TT("dve", CL[d][1][:, pp, :, :], cm[1][:], cm[0][:], ALU.subtract, [RS], [RCL])
                                    fr = bl(fP[0][:, d, ps4], 32); fi = bl(fP[1][:, d, ps4], 32)
                                    TT("dve", bt[0][:], fr, Bl[0][:], ALU.mult, [RS], [RS]); TT("dve", bt[1][:], fi, Bl[1][:], ALU.mult, [RS], [RS])
                                    TT("dve", Bb[d][0][:], bt[0][:], bt[1][:], ALU.subtract, [RS], [RCL])
                                    TT("dve", bt[0][:], fr, Bl[1][:], ALU.mult, [RS], [RS]); TT("dve", bt[1][:], fi, Bl[0][:], ALU.mult, [RS], [RS])
                                    TT("dve", Bb[d][1][:], bt[0][:], bt[1][:], ALU.add, [RS], [RCL])

                            def build_K(q):
                                z = q % 2; CL = CL2[z]; Bb = Bb2[z]; RCL = RCL2[z]
                                for d in range(2):
                                    kb = d
                                    for b in range(4):
                                        MM(bank(kb)[32 * b:32 * b + 32, :], Bb[d][0][:, b, :], CL[d][0][:, b, 0:16, :].rearrange("p t n -> p (t n)"), True, False,
                                           [RCL], [PB[kb]], tp=(0, 32 * b))
                                        MM(bank(kb)[32 * b:32 * b + 32, :], Bb[d][1][:, b, :], CL[d][1][:, b, 0:16, :].rearrange("p t n -> p (t n)"), False, True,
                                           [RCL], [PB[kb]], tp=(0, 32 * b))
                                    for cb in range(4):
                                        ACT(Kblk[d][:, :, 32 * cb:32 * cb + 32], bank(kb).rearrange("p (t n) -> p t n", n=32), AF.Identity,
                                            [PB[kb], Rmask], [RK], scale=maskP[:, cb:cb + 1])

                            def run_y(q):
                                z = q % 2; CL = CL2[z]; RCL = RCL2[z]
                                for tb, (c0, cn) in enumerate(BLKS):
                                    yb = 3 + tb
                                    Yb = bank(yb).rearrange("p (c j) -> p c j", j=16)
                                    sblk = sfT[:, q, c0:c0 + cn].rearrange("p (c j) -> p c j", j=16)
                                    MM(bank(yb), Dbl[:, q, :], sfT[:, q, c0:c0 + cn], True, False, [RDbl, RsfT[q]], [PB[yb]])
                                    for d in range(2):
                                        for t in range(16):
                                            if d == 0:
                                                o_, r_ = Yb[:, :, t:16], sblk[:, :, 0:16 - t]
                                            else:
                                                o_, r_ = Yb[:, :, 0:16 - t], sblk[:, :, t:16]
                                            MM(o_, Kblk[d][:, t, :], r_, False, False, [RK, RsfT[q]], [PB[yb]])
                                    GC = os.environ.get('GC', '1') == '1'
                                    for d in range(2):
                                        for j in range(16):
                                            tt = j + 1 if d == 0 else 16 - j
                                            sh = 0 if d == 0 else 1
                                            for r in range(2):
                                                for b in range(4):
                                                    gfirst = (b == 0 and r == 0 and j == 0 and d == 0)
                                                    if tb < 2:
                                                        rhs = Hb[:, d, r, 4 * q + b, 32 * tb + sh:32 * tb + sh + 32]
                                                        o_ = Yb[32 * b:32 * b + 32, :, j]
                                                    else:
                                                        rhs = Hb[:, d, r, 4 * q + b, 65:99].rearrange("p (s c) -> p s c", s=2)[:, :, sh:sh + 16]
                                                        o_ = bank(yb).rearrange("p (s c j) -> p s c j", s=2, j=16)[32 * b:32 * b + 32, :, :, j]
                                                    last = (d == 1 and b == 3 and j == 15 and r == 1)
                                                    MM(o_, CL[d][r][:, b, tt, :], rhs, False, last, [RCL, RHb], [PB[yb]], tp=(0, 32 * b),
                                                       chain=(("gfirst" if gfirst else "gnext") if GC else False))
                                    if ("d_y%d_%d" % (q, tb)) in Ddbg:
                                        ACT(ydb[:], bank(yb), AF.Identity, [PB[yb]], [Rydb])
                                        DBG("d_y%d_%d" % (q, tb), ydb[:], [Rydb])
                                    g_ = tb % 2
                                    ACT(gt[g_][:], bank(yb), AF.Square, [PB[yb]], [Rgt[g_]], scale=float(0.044715 ** 0.5))
                                    OP("dve", lambda e: e.scalar_tensor_tensor(out=gt[g_][:], in0=gt[g_][:], scalar=1.0, in1=bank(yb), op0=ALU.add, op1=ALU.mult),
                                       [Rgt[g_], PB[yb]], [Rgt[g_]])
                                    ACT(gt[g_][:], gt[g_][:], AF.Sigmoid, [Rgt[g_]], [Rgt[g_]], scale=1.5957691216)
                                    TT("dve", sfT[:, q, c0:c0 + cn], gt[g_][:], bank(yb), ALU.mult, [Rgt[g_], PB[yb]], [RsfT[q]])
                            build_CL(0); build_K(0)
                            for q in range(4):
                                if q + 1 < 4:
                                    build_CL(q + 1)
                                run_y(q)
                                if q + 1 < 4:
                                    build_K(q + 1)
                            fw.barrier()
                    Lp.close()
                    ssmT = sb(L3, "ssmT", [128, 4, TOK], BF16); RssmT = [Res("ssmT%d" % j) for j in range(4)]
                    CH[0] = 'G' in os.environ.get('CHP', 'mabfFGMLUS')
                    with contextlib.ExitStack() as ph:
                        alloc_ws(ph)
                        bgl = sb(ph, "bgl", [128, 4]); Rbgl = Res("bgl"); sg = [sb(ph, "sg%d" % j, [128, 512]) for j in range(2)]; Rsg = [Res("sg0"), Res("sg1")]
                        fw.dma("sp", bgl[:], Din["bglu"][:, l, :], writes=[Rbgl])
                        Wg, RWg = load_w([(lambda t: t[:, 0:2048].rearrange("p (k n) -> p k n", k=4), Din["w_glu"][l].rearrange("(k p) n -> p k n", p=128))])
                        Wgv = Wg[:, 0:2048].rearrange("p (k n) -> p k n", k=4)
                        cnt = 0
                        for m in range(4):
                            for tb, (c0, cn) in enumerate(BLKS):
                                bk = (0, 1, 2)[cnt % 3]; sj = cnt % 2; cnt += 1
                                for k in range(4):
                                    MM(bank(bk), Wgv[:, k, m * 128:(m + 1) * 128], sfT[:, k, c0:c0 + cn], k == 0, k == 3, [RWg, RsfT[k]], [PB[bk]])
                                ACT(sg[sj][:], bank(bk), AF.Sigmoid, [PB[bk], Rbgl], [Rsg[sj]], bias=bgl[:, m:m + 1])
                                TT("dve", ssmT[:, m, c0:c0 + cn], sg[sj][:], sfT[:, m, c0:c0 + cn], ALU.mult, [Rsg[sj], RsfT[m]], [RssmT[m]])
                        fw.barrier()
                    DBG("d_gy", sfT[:, 0:4, :].rearrange("p a b -> p (a b)"), RsfT[0:4])
                    DBG("d_ssmT", ssmT[:].rearrange("p a b -> p (a b)"), RssmT)
                    if stop == "ssm": fw.dead = True
                    CH[0] = 'M' in os.environ.get('CHP', 'mabfFGMLUS')
                    with contextlib.ExitStack() as M:
                        mergedT = sb(M, "mergedT", [128, 8, TOK], BF16); Rmg = [Res("mg%d" % j) for j in range(8)]
                        with contextlib.ExitStack() as ph:
                            alloc_ws(ph)
                            uT = sb(ph, "uT", [128, 8, TOK], BF16); RuT = [Res("uT%d" % i) for i in range(NT)]
                            gs = [sb(ph, "gs%d" % j, [128, 512]) for j in range(3)]; Rgs = [Res("gs%d" % j) for j in range(3)]
                            ac = [sb(ph, "ac%d" % j, [128, 512]) for j in range(2)]; Rac = [Res("ac0"), Res("ac1")]
                            build_uT(uT, RuT, 0)
                            win = Din["w_in"][l].rearrange("(k p) n -> p k n", p=128)
                            wbr = [Din[nm][l].rearrange("(k p) n -> p k n", p=128) for nm in ("w_br_attn", "w_br_ssm", "w_br_four")]
                            srcs = [(attnT, RattnT), (ssmT, RssmT), (fourT, RfourT)]
                            Wg2 = [sb(ph, "Wg2_%d" % z, [128, 3072], BF16) for z in range(2)]; RWg2 = [Res("Wg2a"), Res("Wg2b")]
                            Wb2 = [sb(ph, "Wb2_%d" % z, [128, 1536], BF16) for z in range(2)]; RWb2 = [Res("Wb2a"), Res("Wb2b")]
                            def load_c(c):
                                z = c % 2
                                for g in range(3):
                                    fw.dma("pool", Wg2[z][:, g * 1024:(g + 1) * 1024].rearrange("p (k n) -> p k n", k=8),
                                           win[:, :, 1792 + g * 1024 + c * 128:1792 + g * 1024 + (c + 1) * 128], writes=[RWg2[z]])
                                for x in range(3):
                                    fw.dma("pool", Wb2[z][:, x * 512:(x + 1) * 512].rearrange("p (k n) -> p k n", k=4),
                                           wbr[x][:, :, c * 128:(c + 1) * 128], writes=[RWb2[z]])
                            load_c(0)
                            for c in range(8):
                                if c + 1 < 8:
                                    load_c(c + 1)
                                Wg, RWg, Wb, RWb = Wg2[c % 2], RWg2[c % 2], Wb2[c % 2], RWb2[c % 2]
                                for tb, (c0, cn) in enumerate(BLKS):
                                    for g in range(3):
                                        Wgv = Wg[:, g * 1024:(g + 1) * 1024].rearrange("p (k n) -> p k n", k=8)
                                        for k in range(8):
                                            MM(bank(g), Wgv[:, k, :], uT[:, k, c0:c0 + cn], k == 0, k == 7, RuT[c0 // 128:(c0 + cn) // 128] + [RWg], [PB[g]])
                                        ACT(gs[g][:], bank(g), AF.Sigmoid, [PB[g]], [Rgs[g]])
                                    for x in range(3):
                                        Wbv = Wb[:, x * 512:(x + 1) * 512].rearrange("p (k n) -> p k n", k=4)
                                        st_, Rst = srcs[x]
                                        for k in range(4):
                                            MM(bank(3 + x), Wbv[:, k, :], st_[:, k, c0:c0 + cn], k == 0, k == 3, [Rst[k], RWb], [PB[3 + x]])
                                    TT("dve", ac[0][:], gs[0][:], bank(3), ALU.mult, [Rgs[0], PB[3]], [Rac[0]])
                                    TT("dve", ac[1][:], gs[1][:], bank(4), ALU.mult, [Rgs[1], PB[4]], [Rac[1]])
                                    TT("dve", ac[0][:], ac[0][:], ac[1][:], ALU.add, Rac, [Rac[0]])
                                    TT("dve", ac[1][:], gs[2][:], bank(5), ALU.mult, [Rgs[2], PB[5]], [Rac[1]])
                                    TT("dve", mergedT[:, c, c0:c0 + cn], ac[0][:], ac[1][:], ALU.add, Rac, [Rmg[c]])
                            fw.barrier()
                        DBG("d_mergedT", mergedT[:].rearrange("p a b -> p (a b)"), Rmg)
                        if stop == "merge": fw.dead = True
                        layer_norm_residual(l, 0, 8, lambda i, k: mergedT[:, k, i * 128:(i + 1) * 128], lambda i, k: [Rmg[k]],
                                            Din["w_out"][l].rearrange("(k p) n -> p k n", p=128))
              if "d_x1" in Ddbg:
                  for i in range(NT):
                      fw.dma("sp", Ddbg["d_x1"][i * 128:(i + 1) * 128, :], X[:, i, :], reads=[RX[i]], writes=[Rdbg])
              if stop == "ln1": fw.dead = True
              CH[0] = 'U' in os.environ.get('CHP', 'mabfFGMLUS')
              with contextlib.ExitStack() as Fs:
                  actT = sb(Fs, "actT", [128, 22, TOK], BF16); Ract = [Res("act%d" % j) for j in range(22)]
                  with contextlib.ExitStack() as ph:
                      alloc_ws(ph)
                      uT = sb(ph, "uT", [128, 8, TOK], BF16); RuT = [Res("uT%d" % i) for i in range(NT)]
                      cvp = sb(ph, "cvp", [128, 44, 4]); Rcv = Res("cvp")
                      hc = [sb(ph, "hc%d" % j, [128, TOK]) for j in range(2)]; Rhc = [Res("hc0"), Res("hc1")]
                      fw.dma("sp", cvp[:], Din["convp"][:, l], writes=[Rcv])
                      wup = Din["w_up"][l].rearrange("(k p) n -> p k n", p=128)
                      def load_up(bi):
                          return load_w([(lambda t: t[:, 0:4096].rearrange("p (k n) -> p k n", k=8), wup[:, :, bi * 512:(bi + 1) * 512])])
                      nxt_up = load_up(0)
                      build_uT(uT, RuT, 1)
                      for ch in range(44):
                          if ch % 4 == 0:
                              W, RW = nxt_up
                              if ch // 4 + 1 < 11:
                                  nxt_up = load_up(ch // 4 + 1)
                              Wv = W[:, 0:4096].rearrange("p (k n) -> p k n", k=8)
                          cw = ch % 4; pb0 = 3 * (ch % 2); hj = ch % 2
                          prs = [PB[pb0], PB[pb0 + 1], PB[pb0 + 2]]
                          for tb, (c0, cn) in enumerate(BLKS):
                              for k in range(8):
                                  MM(bank(pb0 + tb), Wv[:, k, cw * 128:(cw + 1) * 128], uT[:, k, c0:c0 + cn], k == 0, k == 7,
                                     RuT[c0 // 128:(c0 + cn) // 128] + [RW], [prs[tb]])
                          hp = bank(pb0, 3); h_ = hc[hj]
                          ACT(h_[:], hp, AF.Identity, prs + [Rcv], [Rhc[hj]], scale=cvp[:, ch, 1:2], bias=cvp[:, ch, 3:4])
                          def stt(o_, i0, sc, i1):
                              OP("dve", lambda e: e.scalar_tensor_tensor(out=o_, in0=i0, scalar=sc, in1=i1, op0=ALU.mult, op1=ALU.add),
                                 prs + [Rcv, Rhc[hj]], [Rhc[hj]])
                          stt(h_[:, 1:1024], hp[:, 0:1023], cvp[:, ch, 0:1], h_[:, 1:1024])
                          stt(h_[:, 0:1023], hp[:, 1:1024], cvp[:, ch, 2:3], h_[:, 0:1023])
                          h3 = h_[:, 1024:1536].rearrange("p (s c) -> p s c", s=2); p3 = hp[:, 1024:1536].rearrange("p (s c) -> p s c", s=2)
                          stt(h3[:, :, 1:256], p3[:, :, 0:255], cvp[:, ch, 0:1], h3[:, :, 1:256])
                          stt(h3[:, :, 0:255], p3[:, :, 1:256], cvp[:, ch, 2:3], h3[:, :, 0:255])
                          if ch < 22:
                              ACT(actT[:, ch, :], h_[:], AF.Silu, [Rhc[hj]], [Ract[ch]])
                          else:
                              TT("dve", actT[:, ch - 22, :], actT[:, ch - 22, :], h_[:], ALU.mult, [Ract[ch - 22], Rhc[hj]], [Ract[ch - 22]])
                      fw.barrier()
                  DBG("d_actT", actT[:].rearrange("p a b -> p (a b)"), Ract)
                  if stop == "ffn": fw.dead = True
                  def y_out(i):
                      fw.dma("sp", Dout["y"][i * 128:(i + 1) * 128, :], X[:, i, :], reads=[RX[i]], writes=[Rdbg])
                  layer_norm_residual(l, 1, 22, lambda i, k: actT[:, k, i * 128:(i + 1) * 128], lambda i, k: [Ract[k]],
                                      Din["w_down"][l].rearrange("(k p) n -> p k n", p=128),
                                      out_tile=(y_out if l == n_layers - 1 else None))
        layers()
        if fw.dead:
            fw.dead = False
            for i in range(NT):
                fw.dma("sp", Dout["y"][i * 128:(i + 1) * 128, :], X[:, i, :], reads=[RX[i]], writes=[Rdbg])
        fw.finish()
    return nc


def kernel(**inputs):
    inp = {k: np.asarray(v) for k, v in inputs.items()}
    S = prep_shared(inp)
    nc = build()
    in_maps = []
    for cid in range(8):
        m = dict(S); m.update(prep_core(inp, cid))
        in_maps.append(m)
    res = run_bass_kernel_spmd(nc, in_maps, core_ids=list(range(8)))
    y_p = np.zeros((16, 256, 1024), np.float32); y_s = np.zeros((8, 1024, 1024), np.float32)
    nk = np.zeros((16, 2, 256, 2, 64), np.float32); nv = np.zeros((16, 2, 256, 2, 64), np.float32)
    nst = np.zeros((16, 2, 2, 32, 64, 2), np.float32)
    for cid in range(8):
        r = res.results[cid]
        y_s[cid] = r["y"][0:1024]
        y_p[2 * cid:2 * cid + 2] = r["y"][1024:1536].reshape(2, 256, 1024)
        nk[2 * cid:2 * cid + 2] = r["nk"].reshape(2, 2, 256, 2, 64)
        nv[2 * cid:2 * cid + 2] = r["nv"].reshape(2, 2, 256, 2, 64)
        nst[2 * cid:2 * cid + 2] = r["nst"]
    return (y_p, y_s, nk, nv, nst)
```
